# Optimizing a Trainium2 kernel written in Bass

```python
import jax
import jax.numpy as jnp
from jax import lax
import numpy as np

D_MODEL = 1024
BATCH = 1
SEQ = 16384
DEPTH = 2

N_MIXERS = 2
N_LAYERS_A = (DEPTH + 1) // 2
N_LAYERS_B = DEPTH // 2

A_HEADS = 16
A_HEAD_DIM = D_MODEL // A_HEADS
A_GROUPS = ((128, 1), (512, 4), (2048, 16))
A_N_GROUPS = len(A_GROUPS)
A_PAD = A_GROUPS[-1][0]
Q_BLOCK = 128

B_INNER = 2 * D_MODEL
B_HEADS = 4
B_HEAD_DIM = B_INNER // B_HEADS
B_CONV = 4
B_QKV_BLOCK = 4
B_CHUNK = 128

D_FF = 4 * D_MODEL
PLE_DIM = 256

EPS = 1e-6
NEG_INF = -1e30

kernel_name = 'hybrid_dilated_attn_mlstm_trunk'


def rms_norm(x, g):
    xf = x.astype(jnp.float32)
    y = xf * lax.rsqrt(jnp.mean(xf * xf, axis=-1, keepdims=True) + EPS)
    return (y * g.astype(jnp.float32)).astype(x.dtype)


def alibi_slopes(n):
    return jnp.asarray([2.0 ** (-8.0 * (h + 1) / n) for h in range(n)], jnp.float32)


def dilated_attention(h, w_qkv, q_gain, k_gain, w_o):
    B, S, _ = h.shape
    qkv = (h @ w_qkv).reshape(B, S, A_N_GROUPS, 3, A_HEADS, A_HEAD_DIM)
    q = rms_norm(qkv[:, :, :, 0], q_gain[:, None, :])
    k = rms_norm(qkv[:, :, :, 1], k_gain[:, None, :])
    v = qkv[:, :, :, 2]
    pad = ((0, 0), (A_PAD, 0), (0, 0), (0, 0))
    k_pads = [jnp.pad(k[:, :, g], pad) for g in range(A_N_GROUPS)]
    v_pads = [jnp.pad(v[:, :, g], pad) for g in range(A_N_GROUPS)]
    slopes = alibi_slopes(A_HEADS)
    scale = A_HEAD_DIM ** -0.5
    n_blocks = S // Q_BLOCK
    q_blocks = jnp.moveaxis(q.reshape(B, n_blocks, Q_BLOCK, A_N_GROUPS, A_HEADS, A_HEAD_DIM), 1, 0)

    def block(args):
        blk, qb = args
        t = blk * Q_BLOCK + jnp.arange(Q_BLOCK)
        lses, outs = [], []
        for g, (window, dil) in enumerate(A_GROUPS):
            n_keys = window // dil + 1
            dist = jnp.arange(n_keys) * dil
            pos = t[:, None] - dist[None, :]
            kg = jnp.take(k_pads[g], pos + A_PAD, axis=1)
            vg = jnp.take(v_pads[g], pos + A_PAD, axis=1)
            s = jnp.einsum('bqhd,bqjhd->bhqj', qb[:, :, g], kg).astype(jnp.float32) * scale
            s = s - slopes[:, None, None] * dist.astype(jnp.float32)
            s = jnp.where((pos >= 0)[None, None], s, NEG_INF)
            lse = jax.nn.logsumexp(s, axis=-1)
            pr = jnp.exp(s - lse[..., None]).astype(vg.dtype)
            outs.append(jnp.einsum('bhqj,bqjhd->bqhd', pr, vg))
            lses.append(lse)
        wgt = jax.nn.softmax(jnp.stack(lses, axis=0), axis=0)
        return jnp.einsum('gbhq,gbqhd->bqhd', wgt.astype(outs[0].dtype), jnp.stack(outs, axis=0))

    o = lax.map(block, (jnp.arange(n_blocks), q_blocks))
    o = jnp.moveaxis(o, 0, 1).reshape(B, S, D_MODEL)
    return o @ w_o


def causal_depthwise_conv(x, w, b):
    y = lax.conv_general_dilated(
        x, w[:, None, :].astype(x.dtype), window_strides=(1,),
        padding=((B_CONV - 1, 0),), dimension_numbers=('NWC', 'WIO', 'NWC'),
        feature_group_count=x.shape[-1])
    return y + b


def block_diag_proj(x, w):
    B, S, C = x.shape
    nb, blk, _ = w.shape
    return jnp.einsum('bsnj,njk->bsnk', x.reshape(B, S, nb, blk), w).reshape(B, S, C)


def mlstm_chunkwise(q, k, v, i_pre, f_pre):
    B, NH, S, DH = q.shape
    L = B_CHUNK
    NC = S // L
    k = k * DH ** -0.5
    logf = jax.nn.log_sigmoid(f_pre)

    def to_chunks(a):
        return jnp.moveaxis(a.reshape(B, NH, NC, L, *a.shape[3:]), 2, 0)

    causal = jnp.tril(jnp.ones((L, L), dtype=bool))

    def step(carry, inp):
        C, n, m = carry
        qt, kt, vt, it, lf = inp
        b = jnp.cumsum(lf, axis=-1)
        d_mat = jnp.where(causal, b[..., :, None] - b[..., None, :] + it[..., None, :], NEG_INF)
        inter = b + m[..., None]
        m_t = jnp.maximum(inter, jnp.max(d_mat, axis=-1))
        s = jnp.einsum('bhtd,bhsd->bhts', qt, kt) * jnp.exp(d_mat - m_t[..., None])
        sc = jnp.exp(inter - m_t)
        num = sc[..., None] * jnp.einsum('bhtd,bhde->bhte', qt, C) + jnp.einsum('bhts,bhse->bhte', s, vt)
        den = sc * jnp.einsum('bhtd,bhd->bht', qt, n) + jnp.sum(s, axis=-1)
        h = num / jnp.maximum(jnp.abs(den), jnp.exp(-m_t))[..., None]
        b_last = b[..., -1]
        g = b_last[..., None] - b + it
        m_new = jnp.maximum(b_last + m, jnp.max(g, axis=-1))
        decay = jnp.exp(b_last + m - m_new)
        wk = kt * jnp.exp(g - m_new[..., None])[..., None]
        C_new = decay[..., None, None] * C + jnp.einsum('bhsd,bhse->bhde', wk, vt)
        n_new = decay[..., None] * n + jnp.sum(wk, axis=2)
        return (C_new, n_new, m_new), h

    init = (jnp.zeros((B, NH, DH, DH), jnp.float32),
            jnp.zeros((B, NH, DH), jnp.float32),
            jnp.full((B, NH), NEG_INF, jnp.float32))
    _, hs = lax.scan(step, init, (to_chunks(q), to_chunks(k), to_chunks(v), to_chunks(i_pre), to_chunks(logf)))
    return jnp.moveaxis(hs, 0, 2).reshape(B, NH, S, DH)


def mlstm_mixer(h, w_up, conv_w, conv_b, w_q, w_k, w_v, w_gate, b_gate, h_gain, skip, w_down):
    B, S, _ = h.shape
    xz = h @ w_up
    xm, z = xz[..., :B_INNER], xz[..., B_INNER:]
    xc = jax.nn.silu(causal_depthwise_conv(xm, conv_w, conv_b))
    q = block_diag_proj(xc, w_q)
    k = block_diag_proj(xc, w_k)
    v = block_diag_proj(xm, w_v)
    gates = (jnp.concatenate([q, k, v], axis=-1) @ w_gate).astype(jnp.float32) + b_gate.astype(jnp.float32)
    i_pre = jnp.transpose(gates[..., :B_HEADS], (0, 2, 1))
    f_pre = jnp.transpose(gates[..., B_HEADS:], (0, 2, 1))

    def heads(a):
        return jnp.transpose(a.reshape(B, S, B_HEADS, B_HEAD_DIM), (0, 2, 1, 3)).astype(jnp.float32)

    hc = mlstm_chunkwise(heads(q), heads(k), heads(v), i_pre, f_pre)
    hc = rms_norm(jnp.transpose(hc, (0, 2, 1, 3)), h_gain.reshape(B_HEADS, B_HEAD_DIM))
    hc = hc.reshape(B, S, B_INNER).astype(h.dtype)
    out = (hc + skip * xc) * jax.nn.silu(z)
    return out @ w_down


def squared_relu_mlp(h, w1, w2):
    return jnp.square(jax.nn.relu(h @ w1)) @ w2


def setup_inputs(seed: int = 0) -> dict:
    key = jax.random.key(seed)
    keys = jax.random.split(key, 32)
    counter = [0]

    def nrm(shape, scale):
        kk = keys[counter[0]]
        counter[0] += 1
        return scale * jax.random.normal(kk, shape, jnp.float32)

    def gain(shape):
        return 1.0 + nrm(shape, 0.01)

    NA, NB = N_LAYERS_A, N_LAYERS_B
    n_blk = B_INNER // B_QKV_BLOCK
    f_bias = jnp.linspace(3.0, 6.0, B_HEADS, dtype=jnp.float32)
    return {
        'x': nrm((BATCH, SEQ, D_MODEL), 1.0),
        'p': nrm((DEPTH, BATCH, SEQ, PLE_DIM), 1.0),
        'a_norm': gain((NA, D_MODEL)),
        'a_w_qkv': nrm((NA, D_MODEL, A_N_GROUPS * 3 * D_MODEL), D_MODEL ** -0.5),
        'a_q_gain': gain((NA, A_N_GROUPS, A_HEAD_DIM)),
        'a_k_gain': gain((NA, A_N_GROUPS, A_HEAD_DIM)),
        'a_w_o': nrm((NA, D_MODEL, D_MODEL), D_MODEL ** -0.5),
        'b_norm': gain((NB, D_MODEL)),
        'b_w_up': nrm((NB, D_MODEL, 2 * B_INNER), D_MODEL ** -0.5),
        'b_conv_w': nrm((NB, B_CONV, B_INNER), B_CONV ** -0.5),
        'b_conv_b': nrm((NB, B_INNER), 0.01),
        'b_w_q': nrm((NB, n_blk, B_QKV_BLOCK, B_QKV_BLOCK), B_QKV_BLOCK ** -0.5),
        'b_w_k': nrm((NB, n_blk, B_QKV_BLOCK, B_QKV_BLOCK), B_QKV_BLOCK ** -0.5),
        'b_w_v': nrm((NB, n_blk, B_QKV_BLOCK, B_QKV_BLOCK), B_QKV_BLOCK ** -0.5),
        'b_w_gate': nrm((NB, 3 * B_INNER, 2 * B_HEADS), (3 * B_INNER) ** -0.5),
        'b_b_gate': jnp.concatenate([nrm((NB, B_HEADS), 0.1), f_bias[None, :] + nrm((NB, B_HEADS), 0.01)], axis=-1),
        'b_h_gain': gain((NB, B_INNER)),
        'b_skip': gain((NB, B_INNER)),
        'b_w_down': nrm((NB, B_INNER, D_MODEL), B_INNER ** -0.5),
        'mlp_norm': gain((DEPTH, D_MODEL)),
        'mlp_w1': nrm((DEPTH, D_MODEL, D_FF), D_MODEL ** -0.5),
        'mlp_w2': nrm((DEPTH, D_FF, D_MODEL), D_FF ** -0.5),
        'ple_norm': gain((DEPTH, D_MODEL)),
        'ple_w_gate': nrm((DEPTH, D_MODEL, D_MODEL), D_MODEL ** -0.5),
        'ple_w_proj': nrm((DEPTH, PLE_DIM, D_MODEL), PLE_DIM ** -0.5),
    }


def reference(x, p, a_norm, a_w_qkv, a_q_gain, a_k_gain, a_w_o,
              b_norm, b_w_up, b_conv_w, b_conv_b, b_w_q, b_w_k, b_w_v,
              b_w_gate, b_b_gate, b_h_gain, b_skip, b_w_down,
              mlp_norm, mlp_w1, mlp_w2, ple_norm, ple_w_gate, ple_w_proj):
    for i in range(DEPTH):
        j = i // N_MIXERS
        if i % N_MIXERS == 0:
            x = x + dilated_attention(rms_norm(x, a_norm[j]), a_w_qkv[j], a_q_gain[j], a_k_gain[j], a_w_o[j])
        else:
            x = x + mlstm_mixer(rms_norm(x, b_norm[j]), b_w_up[j], b_conv_w[j], b_conv_b[j],
                                b_w_q[j], b_w_k[j], b_w_v[j], b_w_gate[j], b_b_gate[j],
                                b_h_gain[j], b_skip[j], b_w_down[j])
        x = x + squared_relu_mlp(rms_norm(x, mlp_norm[i]), mlp_w1[i], mlp_w2[i])
        gate = jax.nn.sigmoid(rms_norm(x, ple_norm[i]) @ ple_w_gate[i])
        x = x + gate * (p[i] @ ple_w_proj[i])
    return x
```

```python
import numpy as np
import concourse.bass as bass
import concourse.mybir as mybir
from concourse.bass_utils import run_bass_kernel_spmd

F32 = mybir.dt.float32
BF16 = mybir.dt.bfloat16
AF = mybir.ActivationFunctionType
ALU = mybir.AluOpType
AX = mybir.AxisListType

NCORES = 8
S = 16384
D = 1024
NT = S // NCORES
NC8 = D // 128
EPS = 1e-6
BIG = 30000.0
A_GROUPS = ((128, 1), (512, 4), (2048, 16))
NDMA = 24
SB_F32 = 51968


class Tk:
    __slots__ = ("w", "r")

    def __init__(self):
        self.w = {}
        self.r = {}


class Prog:
    def __init__(self, nc):
        self.nc = nc
        self.eng = {"act": nc.scalar, "dve": nc.vector, "pool": nc.gpsimd, "pe": nc.tensor, "sync": nc.sync}
        self.sem = {e: nc.alloc_semaphore("s_" + e) for e in ("act", "dve", "pool", "pe")}
        self.cnt = {e: 0 for e in ("act", "dve", "pool", "pe")}
        self.seen = {e: {} for e in self.eng}
        self.dsem = [nc.alloc_semaphore("s_dma%d" % i) for i in range(NDMA)]
        self.dcnt = [0] * NDMA
        self.dnext = 0
        self.nins = {e: 0 for e in self.eng}

    def _semof(self, src):
        if isinstance(src, tuple):
            return self.dsem[src[1]]
        return self.sem[src]

    def _deps(self, e, reads, writes, allraw=False):
        deps = {}

        def add(src, n, raw):
            if src == e and not allraw:
                if e == "pe" or not raw:
                    return
            if deps.get(src, 0) < n:
                deps[src] = n

        for t in reads:
            for src, n in t.w.items():
                add(src, n, True)
        for t in writes:
            for src, n in t.w.items():
                add(src, n, False)
            for src, n in t.r.items():
                add(src, n, False)
        return deps

    def _wait(self, e, deps):
        eng = self.eng[e]
        seen = self.seen[e]
        for src, n in deps.items():
            if seen.get(src, 0) >= n:
                continue
            seen[src] = n
            eng.wait_ge(self._semof(src), n)
            self.nins[e] += 1

    def op(self, e, fn, reads=(), writes=(), signal=True):
        self._wait(e, self._deps(e, reads, writes))
        ins = fn(self.eng[e])
        self.nins[e] += 1
        n = self.cnt[e] + 1
        if signal:
            ins.then_inc(self.sem[e], 1)
            self.cnt[e] = n
        for t in reads:
            if t.r.get(e, 0) < n:
                t.r[e] = n
        for t in writes:
            if t.w.get(e, 0) < n:
                t.w[e] = n
        return ins

    def dma(self, q, out, in_, reads=(), writes=()):
        k = self.dnext
        self.dnext = (k + 1) % NDMA
        src = ("dma", k)
        deps = self._deps(q, reads, writes, allraw=True)
        if self.dcnt[k] > 0:
            deps[src] = max(deps.get(src, 0), self.dcnt[k])
        self._wait(q, deps)
        ins = self.eng[q].dma_start(out=out, in_=in_)
        self.nins[q] += 1
        n = self.dcnt[k] + 16
        ins.then_inc(self.dsem[k], 16)
        self.dcnt[k] = n
        for t in reads:
            t.r[src] = n
        for t in writes:
            t.w[src] = n

    def barrier(self):
        for e in self.eng:
            deps = {}
            for s2 in self.cnt:
                if s2 != e and self.cnt[s2] > 0:
                    deps[s2] = self.cnt[s2]
            for k in range(NDMA):
                if self.dcnt[k] > 0:
                    deps[("dma", k)] = self.dcnt[k]
            self._wait(e, deps)

    def finish(self):
        deps = {}
        for k in range(NDMA):
            if self.dcnt[k] > 0:
                deps[("dma", k)] = self.dcnt[k]
        self._wait("sync", deps)


class Rot:
    def __init__(self, aps):
        self.items = [a if isinstance(a, tuple) else (a, Tk()) for a in aps]
        self.i = 0

    def next(self):
        it = self.items[self.i]
        self.i = (self.i + 1) % len(self.items)
        return it


class Ctx:
    def __init__(self, nc):
        self.nc = nc
        self.P = Prog(nc)
        self.banks = [nc.alloc_psum_tensor("psb%d" % i, [128, 512], F32).ap() for i in range(8)]
        self.nalloc = 0

        self.big = nc.alloc_sbuf_tensor("big", [128, SB_F32], F32).ap()
        self.top = 0

    def sb(self, stack, shape, dt, name=None):
        esz = 2 if dt == BF16 else 4
        n = int(np.prod(shape[1:]))
        nbytes = (n * esz + 63) // 64 * 64
        off = self.top
        assert off + nbytes <= SB_F32 * 4, ("SBUF overflow", name, off, nbytes)
        self.top = off + nbytes
        self.log = getattr(self, 'log', [])
        self.log.append((name, off, nbytes))
        ap = self.big[:, off // 4:(off + nbytes) // 4]
        if dt == BF16:
            ap = ap.bitcast(BF16)
        ap = ap[:, 0:n]
        if len(shape) == 3:
            ap = ap.rearrange("p (a b) -> p a b", a=shape[1])
        elif len(shape) == 4:
            ap = ap.rearrange("p (a b c) -> p a b c", a=shape[1], b=shape[2])
        return ap

    def mark(self):
        return self.top

    def release(self, m):
        self.top = m


def load_consts(cx, stack, cst_ap, ncols):
    P = cx.P
    cf = cx.sb(stack, [128, ncols], F32, "cstf")
    cb = cx.sb(stack, [128, C_END_BF], BF16, "cstb")
    tk = Tk()
    P.dma("sync", cf, cst_ap, writes=[tk])
    P.op("dve", lambda e: e.tensor_copy(out=cb, in_=cf[:, 0:C_END_BF]), reads=[tk], writes=[tk])
    return cf, cb, tk


class WStream:
    def __init__(self, cx, stack, nelem, nstage=2, nslot=2):
        self.cx = cx
        self.nelem = nelem
        self.stage = Rot([cx.sb(stack, [128, nelem], F32, "wstg") for _ in range(nstage)])
        self.slots = Rot([cx.sb(stack, [128, nelem], BF16, "wbf") for _ in range(nslot)])

    def load(self, views):
        P = self.cx.P
        stg, stk = self.stage.next()
        wb, wtk = self.slots.next()
        off = 0
        for v in views:
            shp = v.shape
            n = int(np.prod(shp[1:]))
            dst = stg[:, off:off + n]
            if len(shp) == 3:
                dst = dst.rearrange("p (a b) -> p a b", a=shp[1])
            P.dma("sync", dst, v, writes=[stk])
            off += n
        assert off <= self.nelem
        P.op("pool", lambda e: e.tensor_copy(out=wb[:, 0:off], in_=stg[:, 0:off]), reads=[stk], writes=[wtk])
        return wb, wtk


def rms_stats(cx, xs, n, sq_rot, ps_ap, ps_tk, rstd, rstd_tk, ones_bf, ctk, inv_dim):
    P = cx.P
    nx = len(xs)
    for c, (xa, xt) in enumerate(xs):
        sq, sqt = sq_rot.next()
        P.op("act", lambda e, xa=xa, sq=sq: e.activation(out=sq[:, 0:n], in_=xa, func=AF.Square), reads=[xt], writes=[sqt])
        P.op("pe", lambda e, sq=sq, c=c: e.matmul(ps_ap[:, 0:n], lhsT=ones_bf, rhs=sq[:, 0:n], start=(c == 0), stop=(c == nx - 1)),
             reads=[sqt, ctk], writes=[ps_tk])
    P.op("act", lambda e: e.activation(out=rstd[:, 0:n], in_=ps_ap[:, 0:n], func=AF.Sqrt, bias=cx.eps_col, scale=inv_dim),
         reads=[ps_tk, ctk], writes=[rstd_tk])
    P.op("dve", lambda e: e.reciprocal(out=rstd[:, 0:n], in_=rstd[:, 0:n]), reads=[rstd_tk], writes=[rstd_tk])


C_ID, C_ONES, C_BONES, C_DM, C_HONES = 0, 128, 256, 384, 640
C_OZ = 704
C_HZ = 960
C_END_BF = 1216
C_EPS = 1216
C_GAINS = 1217
G0_ANORM = C_GAINS
G0_QG = G0_ANORM + 8
G0_KG = G0_QG + 3
G0_MLPN = G0_KG + 3
G0_PLEN = G0_MLPN + 8
G0_END = G0_PLEN + 8


def base_consts(core, ncols):
    c = np.zeros((128, ncols), np.float32)
    c[:, C_ID:C_ID + 128] = np.eye(128, dtype=np.float32)
    c[:, C_ONES:C_ONES + 128] = 1.0
    c[0:64, C_BONES:C_BONES + 64] = 1.0
    c[64:128, C_BONES + 64:C_BONES + 128] = 1.0
    kk = np.arange(128)[:, None]
    a = np.arange(128)[None, :]
    diag = np.where(kk <= a, a - kk, BIG)
    prev = np.where(kk >= a, 128 + a - kk, BIG)
    c[:, C_DM:C_DM + 128] = diag
    c[:, C_DM + 128:C_DM + 256] = prev
    hv = 0.0 if core == 0 else 1.0
    c[:, C_HONES:C_HONES + 64] = hv
    c[:, C_OZ:C_OZ + 64] = 1.0
    c[:, C_OZ + 128 + 64:C_OZ + 256] = 1.0
    c[:, C_HZ:C_HZ + 64] = hv
    c[:, C_HZ + 128 + 64:C_HZ + 256] = hv
    c[:, C_EPS] = EPS
    return c


def col_layout(v):
    v = np.asarray(v, np.float32).reshape(-1, 128)
    return np.ascontiguousarray(v.T)


def emit_norm_resident(cx, X, Xtk, gcol, ctk, hT, hTtk, sq_rot, rstd_rot, ps_rot, ones_bf):
    P = cx.P
    for tg in range(NT // 512):
        sl = slice(tg * 512, (tg + 1) * 512)
        xs = [(X[:, c, sl], Xtk[c][tg]) for c in range(NC8)]
        ps_ap, ps_tk = ps_rot.next()
        rstd, rtk = rstd_rot.next()
        rms_stats(cx, xs, 512, sq_rot, ps_ap, ps_tk, rstd, rtk, ones_bf, ctk, 1.0 / D)
        for c in range(NC8):
            P.op("dve", lambda e, c=c, sl=sl, rstd=rstd: e.scalar_tensor_tensor(
                out=hT[:, c, sl], in0=X[:, c, sl], scalar=gcol[:, c:c + 1], in1=rstd[:, 0:512],
                op0=ALU.mult, op1=ALU.mult), reads=[Xtk[c][tg], rtk, ctk], writes=[hTtk[c][tg]])


def emit_mlp(cx, X, Xtk, gcol, ctk, w1, w2, ones_bf):
    P = cx.P
    mk = cx.mark()
    st = None
    hT = cx.sb(st, [128, NC8, NT], BF16, "mlp_hT")
    hTtk = [[Tk() for _ in range(4)] for _ in range(NC8)]
    sq_rot = Rot([cx.sb(st, [128, 512], BF16, "sq") for _ in range(4)])
    rstd_rot = Rot([cx.sb(st, [128, 512], F32, "rstd") for _ in range(2)])
    ps_stat = Rot([cx.banks[7]])
    emit_norm_resident(cx, X, Xtk, gcol, ctk, hT, hTtk, sq_rot, rstd_rot, ps_stat, ones_bf)
    ws = WStream(cx, st, 4096, nstage=2, nslot=2)
    hids = [cx.sb(st, [128, 4, NT], BF16, "hid") for _ in range(2)]
    hid_tks = [[[Tk() for _ in range(4)] for _ in range(4)] for _ in range(2)]
    tmp_rot = Rot([cx.sb(st, [128, 512], F32, "rl") for _ in range(3)])
    psA = Rot(cx.banks[0:4])
    psB = Rot(cx.banks[4:7])
    w1v = w1.rearrange("(c p) n -> p c n", p=128)
    w2v = w2.rearrange("(c p) n -> p c n", p=128)
    NHB = 8
    for hb in range(NHB):
        hid = hids[hb % 2]
        htk = hid_tks[hb % 2]
        wa, watk = ws.load([w1v[:, :, hb * 512:(hb + 1) * 512]])
        wa3 = wa.rearrange("p (c n) -> p c n", c=NC8)
        for hc in range(4):
            for tg in range(4):
                sl = slice(tg * 512, (tg + 1) * 512)
                ps, pstk = psA.next()
                for c in range(NC8):
                    P.op("pe", lambda e, c=c, hc=hc, sl=sl, ps=ps, wa3=wa3: e.matmul(
                        ps, lhsT=wa3[:, c, hc * 128:(hc + 1) * 128], rhs=hT[:, c, sl],
                        start=(c == 0), stop=(c == NC8 - 1)),
                        reads=[watk, hTtk[c][tg]], writes=[pstk], signal=(c == NC8 - 1))
                tmp, ttk = tmp_rot.next()
                P.op("act", lambda e, ps=ps, tmp=tmp: e.activation(out=tmp, in_=ps, func=AF.Square),
                     reads=[pstk], writes=[ttk])
                P.op("dve", lambda e, ps=ps, tmp=tmp, hc=hc, sl=sl, hid=hid: e.scalar_tensor_tensor(
                    out=hid[:, hc, sl], in0=ps, scalar=0.0, in1=tmp, op0=ALU.is_gt, op1=ALU.mult),
                    reads=[pstk, ttk], writes=[htk[hc][tg]])
        wb, wbtk = ws.load([w2v[:, hb * 4:(hb + 1) * 4, :]])
        wb3 = wb.rearrange("p (c n) -> p c n", c=4)
        for oc in range(NC8):
            for tg in range(4):
                sl = slice(tg * 512, (tg + 1) * 512)
                ps, pstk = psB.next()
                for hc in range(4):
                    P.op("pe", lambda e, hc=hc, oc=oc, sl=sl, ps=ps, hid=hid, wb3=wb3: e.matmul(
                        ps, lhsT=wb3[:, hc, oc * 128:(oc + 1) * 128], rhs=hid[:, hc, sl],
                        start=(hc == 0), stop=(hc == 3)),
                        reads=[wbtk, htk[hc][tg]], writes=[pstk], signal=(hc == 3))
                P.op("dve", lambda e, oc=oc, sl=sl, ps=ps: e.tensor_tensor(
                    out=X[:, oc, sl], in0=ps, in1=X[:, oc, sl], op=ALU.add),
                    reads=[pstk, Xtk[oc][tg]], writes=[Xtk[oc][tg]])
    P.barrier()
    cx.release(mk)


def emit_ple(cx, X, Xtk, gcol, ctk, wg, wp, pT_dram, ones_bf):
    P = cx.P
    mk = cx.mark()
    st = None
    hT = cx.sb(st, [128, NC8, NT], BF16, "ple_hT")
    hTtk = [[Tk() for _ in range(4)] for _ in range(NC8)]
    sq_rot = Rot([cx.sb(st, [128, 512], BF16, "sq") for _ in range(4)])
    rstd_rot = Rot([cx.sb(st, [128, 512], F32, "rstd") for _ in range(2)])
    ps_stat = Rot([cx.banks[7]])
    emit_norm_resident(cx, X, Xtk, gcol, ctk, hT, hTtk, sq_rot, rstd_rot, ps_stat, ones_bf)
    ws = WStream(cx, st, 4096, nstage=2, nslot=3)
    pst = cx.sb(st, [128, 2, NT], F32, "pstg")
    pb = cx.sb(st, [128, 2, NT], BF16, "pbf")
    ptk = Tk()
    P.dma("sync", pst, pT_dram.rearrange("(c p) n -> p c n", p=128), writes=[ptk])
    P.op("pool", lambda e: e.tensor_copy(out=pb, in_=pst), reads=[ptk], writes=[ptk])
    wpb, wptk = ws.load([wp.rearrange("(c p) n -> p c n", p=128)])
    wp3 = wpb[:, 0:2048].rearrange("p (c n) -> p c n", c=2)
    gt_rot = Rot([cx.sb(st, [128, 512], F32, "gt") for _ in range(3)])
    psA = Rot(cx.banks[0:3])
    psB = Rot(cx.banks[3:6])
    wgv = wg.rearrange("(c p) n -> p c n", p=128)
    for half in range(2):
        wa, watk = ws.load([wgv[:, :, half * 512:(half + 1) * 512]])
        wa3 = wa.rearrange("p (c n) -> p c n", c=NC8)
        for o4 in range(4):
            oc = half * 4 + o4
            for tg in range(4):
                sl = slice(tg * 512, (tg + 1) * 512)
                ps, pstk = psA.next()
                for c in range(NC8):
                    P.op("pe", lambda e, c=c, o4=o4, sl=sl, ps=ps, wa3=wa3: e.matmul(
                        ps, lhsT=wa3[:, c, o4 * 128:(o4 + 1) * 128], rhs=hT[:, c, sl],
                        start=(c == 0), stop=(c == NC8 - 1)),
                        reads=[watk, hTtk[c][tg]], writes=[pstk], signal=(c == NC8 - 1))
                ps2, ps2tk = psB.next()
                for kc in range(2):
                    P.op("pe", lambda e, kc=kc, oc=oc, sl=sl, ps2=ps2: e.matmul(
                        ps2, lhsT=wp3[:, kc, oc * 128:(oc + 1) * 128], rhs=pb[:, kc, sl],
                        start=(kc == 0), stop=(kc == 1)),
                        reads=[wptk, ptk], writes=[ps2tk], signal=(kc == 1))
                gt, gtk = gt_rot.next()
                P.op("act", lambda e, ps=ps, gt=gt: e.activation(out=gt, in_=ps, func=AF.Sigmoid),
                     reads=[pstk], writes=[gtk])
                P.op("dve", lambda e, ps2=ps2, gt=gt: e.tensor_tensor(out=gt, in0=ps2, in1=gt, op=ALU.mult),
                     reads=[ps2tk, gtk], writes=[gtk])
                P.op("dve", lambda e, oc=oc, sl=sl, gt=gt: e.tensor_tensor(
                    out=X[:, oc, sl], in0=gt, in1=X[:, oc, sl], op=ALU.add),
                    reads=[gtk, Xtk[oc][tg]], writes=[Xtk[oc][tg]])
    P.barrier()
    cx.release(mk)


def alibi_slope(h):
    return 2.0 ** (-8.0 * (h + 1) / 16)


def sslice(start, count, step):
    return slice(start, start + (count - 1) * step + 1, step)


def emit_attention(cx, xT_ext, wqkv, wo, cf, cb, ctk, TOP, R1, lvl=9, hps=8):
    P = cx.P
    st = None
    ones_bf = cb[:, C_ONES:C_ONES + 128]
    bones = cb[:, C_BONES:C_BONES + 128]
    Dm = cf[:, C_DM:C_DM + 256]
    hT = TOP.bitcast(BF16).rearrange("p (c n) -> p c n", c=NC8)
    mk = cx.mark()
    sq_rot = Rot([cx.sb(st, [128, 512], BF16, "sq") for _ in range(2)])
    rstd_rot = Rot([cx.sb(st, [128, 512], F32, "rstd") for _ in range(2)])
    xstg = [R1[:, i * 4096:(i + 1) * 4096].rearrange("p (c n) -> p c n", c=NC8) for i in range(2)]
    xstk = [Tk(), Tk()]
    ps_stat = Rot([cx.banks[7], cx.banks[6]])
    httk = Tk()
    gcol = cf[:, G0_ANORM:G0_ANORM + 8]
    for tg in range(8 if lvl >= 1 else 0):
        xa = xstg[tg % 2]
        xt = xstk[tg % 2]
        P.dma("sync", xa, xT_ext[:, tg * 512:(tg + 1) * 512].rearrange("(c p) n -> p c n", p=128), writes=[xt])
        xs = [(xa[:, c, :], xt) for c in range(NC8)]
        ps_ap, ps_tk = ps_stat.next()
        rstd, rtk = rstd_rot.next()
        rms_stats(cx, xs, 512, sq_rot, ps_ap, ps_tk, rstd, rtk, ones_bf, ctk, 1.0 / D)
        for c in range(NC8):
            P.op("dve", lambda e, c=c, tg=tg, xa=xa, rstd=rstd: e.scalar_tensor_tensor(
                out=hT[:, c, tg * 512:(tg + 1) * 512], in0=xa[:, c, :], scalar=gcol[:, c:c + 1], in1=rstd[:, 0:512],
                op0=ALU.mult, op1=ALU.mult), reads=[xt, rtk, ctk], writes=[httk])
    P.op("dve", lambda e: e.tensor_scalar(out=cf[:, G0_QG:G0_QG + 3], in0=cf[:, G0_QG:G0_QG + 3], scalar1=0.125,
                                          scalar2=None, op0=ALU.mult), reads=[ctk], writes=[ctk])
    P.barrier()
    ACC = R1[:, 0:4096].rearrange("p (a n) -> p a n", a=2)
    oT = R1[:, 4096:12288].bitcast(BF16).rearrange("p (c n) -> p c n", c=NC8)
    acctk = Tk()
    ottk = Tk()
    ws = WStream(cx, st, 3072, nstage=1, nslot=2)
    QTz = [cx.sb(st, [128, NT], BF16, "QTz") for _ in range(2)]
    qtk = Tk()
    KT_rot = Rot([cx.sb(st, [128, 2 * NT], BF16, "KT") for _ in range(2)])
    Vz = [cx.sb(st, [128, 32, 128], BF16, "Vz") for _ in range(2)]
    vtk = Tk()
    for e2 in (0, 1):
        P.op("pool", lambda e, e2=e2: e.memset(QTz[e2], 0.0), writes=[qtk])
        P.op("pool", lambda e, e2=e2: e.memset(Vz[e2], 0.0), writes=[vtk])
    onesz = [cb[:, C_OZ:C_OZ + 128], cb[:, C_OZ + 128:C_OZ + 256]]
    honesz = [cb[:, C_HZ:C_HZ + 128], cb[:, C_HZ + 128:C_HZ + 256]]
    tmp_rot = Rot([cx.sb(st, [128, 256], F32, "stmp") for _ in range(4)])
    pt_rot = Rot([cx.sb(st, [128, 256], BF16, "PT") for _ in range(8)])
    bk = [(cx.banks[i], Tk()) for i in range(8)]

    def half(i):
        return (bk[i][0][:, 0:256], bk[i][1])

    psQ = Rot(bk[0:2])
    psS = Rot([bk[2]])
    psV = Rot([bk[3]])
    psST0 = Rot([half(4), half(0)])
    psST1 = Rot([half(5), half(1)])
    psND = Rot([half(6), half(7), half(2), half(3)])
    wq_view = wqkv.rearrange("(c p) n -> p c n", p=128)

    def perm(ap2d, d):
        if d == 1:
            return ap2d
        return ap2d.rearrange("p (u r) -> p r u", r=d)

    def proj_piece(w3, wtk, j, e0, n, gain_col, out_buf, out_tk, d, Lx, u0):
        ps, pstk = psQ.next()
        for c in range(NC8):
            P.op("pe", lambda e, c=c, ps=ps: e.matmul(ps[:, 0:n], lhsT=w3[:, j, c, :], rhs=hT[:, c, e0:e0 + n],
                                                      start=(c == 0), stop=(c == NC8 - 1)),
                 reads=[wtk], writes=[pstk], signal=(c == NC8 - 1))
        ps2, ps2tk = psS.next()
        rstd, rtk = rstd_rot.next()
        rms_stats(cx, [(ps[:, 0:n], pstk)], n, sq_rot, ps2, ps2tk, rstd, rtk, bones, ctk, 1.0 / 64)
        outs = out_buf if isinstance(out_buf, list) else [(slice(0, 128), out_buf)]
        for (rows, ob) in outs:
            if d == 1:
                o = ob[rows, u0:u0 + n]
            else:
                o = ob[rows, 0:d * Lx].rearrange("p (r u) -> p r u", r=d)[:, :, u0:u0 + n // d]
            P.op("dve", lambda e, ps=ps, rstd=rstd, o=o, rows=rows: e.scalar_tensor_tensor(
                out=o, in0=perm(ps[rows, 0:n], d), scalar=gain_col[rows, :], in1=perm(rstd[rows, 0:n], d),
                op0=ALU.mult, op1=ALU.mult),
                reads=[pstk, rtk, ctk], writes=[out_tk])

    if lvl < 2:
        hps = 0
        P.op('dve', lambda e: e.memset(R1, 0.0), writes=[ottk])
    for hp in range(hps):
        for g, (W, d) in enumerate(A_GROUPS):
            L = NT // d
            Lk = (W + NT) // d
            nb = Lk // 128
            e_start = NT - W
            base = g * 3072 + hp * 128
            wb, wtk = ws.load([wq_view[:, :, base + j * 1024: base + j * 1024 + 128] for j in range(3)])
            w3 = wb[:, 0:3072].rearrange("p (j c n) -> p j c n", j=3, c=NC8)
            KT, ktk = KT_rot.next()
            for tg in range(4):
                proj_piece(w3, wtk, 0, NT + tg * 512, 512, cf[:, G0_QG + g:G0_QG + g + 1],
                           [(slice(0, 64), QTz[0]), (slice(64, 128), QTz[1])], qtk, d, L, tg * 512 // d)
            pieces = []
            if W < 512:
                pieces.append((e_start, W))
                e = NT
            else:
                e = e_start
            while e < 2 * NT:
                pieces.append((e, 512))
                e += 512
            for (e0, n) in pieces:
                proj_piece(w3, wtk, 1, e0, n, cf[:, G0_KG + g:G0_KG + g + 1], KT, ktk, d, Lk, (e0 - e_start) // d)
            nkb = d * nb if lvl >= 3 else 0
            kb = 0
            while kb < nkb:
                nblk = min(4, nkb - kb)
                psv, psvtk = psV.next()
                for b in range(nblk):
                    r, jb = divmod(kb + b, nb)
                    e_first = e_start + d * 128 * jb + r
                    for c in range(NC8):
                        P.op("pe", lambda e, c=c, b=b, e_first=e_first, psv=psv: e.matmul(
                            psv[:, b * 128:(b + 1) * 128], lhsT=hT[:, c, sslice(e_first, 128, d)], rhs=w3[:, 2, c, :],
                            start=(c == 0), stop=(c == NC8 - 1)),
                            reads=[wtk], writes=[psvtk], signal=(c == NC8 - 1 and b == nblk - 1))
                for e2 in (0, 1):
                    cs = slice(64 * e2, 64 * e2 + 64)
                    P.op("act", lambda e, kb=kb, nblk=nblk, psv=psv, e2=e2, cs=cs: e.activation(
                        out=Vz[e2][:, kb:kb + nblk, cs],
                        in_=psv[:, 0:nblk * 128].rearrange("p (b n) -> p b n", b=nblk)[:, :, cs], func=AF.Copy),
                        reads=[psvtk], writes=[vtk])
                kb += nblk
            for r in range(d if lvl >= 4 else 0):
                PTs = {}
                for jb in range(nb):
                    lo = 128 if jb == 0 else 0
                    hi = 128 if jb == nb - 1 else 256
                    qb0 = jb if jb == 0 else jb - 1
                    q_off = r * L + 128 * qb0
                    sTs = [psST0.next(), psST1.next()]
                    for e2 in (0, 1):
                        sT, sTtk = sTs[e2]
                        P.op("pe", lambda e, sT=sT, e2=e2, jb=jb, lo=lo, hi=hi, q_off=q_off, r=r: e.matmul(
                            sT[:, lo:hi], lhsT=KT[:, r * Lk + 128 * jb: r * Lk + 128 * jb + 128],
                            rhs=QTz[e2][:, q_off:q_off + (hi - lo)], start=True, stop=True),
                            reads=[ktk, qtk], writes=[sTtk])
                    for e2 in (0, 1):
                        sig = alibi_slope(2 * hp + e2) * d
                        sT, sTtk = sTs[e2]
                        tmp, tmtk = tmp_rot.next()
                        P.op("dve", lambda e, sT=sT, tmp=tmp, lo=lo, hi=hi, sig=sig: e.scalar_tensor_tensor(
                            out=tmp[:, lo:hi], in0=Dm[:, lo:hi], scalar=-sig, in1=sT[:, lo:hi],
                            op0=ALU.mult, op1=ALU.add),
                            reads=[sTtk, ctk], writes=[tmtk])
                        pt, pttk = pt_rot.next()
                        P.op("act", lambda e, tmp=tmp, pt=pt, lo=lo, hi=hi: e.activation(
                            out=pt[:, lo:hi], in_=tmp[:, lo:hi], func=AF.Exp), reads=[tmtk], writes=[pttk])
                        PTs[(e2, jb)] = (pt, pttk)
                    if jb >= 1:
                        j = jb - 1
                        nd, ndtk = psND.next()
                        kbp = r * nb + j
                        kbd = r * nb + jb
                        for part in (0, 1):
                            co = slice(128 * part, 128 * part + 128)
                            for e2 in (0, 1):
                                ptp, ptptk = PTs[(e2, j)]
                                ptd, ptdtk = PTs[(e2, jb)]
                                if part == 0:
                                    lp, ld = Vz[e2][:, kbp, :], Vz[e2][:, kbd, :]
                                else:
                                    lp, ld = (honesz[e2] if j == 0 else onesz[e2]), onesz[e2]
                                P.op("pe", lambda e, nd=nd, co=co, lp=lp, ptp=ptp, e2=e2: e.matmul(
                                    nd[:, co], lhsT=lp, rhs=ptp[:, 128:256], start=(e2 == 0), stop=False),
                                    reads=[vtk, ctk, ptptk], writes=[ndtk], signal=False)
                                P.op("pe", lambda e, nd=nd, co=co, ld=ld, ptd=ptd, e2=e2: e.matmul(
                                    nd[:, co], lhsT=ld, rhs=ptd[:, 0:128], start=False, stop=(e2 == 1)),
                                    reads=[vtk, ctk, ptdtk], writes=[ndtk], signal=(part == 1 and e2 == 1))
                        t0 = r + d * 128 * j
                        accv = ACC[:, :, sslice(t0, 128, d)]
                        ndv = nd.rearrange("p (a n) -> p a n", a=2)
                        if g == 0:
                            P.op("act", lambda e, accv=accv, ndv=ndv: e.activation(out=accv, in_=ndv, func=AF.Copy),
                                 reads=[ndtk], writes=[acctk])
                        else:
                            P.op("dve", lambda e, accv=accv, ndv=ndv: e.tensor_tensor(out=accv, in0=ndv, in1=accv, op=ALU.add),
                                 reads=[ndtk, acctk], writes=[acctk])
        P.op("dve", lambda e: e.reciprocal(out=ACC[:, 1, :], in_=ACC[:, 1, :]), reads=[acctk], writes=[acctk])
        P.op("dve", lambda e, hp=hp: e.tensor_tensor(out=oT[:, hp, :], in0=ACC[:, 0, :], in1=ACC[:, 1, :], op=ALU.mult),
             reads=[acctk], writes=[ottk])
    P.barrier()
    cx.release(mk)
    mk = cx.mark()
    X = TOP.rearrange("p (c n) -> p c n", c=NC8)
    Xtk = [[Tk() for _ in range(4)] for _ in range(NC8)]
    for c in range(NC8):
        for tg in range(4):
            P.dma("sync", X[:, c, tg * 512:(tg + 1) * 512], xT_ext[c * 128:(c + 1) * 128, NT + tg * 512:NT + (tg + 1) * 512],
                  writes=[Xtk[c][tg]])
    ws2 = WStream(cx, st, 4096, nstage=2, nslot=2)
    wov = wo.rearrange("(c p) n -> p c n", p=128)
    psA = Rot(cx.banks[0:4])
    for half in range(2):
        wa, watk = ws2.load([wov[:, :, half * 512:(half + 1) * 512]])
        wa3 = wa.rearrange("p (c n) -> p c n", c=NC8)
        for o4 in range(4):
            oc = half * 4 + o4
            for tg in range(4):
                sl = slice(tg * 512, (tg + 1) * 512)
                ps, pstk = psA.next()
                for c in range(NC8):
                    P.op("pe", lambda e, c=c, o4=o4, sl=sl, ps=ps, wa3=wa3: e.matmul(
                        ps, lhsT=wa3[:, c, o4 * 128:(o4 + 1) * 128], rhs=oT[:, c, sl],
                        start=(c == 0), stop=(c == NC8 - 1)),
                        reads=[watk, ottk], writes=[pstk], signal=(c == NC8 - 1))
                P.op("dve", lambda e, oc=oc, sl=sl, ps=ps: e.tensor_tensor(
                    out=X[:, oc, sl], in0=ps, in1=X[:, oc, sl], op=ALU.add),
                    reads=[pstk, Xtk[oc][tg]], writes=[Xtk[oc][tg]])
    P.barrier()
    cx.release(mk)
    return X, Xtk


def emit_store(cx, X, Xtk, out_dram):
    P = cx.P
    for c in range(NC8):
        P.dma("sync", out_dram[c * 128:(c + 1) * 128, :], X[:, c, :], reads=Xtk[c])


def build_layer0(debug=False, lvl=9, hps=8):
    nc = bass.Bass("TRN2", target_bir_lowering=False)
    xT_ext = nc.dram_tensor("xT_ext", [D, 2 * NT], F32, kind="ExternalInput").ap()
    pT = nc.dram_tensor("pT", [256, NT], F32, kind="ExternalInput").ap()
    cst = nc.dram_tensor("cst", [128, G0_END], F32, kind="ExternalInput").ap()
    wqkv = nc.dram_tensor("a_w_qkv", [D, 9216], F32, kind="ExternalInput").ap()
    wo = nc.dram_tensor("a_w_o", [D, D], F32, kind="ExternalInput").ap()
    w1 = nc.dram_tensor("mlp_w1", [D, 4096], F32, kind="ExternalInput").ap()
    w2 = nc.dram_tensor("mlp_w2", [4096, D], F32, kind="ExternalInput").ap()
    wg = nc.dram_tensor("ple_w_gate", [D, D], F32, kind="ExternalInput").ap()
    wp = nc.dram_tensor("ple_w_proj", [256, D], F32, kind="ExternalInput").ap()
    out = nc.dram_tensor("xout", [D, NT], F32, kind="ExternalOutput").ap()
    if debug:
        dbg_a = nc.dram_tensor("dbg_a", [D, NT], F32, kind="ExternalOutput").ap()
        dbg_m = nc.dram_tensor("dbg_m", [D, NT], F32, kind="ExternalOutput").ap()
    cx = Ctx(nc)
    P = cx.P
    cf, cb, ctk = load_consts(cx, None, cst, G0_END)
    cx.eps_col = cf[:, C_EPS:C_EPS + 1]
    ones_bf = cb[:, C_ONES:C_ONES + 128]
    TOP = cx.sb(None, [128, 16384], F32, "TOP")
    R1 = cx.sb(None, [128, 12288], F32, "R1")
    mk = cx.mark()
    X, Xtk = emit_attention(cx, xT_ext, wqkv, wo, cf, cb, ctk, TOP, R1, lvl=lvl, hps=hps)
    cx.release(mk)
    cx.top = cx.top - 12288 * 4
    if debug:
        emit_store(cx, X, Xtk, dbg_a)
    emit_mlp(cx, X, Xtk, cf[:, G0_MLPN:G0_MLPN + 8], ctk, w1, w2, ones_bf)
    if debug:
        emit_store(cx, X, Xtk, dbg_m)
    emit_ple(cx, X, Xtk, cf[:, G0_PLEN:G0_PLEN + 8], ctk, wg, wp, pT, ones_bf)
    emit_store(cx, X, Xtk, out)
    P.finish()
    return nc, cx


def layer0_inputs(inputs, core):
    x = inputs["x"][0]
    lo = core * NT
    xe = np.zeros((2 * NT, D), np.float32)
    if core > 0:
        xe[:NT] = x[lo - NT:lo]
    xe[NT:] = x[lo:lo + NT]
    c = base_consts(core, G0_END)
    c[:, G0_ANORM:G0_ANORM + 8] = col_layout(inputs["a_norm"][0])
    c[:, G0_QG:G0_QG + 3] = np.tile(inputs["a_q_gain"][0].T, (2, 1))
    c[:, G0_KG:G0_KG + 3] = np.tile(inputs["a_k_gain"][0].T, (2, 1))
    c[:, G0_MLPN:G0_MLPN + 8] = col_layout(inputs["mlp_norm"][0])
    c[:, G0_PLEN:G0_PLEN + 8] = col_layout(inputs["ple_norm"][0])
    return {
        "xT_ext": np.ascontiguousarray(xe.T),
        "pT": np.ascontiguousarray(inputs["p"][0, 0, lo:lo + NT].T),
        "cst": c,
        "a_w_qkv": inputs["a_w_qkv"][0], "a_w_o": inputs["a_w_o"][0],
        "mlp_w1": inputs["mlp_w1"][0], "mlp_w2": inputs["mlp_w2"][0],
        "ple_w_gate": inputs["ple_w_gate"][0], "ple_w_proj": inputs["ple_w_proj"][0],
    }

BH = 4
DH = 512
NCK = NT // 128
KSCALE = DH ** -0.5
NST = 8192 + 2048 + 8

L_BNORM = C_GAINS
L_MLPN = L_BNORM + 8
L_PLEN = L_MLPN + 8
L_CONVW = L_PLEN + 8
L_CONVB = L_CONVW + 64
L_SKIP = L_CONVB + 16
L_HGAIN = L_SKIP + 16
L_BI = L_HGAIN + 16
L_BF = L_BI + 1
L_MASKLOW = L_BF + 1
L_SEL = L_MASKLOW + 128
L_CMASK = L_SEL + 512
L_NEG = L_CMASK + 7
L_E0 = L_NEG + 1
L_LNK = L_E0 + 1
L_CNEG = L_LNK + 1
L_END = L_CNEG + 7


def layer1_consts(inputs, core):
    c = base_consts(core, L_END)
    c[:, L_BNORM:L_BNORM + 8] = col_layout(inputs["b_norm"][0])
    c[:, L_MLPN:L_MLPN + 8] = col_layout(inputs["mlp_norm"][1])
    c[:, L_PLEN:L_PLEN + 8] = col_layout(inputs["ple_norm"][1])
    cw = inputs["b_conv_w"][0]
    c[:, L_CONVW:L_CONVW + 64] = cw.reshape(4, 16, 128).transpose(2, 1, 0).reshape(128, 64)
    c[:, L_CONVB:L_CONVB + 16] = col_layout(inputs["b_conv_b"][0])
    c[:, L_SKIP:L_SKIP + 16] = col_layout(inputs["b_skip"][0])
    c[:, L_HGAIN:L_HGAIN + 16] = col_layout(inputs["b_h_gain"][0])
    bg = inputs["b_b_gate"][0]
    c[0:4, L_BI] = bg[0:4]
    c[0:4, L_BF] = bg[4:8]
    s_ = np.arange(128)[:, None]
    t_ = np.arange(128)[None, :]
    c[:, L_MASKLOW:L_MASKLOW + 128] = np.where(s_ <= t_, 0.0, BIG)
    for hd in range(4):
        c[hd, L_SEL + hd * 128:L_SEL + (hd + 1) * 128] = 1.0
    for cp in range(7):
        c[:, L_CMASK + cp] = 1.0 if cp < core else 0.0
        c[:, L_CNEG + cp] = 0.0 if cp < core else -1e30
    c[:, L_NEG] = -1e30
    c[0, L_E0] = 1.0
    c[:, L_LNK] = np.log(KSCALE)
    return c


def bd_compact(w, transpose=False):
    out = np.zeros((2048, 128), np.float32)
    n = np.arange(512)
    for j in range(4):
        for k in range(4):
            if transpose:
                out[4 * n + k, (4 * n + j) % 128] = w[:, j, k]
            else:
                out[4 * n + j, (4 * n + k) % 128] = w[:, j, k]
    return out


def layer1_inputs(inputs, core, x1T_full, stage, st_all=None, g_in=None):
    lo = core * NT
    xh = np.zeros((D, 4), np.float32)
    if core > 0:
        xh[:, 1:4] = x1T_full[:, lo - 3:lo]
    m = {
        "x1T": np.ascontiguousarray(x1T_full[:, lo:lo + NT]),
        "xh": xh,
        "cst": layer1_consts(inputs, core),
        "b_w_up": inputs["b_w_up"][0],
        "bd": np.stack([bd_compact(inputs["b_w_q"][0]), bd_compact(inputs["b_w_k"][0]), bd_compact(inputs["b_w_v"][0])]),
        "bdT": np.stack([bd_compact(inputs["b_w_q"][0], True), bd_compact(inputs["b_w_k"][0], True),
                         bd_compact(inputs["b_w_v"][0], True)]),
        "w_gate": inputs["b_w_gate"][0],
    }
    if stage == "C":
        m.update({
            "st_all": st_all,
            "g_in": g_in,
            "b_w_down": inputs["b_w_down"][0],
            "pT": np.ascontiguousarray(inputs["p"][1, 0, lo:lo + NT].T),
            "mlp_w1": inputs["mlp_w1"][1], "mlp_w2": inputs["mlp_w2"][1],
            "ple_w_gate": inputs["ple_w_gate"][1], "ple_w_proj": inputs["ple_w_proj"][1],
        })
    return m


def build_layer1(stage, debug=False, dbg_stop=None):
    nc = bass.Bass("TRN2", target_bir_lowering=False)
    x1T = nc.dram_tensor("x1T", [D, NT], F32, kind="ExternalInput").ap()
    xh = nc.dram_tensor("xh", [D, 4], F32, kind="ExternalInput").ap()
    cst = nc.dram_tensor("cst", [128, L_END], F32, kind="ExternalInput").ap()
    wup = nc.dram_tensor("b_w_up", [D, 4096], F32, kind="ExternalInput").ap()
    bd = nc.dram_tensor("bd", [3, 2048, 128], F32, kind="ExternalInput").ap()
    bdT = nc.dram_tensor("bdT", [3, 2048, 128], F32, kind="ExternalInput").ap()
    wgate = nc.dram_tensor("w_gate", [6144, 8], F32, kind="ExternalInput").ap()
    if stage == "B":
        st_out = nc.dram_tensor("st_out", [128, NST], F32, kind="ExternalOutput").ap()
        g_out = nc.dram_tensor("g_out", [8, NT], F32, kind="ExternalOutput").ap()
    else:
        st_all = nc.dram_tensor("st_all", [7, 128, NST], F32, kind="ExternalInput").ap()
        g_in = nc.dram_tensor("g_in", [8, NT], F32, kind="ExternalInput").ap()
        wdown = nc.dram_tensor("b_w_down", [2048, D], F32, kind="ExternalInput").ap()
        pT = nc.dram_tensor("pT", [256, NT], F32, kind="ExternalInput").ap()
        w1 = nc.dram_tensor("mlp_w1", [D, 4096], F32, kind="ExternalInput").ap()
        w2 = nc.dram_tensor("mlp_w2", [4096, D], F32, kind="ExternalInput").ap()
        wg = nc.dram_tensor("ple_w_gate", [D, D], F32, kind="ExternalInput").ap()
        wp = nc.dram_tensor("ple_w_proj", [256, D], F32, kind="ExternalInput").ap()
        out = nc.dram_tensor("xout", [D, NT], F32, kind="ExternalOutput").ap()
        yscr = nc.dram_tensor("yscr", [2048, NT], BF16).ap()
        if debug:
            dbg_a = nc.dram_tensor("dbg_a", [D, NT], F32, kind="ExternalOutput").ap()
    cx = Ctx(nc)
    P = cx.P
    cf, cb, ctk = load_consts(cx, None, cst, L_END)
    cx.eps_col = cf[:, C_EPS:C_EPS + 1]
    ones_bf = cb[:, C_ONES:C_ONES + 128]
    ones_f = cf[:, C_ONES:C_ONES + 128]
    ident_f = cf[:, C_ID:C_ID + 128]
    one_col = cf[:, C_ONES:C_ONES + 1]
    TOP = cx.sb(None, [128, 16384], F32, "TOP")
    hT = TOP[:, 0:8208].bitcast(BF16)[:, 0:8 * 2052].rearrange("p (c n) -> p c n", c=NC8)
    topfree = TOP[:, 8208:16384]
    base_mark = cx.mark()
    bk = [(cx.banks[i], Tk()) for i in range(8)]
    ws = WStream(cx, None, 4096, nstage=0, nslot=3)
    ws.stage = Rot([topfree[:, 0:4096]])
    bdh = cx.sb(None, [128, 3, 4, 128], BF16, "bdh")
    bdh_st = cx.sb(None, [128, 3, 4, 128], F32, "bdh_st")
    diag = cx.sb(None, [128, 4, 4, 128], BF16, "diag")
    xms = [cx.sb(None, [128, 4, 516], BF16, "xm") for _ in range(2)]
    xc = cx.sb(None, [128, 4, 512], BF16, "xc")
    GF = cx.sb(None, [128, NT], F32, "GF")
    BETAx = cx.sb(None, [128, NT + 1], F32, "BETAx")
    small = cx.sb(None, [128, 64], F32, "small")
    TMw = cx.sb(None, [128, NCK, 4], F32, "TMw")
    TMa = cx.sb(None, [128, NCK, 4], F32, "TMa")
    if stage == "C":
        fold = cx.sb(None, [128, 7, 8], F32, "foldin")
        S1 = cx.sb(None, [128, 7, 4], F32, "S1")
        S2 = cx.sb(None, [128, 7, 4], F32, "S2")
        mrun = cx.sb(None, [128, 4], F32, "mrun")
        fa = cx.sb(None, [128, 4], F32, "fa")
        fb = cx.sb(None, [128, 4], F32, "fb")
        fc_ = cx.sb(None, [128, 4], F32, "fc")
    pers_mark = cx.mark()

    mk = cx.mark()
    sq_rot = Rot([cx.sb(None, [128, 512], BF16, "sq") for _ in range(2)])
    rstd_rot = Rot([cx.sb(None, [128, 512], F32, "rstd") for _ in range(2)])
    xstg = [cx.sb(None, [128, NC8, 512], F32, "xstg") for _ in range(2)]
    xstk = [Tk(), Tk()]
    ps_stat = Rot([bk[7], bk[6]])
    gcol = cf[:, L_BNORM:L_BNORM + 8]
    httk = Tk()
    pieces = [(None, 4)] + [(tg, 512) for tg in range(4)]
    for i, (tg, n) in enumerate(pieces):
        xa = xstg[i % 2]
        xt = xstk[i % 2]
        if tg is None:
            P.dma("sync", xa[:, :, 0:4], xh.rearrange("(c p) n -> p c n", p=128), writes=[xt])
            h0 = 0
        else:
            P.dma("sync", xa, x1T[:, tg * 512:(tg + 1) * 512].rearrange("(c p) n -> p c n", p=128), writes=[xt])
            h0 = 4 + tg * 512
        xs = [(xa[:, c, 0:n], xt) for c in range(NC8)]
        ps_ap, ps_tk = ps_stat.next()
        rstd, rtk = rstd_rot.next()
        rms_stats(cx, xs, n, sq_rot, ps_ap, ps_tk, rstd, rtk, ones_bf, ctk, 1.0 / D)
        for c in range(NC8):
            P.op("dve", lambda e, c=c, xa=xa, rstd=rstd, n=n, h0=h0: e.scalar_tensor_tensor(
                out=hT[:, c, h0:h0 + n], in0=xa[:, c, 0:n], scalar=gcol[:, c:c + 1], in1=rstd[:, 0:n],
                op0=ALU.mult, op1=ALU.mult), reads=[xt, rtk, ctk], writes=[httk])
    P.barrier()
    cx.release(mk)

    GI = cx.sb(None, [128, NT], F32, "GI")
    LF = cx.sb(None, [128, NT], F32, "LF")
    BB = cx.sb(None, [128, NT], F32, "BB")
    T1 = cx.sb(None, [128, NT], F32, "T1")
    wfold = [[cx.sb(None, [128, 16, 128], BF16, "wfold") for _ in range(2)] for _ in range(2)]
    wftk = Tk()
    bdT_sb = topfree[:, 0:6144].rearrange("p (a b) -> p a b", a=48)
    wg_sb = topfree[:, 6144:6528].rearrange("p (a b) -> p a b", a=48)
    btk = Tk()
    for j in range(3 if stage == "B" else 0):
        P.dma("sync", bdT_sb[:, j * 16:(j + 1) * 16, :], bdT[j].rearrange("(c p) n -> p c n", p=128), writes=[btk])
    if stage == "B":
        P.dma("sync", wg_sb, wgate.rearrange("(c p) n -> p c n", p=128), writes=[btk])
    zpad = Rot([topfree[:, 6528 + i * 128:6528 + (i + 1) * 128] for i in range(4)])
    for (za, ztk) in zpad.items:
        P.op("pool", lambda e, za=za: e.memset(za, 0.0), writes=[ztk])
    psF = Rot(bk[0:2])
    for mc in range(16 if stage == "B" else 0):
        for part in range(2):
            for xm_ in range(2):
                ps, pstk = psF.next()
                srcs = (0, 1) if xm_ == 0 else (2,)
                for si, j in enumerate(srcs):
                    za, ztk = zpad.next()
                    P.op("dve", lambda e, za=za, j=j, mc=mc, part=part: e.tensor_copy(
                        out=za[:, 0:4], in_=wg_sb[:, j * 16 + mc, part * 4:part * 4 + 4]), reads=[btk], writes=[ztk])
                    P.op("pe", lambda e, ps=ps, za=za, j=j, mc=mc, si=si, srcs=srcs: e.matmul(
                        ps[:, 0:128], lhsT=bdT_sb[:, j * 16 + mc, :], rhs=za, start=(si == 0), stop=(si == len(srcs) - 1)),
                        reads=[btk, ztk], writes=[pstk])
                P.op("act", lambda e, ps=ps, xm_=xm_, part=part, mc=mc: e.activation(
                    out=wfold[xm_][part][:, mc, :], in_=ps[:, 0:128], func=AF.Copy), reads=[pstk], writes=[wftk])
    P.barrier()

    hdtk = Tk()
    xmtk = [Tk(), Tk()]
    xctk = Tk()
    psA = Rot(bk[0:2])
    wupv = wup.rearrange("(c p) n -> p c n", p=128)
    bdv = bd.rearrange("j (c p) n -> p j c n", p=128)
    state = {"i": 0}

    def head_setup(hd):
        for j in range(3):
            P.dma("sync", bdh_st[:, j, :, :], bdv[:, j, hd * 4:(hd + 1) * 4, :], writes=[hdtk])
        P.op("pool", lambda e: e.tensor_copy(out=bdh, in_=bdh_st), reads=[hdtk], writes=[hdtk])
        for mc in range(4):
            for k in range(4):
                col = L_CONVW + (hd * 4 + mc) * 4 + k
                P.op("pool", lambda e, mc=mc, k=k, col=col: e.tensor_scalar(
                    out=diag[:, mc, k, :], in0=ident_f, scalar1=cf[:, col:col + 1], scalar2=None, op0=ALU.mult),
                    reads=[ctk], writes=[hdtk])
        wx, wxtk = ws.load([wupv[:, :, hd * 512:(hd + 1) * 512]])
        return wx.rearrange("p (c n) -> p c n", c=NC8), wxtk

    def front(hd, tg, wx3, wxtk):
        i = state["i"]
        state["i"] += 1
        xm, xmt = xms[i % 2], xmtk[i % 2]
        xmp, xmpt = xms[(i + 1) % 2], xmtk[(i + 1) % 2]
        for mc in range(4):
            ps, pstk = psA.next()
            for c in range(NC8):
                P.op("pe", lambda e, c=c, mc=mc, ps=ps: e.matmul(
                    ps, lhsT=wx3[:, c, mc * 128:(mc + 1) * 128], rhs=hT[:, c, 4 + tg * 512:4 + (tg + 1) * 512],
                    start=(c == 0), stop=(c == NC8 - 1)), reads=[wxtk], writes=[pstk], signal=(c == NC8 - 1))
            P.op("act", lambda e, ps=ps, mc=mc, xm=xm: e.activation(out=xm[:, mc, 4:516], in_=ps, func=AF.Copy),
                 reads=[pstk], writes=[xmt])
            if tg == 0:
                ps, pstk = psA.next()
                for c in range(NC8):
                    P.op("pe", lambda e, c=c, mc=mc, ps=ps: e.matmul(
                        ps[:, 0:4], lhsT=wx3[:, c, mc * 128:(mc + 1) * 128], rhs=hT[:, c, 0:4],
                        start=(c == 0), stop=(c == NC8 - 1)), reads=[wxtk], writes=[pstk], signal=(c == NC8 - 1))
                P.op("act", lambda e, ps=ps, mc=mc, xm=xm: e.activation(out=xm[:, mc, 0:4], in_=ps[:, 0:4], func=AF.Copy),
                     reads=[pstk], writes=[xmt])
        if tg > 0:
            P.op("pool", lambda e, xm=xm, xmp=xmp: e.tensor_copy(out=xm[:, :, 0:4], in_=xmp[:, :, 512:516]),
                 reads=[xmpt], writes=[xmt])
        for mc in range(4):
            ps, pstk = psA.next()
            for k in range(4):
                P.op("pe", lambda e, k=k, mc=mc, ps=ps, xm=xm: e.matmul(
                    ps, lhsT=diag[:, mc, k, :], rhs=xm[:, mc, 1 + k:1 + k + 512], start=(k == 0), stop=(k == 3)),
                    reads=[hdtk, xmt], writes=[pstk], signal=(k == 3))
            col = L_CONVB + hd * 4 + mc
            P.op("act", lambda e, ps=ps, mc=mc, col=col: e.activation(
                out=xc[:, mc, :], in_=ps, func=AF.Silu, bias=cf[:, col:col + 1]), reads=[pstk, ctk], writes=[xctk])
        return xm, xmt

    gtk = Tk()
    psG = Rot(bk[2:4])
    if stage == "C":
        P.op("pool", lambda e: e.memset(GI, 0.0), writes=[gtk])
        P.op("pool", lambda e: e.memset(GF, 0.0), writes=[gtk])
        P.dma("sync", GI[0:4, :], g_in[0:4, :], writes=[gtk])
        P.dma("sync", GF[0:4, :], g_in[4:8, :], writes=[gtk])
    for hd in range(BH if stage == "B" else 0):
        wx3, wxtk = head_setup(hd)
        for tg in range(4):
            xm, xmt = front(hd, tg, wx3, wxtk)
            for part, Grow in ((0, GI), (1, GF)):
                ps, pstk = psG.next()
                for mc in range(4):
                    P.op("pe", lambda e, ps=ps, mc=mc, part=part: e.matmul(
                        ps, lhsT=wfold[0][part][:, hd * 4 + mc, :], rhs=xc[:, mc, :], start=(mc == 0), stop=False),
                        reads=[wftk, xctk], writes=[pstk], signal=False)
                    P.op("pe", lambda e, ps=ps, mc=mc, part=part, xm=xm: e.matmul(
                        ps, lhsT=wfold[1][part][:, hd * 4 + mc, :], rhs=xm[:, mc, 4:516], start=False, stop=(mc == 3)),
                        reads=[wftk, xmt], writes=[pstk], signal=(mc == 3))
                sl = slice(tg * 512, (tg + 1) * 512)
                if hd == 0:
                    P.op("act", lambda e, ps=ps, Grow=Grow, sl=sl: e.activation(out=Grow[:, sl], in_=ps, func=AF.Copy),
                         reads=[pstk], writes=[gtk])
                else:
                    P.op("dve", lambda e, ps=ps, Grow=Grow, sl=sl: e.tensor_tensor(out=Grow[:, sl], in0=ps, in1=Grow[:, sl], op=ALU.add),
                         reads=[pstk, gtk], writes=[gtk])

    rtk = Tk()
    if stage == "B":
        P.dma("sync", g_out[0:4, :], GI[0:4, :], reads=[gtk])
        P.dma("sync", g_out[4:8, :], GF[0:4, :], reads=[gtk])
    P.op("dve", lambda e: e.tensor_scalar(out=GI, in0=GI, scalar1=cf[:, L_BI:L_BI + 1], scalar2=None, op0=ALU.add),
         reads=[gtk, ctk], writes=[gtk])
    P.op("dve", lambda e: e.tensor_scalar(out=GF, in0=GF, scalar1=cf[:, L_BF:L_BF + 1], scalar2=None, op0=ALU.add),
         reads=[gtk, ctk], writes=[gtk])
    P.op("dve", lambda e: e.tensor_scalar(out=T1, in0=GF, scalar1=-1.0, scalar2=None, op0=ALU.mult), reads=[gtk], writes=[rtk])
    P.op("dve", lambda e: e.tensor_tensor(out=T1, in0=T1, in1=GF, op=ALU.max), reads=[gtk, rtk], writes=[rtk])
    P.op("act", lambda e: e.activation(out=T1, in_=T1, func=AF.Exp, scale=-1.0), reads=[rtk], writes=[rtk])
    P.op("act", lambda e: e.activation(out=T1, in_=T1, func=AF.Ln, bias=one_col), reads=[rtk, ctk], writes=[rtk])
    P.op("dve", lambda e: e.scalar_tensor_tensor(out=LF, in0=GF, scalar=0.0, in1=T1, op0=ALU.min, op1=ALU.subtract),
         reads=[gtk, rtk], writes=[rtk])
    P.op("pool", lambda e: e.memset(T1, 1.0), reads=[rtk], writes=[rtk])
    P.op("dve", lambda e: e.tensor_tensor_scan(out=BB, data0=T1, data1=LF, initial=0.0, op0=ALU.mult, op1=ALU.add),
         reads=[rtk], writes=[rtk])
    P.op("dve", lambda e: e.tensor_tensor(out=T1, in0=GI, in1=BB, op=ALU.subtract), reads=[gtk, rtk], writes=[rtk])
    ALPHA = T1
    psR = Rot([bk[4]])
    tmtk = Tk()

    def to_token_major(row, dst):
        ps, pstk = psR.next()
        for ck in range(NCK):
            P.op("pe", lambda e, ps=ps, ck=ck: e.matmul(ps[:, ck * 4:ck * 4 + 4], lhsT=row[:, ck * 128:(ck + 1) * 128],
                                                        rhs=ident_f[:, 0:4], start=True, stop=True),
                 reads=[rtk, gtk, ctk], writes=[pstk], signal=(ck == NCK - 1))
        P.op("act", lambda e, ps=ps: e.activation(out=dst, in_=ps[:, 0:64].rearrange("p (a b) -> p a b", a=NCK), func=AF.Copy),
             reads=[pstk], writes=[tmtk])

    def replicate_cols(col_ap, dst4):
        ps, pstk = psR.next()
        za = small[:, 32:32 + 4]
        P.op("dve", lambda e: e.tensor_scalar(out=za, in0=ident_f[:, 0:4], scalar1=col_ap, scalar2=None, op0=ALU.mult),
             reads=[rtk, ctk, gtk], writes=[rtk])
        P.op("pe", lambda e, ps=ps: e.matmul(ps[:, 0:4], lhsT=ones_f, rhs=za, start=True, stop=True),
             reads=[rtk, ctk], writes=[pstk])
        P.op("act", lambda e, ps=ps: e.activation(out=dst4, in_=ps[:, 0:4], func=AF.Copy), reads=[pstk], writes=[rtk])

    if stage == "B":
        mx = small[:, 0:1]
        P.op("dve", lambda e: e.tensor_reduce(out=mx, in_=ALPHA, axis=AX.X, op=ALU.max), reads=[rtk], writes=[rtk])
        nb_ = small[:, 1:2]
        P.op("dve", lambda e: e.scalar_tensor_tensor(out=nb_, in0=mx, scalar=-1.0, in1=cf[:, L_LNK:L_LNK + 1],
                                                     op0=ALU.mult, op1=ALU.add), reads=[rtk, ctk], writes=[rtk])
        P.op("act", lambda e: e.activation(out=LF, in_=ALPHA, func=AF.Exp, bias=nb_), reads=[rtk], writes=[rtk])
        to_token_major(LF, TMw)
        ml = small[:, 2:3]
        P.op("dve", lambda e: e.tensor_tensor(out=ml, in0=mx, in1=BB[:, NT - 1:NT], op=ALU.add), reads=[rtk], writes=[rtk])
        fin = cx.sb(None, [128, 8], F32, "fin")
        replicate_cols(BB[:, NT - 1:NT], fin[:, 0:4])
        replicate_cols(ml, fin[:, 4:8])
        P.dma("sync", st_out[:, 10240:10248], fin, reads=[rtk])
        P.barrier()
        cx.release(pers_mark)
        kv_rot = Rot([cx.sb(None, [128, 512], BF16, "kv") for _ in range(4)])
        stC = cx.sb(None, [128, 4, 512], F32, "stC")
        stn = cx.sb(None, [128, 512], F32, "stn")
        sttk = Tk()
        psKV = Rot([bk[2]])
        for hd in range(BH):
            wx3, wxtk = head_setup(hd)
            cacc = [bk[3 + dc] for dc in range(4)]
            nacc, nacctk = bk[7]
            for tg in range(4):
                xm, xmt = front(hd, tg, wx3, wxtk)
                for cl in range(4):
                    ck = tg * 4 + cl
                    tsl = slice(cl * 128, (cl + 1) * 128)
                    ps, pstk = psKV.next()
                    for mc in range(4):
                        P.op("pe", lambda e, ps=ps, mc=mc, tsl=tsl: e.matmul(
                            ps[:, mc * 128:(mc + 1) * 128], lhsT=xc[:, mc, tsl], rhs=bdh[:, 1, mc, :], start=True, stop=True),
                            reads=[xctk, hdtk], writes=[pstk], signal=(mc == 3))
                    wk, wktk = kv_rot.next()
                    P.op("act", lambda e, ps=ps, wk=wk, ck=ck, hd=hd: e.activation(
                        out=wk, in_=ps, func=AF.Copy, scale=TMw[:, ck, hd:hd + 1]), reads=[pstk, tmtk], writes=[wktk])
                    ps, pstk = psKV.next()
                    for mc in range(4):
                        P.op("pe", lambda e, ps=ps, mc=mc, cl=cl, xm=xm: e.matmul(
                            ps[:, mc * 128:(mc + 1) * 128], lhsT=xm[:, mc, 4 + cl * 128:4 + (cl + 1) * 128], rhs=bdh[:, 2, mc, :],
                            start=True, stop=True), reads=[xmt, hdtk], writes=[pstk], signal=(mc == 3))
                    vv, vtk = kv_rot.next()
                    P.op("act", lambda e, ps=ps, vv=vv: e.activation(out=vv, in_=ps, func=AF.Copy), reads=[pstk], writes=[vtk])
                    last = (ck == NCK - 1)
                    for dc in range(4):
                        P.op("pe", lambda e, dc=dc, wk=wk, vv=vv, ck=ck, last=last: e.matmul(
                            cacc[dc][0], lhsT=wk[:, dc * 128:(dc + 1) * 128], rhs=vv, start=(ck == 0), stop=last),
                            reads=[wktk, vtk], writes=[cacc[dc][1]], signal=True)
                    P.op("pe", lambda e, wk=wk, ck=ck, last=last: e.matmul(
                        nacc, lhsT=ones_bf, rhs=wk, start=(ck == 0), stop=last), reads=[wktk, ctk], writes=[nacctk], signal=True)
            for dc in range(4):
                P.op("act", lambda e, dc=dc: e.activation(out=stC[:, dc, :], in_=cacc[dc][0], func=AF.Copy),
                     reads=[cacc[dc][1]], writes=[sttk])
            P.op("dve", lambda e: e.tensor_copy(out=stn, in_=nacc), reads=[nacctk], writes=[sttk])
            P.dma("sync", st_out[:, hd * 2048:(hd + 1) * 2048], stC.rearrange("p a b -> p (a b)"), reads=[sttk])
            P.dma("sync", st_out[:, 8192 + hd * 512:8192 + (hd + 1) * 512], stn, reads=[sttk])
        P.finish()
        return nc, cx

    ftk = Tk()
    P.dma("sync", fold, st_all[:, :, 10240:10248].rearrange("c p n -> p c n"), writes=[ftk])
    negc = cf[:, L_NEG:L_NEG + 1]
    P.op("dve", lambda e: e.memset(mrun, -1e30), writes=[ftk])
    for cp in range(7):
        mu = cf[:, L_CMASK + cp:L_CMASK + cp + 1]
        P.op("dve", lambda e, cp=cp, mu=mu: e.scalar_tensor_tensor(out=fa, in0=fold[:, cp, 0:4], scalar=mu, in1=mrun,
                                                                    op0=ALU.mult, op1=ALU.add), reads=[ftk, ctk], writes=[ftk])
        P.op("dve", lambda e, cp=cp, mu=mu: e.tensor_scalar(out=fb, in0=fold[:, cp, 4:8], scalar1=mu,
                                                            scalar2=cf[:, L_CNEG + cp:L_CNEG + cp + 1], op0=ALU.mult, op1=ALU.add),
             reads=[ftk, ctk], writes=[ftk])
        P.op("dve", lambda e: e.tensor_tensor(out=fc_, in0=fa, in1=fb, op=ALU.max), reads=[ftk], writes=[ftk])
        P.op("dve", lambda e: e.tensor_tensor(out=fa, in0=fa, in1=fc_, op=ALU.subtract), reads=[ftk], writes=[ftk])
        P.op("dve", lambda e: e.tensor_tensor(out=fb, in0=fb, in1=fc_, op=ALU.subtract), reads=[ftk], writes=[ftk])
        P.op("act", lambda e, cp=cp: e.activation(out=S1[:, cp, :], in_=fa, func=AF.Exp), reads=[ftk], writes=[ftk])
        P.op("act", lambda e: e.activation(out=fb, in_=fb, func=AF.Exp), reads=[ftk], writes=[ftk])
        P.op("dve", lambda e, cp=cp, mu=mu: e.tensor_scalar(out=S2[:, cp, :], in0=fb, scalar1=mu, scalar2=None, op0=ALU.mult),
             reads=[ftk, ctk], writes=[ftk])
        P.op("dve", lambda e: e.tensor_copy(out=mrun, in_=fc_), reads=[ftk], writes=[ftk])
    mst = small[:, 4:5]
    P.op("dve", lambda e: e.tensor_tensor(out=small[:, 8:12], in0=mrun, in1=ident_f[:, 0:4], op=ALU.mult), reads=[ftk, ctk], writes=[rtk])
    P.op("dve", lambda e: e.tensor_reduce(out=mst, in_=small[:, 8:12], axis=AX.X, op=ALU.add), reads=[rtk], writes=[rtk])
    P.op("dve", lambda e: e.tensor_tensor_scan(out=GF, data0=LF, data1=GI, initial=mst, op0=ALU.add, op1=ALU.max),
         reads=[rtk, gtk], writes=[gtk])
    MM = GF
    P.op("dve", lambda e: e.tensor_tensor(out=BETAx[:, 1:NT + 1], in0=MM, in1=BB, op=ALU.subtract), reads=[gtk, rtk], writes=[rtk])
    P.op("dve", lambda e: e.tensor_copy(out=BETAx[:, 0:1], in_=mst), reads=[rtk], writes=[rtk])
    BETA = BETAx[:, 1:NT + 1]
    for ck in range(NCK):
        bl = small[:, 16:17]
        P.op("dve", lambda e, ck=ck: e.scalar_tensor_tensor(out=small[:, 16 + ck % 8:17 + ck % 8], in0=BETAx[:, 128 * (ck + 1):128 * (ck + 1) + 1],
                                                            scalar=-1.0, in1=cf[:, L_LNK:L_LNK + 1], op0=ALU.mult, op1=ALU.add),
             reads=[rtk, ctk], writes=[rtk])
        P.op("act", lambda e, ck=ck: e.activation(out=LF[:, ck * 128:(ck + 1) * 128], in_=ALPHA[:, ck * 128:(ck + 1) * 128],
                                                  func=AF.Exp, bias=small[:, 16 + ck % 8:17 + ck % 8]), reads=[rtk], writes=[rtk])
    to_token_major(LF, TMw)
    to_token_major(ALPHA, TMa)
    P.barrier()
    cx.release(pers_mark)
    BETA = BETAx[:, 1:NT + 1]

    qT = cx.sb(None, [128, 4, 512], BF16, "qT")
    kT = cx.sb(None, [128, 4, 512], BF16, "kT")
    zs = cx.sb(None, [128, 4, 512], BF16, "zs")
    yb = cx.sb(None, [128, 4, 512], BF16, "yb")
    qktk, zstk, ytk = Tk(), Tk(), Tk()
    Cst = cx.sb(None, [128, 4, 512], F32, "Cst")
    Caug = cx.sb(None, [128, 4, 640], BF16, "Caug")
    nrow = cx.sb(None, [128, 512], F32, "nrow")
    nm = cx.sb(None, [128, 512], F32, "nm")
    ctk2 = Tk()
    clst = Rot([topfree[:, 4096:6144], topfree[:, 6144:8176][:, 0:2032]])
    wk_rot = Rot([cx.sb(None, [128, 512], BF16, "wk") for _ in range(2)])
    va_rot = Rot([cx.sb(None, [128, 640], BF16, "vaug") for _ in range(2)])
    for (va, vatk) in va_rot.items:
        P.op("pool", lambda e, va=va: e.memset(va[:, 512:640], 1.0), writes=[vatk])
    dt_rot = Rot([cx.sb(None, [128, 128], F32, "dtmp") for _ in range(2)])
    sd_rot = Rot([cx.sb(None, [128, 128], BF16, "SdT") for _ in range(2)])
    qs_rot = Rot([cx.sb(None, [128, 4, 128], BF16, "qs") for _ in range(2)])
    hsq_rot = Rot([cx.sb(None, [128, 512], BF16, "hsq") for _ in range(2)])
    dd_rot = Rot([cx.sb(None, [128, 128], F32, "dd") for _ in range(2)])
    rr_rot = Rot([cx.sb(None, [128, 128], F32, "rr") for _ in range(2)])
    sc_rot = Rot([cx.sb(None, [128, 128], F32, "scsb") for _ in range(2)])
    em_rot = Rot([cx.sb(None, [128, 128], F32, "emsb") for _ in range(2)])
    ul_rot = Rot([cx.sb(None, [128, 1], F32, "ulast") for _ in range(2)])
    tt_rot = Rot([cx.sb(None, [128, 128], F32, "tt") for _ in range(3)])
    psB2 = Rot([bk[2]])
    psS3 = Rot([bk[3]])
    psH = Rot([bk[4]])
    psD = Rot([bk[5]])
    psSS = Rot([bk[6]])
    psRP = Rot([bk[7]])
    wzv = wupv
    yview = yscr.rearrange("(c p) n -> p c n", p=128)
    for hd in range(BH):
        wx3, wxtk = head_setup(hd)
        wz, wztk = ws.load([wzv[:, :, 2048 + hd * 512:2048 + (hd + 1) * 512]])
        wz3 = wz.rearrange("p (c n) -> p c n", c=NC8)
        P.op("pool", lambda e: e.memset(Cst, 0.0), writes=[ctk2])
        P.op("pool", lambda e: e.memset(nrow, 0.0), writes=[ctk2])
        Cflat = Cst.rearrange("p a b -> p (a b)")
        for cp in range(7):
            cl_, cltk = clst.items[0]
            P.dma("sync", cl_, st_all[cp][:, hd * 2048:(hd + 1) * 2048], writes=[cltk])
            P.op("act", lambda e, cp=cp, cl_=cl_: e.activation(out=cl_, in_=cl_, func=AF.Copy, scale=S2[:, cp, hd:hd + 1]),
                 reads=[cltk, ftk], writes=[cltk])
            P.op("dve", lambda e, cp=cp, cl_=cl_: e.scalar_tensor_tensor(out=Cflat, in0=Cflat, scalar=S1[:, cp, hd:hd + 1], in1=cl_,
                                                                          op0=ALU.mult, op1=ALU.add), reads=[cltk, ftk, ctk2], writes=[ctk2])
            nl_, nltk = clst.items[1]
            P.dma("sync", nl_[:, 0:512], st_all[cp][:, 8192 + hd * 512:8192 + (hd + 1) * 512], writes=[nltk])
            P.op("act", lambda e, cp=cp, nl_=nl_: e.activation(out=nl_[:, 0:512], in_=nl_[:, 0:512], func=AF.Copy, scale=S2[:, cp, hd:hd + 1]),
                 reads=[nltk, ftk], writes=[nltk])
            P.op("dve", lambda e, cp=cp, nl_=nl_: e.scalar_tensor_tensor(out=nrow, in0=nrow, scalar=S1[:, cp, hd:hd + 1], in1=nl_[:, 0:512],
                                                                          op0=ALU.mult, op1=ALU.add), reads=[nltk, ftk, ctk2], writes=[ctk2])

        def refresh_caug(full):
            if full:
                for dc in range(4):
                    P.op("act", lambda e, dc=dc: e.activation(out=Caug[:, dc, 0:512], in_=Cst[:, dc, :], func=AF.Copy),
                         reads=[ctk2], writes=[ctk2])
            P.op("dve", lambda e: e.tensor_scalar(out=nm, in0=nrow, scalar1=cf[:, L_E0:L_E0 + 1], scalar2=None, op0=ALU.mult),
                 reads=[ctk2, ctk], writes=[ctk2])
            ps, pstk = psB2.next()
            for dc in range(4):
                P.op("pe", lambda e, ps=ps, dc=dc: e.matmul(ps[:, dc * 128:(dc + 1) * 128], lhsT=nm[:, dc * 128:(dc + 1) * 128], rhs=ones_f,
                                                            start=True, stop=True), reads=[ctk2, ctk], writes=[pstk], signal=(dc == 3))
            P.op("act", lambda e, ps=ps: e.activation(out=Caug[:, :, 512:640], in_=ps.rearrange("p (a b) -> p a b", a=4), func=AF.Copy),
                 reads=[pstk], writes=[ctk2])

        refresh_caug(True)
        for tg in range(4):
            xm, xmt = front(hd, tg, wx3, wxtk)
            for mc in range(4):
                ps, pstk = psA.next()
                for c in range(NC8):
                    P.op("pe", lambda e, c=c, mc=mc, ps=ps: e.matmul(
                        ps, lhsT=wz3[:, c, mc * 128:(mc + 1) * 128], rhs=hT[:, c, 4 + tg * 512:4 + (tg + 1) * 512],
                        start=(c == 0), stop=(c == NC8 - 1)), reads=[wztk], writes=[pstk], signal=(c == NC8 - 1))
                P.op("act", lambda e, ps=ps, mc=mc: e.activation(out=zs[:, mc, :], in_=ps, func=AF.Silu), reads=[pstk], writes=[zstk])
            for j, dst, sc_ in ((0, qT, 1.0), (1, kT, KSCALE)):
                for dc in range(4):
                    ps, pstk = psA.next()
                    P.op("pe", lambda e, ps=ps, j=j, dc=dc: e.matmul(ps, lhsT=bdh[:, j, dc, :], rhs=xc[:, dc, :], start=True, stop=True),
                         reads=[hdtk, xctk], writes=[pstk])
                    P.op("act", lambda e, ps=ps, dst=dst, dc=dc, sc_=sc_: e.activation(out=dst[:, dc, :], in_=ps, func=AF.Copy, scale=sc_),
                         reads=[pstk], writes=[qktk])
            for cl in range(4):
                ck = tg * 4 + cl
                tsl = slice(cl * 128, (cl + 1) * 128)
                gsl = slice(ck * 128, (ck + 1) * 128)
                sel = cf[:, L_SEL + hd * 128:L_SEL + (hd + 1) * 128]
                rp, rptk = psRP.next()
                for i3, row in enumerate((BETA, MM)):
                    P.op("pe", lambda e, rp=rp, i3=i3, row=row, gsl=gsl: e.matmul(
                        rp[:, i3 * 128:(i3 + 1) * 128], lhsT=sel, rhs=row[:, gsl], start=True, stop=True),
                        reads=[rtk, gtk, ctk], writes=[rptk], signal=(i3 == 1))
                bprev = mrun[:, hd:hd + 1] if ck == 0 else ul_prev[0]
                bprev_tk = ftk if ck == 0 else ul_prev[1]
                scsb, sctk = sc_rot.next()
                P.op("act", lambda e, rp=rp, scsb=scsb, bprev=bprev: e.activation(out=scsb, in_=rp[:, 0:128], func=AF.Exp, scale=-1.0, bias=bprev),
                     reads=[rptk, bprev_tk], writes=[sctk])
                emsb, emtk = em_rot.next()
                P.op("act", lambda e, rp=rp, emsb=emsb: e.activation(out=emsb, in_=rp[:, 128:256], func=AF.Exp, scale=-1.0),
                     reads=[rptk], writes=[emtk])
                ul_prev = ul_rot.next()
                P.op("act", lambda e, rp=rp, ul_prev=ul_prev: e.activation(out=ul_prev[0], in_=rp[:, 127:128], func=AF.Copy),
                     reads=[rptk], writes=[ul_prev[1]])
                ps, pstk = psB2.next()
                for mc in range(4):
                    P.op("pe", lambda e, ps=ps, mc=mc, tsl=tsl: e.matmul(
                        ps[:, mc * 128:(mc + 1) * 128], lhsT=xc[:, mc, tsl], rhs=bdh[:, 1, mc, :], start=True, stop=True),
                        reads=[xctk, hdtk], writes=[pstk], signal=(mc == 3))
                wk, wktk = wk_rot.next()
                P.op("act", lambda e, ps=ps, wk=wk, ck=ck: e.activation(out=wk, in_=ps, func=AF.Copy, scale=TMw[:, ck, hd:hd + 1]),
                     reads=[pstk, tmtk], writes=[wktk])
                ps, pstk = psB2.next()
                for mc in range(4):
                    P.op("pe", lambda e, ps=ps, mc=mc, cl=cl, xm=xm: e.matmul(
                        ps[:, mc * 128:(mc + 1) * 128], lhsT=xm[:, mc, 4 + cl * 128:4 + (cl + 1) * 128], rhs=bdh[:, 2, mc, :],
                        start=True, stop=True), reads=[xmt, hdtk], writes=[pstk], signal=(mc == 3))
                va, vatk = va_rot.next()
                P.op("act", lambda e, ps=ps, va=va: e.activation(out=va[:, 0:512], in_=ps, func=AF.Copy), reads=[pstk], writes=[vatk])
                pS, pStk = psS3.next()
                for dc in range(4):
                    P.op("pe", lambda e, pS=pS, dc=dc, tsl=tsl: e.matmul(pS[:, 0:128], lhsT=kT[:, dc, tsl], rhs=qT[:, dc, tsl],
                                                                          start=(dc == 0), stop=(dc == 3)),
                         reads=[qktk], writes=[pStk], signal=(dc == 3))
                dtmp, dttk = dt_rot.next()
                P.op("dve", lambda e, rp=rp, dtmp=dtmp, ck=ck: e.scalar_tensor_tensor(
                    out=dtmp, in0=rp[:, 0:128], scalar=TMa[:, ck, hd:hd + 1], in1=cf[:, L_MASKLOW:L_MASKLOW + 128],
                    op0=ALU.subtract, op1=ALU.max), reads=[rptk, tmtk, ctk], writes=[dttk])
                P.op("act", lambda e, dtmp=dtmp: e.activation(out=dtmp, in_=dtmp, func=AF.Exp, scale=-1.0), reads=[dttk], writes=[dttk])
                sd, sdtk = sd_rot.next()
                P.op("dve", lambda e, pS=pS, dtmp=dtmp, sd=sd: e.tensor_tensor(out=sd, in0=pS[:, 0:128], in1=dtmp, op=ALU.mult),
                     reads=[pStk, dttk], writes=[sdtk])
                qs, qstk = qs_rot.next()
                P.op("dve", lambda e, scsb=scsb, qs=qs, tsl=tsl: e.tensor_tensor(
                    out=qs, in0=qT[:, :, tsl], in1=scsb.unsqueeze(1).to_broadcast([128, 4, 128]), op=ALU.mult),
                    reads=[qktk, sctk], writes=[qstk])
                pH, pHtk = psH.next()
                pD_, pDtk = psD.next()
                for ec in range(5):
                    o = pH[:, ec * 128:(ec + 1) * 128] if ec < 4 else pD_[:, 0:128]
                    otk = pHtk if ec < 4 else pDtk
                    for dc in range(4):
                        P.op("pe", lambda e, o=o, ec=ec, dc=dc, qs=qs: e.matmul(
                            o, lhsT=Caug[:, dc, ec * 128:(ec + 1) * 128], rhs=qs[:, dc, :], start=(dc == 0), stop=False),
                            reads=[ctk2, qstk], writes=[otk], signal=False)
                    P.op("pe", lambda e, o=o, ec=ec, va=va, sd=sd: e.matmul(
                        o, lhsT=va[:, ec * 128:(ec + 1) * 128], rhs=sd, start=False, stop=True),
                        reads=[vatk, sdtk], writes=[otk], signal=True)
                hsq, hsqtk = hsq_rot.next()
                P.op("act", lambda e, pH=pH, hsq=hsq: e.activation(out=hsq, in_=pH, func=AF.Square), reads=[pHtk], writes=[hsqtk])
                pSS, pSStk = psSS.next()
                for ec in range(4):
                    P.op("pe", lambda e, pSS=pSS, hsq=hsq, ec=ec: e.matmul(pSS[:, 0:128], lhsT=ones_bf, rhs=hsq[:, ec * 128:(ec + 1) * 128],
                                                                            start=(ec == 0), stop=(ec == 3)),
                         reads=[hsqtk, ctk], writes=[pSStk], signal=(ec == 3))
                dd, ddtk = dd_rot.next()
                P.op("dve", lambda e, pD_=pD_, dd=dd: e.tensor_scalar(out=dd, in0=pD_[:, 0:128], scalar1=-1.0, scalar2=None, op0=ALU.mult),
                     reads=[pDtk], writes=[ddtk])
                P.op("dve", lambda e, pD_=pD_, dd=dd: e.tensor_tensor(out=dd, in0=dd, in1=pD_[:, 0:128], op=ALU.max),
                     reads=[pDtk, ddtk], writes=[ddtk])
                P.op("dve", lambda e, emsb=emsb, dd=dd: e.tensor_tensor(out=dd, in0=dd, in1=emsb, op=ALU.max),
                     reads=[emtk, ddtk], writes=[ddtk])
                P.op("dve", lambda e, dd=dd: e.scalar_tensor_tensor(out=dd, in0=dd, scalar=EPS, in1=dd, op0=ALU.mult, op1=ALU.mult),
                     reads=[ddtk], writes=[ddtk])
                rr, rrtk = rr_rot.next()
                P.op("dve", lambda e, pSS=pSS, dd=dd, rr=rr: e.scalar_tensor_tensor(out=rr, in0=pSS[:, 0:128], scalar=1.0 / DH, in1=dd,
                                                                                     op0=ALU.mult, op1=ALU.add),
                     reads=[pSStk, ddtk], writes=[rrtk])
                P.op("act", lambda e, rr=rr: e.activation(out=rr, in_=rr, func=AF.Sqrt), reads=[rrtk], writes=[rrtk])
                P.op("dve", lambda e, rr=rr: e.reciprocal(out=rr, in_=rr), reads=[rrtk], writes=[rrtk])
                for ec in range(4):
                    ch = hd * 4 + ec
                    tt, tttk = tt_rot.next()
                    P.op("dve", lambda e, pH=pH, ec=ec, ch=ch, rr=rr, tt=tt: e.scalar_tensor_tensor(
                        out=tt, in0=pH[:, ec * 128:(ec + 1) * 128], scalar=cf[:, L_HGAIN + ch:L_HGAIN + ch + 1], in1=rr,
                        op0=ALU.mult, op1=ALU.mult), reads=[pHtk, rrtk, ctk], writes=[tttk])
                    P.op("dve", lambda e, ec=ec, ch=ch, tt=tt, tsl=tsl: e.scalar_tensor_tensor(
                        out=tt, in0=xc[:, ec, tsl], scalar=cf[:, L_SKIP + ch:L_SKIP + ch + 1], in1=tt,
                        op0=ALU.mult, op1=ALU.add), reads=[xctk, tttk, ctk], writes=[tttk])
                    P.op("dve", lambda e, ec=ec, tt=tt, tsl=tsl: e.tensor_tensor(out=yb[:, ec, tsl], in0=tt, in1=zs[:, ec, tsl], op=ALU.mult),
                         reads=[tttk, zstk], writes=[ytk])
                if dbg_stop is not None and (hd, ck) == tuple(dbg_stop):
                    P.barrier()
                    P.finish()
                    return nc, cx
                decay = scsb[:, 127:128]
                for dc in range(4):
                    ps, pstk = psB2.next()
                    P.op("pe", lambda e, ps=ps, dc=dc, wk=wk, va=va: e.matmul(ps, lhsT=wk[:, dc * 128:(dc + 1) * 128], rhs=va[:, 0:512],
                                                                                start=True, stop=True), reads=[wktk, vatk], writes=[pstk])
                    P.op("dve", lambda e, ps=ps, dc=dc, decay=decay: e.scalar_tensor_tensor(
                        out=Cst[:, dc, :], in0=Cst[:, dc, :], scalar=decay, in1=ps, op0=ALU.mult, op1=ALU.add),
                        reads=[pstk, sctk, ctk2], writes=[ctk2])
                    P.op("act", lambda e, dc=dc: e.activation(out=Caug[:, dc, 0:512], in_=Cst[:, dc, :], func=AF.Copy),
                         reads=[ctk2], writes=[ctk2])
                ps, pstk = psB2.next()
                P.op("pe", lambda e, ps=ps, wk=wk: e.matmul(ps, lhsT=ones_bf, rhs=wk, start=True, stop=True),
                     reads=[wktk, ctk], writes=[pstk])
                P.op("dve", lambda e, ps=ps, decay=decay: e.scalar_tensor_tensor(out=nrow, in0=nrow, scalar=decay, in1=ps,
                                                                                  op0=ALU.mult, op1=ALU.add),
                     reads=[pstk, sctk, ctk2], writes=[ctk2])
                refresh_caug(False)
            P.dma("sync", yview[:, hd * 4:(hd + 1) * 4, tg * 512:(tg + 1) * 512], yb, reads=[ytk])
    P.barrier()
    cx.release(base_mark)

    X = TOP.rearrange("p (c n) -> p c n", c=NC8)
    Xtk = [[Tk() for _ in range(4)] for _ in range(NC8)]
    for c in range(NC8):
        for tg in range(4):
            P.dma("sync", X[:, c, tg * 512:(tg + 1) * 512], x1T[c * 128:(c + 1) * 128, tg * 512:(tg + 1) * 512], writes=[Xtk[c][tg]])
    mk = cx.mark()
    ws3 = WStream(cx, None, 4096, nstage=2, nslot=2)
    wdn = cx.sb(None, [128, 16, D], BF16, "wdn")
    wdtk = Tk()
    wdv = wdown.rearrange("(c p) n -> p c n", p=128)
    for q4 in range(4):
        wb_, wbtk = ws3.load([wdv[:, q4 * 4:(q4 + 1) * 4, :]])
        P.op("pool", lambda e, wb_=wb_, q4=q4: e.tensor_copy(out=wdn[:, q4 * 4:(q4 + 1) * 4, :], in_=wb_.rearrange("p (a b) -> p a b", a=4)),
             reads=[wbtk], writes=[wdtk])
    yts = [cx.sb(None, [128, 16, 512], BF16, "yt") for _ in range(2)]
    yttk = [Tk(), Tk()]
    psA4 = Rot(bk[0:4])
    for tg in range(4):
        yt, ytt = yts[tg % 2], yttk[tg % 2]
        P.dma("sync", yt, yview[:, :, tg * 512:(tg + 1) * 512], writes=[ytt])
        sl = slice(tg * 512, (tg + 1) * 512)
        for oc in range(NC8):
            ps, pstk = psA4.next()
            for mc in range(16):
                P.op("pe", lambda e, ps=ps, mc=mc, oc=oc, yt=yt: e.matmul(ps, lhsT=wdn[:, mc, oc * 128:(oc + 1) * 128], rhs=yt[:, mc, :],
                                                                           start=(mc == 0), stop=(mc == 15)),
                     reads=[wdtk, ytt], writes=[pstk], signal=(mc == 15))
            P.op("dve", lambda e, ps=ps, oc=oc, sl=sl: e.tensor_tensor(out=X[:, oc, sl], in0=ps, in1=X[:, oc, sl], op=ALU.add),
                 reads=[pstk, Xtk[oc][tg]], writes=[Xtk[oc][tg]])
    P.barrier()
    cx.release(mk)
    if debug:
        emit_store(cx, X, Xtk, dbg_a)
    emit_mlp(cx, X, Xtk, cf[:, L_MLPN:L_MLPN + 8], ctk, w1, w2, ones_bf)
    emit_ple(cx, X, Xtk, cf[:, L_PLEN:L_PLEN + 8], ctk, wg, wp, pT, ones_bf)
    emit_store(cx, X, Xtk, out)
    P.finish()
    return nc, cx


_CACHE = {}


def _prog(key, builder):
    return builder()


def kernel(**inputs):
    inputs = {k: np.asarray(v) for k, v in inputs.items()}
    cores = list(range(NCORES))
    nc, _ = build_layer0()
    in_maps = [layer0_inputs(inputs, c) for c in cores]
    res = run_bass_kernel_spmd(nc, in_maps, core_ids=cores)
    x1T = np.concatenate([r["xout"] for r in res.results], axis=1)
    nc, _ = build_layer1("B")
    in_maps = [layer1_inputs(inputs, c, x1T, "B") for c in cores]
    res = run_bass_kernel_spmd(nc, in_maps, core_ids=cores)
    st_all = np.stack([res.results[c]["st_out"] for c in range(7)])
    g_rows = [res.results[c]["g_out"] for c in cores]
    nc, _ = build_layer1("C")
    in_maps = [layer1_inputs(inputs, c, x1T, "C", st_all, g_rows[c]) for c in cores]
    res = run_bass_kernel_spmd(nc, in_maps, core_ids=cores)
    outT = np.concatenate([r["xout"] for r in res.results], axis=1)
    return np.ascontiguousarray(outT.T)[None].astype(np.float32)
```

```python
import numpy as np
import concourse.bass as bass
import concourse.mybir as mybir
from concourse.bass_utils import run_bass_kernel_spmd

F32 = mybir.dt.float32
BF16 = mybir.dt.bfloat16
AF = mybir.ActivationFunctionType
ALU = mybir.AluOpType
AX = mybir.AxisListType

NCORES = 8
S = 16384
D = 1024
NT = S // NCORES
NC8 = D // 128
EPS = 1e-6
BIG = 30000.0
A_GROUPS = ((128, 1), (512, 4), (2048, 16))
NDMA = 24
SB_F32 = 51968


class Tk:
    __slots__ = ("w", "r")

    def __init__(self):
        self.w = {}
        self.r = {}


class Prog:
    def __init__(self, nc):
        self.nc = nc
        self.eng = {"act": nc.scalar, "dve": nc.vector, "pool": nc.gpsimd, "pe": nc.tensor, "sync": nc.sync}
        self.sem = {e: nc.alloc_semaphore("s_" + e) for e in ("act", "dve", "pool", "pe")}
        self.cnt = {e: 0 for e in ("act", "dve", "pool", "pe")}
        self.seen = {e: {} for e in self.eng}
        self.dsem = [nc.alloc_semaphore("s_dma%d" % i) for i in range(NDMA)]
        self.dcnt = [0] * NDMA
        self.dnext = 0
        self.nins = {e: 0 for e in self.eng}

    def _semof(self, src):
        if isinstance(src, tuple):
            return self.dsem[src[1]]
        return self.sem[src]

    def _deps(self, e, reads, writes, allraw=False):
        deps = {}

        def add(src, n, raw):
            if src == e and not allraw:
                if e == "pe" or not raw:
                    return
            if deps.get(src, 0) < n:
                deps[src] = n

        for t in reads:
            for src, n in t.w.items():
                add(src, n, True)
        for t in writes:
            for src, n in t.w.items():
                add(src, n, False)
            for src, n in t.r.items():
                add(src, n, False)
        return deps

    def _wait(self, e, deps):
        eng = self.eng[e]
        seen = self.seen[e]
        for src, n in deps.items():
            if seen.get(src, 0) >= n:
                continue
            seen[src] = n
            eng.wait_ge(self._semof(src), n)
            self.nins[e] += 1

    def op(self, e, fn, reads=(), writes=(), signal=True):
        self._wait(e, self._deps(e, reads, writes))
        ins = fn(self.eng[e])
        self.nins[e] += 1
        n = self.cnt[e] + 1
        if signal:
            ins.then_inc(self.sem[e], 1)
            self.cnt[e] = n
        for t in reads:
            if t.r.get(e, 0) < n:
                t.r[e] = n
        for t in writes:
            if t.w.get(e, 0) < n:
                t.w[e] = n
        return ins

    def dma(self, q, out, in_, reads=(), writes=()):
        k = self.dnext
        self.dnext = (k + 1) % NDMA
        src = ("dma", k)
        deps = self._deps(q, reads, writes, allraw=True)
        if self.dcnt[k] > 0:
            deps[src] = max(deps.get(src, 0), self.dcnt[k])
        self._wait(q, deps)
        ins = self.eng[q].dma_start(out=out, in_=in_)
        self.nins[q] += 1
        n = self.dcnt[k] + 16
        ins.then_inc(self.dsem[k], 16)
        self.dcnt[k] = n
        for t in reads:
            t.r[src] = n
        for t in writes:
            t.w[src] = n

    def barrier(self):
        for e in self.eng:
            deps = {}
            for s2 in self.cnt:
                if s2 != e and self.cnt[s2] > 0:
                    deps[s2] = self.cnt[s2]
            for k in range(NDMA):
                if self.dcnt[k] > 0:
                    deps[("dma", k)] = self.dcnt[k]
            self._wait(e, deps)

    def finish(self):
        deps = {}
        for k in range(NDMA):
            if self.dcnt[k] > 0:
                deps[("dma", k)] = self.dcnt[k]
        self._wait("sync", deps)


class Rot:
    def __init__(self, aps):
        self.items = [a if isinstance(a, tuple) else (a, Tk()) for a in aps]
        self.i = 0

    def next(self):
        it = self.items[self.i]
        self.i = (self.i + 1) % len(self.items)
        return it


class Ctx:
    def __init__(self, nc):
        self.nc = nc
        self.P = Prog(nc)
        self.banks = [nc.alloc_psum_tensor("psb%d" % i, [128, 512], F32).ap() for i in range(8)]
        self.nalloc = 0

        self.big = nc.alloc_sbuf_tensor("big", [128, SB_F32], F32).ap()
        self.top = 0

    def sb(self, stack, shape, dt, name=None):
        esz = 2 if dt == BF16 else 4
        n = int(np.prod(shape[1:]))
        nbytes = (n * esz + 63) // 64 * 64
        off = self.top
        assert off + nbytes <= SB_F32 * 4, ("SBUF overflow", name, off, nbytes)
        self.top = off + nbytes
        self.log = getattr(self, 'log', [])
        self.log.append((name, off, nbytes))
        ap = self.big[:, off // 4:(off + nbytes) // 4]
        if dt == BF16:
            ap = ap.bitcast(BF16)
        ap = ap[:, 0:n]
        if len(shape) == 3:
            ap = ap.rearrange("p (a b) -> p a b", a=shape[1])
        elif len(shape) == 4:
            ap = ap.rearrange("p (a b c) -> p a b c", a=shape[1], b=shape[2])
        return ap

    def mark(self):
        return self.top

    def release(self, m):
        self.top = m


def load_consts(cx, stack, cst_ap, ncols):
    P = cx.P
    cf = cx.sb(stack, [128, ncols], F32, "cstf")
    cb = cx.sb(stack, [128, C_END_BF], BF16, "cstb")
    tk = Tk()
    P.dma("sync", cf, cst_ap, writes=[tk])
    P.op("dve", lambda e: e.tensor_copy(out=cb, in_=cf[:, 0:C_END_BF]), reads=[tk], writes=[tk])
    return cf, cb, tk


class WStream:
    def __init__(self, cx, stack, nelem, nstage=2, nslot=2):
        self.cx = cx
        self.nelem = nelem
        self.stage = Rot([cx.sb(stack, [128, nelem], F32, "wstg") for _ in range(nstage)])
        self.slots = Rot([cx.sb(stack, [128, nelem], BF16, "wbf") for _ in range(nslot)])

    def load(self, views):
        P = self.cx.P
        stg, stk = self.stage.next()
        wb, wtk = self.slots.next()
        off = 0
        for v in views:
            shp = v.shape
            n = int(np.prod(shp[1:]))
            dst = stg[:, off:off + n]
            if len(shp) == 3:
                dst = dst.rearrange("p (a b) -> p a b", a=shp[1])
            P.dma("sync", dst, v, writes=[stk])
            off += n
        assert off <= self.nelem
        P.op("pool", lambda e: e.tensor_copy(out=wb[:, 0:off], in_=stg[:, 0:off]), reads=[stk], writes=[wtk])
        return wb, wtk


def rms_stats(cx, xs, n, sq_rot, ps_ap, ps_tk, rstd, rstd_tk, ones_bf, ctk, inv_dim):
    P = cx.P
    nx = len(xs)
    for c, (xa, xt) in enumerate(xs):
        sq, sqt = sq_rot.next()
        P.op("act", lambda e, xa=xa, sq=sq: e.activation(out=sq[:, 0:n], in_=xa, func=AF.Square), reads=[xt], writes=[sqt])
        P.op("pe", lambda e, sq=sq, c=c: e.matmul(ps_ap[:, 0:n], lhsT=ones_bf, rhs=sq[:, 0:n], start=(c == 0), stop=(c == nx - 1)),
             reads=[sqt, ctk], writes=[ps_tk])
    P.op("act", lambda e: e.activation(out=rstd[:, 0:n], in_=ps_ap[:, 0:n], func=AF.Sqrt, bias=cx.eps_col, scale=inv_dim),
         reads=[ps_tk, ctk], writes=[rstd_tk])
    P.op("dve", lambda e: e.reciprocal(out=rstd[:, 0:n], in_=rstd[:, 0:n]), reads=[rstd_tk], writes=[rstd_tk])


C_ID, C_ONES, C_BONES, C_DM, C_HONES = 0, 128, 256, 384, 640
C_OZ = 704
C_HZ = 960
C_END_BF = 1216
C_EPS = 1216
C_GAINS = 1217
G0_ANORM = C_GAINS
G0_QG = G0_ANORM + 8
G0_KG = G0_QG + 3
G0_MLPN = G0_KG + 3
G0_PLEN = G0_MLPN + 8
G0_END = G0_PLEN + 8


def base_consts(core, ncols):
    c = np.zeros((128, ncols), np.float32)
    c[:, C_ID:C_ID + 128] = np.eye(128, dtype=np.float32)
    c[:, C_ONES:C_ONES + 128] = 1.0
    c[0:64, C_BONES:C_BONES + 64] = 1.0
    c[64:128, C_BONES + 64:C_BONES + 128] = 1.0
    kk = np.arange(128)[:, None]
    a = np.arange(128)[None, :]
    diag = np.where(kk <= a, a - kk, BIG)
    prev = np.where(kk >= a, 128 + a - kk, BIG)
    c[:, C_DM:C_DM + 128] = diag
    c[:, C_DM + 128:C_DM + 256] = prev
    hv = 0.0 if core == 0 else 1.0
    c[:, C_HONES:C_HONES + 64] = hv
    c[:, C_OZ:C_OZ + 64] = 1.0
    c[:, C_OZ + 128 + 64:C_OZ + 256] = 1.0
    c[:, C_HZ:C_HZ + 64] = hv
    c[:, C_HZ + 128 + 64:C_HZ + 256] = hv
    c[:, C_EPS] = EPS
    return c


def col_layout(v):
    v = np.asarray(v, np.float32).reshape(-1, 128)
    return np.ascontiguousarray(v.T)


def emit_norm_resident(cx, X, Xtk, gcol, ctk, hT, hTtk, sq_rot, rstd_rot, ps_rot, ones_bf):
    P = cx.P
    for tg in range(NT // 512):
        sl = slice(tg * 512, (tg + 1) * 512)
        xs = [(X[:, c, sl], Xtk[c][tg]) for c in range(NC8)]
        ps_ap, ps_tk = ps_rot.next()
        rstd, rtk = rstd_rot.next()
        rms_stats(cx, xs, 512, sq_rot, ps_ap, ps_tk, rstd, rtk, ones_bf, ctk, 1.0 / D)
        for c in range(NC8):
            P.op("dve", lambda e, c=c, sl=sl, rstd=rstd: e.scalar_tensor_tensor(
                out=hT[:, c, sl], in0=X[:, c, sl], scalar=gcol[:, c:c + 1], in1=rstd[:, 0:512],
                op0=ALU.mult, op1=ALU.mult), reads=[Xtk[c][tg], rtk, ctk], writes=[hTtk[c][tg]])


def emit_mlp(cx, X, Xtk, gcol, ctk, w1, w2, ones_bf):
    P = cx.P
    mk = cx.mark()
    st = None
    hT = cx.sb(st, [128, NC8, NT], BF16, "mlp_hT")
    hTtk = [[Tk() for _ in range(4)] for _ in range(NC8)]
    sq_rot = Rot([cx.sb(st, [128, 512], BF16, "sq") for _ in range(4)])
    rstd_rot = Rot([cx.sb(st, [128, 512], F32, "rstd") for _ in range(2)])
    ps_stat = Rot([cx.banks[7]])
    emit_norm_resident(cx, X, Xtk, gcol, ctk, hT, hTtk, sq_rot, rstd_rot, ps_stat, ones_bf)
    ws = WStream(cx, st, 4096, nstage=2, nslot=2)
    hids = [cx.sb(st, [128, 4, NT], BF16, "hid") for _ in range(2)]
    hid_tks = [[[Tk() for _ in range(4)] for _ in range(4)] for _ in range(2)]
    tmp_rot = Rot([cx.sb(st, [128, 512], F32, "rl") for _ in range(3)])
    psA = Rot(cx.banks[0:4])
    psB = Rot(cx.banks[4:7])
    w1v = w1.rearrange("(c p) n -> p c n", p=128)
    w2v = w2.rearrange("(c p) n -> p c n", p=128)
    NHB = 8
    for hb in range(NHB):
        hid = hids[hb % 2]
        htk = hid_tks[hb % 2]
        wa, watk = ws.load([w1v[:, :, hb * 512:(hb + 1) * 512]])
        wa3 = wa.rearrange("p (c n) -> p c n", c=NC8)
        for hc in range(4):
            for tg in range(4):
                sl = slice(tg * 512, (tg + 1) * 512)
                ps, pstk = psA.next()
                for c in range(NC8):
                    P.op("pe", lambda e, c=c, hc=hc, sl=sl, ps=ps, wa3=wa3: e.matmul(
                        ps, lhsT=wa3[:, c, hc * 128:(hc + 1) * 128], rhs=hT[:, c, sl],
                        start=(c == 0), stop=(c == NC8 - 1)),
                        reads=[watk, hTtk[c][tg]], writes=[pstk], signal=(c == NC8 - 1))
                tmp, ttk = tmp_rot.next()
                P.op("act", lambda e, ps=ps, tmp=tmp: e.activation(out=tmp, in_=ps, func=AF.Square),
                     reads=[pstk], writes=[ttk])
                P.op("dve", lambda e, ps=ps, tmp=tmp, hc=hc, sl=sl, hid=hid: e.scalar_tensor_tensor(
                    out=hid[:, hc, sl], in0=ps, scalar=0.0, in1=tmp, op0=ALU.is_gt, op1=ALU.mult),
                    reads=[pstk, ttk], writes=[htk[hc][tg]])
        wb, wbtk = ws.load([w2v[:, hb * 4:(hb + 1) * 4, :]])
        wb3 = wb.rearrange("p (c n) -> p c n", c=4)
        for oc in range(NC8):
            for tg in range(4):
                sl = slice(tg * 512, (tg + 1) * 512)
                ps, pstk = psB.next()
                for hc in range(4):
                    P.op("pe", lambda e, hc=hc, oc=oc, sl=sl, ps=ps, hid=hid, wb3=wb3: e.matmul(
                        ps, lhsT=wb3[:, hc, oc * 128:(oc + 1) * 128], rhs=hid[:, hc, sl],
                        start=(hc == 0), stop=(hc == 3)),
                        reads=[wbtk, htk[hc][tg]], writes=[pstk], signal=(hc == 3))
                P.op("dve", lambda e, oc=oc, sl=sl, ps=ps: e.tensor_tensor(
                    out=X[:, oc, sl], in0=ps, in1=X[:, oc, sl], op=ALU.add),
                    reads=[pstk, Xtk[oc][tg]], writes=[Xtk[oc][tg]])
    P.barrier()
    cx.release(mk)


def emit_ple(cx, X, Xtk, gcol, ctk, wg, wp, pT_dram, ones_bf):
    P = cx.P
    mk = cx.mark()
    st = None
    hT = cx.sb(st, [128, NC8, NT], BF16, "ple_hT")
    hTtk = [[Tk() for _ in range(4)] for _ in range(NC8)]
    sq_rot = Rot([cx.sb(st, [128, 512], BF16, "sq") for _ in range(4)])
    rstd_rot = Rot([cx.sb(st, [128, 512], F32, "rstd") for _ in range(2)])
    ps_stat = Rot([cx.banks[7]])
    emit_norm_resident(cx, X, Xtk, gcol, ctk, hT, hTtk, sq_rot, rstd_rot, ps_stat, ones_bf)
    ws = WStream(cx, st, 4096, nstage=2, nslot=3)
    pst = cx.sb(st, [128, 2, NT], F32, "pstg")
    pb = cx.sb(st, [128, 2, NT], BF16, "pbf")
    ptk = Tk()
    P.dma("sync", pst, pT_dram.rearrange("(c p) n -> p c n", p=128), writes=[ptk])
    P.op("pool", lambda e: e.tensor_copy(out=pb, in_=pst), reads=[ptk], writes=[ptk])
    wpb, wptk = ws.load([wp.rearrange("(c p) n -> p c n", p=128)])
    wp3 = wpb[:, 0:2048].rearrange("p (c n) -> p c n", c=2)
    gt_rot = Rot([cx.sb(st, [128, 512], F32, "gt") for _ in range(3)])
    psA = Rot(cx.banks[0:3])
    psB = Rot(cx.banks[3:6])
    wgv = wg.rearrange("(c p) n -> p c n", p=128)
    for half in range(2):
        wa, watk = ws.load([wgv[:, :, half * 512:(half + 1) * 512]])
        wa3 = wa.rearrange("p (c n) -> p c n", c=NC8)
        for o4 in range(4):
            oc = half * 4 + o4
            for tg in range(4):
                sl = slice(tg * 512, (tg + 1) * 512)
                ps, pstk = psA.next()
                for c in range(NC8):
                    P.op("pe", lambda e, c=c, o4=o4, sl=sl, ps=ps, wa3=wa3: e.matmul(
                        ps, lhsT=wa3[:, c, o4 * 128:(o4 + 1) * 128], rhs=hT[:, c, sl],
                        start=(c == 0), stop=(c == NC8 - 1)),
                        reads=[watk, hTtk[c][tg]], writes=[pstk], signal=(c == NC8 - 1))
                ps2, ps2tk = psB.next()
                for kc in range(2):
                    P.op("pe", lambda e, kc=kc, oc=oc, sl=sl, ps2=ps2: e.matmul(
                        ps2, lhsT=wp3[:, kc, oc * 128:(oc + 1) * 128], rhs=pb[:, kc, sl],
                        start=(kc == 0), stop=(kc == 1)),
                        reads=[wptk, ptk], writes=[ps2tk], signal=(kc == 1))
                gt, gtk = gt_rot.next()
                P.op("act", lambda e, ps=ps, gt=gt: e.activation(out=gt, in_=ps, func=AF.Sigmoid),
                     reads=[pstk], writes=[gtk])
                P.op("dve", lambda e, ps2=ps2, gt=gt: e.tensor_tensor(out=gt, in0=ps2, in1=gt, op=ALU.mult),
                     reads=[ps2tk, gtk], writes=[gtk])
                P.op("dve", lambda e, oc=oc, sl=sl, gt=gt: e.tensor_tensor(
                    out=X[:, oc, sl], in0=gt, in1=X[:, oc, sl], op=ALU.add),
                    reads=[gtk, Xtk[oc][tg]], writes=[Xtk[oc][tg]])
    P.barrier()
    cx.release(mk)


def alibi_slope(h):
    return 2.0 ** (-8.0 * (h + 1) / 16)


def sslice(start, count, step):
    return slice(start, start + (count - 1) * step + 1, step)


def emit_attention(cx, xT_ext, wqkv, wo, cf, cb, ctk, TOP, R1, lvl=9, hps=8):
    P = cx.P
    st = None
    ones_bf = cb[:, C_ONES:C_ONES + 128]
    bones = cb[:, C_BONES:C_BONES + 128]
    Dm = cf[:, C_DM:C_DM + 256]
    hT = TOP.bitcast(BF16).rearrange("p (c n) -> p c n", c=NC8)
    mk = cx.mark()
    sq_rot = Rot([cx.sb(st, [128, 512], BF16, "sq") for _ in range(2)])
    rstd_rot = Rot([cx.sb(st, [128, 512], F32, "rstd") for _ in range(2)])
    xstg = [R1[:, i * 4096:(i + 1) * 4096].rearrange("p (c n) -> p c n", c=NC8) for i in range(2)]
    xstk = [Tk(), Tk()]
    ps_stat = Rot([cx.banks[7], cx.banks[6]])
    httk = Tk()
    gcol = cf[:, G0_ANORM:G0_ANORM + 8]
    for tg in range(8 if lvl >= 1 else 0):
        xa = xstg[tg % 2]
        xt = xstk[tg % 2]
        P.dma("sync", xa, xT_ext[:, tg * 512:(tg + 1) * 512].rearrange("(c p) n -> p c n", p=128), writes=[xt])
        xs = [(xa[:, c, :], xt) for c in range(NC8)]
        ps_ap, ps_tk = ps_stat.next()
        rstd, rtk = rstd_rot.next()
        rms_stats(cx, xs, 512, sq_rot, ps_ap, ps_tk, rstd, rtk, ones_bf, ctk, 1.0 / D)
        for c in range(NC8):
            P.op("dve", lambda e, c=c, tg=tg, xa=xa, rstd=rstd: e.scalar_tensor_tensor(
                out=hT[:, c, tg * 512:(tg + 1) * 512], in0=xa[:, c, :], scalar=gcol[:, c:c + 1], in1=rstd[:, 0:512],
                op0=ALU.mult, op1=ALU.mult), reads=[xt, rtk, ctk], writes=[httk])
    P.op("dve", lambda e: e.tensor_scalar(out=cf[:, G0_QG:G0_QG + 3], in0=cf[:, G0_QG:G0_QG + 3], scalar1=0.125,
                                          scalar2=None, op0=ALU.mult), reads=[ctk], writes=[ctk])
    P.barrier()
    ACC = R1[:, 0:4096].rearrange("p (a n) -> p a n", a=2)
    oT = R1[:, 4096:12288].bitcast(BF16).rearrange("p (c n) -> p c n", c=NC8)
    acctk = Tk()
    ottk = Tk()
    ws = WStream(cx, st, 3072, nstage=1, nslot=2)
    QTz = [cx.sb(st, [128, NT], BF16, "QTz") for _ in range(2)]
    qtk = Tk()
    KT_rot = Rot([cx.sb(st, [128, 2 * NT], BF16, "KT") for _ in range(2)])
    Vz = [cx.sb(st, [128, 32, 128], BF16, "Vz") for _ in range(2)]
    vtk = Tk()
    for e2 in (0, 1):
        P.op("pool", lambda e, e2=e2: e.memset(QTz[e2], 0.0), writes=[qtk])
        P.op("pool", lambda e, e2=e2: e.memset(Vz[e2], 0.0), writes=[vtk])
    onesz = [cb[:, C_OZ:C_OZ + 128], cb[:, C_OZ + 128:C_OZ + 256]]
    honesz = [cb[:, C_HZ:C_HZ + 128], cb[:, C_HZ + 128:C_HZ + 256]]
    tmp_rot = Rot([cx.sb(st, [128, 256], F32, "stmp") for _ in range(4)])
    pt_rot = Rot([cx.sb(st, [128, 256], BF16, "PT") for _ in range(8)])
    bk = [(cx.banks[i], Tk()) for i in range(8)]

    def half(i):
        return (bk[i][0][:, 0:256], bk[i][1])

    psQ = Rot(bk[0:2])
    psS = Rot([bk[2]])
    psV = Rot([bk[3]])
    psST0 = Rot([half(4), half(0)])
    psST1 = Rot([half(5), half(1)])
    psND = Rot([half(6), half(7), half(2), half(3)])
    wq_view = wqkv.rearrange("(c p) n -> p c n", p=128)

    def perm(ap2d, d):
        if d == 1:
            return ap2d
        return ap2d.rearrange("p (u r) -> p r u", r=d)

    def proj_piece(w3, wtk, j, e0, n, gain_col, out_buf, out_tk, d, Lx, u0):
        ps, pstk = psQ.next()
        for c in range(NC8):
            P.op("pe", lambda e, c=c, ps=ps: e.matmul(ps[:, 0:n], lhsT=w3[:, j, c, :], rhs=hT[:, c, e0:e0 + n],
                                                      start=(c == 0), stop=(c == NC8 - 1)),
                 reads=[wtk], writes=[pstk], signal=(c == NC8 - 1))
        ps2, ps2tk = psS.next()
        rstd, rtk = rstd_rot.next()
        rms_stats(cx, [(ps[:, 0:n], pstk)], n, sq_rot, ps2, ps2tk, rstd, rtk, bones, ctk, 1.0 / 64)
        outs = out_buf if isinstance(out_buf, list) else [(slice(0, 128), out_buf)]
        for (rows, ob) in outs:
            if d == 1:
                o = ob[rows, u0:u0 + n]
            else:
                o = ob[rows, 0:d * Lx].rearrange("p (r u) -> p r u", r=d)[:, :, u0:u0 + n // d]
            P.op("dve", lambda e, ps=ps, rstd=rstd, o=o, rows=rows: e.scalar_tensor_tensor(
                out=o, in0=perm(ps[rows, 0:n], d), scalar=gain_col[rows, :], in1=perm(rstd[rows, 0:n], d),
                op0=ALU.mult, op1=ALU.mult),
                reads=[pstk, rtk, ctk], writes=[out_tk])

    if lvl < 2:
        hps = 0
        P.op('dve', lambda e: e.memset(R1, 0.0), writes=[ottk])
    for hp in range(hps):
        for g, (W, d) in enumerate(A_GROUPS):
            L = NT // d
            Lk = (W + NT) // d
            nb = Lk // 128
            e_start = NT - W
            base = g * 3072 + hp * 128
            wb, wtk = ws.load([wq_view[:, :, base + j * 1024: base + j * 1024 + 128] for j in range(3)])
            w3 = wb[:, 0:3072].rearrange("p (j c n) -> p j c n", j=3, c=NC8)
            KT, ktk = KT_rot.next()
            for tg in range(4):
                proj_piece(w3, wtk, 0, NT + tg * 512, 512, cf[:, G0_QG + g:G0_QG + g + 1],
                           [(slice(0, 64), QTz[0]), (slice(64, 128), QTz[1])], qtk, d, L, tg * 512 // d)
            pieces = []
            if W < 512:
                pieces.append((e_start, W))
                e = NT
            else:
                e = e_start
            while e < 2 * NT:
                pieces.append((e, 512))
                e += 512
            for (e0, n) in pieces:
                proj_piece(w3, wtk, 1, e0, n, cf[:, G0_KG + g:G0_KG + g + 1], KT, ktk, d, Lk, (e0 - e_start) // d)
            nkb = d * nb if lvl >= 3 else 0
            kb = 0
            while kb < nkb:
                nblk = min(4, nkb - kb)
                psv, psvtk = psV.next()
                for b in range(nblk):
                    r, jb = divmod(kb + b, nb)
                    e_first = e_start + d * 128 * jb + r
                    for c in range(NC8):
                        P.op("pe", lambda e, c=c, b=b, e_first=e_first, psv=psv: e.matmul(
                            psv[:, b * 128:(b + 1) * 128], lhsT=hT[:, c, sslice(e_first, 128, d)], rhs=w3[:, 2, c, :],
                            start=(c == 0), stop=(c == NC8 - 1)),
                            reads=[wtk], writes=[psvtk], signal=(c == NC8 - 1 and b == nblk - 1))
                for e2 in (0, 1):
                    cs = slice(64 * e2, 64 * e2 + 64)
                    P.op("act", lambda e, kb=kb, nblk=nblk, psv=psv, e2=e2, cs=cs: e.activation(
                        out=Vz[e2][:, kb:kb + nblk, cs],
                        in_=psv[:, 0:nblk * 128].rearrange("p (b n) -> p b n", b=nblk)[:, :, cs], func=AF.Copy),
                        reads=[psvtk], writes=[vtk])
                kb += nblk
            for r in range(d if lvl >= 4 else 0):
                PTs = {}
                for jb in range(nb):
                    lo = 128 if jb == 0 else 0
                    hi = 128 if jb == nb - 1 else 256
                    qb0 = jb if jb == 0 else jb - 1
                    q_off = r * L + 128 * qb0
                    sTs = [psST0.next(), psST1.next()]
                    for e2 in (0, 1):
                        sT, sTtk = sTs[e2]
                        P.op("pe", lambda e, sT=sT, e2=e2, jb=jb, lo=lo, hi=hi, q_off=q_off, r=r: e.matmul(
                            sT[:, lo:hi], lhsT=KT[:, r * Lk + 128 * jb: r * Lk + 128 * jb + 128],
                            rhs=QTz[e2][:, q_off:q_off + (hi - lo)], start=True, stop=True),
                            reads=[ktk, qtk], writes=[sTtk])
                    for e2 in (0, 1):
                        sig = alibi_slope(2 * hp + e2) * d
                        sT, sTtk = sTs[e2]
                        tmp, tmtk = tmp_rot.next()
                        P.op("dve", lambda e, sT=sT, tmp=tmp, lo=lo, hi=hi, sig=sig: e.scalar_tensor_tensor(
                            out=tmp[:, lo:hi], in0=Dm[:, lo:hi], scalar=-sig, in1=sT[:, lo:hi],
                            op0=ALU.mult, op1=ALU.add),
                            reads=[sTtk, ctk], writes=[tmtk])
                        pt, pttk = pt_rot.next()
                        P.op("act", lambda e, tmp=tmp, pt=pt, lo=lo, hi=hi: e.activation(
                            out=pt[:, lo:hi], in_=tmp[:, lo:hi], func=AF.Exp), reads=[tmtk], writes=[pttk])
                        PTs[(e2, jb)] = (pt, pttk)
                    if jb >= 1:
                        j = jb - 1
                        nd, ndtk = psND.next()
                        kbp = r * nb + j
                        kbd = r * nb + jb
                        for part in (0, 1):
                            co = slice(128 * part, 128 * part + 128)
                            for e2 in (0, 1):
                                ptp, ptptk = PTs[(e2, j)]
                                ptd, ptdtk = PTs[(e2, jb)]
                                if part == 0:
                                    lp, ld = Vz[e2][:, kbp, :], Vz[e2][:, kbd, :]
                                else:
                                    lp, ld = (honesz[e2] if j == 0 else onesz[e2]), onesz[e2]
                                P.op("pe", lambda e, nd=nd, co=co, lp=lp, ptp=ptp, e2=e2: e.matmul(
                                    nd[:, co], lhsT=lp, rhs=ptp[:, 128:256], start=(e2 == 0), stop=False),
                                    reads=[vtk, ctk, ptptk], writes=[ndtk], signal=False)
                                P.op("pe", lambda e, nd=nd, co=co, ld=ld, ptd=ptd, e2=e2: e.matmul(
                                    nd[:, co], lhsT=ld, rhs=ptd[:, 0:128], start=False, stop=(e2 == 1)),
                                    reads=[vtk, ctk, ptdtk], writes=[ndtk], signal=(part == 1 and e2 == 1))
                        t0 = r + d * 128 * j
                        accv = ACC[:, :, sslice(t0, 128, d)]
                        ndv = nd.rearrange("p (a n) -> p a n", a=2)
                        if g == 0:
                            P.op("act", lambda e, accv=accv, ndv=ndv: e.activation(out=accv, in_=ndv, func=AF.Copy),
                                 reads=[ndtk], writes=[acctk])
                        else:
                            P.op("dve", lambda e, accv=accv, ndv=ndv: e.tensor_tensor(out=accv, in0=ndv, in1=accv, op=ALU.add),
                                 reads=[ndtk, acctk], writes=[acctk])
        P.op("dve", lambda e: e.reciprocal(out=ACC[:, 1, :], in_=ACC[:, 1, :]), reads=[acctk], writes=[acctk])
        P.op("dve", lambda e, hp=hp: e.tensor_tensor(out=oT[:, hp, :], in0=ACC[:, 0, :], in1=ACC[:, 1, :], op=ALU.mult),
             reads=[acctk], writes=[ottk])
    P.barrier()
    cx.release(mk)
    mk = cx.mark()
    X = TOP.rearrange("p (c n) -> p c n", c=NC8)
    Xtk = [[Tk() for _ in range(4)] for _ in range(NC8)]
    for c in range(NC8):
        for tg in range(4):
            P.dma("sync", X[:, c, tg * 512:(tg + 1) * 512], xT_ext[c * 128:(c + 1) * 128, NT + tg * 512:NT + (tg + 1) * 512],
                  writes=[Xtk[c][tg]])
    ws2 = WStream(cx, st, 4096, nstage=2, nslot=2)
    wov = wo.rearrange("(c p) n -> p c n", p=128)
    psA = Rot(cx.banks[0:4])
    for half in range(2):
        wa, watk = ws2.load([wov[:, :, half * 512:(half + 1) * 512]])
        wa3 = wa.rearrange("p (c n) -> p c n", c=NC8)
        for o4 in range(4):
            oc = half * 4 + o4
            for tg in range(4):
                sl = slice(tg * 512, (tg + 1) * 512)
                ps, pstk = psA.next()
                for c in range(NC8):
                    P.op("pe", lambda e, c=c, o4=o4, sl=sl, ps=ps, wa3=wa3: e.matmul(
                        ps, lhsT=wa3[:, c, o4 * 128:(o4 + 1) * 128], rhs=oT[:, c, sl],
                        start=(c == 0), stop=(c == NC8 - 1)),
                        reads=[watk, ottk], writes=[pstk], signal=(c == NC8 - 1))
                P.op("dve", lambda e, oc=oc, sl=sl, ps=ps: e.tensor_tensor(
                    out=X[:, oc, sl], in0=ps, in1=X[:, oc, sl], op=ALU.add),
                    reads=[pstk, Xtk[oc][tg]], writes=[Xtk[oc][tg]])
    P.barrier()
    cx.release(mk)
    return X, Xtk


def emit_store(cx, X, Xtk, out_dram):
    P = cx.P
    for c in range(NC8):
        P.dma("sync", out_dram[c * 128:(c + 1) * 128, :], X[:, c, :], reads=Xtk[c])


def build_layer0(debug=False, lvl=9, hps=8):
    nc = bass.Bass("TRN2", target_bir_lowering=False)
    xT_ext = nc.dram_tensor("xT_ext", [D, 2 * NT], F32, kind="ExternalInput").ap()
    pT = nc.dram_tensor("pT", [256, NT], F32, kind="ExternalInput").ap()
    cst = nc.dram_tensor("cst", [128, G0_END], F32, kind="ExternalInput").ap()
    wqkv = nc.dram_tensor("a_w_qkv", [D, 9216], F32, kind="ExternalInput").ap()
    wo = nc.dram_tensor("a_w_o", [D, D], F32, kind="ExternalInput").ap()
    w1 = nc.dram_tensor("mlp_w1", [D, 4096], F32, kind="ExternalInput").ap()
    w2 = nc.dram_tensor("mlp_w2", [4096, D], F32, kind="ExternalInput").ap()
    wg = nc.dram_tensor("ple_w_gate", [D, D], F32, kind="ExternalInput").ap()
    wp = nc.dram_tensor("ple_w_proj", [256, D], F32, kind="ExternalInput").ap()
    out = nc.dram_tensor("xout", [D, NT], F32, kind="ExternalOutput").ap()
    if debug:
        dbg_a = nc.dram_tensor("dbg_a", [D, NT], F32, kind="ExternalOutput").ap()
        dbg_m = nc.dram_tensor("dbg_m", [D, NT], F32, kind="ExternalOutput").ap()
    cx = Ctx(nc)
    P = cx.P
    cf, cb, ctk = load_consts(cx, None, cst, G0_END)
    cx.eps_col = cf[:, C_EPS:C_EPS + 1]
    ones_bf = cb[:, C_ONES:C_ONES + 128]
    TOP = cx.sb(None, [128, 16384], F32, "TOP")
    R1 = cx.sb(None, [128, 12288], F32, "R1")
    mk = cx.mark()
    X, Xtk = emit_attention(cx, xT_ext, wqkv, wo, cf, cb, ctk, TOP, R1, lvl=lvl, hps=hps)
    cx.release(mk)
    cx.top = cx.top - 12288 * 4
    if debug:
        emit_store(cx, X, Xtk, dbg_a)
    emit_mlp(cx, X, Xtk, cf[:, G0_MLPN:G0_MLPN + 8], ctk, w1, w2, ones_bf)
    if debug:
        emit_store(cx, X, Xtk, dbg_m)
    emit_ple(cx, X, Xtk, cf[:, G0_PLEN:G0_PLEN + 8], ctk, wg, wp, pT, ones_bf)
    emit_store(cx, X, Xtk, out)
    P.finish()
    return nc, cx


def layer0_inputs(inputs, core):
    x = inputs["x"][0]
    lo = core * NT
    xe = np.zeros((2 * NT, D), np.float32)
    if core > 0:
        xe[:NT] = x[lo - NT:lo]
    xe[NT:] = x[lo:lo + NT]
    c = base_consts(core, G0_END)
    c[:, G0_ANORM:G0_ANORM + 8] = col_layout(inputs["a_norm"][0])
    c[:, G0_QG:G0_QG + 3] = np.tile(inputs["a_q_gain"][0].T, (2, 1))
    c[:, G0_KG:G0_KG + 3] = np.tile(inputs["a_k_gain"][0].T, (2, 1))
    c[:, G0_MLPN:G0_MLPN + 8] = col_layout(inputs["mlp_norm"][0])
    c[:, G0_PLEN:G0_PLEN + 8] = col_layout(inputs["ple_norm"][0])
    return {
        "xT_ext": np.ascontiguousarray(xe.T),
        "pT": np.ascontiguousarray(inputs["p"][0, 0, lo:lo + NT].T),
        "cst": c,
        "a_w_qkv": inputs["a_w_qkv"][0], "a_w_o": inputs["a_w_o"][0],
        "mlp_w1": inputs["mlp_w1"][0], "mlp_w2": inputs["mlp_w2"][0],
        "ple_w_gate": inputs["ple_w_gate"][0], "ple_w_proj": inputs["ple_w_proj"][0],
    }

BH = 4
DH = 512
NCK = NT // 128
KSCALE = DH ** -0.5
NST = 8192 + 2048 + 8

L_BNORM = C_GAINS
L_MLPN = L_BNORM + 8
L_PLEN = L_MLPN + 8
L_CONVW = L_PLEN + 8
L_CONVB = L_CONVW + 64
L_SKIP = L_CONVB + 16
L_HGAIN = L_SKIP + 16
L_BI = L_HGAIN + 16
L_BF = L_BI + 1
L_MASKLOW = L_BF + 1
L_SEL = L_MASKLOW + 128
L_CMASK = L_SEL + 512
L_NEG = L_CMASK + 7
L_E0 = L_NEG + 1
L_LNK = L_E0 + 1
L_CNEG = L_LNK + 1
L_END = L_CNEG + 7


def layer1_consts(inputs, core):
    c = base_consts(core, L_END)
    c[:, L_BNORM:L_BNORM + 8] = col_layout(inputs["b_norm"][0])
    c[:, L_MLPN:L_MLPN + 8] = col_layout(inputs["mlp_norm"][1])
    c[:, L_PLEN:L_PLEN + 8] = col_layout(inputs["ple_norm"][1])
    cw = inputs["b_conv_w"][0]
    c[:, L_CONVW:L_CONVW + 64] = cw.reshape(4, 16, 128).transpose(2, 1, 0).reshape(128, 64)
    c[:, L_CONVB:L_CONVB + 16] = col_layout(inputs["b_conv_b"][0])
    c[:, L_SKIP:L_SKIP + 16] = col_layout(inputs["b_skip"][0])
    c[:, L_HGAIN:L_HGAIN + 16] = col_layout(inputs["b_h_gain"][0])
    bg = inputs["b_b_gate"][0]
    c[0:4, L_BI] = bg[0:4]
    c[0:4, L_BF] = bg[4:8]
    s_ = np.arange(128)[:, None]
    t_ = np.arange(128)[None, :]
    c[:, L_MASKLOW:L_MASKLOW + 128] = np.where(s_ <= t_, 0.0, BIG)
    for hd in range(4):
        c[hd, L_SEL + hd * 128:L_SEL + (hd + 1) * 128] = 1.0
    for cp in range(7):
        c[:, L_CMASK + cp] = 1.0 if cp < core else 0.0
        c[:, L_CNEG + cp] = 0.0 if cp < core else -1e30
    c[:, L_NEG] = -1e30
    c[0, L_E0] = 1.0
    c[:, L_LNK] = np.log(KSCALE)
    return c


def bd_compact(w, transpose=False):
    out = np.zeros((2048, 128), np.float32)
    n = np.arange(512)
    for j in range(4):
        for k in range(4):
            if transpose:
                out[4 * n + k, (4 * n + j) % 128] = w[:, j, k]
            else:
                out[4 * n + j, (4 * n + k) % 128] = w[:, j, k]
    return out


def layer1_inputs(inputs, core, x1T_full, stage, st_all=None, g_in=None):
    lo = core * NT
    xh = np.zeros((D, 4), np.float32)
    if core > 0:
        xh[:, 1:4] = x1T_full[:, lo - 3:lo]
    m = {
        "x1T": np.ascontiguousarray(x1T_full[:, lo:lo + NT]),
        "xh": xh,
        "cst": layer1_consts(inputs, core),
        "b_w_up": inputs["b_w_up"][0],
        "bd": np.stack([bd_compact(inputs["b_w_q"][0]), bd_compact(inputs["b_w_k"][0]), bd_compact(inputs["b_w_v"][0])]),
        "bdT": np.stack([bd_compact(inputs["b_w_q"][0], True), bd_compact(inputs["b_w_k"][0], True),
                         bd_compact(inputs["b_w_v"][0], True)]),
        "w_gate": inputs["b_w_gate"][0],
    }
    if stage == "C":
        m.update({
            "st_all": st_all,
            "g_in": g_in,
            "b_w_down": inputs["b_w_down"][0],
            "pT": np.ascontiguousarray(inputs["p"][1, 0, lo:lo + NT].T),
            "mlp_w1": inputs["mlp_w1"][1], "mlp_w2": inputs["mlp_w2"][1],
            "ple_w_gate": inputs["ple_w_gate"][1], "ple_w_proj": inputs["ple_w_proj"][1],
        })
    return m


def build_layer1(stage, debug=False, dbg_stop=None):
    nc = bass.Bass("TRN2", target_bir_lowering=False)
    x1T = nc.dram_tensor("x1T", [D, NT], F32, kind="ExternalInput").ap()
    xh = nc.dram_tensor("xh", [D, 4], F32, kind="ExternalInput").ap()
    cst = nc.dram_tensor("cst", [128, L_END], F32, kind="ExternalInput").ap()
    wup = nc.dram_tensor("b_w_up", [D, 4096], F32, kind="ExternalInput").ap()
    bd = nc.dram_tensor("bd", [3, 2048, 128], F32, kind="ExternalInput").ap()
    bdT = nc.dram_tensor("bdT", [3, 2048, 128], F32, kind="ExternalInput").ap()
    wgate = nc.dram_tensor("w_gate", [6144, 8], F32, kind="ExternalInput").ap()
    if stage == "B":
        st_out = nc.dram_tensor("st_out", [128, NST], F32, kind="ExternalOutput").ap()
        g_out = nc.dram_tensor("g_out", [8, NT], F32, kind="ExternalOutput").ap()
    else:
        st_all = nc.dram_tensor("st_all", [7, 128, NST], F32, kind="ExternalInput").ap()
        g_in = nc.dram_tensor("g_in", [8, NT], F32, kind="ExternalInput").ap()
        wdown = nc.dram_tensor("b_w_down", [2048, D], F32, kind="ExternalInput").ap()
        pT = nc.dram_tensor("pT", [256, NT], F32, kind="ExternalInput").ap()
        w1 = nc.dram_tensor("mlp_w1", [D, 4096], F32, kind="ExternalInput").ap()
        w2 = nc.dram_tensor("mlp_w2", [4096, D], F32, kind="ExternalInput").ap()
        wg = nc.dram_tensor("ple_w_gate", [D, D], F32, kind="ExternalInput").ap()
        wp = nc.dram_tensor("ple_w_proj", [256, D], F32, kind="ExternalInput").ap()
        out = nc.dram_tensor("xout", [D, NT], F32, kind="ExternalOutput").ap()
        yscr = nc.dram_tensor("yscr", [2048, NT], BF16).ap()
        if debug:
            dbg_a = nc.dram_tensor("dbg_a", [D, NT], F32, kind="ExternalOutput").ap()
    cx = Ctx(nc)
    P = cx.P
    cf, cb, ctk = load_consts(cx, None, cst, L_END)
    cx.eps_col = cf[:, C_EPS:C_EPS + 1]
    ones_bf = cb[:, C_ONES:C_ONES + 128]
    ones_f = cf[:, C_ONES:C_ONES + 128]
    ident_f = cf[:, C_ID:C_ID + 128]
    one_col = cf[:, C_ONES:C_ONES + 1]
    TOP = cx.sb(None, [128, 16384], F32, "TOP")
    hT = TOP[:, 0:8208].bitcast(BF16)[:, 0:8 * 2052].rearrange("p (c n) -> p c n", c=NC8)
    topfree = TOP[:, 8208:16384]
    base_mark = cx.mark()
    bk = [(cx.banks[i], Tk()) for i in range(8)]
    ws = WStream(cx, None, 4096, nstage=0, nslot=3)
    ws.stage = Rot([topfree[:, 0:4096]])
    bdh = cx.sb(None, [128, 3, 4, 128], BF16, "bdh")
    bdh_st = cx.sb(None, [128, 3, 4, 128], F32, "bdh_st")
    diag = cx.sb(None, [128, 4, 4, 128], BF16, "diag")
    xms = [cx.sb(None, [128, 4, 516], BF16, "xm") for _ in range(2)]
    xc = cx.sb(None, [128, 4, 512], BF16, "xc")
    GF = cx.sb(None, [128, NT], F32, "GF")
    BETAx = cx.sb(None, [128, NT + 1], F32, "BETAx")
    small = cx.sb(None, [128, 64], F32, "small")
    TMw = cx.sb(None, [128, NCK, 4], F32, "TMw")
    TMa = cx.sb(None, [128, NCK, 4], F32, "TMa")
    if stage == "C":
        fold = cx.sb(None, [128, 7, 8], F32, "foldin")
        S1 = cx.sb(None, [128, 7, 4], F32, "S1")
        S2 = cx.sb(None, [128, 7, 4], F32, "S2")
        mrun = cx.sb(None, [128, 4], F32, "mrun")
        fa = cx.sb(None, [128, 4], F32, "fa")
        fb = cx.sb(None, [128, 4], F32, "fb")
        fc_ = cx.sb(None, [128, 4], F32, "fc")
    pers_mark = cx.mark()

    mk = cx.mark()
    sq_rot = Rot([cx.sb(None, [128, 512], BF16, "sq") for _ in range(2)])
    rstd_rot = Rot([cx.sb(None, [128, 512], F32, "rstd") for _ in range(2)])
    xstg = [cx.sb(None, [128, NC8, 512], F32, "xstg") for _ in range(2)]
    xstk = [Tk(), Tk()]
    ps_stat = Rot([bk[7], bk[6]])
    gcol = cf[:, L_BNORM:L_BNORM + 8]
    httk = Tk()
    pieces = [(None, 4)] + [(tg, 512) for tg in range(4)]
    for i, (tg, n) in enumerate(pieces):
        xa = xstg[i % 2]
        xt = xstk[i % 2]
        if tg is None:
            P.dma("sync", xa[:, :, 0:4], xh.rearrange("(c p) n -> p c n", p=128), writes=[xt])
            h0 = 0
        else:
            P.dma("sync", xa, x1T[:, tg * 512:(tg + 1) * 512].rearrange("(c p) n -> p c n", p=128), writes=[xt])
            h0 = 4 + tg * 512
        xs = [(xa[:, c, 0:n], xt) for c in range(NC8)]
        ps_ap, ps_tk = ps_stat.next()
        rstd, rtk = rstd_rot.next()
        rms_stats(cx, xs, n, sq_rot, ps_ap, ps_tk, rstd, rtk, ones_bf, ctk, 1.0 / D)
        for c in range(NC8):
            P.op("dve", lambda e, c=c, xa=xa, rstd=rstd, n=n, h0=h0: e.scalar_tensor_tensor(
                out=hT[:, c, h0:h0 + n], in0=xa[:, c, 0:n], scalar=gcol[:, c:c + 1], in1=rstd[:, 0:n],
                op0=ALU.mult, op1=ALU.mult), reads=[xt, rtk, ctk], writes=[httk])
    P.barrier()
    cx.release(mk)

    GI = cx.sb(None, [128, NT], F32, "GI")
    LF = cx.sb(None, [128, NT], F32, "LF")
    BB = cx.sb(None, [128, NT], F32, "BB")
    T1 = cx.sb(None, [128, NT], F32, "T1")
    wfold = [[cx.sb(None, [128, 16, 128], BF16, "wfold") for _ in range(2)] for _ in range(2)]
    wftk = Tk()
    bdT_sb = topfree[:, 0:6144].rearrange("p (a b) -> p a b", a=48)
    wg_sb = topfree[:, 6144:6528].rearrange("p (a b) -> p a b", a=48)
    btk = Tk()
    for j in range(3 if stage == "B" else 0):
        P.dma("sync", bdT_sb[:, j * 16:(j + 1) * 16, :], bdT[j].rearrange("(c p) n -> p c n", p=128), writes=[btk])
    if stage == "B":
        P.dma("sync", wg_sb, wgate.rearrange("(c p) n -> p c n", p=128), writes=[btk])
    zpad = Rot([topfree[:, 6528 + i * 128:6528 + (i + 1) * 128] for i in range(4)])
    for (za, ztk) in zpad.items:
        P.op("pool", lambda e, za=za: e.memset(za, 0.0), writes=[ztk])
    psF = Rot(bk[0:2])
    for mc in range(16 if stage == "B" else 0):
        for part in range(2):
            for xm_ in range(2):
                ps, pstk = psF.next()
                srcs = (0, 1) if xm_ == 0 else (2,)
                for si, j in enumerate(srcs):
                    za, ztk = zpad.next()
                    P.op("dve", lambda e, za=za, j=j, mc=mc, part=part: e.tensor_copy(
                        out=za[:, 0:4], in_=wg_sb[:, j * 16 + mc, part * 4:part * 4 + 4]), reads=[btk], writes=[ztk])
                    P.op("pe", lambda e, ps=ps, za=za, j=j, mc=mc, si=si, srcs=srcs: e.matmul(
                        ps[:, 0:128], lhsT=bdT_sb[:, j * 16 + mc, :], rhs=za, start=(si == 0), stop=(si == len(srcs) - 1)),
                        reads=[btk, ztk], writes=[pstk])
                P.op("act", lambda e, ps=ps, xm_=xm_, part=part, mc=mc: e.activation(
                    out=wfold[xm_][part][:, mc, :], in_=ps[:, 0:128], func=AF.Copy), reads=[pstk], writes=[wftk])
    P.barrier()

    hdtk = Tk()
    xmtk = [Tk(), Tk()]
    xctk = Tk()
    psA = Rot(bk[0:2])
    wupv = wup.rearrange("(c p) n -> p c n", p=128)
    bdv = bd.rearrange("j (c p) n -> p j c n", p=128)
    state = {"i": 0}

    def head_setup(hd):
        for j in range(3):
            P.dma("sync", bdh_st[:, j, :, :], bdv[:, j, hd * 4:(hd + 1) * 4, :], writes=[hdtk])
        P.op("pool", lambda e: e.tensor_copy(out=bdh, in_=bdh_st), reads=[hdtk], writes=[hdtk])
        for mc in range(4):
            for k in range(4):
                col = L_CONVW + (hd * 4 + mc) * 4 + k
                P.op("pool", lambda e, mc=mc, k=k, col=col: e.tensor_scalar(
                    out=diag[:, mc, k, :], in0=ident_f, scalar1=cf[:, col:col + 1], scalar2=None, op0=ALU.mult),
                    reads=[ctk], writes=[hdtk])
        wx, wxtk = ws.load([wupv[:, :, hd * 512:(hd + 1) * 512]])
        return wx.rearrange("p (c n) -> p c n", c=NC8), wxtk

    def front(hd, tg, wx3, wxtk):
        i = state["i"]
        state["i"] += 1
        xm, xmt = xms[i % 2], xmtk[i % 2]
        xmp, xmpt = xms[(i + 1) % 2], xmtk[(i + 1) % 2]
        for mc in range(4):
            ps, pstk = psA.next()
            for c in range(NC8):
                P.op("pe", lambda e, c=c, mc=mc, ps=ps: e.matmul(
                    ps, lhsT=wx3[:, c, mc * 128:(mc + 1) * 128], rhs=hT[:, c, 4 + tg * 512:4 + (tg + 1) * 512],
                    start=(c == 0), stop=(c == NC8 - 1)), reads=[wxtk], writes=[pstk], signal=(c == NC8 - 1))
            P.op("act", lambda e, ps=ps, mc=mc, xm=xm: e.activation(out=xm[:, mc, 4:516], in_=ps, func=AF.Copy),
                 reads=[pstk], writes=[xmt])
            if tg == 0:
                ps, pstk = psA.next()
                for c in range(NC8):
                    P.op("pe", lambda e, c=c, mc=mc, ps=ps: e.matmul(
                        ps[:, 0:4], lhsT=wx3[:, c, mc * 128:(mc + 1) * 128], rhs=hT[:, c, 0:4],
                        start=(c == 0), stop=(c == NC8 - 1)), reads=[wxtk], writes=[pstk], signal=(c == NC8 - 1))
                P.op("act", lambda e, ps=ps, mc=mc, xm=xm: e.activation(out=xm[:, mc, 0:4], in_=ps[:, 0:4], func=AF.Copy),
                     reads=[pstk], writes=[xmt])
        if tg > 0:
            P.op("pool", lambda e, xm=xm, xmp=xmp: e.tensor_copy(out=xm[:, :, 0:4], in_=xmp[:, :, 512:516]),
                 reads=[xmpt], writes=[xmt])
        for mc in range(4):
            ps, pstk = psA.next()
            for k in range(4):
                P.op("pe", lambda e, k=k, mc=mc, ps=ps, xm=xm: e.matmul(
                    ps, lhsT=diag[:, mc, k, :], rhs=xm[:, mc, 1 + k:1 + k + 512], start=(k == 0), stop=(k == 3)),
                    reads=[hdtk, xmt], writes=[pstk], signal=(k == 3))
            col = L_CONVB + hd * 4 + mc
            P.op("act", lambda e, ps=ps, mc=mc, col=col: e.activation(
                out=xc[:, mc, :], in_=ps, func=AF.Silu, bias=cf[:, col:col + 1]), reads=[pstk, ctk], writes=[xctk])
        return xm, xmt

    gtk = Tk()
    psG = Rot(bk[2:4])
    if stage == "C":
        P.op("pool", lambda e: e.memset(GI, 0.0), writes=[gtk])
        P.op("pool", lambda e: e.memset(GF, 0.0), writes=[gtk])
        P.dma("sync", GI[0:4, :], g_in[0:4, :], writes=[gtk])
        P.dma("sync", GF[0:4, :], g_in[4:8, :], writes=[gtk])
    for hd in range(BH if stage == "B" else 0):
        wx3, wxtk = head_setup(hd)
        for tg in range(4):
            xm, xmt = front(hd, tg, wx3, wxtk)
            for part, Grow in ((0, GI), (1, GF)):
                ps, pstk = psG.next()
                for mc in range(4):
                    P.op("pe", lambda e, ps=ps, mc=mc, part=part: e.matmul(
                        ps, lhsT=wfold[0][part][:, hd * 4 + mc, :], rhs=xc[:, mc, :], start=(mc == 0), stop=False),
                        reads=[wftk, xctk], writes=[pstk], signal=False)
                    P.op("pe", lambda e, ps=ps, mc=mc, part=part, xm=xm: e.matmul(
                        ps, lhsT=wfold[1][part][:, hd * 4 + mc, :], rhs=xm[:, mc, 4:516], start=False, stop=(mc == 3)),
                        reads=[wftk, xmt], writes=[pstk], signal=(mc == 3))
                sl = slice(tg * 512, (tg + 1) * 512)
                if hd == 0:
                    P.op("act", lambda e, ps=ps, Grow=Grow, sl=sl: e.activation(out=Grow[:, sl], in_=ps, func=AF.Copy),
                         reads=[pstk], writes=[gtk])
                else:
                    P.op("dve", lambda e, ps=ps, Grow=Grow, sl=sl: e.tensor_tensor(out=Grow[:, sl], in0=ps, in1=Grow[:, sl], op=ALU.add),
                         reads=[pstk, gtk], writes=[gtk])

    rtk = Tk()
    if stage == "B":
        P.dma("sync", g_out[0:4, :], GI[0:4, :], reads=[gtk])
        P.dma("sync", g_out[4:8, :], GF[0:4, :], reads=[gtk])
    P.op("dve", lambda e: e.tensor_scalar(out=GI, in0=GI, scalar1=cf[:, L_BI:L_BI + 1], scalar2=None, op0=ALU.add),
         reads=[gtk, ctk], writes=[gtk])
    P.op("dve", lambda e: e.tensor_scalar(out=GF, in0=GF, scalar1=cf[:, L_BF:L_BF + 1], scalar2=None, op0=ALU.add),
         reads=[gtk, ctk], writes=[gtk])
    P.op("dve", lambda e: e.tensor_scalar(out=T1, in0=GF, scalar1=-1.0, scalar2=None, op0=ALU.mult), reads=[gtk], writes=[rtk])
    P.op("dve", lambda e: e.tensor_tensor(out=T1, in0=T1, in1=GF, op=ALU.max), reads=[gtk, rtk], writes=[rtk])
    P.op("act", lambda e: e.activation(out=T1, in_=T1, func=AF.Exp, scale=-1.0), reads=[rtk], writes=[rtk])
    P.op("act", lambda e: e.activation(out=T1, in_=T1, func=AF.Ln, bias=one_col), reads=[rtk, ctk], writes=[rtk])
    P.op("dve", lambda e: e.scalar_tensor_tensor(out=LF, in0=GF, scalar=0.0, in1=T1, op0=ALU.min, op1=ALU.subtract),
         reads=[gtk, rtk], writes=[rtk])
    P.op("pool", lambda e: e.memset(T1, 1.0), reads=[rtk], writes=[rtk])
    P.op("dve", lambda e: e.tensor_tensor_scan(out=BB, data0=T1, data1=LF, initial=0.0, op0=ALU.mult, op1=ALU.add),
         reads=[rtk], writes=[rtk])
    P.op("dve", lambda e: e.tensor_tensor(out=T1, in0=GI, in1=BB, op=ALU.subtract), reads=[gtk, rtk], writes=[rtk])
    ALPHA = T1
    psR = Rot([bk[4]])
    tmtk = Tk()

    def to_token_major(row, dst):
        ps, pstk = psR.next()
        for ck in range(NCK):
            P.op("pe", lambda e, ps=ps, ck=ck: e.matmul(ps[:, ck * 4:ck * 4 + 4], lhsT=row[:, ck * 128:(ck + 1) * 128],
                                                        rhs=ident_f[:, 0:4], start=True, stop=True),
                 reads=[rtk, gtk, ctk], writes=[pstk], signal=(ck == NCK - 1))
        P.op("act", lambda e, ps=ps: e.activation(out=dst, in_=ps[:, 0:64].rearrange("p (a b) -> p a b", a=NCK), func=AF.Copy),
             reads=[pstk], writes=[tmtk])

    def replicate_cols(col_ap, dst4):
        ps, pstk = psR.next()
        za = small[:, 32:32 + 4]
        P.op("dve", lambda e: e.tensor_scalar(out=za, in0=ident_f[:, 0:4], scalar1=col_ap, scalar2=None, op0=ALU.mult),
             reads=[rtk, ctk, gtk], writes=[rtk])
        P.op("pe", lambda e, ps=ps: e.matmul(ps[:, 0:4], lhsT=ones_f, rhs=za, start=True, stop=True),
             reads=[rtk, ctk], writes=[pstk])
        P.op("act", lambda e, ps=ps: e.activation(out=dst4, in_=ps[:, 0:4], func=AF.Copy), reads=[pstk], writes=[rtk])

    if stage == "B":
        mx = small[:, 0:1]
        P.op("dve", lambda e: e.tensor_reduce(out=mx, in_=ALPHA, axis=AX.X, op=ALU.max), reads=[rtk], writes=[rtk])
        nb_ = small[:, 1:2]
        P.op("dve", lambda e: e.scalar_tensor_tensor(out=nb_, in0=mx, scalar=-1.0, in1=cf[:, L_LNK:L_LNK + 1],
                                                     op0=ALU.mult, op1=ALU.add), reads=[rtk, ctk], writes=[rtk])
        P.op("act", lambda e: e.activation(out=LF, in_=ALPHA, func=AF.Exp, bias=nb_), reads=[rtk], writes=[rtk])
        to_token_major(LF, TMw)
        ml = small[:, 2:3]
        P.op("dve", lambda e: e.tensor_tensor(out=ml, in0=mx, in1=BB[:, NT - 1:NT], op=ALU.add), reads=[rtk], writes=[rtk])
        fin = cx.sb(None, [128, 8], F32, "fin")
        replicate_cols(BB[:, NT - 1:NT], fin[:, 0:4])
        replicate_cols(ml, fin[:, 4:8])
        P.dma("sync", st_out[:, 10240:10248], fin, reads=[rtk])
        P.barrier()
        cx.release(pers_mark)
        kv_rot = Rot([cx.sb(None, [128, 512], BF16, "kv") for _ in range(4)])
        stC = cx.sb(None, [128, 4, 512], F32, "stC")
        stn = cx.sb(None, [128, 512], F32, "stn")
        sttk = Tk()
        psKV = Rot([bk[2]])
        for hd in range(BH):
            wx3, wxtk = head_setup(hd)
            cacc = [bk[3 + dc] for dc in range(4)]
            nacc, nacctk = bk[7]
            for tg in range(4):
                xm, xmt = front(hd, tg, wx3, wxtk)
                for cl in range(4):
                    ck = tg * 4 + cl
                    tsl = slice(cl * 128, (cl + 1) * 128)
                    ps, pstk = psKV.next()
                    for mc in range(4):
                        P.op("pe", lambda e, ps=ps, mc=mc, tsl=tsl: e.matmul(
                            ps[:, mc * 128:(mc + 1) * 128], lhsT=xc[:, mc, tsl], rhs=bdh[:, 1, mc, :], start=True, stop=True),
                            reads=[xctk, hdtk], writes=[pstk], signal=(mc == 3))
                    wk, wktk = kv_rot.next()
                    P.op("act", lambda e, ps=ps, wk=wk, ck=ck, hd=hd: e.activation(
                        out=wk, in_=ps, func=AF.Copy, scale=TMw[:, ck, hd:hd + 1]), reads=[pstk, tmtk], writes=[wktk])
                    ps, pstk = psKV.next()
                    for mc in range(4):
                        P.op("pe", lambda e, ps=ps, mc=mc, cl=cl, xm=xm: e.matmul(
                            ps[:, mc * 128:(mc + 1) * 128], lhsT=xm[:, mc, 4 + cl * 128:4 + (cl + 1) * 128], rhs=bdh[:, 2, mc, :],
                            start=True, stop=True), reads=[xmt, hdtk], writes=[pstk], signal=(mc == 3))
                    vv, vtk = kv_rot.next()
                    P.op("act", lambda e, ps=ps, vv=vv: e.activation(out=vv, in_=ps, func=AF.Copy), reads=[pstk], writes=[vtk])
                    last = (ck == NCK - 1)
                    for dc in range(4):
                        P.op("pe", lambda e, dc=dc, wk=wk, vv=vv, ck=ck, last=last: e.matmul(
                            cacc[dc][0], lhsT=wk[:, dc * 128:(dc + 1) * 128], rhs=vv, start=(ck == 0), stop=last),
                            reads=[wktk, vtk], writes=[cacc[dc][1]], signal=True)
                    P.op("pe", lambda e, wk=wk, ck=ck, last=last: e.matmul(
                        nacc, lhsT=ones_bf, rhs=wk, start=(ck == 0), stop=last), reads=[wktk, ctk], writes=[nacctk], signal=True)
            for dc in range(4):
                P.op("act", lambda e, dc=dc: e.activation(out=stC[:, dc, :], in_=cacc[dc][0], func=AF.Copy),
                     reads=[cacc[dc][1]], writes=[sttk])
            P.op("dve", lambda e: e.tensor_copy(out=stn, in_=nacc), reads=[nacctk], writes=[sttk])
            P.dma("sync", st_out[:, hd * 2048:(hd + 1) * 2048], stC.rearrange("p a b -> p (a b)"), reads=[sttk])
            P.dma("sync", st_out[:, 8192 + hd * 512:8192 + (hd + 1) * 512], stn, reads=[sttk])
        P.finish()
        return nc, cx

    ftk = Tk()
    P.dma("sync", fold, st_all[:, :, 10240:10248].rearrange("c p n -> p c n"), writes=[ftk])
    negc = cf[:, L_NEG:L_NEG + 1]
    P.op("dve", lambda e: e.memset(mrun, -1e30), writes=[ftk])
    for cp in range(7):
        mu = cf[:, L_CMASK + cp:L_CMASK + cp + 1]
        P.op("dve", lambda e, cp=cp, mu=mu: e.scalar_tensor_tensor(out=fa, in0=fold[:, cp, 0:4], scalar=mu, in1=mrun,
                                                                    op0=ALU.mult, op1=ALU.add), reads=[ftk, ctk], writes=[ftk])
        P.op("dve", lambda e, cp=cp, mu=mu: e.tensor_scalar(out=fb, in0=fold[:, cp, 4:8], scalar1=mu,
                                                            scalar2=cf[:, L_CNEG + cp:L_CNEG + cp + 1], op0=ALU.mult, op1=ALU.add),
             reads=[ftk, ctk], writes=[ftk])
        P.op("dve", lambda e: e.tensor_tensor(out=fc_, in0=fa, in1=fb, op=ALU.max), reads=[ftk], writes=[ftk])
        P.op("dve", lambda e: e.tensor_tensor(out=fa, in0=fa, in1=fc_, op=ALU.subtract), reads=[ftk], writes=[ftk])
        P.op("dve", lambda e: e.tensor_tensor(out=fb, in0=fb, in1=fc_, op=ALU.subtract), reads=[ftk], writes=[ftk])
        P.op("act", lambda e, cp=cp: e.activation(out=S1[:, cp, :], in_=fa, func=AF.Exp), reads=[ftk], writes=[ftk])
        P.op("act", lambda e: e.activation(out=fb, in_=fb, func=AF.Exp), reads=[ftk], writes=[ftk])
        P.op("dve", lambda e, cp=cp, mu=mu: e.tensor_scalar(out=S2[:, cp, :], in0=fb, scalar1=mu, scalar2=None, op0=ALU.mult),
             reads=[ftk, ctk], writes=[ftk])
        P.op("dve", lambda e: e.tensor_copy(out=mrun, in_=fc_), reads=[ftk], writes=[ftk])
    mst = small[:, 4:5]
    P.op("dve", lambda e: e.tensor_tensor(out=small[:, 8:12], in0=mrun, in1=ident_f[:, 0:4], op=ALU.mult), reads=[ftk, ctk], writes=[rtk])
    P.op("dve", lambda e: e.tensor_reduce(out=mst, in_=small[:, 8:12], axis=AX.X, op=ALU.add), reads=[rtk], writes=[rtk])
    P.op("dve", lambda e: e.tensor_tensor_scan(out=GF, data0=LF, data1=GI, initial=mst, op0=ALU.add, op1=ALU.max),
         reads=[rtk, gtk], writes=[gtk])
    MM = GF
    P.op("dve", lambda e: e.tensor_tensor(out=BETAx[:, 1:NT + 1], in0=MM, in1=BB, op=ALU.subtract), reads=[gtk, rtk], writes=[rtk])
    P.op("dve", lambda e: e.tensor_copy(out=BETAx[:, 0:1], in_=mst), reads=[rtk], writes=[rtk])
    BETA = BETAx[:, 1:NT + 1]
    for ck in range(NCK):
        bl = small[:, 16:17]
        P.op("dve", lambda e, ck=ck: e.scalar_tensor_tensor(out=small[:, 16 + ck % 8:17 + ck % 8], in0=BETAx[:, 128 * (ck + 1):128 * (ck + 1) + 1],
                                                            scalar=-1.0, in1=cf[:, L_LNK:L_LNK + 1], op0=ALU.mult, op1=ALU.add),
             reads=[rtk, ctk], writes=[rtk])
        P.op("act", lambda e, ck=ck: e.activation(out=LF[:, ck * 128:(ck + 1) * 128], in_=ALPHA[:, ck * 128:(ck + 1) * 128],
                                                  func=AF.Exp, bias=small[:, 16 + ck % 8:17 + ck % 8]), reads=[rtk], writes=[rtk])
    to_token_major(LF, TMw)
    to_token_major(ALPHA, TMa)
    P.barrier()
    cx.release(pers_mark)
    BETA = BETAx[:, 1:NT + 1]

    qT = cx.sb(None, [128, 4, 512], BF16, "qT")
    kT = cx.sb(None, [128, 4, 512], BF16, "kT")
    zs = cx.sb(None, [128, 4, 512], BF16, "zs")
    yb = cx.sb(None, [128, 4, 512], BF16, "yb")
    qktk, zstk, ytk = Tk(), Tk(), Tk()
    Cst = cx.sb(None, [128, 4, 512], F32, "Cst")
    Caug = cx.sb(None, [128, 4, 640], BF16, "Caug")
    nrow = cx.sb(None, [128, 512], F32, "nrow")
    nm = cx.sb(None, [128, 512], F32, "nm")
    ctk2 = Tk()
    clst = Rot([topfree[:, 4096:6144], topfree[:, 6144:8176][:, 0:2032]])
    wk_rot = Rot([cx.sb(None, [128, 512], BF16, "wk") for _ in range(2)])
    va_rot = Rot([cx.sb(None, [128, 640], BF16, "vaug") for _ in range(2)])
    for (va, vatk) in va_rot.items:
        P.op("pool", lambda e, va=va: e.memset(va[:, 512:640], 1.0), writes=[vatk])
    dt_rot = Rot([cx.sb(None, [128, 128], F32, "dtmp") for _ in range(2)])
    sd_rot = Rot([cx.sb(None, [128, 128], BF16, "SdT") for _ in range(2)])
    qs_rot = Rot([cx.sb(None, [128, 4, 128], BF16, "qs") for _ in range(2)])
    hsq_rot = Rot([cx.sb(None, [128, 512], BF16, "hsq") for _ in range(2)])
    dd_rot = Rot([cx.sb(None, [128, 128], F32, "dd") for _ in range(2)])
    rr_rot = Rot([cx.sb(None, [128, 128], F32, "rr") for _ in range(2)])
    sc_rot = Rot([cx.sb(None, [128, 128], F32, "scsb") for _ in range(2)])
    em_rot = Rot([cx.sb(None, [128, 128], F32, "emsb") for _ in range(2)])
    ul_rot = Rot([cx.sb(None, [128, 1], F32, "ulast") for _ in range(2)])
    tt_rot = Rot([cx.sb(None, [128, 128], F32, "tt") for _ in range(3)])
    psB2 = psA
    psS3 = Rot([bk[2]])
    psRP = Rot([bk[2]])
    psH = Rot([bk[4], bk[5]])
    psDS = Rot([bk[6], bk[7]])
    psSS = Rot([bk[3]])
    wzv = wupv
    yview = yscr.rearrange("(c p) n -> p c n", p=128)
    ul_prev = None
    for hd in range(BH):
        wx3, wxtk = head_setup(hd)
        wz, wztk = ws.load([wzv[:, :, 2048 + hd * 512:2048 + (hd + 1) * 512]])
        wz3 = wz.rearrange("p (c n) -> p c n", c=NC8)
        P.op("pool", lambda e: e.memset(Cst, 0.0), writes=[ctk2])
        P.op("pool", lambda e: e.memset(nrow, 0.0), writes=[ctk2])
        Cflat = Cst.rearrange("p a b -> p (a b)")
        for cp in range(7):
            cl_, cltk = clst.items[0]
            P.dma("sync", cl_, st_all[cp][:, hd * 2048:(hd + 1) * 2048], writes=[cltk])
            P.op("act", lambda e, cp=cp, cl_=cl_: e.activation(out=cl_, in_=cl_, func=AF.Copy, scale=S2[:, cp, hd:hd + 1]),
                 reads=[cltk, ftk], writes=[cltk])
            P.op("dve", lambda e, cp=cp, cl_=cl_: e.scalar_tensor_tensor(out=Cflat, in0=Cflat, scalar=S1[:, cp, hd:hd + 1], in1=cl_,
                                                                          op0=ALU.mult, op1=ALU.add), reads=[cltk, ftk, ctk2], writes=[ctk2])
            nl_, nltk = clst.items[1]
            P.dma("sync", nl_[:, 0:512], st_all[cp][:, 8192 + hd * 512:8192 + (hd + 1) * 512], writes=[nltk])
            P.op("act", lambda e, cp=cp, nl_=nl_: e.activation(out=nl_[:, 0:512], in_=nl_[:, 0:512], func=AF.Copy, scale=S2[:, cp, hd:hd + 1]),
                 reads=[nltk, ftk], writes=[nltk])
            P.op("dve", lambda e, cp=cp, nl_=nl_: e.scalar_tensor_tensor(out=nrow, in0=nrow, scalar=S1[:, cp, hd:hd + 1], in1=nl_[:, 0:512],
                                                                          op0=ALU.mult, op1=ALU.add), reads=[nltk, ftk, ctk2], writes=[ctk2])

        def refresh_caug(full):
            if full:
                for dc in range(4):
                    P.op("act", lambda e, dc=dc: e.activation(out=Caug[:, dc, 0:512], in_=Cst[:, dc, :], func=AF.Copy),
                         reads=[ctk2], writes=[ctk2])
            P.op("dve", lambda e: e.tensor_scalar(out=nm, in0=nrow, scalar1=cf[:, L_E0:L_E0 + 1], scalar2=None, op0=ALU.mult),
                 reads=[ctk2, ctk], writes=[ctk2])
            ps, pstk = psB2.next()
            for dc in range(4):
                P.op("pe", lambda e, ps=ps, dc=dc: e.matmul(ps[:, dc * 128:(dc + 1) * 128], lhsT=nm[:, dc * 128:(dc + 1) * 128], rhs=ones_f,
                                                            start=True, stop=True), reads=[ctk2, ctk], writes=[pstk], signal=(dc == 3))
            P.op("act", lambda e, ps=ps: e.activation(out=Caug[:, :, 512:640], in_=ps.rearrange("p (a b) -> p a b", a=4), func=AF.Copy),
                 reads=[pstk], writes=[ctk2])

        refresh_caug(True)
        for tg in range(4):
            xm, xmt = front(hd, tg, wx3, wxtk)
            for mc in range(4):
                ps, pstk = psA.next()
                for c in range(NC8):
                    P.op("pe", lambda e, c=c, mc=mc, ps=ps: e.matmul(
                        ps, lhsT=wz3[:, c, mc * 128:(mc + 1) * 128], rhs=hT[:, c, 4 + tg * 512:4 + (tg + 1) * 512],
                        start=(c == 0), stop=(c == NC8 - 1)), reads=[wztk], writes=[pstk], signal=(c == NC8 - 1))
                P.op("act", lambda e, ps=ps, mc=mc: e.activation(out=zs[:, mc, :], in_=ps, func=AF.Silu), reads=[pstk], writes=[zstk])
            for j, dst, sc_ in ((0, qT, 1.0), (1, kT, KSCALE)):
                for dc in range(4):
                    ps, pstk = psA.next()
                    P.op("pe", lambda e, ps=ps, j=j, dc=dc: e.matmul(ps, lhsT=bdh[:, j, dc, :], rhs=xc[:, dc, :], start=True, stop=True),
                         reads=[hdtk, xctk], writes=[pstk])
                    P.op("act", lambda e, ps=ps, dst=dst, dc=dc, sc_=sc_: e.activation(out=dst[:, dc, :], in_=ps, func=AF.Copy, scale=sc_),
                         reads=[pstk], writes=[qktk])
            RS = {}

            def stage_pre(cl):
                nonlocal ul_prev
                ck = tg * 4 + cl
                tsl = slice(cl * 128, (cl + 1) * 128)
                gsl = slice(ck * 128, (ck + 1) * 128)
                sel = cf[:, L_SEL + hd * 128:L_SEL + (hd + 1) * 128]
                rp, rptk = psRP.next()
                for i3, row in enumerate((BETA, MM)):
                    P.op("pe", lambda e, rp=rp, i3=i3, row=row, gsl=gsl: e.matmul(
                        rp[:, i3 * 128:(i3 + 1) * 128], lhsT=sel, rhs=row[:, gsl], start=True, stop=True),
                        reads=[rtk, gtk, ctk], writes=[rptk], signal=(i3 == 1))
                bprev = mrun[:, hd:hd + 1] if ck == 0 else ul_prev[0]
                bprev_tk = ftk if ck == 0 else ul_prev[1]
                scsb, sctk = sc_rot.next()
                P.op("act", lambda e, rp=rp, scsb=scsb, bprev=bprev: e.activation(out=scsb, in_=rp[:, 0:128], func=AF.Exp, scale=-1.0, bias=bprev),
                     reads=[rptk, bprev_tk], writes=[sctk])
                emsb, emtk = em_rot.next()
                P.op("act", lambda e, rp=rp, emsb=emsb: e.activation(out=emsb, in_=rp[:, 128:256], func=AF.Exp, scale=-1.0),
                     reads=[rptk], writes=[emtk])
                ul_prev = ul_rot.next()
                P.op("act", lambda e, rp=rp, ul_prev=ul_prev: e.activation(out=ul_prev[0], in_=rp[:, 127:128], func=AF.Copy),
                     reads=[rptk], writes=[ul_prev[1]])
                ps, pstk = psB2.next()
                for mc in range(4):
                    P.op("pe", lambda e, ps=ps, mc=mc, tsl=tsl: e.matmul(
                        ps[:, mc * 128:(mc + 1) * 128], lhsT=xc[:, mc, tsl], rhs=bdh[:, 1, mc, :], start=True, stop=True),
                        reads=[xctk, hdtk], writes=[pstk], signal=(mc == 3))
                wk, wktk = wk_rot.next()
                P.op("act", lambda e, ps=ps, wk=wk, ck=ck: e.activation(out=wk, in_=ps, func=AF.Copy, scale=TMw[:, ck, hd:hd + 1]),
                     reads=[pstk, tmtk], writes=[wktk])
                ps, pstk = psB2.next()
                for mc in range(4):
                    P.op("pe", lambda e, ps=ps, mc=mc, cl=cl, xm=xm: e.matmul(
                        ps[:, mc * 128:(mc + 1) * 128], lhsT=xm[:, mc, 4 + cl * 128:4 + (cl + 1) * 128], rhs=bdh[:, 2, mc, :],
                        start=True, stop=True), reads=[xmt, hdtk], writes=[pstk], signal=(mc == 3))
                va, vatk = va_rot.next()
                P.op("act", lambda e, ps=ps, va=va: e.activation(out=va[:, 0:512], in_=ps, func=AF.Copy), reads=[pstk], writes=[vatk])
                pS_, pStk = psS3.next()
                pS = pS_[:, 256:384]
                for dc in range(4):
                    P.op("pe", lambda e, pS=pS, dc=dc, tsl=tsl: e.matmul(pS, lhsT=kT[:, dc, tsl], rhs=qT[:, dc, tsl],
                                                                          start=(dc == 0), stop=(dc == 3)),
                         reads=[qktk], writes=[pStk], signal=(dc == 3))
                dtmp, dttk = dt_rot.next()
                P.op("dve", lambda e, rp=rp, dtmp=dtmp, ck=ck: e.scalar_tensor_tensor(
                    out=dtmp, in0=rp[:, 0:128], scalar=TMa[:, ck, hd:hd + 1], in1=cf[:, L_MASKLOW:L_MASKLOW + 128],
                    op0=ALU.subtract, op1=ALU.max), reads=[rptk, tmtk, ctk], writes=[dttk])
                P.op("act", lambda e, dtmp=dtmp: e.activation(out=dtmp, in_=dtmp, func=AF.Exp, scale=-1.0), reads=[dttk], writes=[dttk])
                sd, sdtk = sd_rot.next()
                P.op("dve", lambda e, pS=pS, dtmp=dtmp, sd=sd: e.tensor_tensor(out=sd, in0=pS, in1=dtmp, op=ALU.mult),
                     reads=[pStk, dttk], writes=[sdtk])
                qs, qstk = qs_rot.next()
                P.op("dve", lambda e, scsb=scsb, qs=qs, tsl=tsl: e.tensor_tensor(
                    out=qs, in0=qT[:, :, tsl], in1=scsb.unsqueeze(1).to_broadcast([128, 4, 128]), op=ALU.mult),
                    reads=[qktk, sctk], writes=[qstk])

                RS[cl] = dict(ck=ck, tsl=tsl, wk=wk, wktk=wktk, va=va, vatk=vatk, sd=sd, sdtk=sdtk, qs=qs, qstk=qstk,
                              scsb=scsb, sctk=sctk, emsb=emsb, emtk=emtk)

            def stage_mid(cl):
                r_ = RS[cl]
                ck, tsl, wk, wktk, va, vatk, sd, sdtk, qs, qstk, scsb, sctk = (r_[k_] for k_ in (
                    "ck", "tsl", "wk", "wktk", "va", "vatk", "sd", "sdtk", "qs", "qstk", "scsb", "sctk"))
                pH, pHtk = psH.next()
                pD_, pDtk = psDS.next()
                for ec in range(5):
                    o = pH[:, ec * 128:(ec + 1) * 128] if ec < 4 else pD_[:, 0:128]
                    otk = pHtk if ec < 4 else pDtk
                    for dc in range(4):
                        P.op("pe", lambda e, o=o, ec=ec, dc=dc, qs=qs: e.matmul(
                            o, lhsT=Caug[:, dc, ec * 128:(ec + 1) * 128], rhs=qs[:, dc, :], start=(dc == 0), stop=False),
                            reads=[ctk2, qstk], writes=[otk], signal=False)
                    P.op("pe", lambda e, o=o, ec=ec, va=va, sd=sd: e.matmul(
                        o, lhsT=va[:, ec * 128:(ec + 1) * 128], rhs=sd, start=False, stop=True),
                        reads=[vatk, sdtk], writes=[otk], signal=True)

                r_.update(pH=pH, pHtk=pHtk, pD_=pD_, pDtk=pDtk)
                if dbg_stop is not None and (hd, ck) == tuple(dbg_stop):
                    P.barrier()
                    P.finish()
                    return nc, cx
                decay = scsb[:, 127:128]
                for dc in range(4):
                    ps, pstk = psB2.next()
                    P.op("pe", lambda e, ps=ps, dc=dc, wk=wk, va=va: e.matmul(ps, lhsT=wk[:, dc * 128:(dc + 1) * 128], rhs=va[:, 0:512],
                                                                                start=True, stop=True), reads=[wktk, vatk], writes=[pstk])
                    P.op("dve", lambda e, ps=ps, dc=dc, decay=decay: e.scalar_tensor_tensor(
                        out=Cst[:, dc, :], in0=Cst[:, dc, :], scalar=decay, in1=ps, op0=ALU.mult, op1=ALU.add),
                        reads=[pstk, sctk, ctk2], writes=[ctk2])
                    P.op("act", lambda e, dc=dc: e.activation(out=Caug[:, dc, 0:512], in_=Cst[:, dc, :], func=AF.Copy),
                         reads=[ctk2], writes=[ctk2])
                ps, pstk = psB2.next()
                P.op("pe", lambda e, ps=ps, wk=wk: e.matmul(ps, lhsT=ones_bf, rhs=wk, start=True, stop=True),
                     reads=[wktk, ctk], writes=[pstk])
                P.op("dve", lambda e, ps=ps, decay=decay: e.scalar_tensor_tensor(out=nrow, in0=nrow, scalar=decay, in1=ps,
                                                                                  op0=ALU.mult, op1=ALU.add),
                     reads=[pstk, sctk, ctk2], writes=[ctk2])
                refresh_caug(False)

            def stage_post(cl):
                r_ = RS[cl]
                ck, tsl, emsb, emtk, pH, pHtk, pD_, pDtk = (r_[k_] for k_ in ("ck", "tsl", "emsb", "emtk", "pH", "pHtk", "pD_", "pDtk"))
                hsq, hsqtk = hsq_rot.next()
                P.op("act", lambda e, pH=pH, hsq=hsq: e.activation(out=hsq, in_=pH, func=AF.Square), reads=[pHtk], writes=[hsqtk])
                pSS_, pSStk = psSS.next()
                pSS = pSS_[:, 0:128]
                for ec in range(4):
                    P.op("pe", lambda e, pSS=pSS, hsq=hsq, ec=ec: e.matmul(pSS, lhsT=ones_bf, rhs=hsq[:, ec * 128:(ec + 1) * 128],
                                                                            start=(ec == 0), stop=(ec == 3)),
                         reads=[hsqtk, ctk], writes=[pSStk], signal=(ec == 3))
                dd, ddtk = dd_rot.next()
                P.op("dve", lambda e, pD_=pD_, dd=dd: e.tensor_scalar(out=dd, in0=pD_[:, 0:128], scalar1=-1.0, scalar2=None, op0=ALU.mult),
                     reads=[pDtk], writes=[ddtk])
                P.op("dve", lambda e, pD_=pD_, dd=dd: e.tensor_tensor(out=dd, in0=dd, in1=pD_[:, 0:128], op=ALU.max),
                     reads=[pDtk, ddtk], writes=[ddtk])
                P.op("dve", lambda e, emsb=emsb, dd=dd: e.tensor_tensor(out=dd, in0=dd, in1=emsb, op=ALU.max),
                     reads=[emtk, ddtk], writes=[ddtk])
                P.op("dve", lambda e, dd=dd: e.scalar_tensor_tensor(out=dd, in0=dd, scalar=EPS, in1=dd, op0=ALU.mult, op1=ALU.mult),
                     reads=[ddtk], writes=[ddtk])
                rr, rrtk = rr_rot.next()
                P.op("dve", lambda e, pSS=pSS, dd=dd, rr=rr: e.scalar_tensor_tensor(out=rr, in0=pSS, scalar=1.0 / DH, in1=dd,
                                                                                     op0=ALU.mult, op1=ALU.add),
                     reads=[pSStk, ddtk], writes=[rrtk])
                P.op("act", lambda e, rr=rr: e.activation(out=rr, in_=rr, func=AF.Sqrt), reads=[rrtk], writes=[rrtk])
                P.op("dve", lambda e, rr=rr: e.reciprocal(out=rr, in_=rr), reads=[rrtk], writes=[rrtk])
                for ec in range(4):
                    ch = hd * 4 + ec
                    tt, tttk = tt_rot.next()
                    P.op("dve", lambda e, pH=pH, ec=ec, ch=ch, rr=rr, tt=tt: e.scalar_tensor_tensor(
                        out=tt, in0=pH[:, ec * 128:(ec + 1) * 128], scalar=cf[:, L_HGAIN + ch:L_HGAIN + ch + 1], in1=rr,
                        op0=ALU.mult, op1=ALU.mult), reads=[pHtk, rrtk, ctk], writes=[tttk])
                    P.op("dve", lambda e, ec=ec, ch=ch, tt=tt, tsl=tsl: e.scalar_tensor_tensor(
                        out=tt, in0=xc[:, ec, tsl], scalar=cf[:, L_SKIP + ch:L_SKIP + ch + 1], in1=tt,
                        op0=ALU.mult, op1=ALU.add), reads=[xctk, tttk, ctk], writes=[tttk])
                    P.op("dve", lambda e, ec=ec, tt=tt, tsl=tsl: e.tensor_tensor(out=yb[:, ec, tsl], in0=tt, in1=zs[:, ec, tsl], op=ALU.mult),
                         reads=[tttk, zstk], writes=[ytk])


            stage_pre(0)
            stage_mid(0)
            for cl in range(1, 4):
                stage_pre(cl)
                stage_post(cl - 1)
                stage_mid(cl)
            stage_post(3)

            P.dma("sync", yview[:, hd * 4:(hd + 1) * 4, tg * 512:(tg + 1) * 512], yb, reads=[ytk])
    P.barrier()
    cx.release(base_mark)

    X = TOP.rearrange("p (c n) -> p c n", c=NC8)
    Xtk = [[Tk() for _ in range(4)] for _ in range(NC8)]
    for c in range(NC8):
        for tg in range(4):
            P.dma("sync", X[:, c, tg * 512:(tg + 1) * 512], x1T[c * 128:(c + 1) * 128, tg * 512:(tg + 1) * 512], writes=[Xtk[c][tg]])
    mk = cx.mark()
    ws3 = WStream(cx, None, 4096, nstage=2, nslot=2)
    wdn = cx.sb(None, [128, 16, D], BF16, "wdn")
    wdtk = Tk()
    wdv = wdown.rearrange("(c p) n -> p c n", p=128)
    for q4 in range(4):
        wb_, wbtk = ws3.load([wdv[:, q4 * 4:(q4 + 1) * 4, :]])
        P.op("pool", lambda e, wb_=wb_, q4=q4: e.tensor_copy(out=wdn[:, q4 * 4:(q4 + 1) * 4, :], in_=wb_.rearrange("p (a b) -> p a b", a=4)),
             reads=[wbtk], writes=[wdtk])
    yts = [cx.sb(None, [128, 16, 512], BF16, "yt") for _ in range(2)]
    yttk = [Tk(), Tk()]
    psA4 = Rot(bk[0:4])
    for tg in range(4):
        yt, ytt = yts[tg % 2], yttk[tg % 2]
        P.dma("sync", yt, yview[:, :, tg * 512:(tg + 1) * 512], writes=[ytt])
        sl = slice(tg * 512, (tg + 1) * 512)
        for oc in range(NC8):
            ps, pstk = psA4.next()
            for mc in range(16):
                P.op("pe", lambda e, ps=ps, mc=mc, oc=oc, yt=yt: e.matmul(ps, lhsT=wdn[:, mc, oc * 128:(oc + 1) * 128], rhs=yt[:, mc, :],
                                                                           start=(mc == 0), stop=(mc == 15)),
                     reads=[wdtk, ytt], writes=[pstk], signal=(mc == 15))
            P.op("dve", lambda e, ps=ps, oc=oc, sl=sl: e.tensor_tensor(out=X[:, oc, sl], in0=ps, in1=X[:, oc, sl], op=ALU.add),
                 reads=[pstk, Xtk[oc][tg]], writes=[Xtk[oc][tg]])
    P.barrier()
    cx.release(mk)
    if debug:
        emit_store(cx, X, Xtk, dbg_a)
    emit_mlp(cx, X, Xtk, cf[:, L_MLPN:L_MLPN + 8], ctk, w1, w2, ones_bf)
    emit_ple(cx, X, Xtk, cf[:, L_PLEN:L_PLEN + 8], ctk, wg, wp, pT, ones_bf)
    emit_store(cx, X, Xtk, out)
    P.finish()
    return nc, cx


_CACHE = {}


def _prog(key, builder):
    return builder()


def kernel(**inputs):
    inputs = {k: np.asarray(v) for k, v in inputs.items()}
    cores = list(range(NCORES))
    nc, _ = build_layer0()
    in_maps = [layer0_inputs(inputs, c) for c in cores]
    res = run_bass_kernel_spmd(nc, in_maps, core_ids=cores)
    x1T = np.concatenate([r["xout"] for r in res.results], axis=1)
    nc, _ = build_layer1("B")
    in_maps = [layer1_inputs(inputs, c, x1T, "B") for c in cores]
    res = run_bass_kernel_spmd(nc, in_maps, core_ids=cores)
    st_all = np.stack([res.results[c]["st_out"] for c in range(7)])
    g_rows = [res.results[c]["g_out"] for c in cores]
    nc, _ = build_layer1("C")
    in_maps = [layer1_inputs(inputs, c, x1T, "C", st_all, g_rows[c]) for c in cores]
    res = run_bass_kernel_spmd(nc, in_maps, core_ids=cores)
    outT = np.concatenate([r["xout"] for r in res.results], axis=1)
    return np.ascontiguousarray(outT.T)[None].astype(np.float32)
```

```python
import numpy as np
import concourse.bass as bass
import concourse.mybir as mybir
from concourse.bass_utils import run_bass_kernel_spmd

F32 = mybir.dt.float32
BF16 = mybir.dt.bfloat16
AF = mybir.ActivationFunctionType
ALU = mybir.AluOpType
AX = mybir.AxisListType

NCORES = 8
S = 16384
D = 1024
NT = S // NCORES
NC8 = D // 128
EPS = 1e-6
BIG = 30000.0
A_GROUPS = ((128, 1), (512, 4), (2048, 16))
NDMA = 24
SB_F32 = 51968


class Tk:
    __slots__ = ("w", "r")

    def __init__(self):
        self.w = {}
        self.r = {}


class Prog:
    def __init__(self, nc):
        self.nc = nc
        self.eng = {"act": nc.scalar, "dve": nc.vector, "pool": nc.gpsimd, "pe": nc.tensor, "sync": nc.sync}
        self.sem = {e: nc.alloc_semaphore("s_" + e) for e in ("act", "dve", "pool", "pe")}
        self.cnt = {e: 0 for e in ("act", "dve", "pool", "pe")}
        self.seen = {e: {} for e in self.eng}
        self.dsem = [nc.alloc_semaphore("s_dma%d" % i) for i in range(NDMA)]
        self.dcnt = [0] * NDMA
        self.dnext = 0
        self.nins = {e: 0 for e in self.eng}

    def _semof(self, src):
        if isinstance(src, tuple):
            return self.dsem[src[1]]
        return self.sem[src]

    def _deps(self, e, reads, writes, allraw=False):
        deps = {}

        def add(src, n, raw):
            if src == e and not allraw:
                if e == "pe" or not raw:
                    return
            if deps.get(src, 0) < n:
                deps[src] = n

        for t in reads:
            for src, n in t.w.items():
                add(src, n, True)
        for t in writes:
            for src, n in t.w.items():
                add(src, n, False)
            for src, n in t.r.items():
                add(src, n, False)
        return deps

    def _wait(self, e, deps):
        eng = self.eng[e]
        seen = self.seen[e]
        for src, n in deps.items():
            if seen.get(src, 0) >= n:
                continue
            seen[src] = n
            eng.wait_ge(self._semof(src), n)
            self.nins[e] += 1

    def op(self, e, fn, reads=(), writes=(), signal=True):
        self._wait(e, self._deps(e, reads, writes))
        ins = fn(self.eng[e])
        self.nins[e] += 1
        n = self.cnt[e] + 1
        if signal:
            ins.then_inc(self.sem[e], 1)
            self.cnt[e] = n
        for t in reads:
            if t.r.get(e, 0) < n:
                t.r[e] = n
        for t in writes:
            if t.w.get(e, 0) < n:
                t.w[e] = n
        return ins

    def dma(self, q, out, in_, reads=(), writes=()):
        k = self.dnext
        self.dnext = (k + 1) % NDMA
        src = ("dma", k)
        deps = self._deps(q, reads, writes, allraw=True)
        if self.dcnt[k] > 0:
            deps[src] = max(deps.get(src, 0), self.dcnt[k])
        self._wait(q, deps)
        ins = self.eng[q].dma_start(out=out, in_=in_)
        self.nins[q] += 1
        n = self.dcnt[k] + 16
        ins.then_inc(self.dsem[k], 16)
        self.dcnt[k] = n
        for t in reads:
            t.r[src] = n
        for t in writes:
            t.w[src] = n

    def barrier(self):
        for e in self.eng:
            deps = {}
            for s2 in self.cnt:
                if s2 != e and self.cnt[s2] > 0:
                    deps[s2] = self.cnt[s2]
            for k in range(NDMA):
                if self.dcnt[k] > 0:
                    deps[("dma", k)] = self.dcnt[k]
            self._wait(e, deps)

    def finish(self):
        deps = {}
        for k in range(NDMA):
            if self.dcnt[k] > 0:
                deps[("dma", k)] = self.dcnt[k]
        self._wait("sync", deps)


class Rot:
    def __init__(self, aps):
        self.items = [a if isinstance(a, tuple) else (a, Tk()) for a in aps]
        self.i = 0

    def next(self):
        it = self.items[self.i]
        self.i = (self.i + 1) % len(self.items)
        return it


class Ctx:
    def __init__(self, nc):
        self.nc = nc
        self.P = Prog(nc)
        self.banks = [nc.alloc_psum_tensor("psb%d" % i, [128, 512], F32).ap() for i in range(8)]
        self.nalloc = 0

        self.big = nc.alloc_sbuf_tensor("big", [128, SB_F32], F32).ap()
        self.top = 0

    def sb(self, stack, shape, dt, name=None):
        esz = 2 if dt == BF16 else 4
        n = int(np.prod(shape[1:]))
        nbytes = (n * esz + 63) // 64 * 64
        off = self.top
        assert off + nbytes <= SB_F32 * 4, ("SBUF overflow", name, off, nbytes)
        self.top = off + nbytes
        self.log = getattr(self, 'log', [])
        self.log.append((name, off, nbytes))
        ap = self.big[:, off // 4:(off + nbytes) // 4]
        if dt == BF16:
            ap = ap.bitcast(BF16)
        ap = ap[:, 0:n]
        if len(shape) == 3:
            ap = ap.rearrange("p (a b) -> p a b", a=shape[1])
        elif len(shape) == 4:
            ap = ap.rearrange("p (a b c) -> p a b c", a=shape[1], b=shape[2])
        return ap

    def mark(self):
        return self.top

    def release(self, m):
        self.top = m


def load_consts(cx, stack, cst_ap, ncols):
    P = cx.P
    cf = cx.sb(stack, [128, ncols], F32, "cstf")
    cb = cx.sb(stack, [128, C_END_BF], BF16, "cstb")
    tk = Tk()
    P.dma("sync", cf, cst_ap, writes=[tk])
    P.op("dve", lambda e: e.tensor_copy(out=cb, in_=cf[:, 0:C_END_BF]), reads=[tk], writes=[tk])
    return cf, cb, tk


class WStream:
    def __init__(self, cx, stack, nelem, nstage=2, nslot=2):
        self.cx = cx
        self.nelem = nelem
        self.stage = Rot([cx.sb(stack, [128, nelem], F32, "wstg") for _ in range(nstage)])
        self.slots = Rot([cx.sb(stack, [128, nelem], BF16, "wbf") for _ in range(nslot)])

    def load(self, views):
        P = self.cx.P
        stg, stk = self.stage.next()
        wb, wtk = self.slots.next()
        off = 0
        for v in views:
            shp = v.shape
            n = int(np.prod(shp[1:]))
            dst = stg[:, off:off + n]
            if len(shp) == 3:
                dst = dst.rearrange("p (a b) -> p a b", a=shp[1])
            P.dma("sync", dst, v, writes=[stk])
            off += n
        assert off <= self.nelem
        P.op("pool", lambda e: e.tensor_copy(out=wb[:, 0:off], in_=stg[:, 0:off]), reads=[stk], writes=[wtk])
        return wb, wtk


def rms_stats(cx, xs, n, sq_rot, ps_ap, ps_tk, rstd, rstd_tk, ones_bf, ctk, inv_dim):
    P = cx.P
    nx = len(xs)
    for c, (xa, xt) in enumerate(xs):
        sq, sqt = sq_rot.next()
        P.op("act", lambda e, xa=xa, sq=sq: e.activation(out=sq[:, 0:n], in_=xa, func=AF.Square), reads=[xt], writes=[sqt])
        P.op("pe", lambda e, sq=sq, c=c: e.matmul(ps_ap[:, 0:n], lhsT=ones_bf, rhs=sq[:, 0:n], start=(c == 0), stop=(c == nx - 1)),
             reads=[sqt, ctk], writes=[ps_tk])
    P.op("act", lambda e: e.activation(out=rstd[:, 0:n], in_=ps_ap[:, 0:n], func=AF.Sqrt, bias=cx.eps_col, scale=inv_dim),
         reads=[ps_tk, ctk], writes=[rstd_tk])
    P.op("dve", lambda e: e.reciprocal(out=rstd[:, 0:n], in_=rstd[:, 0:n]), reads=[rstd_tk], writes=[rstd_tk])


C_ID, C_ONES, C_BONES, C_DM, C_HONES = 0, 128, 256, 384, 640
C_OZ = 704
C_HZ = 960
C_END_BF = 1216
C_EPS = 1216
C_GAINS = 1217
G0_ANORM = C_GAINS
G0_QG = G0_ANORM + 8
G0_KG = G0_QG + 3
G0_MLPN = G0_KG + 3
G0_PLEN = G0_MLPN + 8
G0_END = G0_PLEN + 8


def base_consts(core, ncols):
    c = np.zeros((128, ncols), np.float32)
    c[:, C_ID:C_ID + 128] = np.eye(128, dtype=np.float32)
    c[:, C_ONES:C_ONES + 128] = 1.0
    c[0:64, C_BONES:C_BONES + 64] = 1.0
    c[64:128, C_BONES + 64:C_BONES + 128] = 1.0
    kk = np.arange(128)[:, None]
    a = np.arange(128)[None, :]
    diag = np.where(kk <= a, a - kk, BIG)
    prev = np.where(kk >= a, 128 + a - kk, BIG)
    c[:, C_DM:C_DM + 128] = diag
    c[:, C_DM + 128:C_DM + 256] = prev
    hv = 0.0 if core == 0 else 1.0
    c[:, C_HONES:C_HONES + 64] = hv
    c[:, C_OZ:C_OZ + 64] = 1.0
    c[:, C_OZ + 128 + 64:C_OZ + 256] = 1.0
    c[:, C_HZ:C_HZ + 64] = hv
    c[:, C_HZ + 128 + 64:C_HZ + 256] = hv
    c[:, C_EPS] = EPS
    return c


def col_layout(v):
    v = np.asarray(v, np.float32).reshape(-1, 128)
    return np.ascontiguousarray(v.T)


def emit_norm_resident(cx, X, Xtk, gcol, ctk, hT, hTtk, sq_rot, rstd_rot, ps_rot, ones_bf):
    P = cx.P
    for tg in range(NT // 512):
        sl = slice(tg * 512, (tg + 1) * 512)
        xs = [(X[:, c, sl], Xtk[c][tg]) for c in range(NC8)]
        ps_ap, ps_tk = ps_rot.next()
        rstd, rtk = rstd_rot.next()
        rms_stats(cx, xs, 512, sq_rot, ps_ap, ps_tk, rstd, rtk, ones_bf, ctk, 1.0 / D)
        for c in range(NC8):
            P.op("dve", lambda e, c=c, sl=sl, rstd=rstd: e.scalar_tensor_tensor(
                out=hT[:, c, sl], in0=X[:, c, sl], scalar=gcol[:, c:c + 1], in1=rstd[:, 0:512],
                op0=ALU.mult, op1=ALU.mult), reads=[Xtk[c][tg], rtk, ctk], writes=[hTtk[c][tg]])


def emit_mlp(cx, X, Xtk, gcol, ctk, w1, w2, ones_bf):
    P = cx.P
    mk = cx.mark()
    st = None
    hT = cx.sb(st, [128, NC8, NT], BF16, "mlp_hT")
    hTtk = [[Tk() for _ in range(4)] for _ in range(NC8)]
    sq_rot = Rot([cx.sb(st, [128, 512], BF16, "sq") for _ in range(4)])
    rstd_rot = Rot([cx.sb(st, [128, 512], F32, "rstd") for _ in range(2)])
    ps_stat = Rot([cx.banks[7]])
    emit_norm_resident(cx, X, Xtk, gcol, ctk, hT, hTtk, sq_rot, rstd_rot, ps_stat, ones_bf)
    ws = WStream(cx, st, 4096, nstage=2, nslot=2)
    hids = [cx.sb(st, [128, 4, NT], BF16, "hid") for _ in range(2)]
    hid_tks = [[[Tk() for _ in range(4)] for _ in range(4)] for _ in range(2)]
    tmp_rot = Rot([cx.sb(st, [128, 512], F32, "rl") for _ in range(3)])
    psA = Rot(cx.banks[0:4])
    psB = Rot(cx.banks[4:7])
    w1v = w1.rearrange("(c p) n -> p c n", p=128)
    w2v = w2.rearrange("(c p) n -> p c n", p=128)
    NHB = 8
    for hb in range(NHB):
        hid = hids[hb % 2]
        htk = hid_tks[hb % 2]
        wa, watk = ws.load([w1v[:, :, hb * 512:(hb + 1) * 512]])
        wa3 = wa.rearrange("p (c n) -> p c n", c=NC8)
        for hc in range(4):
            for tg in range(4):
                sl = slice(tg * 512, (tg + 1) * 512)
                ps, pstk = psA.next()
                for c in range(NC8):
                    P.op("pe", lambda e, c=c, hc=hc, sl=sl, ps=ps, wa3=wa3: e.matmul(
                        ps, lhsT=wa3[:, c, hc * 128:(hc + 1) * 128], rhs=hT[:, c, sl],
                        start=(c == 0), stop=(c == NC8 - 1)),
                        reads=[watk, hTtk[c][tg]], writes=[pstk], signal=(c == NC8 - 1))
                tmp, ttk = tmp_rot.next()
                P.op("act", lambda e, ps=ps, tmp=tmp: e.activation(out=tmp, in_=ps, func=AF.Square),
                     reads=[pstk], writes=[ttk])
                P.op("dve", lambda e, ps=ps, tmp=tmp, hc=hc, sl=sl, hid=hid: e.scalar_tensor_tensor(
                    out=hid[:, hc, sl], in0=ps, scalar=0.0, in1=tmp, op0=ALU.is_gt, op1=ALU.mult),
                    reads=[pstk, ttk], writes=[htk[hc][tg]])
        wb, wbtk = ws.load([w2v[:, hb * 4:(hb + 1) * 4, :]])
        wb3 = wb.rearrange("p (c n) -> p c n", c=4)
        for oc in range(NC8):
            for tg in range(4):
                sl = slice(tg * 512, (tg + 1) * 512)
                ps, pstk = psB.next()
                for hc in range(4):
                    P.op("pe", lambda e, hc=hc, oc=oc, sl=sl, ps=ps, hid=hid, wb3=wb3: e.matmul(
                        ps, lhsT=wb3[:, hc, oc * 128:(oc + 1) * 128], rhs=hid[:, hc, sl],
                        start=(hc == 0), stop=(hc == 3)),
                        reads=[wbtk, htk[hc][tg]], writes=[pstk], signal=(hc == 3))
                P.op("dve", lambda e, oc=oc, sl=sl, ps=ps: e.tensor_tensor(
                    out=X[:, oc, sl], in0=ps, in1=X[:, oc, sl], op=ALU.add),
                    reads=[pstk, Xtk[oc][tg]], writes=[Xtk[oc][tg]])
    P.barrier()
    cx.release(mk)


def emit_ple(cx, X, Xtk, gcol, ctk, wg, wp, pT_dram, ones_bf):
    P = cx.P
    mk = cx.mark()
    st = None
    hT = cx.sb(st, [128, NC8, NT], BF16, "ple_hT")
    hTtk = [[Tk() for _ in range(4)] for _ in range(NC8)]
    sq_rot = Rot([cx.sb(st, [128, 512], BF16, "sq") for _ in range(4)])
    rstd_rot = Rot([cx.sb(st, [128, 512], F32, "rstd") for _ in range(2)])
    ps_stat = Rot([cx.banks[7]])
    emit_norm_resident(cx, X, Xtk, gcol, ctk, hT, hTtk, sq_rot, rstd_rot, ps_stat, ones_bf)
    ws = WStream(cx, st, 4096, nstage=2, nslot=3)
    pst = cx.sb(st, [128, 2, NT], F32, "pstg")
    pb = cx.sb(st, [128, 2, NT], BF16, "pbf")
    ptk = Tk()
    P.dma("sync", pst, pT_dram.rearrange("(c p) n -> p c n", p=128), writes=[ptk])
    P.op("pool", lambda e: e.tensor_copy(out=pb, in_=pst), reads=[ptk], writes=[ptk])
    wpb, wptk = ws.load([wp.rearrange("(c p) n -> p c n", p=128)])
    wp3 = wpb[:, 0:2048].rearrange("p (c n) -> p c n", c=2)
    gt_rot = Rot([cx.sb(st, [128, 512], F32, "gt") for _ in range(3)])
    psA = Rot(cx.banks[0:3])
    psB = Rot(cx.banks[3:6])
    wgv = wg.rearrange("(c p) n -> p c n", p=128)
    for half in range(2):
        wa, watk = ws.load([wgv[:, :, half * 512:(half + 1) * 512]])
        wa3 = wa.rearrange("p (c n) -> p c n", c=NC8)
        for o4 in range(4):
            oc = half * 4 + o4
            for tg in range(4):
                sl = slice(tg * 512, (tg + 1) * 512)
                ps, pstk = psA.next()
                for c in range(NC8):
                    P.op("pe", lambda e, c=c, o4=o4, sl=sl, ps=ps, wa3=wa3: e.matmul(
                        ps, lhsT=wa3[:, c, o4 * 128:(o4 + 1) * 128], rhs=hT[:, c, sl],
                        start=(c == 0), stop=(c == NC8 - 1)),
                        reads=[watk, hTtk[c][tg]], writes=[pstk], signal=(c == NC8 - 1))
                ps2, ps2tk = psB.next()
                for kc in range(2):
                    P.op("pe", lambda e, kc=kc, oc=oc, sl=sl, ps2=ps2: e.matmul(
                        ps2, lhsT=wp3[:, kc, oc * 128:(oc + 1) * 128], rhs=pb[:, kc, sl],
                        start=(kc == 0), stop=(kc == 1)),
                        reads=[wptk, ptk], writes=[ps2tk], signal=(kc == 1))
                gt, gtk = gt_rot.next()
                P.op("act", lambda e, ps=ps, gt=gt: e.activation(out=gt, in_=ps, func=AF.Sigmoid),
                     reads=[pstk], writes=[gtk])
                P.op("dve", lambda e, ps2=ps2, gt=gt: e.tensor_tensor(out=gt, in0=ps2, in1=gt, op=ALU.mult),
                     reads=[ps2tk, gtk], writes=[gtk])
                P.op("dve", lambda e, oc=oc, sl=sl, gt=gt: e.tensor_tensor(
                    out=X[:, oc, sl], in0=gt, in1=X[:, oc, sl], op=ALU.add),
                    reads=[gtk, Xtk[oc][tg]], writes=[Xtk[oc][tg]])
    P.barrier()
    cx.release(mk)


def alibi_slope(h):
    return 2.0 ** (-8.0 * (h + 1) / 16)


def sslice(start, count, step):
    return slice(start, start + (count - 1) * step + 1, step)


def emit_attention(cx, xT_ext, wqkv, wo, cf, cb, ctk, TOP, R1, lvl=9, hps=8):
    P = cx.P
    st = None
    ones_bf = cb[:, C_ONES:C_ONES + 128]
    bones = cb[:, C_BONES:C_BONES + 128]
    Dm = cf[:, C_DM:C_DM + 256]
    hT = TOP.bitcast(BF16).rearrange("p (c n) -> p c n", c=NC8)
    mk = cx.mark()
    sq_rot = Rot([cx.sb(st, [128, 512], BF16, "sq") for _ in range(2)])
    rstd_rot = Rot([cx.sb(st, [128, 512], F32, "rstd") for _ in range(2)])
    xstg = [R1[:, i * 4096:(i + 1) * 4096].rearrange("p (c n) -> p c n", c=NC8) for i in range(2)]
    xstk = [Tk(), Tk()]
    ps_stat = Rot([cx.banks[7], cx.banks[6]])
    httk = Tk()
    gcol = cf[:, G0_ANORM:G0_ANORM + 8]
    for tg in range(8 if lvl >= 1 else 0):
        xa = xstg[tg % 2]
        xt = xstk[tg % 2]
        P.dma("sync", xa, xT_ext[:, tg * 512:(tg + 1) * 512].rearrange("(c p) n -> p c n", p=128), writes=[xt])
        xs = [(xa[:, c, :], xt) for c in range(NC8)]
        ps_ap, ps_tk = ps_stat.next()
        rstd, rtk = rstd_rot.next()
        rms_stats(cx, xs, 512, sq_rot, ps_ap, ps_tk, rstd, rtk, ones_bf, ctk, 1.0 / D)
        for c in range(NC8):
            P.op("dve", lambda e, c=c, tg=tg, xa=xa, rstd=rstd: e.scalar_tensor_tensor(
                out=hT[:, c, tg * 512:(tg + 1) * 512], in0=xa[:, c, :], scalar=gcol[:, c:c + 1], in1=rstd[:, 0:512],
                op0=ALU.mult, op1=ALU.mult), reads=[xt, rtk, ctk], writes=[httk])
    P.op("dve", lambda e: e.tensor_scalar(out=cf[:, G0_QG:G0_QG + 3], in0=cf[:, G0_QG:G0_QG + 3], scalar1=0.125,
                                          scalar2=None, op0=ALU.mult), reads=[ctk], writes=[ctk])
    P.barrier()
    ACC = R1[:, 0:4096].rearrange("p (a n) -> p a n", a=2)
    oT = R1[:, 4096:12288].bitcast(BF16).rearrange("p (c n) -> p c n", c=NC8)
    acctk = Tk()
    ottk = Tk()
    ws = WStream(cx, st, 3072, nstage=1, nslot=2)
    QTz = [cx.sb(st, [128, NT], BF16, "QTz") for _ in range(2)]
    qtk = Tk()
    KT_rot = Rot([cx.sb(st, [128, 2 * NT], BF16, "KT") for _ in range(2)])
    Vz = [cx.sb(st, [128, 32, 128], BF16, "Vz") for _ in range(2)]
    vtk = Tk()
    for e2 in (0, 1):
        P.op("pool", lambda e, e2=e2: e.memset(QTz[e2], 0.0), writes=[qtk])
        P.op("pool", lambda e, e2=e2: e.memset(Vz[e2], 0.0), writes=[vtk])
    onesz = [cb[:, C_OZ:C_OZ + 128], cb[:, C_OZ + 128:C_OZ + 256]]
    honesz = [cb[:, C_HZ:C_HZ + 128], cb[:, C_HZ + 128:C_HZ + 256]]
    tmp_rot = Rot([cx.sb(st, [128, 256], F32, "stmp") for _ in range(4)])
    pt_rots = [Rot([cx.sb(st, [128, 256], BF16, "PT") for _ in range(6)]) for _ in range(2)]
    bk = [(cx.banks[i], Tk()) for i in range(8)]

    def half(i):
        return (bk[i][0][:, 0:256], bk[i][1])

    psQ = Rot(bk[0:2])
    psS = Rot([bk[2]])
    psV = Rot([bk[3]])
    psST0 = Rot([half(4), half(0)])
    psST1 = Rot([half(5), half(1)])
    psND = Rot([half(6), half(7), half(2), half(3)])
    wq_view = wqkv.rearrange("(c p) n -> p c n", p=128)

    def perm(ap2d, d):
        if d == 1:
            return ap2d
        return ap2d.rearrange("p (u r) -> p r u", r=d)

    def proj_piece(w3, wtk, j, e0, n, gain_col, out_buf, out_tk, d, Lx, u0):
        ps, pstk = psQ.next()
        for c in range(NC8):
            P.op("pe", lambda e, c=c, ps=ps: e.matmul(ps[:, 0:n], lhsT=w3[:, j, c, :], rhs=hT[:, c, e0:e0 + n],
                                                      start=(c == 0), stop=(c == NC8 - 1)),
                 reads=[wtk], writes=[pstk], signal=(c == NC8 - 1))
        ps2, ps2tk = psS.next()
        rstd, rtk = rstd_rot.next()
        rms_stats(cx, [(ps[:, 0:n], pstk)], n, sq_rot, ps2, ps2tk, rstd, rtk, bones, ctk, 1.0 / 64)
        outs = out_buf if isinstance(out_buf, list) else [(slice(0, 128), out_buf)]
        for (rows, ob) in outs:
            if d == 1:
                o = ob[rows, u0:u0 + n]
            else:
                o = ob[rows, 0:d * Lx].rearrange("p (r u) -> p r u", r=d)[:, :, u0:u0 + n // d]
            P.op("dve", lambda e, ps=ps, rstd=rstd, o=o, rows=rows: e.scalar_tensor_tensor(
                out=o, in0=perm(ps[rows, 0:n], d), scalar=gain_col[rows, :], in1=perm(rstd[rows, 0:n], d),
                op0=ALU.mult, op1=ALU.mult),
                reads=[pstk, rtk, ctk], writes=[out_tk])

    if lvl < 2:
        hps = 0
        P.op('dve', lambda e: e.memset(R1, 0.0), writes=[ottk])
    for hp in range(hps):
        for g, (W, d) in enumerate(A_GROUPS):
            L = NT // d
            Lk = (W + NT) // d
            nb = Lk // 128
            e_start = NT - W
            base = g * 3072 + hp * 128
            wb, wtk = ws.load([wq_view[:, :, base + j * 1024: base + j * 1024 + 128] for j in range(3)])
            w3 = wb[:, 0:3072].rearrange("p (j c n) -> p j c n", j=3, c=NC8)
            KT, ktk = KT_rot.next()
            for tg in range(4):
                proj_piece(w3, wtk, 0, NT + tg * 512, 512, cf[:, G0_QG + g:G0_QG + g + 1],
                           [(slice(0, 64), QTz[0]), (slice(64, 128), QTz[1])], qtk, d, L, tg * 512 // d)
            pieces = []
            if W < 512:
                pieces.append((e_start, W))
                e = NT
            else:
                e = e_start
            while e < 2 * NT:
                pieces.append((e, 512))
                e += 512
            for (e0, n) in pieces:
                proj_piece(w3, wtk, 1, e0, n, cf[:, G0_KG + g:G0_KG + g + 1], KT, ktk, d, Lk, (e0 - e_start) // d)
            nkb = d * nb if lvl >= 3 else 0
            kb = 0
            while kb < nkb:
                nblk = min(4, nkb - kb)
                psv, psvtk = psV.next()
                for b in range(nblk):
                    r, jb = divmod(kb + b, nb)
                    e_first = e_start + d * 128 * jb + r
                    for c in range(NC8):
                        P.op("pe", lambda e, c=c, b=b, e_first=e_first, psv=psv: e.matmul(
                            psv[:, b * 128:(b + 1) * 128], lhsT=hT[:, c, sslice(e_first, 128, d)], rhs=w3[:, 2, c, :],
                            start=(c == 0), stop=(c == NC8 - 1)),
                            reads=[wtk], writes=[psvtk], signal=(c == NC8 - 1 and b == nblk - 1))
                for e2 in (0, 1):
                    cs = slice(64 * e2, 64 * e2 + 64)
                    P.op("act", lambda e, kb=kb, nblk=nblk, psv=psv, e2=e2, cs=cs: e.activation(
                        out=Vz[e2][:, kb:kb + nblk, cs],
                        in_=psv[:, 0:nblk * 128].rearrange("p (b n) -> p b n", b=nblk)[:, :, cs], func=AF.Copy),
                        reads=[psvtk], writes=[vtk])
                kb += nblk
            PTs = {}

            def score_task(r, jb):
                lo = 128 if jb == 0 else 0
                hi = 128 if jb == nb - 1 else 256
                qb0 = jb if jb == 0 else jb - 1
                q_off = r * L + 128 * qb0
                sTs = [psST0.next(), psST1.next()]
                for e2 in (0, 1):
                    sT, sTtk = sTs[e2]
                    P.op("pe", lambda e, sT=sT, e2=e2: e.matmul(
                        sT[:, lo:hi], lhsT=KT[:, r * Lk + 128 * jb: r * Lk + 128 * jb + 128],
                        rhs=QTz[e2][:, q_off:q_off + (hi - lo)], start=True, stop=True),
                        reads=[ktk, qtk], writes=[sTtk])
                for e2 in (0, 1):
                    sig = alibi_slope(2 * hp + e2) * d
                    sT, sTtk = sTs[e2]
                    tmp, tmtk = tmp_rot.next()
                    P.op("dve", lambda e, sT=sT, tmp=tmp, sig=sig: e.scalar_tensor_tensor(
                        out=tmp[:, lo:hi], in0=Dm[:, lo:hi], scalar=-sig, in1=sT[:, lo:hi],
                        op0=ALU.mult, op1=ALU.add),
                        reads=[sTtk, ctk], writes=[tmtk])
                    pt, pttk = pt_rots[e2].next()
                    P.op("act", lambda e, tmp=tmp, pt=pt: e.activation(
                        out=pt[:, lo:hi], in_=tmp[:, lo:hi], func=AF.Exp), reads=[tmtk], writes=[pttk])
                    PTs[(e2, r, jb)] = (pt, pttk)

            def pv_task(r, j):
                jb = j + 1
                nd, ndtk = psND.next()
                kbp = r * nb + j
                kbd = r * nb + jb
                for part in (0, 1):
                    co = slice(128 * part, 128 * part + 128)
                    for e2 in (0, 1):
                        ptp, ptptk = PTs[(e2, r, j)]
                        ptd, ptdtk = PTs[(e2, r, jb)]
                        if part == 0:
                            lp, ld = Vz[e2][:, kbp, :], Vz[e2][:, kbd, :]
                        else:
                            lp, ld = (honesz[e2] if j == 0 else onesz[e2]), onesz[e2]
                        P.op("pe", lambda e, nd=nd, co=co, lp=lp, ptp=ptp, e2=e2: e.matmul(
                            nd[:, co], lhsT=lp, rhs=ptp[:, 128:256], start=(e2 == 0), stop=False),
                            reads=[vtk, ctk, ptptk], writes=[ndtk], signal=False)
                        P.op("pe", lambda e, nd=nd, co=co, ld=ld, ptd=ptd, e2=e2: e.matmul(
                            nd[:, co], lhsT=ld, rhs=ptd[:, 0:128], start=False, stop=(e2 == 1)),
                            reads=[vtk, ctk, ptdtk], writes=[ndtk], signal=(part == 1 and e2 == 1))
                t0 = r + d * 128 * j
                accv = ACC[:, :, sslice(t0, 128, d)]
                ndv = nd.rearrange("p (a n) -> p a n", a=2)
                if g == 0:
                    P.op("act", lambda e, accv=accv, ndv=ndv: e.activation(out=accv, in_=ndv, func=AF.Copy),
                         reads=[ndtk], writes=[acctk])
                else:
                    P.op("dve", lambda e, accv=accv, ndv=ndv: e.tensor_tensor(out=accv, in0=ndv, in1=accv, op=ALU.add),
                         reads=[ndtk, acctk], writes=[acctk])

            LA = 2
            pending = []
            tasks = [(r, jb) for r in range(d if lvl >= 4 else 0) for jb in range(nb)]
            for i, (r, jb) in enumerate(tasks):
                score_task(r, jb)
                if jb >= 1:
                    pending.append((i, r, jb - 1))
                while pending and pending[0][0] <= i - LA:
                    _, r_, j_ = pending.pop(0)
                    pv_task(r_, j_)
            for (_, r_, j_) in pending:
                pv_task(r_, j_)
        P.op("dve", lambda e: e.reciprocal(out=ACC[:, 1, :], in_=ACC[:, 1, :]), reads=[acctk], writes=[acctk])
        P.op("dve", lambda e, hp=hp: e.tensor_tensor(out=oT[:, hp, :], in0=ACC[:, 0, :], in1=ACC[:, 1, :], op=ALU.mult),
             reads=[acctk], writes=[ottk])
    P.barrier()
    cx.release(mk)
    mk = cx.mark()
    X = TOP.rearrange("p (c n) -> p c n", c=NC8)
    Xtk = [[Tk() for _ in range(4)] for _ in range(NC8)]
    for c in range(NC8):
        for tg in range(4):
            P.dma("sync", X[:, c, tg * 512:(tg + 1) * 512], xT_ext[c * 128:(c + 1) * 128, NT + tg * 512:NT + (tg + 1) * 512],
                  writes=[Xtk[c][tg]])
    ws2 = WStream(cx, st, 4096, nstage=2, nslot=2)
    wov = wo.rearrange("(c p) n -> p c n", p=128)
    psA = Rot(cx.banks[0:4])
    for half in range(2):
        wa, watk = ws2.load([wov[:, :, half * 512:(half + 1) * 512]])
        wa3 = wa.rearrange("p (c n) -> p c n", c=NC8)
        for o4 in range(4):
            oc = half * 4 + o4
            for tg in range(4):
                sl = slice(tg * 512, (tg + 1) * 512)
                ps, pstk = psA.next()
                for c in range(NC8):
                    P.op("pe", lambda e, c=c, o4=o4, sl=sl, ps=ps, wa3=wa3: e.matmul(
                        ps, lhsT=wa3[:, c, o4 * 128:(o4 + 1) * 128], rhs=oT[:, c, sl],
                        start=(c == 0), stop=(c == NC8 - 1)),
                        reads=[watk, ottk], writes=[pstk], signal=(c == NC8 - 1))
                P.op("dve", lambda e, oc=oc, sl=sl, ps=ps: e.tensor_tensor(
                    out=X[:, oc, sl], in0=ps, in1=X[:, oc, sl], op=ALU.add),
                    reads=[pstk, Xtk[oc][tg]], writes=[Xtk[oc][tg]])
    P.barrier()
    cx.release(mk)
    return X, Xtk


def emit_store(cx, X, Xtk, out_dram):
    P = cx.P
    for c in range(NC8):
        P.dma("sync", out_dram[c * 128:(c + 1) * 128, :], X[:, c, :], reads=Xtk[c])


def build_layer0(debug=False, lvl=9, hps=8):
    nc = bass.Bass("TRN2", target_bir_lowering=False)
    xT_ext = nc.dram_tensor("xT_ext", [D, 2 * NT], F32, kind="ExternalInput").ap()
    pT = nc.dram_tensor("pT", [256, NT], F32, kind="ExternalInput").ap()
    cst = nc.dram_tensor("cst", [128, G0_END], F32, kind="ExternalInput").ap()
    wqkv = nc.dram_tensor("a_w_qkv", [D, 9216], F32, kind="ExternalInput").ap()
    wo = nc.dram_tensor("a_w_o", [D, D], F32, kind="ExternalInput").ap()
    w1 = nc.dram_tensor("mlp_w1", [D, 4096], F32, kind="ExternalInput").ap()
    w2 = nc.dram_tensor("mlp_w2", [4096, D], F32, kind="ExternalInput").ap()
    wg = nc.dram_tensor("ple_w_gate", [D, D], F32, kind="ExternalInput").ap()
    wp = nc.dram_tensor("ple_w_proj", [256, D], F32, kind="ExternalInput").ap()
    out = nc.dram_tensor("xout", [D, NT], F32, kind="ExternalOutput").ap()
    if debug:
        dbg_a = nc.dram_tensor("dbg_a", [D, NT], F32, kind="ExternalOutput").ap()
        dbg_m = nc.dram_tensor("dbg_m", [D, NT], F32, kind="ExternalOutput").ap()
    cx = Ctx(nc)
    P = cx.P
    cf, cb, ctk = load_consts(cx, None, cst, G0_END)
    cx.eps_col = cf[:, C_EPS:C_EPS + 1]
    ones_bf = cb[:, C_ONES:C_ONES + 128]
    TOP = cx.sb(None, [128, 16384], F32, "TOP")
    R1 = cx.sb(None, [128, 12288], F32, "R1")
    mk = cx.mark()
    X, Xtk = emit_attention(cx, xT_ext, wqkv, wo, cf, cb, ctk, TOP, R1, lvl=lvl, hps=hps)
    cx.release(mk)
    cx.top = cx.top - 12288 * 4
    if debug:
        emit_store(cx, X, Xtk, dbg_a)
    emit_mlp(cx, X, Xtk, cf[:, G0_MLPN:G0_MLPN + 8], ctk, w1, w2, ones_bf)
    if debug:
        emit_store(cx, X, Xtk, dbg_m)
    emit_ple(cx, X, Xtk, cf[:, G0_PLEN:G0_PLEN + 8], ctk, wg, wp, pT, ones_bf)
    emit_store(cx, X, Xtk, out)
    P.finish()
    return nc, cx


def layer0_inputs(inputs, core):
    x = inputs["x"][0]
    lo = core * NT
    xe = np.zeros((2 * NT, D), np.float32)
    if core > 0:
        xe[:NT] = x[lo - NT:lo]
    xe[NT:] = x[lo:lo + NT]
    c = base_consts(core, G0_END)
    c[:, G0_ANORM:G0_ANORM + 8] = col_layout(inputs["a_norm"][0])
    c[:, G0_QG:G0_QG + 3] = np.tile(inputs["a_q_gain"][0].T, (2, 1))
    c[:, G0_KG:G0_KG + 3] = np.tile(inputs["a_k_gain"][0].T, (2, 1))
    c[:, G0_MLPN:G0_MLPN + 8] = col_layout(inputs["mlp_norm"][0])
    c[:, G0_PLEN:G0_PLEN + 8] = col_layout(inputs["ple_norm"][0])
    return {
        "xT_ext": np.ascontiguousarray(xe.T),
        "pT": np.ascontiguousarray(inputs["p"][0, 0, lo:lo + NT].T),
        "cst": c,
        "a_w_qkv": inputs["a_w_qkv"][0], "a_w_o": inputs["a_w_o"][0],
        "mlp_w1": inputs["mlp_w1"][0], "mlp_w2": inputs["mlp_w2"][0],
        "ple_w_gate": inputs["ple_w_gate"][0], "ple_w_proj": inputs["ple_w_proj"][0],
    }

BH = 4
DH = 512
NCK = NT // 128
KSCALE = DH ** -0.5
NST = 8192 + 2048 + 8

L_BNORM = C_GAINS
L_MLPN = L_BNORM + 8
L_PLEN = L_MLPN + 8
L_CONVW = L_PLEN + 8
L_CONVB = L_CONVW + 64
L_SKIP = L_CONVB + 16
L_HGAIN = L_SKIP + 16
L_BI = L_HGAIN + 16
L_BF = L_BI + 1
L_MASKLOW = L_BF + 1
L_SEL = L_MASKLOW + 128
L_CMASK = L_SEL + 512
L_NEG = L_CMASK + 7
L_E0 = L_NEG + 1
L_LNK = L_E0 + 1
L_CNEG = L_LNK + 1
L_END = L_CNEG + 7


def layer1_consts(inputs, core):
    c = base_consts(core, L_END)
    c[:, L_BNORM:L_BNORM + 8] = col_layout(inputs["b_norm"][0])
    c[:, L_MLPN:L_MLPN + 8] = col_layout(inputs["mlp_norm"][1])
    c[:, L_PLEN:L_PLEN + 8] = col_layout(inputs["ple_norm"][1])
    cw = inputs["b_conv_w"][0]
    c[:, L_CONVW:L_CONVW + 64] = cw.reshape(4, 16, 128).transpose(2, 1, 0).reshape(128, 64)
    c[:, L_CONVB:L_CONVB + 16] = col_layout(inputs["b_conv_b"][0])
    c[:, L_SKIP:L_SKIP + 16] = col_layout(inputs["b_skip"][0])
    c[:, L_HGAIN:L_HGAIN + 16] = col_layout(inputs["b_h_gain"][0])
    bg = inputs["b_b_gate"][0]
    c[0:4, L_BI] = bg[0:4]
    c[0:4, L_BF] = bg[4:8]
    s_ = np.arange(128)[:, None]
    t_ = np.arange(128)[None, :]
    c[:, L_MASKLOW:L_MASKLOW + 128] = np.where(s_ <= t_, 0.0, BIG)
    for hd in range(4):
        c[hd, L_SEL + hd * 128:L_SEL + (hd + 1) * 128] = 1.0
    for cp in range(7):
        c[:, L_CMASK + cp] = 1.0 if cp < core else 0.0
        c[:, L_CNEG + cp] = 0.0 if cp < core else -1e30
    c[:, L_NEG] = -1e30
    c[0, L_E0] = 1.0
    c[:, L_LNK] = np.log(KSCALE)
    return c


def bd_compact(w, transpose=False):
    out = np.zeros((2048, 128), np.float32)
    n = np.arange(512)
    for j in range(4):
        for k in range(4):
            if transpose:
                out[4 * n + k, (4 * n + j) % 128] = w[:, j, k]
            else:
                out[4 * n + j, (4 * n + k) % 128] = w[:, j, k]
    return out


def layer1_inputs(inputs, core, x1T_full, stage, st_all=None, g_in=None):
    lo = core * NT
    xh = np.zeros((D, 4), np.float32)
    if core > 0:
        xh[:, 1:4] = x1T_full[:, lo - 3:lo]
    m = {
        "x1T": np.ascontiguousarray(x1T_full[:, lo:lo + NT]),
        "xh": xh,
        "cst": layer1_consts(inputs, core),
        "b_w_up": inputs["b_w_up"][0],
        "bd": np.stack([bd_compact(inputs["b_w_q"][0]), bd_compact(inputs["b_w_k"][0]), bd_compact(inputs["b_w_v"][0])]),
        "bdT": np.stack([bd_compact(inputs["b_w_q"][0], True), bd_compact(inputs["b_w_k"][0], True),
                         bd_compact(inputs["b_w_v"][0], True)]),
        "w_gate": inputs["b_w_gate"][0],
    }
    if stage == "C":
        m.update({
            "st_all": st_all,
            "g_in": g_in,
            "b_w_down": inputs["b_w_down"][0],
            "pT": np.ascontiguousarray(inputs["p"][1, 0, lo:lo + NT].T),
            "mlp_w1": inputs["mlp_w1"][1], "mlp_w2": inputs["mlp_w2"][1],
            "ple_w_gate": inputs["ple_w_gate"][1], "ple_w_proj": inputs["ple_w_proj"][1],
        })
    return m


def build_layer1(stage, debug=False, dbg_stop=None):
    nc = bass.Bass("TRN2", target_bir_lowering=False)
    x1T = nc.dram_tensor("x1T", [D, NT], F32, kind="ExternalInput").ap()
    xh = nc.dram_tensor("xh", [D, 4], F32, kind="ExternalInput").ap()
    cst = nc.dram_tensor("cst", [128, L_END], F32, kind="ExternalInput").ap()
    wup = nc.dram_tensor("b_w_up", [D, 4096], F32, kind="ExternalInput").ap()
    bd = nc.dram_tensor("bd", [3, 2048, 128], F32, kind="ExternalInput").ap()
    bdT = nc.dram_tensor("bdT", [3, 2048, 128], F32, kind="ExternalInput").ap()
    wgate = nc.dram_tensor("w_gate", [6144, 8], F32, kind="ExternalInput").ap()
    if stage == "B":
        st_out = nc.dram_tensor("st_out", [128, NST], F32, kind="ExternalOutput").ap()
        g_out = nc.dram_tensor("g_out", [8, NT], F32, kind="ExternalOutput").ap()
    else:
        st_all = nc.dram_tensor("st_all", [7, 128, NST], F32, kind="ExternalInput").ap()
        g_in = nc.dram_tensor("g_in", [8, NT], F32, kind="ExternalInput").ap()
        wdown = nc.dram_tensor("b_w_down", [2048, D], F32, kind="ExternalInput").ap()
        pT = nc.dram_tensor("pT", [256, NT], F32, kind="ExternalInput").ap()
        w1 = nc.dram_tensor("mlp_w1", [D, 4096], F32, kind="ExternalInput").ap()
        w2 = nc.dram_tensor("mlp_w2", [4096, D], F32, kind="ExternalInput").ap()
        wg = nc.dram_tensor("ple_w_gate", [D, D], F32, kind="ExternalInput").ap()
        wp = nc.dram_tensor("ple_w_proj", [256, D], F32, kind="ExternalInput").ap()
        out = nc.dram_tensor("xout", [D, NT], F32, kind="ExternalOutput").ap()
        yscr = nc.dram_tensor("yscr", [2048, NT], BF16).ap()
        if debug:
            dbg_a = nc.dram_tensor("dbg_a", [D, NT], F32, kind="ExternalOutput").ap()
    cx = Ctx(nc)
    P = cx.P
    cf, cb, ctk = load_consts(cx, None, cst, L_END)
    cx.eps_col = cf[:, C_EPS:C_EPS + 1]
    ones_bf = cb[:, C_ONES:C_ONES + 128]
    ones_f = cf[:, C_ONES:C_ONES + 128]
    ident_f = cf[:, C_ID:C_ID + 128]
    one_col = cf[:, C_ONES:C_ONES + 1]
    TOP = cx.sb(None, [128, 16384], F32, "TOP")
    hT = TOP[:, 0:8208].bitcast(BF16)[:, 0:8 * 2052].rearrange("p (c n) -> p c n", c=NC8)
    topfree = TOP[:, 8208:16384]
    base_mark = cx.mark()
    bk = [(cx.banks[i], Tk()) for i in range(8)]
    ws = WStream(cx, None, 4096, nstage=0, nslot=3)
    ws.stage = Rot([topfree[:, 0:4096]])
    bdh = cx.sb(None, [128, 3, 4, 128], BF16, "bdh")
    bdh_st = cx.sb(None, [128, 3, 4, 128], F32, "bdh_st")
    diag = cx.sb(None, [128, 4, 4, 128], BF16, "diag")
    xms = [cx.sb(None, [128, 4, 516], BF16, "xm") for _ in range(2)]
    xc = cx.sb(None, [128, 4, 512], BF16, "xc")
    GF = cx.sb(None, [128, NT], F32, "GF")
    BETAx = cx.sb(None, [128, NT + 1], F32, "BETAx")
    small = cx.sb(None, [128, 64], F32, "small")
    TMw = cx.sb(None, [128, NCK, 4], F32, "TMw")
    TMa = cx.sb(None, [128, NCK, 4], F32, "TMa")
    if stage == "C":
        fold = cx.sb(None, [128, 7, 8], F32, "foldin")
        S1 = cx.sb(None, [128, 7, 4], F32, "S1")
        S2 = cx.sb(None, [128, 7, 4], F32, "S2")
        mrun = cx.sb(None, [128, 4], F32, "mrun")
        fa = cx.sb(None, [128, 4], F32, "fa")
        fb = cx.sb(None, [128, 4], F32, "fb")
        fc_ = cx.sb(None, [128, 4], F32, "fc")
    pers_mark = cx.mark()

    mk = cx.mark()
    sq_rot = Rot([cx.sb(None, [128, 512], BF16, "sq") for _ in range(2)])
    rstd_rot = Rot([cx.sb(None, [128, 512], F32, "rstd") for _ in range(2)])
    xstg = [cx.sb(None, [128, NC8, 512], F32, "xstg") for _ in range(2)]
    xstk = [Tk(), Tk()]
    ps_stat = Rot([bk[7], bk[6]])
    gcol = cf[:, L_BNORM:L_BNORM + 8]
    httk = Tk()
    pieces = [(None, 4)] + [(tg, 512) for tg in range(4)]
    for i, (tg, n) in enumerate(pieces):
        xa = xstg[i % 2]
        xt = xstk[i % 2]
        if tg is None:
            P.dma("sync", xa[:, :, 0:4], xh.rearrange("(c p) n -> p c n", p=128), writes=[xt])
            h0 = 0
        else:
            P.dma("sync", xa, x1T[:, tg * 512:(tg + 1) * 512].rearrange("(c p) n -> p c n", p=128), writes=[xt])
            h0 = 4 + tg * 512
        xs = [(xa[:, c, 0:n], xt) for c in range(NC8)]
        ps_ap, ps_tk = ps_stat.next()
        rstd, rtk = rstd_rot.next()
        rms_stats(cx, xs, n, sq_rot, ps_ap, ps_tk, rstd, rtk, ones_bf, ctk, 1.0 / D)
        for c in range(NC8):
            P.op("dve", lambda e, c=c, xa=xa, rstd=rstd, n=n, h0=h0: e.scalar_tensor_tensor(
                out=hT[:, c, h0:h0 + n], in0=xa[:, c, 0:n], scalar=gcol[:, c:c + 1], in1=rstd[:, 0:n],
                op0=ALU.mult, op1=ALU.mult), reads=[xt, rtk, ctk], writes=[httk])
    P.barrier()
    cx.release(mk)

    GI = cx.sb(None, [128, NT], F32, "GI")
    LF = cx.sb(None, [128, NT], F32, "LF")
    BB = cx.sb(None, [128, NT], F32, "BB")
    T1 = cx.sb(None, [128, NT], F32, "T1")
    wfold = [[cx.sb(None, [128, 16, 128], BF16, "wfold") for _ in range(2)] for _ in range(2)]
    wftk = Tk()
    bdT_sb = topfree[:, 0:6144].rearrange("p (a b) -> p a b", a=48)
    wg_sb = topfree[:, 6144:6528].rearrange("p (a b) -> p a b", a=48)
    btk = Tk()
    for j in range(3 if stage == "B" else 0):
        P.dma("sync", bdT_sb[:, j * 16:(j + 1) * 16, :], bdT[j].rearrange("(c p) n -> p c n", p=128), writes=[btk])
    if stage == "B":
        P.dma("sync", wg_sb, wgate.rearrange("(c p) n -> p c n", p=128), writes=[btk])
    zpad = Rot([topfree[:, 6528 + i * 128:6528 + (i + 1) * 128] for i in range(4)])
    for (za, ztk) in zpad.items:
        P.op("pool", lambda e, za=za: e.memset(za, 0.0), writes=[ztk])
    psF = Rot(bk[0:2])
    for mc in range(16 if stage == "B" else 0):
        for part in range(2):
            for xm_ in range(2):
                ps, pstk = psF.next()
                srcs = (0, 1) if xm_ == 0 else (2,)
                for si, j in enumerate(srcs):
                    za, ztk = zpad.next()
                    P.op("dve", lambda e, za=za, j=j, mc=mc, part=part: e.tensor_copy(
                        out=za[:, 0:4], in_=wg_sb[:, j * 16 + mc, part * 4:part * 4 + 4]), reads=[btk], writes=[ztk])
                    P.op("pe", lambda e, ps=ps, za=za, j=j, mc=mc, si=si, srcs=srcs: e.matmul(
                        ps[:, 0:128], lhsT=bdT_sb[:, j * 16 + mc, :], rhs=za, start=(si == 0), stop=(si == len(srcs) - 1)),
                        reads=[btk, ztk], writes=[pstk])
                P.op("act", lambda e, ps=ps, xm_=xm_, part=part, mc=mc: e.activation(
                    out=wfold[xm_][part][:, mc, :], in_=ps[:, 0:128], func=AF.Copy), reads=[pstk], writes=[wftk])
    P.barrier()

    hdtk = Tk()
    xmtk = [Tk(), Tk()]
    xctk = Tk()
    psA = Rot(bk[0:2])
    wupv = wup.rearrange("(c p) n -> p c n", p=128)
    bdv = bd.rearrange("j (c p) n -> p j c n", p=128)
    state = {"i": 0}

    def head_setup(hd):
        for j in range(3):
            P.dma("sync", bdh_st[:, j, :, :], bdv[:, j, hd * 4:(hd + 1) * 4, :], writes=[hdtk])
        P.op("pool", lambda e: e.tensor_copy(out=bdh, in_=bdh_st), reads=[hdtk], writes=[hdtk])
        for mc in range(4):
            for k in range(4):
                col = L_CONVW + (hd * 4 + mc) * 4 + k
                P.op("pool", lambda e, mc=mc, k=k, col=col: e.tensor_scalar(
                    out=diag[:, mc, k, :], in0=ident_f, scalar1=cf[:, col:col + 1], scalar2=None, op0=ALU.mult),
                    reads=[ctk], writes=[hdtk])
        wx, wxtk = ws.load([wupv[:, :, hd * 512:(hd + 1) * 512]])
        return wx.rearrange("p (c n) -> p c n", c=NC8), wxtk

    def front(hd, tg, wx3, wxtk):
        i = state["i"]
        state["i"] += 1
        xm, xmt = xms[i % 2], xmtk[i % 2]
        xmp, xmpt = xms[(i + 1) % 2], xmtk[(i + 1) % 2]
        for mc in range(4):
            ps, pstk = psA.next()
            for c in range(NC8):
                P.op("pe", lambda e, c=c, mc=mc, ps=ps: e.matmul(
                    ps, lhsT=wx3[:, c, mc * 128:(mc + 1) * 128], rhs=hT[:, c, 4 + tg * 512:4 + (tg + 1) * 512],
                    start=(c == 0), stop=(c == NC8 - 1)), reads=[wxtk], writes=[pstk], signal=(c == NC8 - 1))
            P.op("act", lambda e, ps=ps, mc=mc, xm=xm: e.activation(out=xm[:, mc, 4:516], in_=ps, func=AF.Copy),
                 reads=[pstk], writes=[xmt])
            if tg == 0:
                ps, pstk = psA.next()
                for c in range(NC8):
                    P.op("pe", lambda e, c=c, mc=mc, ps=ps: e.matmul(
                        ps[:, 0:4], lhsT=wx3[:, c, mc * 128:(mc + 1) * 128], rhs=hT[:, c, 0:4],
                        start=(c == 0), stop=(c == NC8 - 1)), reads=[wxtk], writes=[pstk], signal=(c == NC8 - 1))
                P.op("act", lambda e, ps=ps, mc=mc, xm=xm: e.activation(out=xm[:, mc, 0:4], in_=ps[:, 0:4], func=AF.Copy),
                     reads=[pstk], writes=[xmt])
        if tg > 0:
            P.op("pool", lambda e, xm=xm, xmp=xmp: e.tensor_copy(out=xm[:, :, 0:4], in_=xmp[:, :, 512:516]),
                 reads=[xmpt], writes=[xmt])
        for mc in range(4):
            ps, pstk = psA.next()
            for k in range(4):
                P.op("pe", lambda e, k=k, mc=mc, ps=ps, xm=xm: e.matmul(
                    ps, lhsT=diag[:, mc, k, :], rhs=xm[:, mc, 1 + k:1 + k + 512], start=(k == 0), stop=(k == 3)),
                    reads=[hdtk, xmt], writes=[pstk], signal=(k == 3))
            col = L_CONVB + hd * 4 + mc
            P.op("act", lambda e, ps=ps, mc=mc, col=col: e.activation(
                out=xc[:, mc, :], in_=ps, func=AF.Silu, bias=cf[:, col:col + 1]), reads=[pstk, ctk], writes=[xctk])
        return xm, xmt

    gtk = Tk()
    psG = Rot(bk[2:4])
    if stage == "C":
        P.op("pool", lambda e: e.memset(GI, 0.0), writes=[gtk])
        P.op("pool", lambda e: e.memset(GF, 0.0), writes=[gtk])
        P.dma("sync", GI[0:4, :], g_in[0:4, :], writes=[gtk])
        P.dma("sync", GF[0:4, :], g_in[4:8, :], writes=[gtk])
    for hd in range(BH if stage == "B" else 0):
        wx3, wxtk = head_setup(hd)
        for tg in range(4):
            xm, xmt = front(hd, tg, wx3, wxtk)
            for part, Grow in ((0, GI), (1, GF)):
                ps, pstk = psG.next()
                for mc in range(4):
                    P.op("pe", lambda e, ps=ps, mc=mc, part=part: e.matmul(
                        ps, lhsT=wfold[0][part][:, hd * 4 + mc, :], rhs=xc[:, mc, :], start=(mc == 0), stop=False),
                        reads=[wftk, xctk], writes=[pstk], signal=False)
                    P.op("pe", lambda e, ps=ps, mc=mc, part=part, xm=xm: e.matmul(
                        ps, lhsT=wfold[1][part][:, hd * 4 + mc, :], rhs=xm[:, mc, 4:516], start=False, stop=(mc == 3)),
                        reads=[wftk, xmt], writes=[pstk], signal=(mc == 3))
                sl = slice(tg * 512, (tg + 1) * 512)
                if hd == 0:
                    P.op("act", lambda e, ps=ps, Grow=Grow, sl=sl: e.activation(out=Grow[:, sl], in_=ps, func=AF.Copy),
                         reads=[pstk], writes=[gtk])
                else:
                    P.op("dve", lambda e, ps=ps, Grow=Grow, sl=sl: e.tensor_tensor(out=Grow[:, sl], in0=ps, in1=Grow[:, sl], op=ALU.add),
                         reads=[pstk, gtk], writes=[gtk])

    rtk = Tk()
    if stage == "B":
        P.dma("sync", g_out[0:4, :], GI[0:4, :], reads=[gtk])
        P.dma("sync", g_out[4:8, :], GF[0:4, :], reads=[gtk])
    P.op("dve", lambda e: e.tensor_scalar(out=GI, in0=GI, scalar1=cf[:, L_BI:L_BI + 1], scalar2=None, op0=ALU.add),
         reads=[gtk, ctk], writes=[gtk])
    P.op("dve", lambda e: e.tensor_scalar(out=GF, in0=GF, scalar1=cf[:, L_BF:L_BF + 1], scalar2=None, op0=ALU.add),
         reads=[gtk, ctk], writes=[gtk])
    P.op("dve", lambda e: e.tensor_scalar(out=T1, in0=GF, scalar1=-1.0, scalar2=None, op0=ALU.mult), reads=[gtk], writes=[rtk])
    P.op("dve", lambda e: e.tensor_tensor(out=T1, in0=T1, in1=GF, op=ALU.max), reads=[gtk, rtk], writes=[rtk])
    P.op("act", lambda e: e.activation(out=T1, in_=T1, func=AF.Exp, scale=-1.0), reads=[rtk], writes=[rtk])
    P.op("act", lambda e: e.activation(out=T1, in_=T1, func=AF.Ln, bias=one_col), reads=[rtk, ctk], writes=[rtk])
    P.op("dve", lambda e: e.scalar_tensor_tensor(out=LF, in0=GF, scalar=0.0, in1=T1, op0=ALU.min, op1=ALU.subtract),
         reads=[gtk, rtk], writes=[rtk])
    P.op("pool", lambda e: e.memset(T1, 1.0), reads=[rtk], writes=[rtk])
    P.op("dve", lambda e: e.tensor_tensor_scan(out=BB, data0=T1, data1=LF, initial=0.0, op0=ALU.mult, op1=ALU.add),
         reads=[rtk], writes=[rtk])
    P.op("dve", lambda e: e.tensor_tensor(out=T1, in0=GI, in1=BB, op=ALU.subtract), reads=[gtk, rtk], writes=[rtk])
    ALPHA = T1
    psR = Rot([bk[4]])
    tmtk = Tk()

    def to_token_major(row, dst):
        ps, pstk = psR.next()
        for ck in range(NCK):
            P.op("pe", lambda e, ps=ps, ck=ck: e.matmul(ps[:, ck * 4:ck * 4 + 4], lhsT=row[:, ck * 128:(ck + 1) * 128],
                                                        rhs=ident_f[:, 0:4], start=True, stop=True),
                 reads=[rtk, gtk, ctk], writes=[pstk], signal=(ck == NCK - 1))
        P.op("act", lambda e, ps=ps: e.activation(out=dst, in_=ps[:, 0:64].rearrange("p (a b) -> p a b", a=NCK), func=AF.Copy),
             reads=[pstk], writes=[tmtk])

    def replicate_cols(col_ap, dst4):
        ps, pstk = psR.next()
        za = small[:, 32:32 + 4]
        P.op("dve", lambda e: e.tensor_scalar(out=za, in0=ident_f[:, 0:4], scalar1=col_ap, scalar2=None, op0=ALU.mult),
             reads=[rtk, ctk, gtk], writes=[rtk])
        P.op("pe", lambda e, ps=ps: e.matmul(ps[:, 0:4], lhsT=ones_f, rhs=za, start=True, stop=True),
             reads=[rtk, ctk], writes=[pstk])
        P.op("act", lambda e, ps=ps: e.activation(out=dst4, in_=ps[:, 0:4], func=AF.Copy), reads=[pstk], writes=[rtk])

    if stage == "B":
        mx = small[:, 0:1]
        P.op("dve", lambda e: e.tensor_reduce(out=mx, in_=ALPHA, axis=AX.X, op=ALU.max), reads=[rtk], writes=[rtk])
        nb_ = small[:, 1:2]
        P.op("dve", lambda e: e.scalar_tensor_tensor(out=nb_, in0=mx, scalar=-1.0, in1=cf[:, L_LNK:L_LNK + 1],
                                                     op0=ALU.mult, op1=ALU.add), reads=[rtk, ctk], writes=[rtk])
        P.op("act", lambda e: e.activation(out=LF, in_=ALPHA, func=AF.Exp, bias=nb_), reads=[rtk], writes=[rtk])
        to_token_major(LF, TMw)
        ml = small[:, 2:3]
        P.op("dve", lambda e: e.tensor_tensor(out=ml, in0=mx, in1=BB[:, NT - 1:NT], op=ALU.add), reads=[rtk], writes=[rtk])
        fin = cx.sb(None, [128, 8], F32, "fin")
        replicate_cols(BB[:, NT - 1:NT], fin[:, 0:4])
        replicate_cols(ml, fin[:, 4:8])
        P.dma("sync", st_out[:, 10240:10248], fin, reads=[rtk])
        P.barrier()
        cx.release(pers_mark)
        kv_rot = Rot([cx.sb(None, [128, 512], BF16, "kv") for _ in range(4)])
        stC = cx.sb(None, [128, 4, 512], F32, "stC")
        stn = cx.sb(None, [128, 512], F32, "stn")
        sttk = Tk()
        psKV = Rot([bk[2]])
        for hd in range(BH):
            wx3, wxtk = head_setup(hd)
            cacc = [bk[3 + dc] for dc in range(4)]
            nacc, nacctk = bk[7]
            for tg in range(4):
                xm, xmt = front(hd, tg, wx3, wxtk)
                for cl in range(4):
                    ck = tg * 4 + cl
                    tsl = slice(cl * 128, (cl + 1) * 128)
                    ps, pstk = psKV.next()
                    for mc in range(4):
                        P.op("pe", lambda e, ps=ps, mc=mc, tsl=tsl: e.matmul(
                            ps[:, mc * 128:(mc + 1) * 128], lhsT=xc[:, mc, tsl], rhs=bdh[:, 1, mc, :], start=True, stop=True),
                            reads=[xctk, hdtk], writes=[pstk], signal=(mc == 3))
                    wk, wktk = kv_rot.next()
                    P.op("act", lambda e, ps=ps, wk=wk, ck=ck, hd=hd: e.activation(
                        out=wk, in_=ps, func=AF.Copy, scale=TMw[:, ck, hd:hd + 1]), reads=[pstk, tmtk], writes=[wktk])
                    ps, pstk = psKV.next()
                    for mc in range(4):
                        P.op("pe", lambda e, ps=ps, mc=mc, cl=cl, xm=xm: e.matmul(
                            ps[:, mc * 128:(mc + 1) * 128], lhsT=xm[:, mc, 4 + cl * 128:4 + (cl + 1) * 128], rhs=bdh[:, 2, mc, :],
                            start=True, stop=True), reads=[xmt, hdtk], writes=[pstk], signal=(mc == 3))
                    vv, vtk = kv_rot.next()
                    P.op("act", lambda e, ps=ps, vv=vv: e.activation(out=vv, in_=ps, func=AF.Copy), reads=[pstk], writes=[vtk])
                    last = (ck == NCK - 1)
                    for dc in range(4):
                        P.op("pe", lambda e, dc=dc, wk=wk, vv=vv, ck=ck, last=last: e.matmul(
                            cacc[dc][0], lhsT=wk[:, dc * 128:(dc + 1) * 128], rhs=vv, start=(ck == 0), stop=last),
                            reads=[wktk, vtk], writes=[cacc[dc][1]], signal=True)
                    P.op("pe", lambda e, wk=wk, ck=ck, last=last: e.matmul(
                        nacc, lhsT=ones_bf, rhs=wk, start=(ck == 0), stop=last), reads=[wktk, ctk], writes=[nacctk], signal=True)
            for dc in range(4):
                P.op("act", lambda e, dc=dc: e.activation(out=stC[:, dc, :], in_=cacc[dc][0], func=AF.Copy),
                     reads=[cacc[dc][1]], writes=[sttk])
            P.op("dve", lambda e: e.tensor_copy(out=stn, in_=nacc), reads=[nacctk], writes=[sttk])
            P.dma("sync", st_out[:, hd * 2048:(hd + 1) * 2048], stC.rearrange("p a b -> p (a b)"), reads=[sttk])
            P.dma("sync", st_out[:, 8192 + hd * 512:8192 + (hd + 1) * 512], stn, reads=[sttk])
        P.finish()
        return nc, cx

    ftk = Tk()
    P.dma("sync", fold, st_all[:, :, 10240:10248].rearrange("c p n -> p c n"), writes=[ftk])
    negc = cf[:, L_NEG:L_NEG + 1]
    P.op("dve", lambda e: e.memset(mrun, -1e30), writes=[ftk])
    for cp in range(7):
        mu = cf[:, L_CMASK + cp:L_CMASK + cp + 1]
        P.op("dve", lambda e, cp=cp, mu=mu: e.scalar_tensor_tensor(out=fa, in0=fold[:, cp, 0:4], scalar=mu, in1=mrun,
                                                                    op0=ALU.mult, op1=ALU.add), reads=[ftk, ctk], writes=[ftk])
        P.op("dve", lambda e, cp=cp, mu=mu: e.tensor_scalar(out=fb, in0=fold[:, cp, 4:8], scalar1=mu,
                                                            scalar2=cf[:, L_CNEG + cp:L_CNEG + cp + 1], op0=ALU.mult, op1=ALU.add),
             reads=[ftk, ctk], writes=[ftk])
        P.op("dve", lambda e: e.tensor_tensor(out=fc_, in0=fa, in1=fb, op=ALU.max), reads=[ftk], writes=[ftk])
        P.op("dve", lambda e: e.tensor_tensor(out=fa, in0=fa, in1=fc_, op=ALU.subtract), reads=[ftk], writes=[ftk])
        P.op("dve", lambda e: e.tensor_tensor(out=fb, in0=fb, in1=fc_, op=ALU.subtract), reads=[ftk], writes=[ftk])
        P.op("act", lambda e, cp=cp: e.activation(out=S1[:, cp, :], in_=fa, func=AF.Exp), reads=[ftk], writes=[ftk])
        P.op("act", lambda e: e.activation(out=fb, in_=fb, func=AF.Exp), reads=[ftk], writes=[ftk])
        P.op("dve", lambda e, cp=cp, mu=mu: e.tensor_scalar(out=S2[:, cp, :], in0=fb, scalar1=mu, scalar2=None, op0=ALU.mult),
             reads=[ftk, ctk], writes=[ftk])
        P.op("dve", lambda e: e.tensor_copy(out=mrun, in_=fc_), reads=[ftk], writes=[ftk])
    mst = small[:, 4:5]
    P.op("dve", lambda e: e.tensor_tensor(out=small[:, 8:12], in0=mrun, in1=ident_f[:, 0:4], op=ALU.mult), reads=[ftk, ctk], writes=[rtk])
    P.op("dve", lambda e: e.tensor_reduce(out=mst, in_=small[:, 8:12], axis=AX.X, op=ALU.add), reads=[rtk], writes=[rtk])
    P.op("dve", lambda e: e.tensor_tensor_scan(out=GF, data0=LF, data1=GI, initial=mst, op0=ALU.add, op1=ALU.max),
         reads=[rtk, gtk], writes=[gtk])
    MM = GF
    P.op("dve", lambda e: e.tensor_tensor(out=BETAx[:, 1:NT + 1], in0=MM, in1=BB, op=ALU.subtract), reads=[gtk, rtk], writes=[rtk])
    P.op("dve", lambda e: e.tensor_copy(out=BETAx[:, 0:1], in_=mst), reads=[rtk], writes=[rtk])
    BETA = BETAx[:, 1:NT + 1]
    for ck in range(NCK):
        bl = small[:, 16:17]
        P.op("dve", lambda e, ck=ck: e.scalar_tensor_tensor(out=small[:, 16 + ck % 8:17 + ck % 8], in0=BETAx[:, 128 * (ck + 1):128 * (ck + 1) + 1],
                                                            scalar=-1.0, in1=cf[:, L_LNK:L_LNK + 1], op0=ALU.mult, op1=ALU.add),
             reads=[rtk, ctk], writes=[rtk])
        P.op("act", lambda e, ck=ck: e.activation(out=LF[:, ck * 128:(ck + 1) * 128], in_=ALPHA[:, ck * 128:(ck + 1) * 128],
                                                  func=AF.Exp, bias=small[:, 16 + ck % 8:17 + ck % 8]), reads=[rtk], writes=[rtk])
    to_token_major(LF, TMw)
    to_token_major(ALPHA, TMa)
    P.barrier()
    cx.release(pers_mark)
    BETA = BETAx[:, 1:NT + 1]

    qT = cx.sb(None, [128, 4, 512], BF16, "qT")
    kT = cx.sb(None, [128, 4, 512], BF16, "kT")
    zs = cx.sb(None, [128, 4, 512], BF16, "zs")
    yb = cx.sb(None, [128, 4, 512], BF16, "yb")
    qktk, zstk, ytk = Tk(), Tk(), Tk()
    Cst = cx.sb(None, [128, 4, 512], F32, "Cst")
    Caug = cx.sb(None, [128, 4, 640], BF16, "Caug")
    nrow = cx.sb(None, [128, 512], F32, "nrow")
    nm = cx.sb(None, [128, 512], F32, "nm")
    ctk2 = Tk()
    clst = Rot([topfree[:, 4096:6144], topfree[:, 6144:8176][:, 0:2032]])
    wk_rot = Rot([cx.sb(None, [128, 512], BF16, "wk") for _ in range(2)])
    va_rot = Rot([cx.sb(None, [128, 640], BF16, "vaug") for _ in range(2)])
    for (va, vatk) in va_rot.items:
        P.op("pool", lambda e, va=va: e.memset(va[:, 512:640], 1.0), writes=[vatk])
    dt_rot = Rot([cx.sb(None, [128, 128], F32, "dtmp") for _ in range(2)])
    sd_rot = Rot([cx.sb(None, [128, 128], BF16, "SdT") for _ in range(2)])
    qs_rot = Rot([cx.sb(None, [128, 4, 128], BF16, "qs") for _ in range(2)])
    hsq_rot = Rot([cx.sb(None, [128, 512], BF16, "hsq") for _ in range(2)])
    dd_rot = Rot([cx.sb(None, [128, 128], F32, "dd") for _ in range(2)])
    rr_rot = Rot([cx.sb(None, [128, 128], F32, "rr") for _ in range(2)])
    sc_rot = Rot([cx.sb(None, [128, 128], F32, "scsb") for _ in range(2)])
    em_rot = Rot([cx.sb(None, [128, 128], F32, "emsb") for _ in range(2)])
    ul_rot = Rot([cx.sb(None, [128, 1], F32, "ulast") for _ in range(2)])
    tt_rot = Rot([cx.sb(None, [128, 128], F32, "tt") for _ in range(3)])
    psB2 = psA
    psS3 = Rot([bk[2]])
    psRP = Rot([bk[2]])
    psH = Rot([bk[4], bk[5]])
    psDS = Rot([bk[6], bk[7]])
    psSS = Rot([bk[3]])
    wzv = wupv
    yview = yscr.rearrange("(c p) n -> p c n", p=128)
    ul_prev = None
    for hd in range(BH):
        wx3, wxtk = head_setup(hd)
        wz, wztk = ws.load([wzv[:, :, 2048 + hd * 512:2048 + (hd + 1) * 512]])
        wz3 = wz.rearrange("p (c n) -> p c n", c=NC8)
        P.op("pool", lambda e: e.memset(Cst, 0.0), writes=[ctk2])
        P.op("pool", lambda e: e.memset(nrow, 0.0), writes=[ctk2])
        Cflat = Cst.rearrange("p a b -> p (a b)")
        for cp in range(7):
            cl_, cltk = clst.items[0]
            P.dma("sync", cl_, st_all[cp][:, hd * 2048:(hd + 1) * 2048], writes=[cltk])
            P.op("act", lambda e, cp=cp, cl_=cl_: e.activation(out=cl_, in_=cl_, func=AF.Copy, scale=S2[:, cp, hd:hd + 1]),
                 reads=[cltk, ftk], writes=[cltk])
            P.op("dve", lambda e, cp=cp, cl_=cl_: e.scalar_tensor_tensor(out=Cflat, in0=Cflat, scalar=S1[:, cp, hd:hd + 1], in1=cl_,
                                                                          op0=ALU.mult, op1=ALU.add), reads=[cltk, ftk, ctk2], writes=[ctk2])
            nl_, nltk = clst.items[1]
            P.dma("sync", nl_[:, 0:512], st_all[cp][:, 8192 + hd * 512:8192 + (hd + 1) * 512], writes=[nltk])
            P.op("act", lambda e, cp=cp, nl_=nl_: e.activation(out=nl_[:, 0:512], in_=nl_[:, 0:512], func=AF.Copy, scale=S2[:, cp, hd:hd + 1]),
                 reads=[nltk, ftk], writes=[nltk])
            P.op("dve", lambda e, cp=cp, nl_=nl_: e.scalar_tensor_tensor(out=nrow, in0=nrow, scalar=S1[:, cp, hd:hd + 1], in1=nl_[:, 0:512],
                                                                          op0=ALU.mult, op1=ALU.add), reads=[nltk, ftk, ctk2], writes=[ctk2])

        def refresh_caug(full):
            if full:
                for dc in range(4):
                    P.op("act", lambda e, dc=dc: e.activation(out=Caug[:, dc, 0:512], in_=Cst[:, dc, :], func=AF.Copy),
                         reads=[ctk2], writes=[ctk2])
            P.op("dve", lambda e: e.tensor_scalar(out=nm, in0=nrow, scalar1=cf[:, L_E0:L_E0 + 1], scalar2=None, op0=ALU.mult),
                 reads=[ctk2, ctk], writes=[ctk2])
            ps, pstk = psB2.next()
            for dc in range(4):
                P.op("pe", lambda e, ps=ps, dc=dc: e.matmul(ps[:, dc * 128:(dc + 1) * 128], lhsT=nm[:, dc * 128:(dc + 1) * 128], rhs=ones_f,
                                                            start=True, stop=True), reads=[ctk2, ctk], writes=[pstk], signal=(dc == 3))
            P.op("act", lambda e, ps=ps: e.activation(out=Caug[:, :, 512:640], in_=ps.rearrange("p (a b) -> p a b", a=4), func=AF.Copy),
                 reads=[pstk], writes=[ctk2])

        refresh_caug(True)
        for tg in range(4):
            xm, xmt = front(hd, tg, wx3, wxtk)
            for mc in range(4):
                ps, pstk = psA.next()
                for c in range(NC8):
                    P.op("pe", lambda e, c=c, mc=mc, ps=ps: e.matmul(
                        ps, lhsT=wz3[:, c, mc * 128:(mc + 1) * 128], rhs=hT[:, c, 4 + tg * 512:4 + (tg + 1) * 512],
                        start=(c == 0), stop=(c == NC8 - 1)), reads=[wztk], writes=[pstk], signal=(c == NC8 - 1))
                P.op("act", lambda e, ps=ps, mc=mc: e.activation(out=zs[:, mc, :], in_=ps, func=AF.Silu), reads=[pstk], writes=[zstk])
            for j, dst, sc_ in ((0, qT, 1.0), (1, kT, KSCALE)):
                for dc in range(4):
                    ps, pstk = psA.next()
                    P.op("pe", lambda e, ps=ps, j=j, dc=dc: e.matmul(ps, lhsT=bdh[:, j, dc, :], rhs=xc[:, dc, :], start=True, stop=True),
                         reads=[hdtk, xctk], writes=[pstk])
                    P.op("act", lambda e, ps=ps, dst=dst, dc=dc, sc_=sc_: e.activation(out=dst[:, dc, :], in_=ps, func=AF.Copy, scale=sc_),
                         reads=[pstk], writes=[qktk])
            RS = {}

            def stage_pre(cl):
                nonlocal ul_prev
                ck = tg * 4 + cl
                tsl = slice(cl * 128, (cl + 1) * 128)
                gsl = slice(ck * 128, (ck + 1) * 128)
                sel = cf[:, L_SEL + hd * 128:L_SEL + (hd + 1) * 128]
                rp, rptk = psRP.next()
                for i3, row in enumerate((BETA, MM)):
                    P.op("pe", lambda e, rp=rp, i3=i3, row=row, gsl=gsl: e.matmul(
                        rp[:, i3 * 128:(i3 + 1) * 128], lhsT=sel, rhs=row[:, gsl], start=True, stop=True),
                        reads=[rtk, gtk, ctk], writes=[rptk], signal=(i3 == 1))
                bprev = mrun[:, hd:hd + 1] if ck == 0 else ul_prev[0]
                bprev_tk = ftk if ck == 0 else ul_prev[1]
                scsb, sctk = sc_rot.next()
                P.op("act", lambda e, rp=rp, scsb=scsb, bprev=bprev: e.activation(out=scsb, in_=rp[:, 0:128], func=AF.Exp, scale=-1.0, bias=bprev),
                     reads=[rptk, bprev_tk], writes=[sctk])
                emsb, emtk = em_rot.next()
                P.op("act", lambda e, rp=rp, emsb=emsb: e.activation(out=emsb, in_=rp[:, 128:256], func=AF.Exp, scale=-1.0),
                     reads=[rptk], writes=[emtk])
                ul_prev = ul_rot.next()
                P.op("act", lambda e, rp=rp, ul_prev=ul_prev: e.activation(out=ul_prev[0], in_=rp[:, 127:128], func=AF.Copy),
                     reads=[rptk], writes=[ul_prev[1]])
                ps, pstk = psB2.next()
                for mc in range(4):
                    P.op("pe", lambda e, ps=ps, mc=mc, tsl=tsl: e.matmul(
                        ps[:, mc * 128:(mc + 1) * 128], lhsT=xc[:, mc, tsl], rhs=bdh[:, 1, mc, :], start=True, stop=True),
                        reads=[xctk, hdtk], writes=[pstk], signal=(mc == 3))
                wk, wktk = wk_rot.next()
                P.op("act", lambda e, ps=ps, wk=wk, ck=ck: e.activation(out=wk, in_=ps, func=AF.Copy, scale=TMw[:, ck, hd:hd + 1]),
                     reads=[pstk, tmtk], writes=[wktk])
                ps, pstk = psB2.next()
                for mc in range(4):
                    P.op("pe", lambda e, ps=ps, mc=mc, cl=cl, xm=xm: e.matmul(
                        ps[:, mc * 128:(mc + 1) * 128], lhsT=xm[:, mc, 4 + cl * 128:4 + (cl + 1) * 128], rhs=bdh[:, 2, mc, :],
                        start=True, stop=True), reads=[xmt, hdtk], writes=[pstk], signal=(mc == 3))
                va, vatk = va_rot.next()
                P.op("act", lambda e, ps=ps, va=va: e.activation(out=va[:, 0:512], in_=ps, func=AF.Copy), reads=[pstk], writes=[vatk])
                pS_, pStk = psS3.next()
                pS = pS_[:, 256:384]
                for dc in range(4):
                    P.op("pe", lambda e, pS=pS, dc=dc, tsl=tsl: e.matmul(pS, lhsT=kT[:, dc, tsl], rhs=qT[:, dc, tsl],
                                                                          start=(dc == 0), stop=(dc == 3)),
                         reads=[qktk], writes=[pStk], signal=(dc == 3))
                dtmp, dttk = dt_rot.next()
                P.op("dve", lambda e, rp=rp, dtmp=dtmp, ck=ck: e.scalar_tensor_tensor(
                    out=dtmp, in0=rp[:, 0:128], scalar=TMa[:, ck, hd:hd + 1], in1=cf[:, L_MASKLOW:L_MASKLOW + 128],
                    op0=ALU.subtract, op1=ALU.max), reads=[rptk, tmtk, ctk], writes=[dttk])
                P.op("act", lambda e, dtmp=dtmp: e.activation(out=dtmp, in_=dtmp, func=AF.Exp, scale=-1.0), reads=[dttk], writes=[dttk])
                sd, sdtk = sd_rot.next()
                P.op("dve", lambda e, pS=pS, dtmp=dtmp, sd=sd: e.tensor_tensor(out=sd, in0=pS, in1=dtmp, op=ALU.mult),
                     reads=[pStk, dttk], writes=[sdtk])
                qs, qstk = qs_rot.next()
                P.op("dve", lambda e, scsb=scsb, qs=qs, tsl=tsl: e.tensor_tensor(
                    out=qs, in0=qT[:, :, tsl], in1=scsb.unsqueeze(1).to_broadcast([128, 4, 128]), op=ALU.mult),
                    reads=[qktk, sctk], writes=[qstk])

                RS[cl] = dict(ck=ck, tsl=tsl, wk=wk, wktk=wktk, va=va, vatk=vatk, sd=sd, sdtk=sdtk, qs=qs, qstk=qstk,
                              scsb=scsb, sctk=sctk, emsb=emsb, emtk=emtk)

            def stage_mid(cl):
                r_ = RS[cl]
                ck, tsl, wk, wktk, va, vatk, sd, sdtk, qs, qstk, scsb, sctk = (r_[k_] for k_ in (
                    "ck", "tsl", "wk", "wktk", "va", "vatk", "sd", "sdtk", "qs", "qstk", "scsb", "sctk"))
                pH, pHtk = psH.next()
                pD_, pDtk = psDS.next()
                for ec in range(5):
                    o = pH[:, ec * 128:(ec + 1) * 128] if ec < 4 else pD_[:, 0:128]
                    otk = pHtk if ec < 4 else pDtk
                    for dc in range(4):
                        P.op("pe", lambda e, o=o, ec=ec, dc=dc, qs=qs: e.matmul(
                            o, lhsT=Caug[:, dc, ec * 128:(ec + 1) * 128], rhs=qs[:, dc, :], start=(dc == 0), stop=False),
                            reads=[ctk2, qstk], writes=[otk], signal=False)
                    P.op("pe", lambda e, o=o, ec=ec, va=va, sd=sd: e.matmul(
                        o, lhsT=va[:, ec * 128:(ec + 1) * 128], rhs=sd, start=False, stop=True),
                        reads=[vatk, sdtk], writes=[otk], signal=True)

                r_.update(pH=pH, pHtk=pHtk, pD_=pD_, pDtk=pDtk)
                if dbg_stop is not None and (hd, ck) == tuple(dbg_stop):
                    P.barrier()
                    P.finish()
                    return nc, cx
                decay = scsb[:, 127:128]
                for dc in range(4):
                    ps, pstk = psB2.next()
                    P.op("pe", lambda e, ps=ps, dc=dc, wk=wk, va=va: e.matmul(ps, lhsT=wk[:, dc * 128:(dc + 1) * 128], rhs=va[:, 0:512],
                                                                                start=True, stop=True), reads=[wktk, vatk], writes=[pstk])
                    P.op("dve", lambda e, ps=ps, dc=dc, decay=decay: e.scalar_tensor_tensor(
                        out=Cst[:, dc, :], in0=Cst[:, dc, :], scalar=decay, in1=ps, op0=ALU.mult, op1=ALU.add),
                        reads=[pstk, sctk, ctk2], writes=[ctk2])
                    P.op("act", lambda e, dc=dc: e.activation(out=Caug[:, dc, 0:512], in_=Cst[:, dc, :], func=AF.Copy),
                         reads=[ctk2], writes=[ctk2])
                ps, pstk = psB2.next()
                P.op("pe", lambda e, ps=ps, wk=wk: e.matmul(ps, lhsT=ones_bf, rhs=wk, start=True, stop=True),
                     reads=[wktk, ctk], writes=[pstk])
                P.op("dve", lambda e, ps=ps, decay=decay: e.scalar_tensor_tensor(out=nrow, in0=nrow, scalar=decay, in1=ps,
                                                                                  op0=ALU.mult, op1=ALU.add),
                     reads=[pstk, sctk, ctk2], writes=[ctk2])
                refresh_caug(False)

            def stage_post(cl):
                r_ = RS[cl]
                ck, tsl, emsb, emtk, pH, pHtk, pD_, pDtk = (r_[k_] for k_ in ("ck", "tsl", "emsb", "emtk", "pH", "pHtk", "pD_", "pDtk"))
                hsq, hsqtk = hsq_rot.next()
                P.op("act", lambda e, pH=pH, hsq=hsq: e.activation(out=hsq, in_=pH, func=AF.Square), reads=[pHtk], writes=[hsqtk])
                pSS_, pSStk = psSS.next()
                pSS = pSS_[:, 0:128]
                for ec in range(4):
                    P.op("pe", lambda e, pSS=pSS, hsq=hsq, ec=ec: e.matmul(pSS, lhsT=ones_bf, rhs=hsq[:, ec * 128:(ec + 1) * 128],
                                                                            start=(ec == 0), stop=(ec == 3)),
                         reads=[hsqtk, ctk], writes=[pSStk], signal=(ec == 3))
                dd, ddtk = dd_rot.next()
                P.op("dve", lambda e, pD_=pD_, dd=dd: e.tensor_scalar(out=dd, in0=pD_[:, 0:128], scalar1=-1.0, scalar2=None, op0=ALU.mult),
                     reads=[pDtk], writes=[ddtk])
                P.op("dve", lambda e, pD_=pD_, dd=dd: e.tensor_tensor(out=dd, in0=dd, in1=pD_[:, 0:128], op=ALU.max),
                     reads=[pDtk, ddtk], writes=[ddtk])
                P.op("dve", lambda e, emsb=emsb, dd=dd: e.tensor_tensor(out=dd, in0=dd, in1=emsb, op=ALU.max),
                     reads=[emtk, ddtk], writes=[ddtk])
                P.op("dve", lambda e, dd=dd: e.scalar_tensor_tensor(out=dd, in0=dd, scalar=EPS, in1=dd, op0=ALU.mult, op1=ALU.mult),
                     reads=[ddtk], writes=[ddtk])
                rr, rrtk = rr_rot.next()
                P.op("dve", lambda e, pSS=pSS, dd=dd, rr=rr: e.scalar_tensor_tensor(out=rr, in0=pSS, scalar=1.0 / DH, in1=dd,
                                                                                     op0=ALU.mult, op1=ALU.add),
                     reads=[pSStk, ddtk], writes=[rrtk])
                P.op("act", lambda e, rr=rr: e.activation(out=rr, in_=rr, func=AF.Sqrt), reads=[rrtk], writes=[rrtk])
                P.op("dve", lambda e, rr=rr: e.reciprocal(out=rr, in_=rr), reads=[rrtk], writes=[rrtk])
                for ec in range(4):
                    ch = hd * 4 + ec
                    tt, tttk = tt_rot.next()
                    P.op("dve", lambda e, pH=pH, ec=ec, ch=ch, rr=rr, tt=tt: e.scalar_tensor_tensor(
                        out=tt, in0=pH[:, ec * 128:(ec + 1) * 128], scalar=cf[:, L_HGAIN + ch:L_HGAIN + ch + 1], in1=rr,
                        op0=ALU.mult, op1=ALU.mult), reads=[pHtk, rrtk, ctk], writes=[tttk])
                    P.op("dve", lambda e, ec=ec, ch=ch, tt=tt, tsl=tsl: e.scalar_tensor_tensor(
                        out=tt, in0=xc[:, ec, tsl], scalar=cf[:, L_SKIP + ch:L_SKIP + ch + 1], in1=tt,
                        op0=ALU.mult, op1=ALU.add), reads=[xctk, tttk, ctk], writes=[tttk])
                    P.op("dve", lambda e, ec=ec, tt=tt, tsl=tsl: e.tensor_tensor(out=yb[:, ec, tsl], in0=tt, in1=zs[:, ec, tsl], op=ALU.mult),
                         reads=[tttk, zstk], writes=[ytk])


            stage_pre(0)
            stage_mid(0)
            for cl in range(1, 4):
                stage_pre(cl)
                stage_post(cl - 1)
                stage_mid(cl)
            stage_post(3)

            P.dma("sync", yview[:, hd * 4:(hd + 1) * 4, tg * 512:(tg + 1) * 512], yb, reads=[ytk])
    P.barrier()
    cx.release(base_mark)

    X = TOP.rearrange("p (c n) -> p c n", c=NC8)
    Xtk = [[Tk() for _ in range(4)] for _ in range(NC8)]
    for c in range(NC8):
        for tg in range(4):
            P.dma("sync", X[:, c, tg * 512:(tg + 1) * 512], x1T[c * 128:(c + 1) * 128, tg * 512:(tg + 1) * 512], writes=[Xtk[c][tg]])
    mk = cx.mark()
    ws3 = WStream(cx, None, 4096, nstage=2, nslot=2)
    wdn = cx.sb(None, [128, 16, D], BF16, "wdn")
    wdtk = Tk()
    wdv = wdown.rearrange("(c p) n -> p c n", p=128)
    for q4 in range(4):
        wb_, wbtk = ws3.load([wdv[:, q4 * 4:(q4 + 1) * 4, :]])
        P.op("pool", lambda e, wb_=wb_, q4=q4: e.tensor_copy(out=wdn[:, q4 * 4:(q4 + 1) * 4, :], in_=wb_.rearrange("p (a b) -> p a b", a=4)),
             reads=[wbtk], writes=[wdtk])
    yts = [cx.sb(None, [128, 16, 512], BF16, "yt") for _ in range(2)]
    yttk = [Tk(), Tk()]
    psA4 = Rot(bk[0:4])
    for tg in range(4):
        yt, ytt = yts[tg % 2], yttk[tg % 2]
        P.dma("sync", yt, yview[:, :, tg * 512:(tg + 1) * 512], writes=[ytt])
        sl = slice(tg * 512, (tg + 1) * 512)
        for oc in range(NC8):
            ps, pstk = psA4.next()
            for mc in range(16):
                P.op("pe", lambda e, ps=ps, mc=mc, oc=oc, yt=yt: e.matmul(ps, lhsT=wdn[:, mc, oc * 128:(oc + 1) * 128], rhs=yt[:, mc, :],
                                                                           start=(mc == 0), stop=(mc == 15)),
                     reads=[wdtk, ytt], writes=[pstk], signal=(mc == 15))
            P.op("dve", lambda e, ps=ps, oc=oc, sl=sl: e.tensor_tensor(out=X[:, oc, sl], in0=ps, in1=X[:, oc, sl], op=ALU.add),
                 reads=[pstk, Xtk[oc][tg]], writes=[Xtk[oc][tg]])
    P.barrier()
    cx.release(mk)
    if debug:
        emit_store(cx, X, Xtk, dbg_a)
    emit_mlp(cx, X, Xtk, cf[:, L_MLPN:L_MLPN + 8], ctk, w1, w2, ones_bf)
    emit_ple(cx, X, Xtk, cf[:, L_PLEN:L_PLEN + 8], ctk, wg, wp, pT, ones_bf)
    emit_store(cx, X, Xtk, out)
    P.finish()
    return nc, cx


_CACHE = {}


def _prog(key, builder):
    return builder()


def kernel(**inputs):
    inputs = {k: np.asarray(v) for k, v in inputs.items()}
    cores = list(range(NCORES))
    nc, _ = build_layer0()
    in_maps = [layer0_inputs(inputs, c) for c in cores]
    res = run_bass_kernel_spmd(nc, in_maps, core_ids=cores)
    x1T = np.concatenate([r["xout"] for r in res.results], axis=1)
    nc, _ = build_layer1("B")
    in_maps = [layer1_inputs(inputs, c, x1T, "B") for c in cores]
    res = run_bass_kernel_spmd(nc, in_maps, core_ids=cores)
    st_all = np.stack([res.results[c]["st_out"] for c in range(7)])
    g_rows = [res.results[c]["g_out"] for c in cores]
    nc, _ = build_layer1("C")
    in_maps = [layer1_inputs(inputs, c, x1T, "C", st_all, g_rows[c]) for c in cores]
    res = run_bass_kernel_spmd(nc, in_maps, core_ids=cores)
    outT = np.concatenate([r["xout"] for r in res.results], axis=1)
    return np.ascontiguousarray(outT.T)[None].astype(np.float32)
```

```python
import numpy as np
import concourse.bass as bass
import concourse.mybir as mybir
from concourse.bass_utils import run_bass_kernel_spmd

F32 = mybir.dt.float32
BF16 = mybir.dt.bfloat16
AF = mybir.ActivationFunctionType
ALU = mybir.AluOpType
AX = mybir.AxisListType

NCORES = 8
S = 16384
D = 1024
NT = S // NCORES
NC8 = D // 128
EPS = 1e-6
BIG = 30000.0
A_GROUPS = ((128, 1), (512, 4), (2048, 16))
NDMA = 24
SB_F32 = 51968


class Tk:
    __slots__ = ("w", "r")

    def __init__(self):
        self.w = {}
        self.r = {}


class Prog:
    def __init__(self, nc):
        self.nc = nc
        self.eng = {"act": nc.scalar, "dve": nc.vector, "pool": nc.gpsimd, "pe": nc.tensor, "sync": nc.sync}
        self.sem = {e: nc.alloc_semaphore("s_" + e) for e in ("act", "dve", "pool", "pe")}
        self.cnt = {e: 0 for e in ("act", "dve", "pool", "pe")}
        self.seen = {e: {} for e in self.eng}
        self.dsem = [nc.alloc_semaphore("s_dma%d" % i) for i in range(NDMA)]
        self.dcnt = [0] * NDMA
        self.dnext = 0
        self.nins = {e: 0 for e in self.eng}

    def _semof(self, src):
        if isinstance(src, tuple):
            return self.dsem[src[1]]
        return self.sem[src]

    def _deps(self, e, reads, writes, allraw=False):
        deps = {}

        def add(src, n, raw):
            if src == e and not allraw:
                if e == "pe" or not raw:
                    return
            if deps.get(src, 0) < n:
                deps[src] = n

        for t in reads:
            for src, n in t.w.items():
                add(src, n, True)
        for t in writes:
            for src, n in t.w.items():
                add(src, n, False)
            for src, n in t.r.items():
                add(src, n, False)
        return deps

    def _wait(self, e, deps):
        eng = self.eng[e]
        seen = self.seen[e]
        for src, n in deps.items():
            if seen.get(src, 0) >= n:
                continue
            seen[src] = n
            eng.wait_ge(self._semof(src), n)
            self.nins[e] += 1

    def op(self, e, fn, reads=(), writes=(), signal=True):
        self._wait(e, self._deps(e, reads, writes))
        ins = fn(self.eng[e])
        self.nins[e] += 1
        n = self.cnt[e] + 1
        if signal:
            ins.then_inc(self.sem[e], 1)
            self.cnt[e] = n
        for t in reads:
            if t.r.get(e, 0) < n:
                t.r[e] = n
        for t in writes:
            if t.w.get(e, 0) < n:
                t.w[e] = n
        return ins

    def dma(self, q, out, in_, reads=(), writes=()):
        k = self.dnext
        self.dnext = (k + 1) % NDMA
        src = ("dma", k)
        deps = self._deps(q, reads, writes, allraw=True)
        if self.dcnt[k] > 0:
            deps[src] = max(deps.get(src, 0), self.dcnt[k])
        self._wait(q, deps)
        ins = self.eng[q].dma_start(out=out, in_=in_)
        self.nins[q] += 1
        n = self.dcnt[k] + 16
        ins.then_inc(self.dsem[k], 16)
        self.dcnt[k] = n
        for t in reads:
            t.r[src] = n
        for t in writes:
            t.w[src] = n

    def barrier(self):
        for e in self.eng:
            deps = {}
            for s2 in self.cnt:
                if s2 != e and self.cnt[s2] > 0:
                    deps[s2] = self.cnt[s2]
            for k in range(NDMA):
                if self.dcnt[k] > 0:
                    deps[("dma", k)] = self.dcnt[k]
            self._wait(e, deps)

    def finish(self):
        deps = {}
        for k in range(NDMA):
            if self.dcnt[k] > 0:
                deps[("dma", k)] = self.dcnt[k]
        self._wait("sync", deps)


class Rot:
    def __init__(self, aps):
        self.items = [a if isinstance(a, tuple) else (a, Tk()) for a in aps]
        self.i = 0

    def next(self):
        it = self.items[self.i]
        self.i = (self.i + 1) % len(self.items)
        return it


class Ctx:
    def __init__(self, nc):
        self.nc = nc
        self.P = Prog(nc)
        self.banks = [nc.alloc_psum_tensor("psb%d" % i, [128, 512], F32).ap() for i in range(8)]
        self.nalloc = 0

        self.big = nc.alloc_sbuf_tensor("big", [128, SB_F32], F32).ap()
        self.top = 0

    def sb(self, stack, shape, dt, name=None):
        esz = 2 if dt == BF16 else 4
        n = int(np.prod(shape[1:]))
        nbytes = (n * esz + 63) // 64 * 64
        off = self.top
        assert off + nbytes <= SB_F32 * 4, ("SBUF overflow", name, off, nbytes)
        self.top = off + nbytes
        self.log = getattr(self, 'log', [])
        self.log.append((name, off, nbytes))
        ap = self.big[:, off // 4:(off + nbytes) // 4]
        if dt == BF16:
            ap = ap.bitcast(BF16)
        ap = ap[:, 0:n]
        if len(shape) == 3:
            ap = ap.rearrange("p (a b) -> p a b", a=shape[1])
        elif len(shape) == 4:
            ap = ap.rearrange("p (a b c) -> p a b c", a=shape[1], b=shape[2])
        return ap

    def mark(self):
        return self.top

    def release(self, m):
        self.top = m


def load_consts(cx, stack, cst_ap, ncols):
    P = cx.P
    cf = cx.sb(stack, [128, ncols], F32, "cstf")
    cb = cx.sb(stack, [128, C_END_BF], BF16, "cstb")
    tk = Tk()
    P.dma("sync", cf, cst_ap, writes=[tk])
    P.op("dve", lambda e: e.tensor_copy(out=cb, in_=cf[:, 0:C_END_BF]), reads=[tk], writes=[tk])
    return cf, cb, tk


class WStream:
    def __init__(self, cx, stack, nelem, nstage=2, nslot=2):
        self.cx = cx
        self.nelem = nelem
        self.stage = Rot([cx.sb(stack, [128, nelem], F32, "wstg") for _ in range(nstage)])
        self.slots = Rot([cx.sb(stack, [128, nelem], BF16, "wbf") for _ in range(nslot)])

    def load(self, views):
        P = self.cx.P
        stg, stk = self.stage.next()
        wb, wtk = self.slots.next()
        off = 0
        for v in views:
            shp = v.shape
            n = int(np.prod(shp[1:]))
            dst = stg[:, off:off + n]
            if len(shp) == 3:
                dst = dst.rearrange("p (a b) -> p a b", a=shp[1])
            P.dma("sync", dst, v, writes=[stk])
            off += n
        assert off <= self.nelem
        P.op("pool", lambda e: e.tensor_copy(out=wb[:, 0:off], in_=stg[:, 0:off]), reads=[stk], writes=[wtk])
        return wb, wtk


def rms_stats(cx, xs, n, sq_rot, ps_ap, ps_tk, rstd, rstd_tk, ones_bf, ctk, inv_dim):
    P = cx.P
    nx = len(xs)
    for c, (xa, xt) in enumerate(xs):
        sq, sqt = sq_rot.next()
        P.op("act", lambda e, xa=xa, sq=sq: e.activation(out=sq[:, 0:n], in_=xa, func=AF.Square), reads=[xt], writes=[sqt])
        P.op("pe", lambda e, sq=sq, c=c: e.matmul(ps_ap[:, 0:n], lhsT=ones_bf, rhs=sq[:, 0:n], start=(c == 0), stop=(c == nx - 1)),
             reads=[sqt, ctk], writes=[ps_tk])
    P.op("act", lambda e: e.activation(out=rstd[:, 0:n], in_=ps_ap[:, 0:n], func=AF.Sqrt, bias=cx.eps_col, scale=inv_dim),
         reads=[ps_tk, ctk], writes=[rstd_tk])
    P.op("dve", lambda e: e.reciprocal(out=rstd[:, 0:n], in_=rstd[:, 0:n]), reads=[rstd_tk], writes=[rstd_tk])


C_ID, C_ONES, C_BONES, C_DM, C_HONES = 0, 128, 256, 384, 640
C_OZ = 704
C_HZ = 960
C_END_BF = 1216
C_EPS = 1216
C_GAINS = 1217
G0_ANORM = C_GAINS
G0_QG = G0_ANORM + 8
G0_KG = G0_QG + 3
G0_MLPN = G0_KG + 3
G0_PLEN = G0_MLPN + 8
G0_END = G0_PLEN + 8


def base_consts(core, ncols):
    c = np.zeros((128, ncols), np.float32)
    c[:, C_ID:C_ID + 128] = np.eye(128, dtype=np.float32)
    c[:, C_ONES:C_ONES + 128] = 1.0
    c[0:64, C_BONES:C_BONES + 64] = 1.0
    c[64:128, C_BONES + 64:C_BONES + 128] = 1.0
    kk = np.arange(128)[:, None]
    a = np.arange(128)[None, :]
    diag = np.where(kk <= a, a - kk, BIG)
    prev = np.where(kk >= a, 128 + a - kk, BIG)
    c[:, C_DM:C_DM + 128] = diag
    c[:, C_DM + 128:C_DM + 256] = prev
    hv = 0.0 if core == 0 else 1.0
    c[:, C_HONES:C_HONES + 64] = hv
    c[:, C_OZ:C_OZ + 64] = 1.0
    c[:, C_OZ + 128 + 64:C_OZ + 256] = 1.0
    c[:, C_HZ:C_HZ + 64] = hv
    c[:, C_HZ + 128 + 64:C_HZ + 256] = hv
    c[:, C_EPS] = EPS
    return c


def col_layout(v):
    v = np.asarray(v, np.float32).reshape(-1, 128)
    return np.ascontiguousarray(v.T)


def emit_norm_resident(cx, X, Xtk, gcol, ctk, hT, hTtk, sq_rot, rstd_rot, ps_rot, ones_bf):
    P = cx.P
    for tg in range(NT // 512):
        sl = slice(tg * 512, (tg + 1) * 512)
        xs = [(X[:, c, sl], Xtk[c][tg]) for c in range(NC8)]
        ps_ap, ps_tk = ps_rot.next()
        rstd, rtk = rstd_rot.next()
        rms_stats(cx, xs, 512, sq_rot, ps_ap, ps_tk, rstd, rtk, ones_bf, ctk, 1.0 / D)
        for c in range(NC8):
            P.op("dve", lambda e, c=c, sl=sl, rstd=rstd: e.scalar_tensor_tensor(
                out=hT[:, c, sl], in0=X[:, c, sl], scalar=gcol[:, c:c + 1], in1=rstd[:, 0:512],
                op0=ALU.mult, op1=ALU.mult), reads=[Xtk[c][tg], rtk, ctk], writes=[hTtk[c][tg]])


def emit_mlp(cx, X, Xtk, gcol, ctk, w1, w2, ones_bf):
    P = cx.P
    mk = cx.mark()
    st = None
    hT = cx.sb(st, [128, NC8, NT], BF16, "mlp_hT")
    hTtk = [[Tk() for _ in range(4)] for _ in range(NC8)]
    sq_rot = Rot([cx.sb(st, [128, 512], BF16, "sq") for _ in range(4)])
    rstd_rot = Rot([cx.sb(st, [128, 512], F32, "rstd") for _ in range(2)])
    ps_stat = Rot([cx.banks[7]])
    emit_norm_resident(cx, X, Xtk, gcol, ctk, hT, hTtk, sq_rot, rstd_rot, ps_stat, ones_bf)
    ws = WStream(cx, st, 4096, nstage=2, nslot=2)
    hids = [cx.sb(st, [128, 4, NT], BF16, "hid") for _ in range(2)]
    hid_tks = [[[Tk() for _ in range(4)] for _ in range(4)] for _ in range(2)]
    tmp_rot = Rot([cx.sb(st, [128, 512], F32, "rl") for _ in range(3)])
    psA = Rot(cx.banks[0:4])
    psB = Rot(cx.banks[4:7])
    w1v = w1.rearrange("(c p) n -> p c n", p=128)
    w2v = w2.rearrange("(c p) n -> p c n", p=128)
    NHB = 8
    for hb in range(NHB):
        hid = hids[hb % 2]
        htk = hid_tks[hb % 2]
        wa, watk = ws.load([w1v[:, :, hb * 512:(hb + 1) * 512]])
        wa3 = wa.rearrange("p (c n) -> p c n", c=NC8)
        for hc in range(4):
            for tg in range(4):
                sl = slice(tg * 512, (tg + 1) * 512)
                ps, pstk = psA.next()
                for c in range(NC8):
                    P.op("pe", lambda e, c=c, hc=hc, sl=sl, ps=ps, wa3=wa3: e.matmul(
                        ps, lhsT=wa3[:, c, hc * 128:(hc + 1) * 128], rhs=hT[:, c, sl],
                        start=(c == 0), stop=(c == NC8 - 1)),
                        reads=[watk, hTtk[c][tg]], writes=[pstk], signal=(c == NC8 - 1))
                tmp, ttk = tmp_rot.next()
                P.op("act", lambda e, ps=ps, tmp=tmp: e.activation(out=tmp, in_=ps, func=AF.Square),
                     reads=[pstk], writes=[ttk])
                P.op("dve", lambda e, ps=ps, tmp=tmp, hc=hc, sl=sl, hid=hid: e.scalar_tensor_tensor(
                    out=hid[:, hc, sl], in0=ps, scalar=0.0, in1=tmp, op0=ALU.is_gt, op1=ALU.mult),
                    reads=[pstk, ttk], writes=[htk[hc][tg]])
        wb, wbtk = ws.load([w2v[:, hb * 4:(hb + 1) * 4, :]])
        wb3 = wb.rearrange("p (c n) -> p c n", c=4)
        for oc in range(NC8):
            for tg in range(4):
                sl = slice(tg * 512, (tg + 1) * 512)
                ps, pstk = psB.next()
                for hc in range(4):
                    P.op("pe", lambda e, hc=hc, oc=oc, sl=sl, ps=ps, hid=hid, wb3=wb3: e.matmul(
                        ps, lhsT=wb3[:, hc, oc * 128:(oc + 1) * 128], rhs=hid[:, hc, sl],
                        start=(hc == 0), stop=(hc == 3)),
                        reads=[wbtk, htk[hc][tg]], writes=[pstk], signal=(hc == 3))
                P.op("dve", lambda e, oc=oc, sl=sl, ps=ps: e.tensor_tensor(
                    out=X[:, oc, sl], in0=ps, in1=X[:, oc, sl], op=ALU.add),
                    reads=[pstk, Xtk[oc][tg]], writes=[Xtk[oc][tg]])
    P.barrier()
    cx.release(mk)


def emit_ple(cx, X, Xtk, gcol, ctk, wg, wp, pT_dram, ones_bf):
    P = cx.P
    mk = cx.mark()
    st = None
    hT = cx.sb(st, [128, NC8, NT], BF16, "ple_hT")
    hTtk = [[Tk() for _ in range(4)] for _ in range(NC8)]
    sq_rot = Rot([cx.sb(st, [128, 512], BF16, "sq") for _ in range(4)])
    rstd_rot = Rot([cx.sb(st, [128, 512], F32, "rstd") for _ in range(2)])
    ps_stat = Rot([cx.banks[7]])
    emit_norm_resident(cx, X, Xtk, gcol, ctk, hT, hTtk, sq_rot, rstd_rot, ps_stat, ones_bf)
    ws = WStream(cx, st, 4096, nstage=2, nslot=3)
    pst = cx.sb(st, [128, 2, NT], F32, "pstg")
    pb = cx.sb(st, [128, 2, NT], BF16, "pbf")
    ptk = Tk()
    P.dma("sync", pst, pT_dram.rearrange("(c p) n -> p c n", p=128), writes=[ptk])
    P.op("pool", lambda e: e.tensor_copy(out=pb, in_=pst), reads=[ptk], writes=[ptk])
    wpb, wptk = ws.load([wp.rearrange("(c p) n -> p c n", p=128)])
    wp3 = wpb[:, 0:2048].rearrange("p (c n) -> p c n", c=2)
    gt_rot = Rot([cx.sb(st, [128, 512], F32, "gt") for _ in range(3)])
    psA = Rot(cx.banks[0:3])
    psB = Rot(cx.banks[3:6])
    wgv = wg.rearrange("(c p) n -> p c n", p=128)
    for half in range(2):
        wa, watk = ws.load([wgv[:, :, half * 512:(half + 1) * 512]])
        wa3 = wa.rearrange("p (c n) -> p c n", c=NC8)
        for o4 in range(4):
            oc = half * 4 + o4
            for tg in range(4):
                sl = slice(tg * 512, (tg + 1) * 512)
                ps, pstk = psA.next()
                for c in range(NC8):
                    P.op("pe", lambda e, c=c, o4=o4, sl=sl, ps=ps, wa3=wa3: e.matmul(
                        ps, lhsT=wa3[:, c, o4 * 128:(o4 + 1) * 128], rhs=hT[:, c, sl],
                        start=(c == 0), stop=(c == NC8 - 1)),
                        reads=[watk, hTtk[c][tg]], writes=[pstk], signal=(c == NC8 - 1))
                ps2, ps2tk = psB.next()
                for kc in range(2):
                    P.op("pe", lambda e, kc=kc, oc=oc, sl=sl, ps2=ps2: e.matmul(
                        ps2, lhsT=wp3[:, kc, oc * 128:(oc + 1) * 128], rhs=pb[:, kc, sl],
                        start=(kc == 0), stop=(kc == 1)),
                        reads=[wptk, ptk], writes=[ps2tk], signal=(kc == 1))
                gt, gtk = gt_rot.next()
                P.op("act", lambda e, ps=ps, gt=gt: e.activation(out=gt, in_=ps, func=AF.Sigmoid),
                     reads=[pstk], writes=[gtk])
                P.op("dve", lambda e, ps2=ps2, gt=gt: e.tensor_tensor(out=gt, in0=ps2, in1=gt, op=ALU.mult),
                     reads=[ps2tk, gtk], writes=[gtk])
                P.op("dve", lambda e, oc=oc, sl=sl, gt=gt: e.tensor_tensor(
                    out=X[:, oc, sl], in0=gt, in1=X[:, oc, sl], op=ALU.add),
                    reads=[gtk, Xtk[oc][tg]], writes=[Xtk[oc][tg]])
    P.barrier()
    cx.release(mk)


def alibi_slope(h):
    return 2.0 ** (-8.0 * (h + 1) / 16)


def sslice(start, count, step):
    return slice(start, start + (count - 1) * step + 1, step)


def emit_attention(cx, xT_ext, wqkv, wo, cf, cb, ctk, TOP, R1, lvl=9, hps=8):
    P = cx.P
    st = None
    ones_bf = cb[:, C_ONES:C_ONES + 128]
    bones = cb[:, C_BONES:C_BONES + 128]
    Dm = cf[:, C_DM:C_DM + 256]
    hT = TOP.bitcast(BF16).rearrange("p (c n) -> p c n", c=NC8)
    mk = cx.mark()
    sq_rot = Rot([cx.sb(st, [128, 512], BF16, "sq") for _ in range(2)])
    rstd_rot = Rot([cx.sb(st, [128, 512], F32, "rstd") for _ in range(2)])
    xstg = [R1[:, i * 4096:(i + 1) * 4096].rearrange("p (c n) -> p c n", c=NC8) for i in range(2)]
    xstk = [Tk(), Tk()]
    ps_stat = Rot([cx.banks[7], cx.banks[6]])
    httk = Tk()
    gcol = cf[:, G0_ANORM:G0_ANORM + 8]
    for tg in range(8 if lvl >= 1 else 0):
        xa = xstg[tg % 2]
        xt = xstk[tg % 2]
        P.dma("sync", xa, xT_ext[:, tg * 512:(tg + 1) * 512].rearrange("(c p) n -> p c n", p=128), writes=[xt])
        xs = [(xa[:, c, :], xt) for c in range(NC8)]
        ps_ap, ps_tk = ps_stat.next()
        rstd, rtk = rstd_rot.next()
        rms_stats(cx, xs, 512, sq_rot, ps_ap, ps_tk, rstd, rtk, ones_bf, ctk, 1.0 / D)
        for c in range(NC8):
            P.op("dve", lambda e, c=c, tg=tg, xa=xa, rstd=rstd: e.scalar_tensor_tensor(
                out=hT[:, c, tg * 512:(tg + 1) * 512], in0=xa[:, c, :], scalar=gcol[:, c:c + 1], in1=rstd[:, 0:512],
                op0=ALU.mult, op1=ALU.mult), reads=[xt, rtk, ctk], writes=[httk])
    P.op("dve", lambda e: e.tensor_scalar(out=cf[:, G0_QG:G0_QG + 3], in0=cf[:, G0_QG:G0_QG + 3], scalar1=0.125,
                                          scalar2=None, op0=ALU.mult), reads=[ctk], writes=[ctk])
    P.barrier()
    ACC = R1[:, 0:4096].rearrange("p (a n) -> p a n", a=2)
    oT = R1[:, 4096:12288].bitcast(BF16).rearrange("p (c n) -> p c n", c=NC8)
    acctk = Tk()
    ottk = Tk()
    ws = WStream(cx, st, 3072, nstage=1, nslot=2)
    QTz = [cx.sb(st, [128, NT], BF16, "QTz") for _ in range(2)]
    qtk = Tk()
    KT_rot = Rot([cx.sb(st, [128, 2 * NT], BF16, "KT") for _ in range(2)])
    Vz = [cx.sb(st, [128, 32, 128], BF16, "Vz") for _ in range(2)]
    vtk = Tk()
    for e2 in (0, 1):
        P.op("pool", lambda e, e2=e2: e.memset(QTz[e2], 0.0), writes=[qtk])
        P.op("pool", lambda e, e2=e2: e.memset(Vz[e2], 0.0), writes=[vtk])
    onesz = [cb[:, C_OZ:C_OZ + 128], cb[:, C_OZ + 128:C_OZ + 256]]
    honesz = [cb[:, C_HZ:C_HZ + 128], cb[:, C_HZ + 128:C_HZ + 256]]
    tmp_rot = Rot([cx.sb(st, [128, 256], F32, "stmp") for _ in range(4)])
    pt_rots = [Rot([cx.sb(st, [128, 256], BF16, "PT") for _ in range(6)]) for _ in range(2)]
    bk = [(cx.banks[i], Tk()) for i in range(8)]

    def half(i):
        return (bk[i][0][:, 0:256], bk[i][1])

    psQ = Rot(bk[0:2])
    psS = Rot([bk[2]])
    psV = Rot([bk[3]])
    psST0 = Rot([half(4), half(0)])
    psST1 = Rot([half(5), half(1)])
    psND = Rot([half(6), half(7), half(2), half(3)])
    wq_view = wqkv.rearrange("(c p) n -> p c n", p=128)

    def perm(ap2d, d):
        if d == 1:
            return ap2d
        return ap2d.rearrange("p (u r) -> p r u", r=d)

    def proj_piece(w3, wtk, j, e0, n, gain_col, out_buf, out_tk, d, Lx, u0):
        ps, pstk = psQ.next()
        for c in range(NC8):
            P.op("pe", lambda e, c=c, ps=ps: e.matmul(ps[:, 0:n], lhsT=w3[:, j, c, :], rhs=hT[:, c, e0:e0 + n],
                                                      start=(c == 0), stop=(c == NC8 - 1)),
                 reads=[wtk], writes=[pstk], signal=(c == NC8 - 1))
        ps2, ps2tk = psS.next()
        rstd, rtk = rstd_rot.next()
        rms_stats(cx, [(ps[:, 0:n], pstk)], n, sq_rot, ps2, ps2tk, rstd, rtk, bones, ctk, 1.0 / 64)
        outs = out_buf if isinstance(out_buf, list) else [(slice(0, 128), out_buf)]
        for (rows, ob) in outs:
            if d == 1:
                o = ob[rows, u0:u0 + n]
            else:
                o = ob[rows, 0:d * Lx].rearrange("p (r u) -> p r u", r=d)[:, :, u0:u0 + n // d]
            P.op("dve", lambda e, ps=ps, rstd=rstd, o=o, rows=rows: e.scalar_tensor_tensor(
                out=o, in0=perm(ps[rows, 0:n], d), scalar=gain_col[rows, :], in1=perm(rstd[rows, 0:n], d),
                op0=ALU.mult, op1=ALU.mult),
                reads=[pstk, rtk, ctk], writes=[out_tk])

    if lvl < 2:
        hps = 0
        P.op('dve', lambda e: e.memset(R1, 0.0), writes=[ottk])
    for hp in range(hps):
        for g, (W, d) in enumerate(A_GROUPS):
            L = NT // d
            Lk = (W + NT) // d
            nb = Lk // 128
            e_start = NT - W
            base = g * 3072 + hp * 128
            wb, wtk = ws.load([wq_view[:, :, base + j * 1024: base + j * 1024 + 128] for j in range(3)])
            w3 = wb[:, 0:3072].rearrange("p (j c n) -> p j c n", j=3, c=NC8)
            KT, ktk = KT_rot.next()
            for tg in range(4):
                proj_piece(w3, wtk, 0, NT + tg * 512, 512, cf[:, G0_QG + g:G0_QG + g + 1],
                           [(slice(0, 64), QTz[0]), (slice(64, 128), QTz[1])], qtk, d, L, tg * 512 // d)
            pieces = []
            if W < 512:
                pieces.append((e_start, W))
                e = NT
            else:
                e = e_start
            while e < 2 * NT:
                pieces.append((e, 512))
                e += 512
            for (e0, n) in pieces:
                proj_piece(w3, wtk, 1, e0, n, cf[:, G0_KG + g:G0_KG + g + 1], KT, ktk, d, Lk, (e0 - e_start) // d)
            nkb = d * nb if lvl >= 3 else 0
            kb = 0
            while kb < nkb:
                nblk = min(4, nkb - kb)
                psv, psvtk = psV.next()
                for b in range(nblk):
                    r, jb = divmod(kb + b, nb)
                    e_first = e_start + d * 128 * jb + r
                    for c in range(NC8):
                        P.op("pe", lambda e, c=c, b=b, e_first=e_first, psv=psv: e.matmul(
                            psv[:, b * 128:(b + 1) * 128], lhsT=hT[:, c, sslice(e_first, 128, d)], rhs=w3[:, 2, c, :],
                            start=(c == 0), stop=(c == NC8 - 1)),
                            reads=[wtk], writes=[psvtk], signal=(c == NC8 - 1 and b == nblk - 1))
                for e2 in (0, 1):
                    cs = slice(64 * e2, 64 * e2 + 64)
                    P.op("act", lambda e, kb=kb, nblk=nblk, psv=psv, e2=e2, cs=cs: e.activation(
                        out=Vz[e2][:, kb:kb + nblk, cs],
                        in_=psv[:, 0:nblk * 128].rearrange("p (b n) -> p b n", b=nblk)[:, :, cs], func=AF.Copy),
                        reads=[psvtk], writes=[vtk])
                kb += nblk
            PTs = {}

            def score_task(r, jb):
                lo = 128 if jb == 0 else 0
                hi = 128 if jb == nb - 1 else 256
                qb0 = jb if jb == 0 else jb - 1
                q_off = r * L + 128 * qb0
                sTs = [psST0.next(), psST1.next()]
                for e2 in (0, 1):
                    sT, sTtk = sTs[e2]
                    P.op("pe", lambda e, sT=sT, e2=e2: e.matmul(
                        sT[:, lo:hi], lhsT=KT[:, r * Lk + 128 * jb: r * Lk + 128 * jb + 128],
                        rhs=QTz[e2][:, q_off:q_off + (hi - lo)], start=True, stop=True),
                        reads=[ktk, qtk], writes=[sTtk])
                for e2 in (0, 1):
                    sig = alibi_slope(2 * hp + e2) * d
                    sT, sTtk = sTs[e2]
                    tmp, tmtk = tmp_rot.next()
                    P.op("dve", lambda e, sT=sT, tmp=tmp, sig=sig: e.scalar_tensor_tensor(
                        out=tmp[:, lo:hi], in0=Dm[:, lo:hi], scalar=-sig, in1=sT[:, lo:hi],
                        op0=ALU.mult, op1=ALU.add),
                        reads=[sTtk, ctk], writes=[tmtk])
                    pt, pttk = pt_rots[e2].next()
                    P.op("act", lambda e, tmp=tmp, pt=pt: e.activation(
                        out=pt[:, lo:hi], in_=tmp[:, lo:hi], func=AF.Exp), reads=[tmtk], writes=[pttk])
                    PTs[(e2, r, jb)] = (pt, pttk)

            def pv_task(r, j):
                jb = j + 1
                nd, ndtk = psND.next()
                kbp = r * nb + j
                kbd = r * nb + jb
                for part in (0, 1):
                    co = slice(128 * part, 128 * part + 128)
                    for e2 in (0, 1):
                        ptp, ptptk = PTs[(e2, r, j)]
                        ptd, ptdtk = PTs[(e2, r, jb)]
                        if part == 0:
                            lp, ld = Vz[e2][:, kbp, :], Vz[e2][:, kbd, :]
                        else:
                            lp, ld = (honesz[e2] if j == 0 else onesz[e2]), onesz[e2]
                        P.op("pe", lambda e, nd=nd, co=co, lp=lp, ptp=ptp, e2=e2: e.matmul(
                            nd[:, co], lhsT=lp, rhs=ptp[:, 128:256], start=(e2 == 0), stop=False),
                            reads=[vtk, ctk, ptptk], writes=[ndtk], signal=False)
                        P.op("pe", lambda e, nd=nd, co=co, ld=ld, ptd=ptd, e2=e2: e.matmul(
                            nd[:, co], lhsT=ld, rhs=ptd[:, 0:128], start=False, stop=(e2 == 1)),
                            reads=[vtk, ctk, ptdtk], writes=[ndtk], signal=(part == 1 and e2 == 1))
                t0 = r + d * 128 * j
                accv = ACC[:, :, sslice(t0, 128, d)]
                ndv = nd.rearrange("p (a n) -> p a n", a=2)
                if g == 0:
                    P.op("act", lambda e, accv=accv, ndv=ndv: e.activation(out=accv, in_=ndv, func=AF.Copy),
                         reads=[ndtk], writes=[acctk])
                else:
                    P.op("dve", lambda e, accv=accv, ndv=ndv: e.tensor_tensor(out=accv, in0=ndv, in1=accv, op=ALU.add),
                         reads=[ndtk, acctk], writes=[acctk])

            LA = 2
            pending = []
            tasks = [(r, jb) for r in range(d if lvl >= 4 else 0) for jb in range(nb)]
            for i, (r, jb) in enumerate(tasks):
                score_task(r, jb)
                if jb >= 1:
                    pending.append((i, r, jb - 1))
                while pending and pending[0][0] <= i - LA:
                    _, r_, j_ = pending.pop(0)
                    pv_task(r_, j_)
            for (_, r_, j_) in pending:
                pv_task(r_, j_)
        P.op("dve", lambda e: e.reciprocal(out=ACC[:, 1, :], in_=ACC[:, 1, :]), reads=[acctk], writes=[acctk])
        P.op("dve", lambda e, hp=hp: e.tensor_tensor(out=oT[:, hp, :], in0=ACC[:, 0, :], in1=ACC[:, 1, :], op=ALU.mult),
             reads=[acctk], writes=[ottk])
    P.barrier()
    cx.release(mk)
    mk = cx.mark()
    X = TOP.rearrange("p (c n) -> p c n", c=NC8)
    Xtk = [[Tk() for _ in range(4)] for _ in range(NC8)]
    for c in range(NC8):
        for tg in range(4):
            P.dma("sync", X[:, c, tg * 512:(tg + 1) * 512], xT_ext[c * 128:(c + 1) * 128, NT + tg * 512:NT + (tg + 1) * 512],
                  writes=[Xtk[c][tg]])
    ws2 = WStream(cx, st, 4096, nstage=2, nslot=2)
    wov = wo.rearrange("(c p) n -> p c n", p=128)
    psA = Rot(cx.banks[0:4])
    for half in range(2):
        wa, watk = ws2.load([wov[:, :, half * 512:(half + 1) * 512]])
        wa3 = wa.rearrange("p (c n) -> p c n", c=NC8)
        for o4 in range(4):
            oc = half * 4 + o4
            for tg in range(4):
                sl = slice(tg * 512, (tg + 1) * 512)
                ps, pstk = psA.next()
                for c in range(NC8):
                    P.op("pe", lambda e, c=c, o4=o4, sl=sl, ps=ps, wa3=wa3: e.matmul(
                        ps, lhsT=wa3[:, c, o4 * 128:(o4 + 1) * 128], rhs=oT[:, c, sl],
                        start=(c == 0), stop=(c == NC8 - 1)),
                        reads=[watk, ottk], writes=[pstk], signal=(c == NC8 - 1))
                P.op("dve", lambda e, oc=oc, sl=sl, ps=ps: e.tensor_tensor(
                    out=X[:, oc, sl], in0=ps, in1=X[:, oc, sl], op=ALU.add),
                    reads=[pstk, Xtk[oc][tg]], writes=[Xtk[oc][tg]])
    P.barrier()
    cx.release(mk)
    return X, Xtk


def emit_store(cx, X, Xtk, out_dram):
    P = cx.P
    for c in range(NC8):
        P.dma("sync", out_dram[c * 128:(c + 1) * 128, :], X[:, c, :], reads=Xtk[c])


def build_layer0(debug=False, lvl=9, hps=8):
    nc = bass.Bass("TRN2", target_bir_lowering=False)
    xT_ext = nc.dram_tensor("xT_ext", [D, 2 * NT], F32, kind="ExternalInput").ap()
    pT = nc.dram_tensor("pT", [256, NT], F32, kind="ExternalInput").ap()
    cst = nc.dram_tensor("cst", [128, G0_END], F32, kind="ExternalInput").ap()
    wqkv = nc.dram_tensor("a_w_qkv", [D, 9216], F32, kind="ExternalInput").ap()
    wo = nc.dram_tensor("a_w_o", [D, D], F32, kind="ExternalInput").ap()
    w1 = nc.dram_tensor("mlp_w1", [D, 4096], F32, kind="ExternalInput").ap()
    w2 = nc.dram_tensor("mlp_w2", [4096, D], F32, kind="ExternalInput").ap()
    wg = nc.dram_tensor("ple_w_gate", [D, D], F32, kind="ExternalInput").ap()
    wp = nc.dram_tensor("ple_w_proj", [256, D], F32, kind="ExternalInput").ap()
    out = nc.dram_tensor("xout", [D, NT], F32, kind="ExternalOutput").ap()
    if debug:
        dbg_a = nc.dram_tensor("dbg_a", [D, NT], F32, kind="ExternalOutput").ap()
        dbg_m = nc.dram_tensor("dbg_m", [D, NT], F32, kind="ExternalOutput").ap()
    cx = Ctx(nc)
    P = cx.P
    cf, cb, ctk = load_consts(cx, None, cst, G0_END)
    cx.eps_col = cf[:, C_EPS:C_EPS + 1]
    ones_bf = cb[:, C_ONES:C_ONES + 128]
    TOP = cx.sb(None, [128, 16384], F32, "TOP")
    R1 = cx.sb(None, [128, 12288], F32, "R1")
    mk = cx.mark()
    X, Xtk = emit_attention(cx, xT_ext, wqkv, wo, cf, cb, ctk, TOP, R1, lvl=lvl, hps=hps)
    cx.release(mk)
    cx.top = cx.top - 12288 * 4
    if debug:
        emit_store(cx, X, Xtk, dbg_a)
    emit_mlp(cx, X, Xtk, cf[:, G0_MLPN:G0_MLPN + 8], ctk, w1, w2, ones_bf)
    if debug:
        emit_store(cx, X, Xtk, dbg_m)
    emit_ple(cx, X, Xtk, cf[:, G0_PLEN:G0_PLEN + 8], ctk, wg, wp, pT, ones_bf)
    emit_store(cx, X, Xtk, out)
    P.finish()
    return nc, cx


def layer0_inputs(inputs, core):
    x = inputs["x"][0]
    lo = core * NT
    xe = np.zeros((2 * NT, D), np.float32)
    if core > 0:
        xe[:NT] = x[lo - NT:lo]
    xe[NT:] = x[lo:lo + NT]
    c = base_consts(core, G0_END)
    c[:, G0_ANORM:G0_ANORM + 8] = col_layout(inputs["a_norm"][0])
    c[:, G0_QG:G0_QG + 3] = np.tile(inputs["a_q_gain"][0].T, (2, 1))
    c[:, G0_KG:G0_KG + 3] = np.tile(inputs["a_k_gain"][0].T, (2, 1))
    c[:, G0_MLPN:G0_MLPN + 8] = col_layout(inputs["mlp_norm"][0])
    c[:, G0_PLEN:G0_PLEN + 8] = col_layout(inputs["ple_norm"][0])
    return {
        "xT_ext": np.ascontiguousarray(xe.T),
        "pT": np.ascontiguousarray(inputs["p"][0, 0, lo:lo + NT].T),
        "cst": c,
        "a_w_qkv": inputs["a_w_qkv"][0], "a_w_o": inputs["a_w_o"][0],
        "mlp_w1": inputs["mlp_w1"][0], "mlp_w2": inputs["mlp_w2"][0],
        "ple_w_gate": inputs["ple_w_gate"][0], "ple_w_proj": inputs["ple_w_proj"][0],
    }

BH = 4
DH = 512
NCK = NT // 128
KSCALE = DH ** -0.5
NST = 8192 + 2048 + 8

L_BNORM = C_GAINS
L_MLPN = L_BNORM + 8
L_PLEN = L_MLPN + 8
L_CONVW = L_PLEN + 8
L_CONVB = L_CONVW + 64
L_SKIP = L_CONVB + 16
L_HGAIN = L_SKIP + 16
L_BI = L_HGAIN + 16
L_BF = L_BI + 1
L_MASKLOW = L_BF + 1
L_SEL = L_MASKLOW + 128
L_CMASK = L_SEL + 512
L_NEG = L_CMASK + 7
L_E0 = L_NEG + 1
L_LNK = L_E0 + 1
L_CNEG = L_LNK + 1
L_END = L_CNEG + 7


def layer1_consts(inputs, core):
    c = base_consts(core, L_END)
    c[:, L_BNORM:L_BNORM + 8] = col_layout(inputs["b_norm"][0])
    c[:, L_MLPN:L_MLPN + 8] = col_layout(inputs["mlp_norm"][1])
    c[:, L_PLEN:L_PLEN + 8] = col_layout(inputs["ple_norm"][1])
    cw = inputs["b_conv_w"][0]
    c[:, L_CONVW:L_CONVW + 64] = cw.reshape(4, 16, 128).transpose(2, 1, 0).reshape(128, 64)
    c[:, L_CONVB:L_CONVB + 16] = col_layout(inputs["b_conv_b"][0])
    c[:, L_SKIP:L_SKIP + 16] = col_layout(inputs["b_skip"][0])
    c[:, L_HGAIN:L_HGAIN + 16] = col_layout(inputs["b_h_gain"][0])
    bg = inputs["b_b_gate"][0]
    c[0:4, L_BI] = bg[0:4]
    c[0:4, L_BF] = bg[4:8]
    s_ = np.arange(128)[:, None]
    t_ = np.arange(128)[None, :]
    c[:, L_MASKLOW:L_MASKLOW + 128] = np.where(s_ <= t_, 0.0, BIG)
    for hd in range(4):
        c[hd, L_SEL + hd * 128:L_SEL + (hd + 1) * 128] = 1.0
    for cp in range(7):
        c[:, L_CMASK + cp] = 1.0 if cp < core else 0.0
        c[:, L_CNEG + cp] = 0.0 if cp < core else -1e30
    c[:, L_NEG] = -1e30
    c[0, L_E0] = 1.0
    c[:, L_LNK] = np.log(KSCALE)
    return c


def bd_compact(w, transpose=False):
    out = np.zeros((2048, 128), np.float32)
    n = np.arange(512)
    for j in range(4):
        for k in range(4):
            if transpose:
                out[4 * n + k, (4 * n + j) % 128] = w[:, j, k]
            else:
                out[4 * n + j, (4 * n + k) % 128] = w[:, j, k]
    return out


def layer1_inputs(inputs, core, x1T_full, stage, st_all=None, g_in=None):
    lo = core * NT
    xh = np.zeros((D, 4), np.float32)
    if core > 0:
        xh[:, 1:4] = x1T_full[:, lo - 3:lo]
    m = {
        "x1T": np.ascontiguousarray(x1T_full[:, lo:lo + NT]),
        "xh": xh,
        "cst": layer1_consts(inputs, core),
        "b_w_up": inputs["b_w_up"][0],
        "bd": np.stack([bd_compact(inputs["b_w_q"][0]), bd_compact(inputs["b_w_k"][0]), bd_compact(inputs["b_w_v"][0])]),
        "bdT": np.stack([bd_compact(inputs["b_w_q"][0], True), bd_compact(inputs["b_w_k"][0], True),
                         bd_compact(inputs["b_w_v"][0], True)]),
        "w_gate": inputs["b_w_gate"][0],
    }
    if stage == "C":
        m.update({
            "st_all": st_all,
            "g_in": g_in,
            "b_w_down": inputs["b_w_down"][0],
            "pT": np.ascontiguousarray(inputs["p"][1, 0, lo:lo + NT].T),
            "mlp_w1": inputs["mlp_w1"][1], "mlp_w2": inputs["mlp_w2"][1],
            "ple_w_gate": inputs["ple_w_gate"][1], "ple_w_proj": inputs["ple_w_proj"][1],
        })
    return m


def build_layer1(stage, debug=False, dbg_stop=None):
    nc = bass.Bass("TRN2", target_bir_lowering=False)
    x1T = nc.dram_tensor("x1T", [D, NT], F32, kind="ExternalInput").ap()
    xh = nc.dram_tensor("xh", [D, 4], F32, kind="ExternalInput").ap()
    cst = nc.dram_tensor("cst", [128, L_END], F32, kind="ExternalInput").ap()
    wup = nc.dram_tensor("b_w_up", [D, 4096], F32, kind="ExternalInput").ap()
    bd = nc.dram_tensor("bd", [3, 2048, 128], F32, kind="ExternalInput").ap()
    bdT = nc.dram_tensor("bdT", [3, 2048, 128], F32, kind="ExternalInput").ap()
    wgate = nc.dram_tensor("w_gate", [6144, 8], F32, kind="ExternalInput").ap()
    if stage == "B":
        st_out = nc.dram_tensor("st_out", [128, NST], F32, kind="ExternalOutput").ap()
        g_out = nc.dram_tensor("g_out", [8, NT], F32, kind="ExternalOutput").ap()
    else:
        st_all = nc.dram_tensor("st_all", [7, 128, NST], F32, kind="ExternalInput").ap()
        g_in = nc.dram_tensor("g_in", [8, NT], F32, kind="ExternalInput").ap()
        wdown = nc.dram_tensor("b_w_down", [2048, D], F32, kind="ExternalInput").ap()
        pT = nc.dram_tensor("pT", [256, NT], F32, kind="ExternalInput").ap()
        w1 = nc.dram_tensor("mlp_w1", [D, 4096], F32, kind="ExternalInput").ap()
        w2 = nc.dram_tensor("mlp_w2", [4096, D], F32, kind="ExternalInput").ap()
        wg = nc.dram_tensor("ple_w_gate", [D, D], F32, kind="ExternalInput").ap()
        wp = nc.dram_tensor("ple_w_proj", [256, D], F32, kind="ExternalInput").ap()
        out = nc.dram_tensor("xout", [D, NT], F32, kind="ExternalOutput").ap()
        yscr = nc.dram_tensor("yscr", [2048, NT], BF16).ap()
        if debug:
            dbg_a = nc.dram_tensor("dbg_a", [D, NT], F32, kind="ExternalOutput").ap()
    cx = Ctx(nc)
    P = cx.P
    cf, cb, ctk = load_consts(cx, None, cst, L_END)
    cx.eps_col = cf[:, C_EPS:C_EPS + 1]
    ones_bf = cb[:, C_ONES:C_ONES + 128]
    ones_f = cf[:, C_ONES:C_ONES + 128]
    ident_f = cf[:, C_ID:C_ID + 128]
    one_col = cf[:, C_ONES:C_ONES + 1]
    TOP = cx.sb(None, [128, 16384], F32, "TOP")
    hT = TOP[:, 0:8208].bitcast(BF16)[:, 0:8 * 2052].rearrange("p (c n) -> p c n", c=NC8)
    topfree = TOP[:, 8208:16384]
    base_mark = cx.mark()
    bk = [(cx.banks[i], Tk()) for i in range(8)]
    ws = WStream(cx, None, 4096, nstage=0, nslot=3)
    ws.stage = Rot([topfree[:, 0:4096]])
    bdh = cx.sb(None, [128, 3, 4, 128], BF16, "bdh")
    bdh_st = cx.sb(None, [128, 3, 4, 128], F32, "bdh_st")
    diag = cx.sb(None, [128, 4, 4, 128], BF16, "diag")
    xms = [cx.sb(None, [128, 4, 516], BF16, "xm") for _ in range(2)]
    xc = cx.sb(None, [128, 4, 512], BF16, "xc")
    GF = cx.sb(None, [128, NT], F32, "GF")
    BETAx = cx.sb(None, [128, NT + 1], F32, "BETAx")
    small = cx.sb(None, [128, 64], F32, "small")
    TMw = cx.sb(None, [128, NCK, 4], F32, "TMw")
    TMa = cx.sb(None, [128, NCK, 4], F32, "TMa")
    if stage == "C":
        fold = cx.sb(None, [128, 7, 8], F32, "foldin")
        S1 = cx.sb(None, [128, 7, 4], F32, "S1")
        S2 = cx.sb(None, [128, 7, 4], F32, "S2")
        mrun = cx.sb(None, [128, 4], F32, "mrun")
        fa = cx.sb(None, [128, 4], F32, "fa")
        fb = cx.sb(None, [128, 4], F32, "fb")
        fc_ = cx.sb(None, [128, 4], F32, "fc")
    pers_mark = cx.mark()

    mk = cx.mark()
    sq_rot = Rot([cx.sb(None, [128, 512], BF16, "sq") for _ in range(2)])
    rstd_rot = Rot([cx.sb(None, [128, 512], F32, "rstd") for _ in range(2)])
    xstg = [cx.sb(None, [128, NC8, 512], F32, "xstg") for _ in range(2)]
    xstk = [Tk(), Tk()]
    ps_stat = Rot([bk[7], bk[6]])
    gcol = cf[:, L_BNORM:L_BNORM + 8]
    httk = Tk()
    pieces = [(None, 4)] + [(tg, 512) for tg in range(4)]
    for i, (tg, n) in enumerate(pieces):
        xa = xstg[i % 2]
        xt = xstk[i % 2]
        if tg is None:
            P.dma("sync", xa[:, :, 0:4], xh.rearrange("(c p) n -> p c n", p=128), writes=[xt])
            h0 = 0
        else:
            P.dma("sync", xa, x1T[:, tg * 512:(tg + 1) * 512].rearrange("(c p) n -> p c n", p=128), writes=[xt])
            h0 = 4 + tg * 512
        xs = [(xa[:, c, 0:n], xt) for c in range(NC8)]
        ps_ap, ps_tk = ps_stat.next()
        rstd, rtk = rstd_rot.next()
        rms_stats(cx, xs, n, sq_rot, ps_ap, ps_tk, rstd, rtk, ones_bf, ctk, 1.0 / D)
        for c in range(NC8):
            P.op("dve", lambda e, c=c, xa=xa, rstd=rstd, n=n, h0=h0: e.scalar_tensor_tensor(
                out=hT[:, c, h0:h0 + n], in0=xa[:, c, 0:n], scalar=gcol[:, c:c + 1], in1=rstd[:, 0:n],
                op0=ALU.mult, op1=ALU.mult), reads=[xt, rtk, ctk], writes=[httk])
    P.barrier()
    cx.release(mk)

    GI = cx.sb(None, [128, NT], F32, "GI")
    LF = cx.sb(None, [128, NT], F32, "LF")
    BB = cx.sb(None, [128, NT], F32, "BB")
    T1 = cx.sb(None, [128, NT], F32, "T1")
    wfold = [[cx.sb(None, [128, 16, 128], BF16, "wfold") for _ in range(2)] for _ in range(2)]
    wftk = Tk()
    bdT_sb = topfree[:, 0:6144].rearrange("p (a b) -> p a b", a=48)
    wg_sb = topfree[:, 6144:6528].rearrange("p (a b) -> p a b", a=48)
    btk = Tk()
    for j in range(3 if stage == "B" else 0):
        P.dma("sync", bdT_sb[:, j * 16:(j + 1) * 16, :], bdT[j].rearrange("(c p) n -> p c n", p=128), writes=[btk])
    if stage == "B":
        P.dma("sync", wg_sb, wgate.rearrange("(c p) n -> p c n", p=128), writes=[btk])
    zpad = Rot([topfree[:, 6528 + i * 128:6528 + (i + 1) * 128] for i in range(4)])
    for (za, ztk) in zpad.items:
        P.op("pool", lambda e, za=za: e.memset(za, 0.0), writes=[ztk])
    psF = Rot(bk[0:2])
    for mc in range(16 if stage == "B" else 0):
        for part in range(2):
            for xm_ in range(2):
                ps, pstk = psF.next()
                srcs = (0, 1) if xm_ == 0 else (2,)
                for si, j in enumerate(srcs):
                    za, ztk = zpad.next()
                    P.op("dve", lambda e, za=za, j=j, mc=mc, part=part: e.tensor_copy(
                        out=za[:, 0:4], in_=wg_sb[:, j * 16 + mc, part * 4:part * 4 + 4]), reads=[btk], writes=[ztk])
                    P.op("pe", lambda e, ps=ps, za=za, j=j, mc=mc, si=si, srcs=srcs: e.matmul(
                        ps[:, 0:128], lhsT=bdT_sb[:, j * 16 + mc, :], rhs=za, start=(si == 0), stop=(si == len(srcs) - 1)),
                        reads=[btk, ztk], writes=[pstk])
                P.op("act", lambda e, ps=ps, xm_=xm_, part=part, mc=mc: e.activation(
                    out=wfold[xm_][part][:, mc, :], in_=ps[:, 0:128], func=AF.Copy), reads=[pstk], writes=[wftk])
    P.barrier()

    hdtk = Tk()
    xmtk = [Tk(), Tk()]
    xctk = Tk()
    psA = Rot(bk[0:2])
    wupv = wup.rearrange("(c p) n -> p c n", p=128)
    bdv = bd.rearrange("j (c p) n -> p j c n", p=128)
    state = {"i": 0}

    def head_setup(hd):
        for j in range(3):
            P.dma("sync", bdh_st[:, j, :, :], bdv[:, j, hd * 4:(hd + 1) * 4, :], writes=[hdtk])
        P.op("pool", lambda e: e.tensor_copy(out=bdh, in_=bdh_st), reads=[hdtk], writes=[hdtk])
        for mc in range(4):
            for k in range(4):
                col = L_CONVW + (hd * 4 + mc) * 4 + k
                P.op("act", lambda e, mc=mc, k=k, col=col: e.activation(
                    out=diag[:, mc, k, :], in_=ident_f, func=AF.Copy, scale=cf[:, col:col + 1]),
                    reads=[ctk], writes=[hdtk])
        wx, wxtk = ws.load([wupv[:, :, hd * 512:(hd + 1) * 512]])
        return wx.rearrange("p (c n) -> p c n", c=NC8), wxtk

    def front(hd, tg, wx3, wxtk):
        i = state["i"]
        state["i"] += 1
        xm, xmt = xms[i % 2], xmtk[i % 2]
        xmp, xmpt = xms[(i + 1) % 2], xmtk[(i + 1) % 2]
        for mc in range(4):
            ps, pstk = psA.next()
            for c in range(NC8):
                P.op("pe", lambda e, c=c, mc=mc, ps=ps: e.matmul(
                    ps, lhsT=wx3[:, c, mc * 128:(mc + 1) * 128], rhs=hT[:, c, 4 + tg * 512:4 + (tg + 1) * 512],
                    start=(c == 0), stop=(c == NC8 - 1)), reads=[wxtk], writes=[pstk], signal=(c == NC8 - 1))
            P.op("act", lambda e, ps=ps, mc=mc, xm=xm: e.activation(out=xm[:, mc, 4:516], in_=ps, func=AF.Copy),
                 reads=[pstk], writes=[xmt])
            if tg == 0:
                ps, pstk = psA.next()
                for c in range(NC8):
                    P.op("pe", lambda e, c=c, mc=mc, ps=ps: e.matmul(
                        ps[:, 0:4], lhsT=wx3[:, c, mc * 128:(mc + 1) * 128], rhs=hT[:, c, 0:4],
                        start=(c == 0), stop=(c == NC8 - 1)), reads=[wxtk], writes=[pstk], signal=(c == NC8 - 1))
                P.op("act", lambda e, ps=ps, mc=mc, xm=xm: e.activation(out=xm[:, mc, 0:4], in_=ps[:, 0:4], func=AF.Copy),
                     reads=[pstk], writes=[xmt])
        if tg > 0:
            P.op("pool", lambda e, xm=xm, xmp=xmp: e.tensor_copy(out=xm[:, :, 0:4], in_=xmp[:, :, 512:516]),
                 reads=[xmpt], writes=[xmt])
        for mc in range(4):
            ps, pstk = psA.next()
            for k in range(4):
                P.op("pe", lambda e, k=k, mc=mc, ps=ps, xm=xm: e.matmul(
                    ps, lhsT=diag[:, mc, k, :], rhs=xm[:, mc, 1 + k:1 + k + 512], start=(k == 0), stop=(k == 3)),
                    reads=[hdtk, xmt], writes=[pstk], signal=(k == 3))
            col = L_CONVB + hd * 4 + mc
            P.op("act", lambda e, ps=ps, mc=mc, col=col: e.activation(
                out=xc[:, mc, :], in_=ps, func=AF.Silu, bias=cf[:, col:col + 1]), reads=[pstk, ctk], writes=[xctk])
        return xm, xmt

    gtk = Tk()
    psG = Rot(bk[2:4])
    if stage == "C":
        P.op("pool", lambda e: e.memset(GI, 0.0), writes=[gtk])
        P.op("pool", lambda e: e.memset(GF, 0.0), writes=[gtk])
        P.dma("sync", GI[0:4, :], g_in[0:4, :], writes=[gtk])
        P.dma("sync", GF[0:4, :], g_in[4:8, :], writes=[gtk])
    for hd in range(BH if stage == "B" else 0):
        wx3, wxtk = head_setup(hd)
        for tg in range(4):
            xm, xmt = front(hd, tg, wx3, wxtk)
            for part, Grow in ((0, GI), (1, GF)):
                ps, pstk = psG.next()
                for mc in range(4):
                    P.op("pe", lambda e, ps=ps, mc=mc, part=part: e.matmul(
                        ps, lhsT=wfold[0][part][:, hd * 4 + mc, :], rhs=xc[:, mc, :], start=(mc == 0), stop=False),
                        reads=[wftk, xctk], writes=[pstk], signal=False)
                    P.op("pe", lambda e, ps=ps, mc=mc, part=part, xm=xm: e.matmul(
                        ps, lhsT=wfold[1][part][:, hd * 4 + mc, :], rhs=xm[:, mc, 4:516], start=False, stop=(mc == 3)),
                        reads=[wftk, xmt], writes=[pstk], signal=(mc == 3))
                sl = slice(tg * 512, (tg + 1) * 512)
                if hd == 0:
                    P.op("act", lambda e, ps=ps, Grow=Grow, sl=sl: e.activation(out=Grow[:, sl], in_=ps, func=AF.Copy),
                         reads=[pstk], writes=[gtk])
                else:
                    P.op("dve", lambda e, ps=ps, Grow=Grow, sl=sl: e.tensor_tensor(out=Grow[:, sl], in0=ps, in1=Grow[:, sl], op=ALU.add),
                         reads=[pstk, gtk], writes=[gtk])

    rtk = Tk()
    if stage == "B":
        P.dma("sync", g_out[0:4, :], GI[0:4, :], reads=[gtk])
        P.dma("sync", g_out[4:8, :], GF[0:4, :], reads=[gtk])
    P.op("dve", lambda e: e.tensor_scalar(out=GI, in0=GI, scalar1=cf[:, L_BI:L_BI + 1], scalar2=None, op0=ALU.add),
         reads=[gtk, ctk], writes=[gtk])
    P.op("dve", lambda e: e.tensor_scalar(out=GF, in0=GF, scalar1=cf[:, L_BF:L_BF + 1], scalar2=None, op0=ALU.add),
         reads=[gtk, ctk], writes=[gtk])
    P.op("dve", lambda e: e.tensor_scalar(out=T1, in0=GF, scalar1=-1.0, scalar2=None, op0=ALU.mult), reads=[gtk], writes=[rtk])
    P.op("dve", lambda e: e.tensor_tensor(out=T1, in0=T1, in1=GF, op=ALU.max), reads=[gtk, rtk], writes=[rtk])
    P.op("act", lambda e: e.activation(out=T1, in_=T1, func=AF.Exp, scale=-1.0), reads=[rtk], writes=[rtk])
    P.op("act", lambda e: e.activation(out=T1, in_=T1, func=AF.Ln, bias=one_col), reads=[rtk, ctk], writes=[rtk])
    P.op("dve", lambda e: e.scalar_tensor_tensor(out=LF, in0=GF, scalar=0.0, in1=T1, op0=ALU.min, op1=ALU.subtract),
         reads=[gtk, rtk], writes=[rtk])
    P.op("pool", lambda e: e.memset(T1, 1.0), reads=[rtk], writes=[rtk])
    P.op("dve", lambda e: e.tensor_tensor_scan(out=BB, data0=T1, data1=LF, initial=0.0, op0=ALU.mult, op1=ALU.add),
         reads=[rtk], writes=[rtk])
    P.op("dve", lambda e: e.tensor_tensor(out=T1, in0=GI, in1=BB, op=ALU.subtract), reads=[gtk, rtk], writes=[rtk])
    ALPHA = T1
    psR = Rot([bk[4]])
    tmtk = Tk()

    def to_token_major(row, dst):
        ps, pstk = psR.next()
        for ck in range(NCK):
            P.op("pe", lambda e, ps=ps, ck=ck: e.matmul(ps[:, ck * 4:ck * 4 + 4], lhsT=row[:, ck * 128:(ck + 1) * 128],
                                                        rhs=ident_f[:, 0:4], start=True, stop=True),
                 reads=[rtk, gtk, ctk], writes=[pstk], signal=(ck == NCK - 1))
        P.op("act", lambda e, ps=ps: e.activation(out=dst, in_=ps[:, 0:64].rearrange("p (a b) -> p a b", a=NCK), func=AF.Copy),
             reads=[pstk], writes=[tmtk])

    def replicate_cols(col_ap, dst4):
        ps, pstk = psR.next()
        za = small[:, 32:32 + 4]
        P.op("dve", lambda e: e.tensor_scalar(out=za, in0=ident_f[:, 0:4], scalar1=col_ap, scalar2=None, op0=ALU.mult),
             reads=[rtk, ctk, gtk], writes=[rtk])
        P.op("pe", lambda e, ps=ps: e.matmul(ps[:, 0:4], lhsT=ones_f, rhs=za, start=True, stop=True),
             reads=[rtk, ctk], writes=[pstk])
        P.op("act", lambda e, ps=ps: e.activation(out=dst4, in_=ps[:, 0:4], func=AF.Copy), reads=[pstk], writes=[rtk])

    if stage == "B":
        mx = small[:, 0:1]
        P.op("dve", lambda e: e.tensor_reduce(out=mx, in_=ALPHA, axis=AX.X, op=ALU.max), reads=[rtk], writes=[rtk])
        nb_ = small[:, 1:2]
        P.op("dve", lambda e: e.scalar_tensor_tensor(out=nb_, in0=mx, scalar=-1.0, in1=cf[:, L_LNK:L_LNK + 1],
                                                     op0=ALU.mult, op1=ALU.add), reads=[rtk, ctk], writes=[rtk])
        P.op("act", lambda e: e.activation(out=LF, in_=ALPHA, func=AF.Exp, bias=nb_), reads=[rtk], writes=[rtk])
        to_token_major(LF, TMw)
        ml = small[:, 2:3]
        P.op("dve", lambda e: e.tensor_tensor(out=ml, in0=mx, in1=BB[:, NT - 1:NT], op=ALU.add), reads=[rtk], writes=[rtk])
        fin = cx.sb(None, [128, 8], F32, "fin")
        replicate_cols(BB[:, NT - 1:NT], fin[:, 0:4])
        replicate_cols(ml, fin[:, 4:8])
        P.dma("sync", st_out[:, 10240:10248], fin, reads=[rtk])
        P.barrier()
        cx.release(pers_mark)
        kv_rot = Rot([cx.sb(None, [128, 512], BF16, "kv") for _ in range(4)])
        stC = cx.sb(None, [128, 4, 512], F32, "stC")
        stn = cx.sb(None, [128, 512], F32, "stn")
        sttk = Tk()
        psKV = Rot([bk[2]])
        for hd in range(BH):
            wx3, wxtk = head_setup(hd)
            cacc = [bk[3 + dc] for dc in range(4)]
            nacc, nacctk = bk[7]
            for tg in range(4):
                xm, xmt = front(hd, tg, wx3, wxtk)
                for cl in range(4):
                    ck = tg * 4 + cl
                    tsl = slice(cl * 128, (cl + 1) * 128)
                    ps, pstk = psKV.next()
                    for mc in range(4):
                        P.op("pe", lambda e, ps=ps, mc=mc, tsl=tsl: e.matmul(
                            ps[:, mc * 128:(mc + 1) * 128], lhsT=xc[:, mc, tsl], rhs=bdh[:, 1, mc, :], start=True, stop=True),
                            reads=[xctk, hdtk], writes=[pstk], signal=(mc == 3))
                    wk, wktk = kv_rot.next()
                    P.op("act", lambda e, ps=ps, wk=wk, ck=ck, hd=hd: e.activation(
                        out=wk, in_=ps, func=AF.Copy, scale=TMw[:, ck, hd:hd + 1]), reads=[pstk, tmtk], writes=[wktk])
                    ps, pstk = psKV.next()
                    for mc in range(4):
                        P.op("pe", lambda e, ps=ps, mc=mc, cl=cl, xm=xm: e.matmul(
                            ps[:, mc * 128:(mc + 1) * 128], lhsT=xm[:, mc, 4 + cl * 128:4 + (cl + 1) * 128], rhs=bdh[:, 2, mc, :],
                            start=True, stop=True), reads=[xmt, hdtk], writes=[pstk], signal=(mc == 3))
                    vv, vtk = kv_rot.next()
                    P.op("act", lambda e, ps=ps, vv=vv: e.activation(out=vv, in_=ps, func=AF.Copy), reads=[pstk], writes=[vtk])
                    last = (ck == NCK - 1)
                    for dc in range(4):
                        P.op("pe", lambda e, dc=dc, wk=wk, vv=vv, ck=ck, last=last: e.matmul(
                            cacc[dc][0], lhsT=wk[:, dc * 128:(dc + 1) * 128], rhs=vv, start=(ck == 0), stop=last),
                            reads=[wktk, vtk], writes=[cacc[dc][1]], signal=True)
                    P.op("pe", lambda e, wk=wk, ck=ck, last=last: e.matmul(
                        nacc, lhsT=ones_bf, rhs=wk, start=(ck == 0), stop=last), reads=[wktk, ctk], writes=[nacctk], signal=True)
            for dc in range(4):
                P.op("act", lambda e, dc=dc: e.activation(out=stC[:, dc, :], in_=cacc[dc][0], func=AF.Copy),
                     reads=[cacc[dc][1]], writes=[sttk])
            P.op("dve", lambda e: e.tensor_copy(out=stn, in_=nacc), reads=[nacctk], writes=[sttk])
            P.dma("sync", st_out[:, hd * 2048:(hd + 1) * 2048], stC.rearrange("p a b -> p (a b)"), reads=[sttk])
            P.dma("sync", st_out[:, 8192 + hd * 512:8192 + (hd + 1) * 512], stn, reads=[sttk])
        P.finish()
        return nc, cx

    ftk = Tk()
    P.dma("sync", fold, st_all[:, :, 10240:10248].rearrange("c p n -> p c n"), writes=[ftk])
    negc = cf[:, L_NEG:L_NEG + 1]
    P.op("dve", lambda e: e.memset(mrun, -1e30), writes=[ftk])
    for cp in range(7):
        mu = cf[:, L_CMASK + cp:L_CMASK + cp + 1]
        P.op("dve", lambda e, cp=cp, mu=mu: e.scalar_tensor_tensor(out=fa, in0=fold[:, cp, 0:4], scalar=mu, in1=mrun,
                                                                    op0=ALU.mult, op1=ALU.add), reads=[ftk, ctk], writes=[ftk])
        P.op("dve", lambda e, cp=cp, mu=mu: e.tensor_scalar(out=fb, in0=fold[:, cp, 4:8], scalar1=mu,
                                                            scalar2=cf[:, L_CNEG + cp:L_CNEG + cp + 1], op0=ALU.mult, op1=ALU.add),
             reads=[ftk, ctk], writes=[ftk])
        P.op("dve", lambda e: e.tensor_tensor(out=fc_, in0=fa, in1=fb, op=ALU.max), reads=[ftk], writes=[ftk])
        P.op("dve", lambda e: e.tensor_tensor(out=fa, in0=fa, in1=fc_, op=ALU.subtract), reads=[ftk], writes=[ftk])
        P.op("dve", lambda e: e.tensor_tensor(out=fb, in0=fb, in1=fc_, op=ALU.subtract), reads=[ftk], writes=[ftk])
        P.op("act", lambda e, cp=cp: e.activation(out=S1[:, cp, :], in_=fa, func=AF.Exp), reads=[ftk], writes=[ftk])
        P.op("act", lambda e: e.activation(out=fb, in_=fb, func=AF.Exp), reads=[ftk], writes=[ftk])
        P.op("dve", lambda e, cp=cp, mu=mu: e.tensor_scalar(out=S2[:, cp, :], in0=fb, scalar1=mu, scalar2=None, op0=ALU.mult),
             reads=[ftk, ctk], writes=[ftk])
        P.op("dve", lambda e: e.tensor_copy(out=mrun, in_=fc_), reads=[ftk], writes=[ftk])
    mst = small[:, 4:5]
    P.op("dve", lambda e: e.tensor_tensor(out=small[:, 8:12], in0=mrun, in1=ident_f[:, 0:4], op=ALU.mult), reads=[ftk, ctk], writes=[rtk])
    P.op("dve", lambda e: e.tensor_reduce(out=mst, in_=small[:, 8:12], axis=AX.X, op=ALU.add), reads=[rtk], writes=[rtk])
    P.op("dve", lambda e: e.tensor_tensor_scan(out=GF, data0=LF, data1=GI, initial=mst, op0=ALU.add, op1=ALU.max),
         reads=[rtk, gtk], writes=[gtk])
    MM = GF
    P.op("dve", lambda e: e.tensor_tensor(out=BETAx[:, 1:NT + 1], in0=MM, in1=BB, op=ALU.subtract), reads=[gtk, rtk], writes=[rtk])
    P.op("dve", lambda e: e.tensor_copy(out=BETAx[:, 0:1], in_=mst), reads=[rtk], writes=[rtk])
    BETA = BETAx[:, 1:NT + 1]
    for ck in range(NCK):
        bl = small[:, 16:17]
        P.op("dve", lambda e, ck=ck: e.scalar_tensor_tensor(out=small[:, 16 + ck % 8:17 + ck % 8], in0=BETAx[:, 128 * (ck + 1):128 * (ck + 1) + 1],
                                                            scalar=-1.0, in1=cf[:, L_LNK:L_LNK + 1], op0=ALU.mult, op1=ALU.add),
             reads=[rtk, ctk], writes=[rtk])
        P.op("act", lambda e, ck=ck: e.activation(out=LF[:, ck * 128:(ck + 1) * 128], in_=ALPHA[:, ck * 128:(ck + 1) * 128],
                                                  func=AF.Exp, bias=small[:, 16 + ck % 8:17 + ck % 8]), reads=[rtk], writes=[rtk])
    to_token_major(LF, TMw)
    to_token_major(ALPHA, TMa)
    P.barrier()
    cx.release(pers_mark)
    BETA = BETAx[:, 1:NT + 1]

    qT = cx.sb(None, [128, 4, 512], BF16, "qT")
    kT = cx.sb(None, [128, 4, 512], BF16, "kT")
    zs = cx.sb(None, [128, 4, 512], BF16, "zs")
    yb = cx.sb(None, [128, 4, 512], BF16, "yb")
    qktk, zstk, ytk = Tk(), Tk(), Tk()
    Csts = [cx.sb(None, [128, 4, 512], F32, "Cst") for _ in range(2)]
    Caug = cx.sb(None, [128, 4, 640], BF16, "Caug")
    nrows = [cx.sb(None, [128, 512], F32, "nrow") for _ in range(2)]
    nm = cx.sb(None, [128, 512], F32, "nm")
    ctk2s = [Tk(), Tk()]
    caugtk = Tk()
    clst = Rot([topfree[:, 4096:6144], topfree[:, 6144:8176][:, 0:2032]])
    wk_rot = Rot([cx.sb(None, [128, 512], BF16, "wk") for _ in range(2)])
    va_rot = Rot([cx.sb(None, [128, 640], BF16, "vaug") for _ in range(2)])
    for (va, vatk) in va_rot.items:
        P.op("pool", lambda e, va=va: e.memset(va[:, 512:640], 1.0), writes=[vatk])
    dt_rot = Rot([cx.sb(None, [128, 128], F32, "dtmp") for _ in range(2)])
    sd_rot = Rot([cx.sb(None, [128, 128], BF16, "SdT") for _ in range(2)])
    qs_rot = Rot([cx.sb(None, [128, 4, 128], BF16, "qs") for _ in range(2)])
    hsq_rot = Rot([cx.sb(None, [128, 512], BF16, "hsq") for _ in range(2)])
    dd_rot = Rot([cx.sb(None, [128, 128], F32, "dd") for _ in range(2)])
    rr_rot = Rot([cx.sb(None, [128, 128], F32, "rr") for _ in range(2)])
    sc_rot = Rot([cx.sb(None, [128, 128], F32, "scsb") for _ in range(2)])
    em_rot = Rot([cx.sb(None, [128, 128], F32, "emsb") for _ in range(2)])
    ul_rot = Rot([cx.sb(None, [128, 1], F32, "ulast") for _ in range(2)])
    tt_rot = Rot([cx.sb(None, [128, 128], F32, "tt") for _ in range(3)])
    psB2 = psA
    psS3 = Rot([bk[2]])
    psRP = Rot([bk[2]])
    psH = Rot([bk[4], bk[5]])
    psDS = Rot([bk[6], bk[7]])
    psSS = Rot([bk[3]])
    wzv = wupv
    yview = yscr.rearrange("(c p) n -> p c n", p=128)
    def emit_fold(hd):
        Cst, nrow, ctk2 = Csts[hd % 2], nrows[hd % 2], ctk2s[hd % 2]
        P.op("pool", lambda e: e.memset(Cst, 0.0), writes=[ctk2])
        P.op("pool", lambda e: e.memset(nrow, 0.0), writes=[ctk2])
        Cflat = Cst.rearrange("p a b -> p (a b)")
        for cp in range(7):
            cl_, cltk = clst.items[0]
            P.dma("sync", cl_, st_all[cp][:, hd * 2048:(hd + 1) * 2048], writes=[cltk])
            P.op("act", lambda e, cp=cp, cl_=cl_: e.activation(out=cl_, in_=cl_, func=AF.Copy, scale=S2[:, cp, hd:hd + 1]),
                 reads=[cltk, ftk], writes=[cltk])
            P.op("dve", lambda e, cp=cp, cl_=cl_: e.scalar_tensor_tensor(out=Cflat, in0=Cflat, scalar=S1[:, cp, hd:hd + 1], in1=cl_,
                                                                          op0=ALU.mult, op1=ALU.add), reads=[cltk, ftk, ctk2], writes=[ctk2])
            nl_, nltk = clst.items[1]
            P.dma("sync", nl_[:, 0:512], st_all[cp][:, 8192 + hd * 512:8192 + (hd + 1) * 512], writes=[nltk])
            P.op("act", lambda e, cp=cp, nl_=nl_: e.activation(out=nl_[:, 0:512], in_=nl_[:, 0:512], func=AF.Copy, scale=S2[:, cp, hd:hd + 1]),
                 reads=[nltk, ftk], writes=[nltk])
            P.op("dve", lambda e, cp=cp, nl_=nl_: e.scalar_tensor_tensor(out=nrow, in0=nrow, scalar=S1[:, cp, hd:hd + 1], in1=nl_[:, 0:512],
                                                                          op0=ALU.mult, op1=ALU.add), reads=[nltk, ftk, ctk2], writes=[ctk2])

    ul_prev = None
    for hd in range(BH):
        wx3, wxtk = head_setup(hd)
        wz, wztk = ws.load([wzv[:, :, 2048 + hd * 512:2048 + (hd + 1) * 512]])
        wz3 = wz.rearrange("p (c n) -> p c n", c=NC8)
        Cst, nrow, ctk2 = Csts[hd % 2], nrows[hd % 2], ctk2s[hd % 2]
        if hd == 0:
            emit_fold(0)

        def refresh_caug(full):
            if full:
                for dc in range(4):
                    P.op("act", lambda e, dc=dc: e.activation(out=Caug[:, dc, 0:512], in_=Cst[:, dc, :], func=AF.Copy),
                         reads=[ctk2], writes=[caugtk])
            P.op("dve", lambda e: e.tensor_scalar(out=nm, in0=nrow, scalar1=cf[:, L_E0:L_E0 + 1], scalar2=None, op0=ALU.mult),
                 reads=[ctk2, ctk], writes=[caugtk])
            ps, pstk = psB2.next()
            for dc in range(4):
                P.op("pe", lambda e, ps=ps, dc=dc: e.matmul(ps[:, dc * 128:(dc + 1) * 128], lhsT=nm[:, dc * 128:(dc + 1) * 128], rhs=ones_f,
                                                            start=True, stop=True), reads=[caugtk, ctk], writes=[pstk], signal=(dc == 3))
            P.op("act", lambda e, ps=ps: e.activation(out=Caug[:, :, 512:640], in_=ps.rearrange("p (a b) -> p a b", a=4), func=AF.Copy),
                 reads=[pstk], writes=[caugtk])

        refresh_caug(True)
        for tg in range(4):
            xm, xmt = front(hd, tg, wx3, wxtk)
            for mc in range(4):
                ps, pstk = psA.next()
                for c in range(NC8):
                    P.op("pe", lambda e, c=c, mc=mc, ps=ps: e.matmul(
                        ps, lhsT=wz3[:, c, mc * 128:(mc + 1) * 128], rhs=hT[:, c, 4 + tg * 512:4 + (tg + 1) * 512],
                        start=(c == 0), stop=(c == NC8 - 1)), reads=[wztk], writes=[pstk], signal=(c == NC8 - 1))
                P.op("act", lambda e, ps=ps, mc=mc: e.activation(out=zs[:, mc, :], in_=ps, func=AF.Silu), reads=[pstk], writes=[zstk])
            for j, dst, sc_ in ((0, qT, 1.0), (1, kT, KSCALE)):
                for dc in range(4):
                    ps, pstk = psA.next()
                    P.op("pe", lambda e, ps=ps, j=j, dc=dc: e.matmul(ps, lhsT=bdh[:, j, dc, :], rhs=xc[:, dc, :], start=True, stop=True),
                         reads=[hdtk, xctk], writes=[pstk])
                    P.op("act", lambda e, ps=ps, dst=dst, dc=dc, sc_=sc_: e.activation(out=dst[:, dc, :], in_=ps, func=AF.Copy, scale=sc_),
                         reads=[pstk], writes=[qktk])
            RS = {}

            def stage_pre(cl):
                nonlocal ul_prev
                ck = tg * 4 + cl
                tsl = slice(cl * 128, (cl + 1) * 128)
                gsl = slice(ck * 128, (ck + 1) * 128)
                sel = cf[:, L_SEL + hd * 128:L_SEL + (hd + 1) * 128]
                rp, rptk = psRP.next()
                for i3, row in enumerate((BETA, MM)):
                    P.op("pe", lambda e, rp=rp, i3=i3, row=row, gsl=gsl: e.matmul(
                        rp[:, i3 * 128:(i3 + 1) * 128], lhsT=sel, rhs=row[:, gsl], start=True, stop=True),
                        reads=[rtk, gtk, ctk], writes=[rptk], signal=(i3 == 1))
                bprev = mrun[:, hd:hd + 1] if ck == 0 else ul_prev[0]
                bprev_tk = ftk if ck == 0 else ul_prev[1]
                scsb, sctk = sc_rot.next()
                P.op("act", lambda e, rp=rp, scsb=scsb, bprev=bprev: e.activation(out=scsb, in_=rp[:, 0:128], func=AF.Exp, scale=-1.0, bias=bprev),
                     reads=[rptk, bprev_tk], writes=[sctk])
                emsb, emtk = em_rot.next()
                P.op("act", lambda e, rp=rp, emsb=emsb: e.activation(out=emsb, in_=rp[:, 128:256], func=AF.Exp, scale=-1.0),
                     reads=[rptk], writes=[emtk])
                ul_prev = ul_rot.next()
                P.op("act", lambda e, rp=rp, ul_prev=ul_prev: e.activation(out=ul_prev[0], in_=rp[:, 127:128], func=AF.Copy),
                     reads=[rptk], writes=[ul_prev[1]])
                ps, pstk = psB2.next()
                for mc in range(4):
                    P.op("pe", lambda e, ps=ps, mc=mc, tsl=tsl: e.matmul(
                        ps[:, mc * 128:(mc + 1) * 128], lhsT=xc[:, mc, tsl], rhs=bdh[:, 1, mc, :], start=True, stop=True),
                        reads=[xctk, hdtk], writes=[pstk], signal=(mc == 3))
                wk, wktk = wk_rot.next()
                P.op("act", lambda e, ps=ps, wk=wk, ck=ck: e.activation(out=wk, in_=ps, func=AF.Copy, scale=TMw[:, ck, hd:hd + 1]),
                     reads=[pstk, tmtk], writes=[wktk])
                ps, pstk = psB2.next()
                for mc in range(4):
                    P.op("pe", lambda e, ps=ps, mc=mc, cl=cl, xm=xm: e.matmul(
                        ps[:, mc * 128:(mc + 1) * 128], lhsT=xm[:, mc, 4 + cl * 128:4 + (cl + 1) * 128], rhs=bdh[:, 2, mc, :],
                        start=True, stop=True), reads=[xmt, hdtk], writes=[pstk], signal=(mc == 3))
                va, vatk = va_rot.next()
                P.op("act", lambda e, ps=ps, va=va: e.activation(out=va[:, 0:512], in_=ps, func=AF.Copy), reads=[pstk], writes=[vatk])
                pS_, pStk = psS3.next()
                pS = pS_[:, 256:384]
                for dc in range(4):
                    P.op("pe", lambda e, pS=pS, dc=dc, tsl=tsl: e.matmul(pS, lhsT=kT[:, dc, tsl], rhs=qT[:, dc, tsl],
                                                                          start=(dc == 0), stop=(dc == 3)),
                         reads=[qktk], writes=[pStk], signal=(dc == 3))
                dtmp, dttk = dt_rot.next()
                P.op("dve", lambda e, rp=rp, dtmp=dtmp, ck=ck: e.scalar_tensor_tensor(
                    out=dtmp, in0=rp[:, 0:128], scalar=TMa[:, ck, hd:hd + 1], in1=cf[:, L_MASKLOW:L_MASKLOW + 128],
                    op0=ALU.subtract, op1=ALU.max), reads=[rptk, tmtk, ctk], writes=[dttk])
                P.op("act", lambda e, dtmp=dtmp: e.activation(out=dtmp, in_=dtmp, func=AF.Exp, scale=-1.0), reads=[dttk], writes=[dttk])
                sd, sdtk = sd_rot.next()
                P.op("dve", lambda e, pS=pS, dtmp=dtmp, sd=sd: e.tensor_tensor(out=sd, in0=pS, in1=dtmp, op=ALU.mult),
                     reads=[pStk, dttk], writes=[sdtk])
                qs, qstk = qs_rot.next()
                P.op("dve", lambda e, scsb=scsb, qs=qs, tsl=tsl: e.tensor_tensor(
                    out=qs, in0=qT[:, :, tsl], in1=scsb.unsqueeze(1).to_broadcast([128, 4, 128]), op=ALU.mult),
                    reads=[qktk, sctk], writes=[qstk])

                RS[cl] = dict(ck=ck, tsl=tsl, wk=wk, wktk=wktk, va=va, vatk=vatk, sd=sd, sdtk=sdtk, qs=qs, qstk=qstk,
                              scsb=scsb, sctk=sctk, emsb=emsb, emtk=emtk)

            def stage_mid(cl):
                r_ = RS[cl]
                ck, tsl, wk, wktk, va, vatk, sd, sdtk, qs, qstk, scsb, sctk = (r_[k_] for k_ in (
                    "ck", "tsl", "wk", "wktk", "va", "vatk", "sd", "sdtk", "qs", "qstk", "scsb", "sctk"))
                pH, pHtk = psH.next()
                pD_, pDtk = psDS.next()
                for ec in range(5):
                    o = pH[:, ec * 128:(ec + 1) * 128] if ec < 4 else pD_[:, 0:128]
                    otk = pHtk if ec < 4 else pDtk
                    for dc in range(4):
                        P.op("pe", lambda e, o=o, ec=ec, dc=dc, qs=qs: e.matmul(
                            o, lhsT=Caug[:, dc, ec * 128:(ec + 1) * 128], rhs=qs[:, dc, :], start=(dc == 0), stop=False),
                            reads=[caugtk, qstk], writes=[otk], signal=False)
                    P.op("pe", lambda e, o=o, ec=ec, va=va, sd=sd: e.matmul(
                        o, lhsT=va[:, ec * 128:(ec + 1) * 128], rhs=sd, start=False, stop=True),
                        reads=[vatk, sdtk], writes=[otk], signal=True)

                r_.update(pH=pH, pHtk=pHtk, pD_=pD_, pDtk=pDtk)
                if dbg_stop is not None and (hd, ck) == tuple(dbg_stop):
                    P.barrier()
                    P.finish()
                    return nc, cx
                decay = scsb[:, 127:128]
                for dc in range(4):
                    ps, pstk = psB2.next()
                    P.op("pe", lambda e, ps=ps, dc=dc, wk=wk, va=va: e.matmul(ps, lhsT=wk[:, dc * 128:(dc + 1) * 128], rhs=va[:, 0:512],
                                                                                start=True, stop=True), reads=[wktk, vatk], writes=[pstk])
                    P.op("dve", lambda e, ps=ps, dc=dc, decay=decay: e.scalar_tensor_tensor(
                        out=Cst[:, dc, :], in0=Cst[:, dc, :], scalar=decay, in1=ps, op0=ALU.mult, op1=ALU.add),
                        reads=[pstk, sctk, ctk2], writes=[ctk2])
                    P.op("act", lambda e, dc=dc: e.activation(out=Caug[:, dc, 0:512], in_=Cst[:, dc, :], func=AF.Copy),
                         reads=[ctk2], writes=[caugtk])
                ps, pstk = psB2.next()
                P.op("pe", lambda e, ps=ps, wk=wk: e.matmul(ps, lhsT=ones_bf, rhs=wk, start=True, stop=True),
                     reads=[wktk, ctk], writes=[pstk])
                P.op("dve", lambda e, ps=ps, decay=decay: e.scalar_tensor_tensor(out=nrow, in0=nrow, scalar=decay, in1=ps,
                                                                                  op0=ALU.mult, op1=ALU.add),
                     reads=[pstk, sctk, ctk2], writes=[ctk2])
                refresh_caug(False)

            def stage_post(cl):
                r_ = RS[cl]
                ck, tsl, emsb, emtk, pH, pHtk, pD_, pDtk = (r_[k_] for k_ in ("ck", "tsl", "emsb", "emtk", "pH", "pHtk", "pD_", "pDtk"))
                hsq, hsqtk = hsq_rot.next()
                P.op("act", lambda e, pH=pH, hsq=hsq: e.activation(out=hsq, in_=pH, func=AF.Square), reads=[pHtk], writes=[hsqtk])
                pSS_, pSStk = psSS.next()
                pSS = pSS_[:, 0:128]
                for ec in range(4):
                    P.op("pe", lambda e, pSS=pSS, hsq=hsq, ec=ec: e.matmul(pSS, lhsT=ones_bf, rhs=hsq[:, ec * 128:(ec + 1) * 128],
                                                                            start=(ec == 0), stop=(ec == 3)),
                         reads=[hsqtk, ctk], writes=[pSStk], signal=(ec == 3))
                dd, ddtk = dd_rot.next()
                P.op("dve", lambda e, pD_=pD_, dd=dd: e.tensor_scalar(out=dd, in0=pD_[:, 0:128], scalar1=-1.0, scalar2=None, op0=ALU.mult),
                     reads=[pDtk], writes=[ddtk])
                P.op("dve", lambda e, pD_=pD_, dd=dd: e.tensor_tensor(out=dd, in0=dd, in1=pD_[:, 0:128], op=ALU.max),
                     reads=[pDtk, ddtk], writes=[ddtk])
                P.op("dve", lambda e, emsb=emsb, dd=dd: e.tensor_tensor(out=dd, in0=dd, in1=emsb, op=ALU.max),
                     reads=[emtk, ddtk], writes=[ddtk])
                P.op("dve", lambda e, dd=dd: e.scalar_tensor_tensor(out=dd, in0=dd, scalar=EPS, in1=dd, op0=ALU.mult, op1=ALU.mult),
                     reads=[ddtk], writes=[ddtk])
                rr, rrtk = rr_rot.next()
                P.op("dve", lambda e, pSS=pSS, dd=dd, rr=rr: e.scalar_tensor_tensor(out=rr, in0=pSS, scalar=1.0 / DH, in1=dd,
                                                                                     op0=ALU.mult, op1=ALU.add),
                     reads=[pSStk, ddtk], writes=[rrtk])
                P.op("act", lambda e, rr=rr: e.activation(out=rr, in_=rr, func=AF.Sqrt), reads=[rrtk], writes=[rrtk])
                P.op("dve", lambda e, rr=rr: e.reciprocal(out=rr, in_=rr), reads=[rrtk], writes=[rrtk])
                for ec in range(4):
                    ch = hd * 4 + ec
                    tt, tttk = tt_rot.next()
                    P.op("dve", lambda e, pH=pH, ec=ec, ch=ch, rr=rr, tt=tt: e.scalar_tensor_tensor(
                        out=tt, in0=pH[:, ec * 128:(ec + 1) * 128], scalar=cf[:, L_HGAIN + ch:L_HGAIN + ch + 1], in1=rr,
                        op0=ALU.mult, op1=ALU.mult), reads=[pHtk, rrtk, ctk], writes=[tttk])
                    P.op("dve", lambda e, ec=ec, ch=ch, tt=tt, tsl=tsl: e.scalar_tensor_tensor(
                        out=tt, in0=xc[:, ec, tsl], scalar=cf[:, L_SKIP + ch:L_SKIP + ch + 1], in1=tt,
                        op0=ALU.mult, op1=ALU.add), reads=[xctk, tttk, ctk], writes=[tttk])
                    P.op("dve", lambda e, ec=ec, tt=tt, tsl=tsl: e.tensor_tensor(out=yb[:, ec, tsl], in0=tt, in1=zs[:, ec, tsl], op=ALU.mult),
                         reads=[tttk, zstk], writes=[ytk])


            stage_pre(0)
            stage_mid(0)
            for cl in range(1, 4):
                stage_pre(cl)
                stage_post(cl - 1)
                stage_mid(cl)
            stage_post(3)

            if tg == 1 and hd + 1 < BH:
                emit_fold(hd + 1)
            P.dma("sync", yview[:, hd * 4:(hd + 1) * 4, tg * 512:(tg + 1) * 512], yb, reads=[ytk])
    P.barrier()
    cx.release(base_mark)

    X = TOP.rearrange("p (c n) -> p c n", c=NC8)
    Xtk = [[Tk() for _ in range(4)] for _ in range(NC8)]
    for c in range(NC8):
        for tg in range(4):
            P.dma("sync", X[:, c, tg * 512:(tg + 1) * 512], x1T[c * 128:(c + 1) * 128, tg * 512:(tg + 1) * 512], writes=[Xtk[c][tg]])
    mk = cx.mark()
    ws3 = WStream(cx, None, 4096, nstage=2, nslot=2)
    wdn = cx.sb(None, [128, 16, D], BF16, "wdn")
    wdtk = Tk()
    wdv = wdown.rearrange("(c p) n -> p c n", p=128)
    for q4 in range(4):
        wb_, wbtk = ws3.load([wdv[:, q4 * 4:(q4 + 1) * 4, :]])
        P.op("pool", lambda e, wb_=wb_, q4=q4: e.tensor_copy(out=wdn[:, q4 * 4:(q4 + 1) * 4, :], in_=wb_.rearrange("p (a b) -> p a b", a=4)),
             reads=[wbtk], writes=[wdtk])
    yts = [cx.sb(None, [128, 16, 512], BF16, "yt") for _ in range(2)]
    yttk = [Tk(), Tk()]
    psA4 = Rot(bk[0:4])
    for tg in range(4):
        yt, ytt = yts[tg % 2], yttk[tg % 2]
        P.dma("sync", yt, yview[:, :, tg * 512:(tg + 1) * 512], writes=[ytt])
        sl = slice(tg * 512, (tg + 1) * 512)
        for oc in range(NC8):
            ps, pstk = psA4.next()
            for mc in range(16):
                P.op("pe", lambda e, ps=ps, mc=mc, oc=oc, yt=yt: e.matmul(ps, lhsT=wdn[:, mc, oc * 128:(oc + 1) * 128], rhs=yt[:, mc, :],
                                                                           start=(mc == 0), stop=(mc == 15)),
                     reads=[wdtk, ytt], writes=[pstk], signal=(mc == 15))
            P.op("dve", lambda e, ps=ps, oc=oc, sl=sl: e.tensor_tensor(out=X[:, oc, sl], in0=ps, in1=X[:, oc, sl], op=ALU.add),
                 reads=[pstk, Xtk[oc][tg]], writes=[Xtk[oc][tg]])
    P.barrier()
    cx.release(mk)
    if debug:
        emit_store(cx, X, Xtk, dbg_a)
    emit_mlp(cx, X, Xtk, cf[:, L_MLPN:L_MLPN + 8], ctk, w1, w2, ones_bf)
    emit_ple(cx, X, Xtk, cf[:, L_PLEN:L_PLEN + 8], ctk, wg, wp, pT, ones_bf)
    emit_store(cx, X, Xtk, out)
    P.finish()
    return nc, cx


_CACHE = {}


def _prog(key, builder):
    return builder()


def kernel(**inputs):
    inputs = {k: np.asarray(v) for k, v in inputs.items()}
    cores = list(range(NCORES))
    nc, _ = build_layer0()
    in_maps = [layer0_inputs(inputs, c) for c in cores]
    res = run_bass_kernel_spmd(nc, in_maps, core_ids=cores)
    x1T = np.concatenate([r["xout"] for r in res.results], axis=1)
    nc, _ = build_layer1("B")
    in_maps = [layer1_inputs(inputs, c, x1T, "B") for c in cores]
    res = run_bass_kernel_spmd(nc, in_maps, core_ids=cores)
    st_all = np.stack([res.results[c]["st_out"] for c in range(7)])
    g_rows = [res.results[c]["g_out"] for c in cores]
    nc, _ = build_layer1("C")
    in_maps = [layer1_inputs(inputs, c, x1T, "C", st_all, g_rows[c]) for c in cores]
    res = run_bass_kernel_spmd(nc, in_maps, core_ids=cores)
    outT = np.concatenate([r["xout"] for r in res.results], axis=1)
    return np.ascontiguousarray(outT.T)[None].astype(np.float32)
```

```python
import numpy as np
import concourse.bass as bass
import concourse.mybir as mybir
from concourse.bass_utils import run_bass_kernel_spmd

F32 = mybir.dt.float32
BF16 = mybir.dt.bfloat16
AF = mybir.ActivationFunctionType
ALU = mybir.AluOpType
AX = mybir.AxisListType

NCORES = 8
S = 16384
D = 1024
NT = S // NCORES
NC8 = D // 128
EPS = 1e-6
BIG = 30000.0
A_GROUPS = ((128, 1), (512, 4), (2048, 16))
NDMA = 24
SB_F32 = 51968


class Tk:
    __slots__ = ("w", "r")

    def __init__(self):
        self.w = {}
        self.r = {}


class Prog:
    def __init__(self, nc):
        self.nc = nc
        self.eng = {"act": nc.scalar, "dve": nc.vector, "pool": nc.gpsimd, "pe": nc.tensor, "sync": nc.sync}
        self.sem = {e: nc.alloc_semaphore("s_" + e) for e in ("act", "dve", "pool", "pe")}
        self.cnt = {e: 0 for e in ("act", "dve", "pool", "pe")}
        self.seen = {e: {} for e in self.eng}
        self.dsem = [nc.alloc_semaphore("s_dma%d" % i) for i in range(NDMA)]
        self.dcnt = [0] * NDMA
        self.dnext = 0
        self.nins = {e: 0 for e in self.eng}

    def _semof(self, src):
        if isinstance(src, tuple):
            return self.dsem[src[1]]
        return self.sem[src]

    def _deps(self, e, reads, writes, allraw=False):
        deps = {}

        def add(src, n, raw):
            if src == e and not allraw:
                if e == "pe" or not raw:
                    return
            if deps.get(src, 0) < n:
                deps[src] = n

        for t in reads:
            for src, n in t.w.items():
                add(src, n, True)
        for t in writes:
            for src, n in t.w.items():
                add(src, n, False)
            for src, n in t.r.items():
                add(src, n, False)
        return deps

    def _wait(self, e, deps):
        eng = self.eng[e]
        seen = self.seen[e]
        for src, n in deps.items():
            if seen.get(src, 0) >= n:
                continue
            seen[src] = n
            eng.wait_ge(self._semof(src), n)
            self.nins[e] += 1

    def op(self, e, fn, reads=(), writes=(), signal=True):
        self._wait(e, self._deps(e, reads, writes))
        ins = fn(self.eng[e])
        self.nins[e] += 1
        n = self.cnt[e] + 1
        if signal:
            ins.then_inc(self.sem[e], 1)
            self.cnt[e] = n
        for t in reads:
            if t.r.get(e, 0) < n:
                t.r[e] = n
        for t in writes:
            if t.w.get(e, 0) < n:
                t.w[e] = n
        return ins

    def dma(self, q, out, in_, reads=(), writes=()):
        k = self.dnext
        self.dnext = (k + 1) % NDMA
        src = ("dma", k)
        deps = self._deps(q, reads, writes, allraw=True)
        if self.dcnt[k] > 0:
            deps[src] = max(deps.get(src, 0), self.dcnt[k])
        self._wait(q, deps)
        ins = self.eng[q].dma_start(out=out, in_=in_)
        self.nins[q] += 1
        n = self.dcnt[k] + 16
        ins.then_inc(self.dsem[k], 16)
        self.dcnt[k] = n
        for t in reads:
            t.r[src] = n
        for t in writes:
            t.w[src] = n

    def barrier(self):
        for e in self.eng:
            deps = {}
            for s2 in self.cnt:
                if s2 != e and self.cnt[s2] > 0:
                    deps[s2] = self.cnt[s2]
            for k in range(NDMA):
                if self.dcnt[k] > 0:
                    deps[("dma", k)] = self.dcnt[k]
            self._wait(e, deps)

    def finish(self):
        deps = {}
        for k in range(NDMA):
            if self.dcnt[k] > 0:
                deps[("dma", k)] = self.dcnt[k]
        self._wait("sync", deps)


class Rot:
    def __init__(self, aps):
        self.items = [a if isinstance(a, tuple) else (a, Tk()) for a in aps]
        self.i = 0

    def next(self):
        it = self.items[self.i]
        self.i = (self.i + 1) % len(self.items)
        return it


class Ctx:
    def __init__(self, nc):
        self.nc = nc
        self.P = Prog(nc)
        self.banks = [nc.alloc_psum_tensor("psb%d" % i, [128, 512], F32).ap() for i in range(8)]
        self.nalloc = 0

        self.big = nc.alloc_sbuf_tensor("big", [128, SB_F32], F32).ap()
        self.top = 0

    def sb(self, stack, shape, dt, name=None):
        esz = 2 if dt == BF16 else 4
        n = int(np.prod(shape[1:]))
        nbytes = (n * esz + 63) // 64 * 64
        off = self.top
        assert off + nbytes <= SB_F32 * 4, ("SBUF overflow", name, off, nbytes)
        self.top = off + nbytes
        self.log = getattr(self, 'log', [])
        self.log.append((name, off, nbytes))
        ap = self.big[:, off // 4:(off + nbytes) // 4]
        if dt == BF16:
            ap = ap.bitcast(BF16)
        ap = ap[:, 0:n]
        if len(shape) == 3:
            ap = ap.rearrange("p (a b) -> p a b", a=shape[1])
        elif len(shape) == 4:
            ap = ap.rearrange("p (a b c) -> p a b c", a=shape[1], b=shape[2])
        return ap

    def mark(self):
        return self.top

    def release(self, m):
        self.top = m


def load_consts(cx, stack, cst_ap, ncols):
    P = cx.P
    cf = cx.sb(stack, [128, ncols], F32, "cstf")
    cb = cx.sb(stack, [128, C_END_BF], BF16, "cstb")
    tk = Tk()
    P.dma("sync", cf, cst_ap, writes=[tk])
    P.op("dve", lambda e: e.tensor_copy(out=cb, in_=cf[:, 0:C_END_BF]), reads=[tk], writes=[tk])
    return cf, cb, tk


class WStream:
    def __init__(self, cx, stack, nelem, nstage=2, nslot=2):
        self.cx = cx
        self.nelem = nelem
        self.stage = Rot([cx.sb(stack, [128, nelem], F32, "wstg") for _ in range(nstage)])
        self.slots = Rot([cx.sb(stack, [128, nelem], BF16, "wbf") for _ in range(nslot)])

    def load(self, views):
        P = self.cx.P
        stg, stk = self.stage.next()
        wb, wtk = self.slots.next()
        off = 0
        for v in views:
            shp = v.shape
            n = int(np.prod(shp[1:]))
            dst = stg[:, off:off + n]
            if len(shp) == 3:
                dst = dst.rearrange("p (a b) -> p a b", a=shp[1])
            P.dma("sync", dst, v, writes=[stk])
            off += n
        assert off <= self.nelem
        P.op("pool", lambda e: e.tensor_copy(out=wb[:, 0:off], in_=stg[:, 0:off]), reads=[stk], writes=[wtk])
        return wb, wtk


def rms_stats(cx, xs, n, sq_rot, ps_ap, ps_tk, rstd, rstd_tk, ones_bf, ctk, inv_dim):
    P = cx.P
    nx = len(xs)
    for c, (xa, xt) in enumerate(xs):
        sq, sqt = sq_rot.next()
        P.op("act", lambda e, xa=xa, sq=sq: e.activation(out=sq[:, 0:n], in_=xa, func=AF.Square), reads=[xt], writes=[sqt])
        P.op("pe", lambda e, sq=sq, c=c: e.matmul(ps_ap[:, 0:n], lhsT=ones_bf, rhs=sq[:, 0:n], start=(c == 0), stop=(c == nx - 1)),
             reads=[sqt, ctk], writes=[ps_tk])
    P.op("act", lambda e: e.activation(out=rstd[:, 0:n], in_=ps_ap[:, 0:n], func=AF.Sqrt, bias=cx.eps_col, scale=inv_dim),
         reads=[ps_tk, ctk], writes=[rstd_tk])
    P.op("dve", lambda e: e.reciprocal(out=rstd[:, 0:n], in_=rstd[:, 0:n]), reads=[rstd_tk], writes=[rstd_tk])


C_ID, C_ONES, C_BONES, C_DM, C_HONES = 0, 128, 256, 384, 640
C_OZ = 704
C_HZ = 960
C_END_BF = 1216
C_EPS = 1216
C_GAINS = 1217
G0_ANORM = C_GAINS
G0_QG = G0_ANORM + 8
G0_KG = G0_QG + 3
G0_MLPN = G0_KG + 3
G0_PLEN = G0_MLPN + 8
G0_END = G0_PLEN + 8


def base_consts(core, ncols):
    c = np.zeros((128, ncols), np.float32)
    c[:, C_ID:C_ID + 128] = np.eye(128, dtype=np.float32)
    c[:, C_ONES:C_ONES + 128] = 1.0
    c[0:64, C_BONES:C_BONES + 64] = 1.0
    c[64:128, C_BONES + 64:C_BONES + 128] = 1.0
    kk = np.arange(128)[:, None]
    a = np.arange(128)[None, :]
    diag = np.where(kk <= a, a - kk, BIG)
    prev = np.where(kk >= a, 128 + a - kk, BIG)
    c[:, C_DM:C_DM + 128] = diag
    c[:, C_DM + 128:C_DM + 256] = prev
    hv = 0.0 if core == 0 else 1.0
    c[:, C_HONES:C_HONES + 64] = hv
    c[:, C_OZ:C_OZ + 64] = 1.0
    c[:, C_OZ + 128 + 64:C_OZ + 256] = 1.0
    c[:, C_HZ:C_HZ + 64] = hv
    c[:, C_HZ + 128 + 64:C_HZ + 256] = hv
    c[:, C_EPS] = EPS
    return c


def col_layout(v):
    v = np.asarray(v, np.float32).reshape(-1, 128)
    return np.ascontiguousarray(v.T)


def emit_norm_resident(cx, X, Xtk, gcol, ctk, hT, hTtk, sq_rot, rstd_rot, ps_rot, ones_bf):
    P = cx.P
    for tg in range(NT // 512):
        sl = slice(tg * 512, (tg + 1) * 512)
        xs = [(X[:, c, sl], Xtk[c][tg]) for c in range(NC8)]
        ps_ap, ps_tk = ps_rot.next()
        rstd, rtk = rstd_rot.next()
        rms_stats(cx, xs, 512, sq_rot, ps_ap, ps_tk, rstd, rtk, ones_bf, ctk, 1.0 / D)
        for c in range(NC8):
            P.op("dve", lambda e, c=c, sl=sl, rstd=rstd: e.scalar_tensor_tensor(
                out=hT[:, c, sl], in0=X[:, c, sl], scalar=gcol[:, c:c + 1], in1=rstd[:, 0:512],
                op0=ALU.mult, op1=ALU.mult), reads=[Xtk[c][tg], rtk, ctk], writes=[hTtk[c][tg]])


def emit_mlp(cx, X, Xtk, gcol, ctk, w1, w2, ones_bf):
    P = cx.P
    mk = cx.mark()
    st = None
    hT = cx.sb(st, [128, NC8, NT], BF16, "mlp_hT")
    hTtk = [[Tk() for _ in range(4)] for _ in range(NC8)]
    sq_rot = Rot([cx.sb(st, [128, 512], BF16, "sq") for _ in range(4)])
    rstd_rot = Rot([cx.sb(st, [128, 512], F32, "rstd") for _ in range(2)])
    ps_stat = Rot([cx.banks[7]])
    emit_norm_resident(cx, X, Xtk, gcol, ctk, hT, hTtk, sq_rot, rstd_rot, ps_stat, ones_bf)
    ws = WStream(cx, st, 4096, nstage=2, nslot=2)
    hids = [cx.sb(st, [128, 4, NT], BF16, "hid") for _ in range(2)]
    hid_tks = [[[Tk() for _ in range(4)] for _ in range(4)] for _ in range(2)]
    tmp_rot = Rot([cx.sb(st, [128, 512], F32, "rl") for _ in range(3)])
    psA = Rot(cx.banks[0:4])
    psB = Rot(cx.banks[4:7])
    w1v = w1.rearrange("(c p) n -> p c n", p=128)
    w2v = w2.rearrange("(c p) n -> p c n", p=128)
    NHB = 8
    for hb in range(NHB):
        hid = hids[hb % 2]
        htk = hid_tks[hb % 2]
        wa, watk = ws.load([w1v[:, :, hb * 512:(hb + 1) * 512]])
        wa3 = wa.rearrange("p (c n) -> p c n", c=NC8)
        for hc in range(4):
            for tg in range(4):
                sl = slice(tg * 512, (tg + 1) * 512)
                ps, pstk = psA.next()
                for c in range(NC8):
                    P.op("pe", lambda e, c=c, hc=hc, sl=sl, ps=ps, wa3=wa3: e.matmul(
                        ps, lhsT=wa3[:, c, hc * 128:(hc + 1) * 128], rhs=hT[:, c, sl],
                        start=(c == 0), stop=(c == NC8 - 1)),
                        reads=[watk, hTtk[c][tg]], writes=[pstk], signal=(c == NC8 - 1))
                tmp, ttk = tmp_rot.next()
                P.op("act", lambda e, ps=ps, tmp=tmp: e.activation(out=tmp, in_=ps, func=AF.Square),
                     reads=[pstk], writes=[ttk])
                P.op("dve", lambda e, ps=ps, tmp=tmp, hc=hc, sl=sl, hid=hid: e.scalar_tensor_tensor(
                    out=hid[:, hc, sl], in0=ps, scalar=0.0, in1=tmp, op0=ALU.is_gt, op1=ALU.mult),
                    reads=[pstk, ttk], writes=[htk[hc][tg]])
        wb, wbtk = ws.load([w2v[:, hb * 4:(hb + 1) * 4, :]])
        wb3 = wb.rearrange("p (c n) -> p c n", c=4)
        for oc in range(NC8):
            for tg in range(4):
                sl = slice(tg * 512, (tg + 1) * 512)
                ps, pstk = psB.next()
                for hc in range(4):
                    P.op("pe", lambda e, hc=hc, oc=oc, sl=sl, ps=ps, hid=hid, wb3=wb3: e.matmul(
                        ps, lhsT=wb3[:, hc, oc * 128:(oc + 1) * 128], rhs=hid[:, hc, sl],
                        start=(hc == 0), stop=(hc == 3)),
                        reads=[wbtk, htk[hc][tg]], writes=[pstk], signal=(hc == 3))
                P.op("dve", lambda e, oc=oc, sl=sl, ps=ps: e.tensor_tensor(
                    out=X[:, oc, sl], in0=ps, in1=X[:, oc, sl], op=ALU.add),
                    reads=[pstk, Xtk[oc][tg]], writes=[Xtk[oc][tg]])
    P.barrier()
    cx.release(mk)


def emit_ple(cx, X, Xtk, gcol, ctk, wg, wp, pT_dram, ones_bf):
    P = cx.P
    mk = cx.mark()
    st = None
    hT = cx.sb(st, [128, NC8, NT], BF16, "ple_hT")
    hTtk = [[Tk() for _ in range(4)] for _ in range(NC8)]
    sq_rot = Rot([cx.sb(st, [128, 512], BF16, "sq") for _ in range(4)])
    rstd_rot = Rot([cx.sb(st, [128, 512], F32, "rstd") for _ in range(2)])
    ps_stat = Rot([cx.banks[7]])
    emit_norm_resident(cx, X, Xtk, gcol, ctk, hT, hTtk, sq_rot, rstd_rot, ps_stat, ones_bf)
    ws = WStream(cx, st, 4096, nstage=2, nslot=3)
    pst = cx.sb(st, [128, 2, NT], F32, "pstg")
    pb = cx.sb(st, [128, 2, NT], BF16, "pbf")
    ptk = Tk()
    P.dma("sync", pst, pT_dram.rearrange("(c p) n -> p c n", p=128), writes=[ptk])
    P.op("pool", lambda e: e.tensor_copy(out=pb, in_=pst), reads=[ptk], writes=[ptk])
    wpb, wptk = ws.load([wp.rearrange("(c p) n -> p c n", p=128)])
    wp3 = wpb[:, 0:2048].rearrange("p (c n) -> p c n", c=2)
    gt_rot = Rot([cx.sb(st, [128, 512], F32, "gt") for _ in range(3)])
    psA = Rot(cx.banks[0:3])
    psB = Rot(cx.banks[3:6])
    wgv = wg.rearrange("(c p) n -> p c n", p=128)
    for half in range(2):
        wa, watk = ws.load([wgv[:, :, half * 512:(half + 1) * 512]])
        wa3 = wa.rearrange("p (c n) -> p c n", c=NC8)
        for o4 in range(4):
            oc = half * 4 + o4
            for tg in range(4):
                sl = slice(tg * 512, (tg + 1) * 512)
                ps, pstk = psA.next()
                for c in range(NC8):
                    P.op("pe", lambda e, c=c, o4=o4, sl=sl, ps=ps, wa3=wa3: e.matmul(
                        ps, lhsT=wa3[:, c, o4 * 128:(o4 + 1) * 128], rhs=hT[:, c, sl],
                        start=(c == 0), stop=(c == NC8 - 1)),
                        reads=[watk, hTtk[c][tg]], writes=[pstk], signal=(c == NC8 - 1))
                ps2, ps2tk = psB.next()
                for kc in range(2):
                    P.op("pe", lambda e, kc=kc, oc=oc, sl=sl, ps2=ps2: e.matmul(
                        ps2, lhsT=wp3[:, kc, oc * 128:(oc + 1) * 128], rhs=pb[:, kc, sl],
                        start=(kc == 0), stop=(kc == 1)),
                        reads=[wptk, ptk], writes=[ps2tk], signal=(kc == 1))
                gt, gtk = gt_rot.next()
                P.op("act", lambda e, ps=ps, gt=gt: e.activation(out=gt, in_=ps, func=AF.Sigmoid),
                     reads=[pstk], writes=[gtk])
                P.op("dve", lambda e, ps2=ps2, gt=gt: e.tensor_tensor(out=gt, in0=ps2, in1=gt, op=ALU.mult),
                     reads=[ps2tk, gtk], writes=[gtk])
                P.op("dve", lambda e, oc=oc, sl=sl, gt=gt: e.tensor_tensor(
                    out=X[:, oc, sl], in0=gt, in1=X[:, oc, sl], op=ALU.add),
                    reads=[gtk, Xtk[oc][tg]], writes=[Xtk[oc][tg]])
    P.barrier()
    cx.release(mk)


def alibi_slope(h):
    return 2.0 ** (-8.0 * (h + 1) / 16)


def sslice(start, count, step):
    return slice(start, start + (count - 1) * step + 1, step)


def emit_attention(cx, xT_ext, wqkv, wo, cf, cb, ctk, TOP, R1, lvl=9, hps=8):
    P = cx.P
    st = None
    ones_bf = cb[:, C_ONES:C_ONES + 128]
    bones = cb[:, C_BONES:C_BONES + 128]
    Dm = cf[:, C_DM:C_DM + 256]
    hT = TOP.bitcast(BF16).rearrange("p (c n) -> p c n", c=NC8)
    mk = cx.mark()
    sq_rot = Rot([cx.sb(st, [128, 512], BF16, "sq") for _ in range(2)])
    rstd_rot = Rot([cx.sb(st, [128, 512], F32, "rstd") for _ in range(2)])
    xstg = [R1[:, i * 4096:(i + 1) * 4096].rearrange("p (c n) -> p c n", c=NC8) for i in range(2)]
    xstk = [Tk(), Tk()]
    ps_stat = Rot([cx.banks[7], cx.banks[6]])
    httk = Tk()
    gcol = cf[:, G0_ANORM:G0_ANORM + 8]
    for tg in range(8 if lvl >= 1 else 0):
        xa = xstg[tg % 2]
        xt = xstk[tg % 2]
        P.dma("sync", xa, xT_ext[:, tg * 512:(tg + 1) * 512].rearrange("(c p) n -> p c n", p=128), writes=[xt])
        xs = [(xa[:, c, :], xt) for c in range(NC8)]
        ps_ap, ps_tk = ps_stat.next()
        rstd, rtk = rstd_rot.next()
        rms_stats(cx, xs, 512, sq_rot, ps_ap, ps_tk, rstd, rtk, ones_bf, ctk, 1.0 / D)
        for c in range(NC8):
            P.op("dve", lambda e, c=c, tg=tg, xa=xa, rstd=rstd: e.scalar_tensor_tensor(
                out=hT[:, c, tg * 512:(tg + 1) * 512], in0=xa[:, c, :], scalar=gcol[:, c:c + 1], in1=rstd[:, 0:512],
                op0=ALU.mult, op1=ALU.mult), reads=[xt, rtk, ctk], writes=[httk])
    P.op("dve", lambda e: e.tensor_scalar(out=cf[:, G0_QG:G0_QG + 3], in0=cf[:, G0_QG:G0_QG + 3], scalar1=0.125,
                                          scalar2=None, op0=ALU.mult), reads=[ctk], writes=[ctk])
    P.barrier()
    ACC = R1[:, 0:4096].rearrange("p (a n) -> p a n", a=2)
    oT = R1[:, 4096:12288].bitcast(BF16).rearrange("p (c n) -> p c n", c=NC8)
    acctk = Tk()
    ottk = Tk()
    ws = WStream(cx, st, 3072, nstage=1, nslot=2)
    QTz = [cx.sb(st, [128, NT], BF16, "QTz") for _ in range(2)]
    qtk = Tk()
    KT_rot = Rot([cx.sb(st, [128, 2 * NT], BF16, "KT") for _ in range(2)])
    Vz = [cx.sb(st, [128, 32, 128], BF16, "Vz") for _ in range(2)]
    vtk = Tk()
    for e2 in (0, 1):
        P.op("pool", lambda e, e2=e2: e.memset(QTz[e2], 0.0), writes=[qtk])
        P.op("pool", lambda e, e2=e2: e.memset(Vz[e2], 0.0), writes=[vtk])
    onesz = [cb[:, C_OZ:C_OZ + 128], cb[:, C_OZ + 128:C_OZ + 256]]
    honesz = [cb[:, C_HZ:C_HZ + 128], cb[:, C_HZ + 128:C_HZ + 256]]
    tmp_rot = Rot([cx.sb(st, [128, 256], F32, "stmp") for _ in range(4)])
    pt_rots = [Rot([cx.sb(st, [128, 256], BF16, "PT") for _ in range(6)]) for _ in range(2)]
    bk = [(cx.banks[i], Tk()) for i in range(8)]

    def half(i):
        return (bk[i][0][:, 0:256], bk[i][1])

    psQ = Rot(bk[0:2])
    psS = Rot([bk[2]])
    psV = Rot([bk[3]])
    psST0 = Rot([half(4), half(0)])
    psST1 = Rot([half(5), half(1)])
    psND = Rot([half(6), half(7), half(2), half(3)])
    wq_view = wqkv.rearrange("(c p) n -> p c n", p=128)

    def perm(ap2d, d):
        if d == 1:
            return ap2d
        return ap2d.rearrange("p (u r) -> p r u", r=d)

    def proj_piece(w3, wtk, j, e0, n, gain_col, out_buf, out_tk, d, Lx, u0):
        ps, pstk = psQ.next()
        for c in range(NC8):
            P.op("pe", lambda e, c=c, ps=ps: e.matmul(ps[:, 0:n], lhsT=w3[:, j, c, :], rhs=hT[:, c, e0:e0 + n],
                                                      start=(c == 0), stop=(c == NC8 - 1)),
                 reads=[wtk], writes=[pstk], signal=(c == NC8 - 1))
        ps2, ps2tk = psS.next()
        rstd, rtk = rstd_rot.next()
        rms_stats(cx, [(ps[:, 0:n], pstk)], n, sq_rot, ps2, ps2tk, rstd, rtk, bones, ctk, 1.0 / 64)
        outs = out_buf if isinstance(out_buf, list) else [(slice(0, 128), out_buf)]
        for (rows, ob) in outs:
            if d == 1:
                o = ob[rows, u0:u0 + n]
            else:
                o = ob[rows, 0:d * Lx].rearrange("p (r u) -> p r u", r=d)[:, :, u0:u0 + n // d]
            P.op("dve", lambda e, ps=ps, rstd=rstd, o=o, rows=rows: e.scalar_tensor_tensor(
                out=o, in0=perm(ps[rows, 0:n], d), scalar=gain_col[rows, :], in1=perm(rstd[rows, 0:n], d),
                op0=ALU.mult, op1=ALU.mult),
                reads=[pstk, rtk, ctk], writes=[out_tk])

    if lvl < 2:
        hps = 0
        P.op('dve', lambda e: e.memset(R1, 0.0), writes=[ottk])
    for hp in range(hps):
        for g, (W, d) in enumerate(A_GROUPS):
            L = NT // d
            Lk = (W + NT) // d
            nb = Lk // 128
            e_start = NT - W
            base = g * 3072 + hp * 128
            wb, wtk = ws.load([wq_view[:, :, base + j * 1024: base + j * 1024 + 128] for j in range(3)])
            w3 = wb[:, 0:3072].rearrange("p (j c n) -> p j c n", j=3, c=NC8)
            KT, ktk = KT_rot.next()
            for tg in range(4):
                proj_piece(w3, wtk, 0, NT + tg * 512, 512, cf[:, G0_QG + g:G0_QG + g + 1],
                           [(slice(0, 64), QTz[0]), (slice(64, 128), QTz[1])], qtk, d, L, tg * 512 // d)
            pieces = []
            if W < 512:
                pieces.append((e_start, W))
                e = NT
            else:
                e = e_start
            while e < 2 * NT:
                pieces.append((e, 512))
                e += 512
            for (e0, n) in pieces:
                proj_piece(w3, wtk, 1, e0, n, cf[:, G0_KG + g:G0_KG + g + 1], KT, ktk, d, Lk, (e0 - e_start) // d)
            nkb = d * nb if lvl >= 3 else 0
            kb = 0
            while kb < nkb:
                nblk = min(4, nkb - kb)
                psv, psvtk = psV.next()
                for b in range(nblk):
                    r, jb = divmod(kb + b, nb)
                    e_first = e_start + d * 128 * jb + r
                    for c in range(NC8):
                        P.op("pe", lambda e, c=c, b=b, e_first=e_first, psv=psv: e.matmul(
                            psv[:, b * 128:(b + 1) * 128], lhsT=hT[:, c, sslice(e_first, 128, d)], rhs=w3[:, 2, c, :],
                            start=(c == 0), stop=(c == NC8 - 1)),
                            reads=[wtk], writes=[psvtk], signal=(c == NC8 - 1 and b == nblk - 1))
                for e2 in (0, 1):
                    cs = slice(64 * e2, 64 * e2 + 64)
                    P.op("act", lambda e, kb=kb, nblk=nblk, psv=psv, e2=e2, cs=cs: e.activation(
                        out=Vz[e2][:, kb:kb + nblk, cs],
                        in_=psv[:, 0:nblk * 128].rearrange("p (b n) -> p b n", b=nblk)[:, :, cs], func=AF.Copy),
                        reads=[psvtk], writes=[vtk])
                kb += nblk
            PTs = {}

            def score_task(r, jb):
                lo = 128 if jb == 0 else 0
                hi = 128 if jb == nb - 1 else 256
                qb0 = jb if jb == 0 else jb - 1
                q_off = r * L + 128 * qb0
                sTs = [psST0.next(), psST1.next()]
                for e2 in (0, 1):
                    sT, sTtk = sTs[e2]
                    P.op("pe", lambda e, sT=sT, e2=e2: e.matmul(
                        sT[:, lo:hi], lhsT=KT[:, r * Lk + 128 * jb: r * Lk + 128 * jb + 128],
                        rhs=QTz[e2][:, q_off:q_off + (hi - lo)], start=True, stop=True),
                        reads=[ktk, qtk], writes=[sTtk])
                for e2 in (0, 1):
                    sig = alibi_slope(2 * hp + e2) * d
                    sT, sTtk = sTs[e2]
                    tmp, tmtk = tmp_rot.next()
                    P.op("dve", lambda e, sT=sT, tmp=tmp, sig=sig: e.scalar_tensor_tensor(
                        out=tmp[:, lo:hi], in0=Dm[:, lo:hi], scalar=-sig, in1=sT[:, lo:hi],
                        op0=ALU.mult, op1=ALU.add),
                        reads=[sTtk, ctk], writes=[tmtk])
                    pt, pttk = pt_rots[e2].next()
                    P.op("act", lambda e, tmp=tmp, pt=pt: e.activation(
                        out=pt[:, lo:hi], in_=tmp[:, lo:hi], func=AF.Exp), reads=[tmtk], writes=[pttk])
                    PTs[(e2, r, jb)] = (pt, pttk)

            def pv_task(r, j):
                jb = j + 1
                nd, ndtk = psND.next()
                kbp = r * nb + j
                kbd = r * nb + jb
                for part in (0, 1):
                    co = slice(128 * part, 128 * part + 128)
                    for e2 in (0, 1):
                        ptp, ptptk = PTs[(e2, r, j)]
                        ptd, ptdtk = PTs[(e2, r, jb)]
                        if part == 0:
                            lp, ld = Vz[e2][:, kbp, :], Vz[e2][:, kbd, :]
                        else:
                            lp, ld = (honesz[e2] if j == 0 else onesz[e2]), onesz[e2]
                        P.op("pe", lambda e, nd=nd, co=co, lp=lp, ptp=ptp, e2=e2: e.matmul(
                            nd[:, co], lhsT=lp, rhs=ptp[:, 128:256], start=(e2 == 0), stop=False),
                            reads=[vtk, ctk, ptptk], writes=[ndtk], signal=False)
                        P.op("pe", lambda e, nd=nd, co=co, ld=ld, ptd=ptd, e2=e2: e.matmul(
                            nd[:, co], lhsT=ld, rhs=ptd[:, 0:128], start=False, stop=(e2 == 1)),
                            reads=[vtk, ctk, ptdtk], writes=[ndtk], signal=(part == 1 and e2 == 1))
                t0 = r + d * 128 * j
                accv = ACC[:, :, sslice(t0, 128, d)]
                ndv = nd.rearrange("p (a n) -> p a n", a=2)
                if g == 0:
                    P.op("act", lambda e, accv=accv, ndv=ndv: e.activation(out=accv, in_=ndv, func=AF.Copy),
                         reads=[ndtk], writes=[acctk])
                else:
                    P.op("dve", lambda e, accv=accv, ndv=ndv: e.tensor_tensor(out=accv, in0=ndv, in1=accv, op=ALU.add),
                         reads=[ndtk, acctk], writes=[acctk])

            LA = 3
            pending = []
            tasks = [(r, jb) for r in range(d if lvl >= 4 else 0) for jb in range(nb)]
            for i, (r, jb) in enumerate(tasks):
                score_task(r, jb)
                if jb >= 1:
                    pending.append((i, r, jb - 1))
                while pending and pending[0][0] <= i - LA:
                    _, r_, j_ = pending.pop(0)
                    pv_task(r_, j_)
            for (_, r_, j_) in pending:
                pv_task(r_, j_)
        P.op("dve", lambda e: e.reciprocal(out=ACC[:, 1, :], in_=ACC[:, 1, :]), reads=[acctk], writes=[acctk])
        P.op("dve", lambda e, hp=hp: e.tensor_tensor(out=oT[:, hp, :], in0=ACC[:, 0, :], in1=ACC[:, 1, :], op=ALU.mult),
             reads=[acctk], writes=[ottk])
    P.barrier()
    cx.release(mk)
    mk = cx.mark()
    X = TOP.rearrange("p (c n) -> p c n", c=NC8)
    Xtk = [[Tk() for _ in range(4)] for _ in range(NC8)]
    for c in range(NC8):
        for tg in range(4):
            P.dma("sync", X[:, c, tg * 512:(tg + 1) * 512], xT_ext[c * 128:(c + 1) * 128, NT + tg * 512:NT + (tg + 1) * 512],
                  writes=[Xtk[c][tg]])
    ws2 = WStream(cx, st, 4096, nstage=2, nslot=2)
    wov = wo.rearrange("(c p) n -> p c n", p=128)
    psA = Rot(cx.banks[0:4])
    for half in range(2):
        wa, watk = ws2.load([wov[:, :, half * 512:(half + 1) * 512]])
        wa3 = wa.rearrange("p (c n) -> p c n", c=NC8)
        for o4 in range(4):
            oc = half * 4 + o4
            for tg in range(4):
                sl = slice(tg * 512, (tg + 1) * 512)
                ps, pstk = psA.next()
                for c in range(NC8):
                    P.op("pe", lambda e, c=c, o4=o4, sl=sl, ps=ps, wa3=wa3: e.matmul(
                        ps, lhsT=wa3[:, c, o4 * 128:(o4 + 1) * 128], rhs=oT[:, c, sl],
                        start=(c == 0), stop=(c == NC8 - 1)),
                        reads=[watk, ottk], writes=[pstk], signal=(c == NC8 - 1))
                P.op("dve", lambda e, oc=oc, sl=sl, ps=ps: e.tensor_tensor(
                    out=X[:, oc, sl], in0=ps, in1=X[:, oc, sl], op=ALU.add),
                    reads=[pstk, Xtk[oc][tg]], writes=[Xtk[oc][tg]])
    P.barrier()
    cx.release(mk)
    return X, Xtk


def emit_store(cx, X, Xtk, out_dram):
    P = cx.P
    for c in range(NC8):
        P.dma("sync", out_dram[c * 128:(c + 1) * 128, :], X[:, c, :], reads=Xtk[c])


def build_layer0(debug=False, lvl=9, hps=8):
    nc = bass.Bass("TRN2", target_bir_lowering=False)
    xT_ext = nc.dram_tensor("xT_ext", [D, 2 * NT], F32, kind="ExternalInput").ap()
    pT = nc.dram_tensor("pT", [256, NT], F32, kind="ExternalInput").ap()
    cst = nc.dram_tensor("cst", [128, G0_END], F32, kind="ExternalInput").ap()
    wqkv = nc.dram_tensor("a_w_qkv", [D, 9216], F32, kind="ExternalInput").ap()
    wo = nc.dram_tensor("a_w_o", [D, D], F32, kind="ExternalInput").ap()
    w1 = nc.dram_tensor("mlp_w1", [D, 4096], F32, kind="ExternalInput").ap()
    w2 = nc.dram_tensor("mlp_w2", [4096, D], F32, kind="ExternalInput").ap()
    wg = nc.dram_tensor("ple_w_gate", [D, D], F32, kind="ExternalInput").ap()
    wp = nc.dram_tensor("ple_w_proj", [256, D], F32, kind="ExternalInput").ap()
    out = nc.dram_tensor("xout", [D, NT], F32, kind="ExternalOutput").ap()
    if debug:
        dbg_a = nc.dram_tensor("dbg_a", [D, NT], F32, kind="ExternalOutput").ap()
        dbg_m = nc.dram_tensor("dbg_m", [D, NT], F32, kind="ExternalOutput").ap()
    cx = Ctx(nc)
    P = cx.P
    cf, cb, ctk = load_consts(cx, None, cst, G0_END)
    cx.eps_col = cf[:, C_EPS:C_EPS + 1]
    ones_bf = cb[:, C_ONES:C_ONES + 128]
    TOP = cx.sb(None, [128, 16384], F32, "TOP")
    R1 = cx.sb(None, [128, 12288], F32, "R1")
    mk = cx.mark()
    X, Xtk = emit_attention(cx, xT_ext, wqkv, wo, cf, cb, ctk, TOP, R1, lvl=lvl, hps=hps)
    cx.release(mk)
    cx.top = cx.top - 12288 * 4
    if debug:
        emit_store(cx, X, Xtk, dbg_a)
    emit_mlp(cx, X, Xtk, cf[:, G0_MLPN:G0_MLPN + 8], ctk, w1, w2, ones_bf)
    if debug:
        emit_store(cx, X, Xtk, dbg_m)
    emit_ple(cx, X, Xtk, cf[:, G0_PLEN:G0_PLEN + 8], ctk, wg, wp, pT, ones_bf)
    emit_store(cx, X, Xtk, out)
    P.finish()
    return nc, cx


def layer0_inputs(inputs, core):
    x = inputs["x"][0]
    lo = core * NT
    xe = np.zeros((2 * NT, D), np.float32)
    if core > 0:
        xe[:NT] = x[lo - NT:lo]
    xe[NT:] = x[lo:lo + NT]
    c = base_consts(core, G0_END)
    c[:, G0_ANORM:G0_ANORM + 8] = col_layout(inputs["a_norm"][0])
    c[:, G0_QG:G0_QG + 3] = np.tile(inputs["a_q_gain"][0].T, (2, 1))
    c[:, G0_KG:G0_KG + 3] = np.tile(inputs["a_k_gain"][0].T, (2, 1))
    c[:, G0_MLPN:G0_MLPN + 8] = col_layout(inputs["mlp_norm"][0])
    c[:, G0_PLEN:G0_PLEN + 8] = col_layout(inputs["ple_norm"][0])
    return {
        "xT_ext": np.ascontiguousarray(xe.T),
        "pT": np.ascontiguousarray(inputs["p"][0, 0, lo:lo + NT].T),
        "cst": c,
        "a_w_qkv": inputs["a_w_qkv"][0], "a_w_o": inputs["a_w_o"][0],
        "mlp_w1": inputs["mlp_w1"][0], "mlp_w2": inputs["mlp_w2"][0],
        "ple_w_gate": inputs["ple_w_gate"][0], "ple_w_proj": inputs["ple_w_proj"][0],
    }

BH = 4
DH = 512
NCK = NT // 128
KSCALE = DH ** -0.5
NST = 8192 + 2048 + 8

L_BNORM = C_GAINS
L_MLPN = L_BNORM + 8
L_PLEN = L_MLPN + 8
L_CONVW = L_PLEN + 8
L_CONVB = L_CONVW + 64
L_SKIP = L_CONVB + 16
L_HGAIN = L_SKIP + 16
L_BI = L_HGAIN + 16
L_BF = L_BI + 1
L_MASKLOW = L_BF + 1
L_SEL = L_MASKLOW + 128
L_CMASK = L_SEL + 512
L_NEG = L_CMASK + 7
L_E0 = L_NEG + 1
L_LNK = L_E0 + 1
L_CNEG = L_LNK + 1
L_END = L_CNEG + 7


def layer1_consts(inputs, core):
    c = base_consts(core, L_END)
    c[:, L_BNORM:L_BNORM + 8] = col_layout(inputs["b_norm"][0])
    c[:, L_MLPN:L_MLPN + 8] = col_layout(inputs["mlp_norm"][1])
    c[:, L_PLEN:L_PLEN + 8] = col_layout(inputs["ple_norm"][1])
    cw = inputs["b_conv_w"][0]
    c[:, L_CONVW:L_CONVW + 64] = cw.reshape(4, 16, 128).transpose(2, 1, 0).reshape(128, 64)
    c[:, L_CONVB:L_CONVB + 16] = col_layout(inputs["b_conv_b"][0])
    c[:, L_SKIP:L_SKIP + 16] = col_layout(inputs["b_skip"][0])
    c[:, L_HGAIN:L_HGAIN + 16] = col_layout(inputs["b_h_gain"][0])
    bg = inputs["b_b_gate"][0]
    c[0:4, L_BI] = bg[0:4]
    c[0:4, L_BF] = bg[4:8]
    s_ = np.arange(128)[:, None]
    t_ = np.arange(128)[None, :]
    c[:, L_MASKLOW:L_MASKLOW + 128] = np.where(s_ <= t_, 0.0, BIG)
    for hd in range(4):
        c[hd, L_SEL + hd * 128:L_SEL + (hd + 1) * 128] = 1.0
    for cp in range(7):
        c[:, L_CMASK + cp] = 1.0 if cp < core else 0.0
        c[:, L_CNEG + cp] = 0.0 if cp < core else -1e30
    c[:, L_NEG] = -1e30
    c[0, L_E0] = 1.0
    c[:, L_LNK] = np.log(KSCALE)
    return c


def bd_compact(w, transpose=False):
    out = np.zeros((2048, 128), np.float32)
    n = np.arange(512)
    for j in range(4):
        for k in range(4):
            if transpose:
                out[4 * n + k, (4 * n + j) % 128] = w[:, j, k]
            else:
                out[4 * n + j, (4 * n + k) % 128] = w[:, j, k]
    return out


def layer1_inputs(inputs, core, x1T_full, stage, st_all=None, g_in=None):
    lo = core * NT
    xh = np.zeros((D, 4), np.float32)
    if core > 0:
        xh[:, 1:4] = x1T_full[:, lo - 3:lo]
    m = {
        "x1T": np.ascontiguousarray(x1T_full[:, lo:lo + NT]),
        "xh": xh,
        "cst": layer1_consts(inputs, core),
        "b_w_up": inputs["b_w_up"][0],
        "bd": np.stack([bd_compact(inputs["b_w_q"][0]), bd_compact(inputs["b_w_k"][0]), bd_compact(inputs["b_w_v"][0])]),
        "bdT": np.stack([bd_compact(inputs["b_w_q"][0], True), bd_compact(inputs["b_w_k"][0], True),
                         bd_compact(inputs["b_w_v"][0], True)]),
        "w_gate": inputs["b_w_gate"][0],
    }
    if stage == "C":
        m.update({
            "st_all": st_all,
            "g_in": g_in,
            "b_w_down": inputs["b_w_down"][0],
            "pT": np.ascontiguousarray(inputs["p"][1, 0, lo:lo + NT].T),
            "mlp_w1": inputs["mlp_w1"][1], "mlp_w2": inputs["mlp_w2"][1],
            "ple_w_gate": inputs["ple_w_gate"][1], "ple_w_proj": inputs["ple_w_proj"][1],
        })
    return m


def build_layer1(stage, debug=False, dbg_stop=None):
    nc = bass.Bass("TRN2", target_bir_lowering=False)
    x1T = nc.dram_tensor("x1T", [D, NT], F32, kind="ExternalInput").ap()
    xh = nc.dram_tensor("xh", [D, 4], F32, kind="ExternalInput").ap()
    cst = nc.dram_tensor("cst", [128, L_END], F32, kind="ExternalInput").ap()
    wup = nc.dram_tensor("b_w_up", [D, 4096], F32, kind="ExternalInput").ap()
    bd = nc.dram_tensor("bd", [3, 2048, 128], F32, kind="ExternalInput").ap()
    bdT = nc.dram_tensor("bdT", [3, 2048, 128], F32, kind="ExternalInput").ap()
    wgate = nc.dram_tensor("w_gate", [6144, 8], F32, kind="ExternalInput").ap()
    if stage == "B":
        st_out = nc.dram_tensor("st_out", [128, NST], F32, kind="ExternalOutput").ap()
        g_out = nc.dram_tensor("g_out", [8, NT], F32, kind="ExternalOutput").ap()
    else:
        st_all = nc.dram_tensor("st_all", [7, 128, NST], F32, kind="ExternalInput").ap()
        g_in = nc.dram_tensor("g_in", [8, NT], F32, kind="ExternalInput").ap()
        wdown = nc.dram_tensor("b_w_down", [2048, D], F32, kind="ExternalInput").ap()
        pT = nc.dram_tensor("pT", [256, NT], F32, kind="ExternalInput").ap()
        w1 = nc.dram_tensor("mlp_w1", [D, 4096], F32, kind="ExternalInput").ap()
        w2 = nc.dram_tensor("mlp_w2", [4096, D], F32, kind="ExternalInput").ap()
        wg = nc.dram_tensor("ple_w_gate", [D, D], F32, kind="ExternalInput").ap()
        wp = nc.dram_tensor("ple_w_proj", [256, D], F32, kind="ExternalInput").ap()
        out = nc.dram_tensor("xout", [D, NT], F32, kind="ExternalOutput").ap()
        yscr = nc.dram_tensor("yscr", [2048, NT], BF16).ap()
        if debug:
            dbg_a = nc.dram_tensor("dbg_a", [D, NT], F32, kind="ExternalOutput").ap()
    cx = Ctx(nc)
    P = cx.P
    cf, cb, ctk = load_consts(cx, None, cst, L_END)
    cx.eps_col = cf[:, C_EPS:C_EPS + 1]
    ones_bf = cb[:, C_ONES:C_ONES + 128]
    ones_f = cf[:, C_ONES:C_ONES + 128]
    ident_f = cf[:, C_ID:C_ID + 128]
    one_col = cf[:, C_ONES:C_ONES + 1]
    TOP = cx.sb(None, [128, 16384], F32, "TOP")
    hT = TOP[:, 0:8208].bitcast(BF16)[:, 0:8 * 2052].rearrange("p (c n) -> p c n", c=NC8)
    topfree = TOP[:, 8208:16384]
    base_mark = cx.mark()
    bk = [(cx.banks[i], Tk()) for i in range(8)]
    ws = WStream(cx, None, 4096, nstage=0, nslot=3)
    ws.stage = Rot([topfree[:, 0:4096]])
    bdh = cx.sb(None, [128, 3, 4, 128], BF16, "bdh")
    bdh_st = cx.sb(None, [128, 3, 4, 128], F32, "bdh_st")
    diag = cx.sb(None, [128, 4, 4, 128], BF16, "diag")
    xms = [cx.sb(None, [128, 4, 516], BF16, "xm") for _ in range(2)]
    xc = cx.sb(None, [128, 4, 512], BF16, "xc")
    GF = cx.sb(None, [128, NT], F32, "GF")
    BETAx = cx.sb(None, [128, NT + 1], F32, "BETAx")
    small = cx.sb(None, [128, 64], F32, "small")
    TMw = cx.sb(None, [128, NCK, 4], F32, "TMw")
    TMa = cx.sb(None, [128, NCK, 4], F32, "TMa")
    if stage == "C":
        fold = cx.sb(None, [128, 7, 8], F32, "foldin")
        S1 = cx.sb(None, [128, 7, 4], F32, "S1")
        S2 = cx.sb(None, [128, 7, 4], F32, "S2")
        mrun = cx.sb(None, [128, 4], F32, "mrun")
        fa = cx.sb(None, [128, 4], F32, "fa")
        fb = cx.sb(None, [128, 4], F32, "fb")
        fc_ = cx.sb(None, [128, 4], F32, "fc")
    pers_mark = cx.mark()

    mk = cx.mark()
    sq_rot = Rot([cx.sb(None, [128, 512], BF16, "sq") for _ in range(2)])
    rstd_rot = Rot([cx.sb(None, [128, 512], F32, "rstd") for _ in range(2)])
    xstg = [cx.sb(None, [128, NC8, 512], F32, "xstg") for _ in range(2)]
    xstk = [Tk(), Tk()]
    ps_stat = Rot([bk[7], bk[6]])
    gcol = cf[:, L_BNORM:L_BNORM + 8]
    httk = Tk()
    pieces = [(None, 4)] + [(tg, 512) for tg in range(4)]
    for i, (tg, n) in enumerate(pieces):
        xa = xstg[i % 2]
        xt = xstk[i % 2]
        if tg is None:
            P.dma("sync", xa[:, :, 0:4], xh.rearrange("(c p) n -> p c n", p=128), writes=[xt])
            h0 = 0
        else:
            P.dma("sync", xa, x1T[:, tg * 512:(tg + 1) * 512].rearrange("(c p) n -> p c n", p=128), writes=[xt])
            h0 = 4 + tg * 512
        xs = [(xa[:, c, 0:n], xt) for c in range(NC8)]
        ps_ap, ps_tk = ps_stat.next()
        rstd, rtk = rstd_rot.next()
        rms_stats(cx, xs, n, sq_rot, ps_ap, ps_tk, rstd, rtk, ones_bf, ctk, 1.0 / D)
        for c in range(NC8):
            P.op("dve", lambda e, c=c, xa=xa, rstd=rstd, n=n, h0=h0: e.scalar_tensor_tensor(
                out=hT[:, c, h0:h0 + n], in0=xa[:, c, 0:n], scalar=gcol[:, c:c + 1], in1=rstd[:, 0:n],
                op0=ALU.mult, op1=ALU.mult), reads=[xt, rtk, ctk], writes=[httk])
    P.barrier()
    cx.release(mk)

    GI = cx.sb(None, [128, NT], F32, "GI")
    LF = cx.sb(None, [128, NT], F32, "LF")
    BB = cx.sb(None, [128, NT], F32, "BB")
    T1 = cx.sb(None, [128, NT], F32, "T1")
    wfold = [[cx.sb(None, [128, 16, 128], BF16, "wfold") for _ in range(2)] for _ in range(2)]
    wftk = Tk()
    bdT_sb = topfree[:, 0:6144].rearrange("p (a b) -> p a b", a=48)
    wg_sb = topfree[:, 6144:6528].rearrange("p (a b) -> p a b", a=48)
    btk = Tk()
    for j in range(3 if stage == "B" else 0):
        P.dma("sync", bdT_sb[:, j * 16:(j + 1) * 16, :], bdT[j].rearrange("(c p) n -> p c n", p=128), writes=[btk])
    if stage == "B":
        P.dma("sync", wg_sb, wgate.rearrange("(c p) n -> p c n", p=128), writes=[btk])
    zpad = Rot([topfree[:, 6528 + i * 128:6528 + (i + 1) * 128] for i in range(4)])
    for (za, ztk) in zpad.items:
        P.op("pool", lambda e, za=za: e.memset(za, 0.0), writes=[ztk])
    psF = Rot(bk[0:2])
    for mc in range(16 if stage == "B" else 0):
        for part in range(2):
            for xm_ in range(2):
                ps, pstk = psF.next()
                srcs = (0, 1) if xm_ == 0 else (2,)
                for si, j in enumerate(srcs):
                    za, ztk = zpad.next()
                    P.op("dve", lambda e, za=za, j=j, mc=mc, part=part: e.tensor_copy(
                        out=za[:, 0:4], in_=wg_sb[:, j * 16 + mc, part * 4:part * 4 + 4]), reads=[btk], writes=[ztk])
                    P.op("pe", lambda e, ps=ps, za=za, j=j, mc=mc, si=si, srcs=srcs: e.matmul(
                        ps[:, 0:128], lhsT=bdT_sb[:, j * 16 + mc, :], rhs=za, start=(si == 0), stop=(si == len(srcs) - 1)),
                        reads=[btk, ztk], writes=[pstk])
                P.op("act", lambda e, ps=ps, xm_=xm_, part=part, mc=mc: e.activation(
                    out=wfold[xm_][part][:, mc, :], in_=ps[:, 0:128], func=AF.Copy), reads=[pstk], writes=[wftk])
    P.barrier()

    hdtk = Tk()
    xmtk = [Tk(), Tk()]
    xctk = Tk()
    psA = Rot(bk[0:2])
    wupv = wup.rearrange("(c p) n -> p c n", p=128)
    bdv = bd.rearrange("j (c p) n -> p j c n", p=128)
    state = {"i": 0}

    def head_setup(hd):
        for j in range(3):
            P.dma("sync", bdh_st[:, j, :, :], bdv[:, j, hd * 4:(hd + 1) * 4, :], writes=[hdtk])
        P.op("pool", lambda e: e.tensor_copy(out=bdh, in_=bdh_st), reads=[hdtk], writes=[hdtk])
        for mc in range(4):
            for k in range(4):
                col = L_CONVW + (hd * 4 + mc) * 4 + k
                P.op("act", lambda e, mc=mc, k=k, col=col: e.activation(
                    out=diag[:, mc, k, :], in_=ident_f, func=AF.Copy, scale=cf[:, col:col + 1]),
                    reads=[ctk], writes=[hdtk])
        wx, wxtk = ws.load([wupv[:, :, hd * 512:(hd + 1) * 512]])
        return wx.rearrange("p (c n) -> p c n", c=NC8), wxtk

    def front(hd, tg, wx3, wxtk):
        i = state["i"]
        state["i"] += 1
        xm, xmt = xms[i % 2], xmtk[i % 2]
        xmp, xmpt = xms[(i + 1) % 2], xmtk[(i + 1) % 2]
        for mc in range(4):
            ps, pstk = psA.next()
            for c in range(NC8):
                P.op("pe", lambda e, c=c, mc=mc, ps=ps: e.matmul(
                    ps, lhsT=wx3[:, c, mc * 128:(mc + 1) * 128], rhs=hT[:, c, 4 + tg * 512:4 + (tg + 1) * 512],
                    start=(c == 0), stop=(c == NC8 - 1)), reads=[wxtk], writes=[pstk], signal=(c == NC8 - 1))
            P.op("act", lambda e, ps=ps, mc=mc, xm=xm: e.activation(out=xm[:, mc, 4:516], in_=ps, func=AF.Copy),
                 reads=[pstk], writes=[xmt])
            if tg == 0:
                ps, pstk = psA.next()
                for c in range(NC8):
                    P.op("pe", lambda e, c=c, mc=mc, ps=ps: e.matmul(
                        ps[:, 0:4], lhsT=wx3[:, c, mc * 128:(mc + 1) * 128], rhs=hT[:, c, 0:4],
                        start=(c == 0), stop=(c == NC8 - 1)), reads=[wxtk], writes=[pstk], signal=(c == NC8 - 1))
                P.op("act", lambda e, ps=ps, mc=mc, xm=xm: e.activation(out=xm[:, mc, 0:4], in_=ps[:, 0:4], func=AF.Copy),
                     reads=[pstk], writes=[xmt])
        if tg > 0:
            P.op("pool", lambda e, xm=xm, xmp=xmp: e.tensor_copy(out=xm[:, :, 0:4], in_=xmp[:, :, 512:516]),
                 reads=[xmpt], writes=[xmt])
        for mc in range(4):
            ps, pstk = psA.next()
            for k in range(4):
                P.op("pe", lambda e, k=k, mc=mc, ps=ps, xm=xm: e.matmul(
                    ps, lhsT=diag[:, mc, k, :], rhs=xm[:, mc, 1 + k:1 + k + 512], start=(k == 0), stop=(k == 3)),
                    reads=[hdtk, xmt], writes=[pstk], signal=(k == 3))
            col = L_CONVB + hd * 4 + mc
            P.op("act", lambda e, ps=ps, mc=mc, col=col: e.activation(
                out=xc[:, mc, :], in_=ps, func=AF.Silu, bias=cf[:, col:col + 1]), reads=[pstk, ctk], writes=[xctk])
        return xm, xmt

    gtk = Tk()
    psG = Rot(bk[2:4])
    if stage == "C":
        P.op("pool", lambda e: e.memset(GI, 0.0), writes=[gtk])
        P.op("pool", lambda e: e.memset(GF, 0.0), writes=[gtk])
        P.dma("sync", GI[0:4, :], g_in[0:4, :], writes=[gtk])
        P.dma("sync", GF[0:4, :], g_in[4:8, :], writes=[gtk])
    for hd in range(BH if stage == "B" else 0):
        wx3, wxtk = head_setup(hd)
        for tg in range(4):
            xm, xmt = front(hd, tg, wx3, wxtk)
            for part, Grow in ((0, GI), (1, GF)):
                ps, pstk = psG.next()
                for mc in range(4):
                    P.op("pe", lambda e, ps=ps, mc=mc, part=part: e.matmul(
                        ps, lhsT=wfold[0][part][:, hd * 4 + mc, :], rhs=xc[:, mc, :], start=(mc == 0), stop=False),
                        reads=[wftk, xctk], writes=[pstk], signal=False)
                    P.op("pe", lambda e, ps=ps, mc=mc, part=part, xm=xm: e.matmul(
                        ps, lhsT=wfold[1][part][:, hd * 4 + mc, :], rhs=xm[:, mc, 4:516], start=False, stop=(mc == 3)),
                        reads=[wftk, xmt], writes=[pstk], signal=(mc == 3))
                sl = slice(tg * 512, (tg + 1) * 512)
                if hd == 0:
                    P.op("act", lambda e, ps=ps, Grow=Grow, sl=sl: e.activation(out=Grow[:, sl], in_=ps, func=AF.Copy),
                         reads=[pstk], writes=[gtk])
                else:
                    P.op("dve", lambda e, ps=ps, Grow=Grow, sl=sl: e.tensor_tensor(out=Grow[:, sl], in0=ps, in1=Grow[:, sl], op=ALU.add),
                         reads=[pstk, gtk], writes=[gtk])

    rtk = Tk()
    if stage == "B":
        P.dma("sync", g_out[0:4, :], GI[0:4, :], reads=[gtk])
        P.dma("sync", g_out[4:8, :], GF[0:4, :], reads=[gtk])
    P.op("dve", lambda e: e.tensor_scalar(out=GI, in0=GI, scalar1=cf[:, L_BI:L_BI + 1], scalar2=None, op0=ALU.add),
         reads=[gtk, ctk], writes=[gtk])
    P.op("dve", lambda e: e.tensor_scalar(out=GF, in0=GF, scalar1=cf[:, L_BF:L_BF + 1], scalar2=None, op0=ALU.add),
         reads=[gtk, ctk], writes=[gtk])
    P.op("dve", lambda e: e.tensor_scalar(out=T1, in0=GF, scalar1=-1.0, scalar2=None, op0=ALU.mult), reads=[gtk], writes=[rtk])
    P.op("dve", lambda e: e.tensor_tensor(out=T1, in0=T1, in1=GF, op=ALU.max), reads=[gtk, rtk], writes=[rtk])
    P.op("act", lambda e: e.activation(out=T1, in_=T1, func=AF.Exp, scale=-1.0), reads=[rtk], writes=[rtk])
    P.op("act", lambda e: e.activation(out=T1, in_=T1, func=AF.Ln, bias=one_col), reads=[rtk, ctk], writes=[rtk])
    P.op("dve", lambda e: e.scalar_tensor_tensor(out=LF, in0=GF, scalar=0.0, in1=T1, op0=ALU.min, op1=ALU.subtract),
         reads=[gtk, rtk], writes=[rtk])
    P.op("pool", lambda e: e.memset(T1, 1.0), reads=[rtk], writes=[rtk])
    P.op("dve", lambda e: e.tensor_tensor_scan(out=BB, data0=T1, data1=LF, initial=0.0, op0=ALU.mult, op1=ALU.add),
         reads=[rtk], writes=[rtk])
    P.op("dve", lambda e: e.tensor_tensor(out=T1, in0=GI, in1=BB, op=ALU.subtract), reads=[gtk, rtk], writes=[rtk])
    ALPHA = T1
    psR = Rot([bk[4]])
    tmtk = Tk()

    def to_token_major(row, dst):
        ps, pstk = psR.next()
        for ck in range(NCK):
            P.op("pe", lambda e, ps=ps, ck=ck: e.matmul(ps[:, ck * 4:ck * 4 + 4], lhsT=row[:, ck * 128:(ck + 1) * 128],
                                                        rhs=ident_f[:, 0:4], start=True, stop=True),
                 reads=[rtk, gtk, ctk], writes=[pstk], signal=(ck == NCK - 1))
        P.op("act", lambda e, ps=ps: e.activation(out=dst, in_=ps[:, 0:64].rearrange("p (a b) -> p a b", a=NCK), func=AF.Copy),
             reads=[pstk], writes=[tmtk])

    def replicate_cols(col_ap, dst4):
        ps, pstk = psR.next()
        za = small[:, 32:32 + 4]
        P.op("dve", lambda e: e.tensor_scalar(out=za, in0=ident_f[:, 0:4], scalar1=col_ap, scalar2=None, op0=ALU.mult),
             reads=[rtk, ctk, gtk], writes=[rtk])
        P.op("pe", lambda e, ps=ps: e.matmul(ps[:, 0:4], lhsT=ones_f, rhs=za, start=True, stop=True),
             reads=[rtk, ctk], writes=[pstk])
        P.op("act", lambda e, ps=ps: e.activation(out=dst4, in_=ps[:, 0:4], func=AF.Copy), reads=[pstk], writes=[rtk])

    if stage == "B":
        mx = small[:, 0:1]
        P.op("dve", lambda e: e.tensor_reduce(out=mx, in_=ALPHA, axis=AX.X, op=ALU.max), reads=[rtk], writes=[rtk])
        nb_ = small[:, 1:2]
        P.op("dve", lambda e: e.scalar_tensor_tensor(out=nb_, in0=mx, scalar=-1.0, in1=cf[:, L_LNK:L_LNK + 1],
                                                     op0=ALU.mult, op1=ALU.add), reads=[rtk, ctk], writes=[rtk])
        P.op("act", lambda e: e.activation(out=LF, in_=ALPHA, func=AF.Exp, bias=nb_), reads=[rtk], writes=[rtk])
        to_token_major(LF, TMw)
        ml = small[:, 2:3]
        P.op("dve", lambda e: e.tensor_tensor(out=ml, in0=mx, in1=BB[:, NT - 1:NT], op=ALU.add), reads=[rtk], writes=[rtk])
        fin = cx.sb(None, [128, 8], F32, "fin")
        replicate_cols(BB[:, NT - 1:NT], fin[:, 0:4])
        replicate_cols(ml, fin[:, 4:8])
        P.dma("sync", st_out[:, 10240:10248], fin, reads=[rtk])
        P.barrier()
        cx.release(pers_mark)
        kv_rot = Rot([cx.sb(None, [128, 512], BF16, "kv") for _ in range(4)])
        stC = cx.sb(None, [128, 4, 512], F32, "stC")
        stn = cx.sb(None, [128, 512], F32, "stn")
        sttk = Tk()
        psKV = Rot([bk[0], bk[1], bk[2]])
        for hd in range(BH):
            wx3, wxtk = head_setup(hd)
            cacc = [bk[3 + dc] for dc in range(4)]
            nacc, nacctk = bk[7]
            for tg in range(4):
                xm, xmt = front(hd, tg, wx3, wxtk)
                for cl in range(4):
                    ck = tg * 4 + cl
                    tsl = slice(cl * 128, (cl + 1) * 128)
                    ps, pstk = psKV.next()
                    for mc in range(4):
                        P.op("pe", lambda e, ps=ps, mc=mc, tsl=tsl: e.matmul(
                            ps[:, mc * 128:(mc + 1) * 128], lhsT=xc[:, mc, tsl], rhs=bdh[:, 1, mc, :], start=True, stop=True),
                            reads=[xctk, hdtk], writes=[pstk], signal=(mc == 3))
                    wk, wktk = kv_rot.next()
                    P.op("act", lambda e, ps=ps, wk=wk, ck=ck, hd=hd: e.activation(
                        out=wk, in_=ps, func=AF.Copy, scale=TMw[:, ck, hd:hd + 1]), reads=[pstk, tmtk], writes=[wktk])
                    ps, pstk = psKV.next()
                    for mc in range(4):
                        P.op("pe", lambda e, ps=ps, mc=mc, cl=cl, xm=xm: e.matmul(
                            ps[:, mc * 128:(mc + 1) * 128], lhsT=xm[:, mc, 4 + cl * 128:4 + (cl + 1) * 128], rhs=bdh[:, 2, mc, :],
                            start=True, stop=True), reads=[xmt, hdtk], writes=[pstk], signal=(mc == 3))
                    vv, vtk = kv_rot.next()
                    P.op("act", lambda e, ps=ps, vv=vv: e.activation(out=vv, in_=ps, func=AF.Copy), reads=[pstk], writes=[vtk])
                    last = (ck == NCK - 1)
                    for dc in range(4):
                        P.op("pe", lambda e, dc=dc, wk=wk, vv=vv, ck=ck, last=last: e.matmul(
                            cacc[dc][0], lhsT=wk[:, dc * 128:(dc + 1) * 128], rhs=vv, start=(ck == 0), stop=last),
                            reads=[wktk, vtk], writes=[cacc[dc][1]], signal=True)
                    P.op("pe", lambda e, wk=wk, ck=ck, last=last: e.matmul(
                        nacc, lhsT=ones_bf, rhs=wk, start=(ck == 0), stop=last), reads=[wktk, ctk], writes=[nacctk], signal=True)
            for dc in range(4):
                P.op("act", lambda e, dc=dc: e.activation(out=stC[:, dc, :], in_=cacc[dc][0], func=AF.Copy),
                     reads=[cacc[dc][1]], writes=[sttk])
            P.op("dve", lambda e: e.tensor_copy(out=stn, in_=nacc), reads=[nacctk], writes=[sttk])
            P.dma("sync", st_out[:, hd * 2048:(hd + 1) * 2048], stC.rearrange("p a b -> p (a b)"), reads=[sttk])
            P.dma("sync", st_out[:, 8192 + hd * 512:8192 + (hd + 1) * 512], stn, reads=[sttk])
        P.finish()
        return nc, cx

    ftk = Tk()
    P.dma("sync", fold, st_all[:, :, 10240:10248].rearrange("c p n -> p c n"), writes=[ftk])
    negc = cf[:, L_NEG:L_NEG + 1]
    P.op("dve", lambda e: e.memset(mrun, -1e30), writes=[ftk])
    for cp in range(7):
        mu = cf[:, L_CMASK + cp:L_CMASK + cp + 1]
        P.op("dve", lambda e, cp=cp, mu=mu: e.scalar_tensor_tensor(out=fa, in0=fold[:, cp, 0:4], scalar=mu, in1=mrun,
                                                                    op0=ALU.mult, op1=ALU.add), reads=[ftk, ctk], writes=[ftk])
        P.op("dve", lambda e, cp=cp, mu=mu: e.tensor_scalar(out=fb, in0=fold[:, cp, 4:8], scalar1=mu,
                                                            scalar2=cf[:, L_CNEG + cp:L_CNEG + cp + 1], op0=ALU.mult, op1=ALU.add),
             reads=[ftk, ctk], writes=[ftk])
        P.op("dve", lambda e: e.tensor_tensor(out=fc_, in0=fa, in1=fb, op=ALU.max), reads=[ftk], writes=[ftk])
        P.op("dve", lambda e: e.tensor_tensor(out=fa, in0=fa, in1=fc_, op=ALU.subtract), reads=[ftk], writes=[ftk])
        P.op("dve", lambda e: e.tensor_tensor(out=fb, in0=fb, in1=fc_, op=ALU.subtract), reads=[ftk], writes=[ftk])
        P.op("act", lambda e, cp=cp: e.activation(out=S1[:, cp, :], in_=fa, func=AF.Exp), reads=[ftk], writes=[ftk])
        P.op("act", lambda e: e.activation(out=fb, in_=fb, func=AF.Exp), reads=[ftk], writes=[ftk])
        P.op("dve", lambda e, cp=cp, mu=mu: e.tensor_scalar(out=S2[:, cp, :], in0=fb, scalar1=mu, scalar2=None, op0=ALU.mult),
             reads=[ftk, ctk], writes=[ftk])
        P.op("dve", lambda e: e.tensor_copy(out=mrun, in_=fc_), reads=[ftk], writes=[ftk])
    mst = small[:, 4:5]
    P.op("dve", lambda e: e.tensor_tensor(out=small[:, 8:12], in0=mrun, in1=ident_f[:, 0:4], op=ALU.mult), reads=[ftk, ctk], writes=[rtk])
    P.op("dve", lambda e: e.tensor_reduce(out=mst, in_=small[:, 8:12], axis=AX.X, op=ALU.add), reads=[rtk], writes=[rtk])
    P.op("dve", lambda e: e.tensor_tensor_scan(out=GF, data0=LF, data1=GI, initial=mst, op0=ALU.add, op1=ALU.max),
         reads=[rtk, gtk], writes=[gtk])
    MM = GF
    P.op("dve", lambda e: e.tensor_tensor(out=BETAx[:, 1:NT + 1], in0=MM, in1=BB, op=ALU.subtract), reads=[gtk, rtk], writes=[rtk])
    P.op("dve", lambda e: e.tensor_copy(out=BETAx[:, 0:1], in_=mst), reads=[rtk], writes=[rtk])
    BETA = BETAx[:, 1:NT + 1]
    for ck in range(NCK):
        bl = small[:, 16:17]
        P.op("dve", lambda e, ck=ck: e.scalar_tensor_tensor(out=small[:, 16 + ck % 8:17 + ck % 8], in0=BETAx[:, 128 * (ck + 1):128 * (ck + 1) + 1],
                                                            scalar=-1.0, in1=cf[:, L_LNK:L_LNK + 1], op0=ALU.mult, op1=ALU.add),
             reads=[rtk, ctk], writes=[rtk])
        P.op("act", lambda e, ck=ck: e.activation(out=LF[:, ck * 128:(ck + 1) * 128], in_=ALPHA[:, ck * 128:(ck + 1) * 128],
                                                  func=AF.Exp, bias=small[:, 16 + ck % 8:17 + ck % 8]), reads=[rtk], writes=[rtk])
    to_token_major(LF, TMw)
    to_token_major(ALPHA, TMa)
    P.barrier()
    cx.release(pers_mark)
    BETA = BETAx[:, 1:NT + 1]

    qT = cx.sb(None, [128, 4, 512], BF16, "qT")
    kT = cx.sb(None, [128, 4, 512], BF16, "kT")
    zs = cx.sb(None, [128, 4, 512], BF16, "zs")
    yb = cx.sb(None, [128, 4, 512], BF16, "yb")
    qktk, zstk, ytk = Tk(), Tk(), Tk()
    Csts = [cx.sb(None, [128, 4, 512], F32, "Cst") for _ in range(2)]
    Caug = cx.sb(None, [128, 4, 640], BF16, "Caug")
    nrows = [cx.sb(None, [128, 512], F32, "nrow") for _ in range(2)]
    nm = cx.sb(None, [128, 512], F32, "nm")
    ctk2s = [Tk(), Tk()]
    caugtk = Tk()
    clst = Rot([topfree[:, 4096:6144], topfree[:, 6144:8176][:, 0:2032]])
    wk_rot = Rot([cx.sb(None, [128, 512], BF16, "wk") for _ in range(2)])
    va_rot = Rot([cx.sb(None, [128, 640], BF16, "vaug") for _ in range(2)])
    for (va, vatk) in va_rot.items:
        P.op("pool", lambda e, va=va: e.memset(va[:, 512:640], 1.0), writes=[vatk])
    dt_rot = Rot([cx.sb(None, [128, 128], F32, "dtmp") for _ in range(2)])
    sd_rot = Rot([cx.sb(None, [128, 128], BF16, "SdT") for _ in range(2)])
    qs_rot = Rot([cx.sb(None, [128, 4, 128], BF16, "qs") for _ in range(2)])
    hsq_rot = Rot([cx.sb(None, [128, 512], BF16, "hsq") for _ in range(2)])
    dd_rot = Rot([cx.sb(None, [128, 128], F32, "dd") for _ in range(2)])
    rr_rot = Rot([cx.sb(None, [128, 128], F32, "rr") for _ in range(2)])
    sc_rot = Rot([cx.sb(None, [128, 128], F32, "scsb") for _ in range(2)])
    em_rot = Rot([cx.sb(None, [128, 128], F32, "emsb") for _ in range(2)])
    ul_rot = Rot([cx.sb(None, [128, 1], F32, "ulast") for _ in range(2)])
    tt_rot = Rot([cx.sb(None, [128, 128], F32, "tt") for _ in range(3)])
    psB2 = psA
    psS3 = Rot([bk[2]])
    psRP = Rot([bk[2]])
    psH = Rot([bk[4], bk[5]])
    psDS = Rot([bk[6], bk[7]])
    psSS = Rot([bk[3]])
    wzv = wupv
    yview = yscr.rearrange("(c p) n -> p c n", p=128)
    def emit_fold(hd):
        Cst, nrow, ctk2 = Csts[hd % 2], nrows[hd % 2], ctk2s[hd % 2]
        P.op("pool", lambda e: e.memset(Cst, 0.0), writes=[ctk2])
        P.op("pool", lambda e: e.memset(nrow, 0.0), writes=[ctk2])
        Cflat = Cst.rearrange("p a b -> p (a b)")
        for cp in range(7):
            cl_, cltk = clst.items[0]
            P.dma("sync", cl_, st_all[cp][:, hd * 2048:(hd + 1) * 2048], writes=[cltk])
            P.op("act", lambda e, cp=cp, cl_=cl_: e.activation(out=cl_, in_=cl_, func=AF.Copy, scale=S2[:, cp, hd:hd + 1]),
                 reads=[cltk, ftk], writes=[cltk])
            P.op("dve", lambda e, cp=cp, cl_=cl_: e.scalar_tensor_tensor(out=Cflat, in0=Cflat, scalar=S1[:, cp, hd:hd + 1], in1=cl_,
                                                                          op0=ALU.mult, op1=ALU.add), reads=[cltk, ftk, ctk2], writes=[ctk2])
            nl_, nltk = clst.items[1]
            P.dma("sync", nl_[:, 0:512], st_all[cp][:, 8192 + hd * 512:8192 + (hd + 1) * 512], writes=[nltk])
            P.op("act", lambda e, cp=cp, nl_=nl_: e.activation(out=nl_[:, 0:512], in_=nl_[:, 0:512], func=AF.Copy, scale=S2[:, cp, hd:hd + 1]),
                 reads=[nltk, ftk], writes=[nltk])
            P.op("dve", lambda e, cp=cp, nl_=nl_: e.scalar_tensor_tensor(out=nrow, in0=nrow, scalar=S1[:, cp, hd:hd + 1], in1=nl_[:, 0:512],
                                                                          op0=ALU.mult, op1=ALU.add), reads=[nltk, ftk, ctk2], writes=[ctk2])

    ul_prev = None
    for hd in range(BH):
        wx3, wxtk = head_setup(hd)
        wz, wztk = ws.load([wzv[:, :, 2048 + hd * 512:2048 + (hd + 1) * 512]])
        wz3 = wz.rearrange("p (c n) -> p c n", c=NC8)
        Cst, nrow, ctk2 = Csts[hd % 2], nrows[hd % 2], ctk2s[hd % 2]
        if hd == 0:
            emit_fold(0)

        def refresh_caug(full):
            if full:
                for dc in range(4):
                    P.op("act", lambda e, dc=dc: e.activation(out=Caug[:, dc, 0:512], in_=Cst[:, dc, :], func=AF.Copy),
                         reads=[ctk2], writes=[caugtk])
            P.op("dve", lambda e: e.tensor_scalar(out=nm, in0=nrow, scalar1=cf[:, L_E0:L_E0 + 1], scalar2=None, op0=ALU.mult),
                 reads=[ctk2, ctk], writes=[caugtk])
            ps, pstk = psB2.next()
            for dc in range(4):
                P.op("pe", lambda e, ps=ps, dc=dc: e.matmul(ps[:, dc * 128:(dc + 1) * 128], lhsT=nm[:, dc * 128:(dc + 1) * 128], rhs=ones_f,
                                                            start=True, stop=True), reads=[caugtk, ctk], writes=[pstk], signal=(dc == 3))
            P.op("act", lambda e, ps=ps: e.activation(out=Caug[:, :, 512:640], in_=ps.rearrange("p (a b) -> p a b", a=4), func=AF.Copy),
                 reads=[pstk], writes=[caugtk])

        refresh_caug(True)
        for tg in range(4):
            xm, xmt = front(hd, tg, wx3, wxtk)
            for mc in range(4):
                ps, pstk = psA.next()
                for c in range(NC8):
                    P.op("pe", lambda e, c=c, mc=mc, ps=ps: e.matmul(
                        ps, lhsT=wz3[:, c, mc * 128:(mc + 1) * 128], rhs=hT[:, c, 4 + tg * 512:4 + (tg + 1) * 512],
                        start=(c == 0), stop=(c == NC8 - 1)), reads=[wztk], writes=[pstk], signal=(c == NC8 - 1))
                P.op("act", lambda e, ps=ps, mc=mc: e.activation(out=zs[:, mc, :], in_=ps, func=AF.Silu), reads=[pstk], writes=[zstk])
            for j, dst, sc_ in ((0, qT, 1.0), (1, kT, KSCALE)):
                for dc in range(4):
                    ps, pstk = psA.next()
                    P.op("pe", lambda e, ps=ps, j=j, dc=dc: e.matmul(ps, lhsT=bdh[:, j, dc, :], rhs=xc[:, dc, :], start=True, stop=True),
                         reads=[hdtk, xctk], writes=[pstk])
                    P.op("act", lambda e, ps=ps, dst=dst, dc=dc, sc_=sc_: e.activation(out=dst[:, dc, :], in_=ps, func=AF.Copy, scale=sc_),
                         reads=[pstk], writes=[qktk])
            RS = {}

            def stage_pre(cl):
                nonlocal ul_prev
                ck = tg * 4 + cl
                tsl = slice(cl * 128, (cl + 1) * 128)
                gsl = slice(ck * 128, (ck + 1) * 128)
                sel = cf[:, L_SEL + hd * 128:L_SEL + (hd + 1) * 128]
                rp, rptk = psRP.next()
                for i3, row in enumerate((BETA, MM)):
                    P.op("pe", lambda e, rp=rp, i3=i3, row=row, gsl=gsl: e.matmul(
                        rp[:, i3 * 128:(i3 + 1) * 128], lhsT=sel, rhs=row[:, gsl], start=True, stop=True),
                        reads=[rtk, gtk, ctk], writes=[rptk], signal=(i3 == 1))
                bprev = mrun[:, hd:hd + 1] if ck == 0 else ul_prev[0]
                bprev_tk = ftk if ck == 0 else ul_prev[1]
                scsb, sctk = sc_rot.next()
                P.op("act", lambda e, rp=rp, scsb=scsb, bprev=bprev: e.activation(out=scsb, in_=rp[:, 0:128], func=AF.Exp, scale=-1.0, bias=bprev),
                     reads=[rptk, bprev_tk], writes=[sctk])
                emsb, emtk = em_rot.next()
                P.op("act", lambda e, rp=rp, emsb=emsb: e.activation(out=emsb, in_=rp[:, 128:256], func=AF.Exp, scale=-1.0),
                     reads=[rptk], writes=[emtk])
                ul_prev = ul_rot.next()
                P.op("act", lambda e, rp=rp, ul_prev=ul_prev: e.activation(out=ul_prev[0], in_=rp[:, 127:128], func=AF.Copy),
                     reads=[rptk], writes=[ul_prev[1]])
                ps, pstk = psB2.next()
                for mc in range(4):
                    P.op("pe", lambda e, ps=ps, mc=mc, tsl=tsl: e.matmul(
                        ps[:, mc * 128:(mc + 1) * 128], lhsT=xc[:, mc, tsl], rhs=bdh[:, 1, mc, :], start=True, stop=True),
                        reads=[xctk, hdtk], writes=[pstk], signal=(mc == 3))
                wk, wktk = wk_rot.next()
                P.op("act", lambda e, ps=ps, wk=wk, ck=ck: e.activation(out=wk, in_=ps, func=AF.Copy, scale=TMw[:, ck, hd:hd + 1]),
                     reads=[pstk, tmtk], writes=[wktk])
                ps, pstk = psB2.next()
                for mc in range(4):
                    P.op("pe", lambda e, ps=ps, mc=mc, cl=cl, xm=xm: e.matmul(
                        ps[:, mc * 128:(mc + 1) * 128], lhsT=xm[:, mc, 4 + cl * 128:4 + (cl + 1) * 128], rhs=bdh[:, 2, mc, :],
                        start=True, stop=True), reads=[xmt, hdtk], writes=[pstk], signal=(mc == 3))
                va, vatk = va_rot.next()
                P.op("act", lambda e, ps=ps, va=va: e.activation(out=va[:, 0:512], in_=ps, func=AF.Copy), reads=[pstk], writes=[vatk])
                pS_, pStk = psS3.next()
                pS = pS_[:, 256:384]
                for dc in range(4):
                    P.op("pe", lambda e, pS=pS, dc=dc, tsl=tsl: e.matmul(pS, lhsT=kT[:, dc, tsl], rhs=qT[:, dc, tsl],
                                                                          start=(dc == 0), stop=(dc == 3)),
                         reads=[qktk], writes=[pStk], signal=(dc == 3))
                dtmp, dttk = dt_rot.next()
                P.op("dve", lambda e, rp=rp, dtmp=dtmp, ck=ck: e.scalar_tensor_tensor(
                    out=dtmp, in0=rp[:, 0:128], scalar=TMa[:, ck, hd:hd + 1], in1=cf[:, L_MASKLOW:L_MASKLOW + 128],
                    op0=ALU.subtract, op1=ALU.max), reads=[rptk, tmtk, ctk], writes=[dttk])
                P.op("act", lambda e, dtmp=dtmp: e.activation(out=dtmp, in_=dtmp, func=AF.Exp, scale=-1.0), reads=[dttk], writes=[dttk])
                sd, sdtk = sd_rot.next()
                P.op("dve", lambda e, pS=pS, dtmp=dtmp, sd=sd: e.tensor_tensor(out=sd, in0=pS, in1=dtmp, op=ALU.mult),
                     reads=[pStk, dttk], writes=[sdtk])
                qs, qstk = qs_rot.next()
                P.op("dve", lambda e, scsb=scsb, qs=qs, tsl=tsl: e.tensor_tensor(
                    out=qs, in0=qT[:, :, tsl], in1=scsb.unsqueeze(1).to_broadcast([128, 4, 128]), op=ALU.mult),
                    reads=[qktk, sctk], writes=[qstk])

                RS[cl] = dict(ck=ck, tsl=tsl, wk=wk, wktk=wktk, va=va, vatk=vatk, sd=sd, sdtk=sdtk, qs=qs, qstk=qstk,
                              scsb=scsb, sctk=sctk, emsb=emsb, emtk=emtk)

            def stage_mid(cl):
                r_ = RS[cl]
                ck, tsl, wk, wktk, va, vatk, sd, sdtk, qs, qstk, scsb, sctk = (r_[k_] for k_ in (
                    "ck", "tsl", "wk", "wktk", "va", "vatk", "sd", "sdtk", "qs", "qstk", "scsb", "sctk"))
                pH, pHtk = psH.next()
                pD_, pDtk = psDS.next()
                for ec in range(5):
                    o = pH[:, ec * 128:(ec + 1) * 128] if ec < 4 else pD_[:, 0:128]
                    otk = pHtk if ec < 4 else pDtk
                    for dc in range(4):
                        P.op("pe", lambda e, o=o, ec=ec, dc=dc, qs=qs: e.matmul(
                            o, lhsT=Caug[:, dc, ec * 128:(ec + 1) * 128], rhs=qs[:, dc, :], start=(dc == 0), stop=False),
                            reads=[caugtk, qstk], writes=[otk], signal=False)
                    P.op("pe", lambda e, o=o, ec=ec, va=va, sd=sd: e.matmul(
                        o, lhsT=va[:, ec * 128:(ec + 1) * 128], rhs=sd, start=False, stop=True),
                        reads=[vatk, sdtk], writes=[otk], signal=True)

                r_.update(pH=pH, pHtk=pHtk, pD_=pD_, pDtk=pDtk)
                if dbg_stop is not None and (hd, ck) == tuple(dbg_stop):
                    P.barrier()
                    P.finish()
                    return nc, cx
                decay = scsb[:, 127:128]
                for dc in range(4):
                    ps, pstk = psB2.next()
                    P.op("pe", lambda e, ps=ps, dc=dc, wk=wk, va=va: e.matmul(ps, lhsT=wk[:, dc * 128:(dc + 1) * 128], rhs=va[:, 0:512],
                                                                                start=True, stop=True), reads=[wktk, vatk], writes=[pstk])
                    P.op("dve", lambda e, ps=ps, dc=dc, decay=decay: e.scalar_tensor_tensor(
                        out=Cst[:, dc, :], in0=Cst[:, dc, :], scalar=decay, in1=ps, op0=ALU.mult, op1=ALU.add),
                        reads=[pstk, sctk, ctk2], writes=[ctk2])
                    P.op("act", lambda e, dc=dc: e.activation(out=Caug[:, dc, 0:512], in_=Cst[:, dc, :], func=AF.Copy),
                         reads=[ctk2], writes=[caugtk])
                ps, pstk = psB2.next()
                P.op("pe", lambda e, ps=ps, wk=wk: e.matmul(ps, lhsT=ones_bf, rhs=wk, start=True, stop=True),
                     reads=[wktk, ctk], writes=[pstk])
                P.op("dve", lambda e, ps=ps, decay=decay: e.scalar_tensor_tensor(out=nrow, in0=nrow, scalar=decay, in1=ps,
                                                                                  op0=ALU.mult, op1=ALU.add),
                     reads=[pstk, sctk, ctk2], writes=[ctk2])
                refresh_caug(False)

            def stage_post(cl):
                r_ = RS[cl]
                ck, tsl, emsb, emtk, pH, pHtk, pD_, pDtk = (r_[k_] for k_ in ("ck", "tsl", "emsb", "emtk", "pH", "pHtk", "pD_", "pDtk"))
                hsq, hsqtk = hsq_rot.next()
                P.op("act", lambda e, pH=pH, hsq=hsq: e.activation(out=hsq, in_=pH, func=AF.Square), reads=[pHtk], writes=[hsqtk])
                pSS_, pSStk = psSS.next()
                pSS = pSS_[:, 0:128]
                for ec in range(4):
                    P.op("pe", lambda e, pSS=pSS, hsq=hsq, ec=ec: e.matmul(pSS, lhsT=ones_bf, rhs=hsq[:, ec * 128:(ec + 1) * 128],
                                                                            start=(ec == 0), stop=(ec == 3)),
                         reads=[hsqtk, ctk], writes=[pSStk], signal=(ec == 3))
                dd, ddtk = dd_rot.next()
                P.op("dve", lambda e, pD_=pD_, dd=dd: e.tensor_scalar(out=dd, in0=pD_[:, 0:128], scalar1=-1.0, scalar2=None, op0=ALU.mult),
                     reads=[pDtk], writes=[ddtk])
                P.op("dve", lambda e, pD_=pD_, dd=dd: e.tensor_tensor(out=dd, in0=dd, in1=pD_[:, 0:128], op=ALU.max),
                     reads=[pDtk, ddtk], writes=[ddtk])
                P.op("dve", lambda e, emsb=emsb, dd=dd: e.tensor_tensor(out=dd, in0=dd, in1=emsb, op=ALU.max),
                     reads=[emtk, ddtk], writes=[ddtk])
                P.op("dve", lambda e, dd=dd: e.scalar_tensor_tensor(out=dd, in0=dd, scalar=EPS, in1=dd, op0=ALU.mult, op1=ALU.mult),
                     reads=[ddtk], writes=[ddtk])
                rr, rrtk = rr_rot.next()
                P.op("dve", lambda e, pSS=pSS, dd=dd, rr=rr: e.scalar_tensor_tensor(out=rr, in0=pSS, scalar=1.0 / DH, in1=dd,
                                                                                     op0=ALU.mult, op1=ALU.add),
                     reads=[pSStk, ddtk], writes=[rrtk])
                P.op("act", lambda e, rr=rr: e.activation(out=rr, in_=rr, func=AF.Sqrt), reads=[rrtk], writes=[rrtk])
                P.op("dve", lambda e, rr=rr: e.reciprocal(out=rr, in_=rr), reads=[rrtk], writes=[rrtk])
                for ec in range(4):
                    ch = hd * 4 + ec
                    tt, tttk = tt_rot.next()
                    P.op("dve", lambda e, pH=pH, ec=ec, ch=ch, rr=rr, tt=tt: e.scalar_tensor_tensor(
                        out=tt, in0=pH[:, ec * 128:(ec + 1) * 128], scalar=cf[:, L_HGAIN + ch:L_HGAIN + ch + 1], in1=rr,
                        op0=ALU.mult, op1=ALU.mult), reads=[pHtk, rrtk, ctk], writes=[tttk])
                    P.op("dve", lambda e, ec=ec, ch=ch, tt=tt, tsl=tsl: e.scalar_tensor_tensor(
                        out=tt, in0=xc[:, ec, tsl], scalar=cf[:, L_SKIP + ch:L_SKIP + ch + 1], in1=tt,
                        op0=ALU.mult, op1=ALU.add), reads=[xctk, tttk, ctk], writes=[tttk])
                    P.op("dve", lambda e, ec=ec, tt=tt, tsl=tsl: e.tensor_tensor(out=yb[:, ec, tsl], in0=tt, in1=zs[:, ec, tsl], op=ALU.mult),
                         reads=[tttk, zstk], writes=[ytk])


            stage_pre(0)
            stage_mid(0)
            for cl in range(1, 4):
                stage_pre(cl)
                stage_post(cl - 1)
                stage_mid(cl)
            stage_post(3)

            if tg == 1 and hd + 1 < BH:
                emit_fold(hd + 1)
            P.dma("sync", yview[:, hd * 4:(hd + 1) * 4, tg * 512:(tg + 1) * 512], yb, reads=[ytk])
    P.barrier()
    cx.release(base_mark)

    X = TOP.rearrange("p (c n) -> p c n", c=NC8)
    Xtk = [[Tk() for _ in range(4)] for _ in range(NC8)]
    for c in range(NC8):
        for tg in range(4):
            P.dma("sync", X[:, c, tg * 512:(tg + 1) * 512], x1T[c * 128:(c + 1) * 128, tg * 512:(tg + 1) * 512], writes=[Xtk[c][tg]])
    mk = cx.mark()
    wdn = cx.sb(None, [128, 16, D], BF16, "wdn")
    wdtk = Tk()
    wdv = wdown.rearrange("(c p) n -> p c n", p=128)
    wstg3 = Rot([cx.sb(None, [128, 4, D], F32, "wstg3") for _ in range(2)])
    for q4 in range(4):
        stg_, stk_ = wstg3.next()
        P.dma("sync", stg_, wdv[:, q4 * 4:(q4 + 1) * 4, :], writes=[stk_])
        P.op("act", lambda e, stg_=stg_, q4=q4: e.activation(out=wdn[:, q4 * 4:(q4 + 1) * 4, :], in_=stg_, func=AF.Copy),
             reads=[stk_], writes=[wdtk])
    yts = [cx.sb(None, [128, 16, 512], BF16, "yt") for _ in range(2)]
    yttk = [Tk(), Tk()]
    psA4 = Rot(bk[0:4])
    for tg in range(4):
        yt, ytt = yts[tg % 2], yttk[tg % 2]
        P.dma("sync", yt, yview[:, :, tg * 512:(tg + 1) * 512], writes=[ytt])
        sl = slice(tg * 512, (tg + 1) * 512)
        for oc in range(NC8):
            ps, pstk = psA4.next()
            for mc in range(16):
                P.op("pe", lambda e, ps=ps, mc=mc, oc=oc, yt=yt: e.matmul(ps, lhsT=wdn[:, mc, oc * 128:(oc + 1) * 128], rhs=yt[:, mc, :],
                                                                           start=(mc == 0), stop=(mc == 15)),
                     reads=[wdtk, ytt], writes=[pstk], signal=(mc == 15))
            P.op("dve", lambda e, ps=ps, oc=oc, sl=sl: e.tensor_tensor(out=X[:, oc, sl], in0=ps, in1=X[:, oc, sl], op=ALU.add),
                 reads=[pstk, Xtk[oc][tg]], writes=[Xtk[oc][tg]])
    P.barrier()
    cx.release(mk)
    if debug:
        emit_store(cx, X, Xtk, dbg_a)
    emit_mlp(cx, X, Xtk, cf[:, L_MLPN:L_MLPN + 8], ctk, w1, w2, ones_bf)
    emit_ple(cx, X, Xtk, cf[:, L_PLEN:L_PLEN + 8], ctk, wg, wp, pT, ones_bf)
    emit_store(cx, X, Xtk, out)
    P.finish()
    return nc, cx


_CACHE = {}


def _prog(key, builder):
    return builder()


def kernel(**inputs):
    inputs = {k: np.asarray(v) for k, v in inputs.items()}
    cores = list(range(NCORES))
    nc, _ = build_layer0()
    in_maps = [layer0_inputs(inputs, c) for c in cores]
    res = run_bass_kernel_spmd(nc, in_maps, core_ids=cores)
    x1T = np.concatenate([r["xout"] for r in res.results], axis=1)
    nc, _ = build_layer1("B")
    in_maps = [layer1_inputs(inputs, c, x1T, "B") for c in cores]
    res = run_bass_kernel_spmd(nc, in_maps, core_ids=cores)
    st_all = np.stack([res.results[c]["st_out"] for c in range(7)])
    g_rows = [res.results[c]["g_out"] for c in cores]
    nc, _ = build_layer1("C")
    in_maps = [layer1_inputs(inputs, c, x1T, "C", st_all, g_rows[c]) for c in cores]
    res = run_bass_kernel_spmd(nc, in_maps, core_ids=cores)
    outT = np.concatenate([r["xout"] for r in res.results], axis=1)
    return np.ascontiguousarray(outT.T)[None].astype(np.float32)
```

```python
import numpy as np
import concourse.bass as bass
import concourse.mybir as mybir
from concourse.bass_utils import run_bass_kernel_spmd

F32 = mybir.dt.float32
BF16 = mybir.dt.bfloat16
AF = mybir.ActivationFunctionType
ALU = mybir.AluOpType
AX = mybir.AxisListType

NCORES = 8
S = 16384
D = 1024
NT = S // NCORES
NC8 = D // 128
EPS = 1e-6
BIG = 30000.0
A_GROUPS = ((128, 1), (512, 4), (2048, 16))
NDMA = 24
SB_F32 = 51968


class Tk:
    __slots__ = ("w", "r")

    def __init__(self):
        self.w = {}
        self.r = {}


class Prog:
    def __init__(self, nc):
        self.nc = nc
        self.eng = {"act": nc.scalar, "dve": nc.vector, "pool": nc.gpsimd, "pe": nc.tensor, "sync": nc.sync}
        self.sem = {e: nc.alloc_semaphore("s_" + e) for e in ("act", "dve", "pool", "pe")}
        self.cnt = {e: 0 for e in ("act", "dve", "pool", "pe")}
        self.seen = {e: {} for e in self.eng}
        self.dsem = [nc.alloc_semaphore("s_dma%d" % i) for i in range(NDMA)]
        self.dcnt = [0] * NDMA
        self.dnext = 0
        self.nins = {e: 0 for e in self.eng}

    def _semof(self, src):
        if isinstance(src, tuple):
            return self.dsem[src[1]]
        return self.sem[src]

    def _deps(self, e, reads, writes, allraw=False):
        deps = {}

        def add(src, n, raw):
            if src == e and not allraw:
                if e == "pe" or not raw:
                    return
            if deps.get(src, 0) < n:
                deps[src] = n

        for t in reads:
            for src, n in t.w.items():
                add(src, n, True)
        for t in writes:
            for src, n in t.w.items():
                add(src, n, False)
            for src, n in t.r.items():
                add(src, n, False)
        return deps

    def _wait(self, e, deps):
        eng = self.eng[e]
        seen = self.seen[e]
        for src, n in deps.items():
            if seen.get(src, 0) >= n:
                continue
            seen[src] = n
            eng.wait_ge(self._semof(src), n)
            self.nins[e] += 1

    def op(self, e, fn, reads=(), writes=(), signal=True):
        self._wait(e, self._deps(e, reads, writes))
        ins = fn(self.eng[e])
        self.nins[e] += 1
        n = self.cnt[e] + 1
        if signal:
            ins.then_inc(self.sem[e], 1)
            self.cnt[e] = n
        for t in reads:
            if t.r.get(e, 0) < n:
                t.r[e] = n
        for t in writes:
            if t.w.get(e, 0) < n:
                t.w[e] = n
        return ins

    def dma(self, q, out, in_, reads=(), writes=()):
        k = self.dnext
        self.dnext = (k + 1) % NDMA
        src = ("dma", k)
        deps = self._deps(q, reads, writes, allraw=True)
        if self.dcnt[k] > 0:
            deps[src] = max(deps.get(src, 0), self.dcnt[k])
        self._wait(q, deps)
        ins = self.eng[q].dma_start(out=out, in_=in_)
        self.nins[q] += 1
        n = self.dcnt[k] + 16
        ins.then_inc(self.dsem[k], 16)
        self.dcnt[k] = n
        for t in reads:
            t.r[src] = n
        for t in writes:
            t.w[src] = n

    def barrier(self):
        for e in self.eng:
            deps = {}
            for s2 in self.cnt:
                if s2 != e and self.cnt[s2] > 0:
                    deps[s2] = self.cnt[s2]
            for k in range(NDMA):
                if self.dcnt[k] > 0:
                    deps[("dma", k)] = self.dcnt[k]
            self._wait(e, deps)

    def finish(self):
        deps = {}
        for k in range(NDMA):
            if self.dcnt[k] > 0:
                deps[("dma", k)] = self.dcnt[k]
        self._wait("sync", deps)


class Rot:
    def __init__(self, aps):
        self.items = [a if isinstance(a, tuple) else (a, Tk()) for a in aps]
        self.i = 0

    def next(self):
        it = self.items[self.i]
        self.i = (self.i + 1) % len(self.items)
        return it


class Ctx:
    def __init__(self, nc):
        self.nc = nc
        self.P = Prog(nc)
        self.banks = [nc.alloc_psum_tensor("psb%d" % i, [128, 512], F32).ap() for i in range(8)]
        self.nalloc = 0

        self.big = nc.alloc_sbuf_tensor("big", [128, SB_F32], F32).ap()
        self.top = 0

    def sb(self, stack, shape, dt, name=None):
        esz = 2 if dt == BF16 else 4
        n = int(np.prod(shape[1:]))
        nbytes = (n * esz + 63) // 64 * 64
        off = self.top
        assert off + nbytes <= SB_F32 * 4, ("SBUF overflow", name, off, nbytes)
        self.top = off + nbytes
        self.log = getattr(self, 'log', [])
        self.log.append((name, off, nbytes))
        ap = self.big[:, off // 4:(off + nbytes) // 4]
        if dt == BF16:
            ap = ap.bitcast(BF16)
        ap = ap[:, 0:n]
        if len(shape) == 3:
            ap = ap.rearrange("p (a b) -> p a b", a=shape[1])
        elif len(shape) == 4:
            ap = ap.rearrange("p (a b c) -> p a b c", a=shape[1], b=shape[2])
        return ap

    def mark(self):
        return self.top

    def release(self, m):
        self.top = m


def load_consts(cx, stack, cst_ap, ncols):
    P = cx.P
    cf = cx.sb(stack, [128, ncols], F32, "cstf")
    cb = cx.sb(stack, [128, C_END_BF], BF16, "cstb")
    tk = Tk()
    P.dma("sync", cf, cst_ap, writes=[tk])
    P.op("dve", lambda e: e.tensor_copy(out=cb, in_=cf[:, 0:C_END_BF]), reads=[tk], writes=[tk])
    return cf, cb, tk


class WStream:
    def __init__(self, cx, stack, nelem, nstage=2, nslot=2):
        self.cx = cx
        self.nelem = nelem
        self.stage = Rot([cx.sb(stack, [128, nelem], F32, "wstg") for _ in range(nstage)])
        self.slots = Rot([cx.sb(stack, [128, nelem], BF16, "wbf") for _ in range(nslot)])

    def load(self, views):
        P = self.cx.P
        stg, stk = self.stage.next()
        wb, wtk = self.slots.next()
        off = 0
        for v in views:
            shp = v.shape
            n = int(np.prod(shp[1:]))
            dst = stg[:, off:off + n]
            if len(shp) == 3:
                dst = dst.rearrange("p (a b) -> p a b", a=shp[1])
            P.dma("sync", dst, v, writes=[stk])
            off += n
        assert off <= self.nelem
        P.op("pool", lambda e: e.tensor_copy(out=wb[:, 0:off], in_=stg[:, 0:off]), reads=[stk], writes=[wtk])
        return wb, wtk


def rms_stats(cx, xs, n, sq_rot, ps_ap, ps_tk, rstd, rstd_tk, ones_bf, ctk, inv_dim):
    P = cx.P
    nx = len(xs)
    for c, (xa, xt) in enumerate(xs):
        sq, sqt = sq_rot.next()
        P.op("act", lambda e, xa=xa, sq=sq: e.activation(out=sq[:, 0:n], in_=xa, func=AF.Square), reads=[xt], writes=[sqt])
        P.op("pe", lambda e, sq=sq, c=c: e.matmul(ps_ap[:, 0:n], lhsT=ones_bf, rhs=sq[:, 0:n], start=(c == 0), stop=(c == nx - 1)),
             reads=[sqt, ctk], writes=[ps_tk])
    P.op("act", lambda e: e.activation(out=rstd[:, 0:n], in_=ps_ap[:, 0:n], func=AF.Sqrt, bias=cx.eps_col, scale=inv_dim),
         reads=[ps_tk, ctk], writes=[rstd_tk])
    P.op("dve", lambda e: e.reciprocal(out=rstd[:, 0:n], in_=rstd[:, 0:n]), reads=[rstd_tk], writes=[rstd_tk])


C_ID, C_ONES, C_BONES, C_DM, C_HONES = 0, 128, 256, 384, 640
C_OZ = 704
C_HZ = 960
C_END_BF = 1216
C_EPS = 1216
C_GAINS = 1217
G0_ANORM = C_GAINS
G0_QG = G0_ANORM + 8
G0_KG = G0_QG + 3
G0_MLPN = G0_KG + 3
G0_PLEN = G0_MLPN + 8
G0_END = G0_PLEN + 8


def base_consts(core, ncols):
    c = np.zeros((128, ncols), np.float32)
    c[:, C_ID:C_ID + 128] = np.eye(128, dtype=np.float32)
    c[:, C_ONES:C_ONES + 128] = 1.0
    c[0:64, C_BONES:C_BONES + 64] = 1.0
    c[64:128, C_BONES + 64:C_BONES + 128] = 1.0
    kk = np.arange(128)[:, None]
    a = np.arange(128)[None, :]
    diag = np.where(kk <= a, a - kk, BIG)
    prev = np.where(kk >= a, 128 + a - kk, BIG)
    c[:, C_DM:C_DM + 128] = diag
    c[:, C_DM + 128:C_DM + 256] = prev
    hv = 0.0 if core == 0 else 1.0
    c[:, C_HONES:C_HONES + 64] = hv
    c[:, C_OZ:C_OZ + 64] = 1.0
    c[:, C_OZ + 128 + 64:C_OZ + 256] = 1.0
    c[:, C_HZ:C_HZ + 64] = hv
    c[:, C_HZ + 128 + 64:C_HZ + 256] = hv
    c[:, C_EPS] = EPS
    return c


def col_layout(v):
    v = np.asarray(v, np.float32).reshape(-1, 128)
    return np.ascontiguousarray(v.T)


def emit_norm_resident(cx, X, Xtk, gcol, ctk, hT, hTtk, sq_rot, rstd_rot, ps_rot, ones_bf):
    P = cx.P
    for tg in range(NT // 512):
        sl = slice(tg * 512, (tg + 1) * 512)
        xs = [(X[:, c, sl], Xtk[c][tg]) for c in range(NC8)]
        ps_ap, ps_tk = ps_rot.next()
        rstd, rtk = rstd_rot.next()
        rms_stats(cx, xs, 512, sq_rot, ps_ap, ps_tk, rstd, rtk, ones_bf, ctk, 1.0 / D)
        for c in range(NC8):
            P.op("dve", lambda e, c=c, sl=sl, rstd=rstd: e.scalar_tensor_tensor(
                out=hT[:, c, sl], in0=X[:, c, sl], scalar=gcol[:, c:c + 1], in1=rstd[:, 0:512],
                op0=ALU.mult, op1=ALU.mult), reads=[Xtk[c][tg], rtk, ctk], writes=[hTtk[c][tg]])


def emit_mlp(cx, X, Xtk, gcol, ctk, w1, w2, ones_bf):
    P = cx.P
    mk = cx.mark()
    st = None
    hT = cx.sb(st, [128, NC8, NT], BF16, "mlp_hT")
    hTtk = [[Tk() for _ in range(4)] for _ in range(NC8)]
    sq_rot = Rot([cx.sb(st, [128, 512], BF16, "sq") for _ in range(4)])
    rstd_rot = Rot([cx.sb(st, [128, 512], F32, "rstd") for _ in range(2)])
    ps_stat = Rot([cx.banks[7]])
    emit_norm_resident(cx, X, Xtk, gcol, ctk, hT, hTtk, sq_rot, rstd_rot, ps_stat, ones_bf)
    ws = WStream(cx, st, 4096, nstage=2, nslot=2)
    hids = [cx.sb(st, [128, 4, NT], BF16, "hid") for _ in range(2)]
    hid_tks = [[[Tk() for _ in range(4)] for _ in range(4)] for _ in range(2)]
    tmp_rot = Rot([cx.sb(st, [128, 512], F32, "rl") for _ in range(3)])
    psA = Rot(cx.banks[0:4])
    psB = Rot(cx.banks[4:7])
    w1v = w1.rearrange("(c p) n -> p c n", p=128)
    w2v = w2.rearrange("(c p) n -> p c n", p=128)
    NHB = 8
    for hb in range(NHB):
        hid = hids[hb % 2]
        htk = hid_tks[hb % 2]
        wa, watk = ws.load([w1v[:, :, hb * 512:(hb + 1) * 512]])
        wa3 = wa.rearrange("p (c n) -> p c n", c=NC8)
        for hc in range(4):
            for tg in range(4):
                sl = slice(tg * 512, (tg + 1) * 512)
                ps, pstk = psA.next()
                for c in range(NC8):
                    P.op("pe", lambda e, c=c, hc=hc, sl=sl, ps=ps, wa3=wa3: e.matmul(
                        ps, lhsT=wa3[:, c, hc * 128:(hc + 1) * 128], rhs=hT[:, c, sl],
                        start=(c == 0), stop=(c == NC8 - 1)),
                        reads=[watk, hTtk[c][tg]], writes=[pstk], signal=(c == NC8 - 1))
                tmp, ttk = tmp_rot.next()
                P.op("act", lambda e, ps=ps, tmp=tmp: e.activation(out=tmp, in_=ps, func=AF.Square),
                     reads=[pstk], writes=[ttk])
                P.op("dve", lambda e, ps=ps, tmp=tmp, hc=hc, sl=sl, hid=hid: e.scalar_tensor_tensor(
                    out=hid[:, hc, sl], in0=ps, scalar=0.0, in1=tmp, op0=ALU.is_gt, op1=ALU.mult),
                    reads=[pstk, ttk], writes=[htk[hc][tg]])
        wb, wbtk = ws.load([w2v[:, hb * 4:(hb + 1) * 4, :]])
        wb3 = wb.rearrange("p (c n) -> p c n", c=4)
        for oc in range(NC8):
            for tg in range(4):
                sl = slice(tg * 512, (tg + 1) * 512)
                ps, pstk = psB.next()
                for hc in range(4):
                    P.op("pe", lambda e, hc=hc, oc=oc, sl=sl, ps=ps, hid=hid, wb3=wb3: e.matmul(
                        ps, lhsT=wb3[:, hc, oc * 128:(oc + 1) * 128], rhs=hid[:, hc, sl],
                        start=(hc == 0), stop=(hc == 3)),
                        reads=[wbtk, htk[hc][tg]], writes=[pstk], signal=(hc == 3))
                P.op("dve", lambda e, oc=oc, sl=sl, ps=ps: e.tensor_tensor(
                    out=X[:, oc, sl], in0=ps, in1=X[:, oc, sl], op=ALU.add),
                    reads=[pstk, Xtk[oc][tg]], writes=[Xtk[oc][tg]])
    P.barrier()
    cx.release(mk)


def emit_ple(cx, X, Xtk, gcol, ctk, wg, wp, pT_dram, ones_bf):
    P = cx.P
    mk = cx.mark()
    st = None
    hT = cx.sb(st, [128, NC8, NT], BF16, "ple_hT")
    hTtk = [[Tk() for _ in range(4)] for _ in range(NC8)]
    sq_rot = Rot([cx.sb(st, [128, 512], BF16, "sq") for _ in range(4)])
    rstd_rot = Rot([cx.sb(st, [128, 512], F32, "rstd") for _ in range(2)])
    ps_stat = Rot([cx.banks[7]])
    emit_norm_resident(cx, X, Xtk, gcol, ctk, hT, hTtk, sq_rot, rstd_rot, ps_stat, ones_bf)
    ws = WStream(cx, st, 4096, nstage=2, nslot=3)
    pst = cx.sb(st, [128, 2, NT], F32, "pstg")
    pb = cx.sb(st, [128, 2, NT], BF16, "pbf")
    ptk = Tk()
    P.dma("sync", pst, pT_dram.rearrange("(c p) n -> p c n", p=128), writes=[ptk])
    P.op("pool", lambda e: e.tensor_copy(out=pb, in_=pst), reads=[ptk], writes=[ptk])
    wpb, wptk = ws.load([wp.rearrange("(c p) n -> p c n", p=128)])
    wp3 = wpb[:, 0:2048].rearrange("p (c n) -> p c n", c=2)
    gt_rot = Rot([cx.sb(st, [128, 512], F32, "gt") for _ in range(3)])
    psA = Rot(cx.banks[0:3])
    psB = Rot(cx.banks[3:6])
    wgv = wg.rearrange("(c p) n -> p c n", p=128)
    for half in range(2):
        wa, watk = ws.load([wgv[:, :, half * 512:(half + 1) * 512]])
        wa3 = wa.rearrange("p (c n) -> p c n", c=NC8)
        for o4 in range(4):
            oc = half * 4 + o4
            for tg in range(4):
                sl = slice(tg * 512, (tg + 1) * 512)
                ps, pstk = psA.next()
                for c in range(NC8):
                    P.op("pe", lambda e, c=c, o4=o4, sl=sl, ps=ps, wa3=wa3: e.matmul(
                        ps, lhsT=wa3[:, c, o4 * 128:(o4 + 1) * 128], rhs=hT[:, c, sl],
                        start=(c == 0), stop=(c == NC8 - 1)),
                        reads=[watk, hTtk[c][tg]], writes=[pstk], signal=(c == NC8 - 1))
                ps2, ps2tk = psB.next()
                for kc in range(2):
                    P.op("pe", lambda e, kc=kc, oc=oc, sl=sl, ps2=ps2: e.matmul(
                        ps2, lhsT=wp3[:, kc, oc * 128:(oc + 1) * 128], rhs=pb[:, kc, sl],
                        start=(kc == 0), stop=(kc == 1)),
                        reads=[wptk, ptk], writes=[ps2tk], signal=(kc == 1))
                gt, gtk = gt_rot.next()
                P.op("act", lambda e, ps=ps, gt=gt: e.activation(out=gt, in_=ps, func=AF.Sigmoid),
                     reads=[pstk], writes=[gtk])
                P.op("dve", lambda e, ps2=ps2, gt=gt: e.tensor_tensor(out=gt, in0=ps2, in1=gt, op=ALU.mult),
                     reads=[ps2tk, gtk], writes=[gtk])
                P.op("dve", lambda e, oc=oc, sl=sl, gt=gt: e.tensor_tensor(
                    out=X[:, oc, sl], in0=gt, in1=X[:, oc, sl], op=ALU.add),
                    reads=[gtk, Xtk[oc][tg]], writes=[Xtk[oc][tg]])
    P.barrier()
    cx.release(mk)


def alibi_slope(h):
    return 2.0 ** (-8.0 * (h + 1) / 16)


def sslice(start, count, step):
    return slice(start, start + (count - 1) * step + 1, step)


def emit_attention(cx, xT_ext, wqkv, wo, cf, cb, ctk, TOP, R1, lvl=9, hps=8):
    P = cx.P
    st = None
    ones_bf = cb[:, C_ONES:C_ONES + 128]
    bones = cb[:, C_BONES:C_BONES + 128]
    Dm = cf[:, C_DM:C_DM + 256]
    hT = TOP.bitcast(BF16).rearrange("p (c n) -> p c n", c=NC8)
    mk = cx.mark()
    sq_rot = Rot([cx.sb(st, [128, 512], BF16, "sq") for _ in range(2)])
    rstd_rot = Rot([cx.sb(st, [128, 512], F32, "rstd") for _ in range(2)])
    xstg = [R1[:, i * 4096:(i + 1) * 4096].rearrange("p (c n) -> p c n", c=NC8) for i in range(2)]
    xstk = [Tk(), Tk()]
    ps_stat = Rot([cx.banks[7], cx.banks[6]])
    httk = Tk()
    gcol = cf[:, G0_ANORM:G0_ANORM + 8]
    for tg in range(8 if lvl >= 1 else 0):
        xa = xstg[tg % 2]
        xt = xstk[tg % 2]
        P.dma("sync", xa, xT_ext[:, tg * 512:(tg + 1) * 512].rearrange("(c p) n -> p c n", p=128), writes=[xt])
        xs = [(xa[:, c, :], xt) for c in range(NC8)]
        ps_ap, ps_tk = ps_stat.next()
        rstd, rtk = rstd_rot.next()
        rms_stats(cx, xs, 512, sq_rot, ps_ap, ps_tk, rstd, rtk, ones_bf, ctk, 1.0 / D)
        for c in range(NC8):
            P.op("dve", lambda e, c=c, tg=tg, xa=xa, rstd=rstd: e.scalar_tensor_tensor(
                out=hT[:, c, tg * 512:(tg + 1) * 512], in0=xa[:, c, :], scalar=gcol[:, c:c + 1], in1=rstd[:, 0:512],
                op0=ALU.mult, op1=ALU.mult), reads=[xt, rtk, ctk], writes=[httk])
    P.op("dve", lambda e: e.tensor_scalar(out=cf[:, G0_QG:G0_QG + 3], in0=cf[:, G0_QG:G0_QG + 3], scalar1=0.125,
                                          scalar2=None, op0=ALU.mult), reads=[ctk], writes=[ctk])
    P.barrier()
    ACC = R1[:, 0:4096].rearrange("p (a n) -> p a n", a=2)
    oT = R1[:, 4096:12288].bitcast(BF16).rearrange("p (c n) -> p c n", c=NC8)
    acctk = Tk()
    ottk = Tk()
    ws = WStream(cx, st, 3072, nstage=1, nslot=2)
    QTz = [cx.sb(st, [128, NT], BF16, "QTz") for _ in range(2)]
    qtk = Tk()
    KT_rot = Rot([cx.sb(st, [128, 2 * NT], BF16, "KT") for _ in range(2)])
    Vz = [cx.sb(st, [128, 32, 128], BF16, "Vz") for _ in range(2)]
    vtk = Tk()
    for e2 in (0, 1):
        P.op("pool", lambda e, e2=e2: e.memset(QTz[e2], 0.0), writes=[qtk])
        P.op("pool", lambda e, e2=e2: e.memset(Vz[e2], 0.0), writes=[vtk])
    onesz = [cb[:, C_OZ:C_OZ + 128], cb[:, C_OZ + 128:C_OZ + 256]]
    honesz = [cb[:, C_HZ:C_HZ + 128], cb[:, C_HZ + 128:C_HZ + 256]]
    tmp_rot = Rot([cx.sb(st, [128, 256], F32, "stmp") for _ in range(4)])
    pt_rots = [Rot([cx.sb(st, [128, 256], BF16, "PT") for _ in range(6)]) for _ in range(2)]
    bk = [(cx.banks[i], Tk()) for i in range(8)]

    def half(i):
        return (bk[i][0][:, 0:256], bk[i][1])

    psQ = Rot(bk[0:2])
    psS = Rot([bk[2]])
    psV = Rot([bk[3]])
    psST0 = Rot([half(4), half(0)])
    psST1 = Rot([half(5), half(1)])
    psND = Rot([half(6), half(7), half(2), half(3)])
    wq_view = wqkv.rearrange("(c p) n -> p c n", p=128)

    def perm(ap2d, d):
        if d == 1:
            return ap2d
        return ap2d.rearrange("p (u r) -> p r u", r=d)

    def proj_piece(w3, wtk, j, e0, n, gain_col, out_buf, out_tk, d, Lx, u0):
        ps, pstk = psQ.next()
        for c in range(NC8):
            P.op("pe", lambda e, c=c, ps=ps: e.matmul(ps[:, 0:n], lhsT=w3[:, j, c, :], rhs=hT[:, c, e0:e0 + n],
                                                      start=(c == 0), stop=(c == NC8 - 1)),
                 reads=[wtk], writes=[pstk], signal=(c == NC8 - 1))
        ps2, ps2tk = psS.next()
        rstd, rtk = rstd_rot.next()
        rms_stats(cx, [(ps[:, 0:n], pstk)], n, sq_rot, ps2, ps2tk, rstd, rtk, bones, ctk, 1.0 / 64)
        outs = out_buf if isinstance(out_buf, list) else [(slice(0, 128), out_buf)]
        for (rows, ob) in outs:
            if d == 1:
                o = ob[rows, u0:u0 + n]
            else:
                o = ob[rows, 0:d * Lx].rearrange("p (r u) -> p r u", r=d)[:, :, u0:u0 + n // d]
            P.op("dve", lambda e, ps=ps, rstd=rstd, o=o, rows=rows: e.scalar_tensor_tensor(
                out=o, in0=perm(ps[rows, 0:n], d), scalar=gain_col[rows, :], in1=perm(rstd[rows, 0:n], d),
                op0=ALU.mult, op1=ALU.mult),
                reads=[pstk, rtk, ctk], writes=[out_tk])

    if lvl < 2:
        hps = 0
        P.op('dve', lambda e: e.memset(R1, 0.0), writes=[ottk])
    for hp in range(hps):
        for g, (W, d) in enumerate(A_GROUPS):
            L = NT // d
            Lk = (W + NT) // d
            nb = Lk // 128
            e_start = NT - W
            base = g * 3072 + hp * 128
            wb, wtk = ws.load([wq_view[:, :, base + j * 1024: base + j * 1024 + 128] for j in range(3)])
            w3 = wb[:, 0:3072].rearrange("p (j c n) -> p j c n", j=3, c=NC8)
            KT, ktk = KT_rot.next()
            for tg in range(4):
                proj_piece(w3, wtk, 0, NT + tg * 512, 512, cf[:, G0_QG + g:G0_QG + g + 1],
                           [(slice(0, 64), QTz[0]), (slice(64, 128), QTz[1])], qtk, d, L, tg * 512 // d)
            pieces = []
            if W < 512:
                pieces.append((e_start, W))
                e = NT
            else:
                e = e_start
            while e < 2 * NT:
                pieces.append((e, 512))
                e += 512
            for (e0, n) in pieces:
                proj_piece(w3, wtk, 1, e0, n, cf[:, G0_KG + g:G0_KG + g + 1], KT, ktk, d, Lk, (e0 - e_start) // d)
            nkb = d * nb if lvl >= 3 else 0
            kb = 0
            while kb < nkb:
                nblk = min(4, nkb - kb)
                psv, psvtk = psV.next()
                for b in range(nblk):
                    r, jb = divmod(kb + b, nb)
                    e_first = e_start + d * 128 * jb + r
                    for c in range(NC8):
                        P.op("pe", lambda e, c=c, b=b, e_first=e_first, psv=psv: e.matmul(
                            psv[:, b * 128:(b + 1) * 128], lhsT=hT[:, c, sslice(e_first, 128, d)], rhs=w3[:, 2, c, :],
                            start=(c == 0), stop=(c == NC8 - 1)),
                            reads=[wtk], writes=[psvtk], signal=(c == NC8 - 1 and b == nblk - 1))
                for e2 in (0, 1):
                    cs = slice(64 * e2, 64 * e2 + 64)
                    P.op("act", lambda e, kb=kb, nblk=nblk, psv=psv, e2=e2, cs=cs: e.activation(
                        out=Vz[e2][:, kb:kb + nblk, cs],
                        in_=psv[:, 0:nblk * 128].rearrange("p (b n) -> p b n", b=nblk)[:, :, cs], func=AF.Copy),
                        reads=[psvtk], writes=[vtk])
                kb += nblk
            PTs = {}

            def score_task(r, jb):
                lo = 128 if jb == 0 else 0
                hi = 128 if jb == nb - 1 else 256
                qb0 = jb if jb == 0 else jb - 1
                q_off = r * L + 128 * qb0
                sTs = [psST0.next(), psST1.next()]
                for e2 in (0, 1):
                    sT, sTtk = sTs[e2]
                    P.op("pe", lambda e, sT=sT, e2=e2: e.matmul(
                        sT[:, lo:hi], lhsT=KT[:, r * Lk + 128 * jb: r * Lk + 128 * jb + 128],
                        rhs=QTz[e2][:, q_off:q_off + (hi - lo)], start=True, stop=True),
                        reads=[ktk, qtk], writes=[sTtk])
                for e2 in (0, 1):
                    sig = alibi_slope(2 * hp + e2) * d
                    sT, sTtk = sTs[e2]
                    tmp, tmtk = tmp_rot.next()
                    P.op("dve", lambda e, sT=sT, tmp=tmp, sig=sig: e.scalar_tensor_tensor(
                        out=tmp[:, lo:hi], in0=Dm[:, lo:hi], scalar=-sig, in1=sT[:, lo:hi],
                        op0=ALU.mult, op1=ALU.add),
                        reads=[sTtk, ctk], writes=[tmtk])
                    pt, pttk = pt_rots[e2].next()
                    P.op("act", lambda e, tmp=tmp, pt=pt: e.activation(
                        out=pt[:, lo:hi], in_=tmp[:, lo:hi], func=AF.Exp), reads=[tmtk], writes=[pttk])
                    PTs[(e2, r, jb)] = (pt, pttk)

            def pv_task(r, j):
                jb = j + 1
                nd, ndtk = psND.next()
                kbp = r * nb + j
                kbd = r * nb + jb
                for part in (0, 1):
                    co = slice(128 * part, 128 * part + 128)
                    for e2 in (0, 1):
                        ptp, ptptk = PTs[(e2, r, j)]
                        ptd, ptdtk = PTs[(e2, r, jb)]
                        if part == 0:
                            lp, ld = Vz[e2][:, kbp, :], Vz[e2][:, kbd, :]
                        else:
                            lp, ld = (honesz[e2] if j == 0 else onesz[e2]), onesz[e2]
                        P.op("pe", lambda e, nd=nd, co=co, lp=lp, ptp=ptp, e2=e2: e.matmul(
                            nd[:, co], lhsT=lp, rhs=ptp[:, 128:256], start=(e2 == 0), stop=False),
                            reads=[vtk, ctk, ptptk], writes=[ndtk], signal=False)
                        P.op("pe", lambda e, nd=nd, co=co, ld=ld, ptd=ptd, e2=e2: e.matmul(
                            nd[:, co], lhsT=ld, rhs=ptd[:, 0:128], start=False, stop=(e2 == 1)),
                            reads=[vtk, ctk, ptdtk], writes=[ndtk], signal=(part == 1 and e2 == 1))
                t0 = r + d * 128 * j
                accv = ACC[:, :, sslice(t0, 128, d)]
                ndv = nd.rearrange("p (a n) -> p a n", a=2)
                if g == 0:
                    P.op("act", lambda e, accv=accv, ndv=ndv: e.activation(out=accv, in_=ndv, func=AF.Copy),
                         reads=[ndtk], writes=[acctk])
                else:
                    P.op("dve", lambda e, accv=accv, ndv=ndv: e.tensor_tensor(out=accv, in0=ndv, in1=accv, op=ALU.add),
                         reads=[ndtk, acctk], writes=[acctk])

            LA = 3
            pending = []
            tasks = [(r, jb) for r in range(d if lvl >= 4 else 0) for jb in range(nb)]
            for i, (r, jb) in enumerate(tasks):
                score_task(r, jb)
                if jb >= 1:
                    pending.append((i, r, jb - 1))
                while pending and pending[0][0] <= i - LA:
                    _, r_, j_ = pending.pop(0)
                    pv_task(r_, j_)
            for (_, r_, j_) in pending:
                pv_task(r_, j_)
        P.op("dve", lambda e: e.reciprocal(out=ACC[:, 1, :], in_=ACC[:, 1, :]), reads=[acctk], writes=[acctk])
        P.op("dve", lambda e, hp=hp: e.tensor_tensor(out=oT[:, hp, :], in0=ACC[:, 0, :], in1=ACC[:, 1, :], op=ALU.mult),
             reads=[acctk], writes=[ottk])
    P.barrier()
    cx.release(mk)
    mk = cx.mark()
    X = TOP.rearrange("p (c n) -> p c n", c=NC8)
    Xtk = [[Tk() for _ in range(4)] for _ in range(NC8)]
    for c in range(NC8):
        for tg in range(4):
            P.dma("sync", X[:, c, tg * 512:(tg + 1) * 512], xT_ext[c * 128:(c + 1) * 128, NT + tg * 512:NT + (tg + 1) * 512],
                  writes=[Xtk[c][tg]])
    ws2 = WStream(cx, st, 4096, nstage=2, nslot=2)
    wov = wo.rearrange("(c p) n -> p c n", p=128)
    psA = Rot(cx.banks[0:4])
    for half in range(2):
        wa, watk = ws2.load([wov[:, :, half * 512:(half + 1) * 512]])
        wa3 = wa.rearrange("p (c n) -> p c n", c=NC8)
        for o4 in range(4):
            oc = half * 4 + o4
            for tg in range(4):
                sl = slice(tg * 512, (tg + 1) * 512)
                ps, pstk = psA.next()
                for c in range(NC8):
                    P.op("pe", lambda e, c=c, o4=o4, sl=sl, ps=ps, wa3=wa3: e.matmul(
                        ps, lhsT=wa3[:, c, o4 * 128:(o4 + 1) * 128], rhs=oT[:, c, sl],
                        start=(c == 0), stop=(c == NC8 - 1)),
                        reads=[watk, ottk], writes=[pstk], signal=(c == NC8 - 1))
                P.op("dve", lambda e, oc=oc, sl=sl, ps=ps: e.tensor_tensor(
                    out=X[:, oc, sl], in0=ps, in1=X[:, oc, sl], op=ALU.add),
                    reads=[pstk, Xtk[oc][tg]], writes=[Xtk[oc][tg]])
    P.barrier()
    cx.release(mk)
    return X, Xtk


def emit_store(cx, X, Xtk, out_dram):
    P = cx.P
    for c in range(NC8):
        P.dma("sync", out_dram[c * 128:(c + 1) * 128, :], X[:, c, :], reads=Xtk[c])


def build_layer0(debug=False, lvl=9, hps=8):
    nc = bass.Bass("TRN2", target_bir_lowering=False)
    xT_ext = nc.dram_tensor("xT_ext", [D, 2 * NT], F32, kind="ExternalInput").ap()
    pT = nc.dram_tensor("pT", [256, NT], F32, kind="ExternalInput").ap()
    cst = nc.dram_tensor("cst", [128, G0_END], F32, kind="ExternalInput").ap()
    wqkv = nc.dram_tensor("a_w_qkv", [D, 9216], F32, kind="ExternalInput").ap()
    wo = nc.dram_tensor("a_w_o", [D, D], F32, kind="ExternalInput").ap()
    w1 = nc.dram_tensor("mlp_w1", [D, 4096], F32, kind="ExternalInput").ap()
    w2 = nc.dram_tensor("mlp_w2", [4096, D], F32, kind="ExternalInput").ap()
    wg = nc.dram_tensor("ple_w_gate", [D, D], F32, kind="ExternalInput").ap()
    wp = nc.dram_tensor("ple_w_proj", [256, D], F32, kind="ExternalInput").ap()
    out = nc.dram_tensor("xout", [D, NT], F32, kind="ExternalOutput").ap()
    if debug:
        dbg_a = nc.dram_tensor("dbg_a", [D, NT], F32, kind="ExternalOutput").ap()
        dbg_m = nc.dram_tensor("dbg_m", [D, NT], F32, kind="ExternalOutput").ap()
    cx = Ctx(nc)
    P = cx.P
    cf, cb, ctk = load_consts(cx, None, cst, G0_END)
    cx.eps_col = cf[:, C_EPS:C_EPS + 1]
    ones_bf = cb[:, C_ONES:C_ONES + 128]
    TOP = cx.sb(None, [128, 16384], F32, "TOP")
    R1 = cx.sb(None, [128, 12288], F32, "R1")
    mk = cx.mark()
    X, Xtk = emit_attention(cx, xT_ext, wqkv, wo, cf, cb, ctk, TOP, R1, lvl=lvl, hps=hps)
    cx.release(mk)
    cx.top = cx.top - 12288 * 4
    if debug:
        emit_store(cx, X, Xtk, dbg_a)
    emit_mlp(cx, X, Xtk, cf[:, G0_MLPN:G0_MLPN + 8], ctk, w1, w2, ones_bf)
    if debug:
        emit_store(cx, X, Xtk, dbg_m)
    emit_ple(cx, X, Xtk, cf[:, G0_PLEN:G0_PLEN + 8], ctk, wg, wp, pT, ones_bf)
    emit_store(cx, X, Xtk, out)
    P.finish()
    return nc, cx


def layer0_inputs(inputs, core):
    x = inputs["x"][0]
    lo = core * NT
    xe = np.zeros((2 * NT, D), np.float32)
    if core > 0:
        xe[:NT] = x[lo - NT:lo]
    xe[NT:] = x[lo:lo + NT]
    c = base_consts(core, G0_END)
    c[:, G0_ANORM:G0_ANORM + 8] = col_layout(inputs["a_norm"][0])
    c[:, G0_QG:G0_QG + 3] = np.tile(inputs["a_q_gain"][0].T, (2, 1))
    c[:, G0_KG:G0_KG + 3] = np.tile(inputs["a_k_gain"][0].T, (2, 1))
    c[:, G0_MLPN:G0_MLPN + 8] = col_layout(inputs["mlp_norm"][0])
    c[:, G0_PLEN:G0_PLEN + 8] = col_layout(inputs["ple_norm"][0])
    return {
        "xT_ext": np.ascontiguousarray(xe.T),
        "pT": np.ascontiguousarray(inputs["p"][0, 0, lo:lo + NT].T),
        "cst": c,
        "a_w_qkv": inputs["a_w_qkv"][0], "a_w_o": inputs["a_w_o"][0],
        "mlp_w1": inputs["mlp_w1"][0], "mlp_w2": inputs["mlp_w2"][0],
        "ple_w_gate": inputs["ple_w_gate"][0], "ple_w_proj": inputs["ple_w_proj"][0],
    }

BH = 4
DH = 512
NCK = NT // 128
KSCALE = DH ** -0.5
NST = 8192 + 2048 + 8

L_BNORM = C_GAINS
L_MLPN = L_BNORM + 8
L_PLEN = L_MLPN + 8
L_CONVW = L_PLEN + 8
L_CONVB = L_CONVW + 64
L_SKIP = L_CONVB + 16
L_HGAIN = L_SKIP + 16
L_BI = L_HGAIN + 16
L_BF = L_BI + 1
L_MASKLOW = L_BF + 1
L_SEL = L_MASKLOW + 128
L_CMASK = L_SEL + 512
L_NEG = L_CMASK + 7
L_E0 = L_NEG + 1
L_LNK = L_E0 + 1
L_CNEG = L_LNK + 1
L_END = L_CNEG + 7


def layer1_consts(inputs, core):
    c = base_consts(core, L_END)
    c[:, L_BNORM:L_BNORM + 8] = col_layout(inputs["b_norm"][0])
    c[:, L_MLPN:L_MLPN + 8] = col_layout(inputs["mlp_norm"][1])
    c[:, L_PLEN:L_PLEN + 8] = col_layout(inputs["ple_norm"][1])
    cw = inputs["b_conv_w"][0]
    c[:, L_CONVW:L_CONVW + 64] = cw.reshape(4, 16, 128).transpose(2, 1, 0).reshape(128, 64)
    c[:, L_CONVB:L_CONVB + 16] = col_layout(inputs["b_conv_b"][0])
    c[:, L_SKIP:L_SKIP + 16] = col_layout(inputs["b_skip"][0])
    c[:, L_HGAIN:L_HGAIN + 16] = col_layout(inputs["b_h_gain"][0])
    bg = inputs["b_b_gate"][0]
    c[0:4, L_BI] = bg[0:4]
    c[0:4, L_BF] = bg[4:8]
    s_ = np.arange(128)[:, None]
    t_ = np.arange(128)[None, :]
    c[:, L_MASKLOW:L_MASKLOW + 128] = np.where(s_ <= t_, 0.0, BIG)
    for hd in range(4):
        c[hd, L_SEL + hd * 128:L_SEL + (hd + 1) * 128] = 1.0
    for cp in range(7):
        c[:, L_CMASK + cp] = 1.0 if cp < core else 0.0
        c[:, L_CNEG + cp] = 0.0 if cp < core else -1e30
    c[:, L_NEG] = -1e30
    c[0, L_E0] = 1.0
    c[:, L_LNK] = np.log(KSCALE)
    return c


def bd_compact(w, transpose=False):
    out = np.zeros((2048, 128), np.float32)
    n = np.arange(512)
    for j in range(4):
        for k in range(4):
            if transpose:
                out[4 * n + k, (4 * n + j) % 128] = w[:, j, k]
            else:
                out[4 * n + j, (4 * n + k) % 128] = w[:, j, k]
    return out


def layer1_inputs(inputs, core, x1T_full, stage, st_all=None, g_in=None):
    lo = core * NT
    xh = np.zeros((D, 4), np.float32)
    if core > 0:
        xh[:, 1:4] = x1T_full[:, lo - 3:lo]
    m = {
        "x1T": np.ascontiguousarray(x1T_full[:, lo:lo + NT]),
        "xh": xh,
        "cst": layer1_consts(inputs, core),
        "b_w_up": inputs["b_w_up"][0],
        "bd": np.stack([bd_compact(inputs["b_w_q"][0]), bd_compact(inputs["b_w_k"][0]), bd_compact(inputs["b_w_v"][0])]),
        "bdT": np.stack([bd_compact(inputs["b_w_q"][0], True), bd_compact(inputs["b_w_k"][0], True),
                         bd_compact(inputs["b_w_v"][0], True)]),
        "w_gate": inputs["b_w_gate"][0],
    }
    if stage == "C":
        m.update({
            "st_all": st_all,
            "g_in": g_in,
            "b_w_down": inputs["b_w_down"][0],
            "pT": np.ascontiguousarray(inputs["p"][1, 0, lo:lo + NT].T),
            "mlp_w1": inputs["mlp_w1"][1], "mlp_w2": inputs["mlp_w2"][1],
            "ple_w_gate": inputs["ple_w_gate"][1], "ple_w_proj": inputs["ple_w_proj"][1],
        })
    return m


def build_layer1(stage, debug=False, dbg_stop=None):
    nc = bass.Bass("TRN2", target_bir_lowering=False)
    x1T = nc.dram_tensor("x1T", [D, NT], F32, kind="ExternalInput").ap()
    xh = nc.dram_tensor("xh", [D, 4], F32, kind="ExternalInput").ap()
    cst = nc.dram_tensor("cst", [128, L_END], F32, kind="ExternalInput").ap()
    wup = nc.dram_tensor("b_w_up", [D, 4096], F32, kind="ExternalInput").ap()
    bd = nc.dram_tensor("bd", [3, 2048, 128], F32, kind="ExternalInput").ap()
    bdT = nc.dram_tensor("bdT", [3, 2048, 128], F32, kind="ExternalInput").ap()
    wgate = nc.dram_tensor("w_gate", [6144, 8], F32, kind="ExternalInput").ap()
    if stage == "B":
        st_out = nc.dram_tensor("st_out", [128, NST], F32, kind="ExternalOutput").ap()
        g_out = nc.dram_tensor("g_out", [8, NT], F32, kind="ExternalOutput").ap()
    else:
        st_all = nc.dram_tensor("st_all", [7, 128, NST], F32, kind="ExternalInput").ap()
        g_in = nc.dram_tensor("g_in", [8, NT], F32, kind="ExternalInput").ap()
        wdown = nc.dram_tensor("b_w_down", [2048, D], F32, kind="ExternalInput").ap()
        pT = nc.dram_tensor("pT", [256, NT], F32, kind="ExternalInput").ap()
        w1 = nc.dram_tensor("mlp_w1", [D, 4096], F32, kind="ExternalInput").ap()
        w2 = nc.dram_tensor("mlp_w2", [4096, D], F32, kind="ExternalInput").ap()
        wg = nc.dram_tensor("ple_w_gate", [D, D], F32, kind="ExternalInput").ap()
        wp = nc.dram_tensor("ple_w_proj", [256, D], F32, kind="ExternalInput").ap()
        out = nc.dram_tensor("xout", [D, NT], F32, kind="ExternalOutput").ap()
        yscr = nc.dram_tensor("yscr", [2048, NT], BF16).ap()
        if debug:
            dbg_a = nc.dram_tensor("dbg_a", [D, NT], F32, kind="ExternalOutput").ap()
    cx = Ctx(nc)
    P = cx.P
    cf, cb, ctk = load_consts(cx, None, cst, L_END)
    cx.eps_col = cf[:, C_EPS:C_EPS + 1]
    ones_bf = cb[:, C_ONES:C_ONES + 128]
    ones_f = cf[:, C_ONES:C_ONES + 128]
    ident_f = cf[:, C_ID:C_ID + 128]
    one_col = cf[:, C_ONES:C_ONES + 1]
    TOP = cx.sb(None, [128, 16384], F32, "TOP")
    hT = TOP[:, 0:8208].bitcast(BF16)[:, 0:8 * 2052].rearrange("p (c n) -> p c n", c=NC8)
    topfree = TOP[:, 8208:16384]
    base_mark = cx.mark()
    bk = [(cx.banks[i], Tk()) for i in range(8)]
    ws = WStream(cx, None, 4096, nstage=0, nslot=3)
    ws.stage = Rot([topfree[:, 0:4096]])
    bdh = cx.sb(None, [128, 3, 4, 128], BF16, "bdh")
    bdh_st = cx.sb(None, [128, 3, 4, 128], F32, "bdh_st")
    diag = cx.sb(None, [128, 4, 4, 128], BF16, "diag")
    xms = [cx.sb(None, [128, 4, 516], BF16, "xm") for _ in range(2)]
    xc = cx.sb(None, [128, 4, 512], BF16, "xc")
    GF = cx.sb(None, [128, NT], F32, "GF")
    BETAx = cx.sb(None, [128, NT + 1], F32, "BETAx")
    small = cx.sb(None, [128, 64], F32, "small")
    TMw = cx.sb(None, [128, NCK, 4], F32, "TMw")
    TMa = cx.sb(None, [128, NCK, 4], F32, "TMa")
    if stage == "C":
        fold = cx.sb(None, [128, 7, 8], F32, "foldin")
        S1 = cx.sb(None, [128, 7, 4], F32, "S1")
        S2 = cx.sb(None, [128, 7, 4], F32, "S2")
        mrun = cx.sb(None, [128, 4], F32, "mrun")
        fa = cx.sb(None, [128, 4], F32, "fa")
        fb = cx.sb(None, [128, 4], F32, "fb")
        fc_ = cx.sb(None, [128, 4], F32, "fc")
    pers_mark = cx.mark()

    mk = cx.mark()
    sq_rot = Rot([cx.sb(None, [128, 512], BF16, "sq") for _ in range(2)])
    rstd_rot = Rot([cx.sb(None, [128, 512], F32, "rstd") for _ in range(2)])
    xstg = [cx.sb(None, [128, NC8, 512], F32, "xstg") for _ in range(2)]
    xstk = [Tk(), Tk()]
    ps_stat = Rot([bk[7], bk[6]])
    gcol = cf[:, L_BNORM:L_BNORM + 8]
    httk = Tk()
    pieces = [(None, 4)] + [(tg, 512) for tg in range(4)]
    for i, (tg, n) in enumerate(pieces):
        xa = xstg[i % 2]
        xt = xstk[i % 2]
        if tg is None:
            P.dma("sync", xa[:, :, 0:4], xh.rearrange("(c p) n -> p c n", p=128), writes=[xt])
            h0 = 0
        else:
            P.dma("sync", xa, x1T[:, tg * 512:(tg + 1) * 512].rearrange("(c p) n -> p c n", p=128), writes=[xt])
            h0 = 4 + tg * 512
        xs = [(xa[:, c, 0:n], xt) for c in range(NC8)]
        ps_ap, ps_tk = ps_stat.next()
        rstd, rtk = rstd_rot.next()
        rms_stats(cx, xs, n, sq_rot, ps_ap, ps_tk, rstd, rtk, ones_bf, ctk, 1.0 / D)
        for c in range(NC8):
            P.op("dve", lambda e, c=c, xa=xa, rstd=rstd, n=n, h0=h0: e.scalar_tensor_tensor(
                out=hT[:, c, h0:h0 + n], in0=xa[:, c, 0:n], scalar=gcol[:, c:c + 1], in1=rstd[:, 0:n],
                op0=ALU.mult, op1=ALU.mult), reads=[xt, rtk, ctk], writes=[httk])
    P.barrier()
    cx.release(mk)

    GI = cx.sb(None, [128, NT], F32, "GI")
    LF = cx.sb(None, [128, NT], F32, "LF")
    BB = cx.sb(None, [128, NT], F32, "BB")
    T1 = cx.sb(None, [128, NT], F32, "T1")
    wfold = [[cx.sb(None, [128, 16, 128], BF16, "wfold") for _ in range(2)] for _ in range(2)]
    wftk = Tk()
    bdT_sb = topfree[:, 0:6144].rearrange("p (a b) -> p a b", a=48)
    wg_sb = topfree[:, 6144:6528].rearrange("p (a b) -> p a b", a=48)
    btk = Tk()
    for j in range(3 if stage == "B" else 0):
        P.dma("sync", bdT_sb[:, j * 16:(j + 1) * 16, :], bdT[j].rearrange("(c p) n -> p c n", p=128), writes=[btk])
    if stage == "B":
        P.dma("sync", wg_sb, wgate.rearrange("(c p) n -> p c n", p=128), writes=[btk])
    zpad = Rot([topfree[:, 6528 + i * 128:6528 + (i + 1) * 128] for i in range(4)])
    for (za, ztk) in zpad.items:
        P.op("pool", lambda e, za=za: e.memset(za, 0.0), writes=[ztk])
    psF = Rot(bk[0:2])
    for mc in range(16 if stage == "B" else 0):
        for part in range(2):
            for xm_ in range(2):
                ps, pstk = psF.next()
                srcs = (0, 1) if xm_ == 0 else (2,)
                for si, j in enumerate(srcs):
                    za, ztk = zpad.next()
                    P.op("dve", lambda e, za=za, j=j, mc=mc, part=part: e.tensor_copy(
                        out=za[:, 0:4], in_=wg_sb[:, j * 16 + mc, part * 4:part * 4 + 4]), reads=[btk], writes=[ztk])
                    P.op("pe", lambda e, ps=ps, za=za, j=j, mc=mc, si=si, srcs=srcs: e.matmul(
                        ps[:, 0:128], lhsT=bdT_sb[:, j * 16 + mc, :], rhs=za, start=(si == 0), stop=(si == len(srcs) - 1)),
                        reads=[btk, ztk], writes=[pstk])
                P.op("act", lambda e, ps=ps, xm_=xm_, part=part, mc=mc: e.activation(
                    out=wfold[xm_][part][:, mc, :], in_=ps[:, 0:128], func=AF.Copy), reads=[pstk], writes=[wftk])
    P.barrier()

    hdtk = Tk()
    xmtk = [Tk(), Tk()]
    xctk = Tk()
    psA = Rot(bk[0:2])
    wupv = wup.rearrange("(c p) n -> p c n", p=128)
    bdv = bd.rearrange("j (c p) n -> p j c n", p=128)
    state = {"i": 0}

    def head_setup(hd):
        for j in range(3):
            P.dma("sync", bdh_st[:, j, :, :], bdv[:, j, hd * 4:(hd + 1) * 4, :], writes=[hdtk])
        P.op("pool", lambda e: e.tensor_copy(out=bdh, in_=bdh_st), reads=[hdtk], writes=[hdtk])
        for mc in range(4):
            for k in range(4):
                col = L_CONVW + (hd * 4 + mc) * 4 + k
                P.op("act", lambda e, mc=mc, k=k, col=col: e.activation(
                    out=diag[:, mc, k, :], in_=ident_f, func=AF.Copy, scale=cf[:, col:col + 1]),
                    reads=[ctk], writes=[hdtk])
        wx, wxtk = ws.load([wupv[:, :, hd * 512:(hd + 1) * 512]])
        return wx.rearrange("p (c n) -> p c n", c=NC8), wxtk

    def front(hd, tg, wx3, wxtk):
        i = state["i"]
        state["i"] += 1
        xm, xmt = xms[i % 2], xmtk[i % 2]
        xmp, xmpt = xms[(i + 1) % 2], xmtk[(i + 1) % 2]
        for mc in range(4):
            ps, pstk = psA.next()
            for c in range(NC8):
                P.op("pe", lambda e, c=c, mc=mc, ps=ps: e.matmul(
                    ps, lhsT=wx3[:, c, mc * 128:(mc + 1) * 128], rhs=hT[:, c, 4 + tg * 512:4 + (tg + 1) * 512],
                    start=(c == 0), stop=(c == NC8 - 1)), reads=[wxtk], writes=[pstk], signal=(c == NC8 - 1))
            P.op("act", lambda e, ps=ps, mc=mc, xm=xm: e.activation(out=xm[:, mc, 4:516], in_=ps, func=AF.Copy),
                 reads=[pstk], writes=[xmt])
            if tg == 0:
                ps, pstk = psA.next()
                for c in range(NC8):
                    P.op("pe", lambda e, c=c, mc=mc, ps=ps: e.matmul(
                        ps[:, 0:4], lhsT=wx3[:, c, mc * 128:(mc + 1) * 128], rhs=hT[:, c, 0:4],
                        start=(c == 0), stop=(c == NC8 - 1)), reads=[wxtk], writes=[pstk], signal=(c == NC8 - 1))
                P.op("act", lambda e, ps=ps, mc=mc, xm=xm: e.activation(out=xm[:, mc, 0:4], in_=ps[:, 0:4], func=AF.Copy),
                     reads=[pstk], writes=[xmt])
        if tg > 0:
            P.op("pool", lambda e, xm=xm, xmp=xmp: e.tensor_copy(out=xm[:, :, 0:4], in_=xmp[:, :, 512:516]),
                 reads=[xmpt], writes=[xmt])
        for mc in range(4):
            ps, pstk = psA.next()
            for k in range(4):
                P.op("pe", lambda e, k=k, mc=mc, ps=ps, xm=xm: e.matmul(
                    ps, lhsT=diag[:, mc, k, :], rhs=xm[:, mc, 1 + k:1 + k + 512], start=(k == 0), stop=(k == 3)),
                    reads=[hdtk, xmt], writes=[pstk], signal=(k == 3))
            col = L_CONVB + hd * 4 + mc
            P.op("act", lambda e, ps=ps, mc=mc, col=col: e.activation(
                out=xc[:, mc, :], in_=ps, func=AF.Silu, bias=cf[:, col:col + 1]), reads=[pstk, ctk], writes=[xctk])
        return xm, xmt

    gtk = Tk()
    psG = Rot(bk[2:4])
    if stage == "C":
        P.op("pool", lambda e: e.memset(GI, 0.0), writes=[gtk])
        P.op("pool", lambda e: e.memset(GF, 0.0), writes=[gtk])
        P.dma("sync", GI[0:4, :], g_in[0:4, :], writes=[gtk])
        P.dma("sync", GF[0:4, :], g_in[4:8, :], writes=[gtk])
    for hd in range(BH if stage == "B" else 0):
        wx3, wxtk = head_setup(hd)
        for tg in range(4):
            xm, xmt = front(hd, tg, wx3, wxtk)
            for part, Grow in ((0, GI), (1, GF)):
                ps, pstk = psG.next()
                for mc in range(4):
                    P.op("pe", lambda e, ps=ps, mc=mc, part=part: e.matmul(
                        ps, lhsT=wfold[0][part][:, hd * 4 + mc, :], rhs=xc[:, mc, :], start=(mc == 0), stop=False),
                        reads=[wftk, xctk], writes=[pstk], signal=False)
                    P.op("pe", lambda e, ps=ps, mc=mc, part=part, xm=xm: e.matmul(
                        ps, lhsT=wfold[1][part][:, hd * 4 + mc, :], rhs=xm[:, mc, 4:516], start=False, stop=(mc == 3)),
                        reads=[wftk, xmt], writes=[pstk], signal=(mc == 3))
                sl = slice(tg * 512, (tg + 1) * 512)
                if hd == 0:
                    P.op("act", lambda e, ps=ps, Grow=Grow, sl=sl: e.activation(out=Grow[:, sl], in_=ps, func=AF.Copy),
                         reads=[pstk], writes=[gtk])
                else:
                    P.op("dve", lambda e, ps=ps, Grow=Grow, sl=sl: e.tensor_tensor(out=Grow[:, sl], in0=ps, in1=Grow[:, sl], op=ALU.add),
                         reads=[pstk, gtk], writes=[gtk])

    rtk = Tk()
    if stage == "B":
        P.dma("sync", g_out[0:4, :], GI[0:4, :], reads=[gtk])
        P.dma("sync", g_out[4:8, :], GF[0:4, :], reads=[gtk])
    P.op("dve", lambda e: e.tensor_scalar(out=GI, in0=GI, scalar1=cf[:, L_BI:L_BI + 1], scalar2=None, op0=ALU.add),
         reads=[gtk, ctk], writes=[gtk])
    P.op("dve", lambda e: e.tensor_scalar(out=GF, in0=GF, scalar1=cf[:, L_BF:L_BF + 1], scalar2=None, op0=ALU.add),
         reads=[gtk, ctk], writes=[gtk])
    P.op("dve", lambda e: e.tensor_scalar(out=T1, in0=GF, scalar1=-1.0, scalar2=None, op0=ALU.mult), reads=[gtk], writes=[rtk])
    P.op("dve", lambda e: e.tensor_tensor(out=T1, in0=T1, in1=GF, op=ALU.max), reads=[gtk, rtk], writes=[rtk])
    P.op("act", lambda e: e.activation(out=T1, in_=T1, func=AF.Exp, scale=-1.0), reads=[rtk], writes=[rtk])
    P.op("act", lambda e: e.activation(out=T1, in_=T1, func=AF.Ln, bias=one_col), reads=[rtk, ctk], writes=[rtk])
    P.op("dve", lambda e: e.scalar_tensor_tensor(out=LF, in0=GF, scalar=0.0, in1=T1, op0=ALU.min, op1=ALU.subtract),
         reads=[gtk, rtk], writes=[rtk])
    P.op("pool", lambda e: e.memset(T1, 1.0), reads=[rtk], writes=[rtk])
    P.op("dve", lambda e: e.tensor_tensor_scan(out=BB, data0=T1, data1=LF, initial=0.0, op0=ALU.mult, op1=ALU.add),
         reads=[rtk], writes=[rtk])
    P.op("dve", lambda e: e.tensor_tensor(out=T1, in0=GI, in1=BB, op=ALU.subtract), reads=[gtk, rtk], writes=[rtk])
    ALPHA = T1
    psR = Rot([bk[4]])
    tmtk = Tk()

    def to_token_major(row, dst):
        ps, pstk = psR.next()
        for ck in range(NCK):
            P.op("pe", lambda e, ps=ps, ck=ck: e.matmul(ps[:, ck * 4:ck * 4 + 4], lhsT=row[:, ck * 128:(ck + 1) * 128],
                                                        rhs=ident_f[:, 0:4], start=True, stop=True),
                 reads=[rtk, gtk, ctk], writes=[pstk], signal=(ck == NCK - 1))
        P.op("act", lambda e, ps=ps: e.activation(out=dst, in_=ps[:, 0:64].rearrange("p (a b) -> p a b", a=NCK), func=AF.Copy),
             reads=[pstk], writes=[tmtk])

    def replicate_cols(col_ap, dst4):
        ps, pstk = psR.next()
        za = small[:, 32:32 + 4]
        P.op("dve", lambda e: e.tensor_scalar(out=za, in0=ident_f[:, 0:4], scalar1=col_ap, scalar2=None, op0=ALU.mult),
             reads=[rtk, ctk, gtk], writes=[rtk])
        P.op("pe", lambda e, ps=ps: e.matmul(ps[:, 0:4], lhsT=ones_f, rhs=za, start=True, stop=True),
             reads=[rtk, ctk], writes=[pstk])
        P.op("act", lambda e, ps=ps: e.activation(out=dst4, in_=ps[:, 0:4], func=AF.Copy), reads=[pstk], writes=[rtk])

    if stage == "B":
        mx = small[:, 0:1]
        P.op("dve", lambda e: e.tensor_reduce(out=mx, in_=ALPHA, axis=AX.X, op=ALU.max), reads=[rtk], writes=[rtk])
        nb_ = small[:, 1:2]
        P.op("dve", lambda e: e.scalar_tensor_tensor(out=nb_, in0=mx, scalar=-1.0, in1=cf[:, L_LNK:L_LNK + 1],
                                                     op0=ALU.mult, op1=ALU.add), reads=[rtk, ctk], writes=[rtk])
        P.op("act", lambda e: e.activation(out=LF, in_=ALPHA, func=AF.Exp, bias=nb_), reads=[rtk], writes=[rtk])
        to_token_major(LF, TMw)
        ml = small[:, 2:3]
        P.op("dve", lambda e: e.tensor_tensor(out=ml, in0=mx, in1=BB[:, NT - 1:NT], op=ALU.add), reads=[rtk], writes=[rtk])
        fin = cx.sb(None, [128, 8], F32, "fin")
        replicate_cols(BB[:, NT - 1:NT], fin[:, 0:4])
        replicate_cols(ml, fin[:, 4:8])
        P.dma("sync", st_out[:, 10240:10248], fin, reads=[rtk])
        P.barrier()
        cx.release(pers_mark)
        kv_rot = Rot([cx.sb(None, [128, 512], BF16, "kv") for _ in range(4)])
        stC = cx.sb(None, [128, 4, 512], F32, "stC")
        stn = cx.sb(None, [128, 512], F32, "stn")
        sttk = Tk()
        psKV = Rot([bk[0], bk[1], bk[2]])
        for hd in range(BH):
            wx3, wxtk = head_setup(hd)
            cacc = [bk[3 + dc] for dc in range(4)]
            nacc, nacctk = bk[7]
            for tg in range(4):
                xm, xmt = front(hd, tg, wx3, wxtk)
                for cl in range(4):
                    ck = tg * 4 + cl
                    tsl = slice(cl * 128, (cl + 1) * 128)
                    ps, pstk = psKV.next()
                    for mc in range(4):
                        P.op("pe", lambda e, ps=ps, mc=mc, tsl=tsl: e.matmul(
                            ps[:, mc * 128:(mc + 1) * 128], lhsT=xc[:, mc, tsl], rhs=bdh[:, 1, mc, :], start=True, stop=True),
                            reads=[xctk, hdtk], writes=[pstk], signal=(mc == 3))
                    wk, wktk = kv_rot.next()
                    P.op("act", lambda e, ps=ps, wk=wk, ck=ck, hd=hd: e.activation(
                        out=wk, in_=ps, func=AF.Copy, scale=TMw[:, ck, hd:hd + 1]), reads=[pstk, tmtk], writes=[wktk])
                    ps, pstk = psKV.next()
                    for mc in range(4):
                        P.op("pe", lambda e, ps=ps, mc=mc, cl=cl, xm=xm: e.matmul(
                            ps[:, mc * 128:(mc + 1) * 128], lhsT=xm[:, mc, 4 + cl * 128:4 + (cl + 1) * 128], rhs=bdh[:, 2, mc, :],
                            start=True, stop=True), reads=[xmt, hdtk], writes=[pstk], signal=(mc == 3))
                    vv, vtk = kv_rot.next()
                    P.op("act", lambda e, ps=ps, vv=vv: e.activation(out=vv, in_=ps, func=AF.Copy), reads=[pstk], writes=[vtk])
                    last = (ck == NCK - 1)
                    for dc in range(4):
                        P.op("pe", lambda e, dc=dc, wk=wk, vv=vv, ck=ck, last=last: e.matmul(
                            cacc[dc][0], lhsT=wk[:, dc * 128:(dc + 1) * 128], rhs=vv, start=(ck == 0), stop=last),
                            reads=[wktk, vtk], writes=[cacc[dc][1]], signal=True)
                    P.op("pe", lambda e, wk=wk, ck=ck, last=last: e.matmul(
                        nacc, lhsT=ones_bf, rhs=wk, start=(ck == 0), stop=last), reads=[wktk, ctk], writes=[nacctk], signal=True)
            for dc in range(4):
                P.op("act", lambda e, dc=dc: e.activation(out=stC[:, dc, :], in_=cacc[dc][0], func=AF.Copy),
                     reads=[cacc[dc][1]], writes=[sttk])
            P.op("dve", lambda e: e.tensor_copy(out=stn, in_=nacc), reads=[nacctk], writes=[sttk])
            P.dma("sync", st_out[:, hd * 2048:(hd + 1) * 2048], stC.rearrange("p a b -> p (a b)"), reads=[sttk])
            P.dma("sync", st_out[:, 8192 + hd * 512:8192 + (hd + 1) * 512], stn, reads=[sttk])
        P.finish()
        return nc, cx

    ftk = Tk()
    P.dma("sync", fold, st_all[:, :, 10240:10248].rearrange("c p n -> p c n"), writes=[ftk])
    negc = cf[:, L_NEG:L_NEG + 1]
    P.op("dve", lambda e: e.memset(mrun, -1e30), writes=[ftk])
    for cp in range(7):
        mu = cf[:, L_CMASK + cp:L_CMASK + cp + 1]
        P.op("dve", lambda e, cp=cp, mu=mu: e.scalar_tensor_tensor(out=fa, in0=fold[:, cp, 0:4], scalar=mu, in1=mrun,
                                                                    op0=ALU.mult, op1=ALU.add), reads=[ftk, ctk], writes=[ftk])
        P.op("dve", lambda e, cp=cp, mu=mu: e.tensor_scalar(out=fb, in0=fold[:, cp, 4:8], scalar1=mu,
                                                            scalar2=cf[:, L_CNEG + cp:L_CNEG + cp + 1], op0=ALU.mult, op1=ALU.add),
             reads=[ftk, ctk], writes=[ftk])
        P.op("dve", lambda e: e.tensor_tensor(out=fc_, in0=fa, in1=fb, op=ALU.max), reads=[ftk], writes=[ftk])
        P.op("dve", lambda e: e.tensor_tensor(out=fa, in0=fa, in1=fc_, op=ALU.subtract), reads=[ftk], writes=[ftk])
        P.op("dve", lambda e: e.tensor_tensor(out=fb, in0=fb, in1=fc_, op=ALU.subtract), reads=[ftk], writes=[ftk])
        P.op("act", lambda e, cp=cp: e.activation(out=S1[:, cp, :], in_=fa, func=AF.Exp), reads=[ftk], writes=[ftk])
        P.op("act", lambda e: e.activation(out=fb, in_=fb, func=AF.Exp), reads=[ftk], writes=[ftk])
        P.op("dve", lambda e, cp=cp, mu=mu: e.tensor_scalar(out=S2[:, cp, :], in0=fb, scalar1=mu, scalar2=None, op0=ALU.mult),
             reads=[ftk, ctk], writes=[ftk])
        P.op("dve", lambda e: e.tensor_copy(out=mrun, in_=fc_), reads=[ftk], writes=[ftk])
    mst = small[:, 4:5]
    P.op("dve", lambda e: e.tensor_tensor(out=small[:, 8:12], in0=mrun, in1=ident_f[:, 0:4], op=ALU.mult), reads=[ftk, ctk], writes=[rtk])
    P.op("dve", lambda e: e.tensor_reduce(out=mst, in_=small[:, 8:12], axis=AX.X, op=ALU.add), reads=[rtk], writes=[rtk])
    P.op("dve", lambda e: e.tensor_tensor_scan(out=GF, data0=LF, data1=GI, initial=mst, op0=ALU.add, op1=ALU.max),
         reads=[rtk, gtk], writes=[gtk])
    MM = GF
    P.op("dve", lambda e: e.tensor_tensor(out=BETAx[:, 1:NT + 1], in0=MM, in1=BB, op=ALU.subtract), reads=[gtk, rtk], writes=[rtk])
    P.op("dve", lambda e: e.tensor_copy(out=BETAx[:, 0:1], in_=mst), reads=[rtk], writes=[rtk])
    BETA = BETAx[:, 1:NT + 1]
    for ck in range(NCK):
        bl = small[:, 16:17]
        P.op("dve", lambda e, ck=ck: e.scalar_tensor_tensor(out=small[:, 16 + ck % 8:17 + ck % 8], in0=BETAx[:, 128 * (ck + 1):128 * (ck + 1) + 1],
                                                            scalar=-1.0, in1=cf[:, L_LNK:L_LNK + 1], op0=ALU.mult, op1=ALU.add),
             reads=[rtk, ctk], writes=[rtk])
        P.op("act", lambda e, ck=ck: e.activation(out=LF[:, ck * 128:(ck + 1) * 128], in_=ALPHA[:, ck * 128:(ck + 1) * 128],
                                                  func=AF.Exp, bias=small[:, 16 + ck % 8:17 + ck % 8]), reads=[rtk], writes=[rtk])
    to_token_major(LF, TMw)
    to_token_major(ALPHA, TMa)
    P.barrier()
    cx.release(pers_mark)
    BETA = BETAx[:, 1:NT + 1]

    qT = cx.sb(None, [128, 4, 512], BF16, "qT")
    kT = cx.sb(None, [128, 4, 512], BF16, "kT")
    zs = cx.sb(None, [128, 4, 512], BF16, "zs")
    yb = cx.sb(None, [128, 4, 512], BF16, "yb")
    qktk, zstk, ytk = Tk(), Tk(), Tk()
    Csts = [cx.sb(None, [128, 4, 512], F32, "Cst") for _ in range(2)]
    Caug = cx.sb(None, [128, 4, 640], BF16, "Caug")
    nrows = [cx.sb(None, [128, 512], F32, "nrow") for _ in range(2)]
    nm = cx.sb(None, [128, 512], F32, "nm")
    ncol = cx.sb(None, [128, 4], F32, "ncol")
    ctk2s = [Tk(), Tk()]
    caugtk = Tk()
    clst = Rot([topfree[:, 4096:6144], topfree[:, 6144:8176][:, 0:2032]])
    wk_rot = Rot([cx.sb(None, [128, 512], BF16, "wk") for _ in range(2)])
    va_rot = Rot([cx.sb(None, [128, 640], BF16, "vaug") for _ in range(2)])
    for (va, vatk) in va_rot.items:
        P.op("pool", lambda e, va=va: e.memset(va[:, 512:640], 1.0), writes=[vatk])
    dt_rot = Rot([cx.sb(None, [128, 128], F32, "dtmp") for _ in range(2)])
    sd_rot = Rot([cx.sb(None, [128, 128], BF16, "SdT") for _ in range(2)])
    qs_rot = Rot([cx.sb(None, [128, 4, 128], BF16, "qs") for _ in range(2)])
    hsq_rot = Rot([cx.sb(None, [128, 512], BF16, "hsq") for _ in range(2)])
    dd_rot = Rot([cx.sb(None, [128, 128], F32, "dd") for _ in range(2)])
    rr_rot = Rot([cx.sb(None, [128, 128], F32, "rr") for _ in range(2)])
    sc_rot = Rot([cx.sb(None, [128, 128], F32, "scsb") for _ in range(2)])
    em_rot = Rot([cx.sb(None, [128, 128], F32, "emsb") for _ in range(2)])
    ul_rot = Rot([cx.sb(None, [128, 1], F32, "ulast") for _ in range(2)])
    tt_rot = Rot([cx.sb(None, [128, 128], F32, "tt") for _ in range(3)])
    psB2 = psA
    psS3 = Rot([bk[2]])
    psRP = Rot([bk[2]])
    psH = Rot([bk[4], bk[5]])
    psDS = Rot([bk[6], bk[7]])
    psSS = Rot([bk[3]])
    wzv = wupv
    yview = yscr.rearrange("(c p) n -> p c n", p=128)
    def emit_fold(hd):
        Cst, nrow, ctk2 = Csts[hd % 2], nrows[hd % 2], ctk2s[hd % 2]
        P.op("pool", lambda e: e.memset(Cst, 0.0), writes=[ctk2])
        P.op("pool", lambda e: e.memset(nrow, 0.0), writes=[ctk2])
        Cflat = Cst.rearrange("p a b -> p (a b)")
        for cp in range(7):
            cl_, cltk = clst.items[0]
            P.dma("sync", cl_, st_all[cp][:, hd * 2048:(hd + 1) * 2048], writes=[cltk])
            P.op("act", lambda e, cp=cp, cl_=cl_: e.activation(out=cl_, in_=cl_, func=AF.Copy, scale=S2[:, cp, hd:hd + 1]),
                 reads=[cltk, ftk], writes=[cltk])
            P.op("dve", lambda e, cp=cp, cl_=cl_: e.scalar_tensor_tensor(out=Cflat, in0=Cflat, scalar=S1[:, cp, hd:hd + 1], in1=cl_,
                                                                          op0=ALU.mult, op1=ALU.add), reads=[cltk, ftk, ctk2], writes=[ctk2])
            nl_, nltk = clst.items[1]
            P.dma("sync", nl_[:, 0:512], st_all[cp][:, 8192 + hd * 512:8192 + (hd + 1) * 512], writes=[nltk])
            P.op("act", lambda e, cp=cp, nl_=nl_: e.activation(out=nl_[:, 0:512], in_=nl_[:, 0:512], func=AF.Copy, scale=S2[:, cp, hd:hd + 1]),
                 reads=[nltk, ftk], writes=[nltk])
            P.op("dve", lambda e, cp=cp, nl_=nl_: e.scalar_tensor_tensor(out=nrow, in0=nrow, scalar=S1[:, cp, hd:hd + 1], in1=nl_[:, 0:512],
                                                                          op0=ALU.mult, op1=ALU.add), reads=[nltk, ftk, ctk2], writes=[ctk2])

    ul_prev = None
    for hd in range(BH):
        wx3, wxtk = head_setup(hd)
        wz, wztk = ws.load([wzv[:, :, 2048 + hd * 512:2048 + (hd + 1) * 512]])
        wz3 = wz.rearrange("p (c n) -> p c n", c=NC8)
        Cst, nrow, ctk2 = Csts[hd % 2], nrows[hd % 2], ctk2s[hd % 2]
        if hd == 0:
            emit_fold(0)

        def refresh_caug(full):
            for dc in range(4):
                P.op("act", lambda e, dc=dc: e.activation(out=Caug[:, dc, 0:512], in_=Cst[:, dc, :], func=AF.Copy),
                     reads=[ctk2], writes=[caugtk])
            P.op("dve", lambda e: e.tensor_scalar(out=nm, in0=nrow, scalar1=cf[:, L_E0:L_E0 + 1], scalar2=None, op0=ALU.mult),
                 reads=[ctk2, ctk], writes=[caugtk])
            ps, pstk = psB2.next()
            for dc in range(4):
                P.op("pe", lambda e, ps=ps, dc=dc: e.matmul(ps[:, dc:dc + 1], lhsT=nm[:, dc * 128:(dc + 1) * 128], rhs=ones_f[:, 0:1],
                                                            start=True, stop=True), reads=[caugtk, ctk], writes=[pstk], signal=(dc == 3))
            P.op("dve", lambda e, ps=ps: e.tensor_copy(out=ncol, in_=ps[:, 0:4]), reads=[pstk], writes=[caugtk])
            P.op("dve", lambda e: e.tensor_copy(out=Caug[:, :, 512:640], in_=ncol.unsqueeze(2).to_broadcast([128, 4, 128])),
                 reads=[caugtk], writes=[caugtk])

        refresh_caug(True)
        for tg in range(4):
            xm, xmt = front(hd, tg, wx3, wxtk)
            for mc in range(4):
                ps, pstk = psA.next()
                for c in range(NC8):
                    P.op("pe", lambda e, c=c, mc=mc, ps=ps: e.matmul(
                        ps, lhsT=wz3[:, c, mc * 128:(mc + 1) * 128], rhs=hT[:, c, 4 + tg * 512:4 + (tg + 1) * 512],
                        start=(c == 0), stop=(c == NC8 - 1)), reads=[wztk], writes=[pstk], signal=(c == NC8 - 1))
                P.op("act", lambda e, ps=ps, mc=mc: e.activation(out=zs[:, mc, :], in_=ps, func=AF.Silu), reads=[pstk], writes=[zstk])
            for j, dst, sc_ in ((0, qT, 1.0), (1, kT, KSCALE)):
                for dc in range(4):
                    ps, pstk = psA.next()
                    P.op("pe", lambda e, ps=ps, j=j, dc=dc: e.matmul(ps, lhsT=bdh[:, j, dc, :], rhs=xc[:, dc, :], start=True, stop=True),
                         reads=[hdtk, xctk], writes=[pstk])
                    P.op("act", lambda e, ps=ps, dst=dst, dc=dc, sc_=sc_: e.activation(out=dst[:, dc, :], in_=ps, func=AF.Copy, scale=sc_),
                         reads=[pstk], writes=[qktk])
            RS = {}

            def stage_pre(cl):
                nonlocal ul_prev
                ck = tg * 4 + cl
                tsl = slice(cl * 128, (cl + 1) * 128)
                gsl = slice(ck * 128, (ck + 1) * 128)
                sel = cf[:, L_SEL + hd * 128:L_SEL + (hd + 1) * 128]
                rp, rptk = psRP.next()
                for i3, row in enumerate((BETA, MM)):
                    P.op("pe", lambda e, rp=rp, i3=i3, row=row, gsl=gsl: e.matmul(
                        rp[:, i3 * 128:(i3 + 1) * 128], lhsT=sel, rhs=row[:, gsl], start=True, stop=True),
                        reads=[rtk, gtk, ctk], writes=[rptk], signal=(i3 == 1))
                bprev = mrun[:, hd:hd + 1] if ck == 0 else ul_prev[0]
                bprev_tk = ftk if ck == 0 else ul_prev[1]
                scsb, sctk = sc_rot.next()
                P.op("act", lambda e, rp=rp, scsb=scsb, bprev=bprev: e.activation(out=scsb, in_=rp[:, 0:128], func=AF.Exp, scale=-1.0, bias=bprev),
                     reads=[rptk, bprev_tk], writes=[sctk])
                emsb, emtk = em_rot.next()
                P.op("act", lambda e, rp=rp, emsb=emsb: e.activation(out=emsb, in_=rp[:, 128:256], func=AF.Exp, scale=-1.0),
                     reads=[rptk], writes=[emtk])
                ul_prev = ul_rot.next()
                P.op("act", lambda e, rp=rp, ul_prev=ul_prev: e.activation(out=ul_prev[0], in_=rp[:, 127:128], func=AF.Copy),
                     reads=[rptk], writes=[ul_prev[1]])
                ps, pstk = psB2.next()
                for mc in range(4):
                    P.op("pe", lambda e, ps=ps, mc=mc, tsl=tsl: e.matmul(
                        ps[:, mc * 128:(mc + 1) * 128], lhsT=xc[:, mc, tsl], rhs=bdh[:, 1, mc, :], start=True, stop=True),
                        reads=[xctk, hdtk], writes=[pstk], signal=(mc == 3))
                wk, wktk = wk_rot.next()
                P.op("act", lambda e, ps=ps, wk=wk, ck=ck: e.activation(out=wk, in_=ps, func=AF.Copy, scale=TMw[:, ck, hd:hd + 1]),
                     reads=[pstk, tmtk], writes=[wktk])
                ps, pstk = psB2.next()
                for mc in range(4):
                    P.op("pe", lambda e, ps=ps, mc=mc, cl=cl, xm=xm: e.matmul(
                        ps[:, mc * 128:(mc + 1) * 128], lhsT=xm[:, mc, 4 + cl * 128:4 + (cl + 1) * 128], rhs=bdh[:, 2, mc, :],
                        start=True, stop=True), reads=[xmt, hdtk], writes=[pstk], signal=(mc == 3))
                va, vatk = va_rot.next()
                P.op("act", lambda e, ps=ps, va=va: e.activation(out=va[:, 0:512], in_=ps, func=AF.Copy), reads=[pstk], writes=[vatk])
                pS_, pStk = psS3.next()
                pS = pS_[:, 256:384]
                for dc in range(4):
                    P.op("pe", lambda e, pS=pS, dc=dc, tsl=tsl: e.matmul(pS, lhsT=kT[:, dc, tsl], rhs=qT[:, dc, tsl],
                                                                          start=(dc == 0), stop=(dc == 3)),
                         reads=[qktk], writes=[pStk], signal=(dc == 3))
                dtmp, dttk = dt_rot.next()
                P.op("dve", lambda e, rp=rp, dtmp=dtmp, ck=ck: e.scalar_tensor_tensor(
                    out=dtmp, in0=rp[:, 0:128], scalar=TMa[:, ck, hd:hd + 1], in1=cf[:, L_MASKLOW:L_MASKLOW + 128],
                    op0=ALU.subtract, op1=ALU.max), reads=[rptk, tmtk, ctk], writes=[dttk])
                P.op("act", lambda e, dtmp=dtmp: e.activation(out=dtmp, in_=dtmp, func=AF.Exp, scale=-1.0), reads=[dttk], writes=[dttk])
                sd, sdtk = sd_rot.next()
                P.op("dve", lambda e, pS=pS, dtmp=dtmp, sd=sd: e.tensor_tensor(out=sd, in0=pS, in1=dtmp, op=ALU.mult),
                     reads=[pStk, dttk], writes=[sdtk])
                qs, qstk = qs_rot.next()
                P.op("dve", lambda e, scsb=scsb, qs=qs, tsl=tsl: e.tensor_tensor(
                    out=qs, in0=qT[:, :, tsl], in1=scsb.unsqueeze(1).to_broadcast([128, 4, 128]), op=ALU.mult),
                    reads=[qktk, sctk], writes=[qstk])

                RS[cl] = dict(ck=ck, tsl=tsl, wk=wk, wktk=wktk, va=va, vatk=vatk, sd=sd, sdtk=sdtk, qs=qs, qstk=qstk,
                              scsb=scsb, sctk=sctk, emsb=emsb, emtk=emtk)

            def stage_mid(cl):
                r_ = RS[cl]
                ck, tsl, wk, wktk, va, vatk, sd, sdtk, qs, qstk, scsb, sctk = (r_[k_] for k_ in (
                    "ck", "tsl", "wk", "wktk", "va", "vatk", "sd", "sdtk", "qs", "qstk", "scsb", "sctk"))
                pH, pHtk = psH.next()
                pD_, pDtk = psDS.next()
                for ec in range(5):
                    o = pH[:, ec * 128:(ec + 1) * 128] if ec < 4 else pD_[:, 0:128]
                    otk = pHtk if ec < 4 else pDtk
                    for dc in range(4):
                        P.op("pe", lambda e, o=o, ec=ec, dc=dc, qs=qs: e.matmul(
                            o, lhsT=Caug[:, dc, ec * 128:(ec + 1) * 128], rhs=qs[:, dc, :], start=(dc == 0), stop=False),
                            reads=[caugtk, qstk], writes=[otk], signal=False)
                    P.op("pe", lambda e, o=o, ec=ec, va=va, sd=sd: e.matmul(
                        o, lhsT=va[:, ec * 128:(ec + 1) * 128], rhs=sd, start=False, stop=True),
                        reads=[vatk, sdtk], writes=[otk], signal=True)

                r_.update(pH=pH, pHtk=pHtk, pD_=pD_, pDtk=pDtk)
                if dbg_stop is not None and (hd, ck) == tuple(dbg_stop):
                    P.barrier()
                    P.finish()
                    return nc, cx
                decay = scsb[:, 127:128]
                for dc in range(4):
                    ps, pstk = psB2.next()
                    P.op("pe", lambda e, ps=ps, dc=dc, wk=wk, va=va: e.matmul(ps, lhsT=wk[:, dc * 128:(dc + 1) * 128], rhs=va[:, 0:512],
                                                                                start=True, stop=True), reads=[wktk, vatk], writes=[pstk])
                    P.op("dve", lambda e, ps=ps, dc=dc, decay=decay: e.scalar_tensor_tensor(
                        out=Cst[:, dc, :], in0=Cst[:, dc, :], scalar=decay, in1=ps, op0=ALU.mult, op1=ALU.add),
                        reads=[pstk, sctk, ctk2], writes=[ctk2])
                    P.op("dve", lambda e, dc=dc: e.tensor_copy(out=Caug[:, dc, 0:512], in_=Cst[:, dc, :]),
                         reads=[ctk2], writes=[caugtk])
                ps, pstk = psB2.next()
                for dc in range(4):
                    P.op("pe", lambda e, ps=ps, dc=dc, wk=wk: e.matmul(ps[:, dc:dc + 1], lhsT=wk[:, dc * 128:(dc + 1) * 128], rhs=ones_bf[:, 0:1],
                                                                         start=True, stop=True), reads=[wktk, ctk], writes=[pstk], signal=(dc == 3))
                P.op("dve", lambda e, ps=ps, decay=decay: e.scalar_tensor_tensor(out=ncol, in0=ncol, scalar=decay, in1=ps[:, 0:4],
                                                                                  op0=ALU.mult, op1=ALU.add),
                     reads=[pstk, sctk, caugtk], writes=[caugtk])
                P.op("dve", lambda e: e.tensor_copy(out=Caug[:, :, 512:640], in_=ncol.unsqueeze(2).to_broadcast([128, 4, 128])),
                     reads=[caugtk], writes=[caugtk])

            def stage_post(cl):
                r_ = RS[cl]
                ck, tsl, emsb, emtk, pH, pHtk, pD_, pDtk = (r_[k_] for k_ in ("ck", "tsl", "emsb", "emtk", "pH", "pHtk", "pD_", "pDtk"))
                hsq, hsqtk = hsq_rot.next()
                P.op("act", lambda e, pH=pH, hsq=hsq: e.activation(out=hsq, in_=pH, func=AF.Square), reads=[pHtk], writes=[hsqtk])
                pSS_, pSStk = psSS.next()
                pSS = pSS_[:, 0:128]
                for ec in range(4):
                    P.op("pe", lambda e, pSS=pSS, hsq=hsq, ec=ec: e.matmul(pSS, lhsT=ones_bf, rhs=hsq[:, ec * 128:(ec + 1) * 128],
                                                                            start=(ec == 0), stop=(ec == 3)),
                         reads=[hsqtk, ctk], writes=[pSStk], signal=(ec == 3))
                dd, ddtk = dd_rot.next()
                P.op("dve", lambda e, pD_=pD_, dd=dd: e.tensor_scalar(out=dd, in0=pD_[:, 0:128], scalar1=-1.0, scalar2=None, op0=ALU.mult),
                     reads=[pDtk], writes=[ddtk])
                P.op("dve", lambda e, pD_=pD_, dd=dd: e.tensor_tensor(out=dd, in0=dd, in1=pD_[:, 0:128], op=ALU.max),
                     reads=[pDtk, ddtk], writes=[ddtk])
                P.op("dve", lambda e, emsb=emsb, dd=dd: e.tensor_tensor(out=dd, in0=dd, in1=emsb, op=ALU.max),
                     reads=[emtk, ddtk], writes=[ddtk])
                P.op("dve", lambda e, dd=dd: e.scalar_tensor_tensor(out=dd, in0=dd, scalar=EPS, in1=dd, op0=ALU.mult, op1=ALU.mult),
                     reads=[ddtk], writes=[ddtk])
                rr, rrtk = rr_rot.next()
                P.op("dve", lambda e, pSS=pSS, dd=dd, rr=rr: e.scalar_tensor_tensor(out=rr, in0=pSS, scalar=1.0 / DH, in1=dd,
                                                                                     op0=ALU.mult, op1=ALU.add),
                     reads=[pSStk, ddtk], writes=[rrtk])
                P.op("act", lambda e, rr=rr: e.activation(out=rr, in_=rr, func=AF.Sqrt), reads=[rrtk], writes=[rrtk])
                P.op("dve", lambda e, rr=rr: e.reciprocal(out=rr, in_=rr), reads=[rrtk], writes=[rrtk])
                for ec in range(4):
                    ch = hd * 4 + ec
                    tt, tttk = tt_rot.next()
                    P.op("dve", lambda e, pH=pH, ec=ec, ch=ch, rr=rr, tt=tt: e.scalar_tensor_tensor(
                        out=tt, in0=pH[:, ec * 128:(ec + 1) * 128], scalar=cf[:, L_HGAIN + ch:L_HGAIN + ch + 1], in1=rr,
                        op0=ALU.mult, op1=ALU.mult), reads=[pHtk, rrtk, ctk], writes=[tttk])
                    P.op("dve", lambda e, ec=ec, ch=ch, tt=tt, tsl=tsl: e.scalar_tensor_tensor(
                        out=tt, in0=xc[:, ec, tsl], scalar=cf[:, L_SKIP + ch:L_SKIP + ch + 1], in1=tt,
                        op0=ALU.mult, op1=ALU.add), reads=[xctk, tttk, ctk], writes=[tttk])
                    P.op("dve", lambda e, ec=ec, tt=tt, tsl=tsl: e.tensor_tensor(out=yb[:, ec, tsl], in0=tt, in1=zs[:, ec, tsl], op=ALU.mult),
                         reads=[tttk, zstk], writes=[ytk])


            stage_pre(0)
            stage_mid(0)
            for cl in range(1, 4):
                stage_pre(cl)
                stage_post(cl - 1)
                stage_mid(cl)
            stage_post(3)

            if tg == 1 and hd + 1 < BH:
                emit_fold(hd + 1)
            P.dma("sync", yview[:, hd * 4:(hd + 1) * 4, tg * 512:(tg + 1) * 512], yb, reads=[ytk])
    P.barrier()
    cx.release(base_mark)

    X = TOP.rearrange("p (c n) -> p c n", c=NC8)
    Xtk = [[Tk() for _ in range(4)] for _ in range(NC8)]
    for c in range(NC8):
        for tg in range(4):
            P.dma("sync", X[:, c, tg * 512:(tg + 1) * 512], x1T[c * 128:(c + 1) * 128, tg * 512:(tg + 1) * 512], writes=[Xtk[c][tg]])
    mk = cx.mark()
    wdn = cx.sb(None, [128, 16, D], BF16, "wdn")
    wdtk = Tk()
    wdv = wdown.rearrange("(c p) n -> p c n", p=128)
    wstg3 = Rot([cx.sb(None, [128, 4, D], F32, "wstg3") for _ in range(2)])
    for q4 in range(4):
        stg_, stk_ = wstg3.next()
        P.dma("sync", stg_, wdv[:, q4 * 4:(q4 + 1) * 4, :], writes=[stk_])
        P.op("act", lambda e, stg_=stg_, q4=q4: e.activation(out=wdn[:, q4 * 4:(q4 + 1) * 4, :], in_=stg_, func=AF.Copy),
             reads=[stk_], writes=[wdtk])
    yts = [cx.sb(None, [128, 16, 512], BF16, "yt") for _ in range(2)]
    yttk = [Tk(), Tk()]
    psA4 = Rot(bk[0:4])
    for tg in range(4):
        yt, ytt = yts[tg % 2], yttk[tg % 2]
        P.dma("sync", yt, yview[:, :, tg * 512:(tg + 1) * 512], writes=[ytt])
        sl = slice(tg * 512, (tg + 1) * 512)
        for oc in range(NC8):
            ps, pstk = psA4.next()
            for mc in range(16):
                P.op("pe", lambda e, ps=ps, mc=mc, oc=oc, yt=yt: e.matmul(ps, lhsT=wdn[:, mc, oc * 128:(oc + 1) * 128], rhs=yt[:, mc, :],
                                                                           start=(mc == 0), stop=(mc == 15)),
                     reads=[wdtk, ytt], writes=[pstk], signal=(mc == 15))
            P.op("dve", lambda e, ps=ps, oc=oc, sl=sl: e.tensor_tensor(out=X[:, oc, sl], in0=ps, in1=X[:, oc, sl], op=ALU.add),
                 reads=[pstk, Xtk[oc][tg]], writes=[Xtk[oc][tg]])
    P.barrier()
    cx.release(mk)
    if debug:
        emit_store(cx, X, Xtk, dbg_a)
    emit_mlp(cx, X, Xtk, cf[:, L_MLPN:L_MLPN + 8], ctk, w1, w2, ones_bf)
    emit_ple(cx, X, Xtk, cf[:, L_PLEN:L_PLEN + 8], ctk, wg, wp, pT, ones_bf)
    emit_store(cx, X, Xtk, out)
    P.finish()
    return nc, cx


_CACHE = {}


def _prog(key, builder):
    return builder()


def kernel(**inputs):
    inputs = {k: np.asarray(v) for k, v in inputs.items()}
    cores = list(range(NCORES))
    nc, _ = build_layer0()
    in_maps = [layer0_inputs(inputs, c) for c in cores]
    res = run_bass_kernel_spmd(nc, in_maps, core_ids=cores)
    x1T = np.concatenate([r["xout"] for r in res.results], axis=1)
    nc, _ = build_layer1("B")
    in_maps = [layer1_inputs(inputs, c, x1T, "B") for c in cores]
    res = run_bass_kernel_spmd(nc, in_maps, core_ids=cores)
    st_all = np.stack([res.results[c]["st_out"] for c in range(7)])
    g_rows = [res.results[c]["g_out"] for c in cores]
    nc, _ = build_layer1("C")
    in_maps = [layer1_inputs(inputs, c, x1T, "C", st_all, g_rows[c]) for c in cores]
    res = run_bass_kernel_spmd(nc, in_maps, core_ids=cores)
    outT = np.concatenate([r["xout"] for r in res.results], axis=1)
    return np.ascontiguousarray(outT.T)[None].astype(np.float32)
```

```python
import numpy as np
import concourse.bass as bass
import concourse.mybir as mybir
from concourse.bass_utils import run_bass_kernel_spmd

F32 = mybir.dt.float32
BF16 = mybir.dt.bfloat16
AF = mybir.ActivationFunctionType
ALU = mybir.AluOpType
AX = mybir.AxisListType

NCORES = 8
S = 16384
D = 1024
NT = S // NCORES
NC8 = D // 128
EPS = 1e-6
BIG = 30000.0
A_GROUPS = ((128, 1), (512, 4), (2048, 16))
NDMA = 24
SB_F32 = 51968


class Tk:
    __slots__ = ("w", "r")

    def __init__(self):
        self.w = {}
        self.r = {}


class Prog:
    def __init__(self, nc):
        self.nc = nc
        self.eng = {"act": nc.scalar, "dve": nc.vector, "pool": nc.gpsimd, "pe": nc.tensor, "sync": nc.sync}
        self.sem = {e: nc.alloc_semaphore("s_" + e) for e in ("act", "dve", "pool", "pe")}
        self.cnt = {e: 0 for e in ("act", "dve", "pool", "pe")}
        self.seen = {e: {} for e in self.eng}
        self.dsem = [nc.alloc_semaphore("s_dma%d" % i) for i in range(NDMA)]
        self.dcnt = [0] * NDMA
        self.dnext = 0
        self.nins = {e: 0 for e in self.eng}

    def _semof(self, src):
        if isinstance(src, tuple):
            return self.dsem[src[1]]
        return self.sem[src]

    def _deps(self, e, reads, writes, allraw=False):
        deps = {}

        def add(src, n, raw):
            if src == e and not allraw:
                if e == "pe" or not raw:
                    return
            if deps.get(src, 0) < n:
                deps[src] = n

        for t in reads:
            for src, n in t.w.items():
                add(src, n, True)
        for t in writes:
            for src, n in t.w.items():
                add(src, n, False)
            for src, n in t.r.items():
                add(src, n, False)
        return deps

    def _wait(self, e, deps):
        eng = self.eng[e]
        seen = self.seen[e]
        for src, n in deps.items():
            if seen.get(src, 0) >= n:
                continue
            seen[src] = n
            eng.wait_ge(self._semof(src), n)
            self.nins[e] += 1

    def op(self, e, fn, reads=(), writes=(), signal=True):
        self._wait(e, self._deps(e, reads, writes))
        ins = fn(self.eng[e])
        self.nins[e] += 1
        n = self.cnt[e] + 1
        if signal:
            ins.then_inc(self.sem[e], 1)
            self.cnt[e] = n
        for t in reads:
            if t.r.get(e, 0) < n:
                t.r[e] = n
        for t in writes:
            if t.w.get(e, 0) < n:
                t.w[e] = n
        return ins

    def dma(self, q, out, in_, reads=(), writes=()):
        k = self.dnext
        self.dnext = (k + 1) % NDMA
        src = ("dma", k)
        deps = self._deps(q, reads, writes, allraw=True)
        if self.dcnt[k] > 0:
            deps[src] = max(deps.get(src, 0), self.dcnt[k])
        self._wait(q, deps)
        ins = self.eng[q].dma_start(out=out, in_=in_)
        self.nins[q] += 1
        n = self.dcnt[k] + 16
        ins.then_inc(self.dsem[k], 16)
        self.dcnt[k] = n
        for t in reads:
            t.r[src] = n
        for t in writes:
            t.w[src] = n

    def barrier(self):
        for e in self.eng:
            deps = {}
            for s2 in self.cnt:
                if s2 != e and self.cnt[s2] > 0:
                    deps[s2] = self.cnt[s2]
            for k in range(NDMA):
                if self.dcnt[k] > 0:
                    deps[("dma", k)] = self.dcnt[k]
            self._wait(e, deps)

    def finish(self):
        deps = {}
        for k in range(NDMA):
            if self.dcnt[k] > 0:
                deps[("dma", k)] = self.dcnt[k]
        self._wait("sync", deps)


class Rot:
    def __init__(self, aps):
        self.items = [a if isinstance(a, tuple) else (a, Tk()) for a in aps]
        self.i = 0

    def next(self):
        it = self.items[self.i]
        self.i = (self.i + 1) % len(self.items)
        return it


class Ctx:
    def __init__(self, nc):
        self.nc = nc
        self.P = Prog(nc)
        self.banks = [nc.alloc_psum_tensor("psb%d" % i, [128, 512], F32).ap() for i in range(8)]
        self.nalloc = 0

        self.big = nc.alloc_sbuf_tensor("big", [128, SB_F32], F32).ap()
        self.top = 0

    def sb(self, stack, shape, dt, name=None):
        esz = 2 if dt == BF16 else 4
        n = int(np.prod(shape[1:]))
        nbytes = (n * esz + 63) // 64 * 64
        off = self.top
        assert off + nbytes <= SB_F32 * 4, ("SBUF overflow", name, off, nbytes)
        self.top = off + nbytes
        self.log = getattr(self, 'log', [])
        self.log.append((name, off, nbytes))
        ap = self.big[:, off // 4:(off + nbytes) // 4]
        if dt == BF16:
            ap = ap.bitcast(BF16)
        ap = ap[:, 0:n]
        if len(shape) == 3:
            ap = ap.rearrange("p (a b) -> p a b", a=shape[1])
        elif len(shape) == 4:
            ap = ap.rearrange("p (a b c) -> p a b c", a=shape[1], b=shape[2])
        return ap

    def mark(self):
        return self.top

    def release(self, m):
        self.top = m


def load_consts(cx, stack, cst_ap, ncols):
    P = cx.P
    cf = cx.sb(stack, [128, ncols], F32, "cstf")
    cb = cx.sb(stack, [128, C_END_BF], BF16, "cstb")
    tk = Tk()
    P.dma("sync", cf, cst_ap, writes=[tk])
    P.op("dve", lambda e: e.tensor_copy(out=cb, in_=cf[:, 0:C_END_BF]), reads=[tk], writes=[tk])
    return cf, cb, tk


class WStream:
    def __init__(self, cx, stack, nelem, nstage=2, nslot=2):
        self.cx = cx
        self.nelem = nelem
        self.stage = Rot([cx.sb(stack, [128, nelem], F32, "wstg") for _ in range(nstage)])
        self.slots = Rot([cx.sb(stack, [128, nelem], BF16, "wbf") for _ in range(nslot)])

    def load(self, views):
        P = self.cx.P
        stg, stk = self.stage.next()
        wb, wtk = self.slots.next()
        off = 0
        for v in views:
            shp = v.shape
            n = int(np.prod(shp[1:]))
            dst = stg[:, off:off + n]
            if len(shp) == 3:
                dst = dst.rearrange("p (a b) -> p a b", a=shp[1])
            P.dma("sync", dst, v, writes=[stk])
            off += n
        assert off <= self.nelem
        P.op("pool", lambda e: e.tensor_copy(out=wb[:, 0:off], in_=stg[:, 0:off]), reads=[stk], writes=[wtk])
        return wb, wtk


def rms_stats(cx, xs, n, sq_rot, ps_ap, ps_tk, rstd, rstd_tk, ones_bf, ctk, inv_dim):
    P = cx.P
    nx = len(xs)
    for c, (xa, xt) in enumerate(xs):
        sq, sqt = sq_rot.next()
        P.op("act", lambda e, xa=xa, sq=sq: e.activation(out=sq[:, 0:n], in_=xa, func=AF.Square), reads=[xt], writes=[sqt])
        P.op("pe", lambda e, sq=sq, c=c: e.matmul(ps_ap[:, 0:n], lhsT=ones_bf, rhs=sq[:, 0:n], start=(c == 0), stop=(c == nx - 1)),
             reads=[sqt, ctk], writes=[ps_tk])
    P.op("act", lambda e: e.activation(out=rstd[:, 0:n], in_=ps_ap[:, 0:n], func=AF.Sqrt, bias=cx.eps_col, scale=inv_dim),
         reads=[ps_tk, ctk], writes=[rstd_tk])
    P.op("dve", lambda e: e.reciprocal(out=rstd[:, 0:n], in_=rstd[:, 0:n]), reads=[rstd_tk], writes=[rstd_tk])


C_ID, C_ONES, C_BONES, C_DM, C_HONES = 0, 128, 256, 384, 640
C_OZ = 704
C_HZ = 960
C_END_BF = 1216
C_EPS = 1216
C_GAINS = 1217
G0_ANORM = C_GAINS
G0_QG = G0_ANORM + 8
G0_KG = G0_QG + 3
G0_MLPN = G0_KG + 3
G0_PLEN = G0_MLPN + 8
G0_END = G0_PLEN + 8


def base_consts(core, ncols):
    c = np.zeros((128, ncols), np.float32)
    c[:, C_ID:C_ID + 128] = np.eye(128, dtype=np.float32)
    c[:, C_ONES:C_ONES + 128] = 1.0
    c[0:64, C_BONES:C_BONES + 64] = 1.0
    c[64:128, C_BONES + 64:C_BONES + 128] = 1.0
    kk = np.arange(128)[:, None]
    a = np.arange(128)[None, :]
    diag = np.where(kk <= a, a - kk, BIG)
    prev = np.where(kk >= a, 128 + a - kk, BIG)
    c[:, C_DM:C_DM + 128] = diag
    c[:, C_DM + 128:C_DM + 256] = prev
    hv = 0.0 if core == 0 else 1.0
    c[:, C_HONES:C_HONES + 64] = hv
    c[:, C_OZ:C_OZ + 64] = 1.0
    c[:, C_OZ + 128 + 64:C_OZ + 256] = 1.0
    c[:, C_HZ:C_HZ + 64] = hv
    c[:, C_HZ + 128 + 64:C_HZ + 256] = hv
    c[:, C_EPS] = EPS
    return c


def col_layout(v):
    v = np.asarray(v, np.float32).reshape(-1, 128)
    return np.ascontiguousarray(v.T)


def emit_norm_resident(cx, X, Xtk, gcol, ctk, hT, hTtk, sq_rot, rstd_rot, ps_rot, ones_bf):
    P = cx.P
    for tg in range(NT // 512):
        sl = slice(tg * 512, (tg + 1) * 512)
        xs = [(X[:, c, sl], Xtk[c][tg]) for c in range(NC8)]
        ps_ap, ps_tk = ps_rot.next()
        rstd, rtk = rstd_rot.next()
        rms_stats(cx, xs, 512, sq_rot, ps_ap, ps_tk, rstd, rtk, ones_bf, ctk, 1.0 / D)
        for c in range(NC8):
            P.op("dve", lambda e, c=c, sl=sl, rstd=rstd: e.scalar_tensor_tensor(
                out=hT[:, c, sl], in0=X[:, c, sl], scalar=gcol[:, c:c + 1], in1=rstd[:, 0:512],
                op0=ALU.mult, op1=ALU.mult), reads=[Xtk[c][tg], rtk, ctk], writes=[hTtk[c][tg]])


def emit_mlp(cx, X, Xtk, gcol, ctk, w1, w2, ones_bf):
    P = cx.P
    mk = cx.mark()
    st = None
    hT = cx.sb(st, [128, NC8, NT], BF16, "mlp_hT")
    hTtk = [[Tk() for _ in range(4)] for _ in range(NC8)]
    sq_rot = Rot([cx.sb(st, [128, 512], BF16, "sq") for _ in range(4)])
    rstd_rot = Rot([cx.sb(st, [128, 512], F32, "rstd") for _ in range(2)])
    ps_stat = Rot([cx.banks[7]])
    emit_norm_resident(cx, X, Xtk, gcol, ctk, hT, hTtk, sq_rot, rstd_rot, ps_stat, ones_bf)
    ws = WStream(cx, st, 4096, nstage=2, nslot=2)
    hids = [cx.sb(st, [128, 4, NT], BF16, "hid") for _ in range(2)]
    hid_tks = [[[Tk() for _ in range(4)] for _ in range(4)] for _ in range(2)]
    tmp_rot = Rot([cx.sb(st, [128, 512], F32, "rl") for _ in range(3)])
    psA = Rot(cx.banks[0:4])
    psB = Rot(cx.banks[4:7])
    w1v = w1.rearrange("(c p) n -> p c n", p=128)
    w2v = w2.rearrange("(c p) n -> p c n", p=128)
    NHB = 8
    for hb in range(NHB):
        hid = hids[hb % 2]
        htk = hid_tks[hb % 2]
        wa, watk = ws.load([w1v[:, :, hb * 512:(hb + 1) * 512]])
        wa3 = wa.rearrange("p (c n) -> p c n", c=NC8)
        for hc in range(4):
            for tg in range(4):
                sl = slice(tg * 512, (tg + 1) * 512)
                ps, pstk = psA.next()
                for c in range(NC8):
                    P.op("pe", lambda e, c=c, hc=hc, sl=sl, ps=ps, wa3=wa3: e.matmul(
                        ps, lhsT=wa3[:, c, hc * 128:(hc + 1) * 128], rhs=hT[:, c, sl],
                        start=(c == 0), stop=(c == NC8 - 1)),
                        reads=[watk, hTtk[c][tg]], writes=[pstk], signal=(c == NC8 - 1))
                tmp, ttk = tmp_rot.next()
                P.op("act", lambda e, ps=ps, tmp=tmp: e.activation(out=tmp, in_=ps, func=AF.Square),
                     reads=[pstk], writes=[ttk])
                P.op("dve", lambda e, ps=ps, tmp=tmp, hc=hc, sl=sl, hid=hid: e.scalar_tensor_tensor(
                    out=hid[:, hc, sl], in0=ps, scalar=0.0, in1=tmp, op0=ALU.is_gt, op1=ALU.mult),
                    reads=[pstk, ttk], writes=[htk[hc][tg]])
        wb, wbtk = ws.load([w2v[:, hb * 4:(hb + 1) * 4, :]])
        wb3 = wb.rearrange("p (c n) -> p c n", c=4)
        for oc in range(NC8):
            for tg in range(4):
                sl = slice(tg * 512, (tg + 1) * 512)
                ps, pstk = psB.next()
                for hc in range(4):
                    P.op("pe", lambda e, hc=hc, oc=oc, sl=sl, ps=ps, hid=hid, wb3=wb3: e.matmul(
                        ps, lhsT=wb3[:, hc, oc * 128:(oc + 1) * 128], rhs=hid[:, hc, sl],
                        start=(hc == 0), stop=(hc == 3)),
                        reads=[wbtk, htk[hc][tg]], writes=[pstk], signal=(hc == 3))
                P.op("dve", lambda e, oc=oc, sl=sl, ps=ps: e.tensor_tensor(
                    out=X[:, oc, sl], in0=ps, in1=X[:, oc, sl], op=ALU.add),
                    reads=[pstk, Xtk[oc][tg]], writes=[Xtk[oc][tg]])
    P.barrier()
    cx.release(mk)


def emit_ple(cx, X, Xtk, gcol, ctk, wg, wp, pT_dram, ones_bf):
    P = cx.P
    mk = cx.mark()
    st = None
    hT = cx.sb(st, [128, NC8, NT], BF16, "ple_hT")
    hTtk = [[Tk() for _ in range(4)] for _ in range(NC8)]
    sq_rot = Rot([cx.sb(st, [128, 512], BF16, "sq") for _ in range(4)])
    rstd_rot = Rot([cx.sb(st, [128, 512], F32, "rstd") for _ in range(2)])
    ps_stat = Rot([cx.banks[7]])
    emit_norm_resident(cx, X, Xtk, gcol, ctk, hT, hTtk, sq_rot, rstd_rot, ps_stat, ones_bf)
    ws = WStream(cx, st, 4096, nstage=2, nslot=3)
    pst = cx.sb(st, [128, 2, NT], F32, "pstg")
    pb = cx.sb(st, [128, 2, NT], BF16, "pbf")
    ptk = Tk()
    P.dma("sync", pst, pT_dram.rearrange("(c p) n -> p c n", p=128), writes=[ptk])
    P.op("pool", lambda e: e.tensor_copy(out=pb, in_=pst), reads=[ptk], writes=[ptk])
    wpb, wptk = ws.load([wp.rearrange("(c p) n -> p c n", p=128)])
    wp3 = wpb[:, 0:2048].rearrange("p (c n) -> p c n", c=2)
    gt_rot = Rot([cx.sb(st, [128, 512], F32, "gt") for _ in range(3)])
    psA = Rot(cx.banks[0:3])
    psB = Rot(cx.banks[3:6])
    wgv = wg.rearrange("(c p) n -> p c n", p=128)
    for half in range(2):
        wa, watk = ws.load([wgv[:, :, half * 512:(half + 1) * 512]])
        wa3 = wa.rearrange("p (c n) -> p c n", c=NC8)
        for o4 in range(4):
            oc = half * 4 + o4
            for tg in range(4):
                sl = slice(tg * 512, (tg + 1) * 512)
                ps, pstk = psA.next()
                for c in range(NC8):
                    P.op("pe", lambda e, c=c, o4=o4, sl=sl, ps=ps, wa3=wa3: e.matmul(
                        ps, lhsT=wa3[:, c, o4 * 128:(o4 + 1) * 128], rhs=hT[:, c, sl],
                        start=(c == 0), stop=(c == NC8 - 1)),
                        reads=[watk, hTtk[c][tg]], writes=[pstk], signal=(c == NC8 - 1))
                ps2, ps2tk = psB.next()
                for kc in range(2):
                    P.op("pe", lambda e, kc=kc, oc=oc, sl=sl, ps2=ps2: e.matmul(
                        ps2, lhsT=wp3[:, kc, oc * 128:(oc + 1) * 128], rhs=pb[:, kc, sl],
                        start=(kc == 0), stop=(kc == 1)),
                        reads=[wptk, ptk], writes=[ps2tk], signal=(kc == 1))
                gt, gtk = gt_rot.next()
                P.op("act", lambda e, ps=ps, gt=gt: e.activation(out=gt, in_=ps, func=AF.Sigmoid),
                     reads=[pstk], writes=[gtk])
                P.op("dve", lambda e, ps2=ps2, gt=gt: e.tensor_tensor(out=gt, in0=ps2, in1=gt, op=ALU.mult),
                     reads=[ps2tk, gtk], writes=[gtk])
                P.op("dve", lambda e, oc=oc, sl=sl, gt=gt: e.tensor_tensor(
                    out=X[:, oc, sl], in0=gt, in1=X[:, oc, sl], op=ALU.add),
                    reads=[gtk, Xtk[oc][tg]], writes=[Xtk[oc][tg]])
    P.barrier()
    cx.release(mk)


def alibi_slope(h):
    return 2.0 ** (-8.0 * (h + 1) / 16)


def sslice(start, count, step):
    return slice(start, start + (count - 1) * step + 1, step)


def emit_attention(cx, xT_ext, wqkv, wo, cf, cb, ctk, TOP, R1, lvl=9, hps=8):
    P = cx.P
    st = None
    ones_bf = cb[:, C_ONES:C_ONES + 128]
    bones = cb[:, C_BONES:C_BONES + 128]
    Dm = cf[:, C_DM:C_DM + 256]
    hT = TOP.bitcast(BF16).rearrange("p (c n) -> p c n", c=NC8)
    mk = cx.mark()
    sq_rot = Rot([cx.sb(st, [128, 512], BF16, "sq") for _ in range(2)])
    rstd_rot = Rot([cx.sb(st, [128, 512], F32, "rstd") for _ in range(2)])
    xstg = [R1[:, i * 4096:(i + 1) * 4096].rearrange("p (c n) -> p c n", c=NC8) for i in range(2)]
    xstk = [Tk(), Tk()]
    ps_stat = Rot([cx.banks[7], cx.banks[6]])
    httk = Tk()
    gcol = cf[:, G0_ANORM:G0_ANORM + 8]
    for tg in range(8 if lvl >= 1 else 0):
        xa = xstg[tg % 2]
        xt = xstk[tg % 2]
        P.dma("sync", xa, xT_ext[:, tg * 512:(tg + 1) * 512].rearrange("(c p) n -> p c n", p=128), writes=[xt])
        xs = [(xa[:, c, :], xt) for c in range(NC8)]
        ps_ap, ps_tk = ps_stat.next()
        rstd, rtk = rstd_rot.next()
        rms_stats(cx, xs, 512, sq_rot, ps_ap, ps_tk, rstd, rtk, ones_bf, ctk, 1.0 / D)
        for c in range(NC8):
            P.op("dve", lambda e, c=c, tg=tg, xa=xa, rstd=rstd: e.scalar_tensor_tensor(
                out=hT[:, c, tg * 512:(tg + 1) * 512], in0=xa[:, c, :], scalar=gcol[:, c:c + 1], in1=rstd[:, 0:512],
                op0=ALU.mult, op1=ALU.mult), reads=[xt, rtk, ctk], writes=[httk])
    P.op("dve", lambda e: e.tensor_scalar(out=cf[:, G0_QG:G0_QG + 3], in0=cf[:, G0_QG:G0_QG + 3], scalar1=0.125,
                                          scalar2=None, op0=ALU.mult), reads=[ctk], writes=[ctk])
    P.barrier()
    ACC = R1[:, 0:4096].rearrange("p (a n) -> p a n", a=2)
    oT = R1[:, 4096:12288].bitcast(BF16).rearrange("p (c n) -> p c n", c=NC8)
    acctk = Tk()
    ottk = Tk()
    ws = WStream(cx, st, 3072, nstage=1, nslot=2)
    QTz = [cx.sb(st, [128, NT], BF16, "QTz") for _ in range(2)]
    qtk = Tk()
    KT_rot = Rot([cx.sb(st, [128, 2 * NT], BF16, "KT") for _ in range(2)])
    Vz = [cx.sb(st, [128, 32, 128], BF16, "Vz") for _ in range(2)]
    vtk = Tk()
    for e2 in (0, 1):
        P.op("pool", lambda e, e2=e2: e.memset(QTz[e2], 0.0), writes=[qtk])
        P.op("pool", lambda e, e2=e2: e.memset(Vz[e2], 0.0), writes=[vtk])
    onesz = [cb[:, C_OZ:C_OZ + 128], cb[:, C_OZ + 128:C_OZ + 256]]
    honesz = [cb[:, C_HZ:C_HZ + 128], cb[:, C_HZ + 128:C_HZ + 256]]
    tmp_rot = Rot([cx.sb(st, [128, 256], F32, "stmp") for _ in range(4)])
    pt_rots = [Rot([cx.sb(st, [128, 256], BF16, "PT") for _ in range(6)]) for _ in range(2)]
    bk = [(cx.banks[i], Tk()) for i in range(8)]

    def half(i):
        return (bk[i][0][:, 0:256], bk[i][1])

    psQ = Rot(bk[0:2])
    psS = Rot([bk[2]])
    psV = Rot([bk[3]])
    psST0 = Rot([half(4), half(0)])
    psST1 = Rot([half(5), half(1)])
    psND = Rot([half(6), half(7), half(2), half(3)])
    wq_view = wqkv.rearrange("(c p) n -> p c n", p=128)

    def perm(ap2d, d):
        if d == 1:
            return ap2d
        return ap2d.rearrange("p (u r) -> p r u", r=d)

    def proj_piece(w3, wtk, j, e0, n, gain_col, out_buf, out_tk, d, Lx, u0):
        ps, pstk = psQ.next()
        for c in range(NC8):
            P.op("pe", lambda e, c=c, ps=ps: e.matmul(ps[:, 0:n], lhsT=w3[:, j, c, :], rhs=hT[:, c, e0:e0 + n],
                                                      start=(c == 0), stop=(c == NC8 - 1)),
                 reads=[wtk], writes=[pstk], signal=(c == NC8 - 1))
        ps2, ps2tk = psS.next()
        rstd, rtk = rstd_rot.next()
        rms_stats(cx, [(ps[:, 0:n], pstk)], n, sq_rot, ps2, ps2tk, rstd, rtk, bones, ctk, 1.0 / 64)
        outs = out_buf if isinstance(out_buf, list) else [(slice(0, 128), out_buf)]
        for (rows, ob) in outs:
            if d == 1:
                o = ob[rows, u0:u0 + n]
            else:
                o = ob[rows, 0:d * Lx].rearrange("p (r u) -> p r u", r=d)[:, :, u0:u0 + n // d]
            P.op("dve", lambda e, ps=ps, rstd=rstd, o=o, rows=rows: e.scalar_tensor_tensor(
                out=o, in0=perm(ps[rows, 0:n], d), scalar=gain_col[rows, :], in1=perm(rstd[rows, 0:n], d),
                op0=ALU.mult, op1=ALU.mult),
                reads=[pstk, rtk, ctk], writes=[out_tk])

    if lvl < 2:
        hps = 0
        P.op('dve', lambda e: e.memset(R1, 0.0), writes=[ottk])
    for hp in range(hps):
        for g, (W, d) in enumerate(A_GROUPS):
            L = NT // d
            Lk = (W + NT) // d
            nb = Lk // 128
            e_start = NT - W
            base = g * 3072 + hp * 128
            wb, wtk = ws.load([wq_view[:, :, base + j * 1024: base + j * 1024 + 128] for j in range(3)])
            w3 = wb[:, 0:3072].rearrange("p (j c n) -> p j c n", j=3, c=NC8)
            KT, ktk = KT_rot.next()
            for tg in range(4):
                proj_piece(w3, wtk, 0, NT + tg * 512, 512, cf[:, G0_QG + g:G0_QG + g + 1],
                           [(slice(0, 64), QTz[0]), (slice(64, 128), QTz[1])], qtk, d, L, tg * 512 // d)
            pieces = []
            if W < 512:
                pieces.append((e_start, W))
                e = NT
            else:
                e = e_start
            while e < 2 * NT:
                pieces.append((e, 512))
                e += 512
            for (e0, n) in pieces:
                proj_piece(w3, wtk, 1, e0, n, cf[:, G0_KG + g:G0_KG + g + 1], KT, ktk, d, Lk, (e0 - e_start) // d)
            nkb = d * nb if lvl >= 3 else 0
            kb = 0
            while kb < nkb:
                nblk = min(4, nkb - kb)
                psv, psvtk = psV.next()
                for b in range(nblk):
                    r, jb = divmod(kb + b, nb)
                    e_first = e_start + d * 128 * jb + r
                    for c in range(NC8):
                        P.op("pe", lambda e, c=c, b=b, e_first=e_first, psv=psv: e.matmul(
                            psv[:, b * 128:(b + 1) * 128], lhsT=hT[:, c, sslice(e_first, 128, d)], rhs=w3[:, 2, c, :],
                            start=(c == 0), stop=(c == NC8 - 1)),
                            reads=[wtk], writes=[psvtk], signal=(c == NC8 - 1 and b == nblk - 1))
                for e2 in (0, 1):
                    cs = slice(64 * e2, 64 * e2 + 64)
                    P.op("act", lambda e, kb=kb, nblk=nblk, psv=psv, e2=e2, cs=cs: e.activation(
                        out=Vz[e2][:, kb:kb + nblk, cs],
                        in_=psv[:, 0:nblk * 128].rearrange("p (b n) -> p b n", b=nblk)[:, :, cs], func=AF.Copy),
                        reads=[psvtk], writes=[vtk])
                kb += nblk
            PTs = {}

            def score_task(r, jb):
                lo = 128 if jb == 0 else 0
                hi = 128 if jb == nb - 1 else 256
                qb0 = jb if jb == 0 else jb - 1
                q_off = r * L + 128 * qb0
                sTs = [psST0.next(), psST1.next()]
                for e2 in (0, 1):
                    sT, sTtk = sTs[e2]
                    P.op("pe", lambda e, sT=sT, e2=e2: e.matmul(
                        sT[:, lo:hi], lhsT=KT[:, r * Lk + 128 * jb: r * Lk + 128 * jb + 128],
                        rhs=QTz[e2][:, q_off:q_off + (hi - lo)], start=True, stop=True),
                        reads=[ktk, qtk], writes=[sTtk])
                for e2 in (0, 1):
                    sig = alibi_slope(2 * hp + e2) * d
                    sT, sTtk = sTs[e2]
                    tmp, tmtk = tmp_rot.next()
                    P.op("dve", lambda e, sT=sT, tmp=tmp, sig=sig: e.scalar_tensor_tensor(
                        out=tmp[:, lo:hi], in0=Dm[:, lo:hi], scalar=-sig, in1=sT[:, lo:hi],
                        op0=ALU.mult, op1=ALU.add),
                        reads=[sTtk, ctk], writes=[tmtk])
                    pt, pttk = pt_rots[e2].next()
                    P.op("act", lambda e, tmp=tmp, pt=pt: e.activation(
                        out=pt[:, lo:hi], in_=tmp[:, lo:hi], func=AF.Exp), reads=[tmtk], writes=[pttk])
                    PTs[(e2, r, jb)] = (pt, pttk)

            def pv_task(r, j):
                jb = j + 1
                nd, ndtk = psND.next()
                kbp = r * nb + j
                kbd = r * nb + jb
                for part in (0, 1):
                    co = slice(128 * part, 128 * part + 128)
                    for e2 in (0, 1):
                        ptp, ptptk = PTs[(e2, r, j)]
                        ptd, ptdtk = PTs[(e2, r, jb)]
                        if part == 0:
                            lp, ld = Vz[e2][:, kbp, :], Vz[e2][:, kbd, :]
                        else:
                            lp, ld = (honesz[e2] if j == 0 else onesz[e2]), onesz[e2]
                        P.op("pe", lambda e, nd=nd, co=co, lp=lp, ptp=ptp, e2=e2: e.matmul(
                            nd[:, co], lhsT=lp, rhs=ptp[:, 128:256], start=(e2 == 0), stop=False),
                            reads=[vtk, ctk, ptptk], writes=[ndtk], signal=False)
                        P.op("pe", lambda e, nd=nd, co=co, ld=ld, ptd=ptd, e2=e2: e.matmul(
                            nd[:, co], lhsT=ld, rhs=ptd[:, 0:128], start=False, stop=(e2 == 1)),
                            reads=[vtk, ctk, ptdtk], writes=[ndtk], signal=(part == 1 and e2 == 1))
                t0 = r + d * 128 * j
                accv = ACC[:, :, sslice(t0, 128, d)]
                ndv = nd.rearrange("p (a n) -> p a n", a=2)
                if g == 0:
                    P.op("act", lambda e, accv=accv, ndv=ndv: e.activation(out=accv, in_=ndv, func=AF.Copy),
                         reads=[ndtk], writes=[acctk])
                else:
                    P.op("dve", lambda e, accv=accv, ndv=ndv: e.tensor_tensor(out=accv, in0=ndv, in1=accv, op=ALU.add),
                         reads=[ndtk, acctk], writes=[acctk])

            LA = 3
            pending = []
            tasks = [(r, jb) for r in range(d if lvl >= 4 else 0) for jb in range(nb)]
            for i, (r, jb) in enumerate(tasks):
                score_task(r, jb)
                if jb >= 1:
                    pending.append((i, r, jb - 1))
                while pending and pending[0][0] <= i - LA:
                    _, r_, j_ = pending.pop(0)
                    pv_task(r_, j_)
            for (_, r_, j_) in pending:
                pv_task(r_, j_)
        P.op("dve", lambda e: e.reciprocal(out=ACC[:, 1, :], in_=ACC[:, 1, :]), reads=[acctk], writes=[acctk])
        P.op("dve", lambda e, hp=hp: e.tensor_tensor(out=oT[:, hp, :], in0=ACC[:, 0, :], in1=ACC[:, 1, :], op=ALU.mult),
             reads=[acctk], writes=[ottk])
    P.barrier()
    cx.release(mk)
    mk = cx.mark()
    X = TOP.rearrange("p (c n) -> p c n", c=NC8)
    Xtk = [[Tk() for _ in range(4)] for _ in range(NC8)]
    for c in range(NC8):
        for tg in range(4):
            P.dma("sync", X[:, c, tg * 512:(tg + 1) * 512], xT_ext[c * 128:(c + 1) * 128, NT + tg * 512:NT + (tg + 1) * 512],
                  writes=[Xtk[c][tg]])
    ws2 = WStream(cx, st, 4096, nstage=2, nslot=2)
    wov = wo.rearrange("(c p) n -> p c n", p=128)
    psA = Rot(cx.banks[0:4])
    for half in range(2):
        wa, watk = ws2.load([wov[:, :, half * 512:(half + 1) * 512]])
        wa3 = wa.rearrange("p (c n) -> p c n", c=NC8)
        for o4 in range(4):
            oc = half * 4 + o4
            for tg in range(4):
                sl = slice(tg * 512, (tg + 1) * 512)
                ps, pstk = psA.next()
                for c in range(NC8):
                    P.op("pe", lambda e, c=c, o4=o4, sl=sl, ps=ps, wa3=wa3: e.matmul(
                        ps, lhsT=wa3[:, c, o4 * 128:(o4 + 1) * 128], rhs=oT[:, c, sl],
                        start=(c == 0), stop=(c == NC8 - 1)),
                        reads=[watk, ottk], writes=[pstk], signal=(c == NC8 - 1))
                P.op("dve", lambda e, oc=oc, sl=sl, ps=ps: e.tensor_tensor(
                    out=X[:, oc, sl], in0=ps, in1=X[:, oc, sl], op=ALU.add),
                    reads=[pstk, Xtk[oc][tg]], writes=[Xtk[oc][tg]])
    P.barrier()
    cx.release(mk)
    return X, Xtk


def emit_store(cx, X, Xtk, out_dram):
    P = cx.P
    for c in range(NC8):
        P.dma("sync", out_dram[c * 128:(c + 1) * 128, :], X[:, c, :], reads=Xtk[c])


def build_layer0(debug=False, lvl=9, hps=8):
    nc = bass.Bass("TRN2", target_bir_lowering=False)
    xT_ext = nc.dram_tensor("xT_ext", [D, 2 * NT], F32, kind="ExternalInput").ap()
    pT = nc.dram_tensor("pT", [256, NT], F32, kind="ExternalInput").ap()
    cst = nc.dram_tensor("cst", [128, G0_END], F32, kind="ExternalInput").ap()
    wqkv = nc.dram_tensor("a_w_qkv", [D, 9216], F32, kind="ExternalInput").ap()
    wo = nc.dram_tensor("a_w_o", [D, D], F32, kind="ExternalInput").ap()
    w1 = nc.dram_tensor("mlp_w1", [D, 4096], F32, kind="ExternalInput").ap()
    w2 = nc.dram_tensor("mlp_w2", [4096, D], F32, kind="ExternalInput").ap()
    wg = nc.dram_tensor("ple_w_gate", [D, D], F32, kind="ExternalInput").ap()
    wp = nc.dram_tensor("ple_w_proj", [256, D], F32, kind="ExternalInput").ap()
    out = nc.dram_tensor("xout", [D, NT], F32, kind="ExternalOutput").ap()
    if debug:
        dbg_a = nc.dram_tensor("dbg_a", [D, NT], F32, kind="ExternalOutput").ap()
        dbg_m = nc.dram_tensor("dbg_m", [D, NT], F32, kind="ExternalOutput").ap()
    cx = Ctx(nc)
    P = cx.P
    cf, cb, ctk = load_consts(cx, None, cst, G0_END)
    cx.eps_col = cf[:, C_EPS:C_EPS + 1]
    ones_bf = cb[:, C_ONES:C_ONES + 128]
    TOP = cx.sb(None, [128, 16384], F32, "TOP")
    R1 = cx.sb(None, [128, 12288], F32, "R1")
    mk = cx.mark()
    X, Xtk = emit_attention(cx, xT_ext, wqkv, wo, cf, cb, ctk, TOP, R1, lvl=lvl, hps=hps)
    cx.release(mk)
    cx.top = cx.top - 12288 * 4
    if debug:
        emit_store(cx, X, Xtk, dbg_a)
    emit_mlp(cx, X, Xtk, cf[:, G0_MLPN:G0_MLPN + 8], ctk, w1, w2, ones_bf)
    if debug:
        emit_store(cx, X, Xtk, dbg_m)
    emit_ple(cx, X, Xtk, cf[:, G0_PLEN:G0_PLEN + 8], ctk, wg, wp, pT, ones_bf)
    emit_store(cx, X, Xtk, out)
    P.finish()
    return nc, cx


def layer0_inputs(inputs, core):
    x = inputs["x"][0]
    lo = core * NT
    xe = np.zeros((2 * NT, D), np.float32)
    if core > 0:
        xe[:NT] = x[lo - NT:lo]
    xe[NT:] = x[lo:lo + NT]
    c = base_consts(core, G0_END)
    c[:, G0_ANORM:G0_ANORM + 8] = col_layout(inputs["a_norm"][0])
    c[:, G0_QG:G0_QG + 3] = np.tile(inputs["a_q_gain"][0].T, (2, 1))
    c[:, G0_KG:G0_KG + 3] = np.tile(inputs["a_k_gain"][0].T, (2, 1))
    c[:, G0_MLPN:G0_MLPN + 8] = col_layout(inputs["mlp_norm"][0])
    c[:, G0_PLEN:G0_PLEN + 8] = col_layout(inputs["ple_norm"][0])
    return {
        "xT_ext": np.ascontiguousarray(xe.T),
        "pT": np.ascontiguousarray(inputs["p"][0, 0, lo:lo + NT].T),
        "cst": c,
        "a_w_qkv": inputs["a_w_qkv"][0], "a_w_o": inputs["a_w_o"][0],
        "mlp_w1": inputs["mlp_w1"][0], "mlp_w2": inputs["mlp_w2"][0],
        "ple_w_gate": inputs["ple_w_gate"][0], "ple_w_proj": inputs["ple_w_proj"][0],
    }

BH = 4
DH = 512
NCK = NT // 128
KSCALE = DH ** -0.5
NST = 8192 + 2048 + 8

L_BNORM = C_GAINS
L_MLPN = L_BNORM + 8
L_PLEN = L_MLPN + 8
L_CONVW = L_PLEN + 8
L_CONVB = L_CONVW + 64
L_SKIP = L_CONVB + 16
L_HGAIN = L_SKIP + 16
L_BI = L_HGAIN + 16
L_BF = L_BI + 1
L_MASKLOW = L_BF + 1
L_SEL = L_MASKLOW + 128
L_CMASK = L_SEL + 512
L_NEG = L_CMASK + 7
L_E0 = L_NEG + 1
L_LNK = L_E0 + 1
L_CNEG = L_LNK + 1
L_END = L_CNEG + 7


def layer1_consts(inputs, core):
    c = base_consts(core, L_END)
    c[:, L_BNORM:L_BNORM + 8] = col_layout(inputs["b_norm"][0])
    c[:, L_MLPN:L_MLPN + 8] = col_layout(inputs["mlp_norm"][1])
    c[:, L_PLEN:L_PLEN + 8] = col_layout(inputs["ple_norm"][1])
    cw = inputs["b_conv_w"][0]
    c[:, L_CONVW:L_CONVW + 64] = cw.reshape(4, 16, 128).transpose(2, 1, 0).reshape(128, 64)
    c[:, L_CONVB:L_CONVB + 16] = col_layout(inputs["b_conv_b"][0])
    c[:, L_SKIP:L_SKIP + 16] = col_layout(inputs["b_skip"][0])
    c[:, L_HGAIN:L_HGAIN + 16] = col_layout(inputs["b_h_gain"][0])
    bg = inputs["b_b_gate"][0]
    c[0:4, L_BI] = bg[0:4]
    c[0:4, L_BF] = bg[4:8]
    s_ = np.arange(128)[:, None]
    t_ = np.arange(128)[None, :]
    c[:, L_MASKLOW:L_MASKLOW + 128] = np.where(s_ <= t_, 0.0, BIG)
    for hd in range(4):
        c[hd, L_SEL + hd * 128:L_SEL + (hd + 1) * 128] = 1.0
    for cp in range(7):
        c[:, L_CMASK + cp] = 1.0 if cp < core else 0.0
        c[:, L_CNEG + cp] = 0.0 if cp < core else -1e30
    c[:, L_NEG] = -1e30
    c[0, L_E0] = 1.0
    c[:, L_LNK] = np.log(KSCALE)
    return c


def bd_compact(w, transpose=False):
    out = np.zeros((2048, 128), np.float32)
    n = np.arange(512)
    for j in range(4):
        for k in range(4):
            if transpose:
                out[4 * n + k, (4 * n + j) % 128] = w[:, j, k]
            else:
                out[4 * n + j, (4 * n + k) % 128] = w[:, j, k]
    return out


def layer1_inputs(inputs, core, x1T_full, stage, st_all=None, g_in=None):
    lo = core * NT
    xh = np.zeros((D, 4), np.float32)
    if core > 0:
        xh[:, 1:4] = x1T_full[:, lo - 3:lo]
    m = {
        "x1T": np.ascontiguousarray(x1T_full[:, lo:lo + NT]),
        "xh": xh,
        "cst": layer1_consts(inputs, core),
        "b_w_up": inputs["b_w_up"][0],
        "bd": np.stack([bd_compact(inputs["b_w_q"][0]), bd_compact(inputs["b_w_k"][0]), bd_compact(inputs["b_w_v"][0])]),
        "bdT": np.stack([bd_compact(inputs["b_w_q"][0], True), bd_compact(inputs["b_w_k"][0], True),
                         bd_compact(inputs["b_w_v"][0], True)]),
        "w_gate": inputs["b_w_gate"][0],
    }
    if stage == "C":
        m.update({
            "st_all": st_all,
            "g_in": g_in,
            "b_w_down": inputs["b_w_down"][0],
            "pT": np.ascontiguousarray(inputs["p"][1, 0, lo:lo + NT].T),
            "mlp_w1": inputs["mlp_w1"][1], "mlp_w2": inputs["mlp_w2"][1],
            "ple_w_gate": inputs["ple_w_gate"][1], "ple_w_proj": inputs["ple_w_proj"][1],
        })
    return m


def build_layer1(stage, debug=False, dbg_stop=None):
    nc = bass.Bass("TRN2", target_bir_lowering=False)
    x1T = nc.dram_tensor("x1T", [D, NT], F32, kind="ExternalInput").ap()
    xh = nc.dram_tensor("xh", [D, 4], F32, kind="ExternalInput").ap()
    cst = nc.dram_tensor("cst", [128, L_END], F32, kind="ExternalInput").ap()
    wup = nc.dram_tensor("b_w_up", [D, 4096], F32, kind="ExternalInput").ap()
    bd = nc.dram_tensor("bd", [3, 2048, 128], F32, kind="ExternalInput").ap()
    bdT = nc.dram_tensor("bdT", [3, 2048, 128], F32, kind="ExternalInput").ap()
    wgate = nc.dram_tensor("w_gate", [6144, 8], F32, kind="ExternalInput").ap()
    if stage == "B":
        st_out = nc.dram_tensor("st_out", [128, NST], F32, kind="ExternalOutput").ap()
        g_out = nc.dram_tensor("g_out", [8, NT], F32, kind="ExternalOutput").ap()
    else:
        st_all = nc.dram_tensor("st_all", [7, 128, NST], F32, kind="ExternalInput").ap()
        g_in = nc.dram_tensor("g_in", [8, NT], F32, kind="ExternalInput").ap()
        wdown = nc.dram_tensor("b_w_down", [2048, D], F32, kind="ExternalInput").ap()
        pT = nc.dram_tensor("pT", [256, NT], F32, kind="ExternalInput").ap()
        w1 = nc.dram_tensor("mlp_w1", [D, 4096], F32, kind="ExternalInput").ap()
        w2 = nc.dram_tensor("mlp_w2", [4096, D], F32, kind="ExternalInput").ap()
        wg = nc.dram_tensor("ple_w_gate", [D, D], F32, kind="ExternalInput").ap()
        wp = nc.dram_tensor("ple_w_proj", [256, D], F32, kind="ExternalInput").ap()
        out = nc.dram_tensor("xout", [D, NT], F32, kind="ExternalOutput").ap()
        yscr = nc.dram_tensor("yscr", [2048, NT], BF16).ap()
        if debug:
            dbg_a = nc.dram_tensor("dbg_a", [D, NT], F32, kind="ExternalOutput").ap()
    cx = Ctx(nc)
    P = cx.P
    cf, cb, ctk = load_consts(cx, None, cst, L_END)
    cx.eps_col = cf[:, C_EPS:C_EPS + 1]
    ones_bf = cb[:, C_ONES:C_ONES + 128]
    ones_f = cf[:, C_ONES:C_ONES + 128]
    ident_f = cf[:, C_ID:C_ID + 128]
    one_col = cf[:, C_ONES:C_ONES + 1]
    TOP = cx.sb(None, [128, 16384], F32, "TOP")
    hT = TOP[:, 0:8208].bitcast(BF16)[:, 0:8 * 2052].rearrange("p (c n) -> p c n", c=NC8)
    topfree = TOP[:, 8208:16384]
    base_mark = cx.mark()
    bk = [(cx.banks[i], Tk()) for i in range(8)]
    ws = WStream(cx, None, 4096, nstage=0, nslot=3)
    ws.stage = Rot([topfree[:, 0:4096]])
    bdh = cx.sb(None, [128, 3, 4, 128], BF16, "bdh")
    bdh_st = cx.sb(None, [128, 3, 4, 128], F32, "bdh_st")
    diag = cx.sb(None, [128, 4, 4, 128], BF16, "diag")
    xms = [cx.sb(None, [128, 4, 516], BF16, "xm") for _ in range(2)]
    xc = cx.sb(None, [128, 4, 512], BF16, "xc")
    GF = cx.sb(None, [128, NT], F32, "GF")
    BETAx = cx.sb(None, [128, NT + 1], F32, "BETAx")
    small = cx.sb(None, [128, 64], F32, "small")
    TMw = cx.sb(None, [128, NCK, 4], F32, "TMw")
    TMa = cx.sb(None, [128, NCK, 4], F32, "TMa")
    if stage == "C":
        fold = cx.sb(None, [128, 7, 8], F32, "foldin")
        S1 = cx.sb(None, [128, 7, 4], F32, "S1")
        S2 = cx.sb(None, [128, 7, 4], F32, "S2")
        mrun = cx.sb(None, [128, 4], F32, "mrun")
        fa = cx.sb(None, [128, 4], F32, "fa")
        fb = cx.sb(None, [128, 4], F32, "fb")
        fc_ = cx.sb(None, [128, 4], F32, "fc")
    pers_mark = cx.mark()

    mk = cx.mark()
    sq_rot = Rot([cx.sb(None, [128, 512], BF16, "sq") for _ in range(2)])
    rstd_rot = Rot([cx.sb(None, [128, 512], F32, "rstd") for _ in range(2)])
    xstg = [cx.sb(None, [128, NC8, 512], F32, "xstg") for _ in range(2)]
    xstk = [Tk(), Tk()]
    ps_stat = Rot([bk[7], bk[6]])
    gcol = cf[:, L_BNORM:L_BNORM + 8]
    httk = Tk()
    pieces = [(None, 4)] + [(tg, 512) for tg in range(4)]
    for i, (tg, n) in enumerate(pieces):
        xa = xstg[i % 2]
        xt = xstk[i % 2]
        if tg is None:
            P.dma("sync", xa[:, :, 0:4], xh.rearrange("(c p) n -> p c n", p=128), writes=[xt])
            h0 = 0
        else:
            P.dma("sync", xa, x1T[:, tg * 512:(tg + 1) * 512].rearrange("(c p) n -> p c n", p=128), writes=[xt])
            h0 = 4 + tg * 512
        xs = [(xa[:, c, 0:n], xt) for c in range(NC8)]
        ps_ap, ps_tk = ps_stat.next()
        rstd, rtk = rstd_rot.next()
        rms_stats(cx, xs, n, sq_rot, ps_ap, ps_tk, rstd, rtk, ones_bf, ctk, 1.0 / D)
        for c in range(NC8):
            P.op("dve", lambda e, c=c, xa=xa, rstd=rstd, n=n, h0=h0: e.scalar_tensor_tensor(
                out=hT[:, c, h0:h0 + n], in0=xa[:, c, 0:n], scalar=gcol[:, c:c + 1], in1=rstd[:, 0:n],
                op0=ALU.mult, op1=ALU.mult), reads=[xt, rtk, ctk], writes=[httk])
    P.barrier()
    cx.release(mk)

    GI = cx.sb(None, [128, NT], F32, "GI")
    LF = cx.sb(None, [128, NT], F32, "LF")
    BB = cx.sb(None, [128, NT], F32, "BB")
    T1 = cx.sb(None, [128, NT], F32, "T1")
    wfold = [[cx.sb(None, [128, 16, 128], BF16, "wfold") for _ in range(2)] for _ in range(2)]
    wftk = Tk()
    bdT_sb = topfree[:, 0:6144].rearrange("p (a b) -> p a b", a=48)
    wg_sb = topfree[:, 6144:6528].rearrange("p (a b) -> p a b", a=48)
    btk = Tk()
    for j in range(3 if stage == "B" else 0):
        P.dma("sync", bdT_sb[:, j * 16:(j + 1) * 16, :], bdT[j].rearrange("(c p) n -> p c n", p=128), writes=[btk])
    if stage == "B":
        P.dma("sync", wg_sb, wgate.rearrange("(c p) n -> p c n", p=128), writes=[btk])
    zpad = Rot([topfree[:, 6528 + i * 128:6528 + (i + 1) * 128] for i in range(4)])
    for (za, ztk) in zpad.items:
        P.op("pool", lambda e, za=za: e.memset(za, 0.0), writes=[ztk])
    psF = Rot(bk[0:2])
    for mc in range(16 if stage == "B" else 0):
        for part in range(2):
            for xm_ in range(2):
                ps, pstk = psF.next()
                srcs = (0, 1) if xm_ == 0 else (2,)
                for si, j in enumerate(srcs):
                    za, ztk = zpad.next()
                    P.op("dve", lambda e, za=za, j=j, mc=mc, part=part: e.tensor_copy(
                        out=za[:, 0:4], in_=wg_sb[:, j * 16 + mc, part * 4:part * 4 + 4]), reads=[btk], writes=[ztk])
                    P.op("pe", lambda e, ps=ps, za=za, j=j, mc=mc, si=si, srcs=srcs: e.matmul(
                        ps[:, 0:128], lhsT=bdT_sb[:, j * 16 + mc, :], rhs=za, start=(si == 0), stop=(si == len(srcs) - 1)),
                        reads=[btk, ztk], writes=[pstk])
                P.op("act", lambda e, ps=ps, xm_=xm_, part=part, mc=mc: e.activation(
                    out=wfold[xm_][part][:, mc, :], in_=ps[:, 0:128], func=AF.Copy), reads=[pstk], writes=[wftk])
    P.barrier()

    hdtk = Tk()
    xmtk = [Tk(), Tk()]
    xctk = Tk()
    psA = Rot(bk[0:2])
    wupv = wup.rearrange("(c p) n -> p c n", p=128)
    bdv = bd.rearrange("j (c p) n -> p j c n", p=128)
    state = {"i": 0}

    def head_setup(hd):
        for j in range(3):
            P.dma("sync", bdh_st[:, j, :, :], bdv[:, j, hd * 4:(hd + 1) * 4, :], writes=[hdtk])
        P.op("pool", lambda e: e.tensor_copy(out=bdh, in_=bdh_st), reads=[hdtk], writes=[hdtk])
        for mc in range(4):
            for k in range(4):
                col = L_CONVW + (hd * 4 + mc) * 4 + k
                P.op("act", lambda e, mc=mc, k=k, col=col: e.activation(
                    out=diag[:, mc, k, :], in_=ident_f, func=AF.Copy, scale=cf[:, col:col + 1]),
                    reads=[ctk], writes=[hdtk])
        wx, wxtk = ws.load([wupv[:, :, hd * 512:(hd + 1) * 512]])
        return wx.rearrange("p (c n) -> p c n", c=NC8), wxtk

    def front(hd, tg, wx3, wxtk):
        i = state["i"]
        state["i"] += 1
        xm, xmt = xms[i % 2], xmtk[i % 2]
        xmp, xmpt = xms[(i + 1) % 2], xmtk[(i + 1) % 2]
        for mc in range(4):
            ps, pstk = psA.next()
            for c in range(NC8):
                P.op("pe", lambda e, c=c, mc=mc, ps=ps: e.matmul(
                    ps, lhsT=wx3[:, c, mc * 128:(mc + 1) * 128], rhs=hT[:, c, 4 + tg * 512:4 + (tg + 1) * 512],
                    start=(c == 0), stop=(c == NC8 - 1)), reads=[wxtk], writes=[pstk], signal=(c == NC8 - 1))
            P.op("act", lambda e, ps=ps, mc=mc, xm=xm: e.activation(out=xm[:, mc, 4:516], in_=ps, func=AF.Copy),
                 reads=[pstk], writes=[xmt])
            if tg == 0:
                ps, pstk = psA.next()
                for c in range(NC8):
                    P.op("pe", lambda e, c=c, mc=mc, ps=ps: e.matmul(
                        ps[:, 0:4], lhsT=wx3[:, c, mc * 128:(mc + 1) * 128], rhs=hT[:, c, 0:4],
                        start=(c == 0), stop=(c == NC8 - 1)), reads=[wxtk], writes=[pstk], signal=(c == NC8 - 1))
                P.op("act", lambda e, ps=ps, mc=mc, xm=xm: e.activation(out=xm[:, mc, 0:4], in_=ps[:, 0:4], func=AF.Copy),
                     reads=[pstk], writes=[xmt])
        if tg > 0:
            P.op("pool", lambda e, xm=xm, xmp=xmp: e.tensor_copy(out=xm[:, :, 0:4], in_=xmp[:, :, 512:516]),
                 reads=[xmpt], writes=[xmt])
        for mc in range(4):
            ps, pstk = psA.next()
            for k in range(4):
                P.op("pe", lambda e, k=k, mc=mc, ps=ps, xm=xm: e.matmul(
                    ps, lhsT=diag[:, mc, k, :], rhs=xm[:, mc, 1 + k:1 + k + 512], start=(k == 0), stop=(k == 3)),
                    reads=[hdtk, xmt], writes=[pstk], signal=(k == 3))
            col = L_CONVB + hd * 4 + mc
            P.op("act", lambda e, ps=ps, mc=mc, col=col: e.activation(
                out=xc[:, mc, :], in_=ps, func=AF.Silu, bias=cf[:, col:col + 1]), reads=[pstk, ctk], writes=[xctk])
        return xm, xmt

    gtk = Tk()
    psG = Rot(bk[2:4])
    if stage == "C":
        P.op("pool", lambda e: e.memset(GI, 0.0), writes=[gtk])
        P.op("pool", lambda e: e.memset(GF, 0.0), writes=[gtk])
        P.dma("sync", GI[0:4, :], g_in[0:4, :], writes=[gtk])
        P.dma("sync", GF[0:4, :], g_in[4:8, :], writes=[gtk])
    for hd in range(BH if stage == "B" else 0):
        wx3, wxtk = head_setup(hd)
        for tg in range(4):
            xm, xmt = front(hd, tg, wx3, wxtk)
            for part, Grow in ((0, GI), (1, GF)):
                ps, pstk = psG.next()
                for mc in range(4):
                    P.op("pe", lambda e, ps=ps, mc=mc, part=part: e.matmul(
                        ps, lhsT=wfold[0][part][:, hd * 4 + mc, :], rhs=xc[:, mc, :], start=(mc == 0), stop=False),
                        reads=[wftk, xctk], writes=[pstk], signal=False)
                    P.op("pe", lambda e, ps=ps, mc=mc, part=part, xm=xm: e.matmul(
                        ps, lhsT=wfold[1][part][:, hd * 4 + mc, :], rhs=xm[:, mc, 4:516], start=False, stop=(mc == 3)),
                        reads=[wftk, xmt], writes=[pstk], signal=(mc == 3))
                sl = slice(tg * 512, (tg + 1) * 512)
                if hd == 0:
                    P.op("act", lambda e, ps=ps, Grow=Grow, sl=sl: e.activation(out=Grow[:, sl], in_=ps, func=AF.Copy),
                         reads=[pstk], writes=[gtk])
                else:
                    P.op("dve", lambda e, ps=ps, Grow=Grow, sl=sl: e.tensor_tensor(out=Grow[:, sl], in0=ps, in1=Grow[:, sl], op=ALU.add),
                         reads=[pstk, gtk], writes=[gtk])

    rtk = Tk()
    if stage == "B":
        P.dma("sync", g_out[0:4, :], GI[0:4, :], reads=[gtk])
        P.dma("sync", g_out[4:8, :], GF[0:4, :], reads=[gtk])
    P.op("dve", lambda e: e.tensor_scalar(out=GI, in0=GI, scalar1=cf[:, L_BI:L_BI + 1], scalar2=None, op0=ALU.add),
         reads=[gtk, ctk], writes=[gtk])
    P.op("dve", lambda e: e.tensor_scalar(out=GF, in0=GF, scalar1=cf[:, L_BF:L_BF + 1], scalar2=None, op0=ALU.add),
         reads=[gtk, ctk], writes=[gtk])
    P.op("dve", lambda e: e.tensor_scalar(out=T1, in0=GF, scalar1=-1.0, scalar2=None, op0=ALU.mult), reads=[gtk], writes=[rtk])
    P.op("dve", lambda e: e.tensor_tensor(out=T1, in0=T1, in1=GF, op=ALU.max), reads=[gtk, rtk], writes=[rtk])
    P.op("act", lambda e: e.activation(out=T1, in_=T1, func=AF.Exp, scale=-1.0), reads=[rtk], writes=[rtk])
    P.op("act", lambda e: e.activation(out=T1, in_=T1, func=AF.Ln, bias=one_col), reads=[rtk, ctk], writes=[rtk])
    P.op("dve", lambda e: e.scalar_tensor_tensor(out=LF, in0=GF, scalar=0.0, in1=T1, op0=ALU.min, op1=ALU.subtract),
         reads=[gtk, rtk], writes=[rtk])
    P.op("pool", lambda e: e.memset(T1, 1.0), reads=[rtk], writes=[rtk])
    P.op("dve", lambda e: e.tensor_tensor_scan(out=BB, data0=T1, data1=LF, initial=0.0, op0=ALU.mult, op1=ALU.add),
         reads=[rtk], writes=[rtk])
    P.op("dve", lambda e: e.tensor_tensor(out=T1, in0=GI, in1=BB, op=ALU.subtract), reads=[gtk, rtk], writes=[rtk])
    ALPHA = T1
    psR = Rot([bk[4]])
    tmtk = Tk()

    def to_token_major(row, dst):
        ps, pstk = psR.next()
        for ck in range(NCK):
            P.op("pe", lambda e, ps=ps, ck=ck: e.matmul(ps[:, ck * 4:ck * 4 + 4], lhsT=row[:, ck * 128:(ck + 1) * 128],
                                                        rhs=ident_f[:, 0:4], start=True, stop=True),
                 reads=[rtk, gtk, ctk], writes=[pstk], signal=(ck == NCK - 1))
        P.op("act", lambda e, ps=ps: e.activation(out=dst, in_=ps[:, 0:64].rearrange("p (a b) -> p a b", a=NCK), func=AF.Copy),
             reads=[pstk], writes=[tmtk])

    def replicate_cols(col_ap, dst4):
        ps, pstk = psR.next()
        za = small[:, 32:32 + 4]
        P.op("dve", lambda e: e.tensor_scalar(out=za, in0=ident_f[:, 0:4], scalar1=col_ap, scalar2=None, op0=ALU.mult),
             reads=[rtk, ctk, gtk], writes=[rtk])
        P.op("pe", lambda e, ps=ps: e.matmul(ps[:, 0:4], lhsT=ones_f, rhs=za, start=True, stop=True),
             reads=[rtk, ctk], writes=[pstk])
        P.op("act", lambda e, ps=ps: e.activation(out=dst4, in_=ps[:, 0:4], func=AF.Copy), reads=[pstk], writes=[rtk])

    if stage == "B":
        mx = small[:, 0:1]
        P.op("dve", lambda e: e.tensor_reduce(out=mx, in_=ALPHA, axis=AX.X, op=ALU.max), reads=[rtk], writes=[rtk])
        nb_ = small[:, 1:2]
        P.op("dve", lambda e: e.scalar_tensor_tensor(out=nb_, in0=mx, scalar=-1.0, in1=cf[:, L_LNK:L_LNK + 1],
                                                     op0=ALU.mult, op1=ALU.add), reads=[rtk, ctk], writes=[rtk])
        P.op("act", lambda e: e.activation(out=LF, in_=ALPHA, func=AF.Exp, bias=nb_), reads=[rtk], writes=[rtk])
        to_token_major(LF, TMw)
        ml = small[:, 2:3]
        P.op("dve", lambda e: e.tensor_tensor(out=ml, in0=mx, in1=BB[:, NT - 1:NT], op=ALU.add), reads=[rtk], writes=[rtk])
        fin = cx.sb(None, [128, 8], F32, "fin")
        replicate_cols(BB[:, NT - 1:NT], fin[:, 0:4])
        replicate_cols(ml, fin[:, 4:8])
        P.dma("sync", st_out[:, 10240:10248], fin, reads=[rtk])
        P.barrier()
        cx.release(pers_mark)
        kv_rot = Rot([cx.sb(None, [128, 512], BF16, "kv") for _ in range(4)])
        stC = cx.sb(None, [128, 4, 512], F32, "stC")
        stn = cx.sb(None, [128, 512], F32, "stn")
        sttk = Tk()
        psKV = Rot([bk[0], bk[1], bk[2]])
        for hd in range(BH):
            wx3, wxtk = head_setup(hd)
            cacc = [bk[3 + dc] for dc in range(4)]
            nacc, nacctk = bk[7]
            for tg in range(4):
                xm, xmt = front(hd, tg, wx3, wxtk)
                KV = {}

                def s1_pre(cl):
                    ck = tg * 4 + cl
                    tsl = slice(cl * 128, (cl + 1) * 128)
                    ps, pstk = psKV.next()
                    for mc in range(4):
                        P.op("pe", lambda e, ps=ps, mc=mc, tsl=tsl: e.matmul(
                            ps[:, mc * 128:(mc + 1) * 128], lhsT=xc[:, mc, tsl], rhs=bdh[:, 1, mc, :], start=True, stop=True),
                            reads=[xctk, hdtk], writes=[pstk], signal=(mc == 3))
                    wk, wktk = kv_rot.next()
                    P.op("act", lambda e, ps=ps, wk=wk, ck=ck, hd=hd: e.activation(
                        out=wk, in_=ps, func=AF.Copy, scale=TMw[:, ck, hd:hd + 1]), reads=[pstk, tmtk], writes=[wktk])
                    ps, pstk = psKV.next()
                    for mc in range(4):
                        P.op("pe", lambda e, ps=ps, mc=mc, cl=cl, xm=xm: e.matmul(
                            ps[:, mc * 128:(mc + 1) * 128], lhsT=xm[:, mc, 4 + cl * 128:4 + (cl + 1) * 128], rhs=bdh[:, 2, mc, :],
                            start=True, stop=True), reads=[xmt, hdtk], writes=[pstk], signal=(mc == 3))
                    vv, vtk = kv_rot.next()
                    P.op("act", lambda e, ps=ps, vv=vv: e.activation(out=vv, in_=ps, func=AF.Copy), reads=[pstk], writes=[vtk])
                    KV[cl] = (ck, wk, wktk, vv, vtk)

                def s1_acc(cl):
                    ck, wk, wktk, vv, vtk = KV[cl]
                    last = (ck == NCK - 1)
                    for dc in range(4):
                        P.op("pe", lambda e, dc=dc, wk=wk, vv=vv, ck=ck, last=last: e.matmul(
                            cacc[dc][0], lhsT=wk[:, dc * 128:(dc + 1) * 128], rhs=vv, start=(ck == 0), stop=last),
                            reads=[wktk, vtk], writes=[cacc[dc][1]], signal=True)
                    P.op("pe", lambda e, wk=wk, ck=ck, last=last: e.matmul(
                        nacc, lhsT=ones_bf, rhs=wk, start=(ck == 0), stop=last), reads=[wktk, ctk], writes=[nacctk], signal=True)
                s1_pre(0)
                for cl in range(1, 4):
                    s1_pre(cl)
                    s1_acc(cl - 1)
                s1_acc(3)
            for dc in range(4):
                P.op("act", lambda e, dc=dc: e.activation(out=stC[:, dc, :], in_=cacc[dc][0], func=AF.Copy),
                     reads=[cacc[dc][1]], writes=[sttk])
            P.op("dve", lambda e: e.tensor_copy(out=stn, in_=nacc), reads=[nacctk], writes=[sttk])
            P.dma("sync", st_out[:, hd * 2048:(hd + 1) * 2048], stC.rearrange("p a b -> p (a b)"), reads=[sttk])
            P.dma("sync", st_out[:, 8192 + hd * 512:8192 + (hd + 1) * 512], stn, reads=[sttk])
        P.finish()
        return nc, cx

    ftk = Tk()
    P.dma("sync", fold, st_all[:, :, 10240:10248].rearrange("c p n -> p c n"), writes=[ftk])
    negc = cf[:, L_NEG:L_NEG + 1]
    P.op("dve", lambda e: e.memset(mrun, -1e30), writes=[ftk])
    for cp in range(7):
        mu = cf[:, L_CMASK + cp:L_CMASK + cp + 1]
        P.op("dve", lambda e, cp=cp, mu=mu: e.scalar_tensor_tensor(out=fa, in0=fold[:, cp, 0:4], scalar=mu, in1=mrun,
                                                                    op0=ALU.mult, op1=ALU.add), reads=[ftk, ctk], writes=[ftk])
        P.op("dve", lambda e, cp=cp, mu=mu: e.tensor_scalar(out=fb, in0=fold[:, cp, 4:8], scalar1=mu,
                                                            scalar2=cf[:, L_CNEG + cp:L_CNEG + cp + 1], op0=ALU.mult, op1=ALU.add),
             reads=[ftk, ctk], writes=[ftk])
        P.op("dve", lambda e: e.tensor_tensor(out=fc_, in0=fa, in1=fb, op=ALU.max), reads=[ftk], writes=[ftk])
        P.op("dve", lambda e: e.tensor_tensor(out=fa, in0=fa, in1=fc_, op=ALU.subtract), reads=[ftk], writes=[ftk])
        P.op("dve", lambda e: e.tensor_tensor(out=fb, in0=fb, in1=fc_, op=ALU.subtract), reads=[ftk], writes=[ftk])
        P.op("act", lambda e, cp=cp: e.activation(out=S1[:, cp, :], in_=fa, func=AF.Exp), reads=[ftk], writes=[ftk])
        P.op("act", lambda e: e.activation(out=fb, in_=fb, func=AF.Exp), reads=[ftk], writes=[ftk])
        P.op("dve", lambda e, cp=cp, mu=mu: e.tensor_scalar(out=S2[:, cp, :], in0=fb, scalar1=mu, scalar2=None, op0=ALU.mult),
             reads=[ftk, ctk], writes=[ftk])
        P.op("dve", lambda e: e.tensor_copy(out=mrun, in_=fc_), reads=[ftk], writes=[ftk])
    mst = small[:, 4:5]
    P.op("dve", lambda e: e.tensor_tensor(out=small[:, 8:12], in0=mrun, in1=ident_f[:, 0:4], op=ALU.mult), reads=[ftk, ctk], writes=[rtk])
    P.op("dve", lambda e: e.tensor_reduce(out=mst, in_=small[:, 8:12], axis=AX.X, op=ALU.add), reads=[rtk], writes=[rtk])
    P.op("dve", lambda e: e.tensor_tensor_scan(out=GF, data0=LF, data1=GI, initial=mst, op0=ALU.add, op1=ALU.max),
         reads=[rtk, gtk], writes=[gtk])
    MM = GF
    P.op("dve", lambda e: e.tensor_tensor(out=BETAx[:, 1:NT + 1], in0=MM, in1=BB, op=ALU.subtract), reads=[gtk, rtk], writes=[rtk])
    P.op("dve", lambda e: e.tensor_copy(out=BETAx[:, 0:1], in_=mst), reads=[rtk], writes=[rtk])
    BETA = BETAx[:, 1:NT + 1]
    for ck in range(NCK):
        bl = small[:, 16:17]
        P.op("dve", lambda e, ck=ck: e.scalar_tensor_tensor(out=small[:, 16 + ck % 8:17 + ck % 8], in0=BETAx[:, 128 * (ck + 1):128 * (ck + 1) + 1],
                                                            scalar=-1.0, in1=cf[:, L_LNK:L_LNK + 1], op0=ALU.mult, op1=ALU.add),
             reads=[rtk, ctk], writes=[rtk])
        P.op("act", lambda e, ck=ck: e.activation(out=LF[:, ck * 128:(ck + 1) * 128], in_=ALPHA[:, ck * 128:(ck + 1) * 128],
                                                  func=AF.Exp, bias=small[:, 16 + ck % 8:17 + ck % 8]), reads=[rtk], writes=[rtk])
    to_token_major(LF, TMw)
    to_token_major(ALPHA, TMa)
    P.barrier()
    cx.release(pers_mark)
    BETA = BETAx[:, 1:NT + 1]

    qT = cx.sb(None, [128, 4, 512], BF16, "qT")
    kT = cx.sb(None, [128, 4, 512], BF16, "kT")
    zs = cx.sb(None, [128, 4, 512], BF16, "zs")
    yb = cx.sb(None, [128, 4, 512], BF16, "yb")
    qktk, zstk, ytk = Tk(), Tk(), Tk()
    Csts = [cx.sb(None, [128, 4, 512], F32, "Cst") for _ in range(2)]
    Caug = cx.sb(None, [128, 4, 640], BF16, "Caug")
    nrows = [cx.sb(None, [128, 512], F32, "nrow") for _ in range(2)]
    nm = cx.sb(None, [128, 512], F32, "nm")
    ncol = cx.sb(None, [128, 4], F32, "ncol")
    ctk2s = [Tk(), Tk()]
    caugtk = Tk()
    clst = Rot([topfree[:, 4096:6144], topfree[:, 6144:8176][:, 0:2032]])
    wk_rot = Rot([cx.sb(None, [128, 512], BF16, "wk") for _ in range(2)])
    va_rot = Rot([cx.sb(None, [128, 640], BF16, "vaug") for _ in range(2)])
    for (va, vatk) in va_rot.items:
        P.op("pool", lambda e, va=va: e.memset(va[:, 512:640], 1.0), writes=[vatk])
    dt_rot = Rot([cx.sb(None, [128, 128], F32, "dtmp") for _ in range(2)])
    sd_rot = Rot([cx.sb(None, [128, 128], BF16, "SdT") for _ in range(2)])
    qs_rot = Rot([cx.sb(None, [128, 4, 128], BF16, "qs") for _ in range(2)])
    hsq_rot = Rot([cx.sb(None, [128, 512], BF16, "hsq") for _ in range(2)])
    dd_rot = Rot([cx.sb(None, [128, 128], F32, "dd") for _ in range(2)])
    rr_rot = Rot([cx.sb(None, [128, 128], F32, "rr") for _ in range(2)])
    sc_rot = Rot([cx.sb(None, [128, 128], F32, "scsb") for _ in range(2)])
    em_rot = Rot([cx.sb(None, [128, 128], F32, "emsb") for _ in range(2)])
    ul_rot = Rot([cx.sb(None, [128, 1], F32, "ulast") for _ in range(2)])
    tt_rot = Rot([cx.sb(None, [128, 128], F32, "tt") for _ in range(3)])
    psB2 = psA
    psS3 = Rot([bk[2]])
    psRP = Rot([bk[2]])
    psH = Rot([bk[4], bk[5]])
    psDS = Rot([bk[6], bk[7]])
    psSS = Rot([bk[3]])
    wzv = wupv
    yview = yscr.rearrange("(c p) n -> p c n", p=128)
    def emit_fold(hd):
        Cst, nrow, ctk2 = Csts[hd % 2], nrows[hd % 2], ctk2s[hd % 2]
        P.op("pool", lambda e: e.memset(Cst, 0.0), writes=[ctk2])
        P.op("pool", lambda e: e.memset(nrow, 0.0), writes=[ctk2])
        Cflat = Cst.rearrange("p a b -> p (a b)")
        for cp in range(7):
            cl_, cltk = clst.items[0]
            P.dma("sync", cl_, st_all[cp][:, hd * 2048:(hd + 1) * 2048], writes=[cltk])
            P.op("act", lambda e, cp=cp, cl_=cl_: e.activation(out=cl_, in_=cl_, func=AF.Copy, scale=S2[:, cp, hd:hd + 1]),
                 reads=[cltk, ftk], writes=[cltk])
            P.op("dve", lambda e, cp=cp, cl_=cl_: e.scalar_tensor_tensor(out=Cflat, in0=Cflat, scalar=S1[:, cp, hd:hd + 1], in1=cl_,
                                                                          op0=ALU.mult, op1=ALU.add), reads=[cltk, ftk, ctk2], writes=[ctk2])
            nl_, nltk = clst.items[1]
            P.dma("sync", nl_[:, 0:512], st_all[cp][:, 8192 + hd * 512:8192 + (hd + 1) * 512], writes=[nltk])
            P.op("act", lambda e, cp=cp, nl_=nl_: e.activation(out=nl_[:, 0:512], in_=nl_[:, 0:512], func=AF.Copy, scale=S2[:, cp, hd:hd + 1]),
                 reads=[nltk, ftk], writes=[nltk])
            P.op("dve", lambda e, cp=cp, nl_=nl_: e.scalar_tensor_tensor(out=nrow, in0=nrow, scalar=S1[:, cp, hd:hd + 1], in1=nl_[:, 0:512],
                                                                          op0=ALU.mult, op1=ALU.add), reads=[nltk, ftk, ctk2], writes=[ctk2])

    ul_prev = None
    for hd in range(BH):
        wx3, wxtk = head_setup(hd)
        wz, wztk = ws.load([wzv[:, :, 2048 + hd * 512:2048 + (hd + 1) * 512]])
        wz3 = wz.rearrange("p (c n) -> p c n", c=NC8)
        Cst, nrow, ctk2 = Csts[hd % 2], nrows[hd % 2], ctk2s[hd % 2]
        if hd == 0:
            emit_fold(0)

        def refresh_caug(full):
            for dc in range(4):
                P.op("act", lambda e, dc=dc: e.activation(out=Caug[:, dc, 0:512], in_=Cst[:, dc, :], func=AF.Copy),
                     reads=[ctk2], writes=[caugtk])
            P.op("dve", lambda e: e.tensor_scalar(out=nm, in0=nrow, scalar1=cf[:, L_E0:L_E0 + 1], scalar2=None, op0=ALU.mult),
                 reads=[ctk2, ctk], writes=[caugtk])
            ps, pstk = psB2.next()
            for dc in range(4):
                P.op("pe", lambda e, ps=ps, dc=dc: e.matmul(ps[:, dc:dc + 1], lhsT=nm[:, dc * 128:(dc + 1) * 128], rhs=ones_f[:, 0:1],
                                                            start=True, stop=True), reads=[caugtk, ctk], writes=[pstk], signal=(dc == 3))
            P.op("dve", lambda e, ps=ps: e.tensor_copy(out=ncol, in_=ps[:, 0:4]), reads=[pstk], writes=[caugtk])
            P.op("dve", lambda e: e.tensor_copy(out=Caug[:, :, 512:640], in_=ncol.unsqueeze(2).to_broadcast([128, 4, 128])),
                 reads=[caugtk], writes=[caugtk])

        refresh_caug(True)
        for tg in range(4):
            xm, xmt = front(hd, tg, wx3, wxtk)
            for mc in range(4):
                ps, pstk = psA.next()
                for c in range(NC8):
                    P.op("pe", lambda e, c=c, mc=mc, ps=ps: e.matmul(
                        ps, lhsT=wz3[:, c, mc * 128:(mc + 1) * 128], rhs=hT[:, c, 4 + tg * 512:4 + (tg + 1) * 512],
                        start=(c == 0), stop=(c == NC8 - 1)), reads=[wztk], writes=[pstk], signal=(c == NC8 - 1))
                P.op("act", lambda e, ps=ps, mc=mc: e.activation(out=zs[:, mc, :], in_=ps, func=AF.Silu), reads=[pstk], writes=[zstk])
            for j, dst, sc_ in ((0, qT, 1.0), (1, kT, KSCALE)):
                for dc in range(4):
                    ps, pstk = psA.next()
                    P.op("pe", lambda e, ps=ps, j=j, dc=dc: e.matmul(ps, lhsT=bdh[:, j, dc, :], rhs=xc[:, dc, :], start=True, stop=True),
                         reads=[hdtk, xctk], writes=[pstk])
                    P.op("act", lambda e, ps=ps, dst=dst, dc=dc, sc_=sc_: e.activation(out=dst[:, dc, :], in_=ps, func=AF.Copy, scale=sc_),
                         reads=[pstk], writes=[qktk])
            RS = {}

            def stage_pre(cl):
                nonlocal ul_prev
                ck = tg * 4 + cl
                tsl = slice(cl * 128, (cl + 1) * 128)
                gsl = slice(ck * 128, (ck + 1) * 128)
                sel = cf[:, L_SEL + hd * 128:L_SEL + (hd + 1) * 128]
                rp, rptk = psRP.next()
                for i3, row in enumerate((BETA, MM)):
                    P.op("pe", lambda e, rp=rp, i3=i3, row=row, gsl=gsl: e.matmul(
                        rp[:, i3 * 128:(i3 + 1) * 128], lhsT=sel, rhs=row[:, gsl], start=True, stop=True),
                        reads=[rtk, gtk, ctk], writes=[rptk], signal=(i3 == 1))
                bprev = mrun[:, hd:hd + 1] if ck == 0 else ul_prev[0]
                bprev_tk = ftk if ck == 0 else ul_prev[1]
                scsb, sctk = sc_rot.next()
                P.op("act", lambda e, rp=rp, scsb=scsb, bprev=bprev: e.activation(out=scsb, in_=rp[:, 0:128], func=AF.Exp, scale=-1.0, bias=bprev),
                     reads=[rptk, bprev_tk], writes=[sctk])
                emsb, emtk = em_rot.next()
                P.op("act", lambda e, rp=rp, emsb=emsb: e.activation(out=emsb, in_=rp[:, 128:256], func=AF.Exp, scale=-1.0),
                     reads=[rptk], writes=[emtk])
                ul_prev = ul_rot.next()
                P.op("act", lambda e, rp=rp, ul_prev=ul_prev: e.activation(out=ul_prev[0], in_=rp[:, 127:128], func=AF.Copy),
                     reads=[rptk], writes=[ul_prev[1]])
                ps, pstk = psB2.next()
                for mc in range(4):
                    P.op("pe", lambda e, ps=ps, mc=mc, tsl=tsl: e.matmul(
                        ps[:, mc * 128:(mc + 1) * 128], lhsT=xc[:, mc, tsl], rhs=bdh[:, 1, mc, :], start=True, stop=True),
                        reads=[xctk, hdtk], writes=[pstk], signal=(mc == 3))
                wk, wktk = wk_rot.next()
                P.op("act", lambda e, ps=ps, wk=wk, ck=ck: e.activation(out=wk, in_=ps, func=AF.Copy, scale=TMw[:, ck, hd:hd + 1]),
                     reads=[pstk, tmtk], writes=[wktk])
                ps, pstk = psB2.next()
                for mc in range(4):
                    P.op("pe", lambda e, ps=ps, mc=mc, cl=cl, xm=xm: e.matmul(
                        ps[:, mc * 128:(mc + 1) * 128], lhsT=xm[:, mc, 4 + cl * 128:4 + (cl + 1) * 128], rhs=bdh[:, 2, mc, :],
                        start=True, stop=True), reads=[xmt, hdtk], writes=[pstk], signal=(mc == 3))
                va, vatk = va_rot.next()
                P.op("act", lambda e, ps=ps, va=va: e.activation(out=va[:, 0:512], in_=ps, func=AF.Copy), reads=[pstk], writes=[vatk])
                pS_, pStk = psS3.next()
                pS = pS_[:, 256:384]
                for dc in range(4):
                    P.op("pe", lambda e, pS=pS, dc=dc, tsl=tsl: e.matmul(pS, lhsT=kT[:, dc, tsl], rhs=qT[:, dc, tsl],
                                                                          start=(dc == 0), stop=(dc == 3)),
                         reads=[qktk], writes=[pStk], signal=(dc == 3))
                dtmp, dttk = dt_rot.next()
                P.op("dve", lambda e, rp=rp, dtmp=dtmp, ck=ck: e.scalar_tensor_tensor(
                    out=dtmp, in0=rp[:, 0:128], scalar=TMa[:, ck, hd:hd + 1], in1=cf[:, L_MASKLOW:L_MASKLOW + 128],
                    op0=ALU.subtract, op1=ALU.max), reads=[rptk, tmtk, ctk], writes=[dttk])
                P.op("act", lambda e, dtmp=dtmp: e.activation(out=dtmp, in_=dtmp, func=AF.Exp, scale=-1.0), reads=[dttk], writes=[dttk])
                sd, sdtk = sd_rot.next()
                P.op("dve", lambda e, pS=pS, dtmp=dtmp, sd=sd: e.tensor_tensor(out=sd, in0=pS, in1=dtmp, op=ALU.mult),
                     reads=[pStk, dttk], writes=[sdtk])
                qs, qstk = qs_rot.next()
                P.op("dve", lambda e, scsb=scsb, qs=qs, tsl=tsl: e.tensor_tensor(
                    out=qs, in0=qT[:, :, tsl], in1=scsb.unsqueeze(1).to_broadcast([128, 4, 128]), op=ALU.mult),
                    reads=[qktk, sctk], writes=[qstk])

                RS[cl] = dict(ck=ck, tsl=tsl, wk=wk, wktk=wktk, va=va, vatk=vatk, sd=sd, sdtk=sdtk, qs=qs, qstk=qstk,
                              scsb=scsb, sctk=sctk, emsb=emsb, emtk=emtk)

            def stage_mid(cl):
                r_ = RS[cl]
                ck, tsl, wk, wktk, va, vatk, sd, sdtk, qs, qstk, scsb, sctk = (r_[k_] for k_ in (
                    "ck", "tsl", "wk", "wktk", "va", "vatk", "sd", "sdtk", "qs", "qstk", "scsb", "sctk"))
                pH, pHtk = psH.next()
                pD_, pDtk = psDS.next()
                for ec in range(5):
                    o = pH[:, ec * 128:(ec + 1) * 128] if ec < 4 else pD_[:, 0:128]
                    otk = pHtk if ec < 4 else pDtk
                    for dc in range(4):
                        P.op("pe", lambda e, o=o, ec=ec, dc=dc, qs=qs: e.matmul(
                            o, lhsT=Caug[:, dc, ec * 128:(ec + 1) * 128], rhs=qs[:, dc, :], start=(dc == 0), stop=False),
                            reads=[caugtk, qstk], writes=[otk], signal=False)
                    P.op("pe", lambda e, o=o, ec=ec, va=va, sd=sd: e.matmul(
                        o, lhsT=va[:, ec * 128:(ec + 1) * 128], rhs=sd, start=False, stop=True),
                        reads=[vatk, sdtk], writes=[otk], signal=True)

                r_.update(pH=pH, pHtk=pHtk, pD_=pD_, pDtk=pDtk)
                if dbg_stop is not None and (hd, ck) == tuple(dbg_stop):
                    P.barrier()
                    P.finish()
                    return nc, cx
                decay = scsb[:, 127:128]
                for dc in range(4):
                    ps, pstk = psB2.next()
                    P.op("pe", lambda e, ps=ps, dc=dc, wk=wk, va=va: e.matmul(ps, lhsT=wk[:, dc * 128:(dc + 1) * 128], rhs=va[:, 0:512],
                                                                                start=True, stop=True), reads=[wktk, vatk], writes=[pstk])
                    P.op("dve", lambda e, ps=ps, dc=dc, decay=decay: e.scalar_tensor_tensor(
                        out=Cst[:, dc, :], in0=Cst[:, dc, :], scalar=decay, in1=ps, op0=ALU.mult, op1=ALU.add),
                        reads=[pstk, sctk, ctk2], writes=[ctk2])
                    P.op("dve", lambda e, dc=dc: e.tensor_copy(out=Caug[:, dc, 0:512], in_=Cst[:, dc, :]),
                         reads=[ctk2], writes=[caugtk])
                ps, pstk = psB2.next()
                for dc in range(4):
                    P.op("pe", lambda e, ps=ps, dc=dc, wk=wk: e.matmul(ps[:, dc:dc + 1], lhsT=wk[:, dc * 128:(dc + 1) * 128], rhs=ones_bf[:, 0:1],
                                                                         start=True, stop=True), reads=[wktk, ctk], writes=[pstk], signal=(dc == 3))
                P.op("dve", lambda e, ps=ps, decay=decay: e.scalar_tensor_tensor(out=ncol, in0=ncol, scalar=decay, in1=ps[:, 0:4],
                                                                                  op0=ALU.mult, op1=ALU.add),
                     reads=[pstk, sctk, caugtk], writes=[caugtk])
                P.op("dve", lambda e: e.tensor_copy(out=Caug[:, :, 512:640], in_=ncol.unsqueeze(2).to_broadcast([128, 4, 128])),
                     reads=[caugtk], writes=[caugtk])

            def stage_post(cl):
                r_ = RS[cl]
                ck, tsl, emsb, emtk, pH, pHtk, pD_, pDtk = (r_[k_] for k_ in ("ck", "tsl", "emsb", "emtk", "pH", "pHtk", "pD_", "pDtk"))
                hsq, hsqtk = hsq_rot.next()
                P.op("act", lambda e, pH=pH, hsq=hsq: e.activation(out=hsq, in_=pH, func=AF.Square), reads=[pHtk], writes=[hsqtk])
                pSS_, pSStk = psSS.next()
                pSS = pSS_[:, 0:128]
                for ec in range(4):
                    P.op("pe", lambda e, pSS=pSS, hsq=hsq, ec=ec: e.matmul(pSS, lhsT=ones_bf, rhs=hsq[:, ec * 128:(ec + 1) * 128],
                                                                            start=(ec == 0), stop=(ec == 3)),
                         reads=[hsqtk, ctk], writes=[pSStk], signal=(ec == 3))
                dd, ddtk = dd_rot.next()
                P.op("dve", lambda e, pD_=pD_, dd=dd: e.tensor_scalar(out=dd, in0=pD_[:, 0:128], scalar1=-1.0, scalar2=None, op0=ALU.mult),
                     reads=[pDtk], writes=[ddtk])
                P.op("dve", lambda e, pD_=pD_, dd=dd: e.tensor_tensor(out=dd, in0=dd, in1=pD_[:, 0:128], op=ALU.max),
                     reads=[pDtk, ddtk], writes=[ddtk])
                P.op("dve", lambda e, emsb=emsb, dd=dd: e.tensor_tensor(out=dd, in0=dd, in1=emsb, op=ALU.max),
                     reads=[emtk, ddtk], writes=[ddtk])
                P.op("dve", lambda e, dd=dd: e.scalar_tensor_tensor(out=dd, in0=dd, scalar=EPS, in1=dd, op0=ALU.mult, op1=ALU.mult),
                     reads=[ddtk], writes=[ddtk])
                rr, rrtk = rr_rot.next()
                P.op("dve", lambda e, pSS=pSS, dd=dd, rr=rr: e.scalar_tensor_tensor(out=rr, in0=pSS, scalar=1.0 / DH, in1=dd,
                                                                                     op0=ALU.mult, op1=ALU.add),
                     reads=[pSStk, ddtk], writes=[rrtk])
                P.op("act", lambda e, rr=rr: e.activation(out=rr, in_=rr, func=AF.Sqrt), reads=[rrtk], writes=[rrtk])
                P.op("dve", lambda e, rr=rr: e.reciprocal(out=rr, in_=rr), reads=[rrtk], writes=[rrtk])
                for ec in range(4):
                    ch = hd * 4 + ec
                    tt, tttk = tt_rot.next()
                    P.op("dve", lambda e, pH=pH, ec=ec, ch=ch, rr=rr, tt=tt: e.scalar_tensor_tensor(
                        out=tt, in0=pH[:, ec * 128:(ec + 1) * 128], scalar=cf[:, L_HGAIN + ch:L_HGAIN + ch + 1], in1=rr,
                        op0=ALU.mult, op1=ALU.mult), reads=[pHtk, rrtk, ctk], writes=[tttk])
                    P.op("dve", lambda e, ec=ec, ch=ch, tt=tt, tsl=tsl: e.scalar_tensor_tensor(
                        out=tt, in0=xc[:, ec, tsl], scalar=cf[:, L_SKIP + ch:L_SKIP + ch + 1], in1=tt,
                        op0=ALU.mult, op1=ALU.add), reads=[xctk, tttk, ctk], writes=[tttk])
                    P.op("dve", lambda e, ec=ec, tt=tt, tsl=tsl: e.tensor_tensor(out=yb[:, ec, tsl], in0=tt, in1=zs[:, ec, tsl], op=ALU.mult),
                         reads=[tttk, zstk], writes=[ytk])


            stage_pre(0)
            stage_mid(0)
            for cl in range(1, 4):
                stage_pre(cl)
                stage_post(cl - 1)
                stage_mid(cl)
            stage_post(3)

            if tg == 1 and hd + 1 < BH:
                emit_fold(hd + 1)
            P.dma("sync", yview[:, hd * 4:(hd + 1) * 4, tg * 512:(tg + 1) * 512], yb, reads=[ytk])
    P.barrier()
    cx.release(base_mark)

    X = TOP.rearrange("p (c n) -> p c n", c=NC8)
    Xtk = [[Tk() for _ in range(4)] for _ in range(NC8)]
    for c in range(NC8):
        for tg in range(4):
            P.dma("sync", X[:, c, tg * 512:(tg + 1) * 512], x1T[c * 128:(c + 1) * 128, tg * 512:(tg + 1) * 512], writes=[Xtk[c][tg]])
    mk = cx.mark()
    wdn = cx.sb(None, [128, 16, D], BF16, "wdn")
    wdtk = Tk()
    wdv = wdown.rearrange("(c p) n -> p c n", p=128)
    wstg3 = Rot([cx.sb(None, [128, 4, D], F32, "wstg3") for _ in range(2)])
    for q4 in range(4):
        stg_, stk_ = wstg3.next()
        P.dma("sync", stg_, wdv[:, q4 * 4:(q4 + 1) * 4, :], writes=[stk_])
        P.op("act", lambda e, stg_=stg_, q4=q4: e.activation(out=wdn[:, q4 * 4:(q4 + 1) * 4, :], in_=stg_, func=AF.Copy),
             reads=[stk_], writes=[wdtk])
    yts = [cx.sb(None, [128, 16, 512], BF16, "yt") for _ in range(2)]
    yttk = [Tk(), Tk()]
    psA4 = Rot(bk[0:4])
    for tg in range(4):
        yt, ytt = yts[tg % 2], yttk[tg % 2]
        P.dma("sync", yt, yview[:, :, tg * 512:(tg + 1) * 512], writes=[ytt])
        sl = slice(tg * 512, (tg + 1) * 512)
        for oc in range(NC8):
            ps, pstk = psA4.next()
            for mc in range(16):
                P.op("pe", lambda e, ps=ps, mc=mc, oc=oc, yt=yt: e.matmul(ps, lhsT=wdn[:, mc, oc * 128:(oc + 1) * 128], rhs=yt[:, mc, :],
                                                                           start=(mc == 0), stop=(mc == 15)),
                     reads=[wdtk, ytt], writes=[pstk], signal=(mc == 15))
            P.op("dve", lambda e, ps=ps, oc=oc, sl=sl: e.tensor_tensor(out=X[:, oc, sl], in0=ps, in1=X[:, oc, sl], op=ALU.add),
                 reads=[pstk, Xtk[oc][tg]], writes=[Xtk[oc][tg]])
    P.barrier()
    cx.release(mk)
    if debug:
        emit_store(cx, X, Xtk, dbg_a)
    emit_mlp(cx, X, Xtk, cf[:, L_MLPN:L_MLPN + 8], ctk, w1, w2, ones_bf)
    emit_ple(cx, X, Xtk, cf[:, L_PLEN:L_PLEN + 8], ctk, wg, wp, pT, ones_bf)
    emit_store(cx, X, Xtk, out)
    P.finish()
    return nc, cx


_CACHE = {}


def _prog(key, builder):
    return builder()


def kernel(**inputs):
    inputs = {k: np.asarray(v) for k, v in inputs.items()}
    cores = list(range(NCORES))
    nc, _ = build_layer0()
    in_maps = [layer0_inputs(inputs, c) for c in cores]
    res = run_bass_kernel_spmd(nc, in_maps, core_ids=cores)
    x1T = np.concatenate([r["xout"] for r in res.results], axis=1)
    nc, _ = build_layer1("B")
    in_maps = [layer1_inputs(inputs, c, x1T, "B") for c in cores]
    res = run_bass_kernel_spmd(nc, in_maps, core_ids=cores)
    st_all = np.stack([res.results[c]["st_out"] for c in range(7)])
    g_rows = [res.results[c]["g_out"] for c in cores]
    nc, _ = build_layer1("C")
    in_maps = [layer1_inputs(inputs, c, x1T, "C", st_all, g_rows[c]) for c in cores]
    res = run_bass_kernel_spmd(nc, in_maps, core_ids=cores)
    outT = np.concatenate([r["xout"] for r in res.results], axis=1)
    return np.ascontiguousarray(outT.T)[None].astype(np.float32)
```

```python
import numpy as np
import concourse.bass as bass
import concourse.mybir as mybir
from concourse.bass_utils import run_bass_kernel_spmd

F32 = mybir.dt.float32
BF16 = mybir.dt.bfloat16
AF = mybir.ActivationFunctionType
ALU = mybir.AluOpType
AX = mybir.AxisListType

NCORES = 8
S = 16384
D = 1024
NT = S // NCORES
NC8 = D // 128
EPS = 1e-6
BIG = 30000.0
A_GROUPS = ((128, 1), (512, 4), (2048, 16))
NDMA = 24
SB_F32 = 51968


class Tk:
    __slots__ = ("w", "r")

    def __init__(self):
        self.w = {}
        self.r = {}


class Prog:
    def __init__(self, nc):
        self.nc = nc
        self.eng = {"act": nc.scalar, "dve": nc.vector, "pool": nc.gpsimd, "pe": nc.tensor, "sync": nc.sync}
        self.sem = {e: nc.alloc_semaphore("s_" + e) for e in ("act", "dve", "pool", "pe")}
        self.cnt = {e: 0 for e in ("act", "dve", "pool", "pe")}
        self.seen = {e: {} for e in self.eng}
        self.dsem = [nc.alloc_semaphore("s_dma%d" % i) for i in range(NDMA)]
        self.dcnt = [0] * NDMA
        self.dnext = 0
        self.nins = {e: 0 for e in self.eng}

    def _semof(self, src):
        if isinstance(src, tuple):
            return self.dsem[src[1]]
        return self.sem[src]

    def _deps(self, e, reads, writes, allraw=False):
        deps = {}

        def add(src, n, raw):
            if src == e and not allraw:
                if e == "pe" or not raw:
                    return
            if deps.get(src, 0) < n:
                deps[src] = n

        for t in reads:
            for src, n in t.w.items():
                add(src, n, True)
        for t in writes:
            for src, n in t.w.items():
                add(src, n, False)
            for src, n in t.r.items():
                add(src, n, False)
        return deps

    def _wait(self, e, deps):
        eng = self.eng[e]
        seen = self.seen[e]
        for src, n in deps.items():
            if seen.get(src, 0) >= n:
                continue
            seen[src] = n
            eng.wait_ge(self._semof(src), n)
            self.nins[e] += 1

    def op(self, e, fn, reads=(), writes=(), signal=True):
        self._wait(e, self._deps(e, reads, writes))
        ins = fn(self.eng[e])
        self.nins[e] += 1
        n = self.cnt[e] + 1
        if signal:
            ins.then_inc(self.sem[e], 1)
            self.cnt[e] = n
        for t in reads:
            if t.r.get(e, 0) < n:
                t.r[e] = n
        for t in writes:
            if t.w.get(e, 0) < n:
                t.w[e] = n
        return ins

    def dma(self, q, out, in_, reads=(), writes=()):
        k = self.dnext
        self.dnext = (k + 1) % NDMA
        src = ("dma", k)
        deps = self._deps(q, reads, writes, allraw=True)
        if self.dcnt[k] > 0:
            deps[src] = max(deps.get(src, 0), self.dcnt[k])
        self._wait(q, deps)
        ins = self.eng[q].dma_start(out=out, in_=in_)
        self.nins[q] += 1
        n = self.dcnt[k] + 16
        ins.then_inc(self.dsem[k], 16)
        self.dcnt[k] = n
        for t in reads:
            t.r[src] = n
        for t in writes:
            t.w[src] = n

    def barrier(self):
        for e in self.eng:
            deps = {}
            for s2 in self.cnt:
                if s2 != e and self.cnt[s2] > 0:
                    deps[s2] = self.cnt[s2]
            for k in range(NDMA):
                if self.dcnt[k] > 0:
                    deps[("dma", k)] = self.dcnt[k]
            self._wait(e, deps)

    def finish(self):
        deps = {}
        for k in range(NDMA):
            if self.dcnt[k] > 0:
                deps[("dma", k)] = self.dcnt[k]
        self._wait("sync", deps)


class Rot:
    def __init__(self, aps):
        self.items = [a if isinstance(a, tuple) else (a, Tk()) for a in aps]
        self.i = 0

    def next(self):
        it = self.items[self.i]
        self.i = (self.i + 1) % len(self.items)
        return it


class Ctx:
    def __init__(self, nc):
        self.nc = nc
        self.P = Prog(nc)
        self.banks = [nc.alloc_psum_tensor("psb%d" % i, [128, 512], F32).ap() for i in range(8)]
        self.nalloc = 0

        self.big = nc.alloc_sbuf_tensor("big", [128, SB_F32], F32).ap()
        self.top = 0

    def sb(self, stack, shape, dt, name=None):
        esz = 2 if dt == BF16 else 4
        n = int(np.prod(shape[1:]))
        nbytes = (n * esz + 63) // 64 * 64
        off = self.top
        assert off + nbytes <= SB_F32 * 4, ("SBUF overflow", name, off, nbytes)
        self.top = off + nbytes
        self.log = getattr(self, 'log', [])
        self.log.append((name, off, nbytes))
        ap = self.big[:, off // 4:(off + nbytes) // 4]
        if dt == BF16:
            ap = ap.bitcast(BF16)
        ap = ap[:, 0:n]
        if len(shape) == 3:
            ap = ap.rearrange("p (a b) -> p a b", a=shape[1])
        elif len(shape) == 4:
            ap = ap.rearrange("p (a b c) -> p a b c", a=shape[1], b=shape[2])
        return ap

    def mark(self):
        return self.top

    def release(self, m):
        self.top = m


def load_consts(cx, stack, cst_ap, ncols):
    P = cx.P
    cf = cx.sb(stack, [128, ncols], F32, "cstf")
    cb = cx.sb(stack, [128, C_END_BF], BF16, "cstb")
    tk = Tk()
    P.dma("sync", cf, cst_ap, writes=[tk])
    P.op("dve", lambda e: e.tensor_copy(out=cb, in_=cf[:, 0:C_END_BF]), reads=[tk], writes=[tk])
    return cf, cb, tk


class WStream:
    def __init__(self, cx, stack, nelem, nstage=2, nslot=2):
        self.cx = cx
        self.nelem = nelem
        self.stage = Rot([cx.sb(stack, [128, nelem], F32, "wstg") for _ in range(nstage)])
        self.slots = Rot([cx.sb(stack, [128, nelem], BF16, "wbf") for _ in range(nslot)])

    def load(self, views):
        P = self.cx.P
        stg, stk = self.stage.next()
        wb, wtk = self.slots.next()
        off = 0
        for v in views:
            shp = v.shape
            n = int(np.prod(shp[1:]))
            dst = stg[:, off:off + n]
            if len(shp) == 3:
                dst = dst.rearrange("p (a b) -> p a b", a=shp[1])
            P.dma("sync", dst, v, writes=[stk])
            off += n
        assert off <= self.nelem
        P.op("pool", lambda e: e.tensor_copy(out=wb[:, 0:off], in_=stg[:, 0:off]), reads=[stk], writes=[wtk])
        return wb, wtk


def rms_stats(cx, xs, n, sq_rot, ps_ap, ps_tk, rstd, rstd_tk, ones_bf, ctk, inv_dim):
    P = cx.P
    nx = len(xs)
    for c, (xa, xt) in enumerate(xs):
        sq, sqt = sq_rot.next()
        P.op("act", lambda e, xa=xa, sq=sq: e.activation(out=sq[:, 0:n], in_=xa, func=AF.Square), reads=[xt], writes=[sqt])
        P.op("pe", lambda e, sq=sq, c=c: e.matmul(ps_ap[:, 0:n], lhsT=ones_bf, rhs=sq[:, 0:n], start=(c == 0), stop=(c == nx - 1)),
             reads=[sqt, ctk], writes=[ps_tk])
    P.op("act", lambda e: e.activation(out=rstd[:, 0:n], in_=ps_ap[:, 0:n], func=AF.Sqrt, bias=cx.eps_col, scale=inv_dim),
         reads=[ps_tk, ctk], writes=[rstd_tk])
    P.op("dve", lambda e: e.reciprocal(out=rstd[:, 0:n], in_=rstd[:, 0:n]), reads=[rstd_tk], writes=[rstd_tk])


C_ID, C_ONES, C_BONES, C_DM, C_HONES = 0, 128, 256, 384, 640
C_OZ = 704
C_HZ = 960
C_END_BF = 1216
C_EPS = 1216
C_GAINS = 1217
G0_ANORM = C_GAINS
G0_QG = G0_ANORM + 8
G0_KG = G0_QG + 3
G0_MLPN = G0_KG + 3
G0_PLEN = G0_MLPN + 8
G0_END = G0_PLEN + 8


def base_consts(core, ncols):
    c = np.zeros((128, ncols), np.float32)
    c[:, C_ID:C_ID + 128] = np.eye(128, dtype=np.float32)
    c[:, C_ONES:C_ONES + 128] = 1.0
    c[0:64, C_BONES:C_BONES + 64] = 1.0
    c[64:128, C_BONES + 64:C_BONES + 128] = 1.0
    kk = np.arange(128)[:, None]
    a = np.arange(128)[None, :]
    diag = np.where(kk <= a, a - kk, BIG)
    prev = np.where(kk >= a, 128 + a - kk, BIG)
    c[:, C_DM:C_DM + 128] = diag
    c[:, C_DM + 128:C_DM + 256] = prev
    hv = 0.0 if core == 0 else 1.0
    c[:, C_HONES:C_HONES + 64] = hv
    c[:, C_OZ:C_OZ + 64] = 1.0
    c[:, C_OZ + 128 + 64:C_OZ + 256] = 1.0
    c[:, C_HZ:C_HZ + 64] = hv
    c[:, C_HZ + 128 + 64:C_HZ + 256] = hv
    c[:, C_EPS] = EPS
    return c


def col_layout(v):
    v = np.asarray(v, np.float32).reshape(-1, 128)
    return np.ascontiguousarray(v.T)


def emit_norm_resident(cx, X, Xtk, gcol, ctk, hT, hTtk, sq_rot, rstd_rot, ps_rot, ones_bf):
    P = cx.P
    for tg in range(NT // 512):
        sl = slice(tg * 512, (tg + 1) * 512)
        xs = [(X[:, c, sl], Xtk[c][tg]) for c in range(NC8)]
        ps_ap, ps_tk = ps_rot.next()
        rstd, rtk = rstd_rot.next()
        rms_stats(cx, xs, 512, sq_rot, ps_ap, ps_tk, rstd, rtk, ones_bf, ctk, 1.0 / D)
        for c in range(NC8):
            P.op("dve", lambda e, c=c, sl=sl, rstd=rstd: e.scalar_tensor_tensor(
                out=hT[:, c, sl], in0=X[:, c, sl], scalar=gcol[:, c:c + 1], in1=rstd[:, 0:512],
                op0=ALU.mult, op1=ALU.mult), reads=[Xtk[c][tg], rtk, ctk], writes=[hTtk[c][tg]])


def emit_mlp(cx, X, Xtk, gcol, ctk, w1, w2, ones_bf):
    P = cx.P
    mk = cx.mark()
    st = None
    hT = cx.sb(st, [128, NC8, NT], BF16, "mlp_hT")
    hTtk = [[Tk() for _ in range(4)] for _ in range(NC8)]
    sq_rot = Rot([cx.sb(st, [128, 512], BF16, "sq") for _ in range(4)])
    rstd_rot = Rot([cx.sb(st, [128, 512], F32, "rstd") for _ in range(2)])
    ps_stat = Rot([cx.banks[7]])
    emit_norm_resident(cx, X, Xtk, gcol, ctk, hT, hTtk, sq_rot, rstd_rot, ps_stat, ones_bf)
    ws = WStream(cx, st, 4096, nstage=2, nslot=2)
    hids = [cx.sb(st, [128, 4, NT], BF16, "hid") for _ in range(2)]
    hid_tks = [[[Tk() for _ in range(4)] for _ in range(4)] for _ in range(2)]
    tmp_rot = Rot([cx.sb(st, [128, 512], F32, "rl") for _ in range(3)])
    psA = Rot(cx.banks[0:4])
    psB = Rot(cx.banks[4:7])
    w1v = w1.rearrange("(c p) n -> p c n", p=128)
    w2v = w2.rearrange("(c p) n -> p c n", p=128)
    NHB = 8
    for hb in range(NHB):
        hid = hids[hb % 2]
        htk = hid_tks[hb % 2]
        wa, watk = ws.load([w1v[:, :, hb * 512:(hb + 1) * 512]])
        wa3 = wa.rearrange("p (c n) -> p c n", c=NC8)
        for hc in range(4):
            for tg in range(4):
                sl = slice(tg * 512, (tg + 1) * 512)
                ps, pstk = psA.next()
                for c in range(NC8):
                    P.op("pe", lambda e, c=c, hc=hc, sl=sl, ps=ps, wa3=wa3: e.matmul(
                        ps, lhsT=wa3[:, c, hc * 128:(hc + 1) * 128], rhs=hT[:, c, sl],
                        start=(c == 0), stop=(c == NC8 - 1)),
                        reads=[watk, hTtk[c][tg]], writes=[pstk], signal=(c == NC8 - 1))
                tmp, ttk = tmp_rot.next()
                P.op("act", lambda e, ps=ps, tmp=tmp: e.activation(out=tmp, in_=ps, func=AF.Square),
                     reads=[pstk], writes=[ttk])
                P.op("dve", lambda e, ps=ps, tmp=tmp, hc=hc, sl=sl, hid=hid: e.scalar_tensor_tensor(
                    out=hid[:, hc, sl], in0=ps, scalar=0.0, in1=tmp, op0=ALU.is_gt, op1=ALU.mult),
                    reads=[pstk, ttk], writes=[htk[hc][tg]])
        wb, wbtk = ws.load([w2v[:, hb * 4:(hb + 1) * 4, :]])
        wb3 = wb.rearrange("p (c n) -> p c n", c=4)
        for oc in range(NC8):
            for tg in range(4):
                sl = slice(tg * 512, (tg + 1) * 512)
                ps, pstk = psB.next()
                for hc in range(4):
                    P.op("pe", lambda e, hc=hc, oc=oc, sl=sl, ps=ps, hid=hid, wb3=wb3: e.matmul(
                        ps, lhsT=wb3[:, hc, oc * 128:(oc + 1) * 128], rhs=hid[:, hc, sl],
                        start=(hc == 0), stop=(hc == 3)),
                        reads=[wbtk, htk[hc][tg]], writes=[pstk], signal=(hc == 3))
                P.op("dve", lambda e, oc=oc, sl=sl, ps=ps: e.tensor_tensor(
                    out=X[:, oc, sl], in0=ps, in1=X[:, oc, sl], op=ALU.add),
                    reads=[pstk, Xtk[oc][tg]], writes=[Xtk[oc][tg]])
    P.barrier()
    cx.release(mk)


def emit_ple(cx, X, Xtk, gcol, ctk, wg, wp, pT_dram, ones_bf):
    P = cx.P
    mk = cx.mark()
    st = None
    hT = cx.sb(st, [128, NC8, NT], BF16, "ple_hT")
    hTtk = [[Tk() for _ in range(4)] for _ in range(NC8)]
    sq_rot = Rot([cx.sb(st, [128, 512], BF16, "sq") for _ in range(4)])
    rstd_rot = Rot([cx.sb(st, [128, 512], F32, "rstd") for _ in range(2)])
    ps_stat = Rot([cx.banks[7]])
    emit_norm_resident(cx, X, Xtk, gcol, ctk, hT, hTtk, sq_rot, rstd_rot, ps_stat, ones_bf)
    ws = WStream(cx, st, 4096, nstage=2, nslot=3)
    pst = cx.sb(st, [128, 2, NT], F32, "pstg")
    pb = cx.sb(st, [128, 2, NT], BF16, "pbf")
    ptk = Tk()
    P.dma("sync", pst, pT_dram.rearrange("(c p) n -> p c n", p=128), writes=[ptk])
    P.op("pool", lambda e: e.tensor_copy(out=pb, in_=pst), reads=[ptk], writes=[ptk])
    wpb, wptk = ws.load([wp.rearrange("(c p) n -> p c n", p=128)])
    wp3 = wpb[:, 0:2048].rearrange("p (c n) -> p c n", c=2)
    gt_rot = Rot([cx.sb(st, [128, 512], F32, "gt") for _ in range(3)])
    psA = Rot(cx.banks[0:3])
    psB = Rot(cx.banks[3:6])
    wgv = wg.rearrange("(c p) n -> p c n", p=128)
    for half in range(2):
        wa, watk = ws.load([wgv[:, :, half * 512:(half + 1) * 512]])
        wa3 = wa.rearrange("p (c n) -> p c n", c=NC8)
        for o4 in range(4):
            oc = half * 4 + o4
            for tg in range(4):
                sl = slice(tg * 512, (tg + 1) * 512)
                ps, pstk = psA.next()
                for c in range(NC8):
                    P.op("pe", lambda e, c=c, o4=o4, sl=sl, ps=ps, wa3=wa3: e.matmul(
                        ps, lhsT=wa3[:, c, o4 * 128:(o4 + 1) * 128], rhs=hT[:, c, sl],
                        start=(c == 0), stop=(c == NC8 - 1)),
                        reads=[watk, hTtk[c][tg]], writes=[pstk], signal=(c == NC8 - 1))
                ps2, ps2tk = psB.next()
                for kc in range(2):
                    P.op("pe", lambda e, kc=kc, oc=oc, sl=sl, ps2=ps2: e.matmul(
                        ps2, lhsT=wp3[:, kc, oc * 128:(oc + 1) * 128], rhs=pb[:, kc, sl],
                        start=(kc == 0), stop=(kc == 1)),
                        reads=[wptk, ptk], writes=[ps2tk], signal=(kc == 1))
                gt, gtk = gt_rot.next()
                P.op("act", lambda e, ps=ps, gt=gt: e.activation(out=gt, in_=ps, func=AF.Sigmoid),
                     reads=[pstk], writes=[gtk])
                P.op("dve", lambda e, ps2=ps2, gt=gt: e.tensor_tensor(out=gt, in0=ps2, in1=gt, op=ALU.mult),
                     reads=[ps2tk, gtk], writes=[gtk])
                P.op("dve", lambda e, oc=oc, sl=sl, gt=gt: e.tensor_tensor(
                    out=X[:, oc, sl], in0=gt, in1=X[:, oc, sl], op=ALU.add),
                    reads=[gtk, Xtk[oc][tg]], writes=[Xtk[oc][tg]])
    P.barrier()
    cx.release(mk)


def alibi_slope(h):
    return 2.0 ** (-8.0 * (h + 1) / 16)


def sslice(start, count, step):
    return slice(start, start + (count - 1) * step + 1, step)


def emit_attention(cx, xT_ext, wqkv, wo, cf, cb, ctk, TOP, R1, lvl=9, hps=8):
    P = cx.P
    st = None
    ones_bf = cb[:, C_ONES:C_ONES + 128]
    bones = cb[:, C_BONES:C_BONES + 128]
    Dm = cf[:, C_DM:C_DM + 256]
    hT = TOP.bitcast(BF16).rearrange("p (c n) -> p c n", c=NC8)
    mk = cx.mark()
    sq_rot = Rot([cx.sb(st, [128, 512], BF16, "sq") for _ in range(2)])
    rstd_rot = Rot([cx.sb(st, [128, 512], F32, "rstd") for _ in range(2)])
    xstg = [R1[:, i * 4096:(i + 1) * 4096].rearrange("p (c n) -> p c n", c=NC8) for i in range(2)]
    xstk = [Tk(), Tk()]
    ps_stat = Rot([cx.banks[7], cx.banks[6]])
    httk = Tk()
    gcol = cf[:, G0_ANORM:G0_ANORM + 8]
    for tg in range(8 if lvl >= 1 else 0):
        xa = xstg[tg % 2]
        xt = xstk[tg % 2]
        P.dma("sync", xa, xT_ext[:, tg * 512:(tg + 1) * 512].rearrange("(c p) n -> p c n", p=128), writes=[xt])
        xs = [(xa[:, c, :], xt) for c in range(NC8)]
        ps_ap, ps_tk = ps_stat.next()
        rstd, rtk = rstd_rot.next()
        rms_stats(cx, xs, 512, sq_rot, ps_ap, ps_tk, rstd, rtk, ones_bf, ctk, 1.0 / D)
        for c in range(NC8):
            P.op("dve", lambda e, c=c, tg=tg, xa=xa, rstd=rstd: e.scalar_tensor_tensor(
                out=hT[:, c, tg * 512:(tg + 1) * 512], in0=xa[:, c, :], scalar=gcol[:, c:c + 1], in1=rstd[:, 0:512],
                op0=ALU.mult, op1=ALU.mult), reads=[xt, rtk, ctk], writes=[httk])
    P.op("dve", lambda e: e.tensor_scalar(out=cf[:, G0_QG:G0_QG + 3], in0=cf[:, G0_QG:G0_QG + 3], scalar1=0.125,
                                          scalar2=None, op0=ALU.mult), reads=[ctk], writes=[ctk])
    P.barrier()
    ACC = R1[:, 0:4096].rearrange("p (a n) -> p a n", a=2)
    oT = R1[:, 4096:12288].bitcast(BF16).rearrange("p (c n) -> p c n", c=NC8)
    acctk = Tk()
    ottk = Tk()
    ws = WStream(cx, st, 3072, nstage=1, nslot=2)
    QTz = [cx.sb(st, [128, NT], BF16, "QTz") for _ in range(2)]
    qtk = Tk()
    KT_rot = Rot([cx.sb(st, [128, 2 * NT], BF16, "KT") for _ in range(2)])
    Vz = [cx.sb(st, [128, 32, 128], BF16, "Vz") for _ in range(2)]
    vtk = Tk()
    for e2 in (0, 1):
        P.op("pool", lambda e, e2=e2: e.memset(QTz[e2], 0.0), writes=[qtk])
        P.op("pool", lambda e, e2=e2: e.memset(Vz[e2], 0.0), writes=[vtk])
    onesz = [cb[:, C_OZ:C_OZ + 128], cb[:, C_OZ + 128:C_OZ + 256]]
    honesz = [cb[:, C_HZ:C_HZ + 128], cb[:, C_HZ + 128:C_HZ + 256]]
    tmp_rot = Rot([cx.sb(st, [128, 256], F32, "stmp") for _ in range(4)])
    pt_rots = [Rot([cx.sb(st, [128, 256], BF16, "PT") for _ in range(6)]) for _ in range(2)]
    bk = [(cx.banks[i], Tk()) for i in range(8)]

    def half(i):
        return (bk[i][0][:, 0:256], bk[i][1])

    psQ = Rot(bk[0:2])
    psS = Rot([bk[2]])
    psV = Rot([bk[3]])
    psST0 = Rot([half(4), half(0)])
    psST1 = Rot([half(5), half(1)])
    psND = Rot([half(6), half(7), half(2), half(3)])
    wq_view = wqkv.rearrange("(c p) n -> p c n", p=128)

    def perm(ap2d, d):
        if d == 1:
            return ap2d
        return ap2d.rearrange("p (u r) -> p r u", r=d)

    def proj_piece(w3, wtk, j, e0, n, gain_col, out_buf, out_tk, d, Lx, u0):
        ps, pstk = psQ.next()
        for c in range(NC8):
            P.op("pe", lambda e, c=c, ps=ps: e.matmul(ps[:, 0:n], lhsT=w3[:, j, c, :], rhs=hT[:, c, e0:e0 + n],
                                                      start=(c == 0), stop=(c == NC8 - 1)),
                 reads=[wtk], writes=[pstk], signal=(c == NC8 - 1))
        ps2, ps2tk = psS.next()
        rstd, rtk = rstd_rot.next()
        rms_stats(cx, [(ps[:, 0:n], pstk)], n, sq_rot, ps2, ps2tk, rstd, rtk, bones, ctk, 1.0 / 64)
        outs = out_buf if isinstance(out_buf, list) else [(slice(0, 128), out_buf)]
        for (rows, ob) in outs:
            if d == 1:
                o = ob[rows, u0:u0 + n]
            else:
                o = ob[rows, 0:d * Lx].rearrange("p (r u) -> p r u", r=d)[:, :, u0:u0 + n // d]
            P.op("dve", lambda e, ps=ps, rstd=rstd, o=o, rows=rows: e.scalar_tensor_tensor(
                out=o, in0=perm(ps[rows, 0:n], d), scalar=gain_col[rows, :], in1=perm(rstd[rows, 0:n], d),
                op0=ALU.mult, op1=ALU.mult),
                reads=[pstk, rtk, ctk], writes=[out_tk])

    if lvl < 2:
        hps = 0
        P.op('dve', lambda e: e.memset(R1, 0.0), writes=[ottk])
    for hp in range(hps):
        for g, (W, d) in enumerate(A_GROUPS):
            L = NT // d
            Lk = (W + NT) // d
            nb = Lk // 128
            e_start = NT - W
            base = g * 3072 + hp * 128
            wb, wtk = ws.load([wq_view[:, :, base + j * 1024: base + j * 1024 + 128] for j in range(3)])
            w3 = wb[:, 0:3072].rearrange("p (j c n) -> p j c n", j=3, c=NC8)
            KT, ktk = KT_rot.next()
            for tg in range(4):
                proj_piece(w3, wtk, 0, NT + tg * 512, 512, cf[:, G0_QG + g:G0_QG + g + 1],
                           [(slice(0, 64), QTz[0]), (slice(64, 128), QTz[1])], qtk, d, L, tg * 512 // d)
            pieces = []
            if W < 512:
                pieces.append((e_start, W))
                e = NT
            else:
                e = e_start
            while e < 2 * NT:
                pieces.append((e, 512))
                e += 512
            for (e0, n) in pieces:
                proj_piece(w3, wtk, 1, e0, n, cf[:, G0_KG + g:G0_KG + g + 1], KT, ktk, d, Lk, (e0 - e_start) // d)
            nkb = d * nb if lvl >= 3 else 0
            kb = 0
            while kb < nkb:
                nblk = min(4, nkb - kb)
                psv, psvtk = psV.next()
                for b in range(nblk):
                    r, jb = divmod(kb + b, nb)
                    e_first = e_start + d * 128 * jb + r
                    for c in range(NC8):
                        P.op("pe", lambda e, c=c, b=b, e_first=e_first, psv=psv: e.matmul(
                            psv[:, b * 128:(b + 1) * 128], lhsT=hT[:, c, sslice(e_first, 128, d)], rhs=w3[:, 2, c, :],
                            start=(c == 0), stop=(c == NC8 - 1)),
                            reads=[wtk], writes=[psvtk], signal=(c == NC8 - 1 and b == nblk - 1))
                for e2 in (0, 1):
                    cs = slice(64 * e2, 64 * e2 + 64)
                    P.op("act", lambda e, kb=kb, nblk=nblk, psv=psv, e2=e2, cs=cs: e.activation(
                        out=Vz[e2][:, kb:kb + nblk, cs],
                        in_=psv[:, 0:nblk * 128].rearrange("p (b n) -> p b n", b=nblk)[:, :, cs], func=AF.Copy),
                        reads=[psvtk], writes=[vtk])
                kb += nblk
            PTs = {}

            def score_task(r, jb):
                lo = 128 if jb == 0 else 0
                hi = 128 if jb == nb - 1 else 256
                qb0 = jb if jb == 0 else jb - 1
                q_off = r * L + 128 * qb0
                sTs = [psST0.next(), psST1.next()]
                for e2 in (0, 1):
                    sT, sTtk = sTs[e2]
                    P.op("pe", lambda e, sT=sT, e2=e2: e.matmul(
                        sT[:, lo:hi], lhsT=KT[:, r * Lk + 128 * jb: r * Lk + 128 * jb + 128],
                        rhs=QTz[e2][:, q_off:q_off + (hi - lo)], start=True, stop=True),
                        reads=[ktk, qtk], writes=[sTtk])
                for e2 in (0, 1):
                    sig = alibi_slope(2 * hp + e2) * d
                    sT, sTtk = sTs[e2]
                    tmp, tmtk = tmp_rot.next()
                    P.op("dve", lambda e, sT=sT, tmp=tmp, sig=sig: e.scalar_tensor_tensor(
                        out=tmp[:, lo:hi], in0=Dm[:, lo:hi], scalar=-sig, in1=sT[:, lo:hi],
                        op0=ALU.mult, op1=ALU.add),
                        reads=[sTtk, ctk], writes=[tmtk])
                    pt, pttk = pt_rots[e2].next()
                    P.op("act", lambda e, tmp=tmp, pt=pt: e.activation(
                        out=pt[:, lo:hi], in_=tmp[:, lo:hi], func=AF.Exp), reads=[tmtk], writes=[pttk])
                    PTs[(e2, r, jb)] = (pt, pttk)

            def pv_task(r, j):
                jb = j + 1
                nd, ndtk = psND.next()
                kbp = r * nb + j
                kbd = r * nb + jb
                for part in (0, 1):
                    co = slice(128 * part, 128 * part + 128)
                    for e2 in (0, 1):
                        ptp, ptptk = PTs[(e2, r, j)]
                        ptd, ptdtk = PTs[(e2, r, jb)]
                        if part == 0:
                            lp, ld = Vz[e2][:, kbp, :], Vz[e2][:, kbd, :]
                        else:
                            lp, ld = (honesz[e2] if j == 0 else onesz[e2]), onesz[e2]
                        P.op("pe", lambda e, nd=nd, co=co, lp=lp, ptp=ptp, e2=e2: e.matmul(
                            nd[:, co], lhsT=lp, rhs=ptp[:, 128:256], start=(e2 == 0), stop=False),
                            reads=[vtk, ctk, ptptk], writes=[ndtk], signal=False)
                        P.op("pe", lambda e, nd=nd, co=co, ld=ld, ptd=ptd, e2=e2: e.matmul(
                            nd[:, co], lhsT=ld, rhs=ptd[:, 0:128], start=False, stop=(e2 == 1)),
                            reads=[vtk, ctk, ptdtk], writes=[ndtk], signal=(part == 1 and e2 == 1))
                t0 = r + d * 128 * j
                accv = ACC[:, :, sslice(t0, 128, d)]
                ndv = nd.rearrange("p (a n) -> p a n", a=2)
                if g == 0:
                    P.op("act", lambda e, accv=accv, ndv=ndv: e.activation(out=accv, in_=ndv, func=AF.Copy),
                         reads=[ndtk], writes=[acctk])
                else:
                    P.op("dve", lambda e, accv=accv, ndv=ndv: e.tensor_tensor(out=accv, in0=ndv, in1=accv, op=ALU.add),
                         reads=[ndtk, acctk], writes=[acctk])

            LA = 3
            pending = []
            tasks = [(r, jb) for r in range(d if lvl >= 4 else 0) for jb in range(nb)]
            for i, (r, jb) in enumerate(tasks):
                score_task(r, jb)
                if jb >= 1:
                    pending.append((i, r, jb - 1))
                while pending and pending[0][0] <= i - LA:
                    _, r_, j_ = pending.pop(0)
                    pv_task(r_, j_)
            for (_, r_, j_) in pending:
                pv_task(r_, j_)
        P.op("dve", lambda e: e.reciprocal(out=ACC[:, 1, :], in_=ACC[:, 1, :]), reads=[acctk], writes=[acctk])
        P.op("dve", lambda e, hp=hp: e.tensor_tensor(out=oT[:, hp, :], in0=ACC[:, 0, :], in1=ACC[:, 1, :], op=ALU.mult),
             reads=[acctk], writes=[ottk])
    P.barrier()
    cx.release(mk)
    mk = cx.mark()
    X = TOP.rearrange("p (c n) -> p c n", c=NC8)
    Xtk = [[Tk() for _ in range(4)] for _ in range(NC8)]
    for c in range(NC8):
        for tg in range(4):
            P.dma("sync", X[:, c, tg * 512:(tg + 1) * 512], xT_ext[c * 128:(c + 1) * 128, NT + tg * 512:NT + (tg + 1) * 512],
                  writes=[Xtk[c][tg]])
    ws2 = WStream(cx, st, 4096, nstage=2, nslot=2)
    wov = wo.rearrange("(c p) n -> p c n", p=128)
    psA = Rot(cx.banks[0:4])
    for half in range(2):
        wa, watk = ws2.load([wov[:, :, half * 512:(half + 1) * 512]])
        wa3 = wa.rearrange("p (c n) -> p c n", c=NC8)
        for o4 in range(4):
            oc = half * 4 + o4
            for tg in range(4):
                sl = slice(tg * 512, (tg + 1) * 512)
                ps, pstk = psA.next()
                for c in range(NC8):
                    P.op("pe", lambda e, c=c, o4=o4, sl=sl, ps=ps, wa3=wa3: e.matmul(
                        ps, lhsT=wa3[:, c, o4 * 128:(o4 + 1) * 128], rhs=oT[:, c, sl],
                        start=(c == 0), stop=(c == NC8 - 1)),
                        reads=[watk, ottk], writes=[pstk], signal=(c == NC8 - 1))
                P.op("dve", lambda e, oc=oc, sl=sl, ps=ps: e.tensor_tensor(
                    out=X[:, oc, sl], in0=ps, in1=X[:, oc, sl], op=ALU.add),
                    reads=[pstk, Xtk[oc][tg]], writes=[Xtk[oc][tg]])
    P.barrier()
    cx.release(mk)
    return X, Xtk


def emit_store(cx, X, Xtk, out_dram):
    P = cx.P
    for c in range(NC8):
        P.dma("sync", out_dram[c * 128:(c + 1) * 128, :], X[:, c, :], reads=Xtk[c])


def build_layer0(debug=False, lvl=9, hps=8):
    nc = bass.Bass("TRN2", target_bir_lowering=False)
    xT_ext = nc.dram_tensor("xT_ext", [D, 2 * NT], F32, kind="ExternalInput").ap()
    pT = nc.dram_tensor("pT", [256, NT], F32, kind="ExternalInput").ap()
    cst = nc.dram_tensor("cst", [128, G0_END], F32, kind="ExternalInput").ap()
    wqkv = nc.dram_tensor("a_w_qkv", [D, 9216], F32, kind="ExternalInput").ap()
    wo = nc.dram_tensor("a_w_o", [D, D], F32, kind="ExternalInput").ap()
    w1 = nc.dram_tensor("mlp_w1", [D, 4096], F32, kind="ExternalInput").ap()
    w2 = nc.dram_tensor("mlp_w2", [4096, D], F32, kind="ExternalInput").ap()
    wg = nc.dram_tensor("ple_w_gate", [D, D], F32, kind="ExternalInput").ap()
    wp = nc.dram_tensor("ple_w_proj", [256, D], F32, kind="ExternalInput").ap()
    out = nc.dram_tensor("xout", [D, NT], F32, kind="ExternalOutput").ap()
    if debug:
        dbg_a = nc.dram_tensor("dbg_a", [D, NT], F32, kind="ExternalOutput").ap()
        dbg_m = nc.dram_tensor("dbg_m", [D, NT], F32, kind="ExternalOutput").ap()
    cx = Ctx(nc)
    P = cx.P
    cf, cb, ctk = load_consts(cx, None, cst, G0_END)
    cx.eps_col = cf[:, C_EPS:C_EPS + 1]
    ones_bf = cb[:, C_ONES:C_ONES + 128]
    TOP = cx.sb(None, [128, 16384], F32, "TOP")
    R1 = cx.sb(None, [128, 12288], F32, "R1")
    mk = cx.mark()
    X, Xtk = emit_attention(cx, xT_ext, wqkv, wo, cf, cb, ctk, TOP, R1, lvl=lvl, hps=hps)
    cx.release(mk)
    cx.top = cx.top - 12288 * 4
    if debug:
        emit_store(cx, X, Xtk, dbg_a)
    emit_mlp(cx, X, Xtk, cf[:, G0_MLPN:G0_MLPN + 8], ctk, w1, w2, ones_bf)
    if debug:
        emit_store(cx, X, Xtk, dbg_m)
    emit_ple(cx, X, Xtk, cf[:, G0_PLEN:G0_PLEN + 8], ctk, wg, wp, pT, ones_bf)
    emit_store(cx, X, Xtk, out)
    P.finish()
    return nc, cx


def layer0_inputs(inputs, core):
    x = inputs["x"][0]
    lo = core * NT
    xe = np.zeros((2 * NT, D), np.float32)
    if core > 0:
        xe[:NT] = x[lo - NT:lo]
    xe[NT:] = x[lo:lo + NT]
    c = base_consts(core, G0_END)
    c[:, G0_ANORM:G0_ANORM + 8] = col_layout(inputs["a_norm"][0])
    c[:, G0_QG:G0_QG + 3] = np.tile(inputs["a_q_gain"][0].T, (2, 1))
    c[:, G0_KG:G0_KG + 3] = np.tile(inputs["a_k_gain"][0].T, (2, 1))
    c[:, G0_MLPN:G0_MLPN + 8] = col_layout(inputs["mlp_norm"][0])
    c[:, G0_PLEN:G0_PLEN + 8] = col_layout(inputs["ple_norm"][0])
    return {
        "xT_ext": np.ascontiguousarray(xe.T),
        "pT": np.ascontiguousarray(inputs["p"][0, 0, lo:lo + NT].T),
        "cst": c,
        "a_w_qkv": inputs["a_w_qkv"][0], "a_w_o": inputs["a_w_o"][0],
        "mlp_w1": inputs["mlp_w1"][0], "mlp_w2": inputs["mlp_w2"][0],
        "ple_w_gate": inputs["ple_w_gate"][0], "ple_w_proj": inputs["ple_w_proj"][0],
    }

BH = 4
DH = 512
NCK = NT // 128
KSCALE = DH ** -0.5
NST = 8192 + 2048 + 8

L_BNORM = C_GAINS
L_MLPN = L_BNORM + 8
L_PLEN = L_MLPN + 8
L_CONVW = L_PLEN + 8
L_CONVB = L_CONVW + 64
L_SKIP = L_CONVB + 16
L_HGAIN = L_SKIP + 16
L_BI = L_HGAIN + 16
L_BF = L_BI + 1
L_MASKLOW = L_BF + 1
L_SEL = L_MASKLOW + 128
L_CMASK = L_SEL + 512
L_NEG = L_CMASK + 7
L_E0 = L_NEG + 1
L_LNK = L_E0 + 1
L_CNEG = L_LNK + 1
L_END = L_CNEG + 7


def layer1_consts(inputs, core):
    c = base_consts(core, L_END)
    c[:, L_BNORM:L_BNORM + 8] = col_layout(inputs["b_norm"][0])
    c[:, L_MLPN:L_MLPN + 8] = col_layout(inputs["mlp_norm"][1])
    c[:, L_PLEN:L_PLEN + 8] = col_layout(inputs["ple_norm"][1])
    cw = inputs["b_conv_w"][0]
    c[:, L_CONVW:L_CONVW + 64] = cw.reshape(4, 16, 128).transpose(2, 1, 0).reshape(128, 64)
    c[:, L_CONVB:L_CONVB + 16] = col_layout(inputs["b_conv_b"][0])
    c[:, L_SKIP:L_SKIP + 16] = col_layout(inputs["b_skip"][0])
    c[:, L_HGAIN:L_HGAIN + 16] = col_layout(inputs["b_h_gain"][0])
    bg = inputs["b_b_gate"][0]
    c[0:4, L_BI] = bg[0:4]
    c[0:4, L_BF] = bg[4:8]
    s_ = np.arange(128)[:, None]
    t_ = np.arange(128)[None, :]
    c[:, L_MASKLOW:L_MASKLOW + 128] = np.where(s_ <= t_, 0.0, BIG)
    for hd in range(4):
        c[hd, L_SEL + hd * 128:L_SEL + (hd + 1) * 128] = 1.0
    for cp in range(7):
        c[:, L_CMASK + cp] = 1.0 if cp < core else 0.0
        c[:, L_CNEG + cp] = 0.0 if cp < core else -1e30
    c[:, L_NEG] = -1e30
    c[0, L_E0] = 1.0
    c[:, L_LNK] = np.log(KSCALE)
    return c


def bd_compact(w, transpose=False):
    out = np.zeros((2048, 128), np.float32)
    n = np.arange(512)
    for j in range(4):
        for k in range(4):
            if transpose:
                out[4 * n + k, (4 * n + j) % 128] = w[:, j, k]
            else:
                out[4 * n + j, (4 * n + k) % 128] = w[:, j, k]
    return out


def layer1_inputs(inputs, core, x1T_full, stage, st_all=None, g_in=None):
    lo = core * NT
    xh = np.zeros((D, 4), np.float32)
    if core > 0:
        xh[:, 1:4] = x1T_full[:, lo - 3:lo]
    m = {
        "x1T": np.ascontiguousarray(x1T_full[:, lo:lo + NT]),
        "xh": xh,
        "cst": layer1_consts(inputs, core),
        "b_w_up": inputs["b_w_up"][0],
        "bd": np.stack([bd_compact(inputs["b_w_q"][0]), bd_compact(inputs["b_w_k"][0]), bd_compact(inputs["b_w_v"][0])]),
        "bdT": np.stack([bd_compact(inputs["b_w_q"][0], True), bd_compact(inputs["b_w_k"][0], True),
                         bd_compact(inputs["b_w_v"][0], True)]),
        "w_gate": inputs["b_w_gate"][0],
    }
    if stage == "C":
        m.update({
            "st_all": st_all,
            "g_in": g_in,
            "b_w_down": inputs["b_w_down"][0],
            "pT": np.ascontiguousarray(inputs["p"][1, 0, lo:lo + NT].T),
            "mlp_w1": inputs["mlp_w1"][1], "mlp_w2": inputs["mlp_w2"][1],
            "ple_w_gate": inputs["ple_w_gate"][1], "ple_w_proj": inputs["ple_w_proj"][1],
        })
    return m


def build_layer1(stage, debug=False, dbg_stop=None):
    nc = bass.Bass("TRN2", target_bir_lowering=False)
    x1T = nc.dram_tensor("x1T", [D, NT], F32, kind="ExternalInput").ap()
    xh = nc.dram_tensor("xh", [D, 4], F32, kind="ExternalInput").ap()
    cst = nc.dram_tensor("cst", [128, L_END], F32, kind="ExternalInput").ap()
    wup = nc.dram_tensor("b_w_up", [D, 4096], F32, kind="ExternalInput").ap()
    bd = nc.dram_tensor("bd", [3, 2048, 128], F32, kind="ExternalInput").ap()
    bdT = nc.dram_tensor("bdT", [3, 2048, 128], F32, kind="ExternalInput").ap()
    wgate = nc.dram_tensor("w_gate", [6144, 8], F32, kind="ExternalInput").ap()
    if stage == "B":
        st_out = nc.dram_tensor("st_out", [128, NST], F32, kind="ExternalOutput").ap()
        g_out = nc.dram_tensor("g_out", [8, NT], F32, kind="ExternalOutput").ap()
    else:
        st_all = nc.dram_tensor("st_all", [7, 128, NST], F32, kind="ExternalInput").ap()
        g_in = nc.dram_tensor("g_in", [8, NT], F32, kind="ExternalInput").ap()
        wdown = nc.dram_tensor("b_w_down", [2048, D], F32, kind="ExternalInput").ap()
        pT = nc.dram_tensor("pT", [256, NT], F32, kind="ExternalInput").ap()
        w1 = nc.dram_tensor("mlp_w1", [D, 4096], F32, kind="ExternalInput").ap()
        w2 = nc.dram_tensor("mlp_w2", [4096, D], F32, kind="ExternalInput").ap()
        wg = nc.dram_tensor("ple_w_gate", [D, D], F32, kind="ExternalInput").ap()
        wp = nc.dram_tensor("ple_w_proj", [256, D], F32, kind="ExternalInput").ap()
        out = nc.dram_tensor("xout", [D, NT], F32, kind="ExternalOutput").ap()
        yscr = nc.dram_tensor("yscr", [2048, NT], BF16).ap()
        if debug:
            dbg_a = nc.dram_tensor("dbg_a", [D, NT], F32, kind="ExternalOutput").ap()
    cx = Ctx(nc)
    P = cx.P
    cf, cb, ctk = load_consts(cx, None, cst, L_END)
    cx.eps_col = cf[:, C_EPS:C_EPS + 1]
    ones_bf = cb[:, C_ONES:C_ONES + 128]
    ones_f = cf[:, C_ONES:C_ONES + 128]
    ident_f = cf[:, C_ID:C_ID + 128]
    one_col = cf[:, C_ONES:C_ONES + 1]
    TOP = cx.sb(None, [128, 16384], F32, "TOP")
    hT = TOP[:, 0:8208].bitcast(BF16)[:, 0:8 * 2052].rearrange("p (c n) -> p c n", c=NC8)
    topfree = TOP[:, 8208:16384]
    base_mark = cx.mark()
    bk = [(cx.banks[i], Tk()) for i in range(8)]
    ws = WStream(cx, None, 4096, nstage=0, nslot=3)
    ws.stage = Rot([topfree[:, 0:4096]])
    bdh = cx.sb(None, [128, 3, 4, 128], BF16, "bdh")
    bdh_st = cx.sb(None, [128, 3, 4, 128], F32, "bdh_st")
    diag = cx.sb(None, [128, 4, 4, 128], BF16, "diag")
    xms = [cx.sb(None, [128, 4, 516], BF16, "xm") for _ in range(2)]
    xc = cx.sb(None, [128, 4, 512], BF16, "xc")
    GF = cx.sb(None, [128, NT], F32, "GF")
    BETAx = cx.sb(None, [128, NT + 1], F32, "BETAx")
    small = cx.sb(None, [128, 64], F32, "small")
    TMw = cx.sb(None, [128, NCK, 4], F32, "TMw")
    TMa = cx.sb(None, [128, NCK, 4], F32, "TMa")
    if stage == "C":
        fold = cx.sb(None, [128, 7, 8], F32, "foldin")
        S1 = cx.sb(None, [128, 7, 4], F32, "S1")
        S2 = cx.sb(None, [128, 7, 4], F32, "S2")
        mrun = cx.sb(None, [128, 4], F32, "mrun")
        fa = cx.sb(None, [128, 4], F32, "fa")
        fb = cx.sb(None, [128, 4], F32, "fb")
        fc_ = cx.sb(None, [128, 4], F32, "fc")
    pers_mark = cx.mark()

    mk = cx.mark()
    sq_rot = Rot([cx.sb(None, [128, 512], BF16, "sq") for _ in range(2)])
    rstd_rot = Rot([cx.sb(None, [128, 512], F32, "rstd") for _ in range(2)])
    xstg = [cx.sb(None, [128, NC8, 512], F32, "xstg") for _ in range(2)]
    xstk = [Tk(), Tk()]
    ps_stat = Rot([bk[7], bk[6]])
    gcol = cf[:, L_BNORM:L_BNORM + 8]
    httk = Tk()
    pieces = [(None, 4)] + [(tg, 512) for tg in range(4)]
    for i, (tg, n) in enumerate(pieces):
        xa = xstg[i % 2]
        xt = xstk[i % 2]
        if tg is None:
            P.dma("sync", xa[:, :, 0:4], xh.rearrange("(c p) n -> p c n", p=128), writes=[xt])
            h0 = 0
        else:
            P.dma("sync", xa, x1T[:, tg * 512:(tg + 1) * 512].rearrange("(c p) n -> p c n", p=128), writes=[xt])
            h0 = 4 + tg * 512
        xs = [(xa[:, c, 0:n], xt) for c in range(NC8)]
        ps_ap, ps_tk = ps_stat.next()
        rstd, rtk = rstd_rot.next()
        rms_stats(cx, xs, n, sq_rot, ps_ap, ps_tk, rstd, rtk, ones_bf, ctk, 1.0 / D)
        for c in range(NC8):
            P.op("dve", lambda e, c=c, xa=xa, rstd=rstd, n=n, h0=h0: e.scalar_tensor_tensor(
                out=hT[:, c, h0:h0 + n], in0=xa[:, c, 0:n], scalar=gcol[:, c:c + 1], in1=rstd[:, 0:n],
                op0=ALU.mult, op1=ALU.mult), reads=[xt, rtk, ctk], writes=[httk])
    P.barrier()
    cx.release(mk)

    GI = cx.sb(None, [128, NT], F32, "GI")
    LF = cx.sb(None, [128, NT], F32, "LF")
    BB = cx.sb(None, [128, NT], F32, "BB")
    T1 = cx.sb(None, [128, NT], F32, "T1")
    wfold = [[cx.sb(None, [128, 16, 128], BF16, "wfold") for _ in range(2)] for _ in range(2)]
    wftk = Tk()
    bdT_sb = topfree[:, 0:6144].rearrange("p (a b) -> p a b", a=48)
    wg_sb = topfree[:, 6144:6528].rearrange("p (a b) -> p a b", a=48)
    btk = Tk()
    for j in range(3 if stage == "B" else 0):
        P.dma("sync", bdT_sb[:, j * 16:(j + 1) * 16, :], bdT[j].rearrange("(c p) n -> p c n", p=128), writes=[btk])
    if stage == "B":
        P.dma("sync", wg_sb, wgate.rearrange("(c p) n -> p c n", p=128), writes=[btk])
    zpad = Rot([topfree[:, 6528 + i * 128:6528 + (i + 1) * 128] for i in range(4)])
    for (za, ztk) in zpad.items:
        P.op("pool", lambda e, za=za: e.memset(za, 0.0), writes=[ztk])
    psF = Rot(bk[0:2])
    for mc in range(16 if stage == "B" else 0):
        for part in range(2):
            for xm_ in range(2):
                ps, pstk = psF.next()
                srcs = (0, 1) if xm_ == 0 else (2,)
                for si, j in enumerate(srcs):
                    za, ztk = zpad.next()
                    P.op("dve", lambda e, za=za, j=j, mc=mc, part=part: e.tensor_copy(
                        out=za[:, 0:4], in_=wg_sb[:, j * 16 + mc, part * 4:part * 4 + 4]), reads=[btk], writes=[ztk])
                    P.op("pe", lambda e, ps=ps, za=za, j=j, mc=mc, si=si, srcs=srcs: e.matmul(
                        ps[:, 0:128], lhsT=bdT_sb[:, j * 16 + mc, :], rhs=za, start=(si == 0), stop=(si == len(srcs) - 1)),
                        reads=[btk, ztk], writes=[pstk])
                P.op("act", lambda e, ps=ps, xm_=xm_, part=part, mc=mc: e.activation(
                    out=wfold[xm_][part][:, mc, :], in_=ps[:, 0:128], func=AF.Copy), reads=[pstk], writes=[wftk])
    P.barrier()

    hdtk = Tk()
    xmtk = [Tk(), Tk()]
    xctk = Tk()
    psA = Rot(bk[0:2])
    wupv = wup.rearrange("(c p) n -> p c n", p=128)
    bdv = bd.rearrange("j (c p) n -> p j c n", p=128)
    state = {"i": 0}

    def head_setup(hd):
        for j in range(3):
            P.dma("sync", bdh_st[:, j, :, :], bdv[:, j, hd * 4:(hd + 1) * 4, :], writes=[hdtk])
        P.op("pool", lambda e: e.tensor_copy(out=bdh, in_=bdh_st), reads=[hdtk], writes=[hdtk])
        for mc in range(4):
            for k in range(4):
                col = L_CONVW + (hd * 4 + mc) * 4 + k
                P.op("act", lambda e, mc=mc, k=k, col=col: e.activation(
                    out=diag[:, mc, k, :], in_=ident_f, func=AF.Copy, scale=cf[:, col:col + 1]),
                    reads=[ctk], writes=[hdtk])
        wx, wxtk = ws.load([wupv[:, :, hd * 512:(hd + 1) * 512]])
        return wx.rearrange("p (c n) -> p c n", c=NC8), wxtk

    def front(hd, tg, wx3, wxtk):
        i = state["i"]
        state["i"] += 1
        xm, xmt = xms[i % 2], xmtk[i % 2]
        xmp, xmpt = xms[(i + 1) % 2], xmtk[(i + 1) % 2]
        for mc in range(4):
            ps, pstk = psA.next()
            for c in range(NC8):
                P.op("pe", lambda e, c=c, mc=mc, ps=ps: e.matmul(
                    ps, lhsT=wx3[:, c, mc * 128:(mc + 1) * 128], rhs=hT[:, c, 4 + tg * 512:4 + (tg + 1) * 512],
                    start=(c == 0), stop=(c == NC8 - 1)), reads=[wxtk], writes=[pstk], signal=(c == NC8 - 1))
            P.op("act", lambda e, ps=ps, mc=mc, xm=xm: e.activation(out=xm[:, mc, 4:516], in_=ps, func=AF.Copy),
                 reads=[pstk], writes=[xmt])
            if tg == 0:
                ps, pstk = psA.next()
                for c in range(NC8):
                    P.op("pe", lambda e, c=c, mc=mc, ps=ps: e.matmul(
                        ps[:, 0:4], lhsT=wx3[:, c, mc * 128:(mc + 1) * 128], rhs=hT[:, c, 0:4],
                        start=(c == 0), stop=(c == NC8 - 1)), reads=[wxtk], writes=[pstk], signal=(c == NC8 - 1))
                P.op("act", lambda e, ps=ps, mc=mc, xm=xm: e.activation(out=xm[:, mc, 0:4], in_=ps[:, 0:4], func=AF.Copy),
                     reads=[pstk], writes=[xmt])
        if tg > 0:
            P.op("pool", lambda e, xm=xm, xmp=xmp: e.tensor_copy(out=xm[:, :, 0:4], in_=xmp[:, :, 512:516]),
                 reads=[xmpt], writes=[xmt])
        for mc in range(4):
            ps, pstk = psA.next()
            for k in range(4):
                P.op("pe", lambda e, k=k, mc=mc, ps=ps, xm=xm: e.matmul(
                    ps, lhsT=diag[:, mc, k, :], rhs=xm[:, mc, 1 + k:1 + k + 512], start=(k == 0), stop=(k == 3)),
                    reads=[hdtk, xmt], writes=[pstk], signal=(k == 3))
            col = L_CONVB + hd * 4 + mc
            P.op("act", lambda e, ps=ps, mc=mc, col=col: e.activation(
                out=xc[:, mc, :], in_=ps, func=AF.Silu, bias=cf[:, col:col + 1]), reads=[pstk, ctk], writes=[xctk])
        return xm, xmt

    gtk = Tk()
    psG = Rot(bk[2:4])
    if stage == "C":
        P.op("pool", lambda e: e.memset(GI, 0.0), writes=[gtk])
        P.op("pool", lambda e: e.memset(GF, 0.0), writes=[gtk])
        P.dma("sync", GI[0:4, :], g_in[0:4, :], writes=[gtk])
        P.dma("sync", GF[0:4, :], g_in[4:8, :], writes=[gtk])
    for hd in range(BH if stage == "B" else 0):
        wx3, wxtk = head_setup(hd)
        for tg in range(4):
            xm, xmt = front(hd, tg, wx3, wxtk)
            for part, Grow in ((0, GI), (1, GF)):
                ps, pstk = psG.next()
                for mc in range(4):
                    P.op("pe", lambda e, ps=ps, mc=mc, part=part: e.matmul(
                        ps, lhsT=wfold[0][part][:, hd * 4 + mc, :], rhs=xc[:, mc, :], start=(mc == 0), stop=False),
                        reads=[wftk, xctk], writes=[pstk], signal=False)
                    P.op("pe", lambda e, ps=ps, mc=mc, part=part, xm=xm: e.matmul(
                        ps, lhsT=wfold[1][part][:, hd * 4 + mc, :], rhs=xm[:, mc, 4:516], start=False, stop=(mc == 3)),
                        reads=[wftk, xmt], writes=[pstk], signal=(mc == 3))
                sl = slice(tg * 512, (tg + 1) * 512)
                if hd == 0:
                    P.op("act", lambda e, ps=ps, Grow=Grow, sl=sl: e.activation(out=Grow[:, sl], in_=ps, func=AF.Copy),
                         reads=[pstk], writes=[gtk])
                else:
                    P.op("dve", lambda e, ps=ps, Grow=Grow, sl=sl: e.tensor_tensor(out=Grow[:, sl], in0=ps, in1=Grow[:, sl], op=ALU.add),
                         reads=[pstk, gtk], writes=[gtk])

    rtk = Tk()
    if stage == "B":
        P.dma("sync", g_out[0:4, :], GI[0:4, :], reads=[gtk])
        P.dma("sync", g_out[4:8, :], GF[0:4, :], reads=[gtk])
    P.op("dve", lambda e: e.tensor_scalar(out=GI, in0=GI, scalar1=cf[:, L_BI:L_BI + 1], scalar2=None, op0=ALU.add),
         reads=[gtk, ctk], writes=[gtk])
    P.op("dve", lambda e: e.tensor_scalar(out=GF, in0=GF, scalar1=cf[:, L_BF:L_BF + 1], scalar2=None, op0=ALU.add),
         reads=[gtk, ctk], writes=[gtk])
    P.op("dve", lambda e: e.tensor_scalar(out=T1, in0=GF, scalar1=-1.0, scalar2=None, op0=ALU.mult), reads=[gtk], writes=[rtk])
    P.op("dve", lambda e: e.tensor_tensor(out=T1, in0=T1, in1=GF, op=ALU.max), reads=[gtk, rtk], writes=[rtk])
    P.op("act", lambda e: e.activation(out=T1, in_=T1, func=AF.Exp, scale=-1.0), reads=[rtk], writes=[rtk])
    P.op("act", lambda e: e.activation(out=T1, in_=T1, func=AF.Ln, bias=one_col), reads=[rtk, ctk], writes=[rtk])
    P.op("dve", lambda e: e.scalar_tensor_tensor(out=LF, in0=GF, scalar=0.0, in1=T1, op0=ALU.min, op1=ALU.subtract),
         reads=[gtk, rtk], writes=[rtk])
    P.op("pool", lambda e: e.memset(T1, 1.0), reads=[rtk], writes=[rtk])
    P.op("dve", lambda e: e.tensor_tensor_scan(out=BB, data0=T1, data1=LF, initial=0.0, op0=ALU.mult, op1=ALU.add),
         reads=[rtk], writes=[rtk])
    P.op("dve", lambda e: e.tensor_tensor(out=T1, in0=GI, in1=BB, op=ALU.subtract), reads=[gtk, rtk], writes=[rtk])
    ALPHA = T1
    psR = Rot([bk[4]])
    tmtk = Tk()

    def to_token_major(row, dst):
        ps, pstk = psR.next()
        for ck in range(NCK):
            P.op("pe", lambda e, ps=ps, ck=ck: e.matmul(ps[:, ck * 4:ck * 4 + 4], lhsT=row[:, ck * 128:(ck + 1) * 128],
                                                        rhs=ident_f[:, 0:4], start=True, stop=True),
                 reads=[rtk, gtk, ctk], writes=[pstk], signal=(ck == NCK - 1))
        P.op("act", lambda e, ps=ps: e.activation(out=dst, in_=ps[:, 0:64].rearrange("p (a b) -> p a b", a=NCK), func=AF.Copy),
             reads=[pstk], writes=[tmtk])

    def replicate_cols(col_ap, dst4):
        ps, pstk = psR.next()
        za = small[:, 32:32 + 4]
        P.op("dve", lambda e: e.tensor_scalar(out=za, in0=ident_f[:, 0:4], scalar1=col_ap, scalar2=None, op0=ALU.mult),
             reads=[rtk, ctk, gtk], writes=[rtk])
        P.op("pe", lambda e, ps=ps: e.matmul(ps[:, 0:4], lhsT=ones_f, rhs=za, start=True, stop=True),
             reads=[rtk, ctk], writes=[pstk])
        P.op("act", lambda e, ps=ps: e.activation(out=dst4, in_=ps[:, 0:4], func=AF.Copy), reads=[pstk], writes=[rtk])

    if stage == "B":
        mx = small[:, 0:1]
        P.op("dve", lambda e: e.tensor_reduce(out=mx, in_=ALPHA, axis=AX.X, op=ALU.max), reads=[rtk], writes=[rtk])
        nb_ = small[:, 1:2]
        P.op("dve", lambda e: e.scalar_tensor_tensor(out=nb_, in0=mx, scalar=-1.0, in1=cf[:, L_LNK:L_LNK + 1],
                                                     op0=ALU.mult, op1=ALU.add), reads=[rtk, ctk], writes=[rtk])
        P.op("act", lambda e: e.activation(out=LF, in_=ALPHA, func=AF.Exp, bias=nb_), reads=[rtk], writes=[rtk])
        to_token_major(LF, TMw)
        ml = small[:, 2:3]
        P.op("dve", lambda e: e.tensor_tensor(out=ml, in0=mx, in1=BB[:, NT - 1:NT], op=ALU.add), reads=[rtk], writes=[rtk])
        fin = cx.sb(None, [128, 8], F32, "fin")
        replicate_cols(BB[:, NT - 1:NT], fin[:, 0:4])
        replicate_cols(ml, fin[:, 4:8])
        P.dma("sync", st_out[:, 10240:10248], fin, reads=[rtk])
        P.barrier()
        cx.release(pers_mark)
        kv_rot = Rot([cx.sb(None, [128, 512], BF16, "kv") for _ in range(4)])
        stC = cx.sb(None, [128, 4, 512], F32, "stC")
        stn = cx.sb(None, [128, 512], F32, "stn")
        sttk = Tk()
        psKV = Rot([bk[0], bk[1], bk[2]])
        for hd in range(BH):
            wx3, wxtk = head_setup(hd)
            cacc = [bk[3 + dc] for dc in range(4)]
            nacc, nacctk = bk[7]
            for tg in range(4):
                xm, xmt = front(hd, tg, wx3, wxtk)
                KV = {}

                def s1_pre(cl):
                    ck = tg * 4 + cl
                    tsl = slice(cl * 128, (cl + 1) * 128)
                    ps, pstk = psKV.next()
                    for mc in range(4):
                        P.op("pe", lambda e, ps=ps, mc=mc, tsl=tsl: e.matmul(
                            ps[:, mc * 128:(mc + 1) * 128], lhsT=xc[:, mc, tsl], rhs=bdh[:, 1, mc, :], start=True, stop=True),
                            reads=[xctk, hdtk], writes=[pstk], signal=(mc == 3))
                    wk, wktk = kv_rot.next()
                    P.op("act", lambda e, ps=ps, wk=wk, ck=ck, hd=hd: e.activation(
                        out=wk, in_=ps, func=AF.Copy, scale=TMw[:, ck, hd:hd + 1]), reads=[pstk, tmtk], writes=[wktk])
                    ps, pstk = psKV.next()
                    for mc in range(4):
                        P.op("pe", lambda e, ps=ps, mc=mc, cl=cl, xm=xm: e.matmul(
                            ps[:, mc * 128:(mc + 1) * 128], lhsT=xm[:, mc, 4 + cl * 128:4 + (cl + 1) * 128], rhs=bdh[:, 2, mc, :],
                            start=True, stop=True), reads=[xmt, hdtk], writes=[pstk], signal=(mc == 3))
                    vv, vtk = kv_rot.next()
                    P.op("act", lambda e, ps=ps, vv=vv: e.activation(out=vv, in_=ps, func=AF.Copy), reads=[pstk], writes=[vtk])
                    KV[cl] = (ck, wk, wktk, vv, vtk)

                def s1_acc(cl):
                    ck, wk, wktk, vv, vtk = KV[cl]
                    last = (ck == NCK - 1)
                    for dc in range(4):
                        P.op("pe", lambda e, dc=dc, wk=wk, vv=vv, ck=ck, last=last: e.matmul(
                            cacc[dc][0], lhsT=wk[:, dc * 128:(dc + 1) * 128], rhs=vv, start=(ck == 0), stop=last),
                            reads=[wktk, vtk], writes=[cacc[dc][1]], signal=True)
                    P.op("pe", lambda e, wk=wk, ck=ck, last=last: e.matmul(
                        nacc, lhsT=ones_bf, rhs=wk, start=(ck == 0), stop=last), reads=[wktk, ctk], writes=[nacctk], signal=True)
                s1_pre(0)
                for cl in range(1, 4):
                    s1_pre(cl)
                    s1_acc(cl - 1)
                s1_acc(3)
            for dc in range(4):
                P.op("act", lambda e, dc=dc: e.activation(out=stC[:, dc, :], in_=cacc[dc][0], func=AF.Copy),
                     reads=[cacc[dc][1]], writes=[sttk])
            P.op("dve", lambda e: e.tensor_copy(out=stn, in_=nacc), reads=[nacctk], writes=[sttk])
            P.dma("sync", st_out[:, hd * 2048:(hd + 1) * 2048], stC.rearrange("p a b -> p (a b)"), reads=[sttk])
            P.dma("sync", st_out[:, 8192 + hd * 512:8192 + (hd + 1) * 512], stn, reads=[sttk])
        P.finish()
        return nc, cx

    ftk = Tk()
    P.dma("sync", fold, st_all[:, :, 10240:10248].rearrange("c p n -> p c n"), writes=[ftk])
    negc = cf[:, L_NEG:L_NEG + 1]
    P.op("dve", lambda e: e.memset(mrun, -1e30), writes=[ftk])
    for cp in range(7):
        mu = cf[:, L_CMASK + cp:L_CMASK + cp + 1]
        P.op("dve", lambda e, cp=cp, mu=mu: e.scalar_tensor_tensor(out=fa, in0=fold[:, cp, 0:4], scalar=mu, in1=mrun,
                                                                    op0=ALU.mult, op1=ALU.add), reads=[ftk, ctk], writes=[ftk])
        P.op("dve", lambda e, cp=cp, mu=mu: e.tensor_scalar(out=fb, in0=fold[:, cp, 4:8], scalar1=mu,
                                                            scalar2=cf[:, L_CNEG + cp:L_CNEG + cp + 1], op0=ALU.mult, op1=ALU.add),
             reads=[ftk, ctk], writes=[ftk])
        P.op("dve", lambda e: e.tensor_tensor(out=fc_, in0=fa, in1=fb, op=ALU.max), reads=[ftk], writes=[ftk])
        P.op("dve", lambda e: e.tensor_tensor(out=fa, in0=fa, in1=fc_, op=ALU.subtract), reads=[ftk], writes=[ftk])
        P.op("dve", lambda e: e.tensor_tensor(out=fb, in0=fb, in1=fc_, op=ALU.subtract), reads=[ftk], writes=[ftk])
        P.op("act", lambda e, cp=cp: e.activation(out=S1[:, cp, :], in_=fa, func=AF.Exp), reads=[ftk], writes=[ftk])
        P.op("act", lambda e: e.activation(out=fb, in_=fb, func=AF.Exp), reads=[ftk], writes=[ftk])
        P.op("dve", lambda e, cp=cp, mu=mu: e.tensor_scalar(out=S2[:, cp, :], in0=fb, scalar1=mu, scalar2=None, op0=ALU.mult),
             reads=[ftk, ctk], writes=[ftk])
        P.op("dve", lambda e: e.tensor_copy(out=mrun, in_=fc_), reads=[ftk], writes=[ftk])
    mst = small[:, 4:5]
    P.op("dve", lambda e: e.tensor_tensor(out=small[:, 8:12], in0=mrun, in1=ident_f[:, 0:4], op=ALU.mult), reads=[ftk, ctk], writes=[rtk])
    P.op("dve", lambda e: e.tensor_reduce(out=mst, in_=small[:, 8:12], axis=AX.X, op=ALU.add), reads=[rtk], writes=[rtk])
    P.op("dve", lambda e: e.tensor_tensor_scan(out=GF, data0=LF, data1=GI, initial=mst, op0=ALU.add, op1=ALU.max),
         reads=[rtk, gtk], writes=[gtk])
    MM = GF
    P.op("dve", lambda e: e.tensor_tensor(out=BETAx[:, 1:NT + 1], in0=MM, in1=BB, op=ALU.subtract), reads=[gtk, rtk], writes=[rtk])
    P.op("dve", lambda e: e.tensor_copy(out=BETAx[:, 0:1], in_=mst), reads=[rtk], writes=[rtk])
    BETA = BETAx[:, 1:NT + 1]
    for ck in range(NCK):
        bl = small[:, 16:17]
        P.op("dve", lambda e, ck=ck: e.scalar_tensor_tensor(out=small[:, 16 + ck % 8:17 + ck % 8], in0=BETAx[:, 128 * (ck + 1):128 * (ck + 1) + 1],
                                                            scalar=-1.0, in1=cf[:, L_LNK:L_LNK + 1], op0=ALU.mult, op1=ALU.add),
             reads=[rtk, ctk], writes=[rtk])
        P.op("act", lambda e, ck=ck: e.activation(out=LF[:, ck * 128:(ck + 1) * 128], in_=ALPHA[:, ck * 128:(ck + 1) * 128],
                                                  func=AF.Exp, bias=small[:, 16 + ck % 8:17 + ck % 8]), reads=[rtk], writes=[rtk])
    to_token_major(LF, TMw)
    to_token_major(ALPHA, TMa)
    P.barrier()
    cx.release(pers_mark)
    BETA = BETAx[:, 1:NT + 1]

    qT = cx.sb(None, [128, 4, 512], BF16, "qT")
    kT = cx.sb(None, [128, 4, 512], BF16, "kT")
    zs = cx.sb(None, [128, 4, 512], BF16, "zs")
    yb = cx.sb(None, [128, 4, 512], BF16, "yb")
    qktk, zstk, ytk = Tk(), Tk(), Tk()
    Csts = [cx.sb(None, [128, 4, 512], F32, "Cst") for _ in range(2)]
    Caugs = [cx.sb(None, [128, 4, 640], BF16, "Caug") for _ in range(2)]
    caugtks = [Tk(), Tk()]
    nrow = cx.sb(None, [128, 512], F32, "nrow")
    nrtk = Tk()
    ncol = cx.sb(None, [128, 4], F32, "ncol")
    nctk = Tk()
    ctk2s = [Tk(), Tk()]
    cpar = {"p": 0}
    clst = Rot([topfree[:, 4096:6144], topfree[:, 6144:8176][:, 0:2032]])
    wk_rot = Rot([cx.sb(None, [128, 512], BF16, "wk") for _ in range(2)])
    va_rot = Rot([cx.sb(None, [128, 640], BF16, "vaug") for _ in range(2)])
    for (va, vatk) in va_rot.items:
        P.op("pool", lambda e, va=va: e.memset(va[:, 512:640], 1.0), writes=[vatk])
    dt_rot = Rot([cx.sb(None, [128, 128], F32, "dtmp") for _ in range(2)])
    sd_rot = Rot([cx.sb(None, [128, 128], BF16, "SdT") for _ in range(2)])
    qs_rot = Rot([cx.sb(None, [128, 4, 128], BF16, "qs") for _ in range(2)])
    hsq_rot = Rot([cx.sb(None, [128, 512], BF16, "hsq") for _ in range(2)])
    dd_rot = Rot([cx.sb(None, [128, 128], F32, "dd") for _ in range(2)])
    rr_rot = Rot([cx.sb(None, [128, 128], F32, "rr") for _ in range(2)])
    sc_rot = Rot([cx.sb(None, [128, 128], F32, "scsb") for _ in range(2)])
    em_rot = Rot([cx.sb(None, [128, 128], F32, "emsb") for _ in range(2)])
    ul_rot = Rot([cx.sb(None, [128, 1], F32, "ulast") for _ in range(2)])
    tt_rot = Rot([cx.sb(None, [128, 128], F32, "tt") for _ in range(3)])
    psB2 = psA
    psS3 = Rot([bk[2]])
    psRP = Rot([bk[2]])
    psH = Rot([bk[4], bk[5]])
    psDS = Rot([bk[6], bk[7]])
    psSS = Rot([bk[3]])
    wzv = wupv
    yview = yscr.rearrange("(c p) n -> p c n", p=128)
    def emit_fold(hd):
        Cst, ctk2 = Csts[hd % 2], ctk2s[hd % 2]
        P.op("pool", lambda e: e.memset(Cst, 0.0), writes=[ctk2])
        P.op("pool", lambda e: e.memset(nrow, 0.0), writes=[nrtk])
        Cflat = Cst.rearrange("p a b -> p (a b)")
        for cp in range(7):
            cl_, cltk = clst.items[0]
            P.dma("sync", cl_, st_all[cp][:, hd * 2048:(hd + 1) * 2048], writes=[cltk])
            P.op("act", lambda e, cp=cp, cl_=cl_: e.activation(out=cl_, in_=cl_, func=AF.Copy, scale=S2[:, cp, hd:hd + 1]),
                 reads=[cltk, ftk], writes=[cltk])
            P.op("dve", lambda e, cp=cp, cl_=cl_: e.scalar_tensor_tensor(out=Cflat, in0=Cflat, scalar=S1[:, cp, hd:hd + 1], in1=cl_,
                                                                          op0=ALU.mult, op1=ALU.add), reads=[cltk, ftk, ctk2], writes=[ctk2])
            nl_, nltk = clst.items[1]
            P.dma("sync", nl_[:, 0:512], st_all[cp][:, 8192 + hd * 512:8192 + (hd + 1) * 512], writes=[nltk])
            P.op("act", lambda e, cp=cp, nl_=nl_: e.activation(out=nl_[:, 0:512], in_=nl_[:, 0:512], func=AF.Copy, scale=S2[:, cp, hd:hd + 1]),
                 reads=[nltk, ftk], writes=[nltk])
            P.op("dve", lambda e, cp=cp, nl_=nl_: e.scalar_tensor_tensor(out=nrow, in0=nrow, scalar=S1[:, cp, hd:hd + 1], in1=nl_[:, 0:512],
                                                                          op0=ALU.mult, op1=ALU.add), reads=[nltk, ftk, nrtk], writes=[nrtk])

    ul_prev = None
    for hd in range(BH):
        wx3, wxtk = head_setup(hd)
        wz, wztk = ws.load([wzv[:, :, 2048 + hd * 512:2048 + (hd + 1) * 512]])
        wz3 = wz.rearrange("p (c n) -> p c n", c=NC8)
        Cst, ctk2 = Csts[hd % 2], ctk2s[hd % 2]
        if hd == 0:
            emit_fold(0)

        def refresh_caug(full):
            Caug, caugtk = Caugs[cpar["p"]], caugtks[cpar["p"]]
            for dc in range(4):
                P.op("act", lambda e, dc=dc: e.activation(out=Caug[:, dc, 0:512], in_=Cst[:, dc, :], func=AF.Copy),
                     reads=[ctk2], writes=[caugtk])
            P.op("dve", lambda e: e.tensor_scalar(out=nrow, in0=nrow, scalar1=cf[:, L_E0:L_E0 + 1], scalar2=None, op0=ALU.mult),
                 reads=[nrtk, ctk], writes=[nrtk])
            ps, pstk = psB2.next()
            for dc in range(4):
                P.op("pe", lambda e, ps=ps, dc=dc: e.matmul(ps[:, dc:dc + 1], lhsT=nrow[:, dc * 128:(dc + 1) * 128], rhs=ones_f[:, 0:1],
                                                            start=True, stop=True), reads=[nrtk, ctk], writes=[pstk], signal=(dc == 3))
            P.op("dve", lambda e, ps=ps: e.tensor_copy(out=ncol, in_=ps[:, 0:4]), reads=[pstk], writes=[nctk])
            P.op("dve", lambda e: e.tensor_copy(out=Caug[:, :, 512:640], in_=ncol.unsqueeze(2).to_broadcast([128, 4, 128])),
                 reads=[nctk], writes=[caugtk])

        refresh_caug(True)
        for tg in range(4):
            xm, xmt = front(hd, tg, wx3, wxtk)
            for mc in range(4):
                ps, pstk = psA.next()
                for c in range(NC8):
                    P.op("pe", lambda e, c=c, mc=mc, ps=ps: e.matmul(
                        ps, lhsT=wz3[:, c, mc * 128:(mc + 1) * 128], rhs=hT[:, c, 4 + tg * 512:4 + (tg + 1) * 512],
                        start=(c == 0), stop=(c == NC8 - 1)), reads=[wztk], writes=[pstk], signal=(c == NC8 - 1))
                P.op("act", lambda e, ps=ps, mc=mc: e.activation(out=zs[:, mc, :], in_=ps, func=AF.Silu), reads=[pstk], writes=[zstk])
            for j, dst, sc_ in ((0, qT, 1.0), (1, kT, KSCALE)):
                for dc in range(4):
                    ps, pstk = psA.next()
                    P.op("pe", lambda e, ps=ps, j=j, dc=dc: e.matmul(ps, lhsT=bdh[:, j, dc, :], rhs=xc[:, dc, :], start=True, stop=True),
                         reads=[hdtk, xctk], writes=[pstk])
                    P.op("act", lambda e, ps=ps, dst=dst, dc=dc, sc_=sc_: e.activation(out=dst[:, dc, :], in_=ps, func=AF.Copy, scale=sc_),
                         reads=[pstk], writes=[qktk])
            RS = {}

            def stage_pre(cl):
                nonlocal ul_prev
                ck = tg * 4 + cl
                tsl = slice(cl * 128, (cl + 1) * 128)
                gsl = slice(ck * 128, (ck + 1) * 128)
                sel = cf[:, L_SEL + hd * 128:L_SEL + (hd + 1) * 128]
                rp, rptk = psRP.next()
                for i3, row in enumerate((BETA, MM)):
                    P.op("pe", lambda e, rp=rp, i3=i3, row=row, gsl=gsl: e.matmul(
                        rp[:, i3 * 128:(i3 + 1) * 128], lhsT=sel, rhs=row[:, gsl], start=True, stop=True),
                        reads=[rtk, gtk, ctk], writes=[rptk], signal=(i3 == 1))
                bprev = mrun[:, hd:hd + 1] if ck == 0 else ul_prev[0]
                bprev_tk = ftk if ck == 0 else ul_prev[1]
                scsb, sctk = sc_rot.next()
                P.op("act", lambda e, rp=rp, scsb=scsb, bprev=bprev: e.activation(out=scsb, in_=rp[:, 0:128], func=AF.Exp, scale=-1.0, bias=bprev),
                     reads=[rptk, bprev_tk], writes=[sctk])
                emsb, emtk = em_rot.next()
                P.op("act", lambda e, rp=rp, emsb=emsb: e.activation(out=emsb, in_=rp[:, 128:256], func=AF.Exp, scale=-1.0),
                     reads=[rptk], writes=[emtk])
                ul_prev = ul_rot.next()
                P.op("act", lambda e, rp=rp, ul_prev=ul_prev: e.activation(out=ul_prev[0], in_=rp[:, 127:128], func=AF.Copy),
                     reads=[rptk], writes=[ul_prev[1]])
                ps, pstk = psB2.next()
                for mc in range(4):
                    P.op("pe", lambda e, ps=ps, mc=mc, tsl=tsl: e.matmul(
                        ps[:, mc * 128:(mc + 1) * 128], lhsT=xc[:, mc, tsl], rhs=bdh[:, 1, mc, :], start=True, stop=True),
                        reads=[xctk, hdtk], writes=[pstk], signal=(mc == 3))
                wk, wktk = wk_rot.next()
                P.op("act", lambda e, ps=ps, wk=wk, ck=ck: e.activation(out=wk, in_=ps, func=AF.Copy, scale=TMw[:, ck, hd:hd + 1]),
                     reads=[pstk, tmtk], writes=[wktk])
                ps, pstk = psB2.next()
                for mc in range(4):
                    P.op("pe", lambda e, ps=ps, mc=mc, cl=cl, xm=xm: e.matmul(
                        ps[:, mc * 128:(mc + 1) * 128], lhsT=xm[:, mc, 4 + cl * 128:4 + (cl + 1) * 128], rhs=bdh[:, 2, mc, :],
                        start=True, stop=True), reads=[xmt, hdtk], writes=[pstk], signal=(mc == 3))
                va, vatk = va_rot.next()
                P.op("act", lambda e, ps=ps, va=va: e.activation(out=va[:, 0:512], in_=ps, func=AF.Copy), reads=[pstk], writes=[vatk])
                pS_, pStk = psS3.next()
                pS = pS_[:, 256:384]
                for dc in range(4):
                    P.op("pe", lambda e, pS=pS, dc=dc, tsl=tsl: e.matmul(pS, lhsT=kT[:, dc, tsl], rhs=qT[:, dc, tsl],
                                                                          start=(dc == 0), stop=(dc == 3)),
                         reads=[qktk], writes=[pStk], signal=(dc == 3))
                dtmp, dttk = dt_rot.next()
                P.op("dve", lambda e, rp=rp, dtmp=dtmp, ck=ck: e.scalar_tensor_tensor(
                    out=dtmp, in0=rp[:, 0:128], scalar=TMa[:, ck, hd:hd + 1], in1=cf[:, L_MASKLOW:L_MASKLOW + 128],
                    op0=ALU.subtract, op1=ALU.max), reads=[rptk, tmtk, ctk], writes=[dttk])
                P.op("act", lambda e, dtmp=dtmp: e.activation(out=dtmp, in_=dtmp, func=AF.Exp, scale=-1.0), reads=[dttk], writes=[dttk])
                sd, sdtk = sd_rot.next()
                P.op("dve", lambda e, pS=pS, dtmp=dtmp, sd=sd: e.tensor_tensor(out=sd, in0=pS, in1=dtmp, op=ALU.mult),
                     reads=[pStk, dttk], writes=[sdtk])
                qs, qstk = qs_rot.next()
                P.op("dve", lambda e, scsb=scsb, qs=qs, tsl=tsl: e.tensor_tensor(
                    out=qs, in0=qT[:, :, tsl], in1=scsb.unsqueeze(1).to_broadcast([128, 4, 128]), op=ALU.mult),
                    reads=[qktk, sctk], writes=[qstk])

                RS[cl] = dict(ck=ck, tsl=tsl, wk=wk, wktk=wktk, va=va, vatk=vatk, sd=sd, sdtk=sdtk, qs=qs, qstk=qstk,
                              scsb=scsb, sctk=sctk, emsb=emsb, emtk=emtk)

            def stage_mid(cl):
                r_ = RS[cl]
                ck, tsl, wk, wktk, va, vatk, sd, sdtk, qs, qstk, scsb, sctk = (r_[k_] for k_ in (
                    "ck", "tsl", "wk", "wktk", "va", "vatk", "sd", "sdtk", "qs", "qstk", "scsb", "sctk"))
                Caug, caugtk = Caugs[cpar["p"]], caugtks[cpar["p"]]
                CaugN, caugNtk = Caugs[1 - cpar["p"]], caugtks[1 - cpar["p"]]
                cpar["p"] = 1 - cpar["p"]
                if dbg_stop is not None and (hd, ck) == tuple(dbg_stop):
                    P.barrier()
                    P.finish()
                    return nc, cx
                decay = scsb[:, 127:128]
                for dc in range(4):
                    ps, pstk = psB2.next()
                    P.op("pe", lambda e, ps=ps, dc=dc, wk=wk, va=va: e.matmul(ps, lhsT=wk[:, dc * 128:(dc + 1) * 128], rhs=va[:, 0:512],
                                                                                start=True, stop=True), reads=[wktk, vatk], writes=[pstk])
                    P.op("dve", lambda e, ps=ps, dc=dc, decay=decay: e.scalar_tensor_tensor(
                        out=Cst[:, dc, :], in0=Cst[:, dc, :], scalar=decay, in1=ps, op0=ALU.mult, op1=ALU.add),
                        reads=[pstk, sctk, ctk2], writes=[ctk2])
                    P.op("dve", lambda e, dc=dc: e.tensor_copy(out=CaugN[:, dc, 0:512], in_=Cst[:, dc, :]),
                         reads=[ctk2], writes=[caugNtk])
                ps, pstk = psB2.next()
                for dc in range(4):
                    P.op("pe", lambda e, ps=ps, dc=dc, wk=wk: e.matmul(ps[:, dc:dc + 1], lhsT=wk[:, dc * 128:(dc + 1) * 128], rhs=ones_bf[:, 0:1],
                                                                         start=True, stop=True), reads=[wktk, ctk], writes=[pstk], signal=(dc == 3))
                P.op("dve", lambda e, ps=ps, decay=decay: e.scalar_tensor_tensor(out=ncol, in0=ncol, scalar=decay, in1=ps[:, 0:4],
                                                                                  op0=ALU.mult, op1=ALU.add),
                     reads=[pstk, sctk, nctk], writes=[nctk])
                P.op("dve", lambda e: e.tensor_copy(out=CaugN[:, :, 512:640], in_=ncol.unsqueeze(2).to_broadcast([128, 4, 128])),
                     reads=[nctk], writes=[caugNtk])
                pH, pHtk = psH.next()
                pD_, pDtk = psDS.next()
                for ec in range(5):
                    o = pH[:, ec * 128:(ec + 1) * 128] if ec < 4 else pD_[:, 0:128]
                    otk = pHtk if ec < 4 else pDtk
                    for dc in range(4):
                        P.op("pe", lambda e, o=o, ec=ec, dc=dc, qs=qs: e.matmul(
                            o, lhsT=Caug[:, dc, ec * 128:(ec + 1) * 128], rhs=qs[:, dc, :], start=(dc == 0), stop=False),
                            reads=[caugtk, qstk], writes=[otk], signal=False)
                    P.op("pe", lambda e, o=o, ec=ec, va=va, sd=sd: e.matmul(
                        o, lhsT=va[:, ec * 128:(ec + 1) * 128], rhs=sd, start=False, stop=True),
                        reads=[vatk, sdtk], writes=[otk], signal=True)

                r_.update(pH=pH, pHtk=pHtk, pD_=pD_, pDtk=pDtk)

            def stage_post(cl):
                r_ = RS[cl]
                ck, tsl, emsb, emtk, pH, pHtk, pD_, pDtk = (r_[k_] for k_ in ("ck", "tsl", "emsb", "emtk", "pH", "pHtk", "pD_", "pDtk"))
                hsq, hsqtk = hsq_rot.next()
                P.op("act", lambda e, pH=pH, hsq=hsq: e.activation(out=hsq, in_=pH, func=AF.Square), reads=[pHtk], writes=[hsqtk])
                pSS_, pSStk = psSS.next()
                pSS = pSS_[:, 0:128]
                for ec in range(4):
                    P.op("pe", lambda e, pSS=pSS, hsq=hsq, ec=ec: e.matmul(pSS, lhsT=ones_bf, rhs=hsq[:, ec * 128:(ec + 1) * 128],
                                                                            start=(ec == 0), stop=(ec == 3)),
                         reads=[hsqtk, ctk], writes=[pSStk], signal=(ec == 3))
                dd, ddtk = dd_rot.next()
                P.op("dve", lambda e, pD_=pD_, dd=dd: e.tensor_scalar(out=dd, in0=pD_[:, 0:128], scalar1=-1.0, scalar2=None, op0=ALU.mult),
                     reads=[pDtk], writes=[ddtk])
                P.op("dve", lambda e, pD_=pD_, dd=dd: e.tensor_tensor(out=dd, in0=dd, in1=pD_[:, 0:128], op=ALU.max),
                     reads=[pDtk, ddtk], writes=[ddtk])
                P.op("dve", lambda e, emsb=emsb, dd=dd: e.tensor_tensor(out=dd, in0=dd, in1=emsb, op=ALU.max),
                     reads=[emtk, ddtk], writes=[ddtk])
                P.op("dve", lambda e, dd=dd: e.scalar_tensor_tensor(out=dd, in0=dd, scalar=EPS, in1=dd, op0=ALU.mult, op1=ALU.mult),
                     reads=[ddtk], writes=[ddtk])
                rr, rrtk = rr_rot.next()
                P.op("dve", lambda e, pSS=pSS, dd=dd, rr=rr: e.scalar_tensor_tensor(out=rr, in0=pSS, scalar=1.0 / DH, in1=dd,
                                                                                     op0=ALU.mult, op1=ALU.add),
                     reads=[pSStk, ddtk], writes=[rrtk])
                P.op("act", lambda e, rr=rr: e.activation(out=rr, in_=rr, func=AF.Sqrt), reads=[rrtk], writes=[rrtk])
                P.op("dve", lambda e, rr=rr: e.reciprocal(out=rr, in_=rr), reads=[rrtk], writes=[rrtk])
                for ec in range(4):
                    ch = hd * 4 + ec
                    tt, tttk = tt_rot.next()
                    P.op("dve", lambda e, pH=pH, ec=ec, ch=ch, rr=rr, tt=tt: e.scalar_tensor_tensor(
                        out=tt, in0=pH[:, ec * 128:(ec + 1) * 128], scalar=cf[:, L_HGAIN + ch:L_HGAIN + ch + 1], in1=rr,
                        op0=ALU.mult, op1=ALU.mult), reads=[pHtk, rrtk, ctk], writes=[tttk])
                    P.op("dve", lambda e, ec=ec, ch=ch, tt=tt, tsl=tsl: e.scalar_tensor_tensor(
                        out=tt, in0=xc[:, ec, tsl], scalar=cf[:, L_SKIP + ch:L_SKIP + ch + 1], in1=tt,
                        op0=ALU.mult, op1=ALU.add), reads=[xctk, tttk, ctk], writes=[tttk])
                    P.op("dve", lambda e, ec=ec, tt=tt, tsl=tsl: e.tensor_tensor(out=yb[:, ec, tsl], in0=tt, in1=zs[:, ec, tsl], op=ALU.mult),
                         reads=[tttk, zstk], writes=[ytk])


            stage_pre(0)
            stage_mid(0)
            for cl in range(1, 4):
                stage_pre(cl)
                stage_mid(cl)
                stage_post(cl - 1)
            stage_post(3)

            if tg == 1 and hd + 1 < BH:
                emit_fold(hd + 1)
            P.dma("sync", yview[:, hd * 4:(hd + 1) * 4, tg * 512:(tg + 1) * 512], yb, reads=[ytk])
    P.barrier()
    cx.release(base_mark)

    X = TOP.rearrange("p (c n) -> p c n", c=NC8)
    Xtk = [[Tk() for _ in range(4)] for _ in range(NC8)]
    for c in range(NC8):
        for tg in range(4):
            P.dma("sync", X[:, c, tg * 512:(tg + 1) * 512], x1T[c * 128:(c + 1) * 128, tg * 512:(tg + 1) * 512], writes=[Xtk[c][tg]])
    mk = cx.mark()
    wdn = cx.sb(None, [128, 16, D], BF16, "wdn")
    wdtk = Tk()
    wdv = wdown.rearrange("(c p) n -> p c n", p=128)
    wstg3 = Rot([cx.sb(None, [128, 4, D], F32, "wstg3") for _ in range(2)])
    for q4 in range(4):
        stg_, stk_ = wstg3.next()
        P.dma("sync", stg_, wdv[:, q4 * 4:(q4 + 1) * 4, :], writes=[stk_])
        P.op("act", lambda e, stg_=stg_, q4=q4: e.activation(out=wdn[:, q4 * 4:(q4 + 1) * 4, :], in_=stg_, func=AF.Copy),
             reads=[stk_], writes=[wdtk])
    yts = [cx.sb(None, [128, 16, 512], BF16, "yt") for _ in range(2)]
    yttk = [Tk(), Tk()]
    psA4 = Rot(bk[0:4])
    for tg in range(4):
        yt, ytt = yts[tg % 2], yttk[tg % 2]
        P.dma("sync", yt, yview[:, :, tg * 512:(tg + 1) * 512], writes=[ytt])
        sl = slice(tg * 512, (tg + 1) * 512)
        for oc in range(NC8):
            ps, pstk = psA4.next()
            for mc in range(16):
                P.op("pe", lambda e, ps=ps, mc=mc, oc=oc, yt=yt: e.matmul(ps, lhsT=wdn[:, mc, oc * 128:(oc + 1) * 128], rhs=yt[:, mc, :],
                                                                           start=(mc == 0), stop=(mc == 15)),
                     reads=[wdtk, ytt], writes=[pstk], signal=(mc == 15))
            P.op("dve", lambda e, ps=ps, oc=oc, sl=sl: e.tensor_tensor(out=X[:, oc, sl], in0=ps, in1=X[:, oc, sl], op=ALU.add),
                 reads=[pstk, Xtk[oc][tg]], writes=[Xtk[oc][tg]])
    P.barrier()
    cx.release(mk)
    if debug:
        emit_store(cx, X, Xtk, dbg_a)
    emit_mlp(cx, X, Xtk, cf[:, L_MLPN:L_MLPN + 8], ctk, w1, w2, ones_bf)
    emit_ple(cx, X, Xtk, cf[:, L_PLEN:L_PLEN + 8], ctk, wg, wp, pT, ones_bf)
    emit_store(cx, X, Xtk, out)
    P.finish()
    return nc, cx


_CACHE = {}


def _prog(key, builder):
    return builder()


def kernel(**inputs):
    inputs = {k: np.asarray(v) for k, v in inputs.items()}
    cores = list(range(NCORES))
    nc, _ = build_layer0()
    in_maps = [layer0_inputs(inputs, c) for c in cores]
    res = run_bass_kernel_spmd(nc, in_maps, core_ids=cores)
    x1T = np.concatenate([r["xout"] for r in res.results], axis=1)
    nc, _ = build_layer1("B")
    in_maps = [layer1_inputs(inputs, c, x1T, "B") for c in cores]
    res = run_bass_kernel_spmd(nc, in_maps, core_ids=cores)
    st_all = np.stack([res.results[c]["st_out"] for c in range(7)])
    g_rows = [res.results[c]["g_out"] for c in cores]
    nc, _ = build_layer1("C")
    in_maps = [layer1_inputs(inputs, c, x1T, "C", st_all, g_rows[c]) for c in cores]
    res = run_bass_kernel_spmd(nc, in_maps, core_ids=cores)
    outT = np.concatenate([r["xout"] for r in res.results], axis=1)
    return np.ascontiguousarray(outT.T)[None].astype(np.float32)
```

```python
import numpy as np
import concourse.bass as bass
import concourse.mybir as mybir
from concourse.bass_utils import run_bass_kernel_spmd

F32 = mybir.dt.float32
BF16 = mybir.dt.bfloat16
AF = mybir.ActivationFunctionType
ALU = mybir.AluOpType
AX = mybir.AxisListType

NCORES = 8
S = 16384
D = 1024
NT = S // NCORES
NC8 = D // 128
EPS = 1e-6
BIG = 30000.0
A_GROUPS = ((128, 1), (512, 4), (2048, 16))
NDMA = 24
SB_F32 = 51968


class Tk:
    __slots__ = ("w", "r")

    def __init__(self):
        self.w = {}
        self.r = {}


class Prog:
    def __init__(self, nc):
        self.nc = nc
        self.eng = {"act": nc.scalar, "dve": nc.vector, "pool": nc.gpsimd, "pe": nc.tensor, "sync": nc.sync}
        self.sem = {e: nc.alloc_semaphore("s_" + e) for e in ("act", "dve", "pool", "pe")}
        self.cnt = {e: 0 for e in ("act", "dve", "pool", "pe")}
        self.seen = {e: {} for e in self.eng}
        self.dsem = [nc.alloc_semaphore("s_dma%d" % i) for i in range(NDMA)]
        self.dcnt = [0] * NDMA
        self.dnext = 0
        self.nins = {e: 0 for e in self.eng}

    def _semof(self, src):
        if isinstance(src, tuple):
            return self.dsem[src[1]]
        return self.sem[src]

    def _deps(self, e, reads, writes, allraw=False):
        deps = {}

        def add(src, n, raw):
            if src == e and not allraw:
                if e == "pe" or not raw:
                    return
            if deps.get(src, 0) < n:
                deps[src] = n

        for t in reads:
            for src, n in t.w.items():
                add(src, n, True)
        for t in writes:
            for src, n in t.w.items():
                add(src, n, False)
            for src, n in t.r.items():
                add(src, n, False)
        return deps

    def _wait(self, e, deps):
        eng = self.eng[e]
        seen = self.seen[e]
        for src, n in deps.items():
            if seen.get(src, 0) >= n:
                continue
            seen[src] = n
            eng.wait_ge(self._semof(src), n)
            self.nins[e] += 1

    def op(self, e, fn, reads=(), writes=(), signal=True):
        self._wait(e, self._deps(e, reads, writes))
        ins = fn(self.eng[e])
        self.nins[e] += 1
        n = self.cnt[e] + 1
        if signal:
            ins.then_inc(self.sem[e], 1)
            self.cnt[e] = n
        for t in reads:
            if t.r.get(e, 0) < n:
                t.r[e] = n
        for t in writes:
            if t.w.get(e, 0) < n:
                t.w[e] = n
        return ins

    def dma(self, q, out, in_, reads=(), writes=()):
        k = self.dnext
        self.dnext = (k + 1) % NDMA
        src = ("dma", k)
        deps = self._deps(q, reads, writes, allraw=True)
        if self.dcnt[k] > 0:
            deps[src] = max(deps.get(src, 0), self.dcnt[k])
        self._wait(q, deps)
        ins = self.eng[q].dma_start(out=out, in_=in_)
        self.nins[q] += 1
        n = self.dcnt[k] + 16
        ins.then_inc(self.dsem[k], 16)
        self.dcnt[k] = n
        for t in reads:
            t.r[src] = n
        for t in writes:
            t.w[src] = n

    def barrier(self):
        for e in self.eng:
            deps = {}
            for s2 in self.cnt:
                if s2 != e and self.cnt[s2] > 0:
                    deps[s2] = self.cnt[s2]
            for k in range(NDMA):
                if self.dcnt[k] > 0:
                    deps[("dma", k)] = self.dcnt[k]
            self._wait(e, deps)

    def finish(self):
        deps = {}
        for k in range(NDMA):
            if self.dcnt[k] > 0:
                deps[("dma", k)] = self.dcnt[k]
        self._wait("sync", deps)


class Rot:
    def __init__(self, aps):
        self.items = [a if isinstance(a, tuple) else (a, Tk()) for a in aps]
        self.i = 0

    def next(self):
        it = self.items[self.i]
        self.i = (self.i + 1) % len(self.items)
        return it


class Ctx:
    def __init__(self, nc):
        self.nc = nc
        self.P = Prog(nc)
        self.banks = [nc.alloc_psum_tensor("psb%d" % i, [128, 512], F32).ap() for i in range(8)]
        self.nalloc = 0

        self.big = nc.alloc_sbuf_tensor("big", [128, SB_F32], F32).ap()
        self.top = 0

    def sb(self, stack, shape, dt, name=None):
        esz = 2 if dt == BF16 else 4
        n = int(np.prod(shape[1:]))
        nbytes = (n * esz + 63) // 64 * 64
        off = self.top
        assert off + nbytes <= SB_F32 * 4, ("SBUF overflow", name, off, nbytes)
        self.top = off + nbytes
        self.log = getattr(self, 'log', [])
        self.log.append((name, off, nbytes))
        ap = self.big[:, off // 4:(off + nbytes) // 4]
        if dt == BF16:
            ap = ap.bitcast(BF16)
        ap = ap[:, 0:n]
        if len(shape) == 3:
            ap = ap.rearrange("p (a b) -> p a b", a=shape[1])
        elif len(shape) == 4:
            ap = ap.rearrange("p (a b c) -> p a b c", a=shape[1], b=shape[2])
        return ap

    def mark(self):
        return self.top

    def release(self, m):
        self.top = m


def load_consts(cx, stack, cst_ap, ncols):
    P = cx.P
    cf = cx.sb(stack, [128, ncols], F32, "cstf")
    cb = cx.sb(stack, [128, C_END_BF], BF16, "cstb")
    tk = Tk()
    P.dma("sync", cf, cst_ap, writes=[tk])
    P.op("dve", lambda e: e.tensor_copy(out=cb, in_=cf[:, 0:C_END_BF]), reads=[tk], writes=[tk])
    return cf, cb, tk


class WStream:
    def __init__(self, cx, stack, nelem, nstage=2, nslot=2):
        self.cx = cx
        self.nelem = nelem
        self.stage = Rot([cx.sb(stack, [128, nelem], F32, "wstg") for _ in range(nstage)])
        self.slots = Rot([cx.sb(stack, [128, nelem], BF16, "wbf") for _ in range(nslot)])

    def load(self, views):
        P = self.cx.P
        stg, stk = self.stage.next()
        wb, wtk = self.slots.next()
        off = 0
        for v in views:
            shp = v.shape
            n = int(np.prod(shp[1:]))
            dst = stg[:, off:off + n]
            if len(shp) == 3:
                dst = dst.rearrange("p (a b) -> p a b", a=shp[1])
            P.dma("sync", dst, v, writes=[stk])
            off += n
        assert off <= self.nelem
        P.op("pool", lambda e: e.tensor_copy(out=wb[:, 0:off], in_=stg[:, 0:off]), reads=[stk], writes=[wtk])
        return wb, wtk


def rms_stats(cx, xs, n, sq_rot, ps_ap, ps_tk, rstd, rstd_tk, ones_bf, ctk, inv_dim):
    P = cx.P
    nx = len(xs)
    for c, (xa, xt) in enumerate(xs):
        sq, sqt = sq_rot.next()
        P.op("act", lambda e, xa=xa, sq=sq: e.activation(out=sq[:, 0:n], in_=xa, func=AF.Square), reads=[xt], writes=[sqt])
        P.op("pe", lambda e, sq=sq, c=c: e.matmul(ps_ap[:, 0:n], lhsT=ones_bf, rhs=sq[:, 0:n], start=(c == 0), stop=(c == nx - 1)),
             reads=[sqt, ctk], writes=[ps_tk])
    P.op("act", lambda e: e.activation(out=rstd[:, 0:n], in_=ps_ap[:, 0:n], func=AF.Sqrt, bias=cx.eps_col, scale=inv_dim),
         reads=[ps_tk, ctk], writes=[rstd_tk])
    P.op("dve", lambda e: e.reciprocal(out=rstd[:, 0:n], in_=rstd[:, 0:n]), reads=[rstd_tk], writes=[rstd_tk])


C_ID, C_ONES, C_BONES, C_DM, C_HONES = 0, 128, 256, 384, 640
C_OZ = 704
C_HZ = 960
C_END_BF = 1216
C_EPS = 1216
C_GAINS = 1217
G0_ANORM = C_GAINS
G0_QG = G0_ANORM + 8
G0_KG = G0_QG + 3
G0_MLPN = G0_KG + 3
G0_PLEN = G0_MLPN + 8
G0_END = G0_PLEN + 8


def base_consts(core, ncols):
    c = np.zeros((128, ncols), np.float32)
    c[:, C_ID:C_ID + 128] = np.eye(128, dtype=np.float32)
    c[:, C_ONES:C_ONES + 128] = 1.0
    c[0:64, C_BONES:C_BONES + 64] = 1.0
    c[64:128, C_BONES + 64:C_BONES + 128] = 1.0
    kk = np.arange(128)[:, None]
    a = np.arange(128)[None, :]
    diag = np.where(kk <= a, a - kk, BIG)
    prev = np.where(kk >= a, 128 + a - kk, BIG)
    c[:, C_DM:C_DM + 128] = diag
    c[:, C_DM + 128:C_DM + 256] = prev
    hv = 0.0 if core == 0 else 1.0
    c[:, C_HONES:C_HONES + 64] = hv
    c[:, C_OZ:C_OZ + 64] = 1.0
    c[:, C_OZ + 128 + 64:C_OZ + 256] = 1.0
    c[:, C_HZ:C_HZ + 64] = hv
    c[:, C_HZ + 128 + 64:C_HZ + 256] = hv
    c[:, C_EPS] = EPS
    return c


def col_layout(v):
    v = np.asarray(v, np.float32).reshape(-1, 128)
    return np.ascontiguousarray(v.T)


def emit_norm_resident(cx, X, Xtk, gcol, ctk, hT, hTtk, sq_rot, rstd_rot, ps_rot, ones_bf):
    P = cx.P
    for tg in range(NT // 512):
        sl = slice(tg * 512, (tg + 1) * 512)
        xs = [(X[:, c, sl], Xtk[c][tg]) for c in range(NC8)]
        ps_ap, ps_tk = ps_rot.next()
        rstd, rtk = rstd_rot.next()
        rms_stats(cx, xs, 512, sq_rot, ps_ap, ps_tk, rstd, rtk, ones_bf, ctk, 1.0 / D)
        for c in range(NC8):
            P.op("dve", lambda e, c=c, sl=sl, rstd=rstd: e.scalar_tensor_tensor(
                out=hT[:, c, sl], in0=X[:, c, sl], scalar=gcol[:, c:c + 1], in1=rstd[:, 0:512],
                op0=ALU.mult, op1=ALU.mult), reads=[Xtk[c][tg], rtk, ctk], writes=[hTtk[c][tg]])


def emit_mlp(cx, X, Xtk, gcol, ctk, w1, w2, ones_bf):
    P = cx.P
    mk = cx.mark()
    st = None
    hT = cx.sb(st, [128, NC8, NT], BF16, "mlp_hT")
    hTtk = [[Tk() for _ in range(4)] for _ in range(NC8)]
    sq_rot = Rot([cx.sb(st, [128, 512], BF16, "sq") for _ in range(4)])
    rstd_rot = Rot([cx.sb(st, [128, 512], F32, "rstd") for _ in range(2)])
    ps_stat = Rot([cx.banks[7]])
    emit_norm_resident(cx, X, Xtk, gcol, ctk, hT, hTtk, sq_rot, rstd_rot, ps_stat, ones_bf)
    ws = WStream(cx, st, 4096, nstage=2, nslot=2)
    hids = [cx.sb(st, [128, 4, NT], BF16, "hid") for _ in range(2)]
    hid_tks = [[[Tk() for _ in range(4)] for _ in range(4)] for _ in range(2)]
    tmp_rot = Rot([cx.sb(st, [128, 512], F32, "rl") for _ in range(3)])
    psA = Rot(cx.banks[0:4])
    psB = Rot(cx.banks[4:7])
    w1v = w1.rearrange("(c p) n -> p c n", p=128)
    w2v = w2.rearrange("(c p) n -> p c n", p=128)
    NHB = 8
    for hb in range(NHB):
        hid = hids[hb % 2]
        htk = hid_tks[hb % 2]
        wa, watk = ws.load([w1v[:, :, hb * 512:(hb + 1) * 512]])
        wa3 = wa.rearrange("p (c n) -> p c n", c=NC8)
        for hc in range(4):
            for tg in range(4):
                sl = slice(tg * 512, (tg + 1) * 512)
                ps, pstk = psA.next()
                for c in range(NC8):
                    P.op("pe", lambda e, c=c, hc=hc, sl=sl, ps=ps, wa3=wa3: e.matmul(
                        ps, lhsT=wa3[:, c, hc * 128:(hc + 1) * 128], rhs=hT[:, c, sl],
                        start=(c == 0), stop=(c == NC8 - 1)),
                        reads=[watk, hTtk[c][tg]], writes=[pstk], signal=(c == NC8 - 1))
                tmp, ttk = tmp_rot.next()
                P.op("act", lambda e, ps=ps, tmp=tmp: e.activation(out=tmp, in_=ps, func=AF.Square),
                     reads=[pstk], writes=[ttk])
                P.op("dve", lambda e, ps=ps, tmp=tmp, hc=hc, sl=sl, hid=hid: e.scalar_tensor_tensor(
                    out=hid[:, hc, sl], in0=ps, scalar=0.0, in1=tmp, op0=ALU.is_gt, op1=ALU.mult),
                    reads=[pstk, ttk], writes=[htk[hc][tg]])
        wb, wbtk = ws.load([w2v[:, hb * 4:(hb + 1) * 4, :]])
        wb3 = wb.rearrange("p (c n) -> p c n", c=4)
        for oc in range(NC8):
            for tg in range(4):
                sl = slice(tg * 512, (tg + 1) * 512)
                ps, pstk = psB.next()
                for hc in range(4):
                    P.op("pe", lambda e, hc=hc, oc=oc, sl=sl, ps=ps, hid=hid, wb3=wb3: e.matmul(
                        ps, lhsT=wb3[:, hc, oc * 128:(oc + 1) * 128], rhs=hid[:, hc, sl],
                        start=(hc == 0), stop=(hc == 3)),
                        reads=[wbtk, htk[hc][tg]], writes=[pstk], signal=(hc == 3))
                P.op("dve", lambda e, oc=oc, sl=sl, ps=ps: e.tensor_tensor(
                    out=X[:, oc, sl], in0=ps, in1=X[:, oc, sl], op=ALU.add),
                    reads=[pstk, Xtk[oc][tg]], writes=[Xtk[oc][tg]])
    P.barrier()
    cx.release(mk)


def emit_ple(cx, X, Xtk, gcol, ctk, wg, wp, pT_dram, ones_bf):
    P = cx.P
    mk = cx.mark()
    st = None
    hT = cx.sb(st, [128, NC8, NT], BF16, "ple_hT")
    hTtk = [[Tk() for _ in range(4)] for _ in range(NC8)]
    sq_rot = Rot([cx.sb(st, [128, 512], BF16, "sq") for _ in range(4)])
    rstd_rot = Rot([cx.sb(st, [128, 512], F32, "rstd") for _ in range(2)])
    ps_stat = Rot([cx.banks[7]])
    emit_norm_resident(cx, X, Xtk, gcol, ctk, hT, hTtk, sq_rot, rstd_rot, ps_stat, ones_bf)
    ws = WStream(cx, st, 4096, nstage=2, nslot=3)
    pst = cx.sb(st, [128, 2, NT], F32, "pstg")
    pb = cx.sb(st, [128, 2, NT], BF16, "pbf")
    ptk = Tk()
    P.dma("sync", pst, pT_dram.rearrange("(c p) n -> p c n", p=128), writes=[ptk])
    P.op("pool", lambda e: e.tensor_copy(out=pb, in_=pst), reads=[ptk], writes=[ptk])
    wpb, wptk = ws.load([wp.rearrange("(c p) n -> p c n", p=128)])
    wp3 = wpb[:, 0:2048].rearrange("p (c n) -> p c n", c=2)
    gt_rot = Rot([cx.sb(st, [128, 512], F32, "gt") for _ in range(3)])
    psA = Rot(cx.banks[0:3])
    psB = Rot(cx.banks[3:6])
    wgv = wg.rearrange("(c p) n -> p c n", p=128)
    for half in range(2):
        wa, watk = ws.load([wgv[:, :, half * 512:(half + 1) * 512]])
        wa3 = wa.rearrange("p (c n) -> p c n", c=NC8)
        for o4 in range(4):
            oc = half * 4 + o4
            for tg in range(4):
                sl = slice(tg * 512, (tg + 1) * 512)
                ps, pstk = psA.next()
                for c in range(NC8):
                    P.op("pe", lambda e, c=c, o4=o4, sl=sl, ps=ps, wa3=wa3: e.matmul(
                        ps, lhsT=wa3[:, c, o4 * 128:(o4 + 1) * 128], rhs=hT[:, c, sl],
                        start=(c == 0), stop=(c == NC8 - 1)),
                        reads=[watk, hTtk[c][tg]], writes=[pstk], signal=(c == NC8 - 1))
                ps2, ps2tk = psB.next()
                for kc in range(2):
                    P.op("pe", lambda e, kc=kc, oc=oc, sl=sl, ps2=ps2: e.matmul(
                        ps2, lhsT=wp3[:, kc, oc * 128:(oc + 1) * 128], rhs=pb[:, kc, sl],
                        start=(kc == 0), stop=(kc == 1)),
                        reads=[wptk, ptk], writes=[ps2tk], signal=(kc == 1))
                gt, gtk = gt_rot.next()
                P.op("act", lambda e, ps=ps, gt=gt: e.activation(out=gt, in_=ps, func=AF.Sigmoid),
                     reads=[pstk], writes=[gtk])
                P.op("dve", lambda e, ps2=ps2, gt=gt: e.tensor_tensor(out=gt, in0=ps2, in1=gt, op=ALU.mult),
                     reads=[ps2tk, gtk], writes=[gtk])
                P.op("dve", lambda e, oc=oc, sl=sl, gt=gt: e.tensor_tensor(
                    out=X[:, oc, sl], in0=gt, in1=X[:, oc, sl], op=ALU.add),
                    reads=[gtk, Xtk[oc][tg]], writes=[Xtk[oc][tg]])
    P.barrier()
    cx.release(mk)


def alibi_slope(h):
    return 2.0 ** (-8.0 * (h + 1) / 16)


def sslice(start, count, step):
    return slice(start, start + (count - 1) * step + 1, step)


def emit_attention(cx, xT_ext, wqkv, wo, cf, cb, ctk, TOP, R1, lvl=9, hps=8):
    P = cx.P
    st = None
    ones_bf = cb[:, C_ONES:C_ONES + 128]
    bones = cb[:, C_BONES:C_BONES + 128]
    Dm = cf[:, C_DM:C_DM + 256]
    hT = TOP.bitcast(BF16).rearrange("p (c n) -> p c n", c=NC8)
    mk = cx.mark()
    sq_rot = Rot([cx.sb(st, [128, 512], BF16, "sq") for _ in range(2)])
    rstd_rot = Rot([cx.sb(st, [128, 512], F32, "rstd") for _ in range(2)])
    xstg = [R1[:, i * 4096:(i + 1) * 4096].rearrange("p (c n) -> p c n", c=NC8) for i in range(2)]
    xstk = [Tk(), Tk()]
    ps_stat = Rot([cx.banks[7], cx.banks[6]])
    httk = Tk()
    gcol = cf[:, G0_ANORM:G0_ANORM + 8]
    for tg in range(8 if lvl >= 1 else 0):
        xa = xstg[tg % 2]
        xt = xstk[tg % 2]
        P.dma("sync", xa, xT_ext[:, tg * 512:(tg + 1) * 512].rearrange("(c p) n -> p c n", p=128), writes=[xt])
        xs = [(xa[:, c, :], xt) for c in range(NC8)]
        ps_ap, ps_tk = ps_stat.next()
        rstd, rtk = rstd_rot.next()
        rms_stats(cx, xs, 512, sq_rot, ps_ap, ps_tk, rstd, rtk, ones_bf, ctk, 1.0 / D)
        for c in range(NC8):
            P.op("dve", lambda e, c=c, tg=tg, xa=xa, rstd=rstd: e.scalar_tensor_tensor(
                out=hT[:, c, tg * 512:(tg + 1) * 512], in0=xa[:, c, :], scalar=gcol[:, c:c + 1], in1=rstd[:, 0:512],
                op0=ALU.mult, op1=ALU.mult), reads=[xt, rtk, ctk], writes=[httk])
    P.op("dve", lambda e: e.tensor_scalar(out=cf[:, G0_QG:G0_QG + 3], in0=cf[:, G0_QG:G0_QG + 3], scalar1=0.125,
                                          scalar2=None, op0=ALU.mult), reads=[ctk], writes=[ctk])
    P.barrier()
    ACC = R1[:, 0:4096].rearrange("p (a n) -> p a n", a=2)
    oT = R1[:, 4096:12288].bitcast(BF16).rearrange("p (c n) -> p c n", c=NC8)
    acctk = Tk()
    ottk = Tk()
    ws = WStream(cx, st, 3072, nstage=1, nslot=2)
    QTz = [cx.sb(st, [128, NT], BF16, "QTz") for _ in range(2)]
    qtk = Tk()
    KT_rot = Rot([cx.sb(st, [128, 2 * NT], BF16, "KT") for _ in range(2)])
    Vz = [cx.sb(st, [128, 32, 128], BF16, "Vz") for _ in range(2)]
    vtk = Tk()
    for e2 in (0, 1):
        P.op("pool", lambda e, e2=e2: e.memset(QTz[e2], 0.0), writes=[qtk])
        P.op("pool", lambda e, e2=e2: e.memset(Vz[e2], 0.0), writes=[vtk])
    onesz = [cb[:, C_OZ:C_OZ + 128], cb[:, C_OZ + 128:C_OZ + 256]]
    honesz = [cb[:, C_HZ:C_HZ + 128], cb[:, C_HZ + 128:C_HZ + 256]]
    tmp_rot = Rot([cx.sb(st, [128, 256], F32, "stmp") for _ in range(4)])
    pt_rots = [Rot([cx.sb(st, [128, 256], BF16, "PT") for _ in range(6)]) for _ in range(2)]
    bk = [(cx.banks[i], Tk()) for i in range(8)]

    def half(i):
        return (bk[i][0][:, 0:256], bk[i][1])

    psQ = Rot([bk[0], bk[1], bk[4], bk[5]])
    psS = Rot([bk[2]])
    psV = Rot([bk[3], bk[6], bk[7]])
    psST0 = Rot([half(4), half(0)])
    psST1 = Rot([half(5), half(1)])
    psND = Rot([half(6), half(7), half(2), half(3)])
    wq_view = wqkv.rearrange("(c p) n -> p c n", p=128)

    def perm(ap2d, d):
        if d == 1:
            return ap2d
        return ap2d.rearrange("p (u r) -> p r u", r=d)

    def proj_piece(w3, wtk, j, e0, n, gain_col, out_buf, out_tk, d, Lx, u0):
        ps, pstk = psQ.next()
        for c in range(NC8):
            P.op("pe", lambda e, c=c, ps=ps: e.matmul(ps[:, 0:n], lhsT=w3[:, j, c, :], rhs=hT[:, c, e0:e0 + n],
                                                      start=(c == 0), stop=(c == NC8 - 1)),
                 reads=[wtk], writes=[pstk], signal=(c == NC8 - 1))
        ps2, ps2tk = psS.next()
        rstd, rtk = rstd_rot.next()
        rms_stats(cx, [(ps[:, 0:n], pstk)], n, sq_rot, ps2, ps2tk, rstd, rtk, bones, ctk, 1.0 / 64)
        outs = out_buf if isinstance(out_buf, list) else [(slice(0, 128), out_buf)]
        for (rows, ob) in outs:
            if d == 1:
                o = ob[rows, u0:u0 + n]
            else:
                o = ob[rows, 0:d * Lx].rearrange("p (r u) -> p r u", r=d)[:, :, u0:u0 + n // d]
            P.op("dve", lambda e, ps=ps, rstd=rstd, o=o, rows=rows: e.scalar_tensor_tensor(
                out=o, in0=perm(ps[rows, 0:n], d), scalar=gain_col[rows, :], in1=perm(rstd[rows, 0:n], d),
                op0=ALU.mult, op1=ALU.mult),
                reads=[pstk, rtk, ctk], writes=[out_tk])

    if lvl < 2:
        hps = 0
        P.op('dve', lambda e: e.memset(R1, 0.0), writes=[ottk])
    for hp in range(hps):
        for g, (W, d) in enumerate(A_GROUPS):
            L = NT // d
            Lk = (W + NT) // d
            nb = Lk // 128
            e_start = NT - W
            base = g * 3072 + hp * 128
            wb, wtk = ws.load([wq_view[:, :, base + j * 1024: base + j * 1024 + 128] for j in range(3)])
            w3 = wb[:, 0:3072].rearrange("p (j c n) -> p j c n", j=3, c=NC8)
            KT, ktk = KT_rot.next()
            for tg in range(4):
                proj_piece(w3, wtk, 0, NT + tg * 512, 512, cf[:, G0_QG + g:G0_QG + g + 1],
                           [(slice(0, 64), QTz[0]), (slice(64, 128), QTz[1])], qtk, d, L, tg * 512 // d)
            pieces = []
            if W < 512:
                pieces.append((e_start, W))
                e = NT
            else:
                e = e_start
            while e < 2 * NT:
                pieces.append((e, 512))
                e += 512
            for (e0, n) in pieces:
                proj_piece(w3, wtk, 1, e0, n, cf[:, G0_KG + g:G0_KG + g + 1], KT, ktk, d, Lk, (e0 - e_start) // d)
            nkb = d * nb if lvl >= 3 else 0
            kb = 0
            while kb < nkb:
                nblk = min(4, nkb - kb)
                psv, psvtk = psV.next()
                for b in range(nblk):
                    r, jb = divmod(kb + b, nb)
                    e_first = e_start + d * 128 * jb + r
                    for c in range(NC8):
                        P.op("pe", lambda e, c=c, b=b, e_first=e_first, psv=psv: e.matmul(
                            psv[:, b * 128:(b + 1) * 128], lhsT=hT[:, c, sslice(e_first, 128, d)], rhs=w3[:, 2, c, :],
                            start=(c == 0), stop=(c == NC8 - 1)),
                            reads=[wtk], writes=[psvtk], signal=(c == NC8 - 1 and b == nblk - 1))
                for e2 in (0, 1):
                    cs = slice(64 * e2, 64 * e2 + 64)
                    P.op("act", lambda e, kb=kb, nblk=nblk, psv=psv, e2=e2, cs=cs: e.activation(
                        out=Vz[e2][:, kb:kb + nblk, cs],
                        in_=psv[:, 0:nblk * 128].rearrange("p (b n) -> p b n", b=nblk)[:, :, cs], func=AF.Copy),
                        reads=[psvtk], writes=[vtk])
                kb += nblk
            PTs = {}

            def score_task(r, jb):
                lo = 128 if jb == 0 else 0
                hi = 128 if jb == nb - 1 else 256
                qb0 = jb if jb == 0 else jb - 1
                q_off = r * L + 128 * qb0
                sTs = [psST0.next(), psST1.next()]
                for e2 in (0, 1):
                    sT, sTtk = sTs[e2]
                    P.op("pe", lambda e, sT=sT, e2=e2: e.matmul(
                        sT[:, lo:hi], lhsT=KT[:, r * Lk + 128 * jb: r * Lk + 128 * jb + 128],
                        rhs=QTz[e2][:, q_off:q_off + (hi - lo)], start=True, stop=True),
                        reads=[ktk, qtk], writes=[sTtk])
                for e2 in (0, 1):
                    sig = alibi_slope(2 * hp + e2) * d
                    sT, sTtk = sTs[e2]
                    tmp, tmtk = tmp_rot.next()
                    P.op("dve", lambda e, sT=sT, tmp=tmp, sig=sig: e.scalar_tensor_tensor(
                        out=tmp[:, lo:hi], in0=Dm[:, lo:hi], scalar=-sig, in1=sT[:, lo:hi],
                        op0=ALU.mult, op1=ALU.add),
                        reads=[sTtk, ctk], writes=[tmtk])
                    pt, pttk = pt_rots[e2].next()
                    P.op("act", lambda e, tmp=tmp, pt=pt: e.activation(
                        out=pt[:, lo:hi], in_=tmp[:, lo:hi], func=AF.Exp), reads=[tmtk], writes=[pttk])
                    PTs[(e2, r, jb)] = (pt, pttk)

            def pv_task(r, j):
                jb = j + 1
                nd, ndtk = psND.next()
                kbp = r * nb + j
                kbd = r * nb + jb
                for part in (0, 1):
                    co = slice(128 * part, 128 * part + 128)
                    for e2 in (0, 1):
                        ptp, ptptk = PTs[(e2, r, j)]
                        ptd, ptdtk = PTs[(e2, r, jb)]
                        if part == 0:
                            lp, ld = Vz[e2][:, kbp, :], Vz[e2][:, kbd, :]
                        else:
                            lp, ld = (honesz[e2] if j == 0 else onesz[e2]), onesz[e2]
                        P.op("pe", lambda e, nd=nd, co=co, lp=lp, ptp=ptp, e2=e2: e.matmul(
                            nd[:, co], lhsT=lp, rhs=ptp[:, 128:256], start=(e2 == 0), stop=False),
                            reads=[vtk, ctk, ptptk], writes=[ndtk], signal=False)
                        P.op("pe", lambda e, nd=nd, co=co, ld=ld, ptd=ptd, e2=e2: e.matmul(
                            nd[:, co], lhsT=ld, rhs=ptd[:, 0:128], start=False, stop=(e2 == 1)),
                            reads=[vtk, ctk, ptdtk], writes=[ndtk], signal=(part == 1 and e2 == 1))
                t0 = r + d * 128 * j
                accv = ACC[:, :, sslice(t0, 128, d)]
                ndv = nd.rearrange("p (a n) -> p a n", a=2)
                if g == 0:
                    P.op("act", lambda e, accv=accv, ndv=ndv: e.activation(out=accv, in_=ndv, func=AF.Copy),
                         reads=[ndtk], writes=[acctk])
                else:
                    P.op("dve", lambda e, accv=accv, ndv=ndv: e.tensor_tensor(out=accv, in0=ndv, in1=accv, op=ALU.add),
                         reads=[ndtk, acctk], writes=[acctk])

            LA = 3
            pending = []
            tasks = [(r, jb) for r in range(d if lvl >= 4 else 0) for jb in range(nb)]
            for i, (r, jb) in enumerate(tasks):
                score_task(r, jb)
                if jb >= 1:
                    pending.append((i, r, jb - 1))
                while pending and pending[0][0] <= i - LA:
                    _, r_, j_ = pending.pop(0)
                    pv_task(r_, j_)
            for (_, r_, j_) in pending:
                pv_task(r_, j_)
        P.op("dve", lambda e: e.reciprocal(out=ACC[:, 1, :], in_=ACC[:, 1, :]), reads=[acctk], writes=[acctk])
        P.op("dve", lambda e, hp=hp: e.tensor_tensor(out=oT[:, hp, :], in0=ACC[:, 0, :], in1=ACC[:, 1, :], op=ALU.mult),
             reads=[acctk], writes=[ottk])
    P.barrier()
    cx.release(mk)
    mk = cx.mark()
    X = TOP.rearrange("p (c n) -> p c n", c=NC8)
    Xtk = [[Tk() for _ in range(4)] for _ in range(NC8)]
    for c in range(NC8):
        for tg in range(4):
            P.dma("sync", X[:, c, tg * 512:(tg + 1) * 512], xT_ext[c * 128:(c + 1) * 128, NT + tg * 512:NT + (tg + 1) * 512],
                  writes=[Xtk[c][tg]])
    ws2 = WStream(cx, st, 4096, nstage=2, nslot=2)
    wov = wo.rearrange("(c p) n -> p c n", p=128)
    psA = Rot(cx.banks[0:4])
    for half in range(2):
        wa, watk = ws2.load([wov[:, :, half * 512:(half + 1) * 512]])
        wa3 = wa.rearrange("p (c n) -> p c n", c=NC8)
        for o4 in range(4):
            oc = half * 4 + o4
            for tg in range(4):
                sl = slice(tg * 512, (tg + 1) * 512)
                ps, pstk = psA.next()
                for c in range(NC8):
                    P.op("pe", lambda e, c=c, o4=o4, sl=sl, ps=ps, wa3=wa3: e.matmul(
                        ps, lhsT=wa3[:, c, o4 * 128:(o4 + 1) * 128], rhs=oT[:, c, sl],
                        start=(c == 0), stop=(c == NC8 - 1)),
                        reads=[watk, ottk], writes=[pstk], signal=(c == NC8 - 1))
                P.op("dve", lambda e, oc=oc, sl=sl, ps=ps: e.tensor_tensor(
                    out=X[:, oc, sl], in0=ps, in1=X[:, oc, sl], op=ALU.add),
                    reads=[pstk, Xtk[oc][tg]], writes=[Xtk[oc][tg]])
    P.barrier()
    cx.release(mk)
    return X, Xtk


def emit_store(cx, X, Xtk, out_dram):
    P = cx.P
    for c in range(NC8):
        P.dma("sync", out_dram[c * 128:(c + 1) * 128, :], X[:, c, :], reads=Xtk[c])


def build_layer0(debug=False, lvl=9, hps=8):
    nc = bass.Bass("TRN2", target_bir_lowering=False)
    xT_ext = nc.dram_tensor("xT_ext", [D, 2 * NT], F32, kind="ExternalInput").ap()
    pT = nc.dram_tensor("pT", [256, NT], F32, kind="ExternalInput").ap()
    cst = nc.dram_tensor("cst", [128, G0_END], F32, kind="ExternalInput").ap()
    wqkv = nc.dram_tensor("a_w_qkv", [D, 9216], F32, kind="ExternalInput").ap()
    wo = nc.dram_tensor("a_w_o", [D, D], F32, kind="ExternalInput").ap()
    w1 = nc.dram_tensor("mlp_w1", [D, 4096], F32, kind="ExternalInput").ap()
    w2 = nc.dram_tensor("mlp_w2", [4096, D], F32, kind="ExternalInput").ap()
    wg = nc.dram_tensor("ple_w_gate", [D, D], F32, kind="ExternalInput").ap()
    wp = nc.dram_tensor("ple_w_proj", [256, D], F32, kind="ExternalInput").ap()
    out = nc.dram_tensor("xout", [D, NT], F32, kind="ExternalOutput").ap()
    if debug:
        dbg_a = nc.dram_tensor("dbg_a", [D, NT], F32, kind="ExternalOutput").ap()
        dbg_m = nc.dram_tensor("dbg_m", [D, NT], F32, kind="ExternalOutput").ap()
    cx = Ctx(nc)
    P = cx.P
    cf, cb, ctk = load_consts(cx, None, cst, G0_END)
    cx.eps_col = cf[:, C_EPS:C_EPS + 1]
    ones_bf = cb[:, C_ONES:C_ONES + 128]
    TOP = cx.sb(None, [128, 16384], F32, "TOP")
    R1 = cx.sb(None, [128, 12288], F32, "R1")
    mk = cx.mark()
    X, Xtk = emit_attention(cx, xT_ext, wqkv, wo, cf, cb, ctk, TOP, R1, lvl=lvl, hps=hps)
    cx.release(mk)
    cx.top = cx.top - 12288 * 4
    if debug:
        emit_store(cx, X, Xtk, dbg_a)
    emit_mlp(cx, X, Xtk, cf[:, G0_MLPN:G0_MLPN + 8], ctk, w1, w2, ones_bf)
    if debug:
        emit_store(cx, X, Xtk, dbg_m)
    emit_ple(cx, X, Xtk, cf[:, G0_PLEN:G0_PLEN + 8], ctk, wg, wp, pT, ones_bf)
    emit_store(cx, X, Xtk, out)
    P.finish()
    return nc, cx


def layer0_inputs(inputs, core):
    x = inputs["x"][0]
    lo = core * NT
    xe = np.zeros((2 * NT, D), np.float32)
    if core > 0:
        xe[:NT] = x[lo - NT:lo]
    xe[NT:] = x[lo:lo + NT]
    c = base_consts(core, G0_END)
    c[:, G0_ANORM:G0_ANORM + 8] = col_layout(inputs["a_norm"][0])
    c[:, G0_QG:G0_QG + 3] = np.tile(inputs["a_q_gain"][0].T, (2, 1))
    c[:, G0_KG:G0_KG + 3] = np.tile(inputs["a_k_gain"][0].T, (2, 1))
    c[:, G0_MLPN:G0_MLPN + 8] = col_layout(inputs["mlp_norm"][0])
    c[:, G0_PLEN:G0_PLEN + 8] = col_layout(inputs["ple_norm"][0])
    return {
        "xT_ext": np.ascontiguousarray(xe.T),
        "pT": np.ascontiguousarray(inputs["p"][0, 0, lo:lo + NT].T),
        "cst": c,
        "a_w_qkv": inputs["a_w_qkv"][0], "a_w_o": inputs["a_w_o"][0],
        "mlp_w1": inputs["mlp_w1"][0], "mlp_w2": inputs["mlp_w2"][0],
        "ple_w_gate": inputs["ple_w_gate"][0], "ple_w_proj": inputs["ple_w_proj"][0],
    }

BH = 4
DH = 512
NCK = NT // 128
KSCALE = DH ** -0.5
NST = 8192 + 2048 + 8

L_BNORM = C_GAINS
L_MLPN = L_BNORM + 8
L_PLEN = L_MLPN + 8
L_CONVW = L_PLEN + 8
L_CONVB = L_CONVW + 64
L_SKIP = L_CONVB + 16
L_HGAIN = L_SKIP + 16
L_BI = L_HGAIN + 16
L_BF = L_BI + 1
L_MASKLOW = L_BF + 1
L_SEL = L_MASKLOW + 128
L_CMASK = L_SEL + 512
L_NEG = L_CMASK + 7
L_E0 = L_NEG + 1
L_LNK = L_E0 + 1
L_CNEG = L_LNK + 1
L_END = L_CNEG + 7


def layer1_consts(inputs, core):
    c = base_consts(core, L_END)
    c[:, L_BNORM:L_BNORM + 8] = col_layout(inputs["b_norm"][0])
    c[:, L_MLPN:L_MLPN + 8] = col_layout(inputs["mlp_norm"][1])
    c[:, L_PLEN:L_PLEN + 8] = col_layout(inputs["ple_norm"][1])
    cw = inputs["b_conv_w"][0]
    c[:, L_CONVW:L_CONVW + 64] = cw.reshape(4, 16, 128).transpose(2, 1, 0).reshape(128, 64)
    c[:, L_CONVB:L_CONVB + 16] = col_layout(inputs["b_conv_b"][0])
    c[:, L_SKIP:L_SKIP + 16] = col_layout(inputs["b_skip"][0])
    c[:, L_HGAIN:L_HGAIN + 16] = col_layout(inputs["b_h_gain"][0])
    bg = inputs["b_b_gate"][0]
    c[0:4, L_BI] = bg[0:4]
    c[0:4, L_BF] = bg[4:8]
    s_ = np.arange(128)[:, None]
    t_ = np.arange(128)[None, :]
    c[:, L_MASKLOW:L_MASKLOW + 128] = np.where(s_ <= t_, 0.0, BIG)
    for hd in range(4):
        c[hd, L_SEL + hd * 128:L_SEL + (hd + 1) * 128] = 1.0
    for cp in range(7):
        c[:, L_CMASK + cp] = 1.0 if cp < core else 0.0
        c[:, L_CNEG + cp] = 0.0 if cp < core else -1e30
    c[:, L_NEG] = -1e30
    c[0, L_E0] = 1.0
    c[:, L_LNK] = np.log(KSCALE)
    return c


def bd_compact(w, transpose=False):
    out = np.zeros((2048, 128), np.float32)
    n = np.arange(512)
    for j in range(4):
        for k in range(4):
            if transpose:
                out[4 * n + k, (4 * n + j) % 128] = w[:, j, k]
            else:
                out[4 * n + j, (4 * n + k) % 128] = w[:, j, k]
    return out


def layer1_inputs(inputs, core, x1T_full, stage, st_all=None, g_in=None):
    lo = core * NT
    xh = np.zeros((D, 4), np.float32)
    if core > 0:
        xh[:, 1:4] = x1T_full[:, lo - 3:lo]
    m = {
        "x1T": np.ascontiguousarray(x1T_full[:, lo:lo + NT]),
        "xh": xh,
        "cst": layer1_consts(inputs, core),
        "b_w_up": inputs["b_w_up"][0],
        "bd": np.stack([bd_compact(inputs["b_w_q"][0]), bd_compact(inputs["b_w_k"][0]), bd_compact(inputs["b_w_v"][0])]),
        "bdT": np.stack([bd_compact(inputs["b_w_q"][0], True), bd_compact(inputs["b_w_k"][0], True),
                         bd_compact(inputs["b_w_v"][0], True)]),
        "w_gate": inputs["b_w_gate"][0],
    }
    if stage == "C":
        m.update({
            "st_all": st_all,
            "g_in": g_in,
            "b_w_down": inputs["b_w_down"][0],
            "pT": np.ascontiguousarray(inputs["p"][1, 0, lo:lo + NT].T),
            "mlp_w1": inputs["mlp_w1"][1], "mlp_w2": inputs["mlp_w2"][1],
            "ple_w_gate": inputs["ple_w_gate"][1], "ple_w_proj": inputs["ple_w_proj"][1],
        })
    return m


def build_layer1(stage, debug=False, dbg_stop=None):
    nc = bass.Bass("TRN2", target_bir_lowering=False)
    x1T = nc.dram_tensor("x1T", [D, NT], F32, kind="ExternalInput").ap()
    xh = nc.dram_tensor("xh", [D, 4], F32, kind="ExternalInput").ap()
    cst = nc.dram_tensor("cst", [128, L_END], F32, kind="ExternalInput").ap()
    wup = nc.dram_tensor("b_w_up", [D, 4096], F32, kind="ExternalInput").ap()
    bd = nc.dram_tensor("bd", [3, 2048, 128], F32, kind="ExternalInput").ap()
    bdT = nc.dram_tensor("bdT", [3, 2048, 128], F32, kind="ExternalInput").ap()
    wgate = nc.dram_tensor("w_gate", [6144, 8], F32, kind="ExternalInput").ap()
    if stage == "B":
        st_out = nc.dram_tensor("st_out", [128, NST], F32, kind="ExternalOutput").ap()
        g_out = nc.dram_tensor("g_out", [8, NT], F32, kind="ExternalOutput").ap()
    else:
        st_all = nc.dram_tensor("st_all", [7, 128, NST], F32, kind="ExternalInput").ap()
        g_in = nc.dram_tensor("g_in", [8, NT], F32, kind="ExternalInput").ap()
        wdown = nc.dram_tensor("b_w_down", [2048, D], F32, kind="ExternalInput").ap()
        pT = nc.dram_tensor("pT", [256, NT], F32, kind="ExternalInput").ap()
        w1 = nc.dram_tensor("mlp_w1", [D, 4096], F32, kind="ExternalInput").ap()
        w2 = nc.dram_tensor("mlp_w2", [4096, D], F32, kind="ExternalInput").ap()
        wg = nc.dram_tensor("ple_w_gate", [D, D], F32, kind="ExternalInput").ap()
        wp = nc.dram_tensor("ple_w_proj", [256, D], F32, kind="ExternalInput").ap()
        out = nc.dram_tensor("xout", [D, NT], F32, kind="ExternalOutput").ap()
        yscr = nc.dram_tensor("yscr", [2048, NT], BF16).ap()
        if debug:
            dbg_a = nc.dram_tensor("dbg_a", [D, NT], F32, kind="ExternalOutput").ap()
    cx = Ctx(nc)
    P = cx.P
    cf, cb, ctk = load_consts(cx, None, cst, L_END)
    cx.eps_col = cf[:, C_EPS:C_EPS + 1]
    ones_bf = cb[:, C_ONES:C_ONES + 128]
    ones_f = cf[:, C_ONES:C_ONES + 128]
    ident_f = cf[:, C_ID:C_ID + 128]
    one_col = cf[:, C_ONES:C_ONES + 1]
    TOP = cx.sb(None, [128, 16384], F32, "TOP")
    hT = TOP[:, 0:8208].bitcast(BF16)[:, 0:8 * 2052].rearrange("p (c n) -> p c n", c=NC8)
    topfree = TOP[:, 8208:16384]
    base_mark = cx.mark()
    bk = [(cx.banks[i], Tk()) for i in range(8)]
    ws = WStream(cx, None, 4096, nstage=0, nslot=3)
    ws.stage = Rot([topfree[:, 0:4096]])
    bdh = cx.sb(None, [128, 3, 4, 128], BF16, "bdh")
    bdh_st = cx.sb(None, [128, 3, 4, 128], F32, "bdh_st")
    diag = cx.sb(None, [128, 4, 4, 128], BF16, "diag")
    xms = [cx.sb(None, [128, 4, 516], BF16, "xm") for _ in range(2)]
    xc = cx.sb(None, [128, 4, 512], BF16, "xc")
    GF = cx.sb(None, [128, NT], F32, "GF")
    BETAx = cx.sb(None, [128, NT + 1], F32, "BETAx")
    small = cx.sb(None, [128, 64], F32, "small")
    TMw = cx.sb(None, [128, NCK, 4], F32, "TMw")
    TMa = cx.sb(None, [128, NCK, 4], F32, "TMa")
    if stage == "C":
        fold = cx.sb(None, [128, 7, 8], F32, "foldin")
        S1 = cx.sb(None, [128, 7, 4], F32, "S1")
        S2 = cx.sb(None, [128, 7, 4], F32, "S2")
        mrun = cx.sb(None, [128, 4], F32, "mrun")
        fa = cx.sb(None, [128, 4], F32, "fa")
        fb = cx.sb(None, [128, 4], F32, "fb")
        fc_ = cx.sb(None, [128, 4], F32, "fc")
    pers_mark = cx.mark()

    mk = cx.mark()
    sq_rot = Rot([cx.sb(None, [128, 512], BF16, "sq") for _ in range(2)])
    rstd_rot = Rot([cx.sb(None, [128, 512], F32, "rstd") for _ in range(2)])
    xstg = [cx.sb(None, [128, NC8, 512], F32, "xstg") for _ in range(2)]
    xstk = [Tk(), Tk()]
    ps_stat = Rot([bk[7], bk[6]])
    gcol = cf[:, L_BNORM:L_BNORM + 8]
    httk = Tk()
    pieces = [(None, 4)] + [(tg, 512) for tg in range(4)]
    for i, (tg, n) in enumerate(pieces):
        xa = xstg[i % 2]
        xt = xstk[i % 2]
        if tg is None:
            P.dma("sync", xa[:, :, 0:4], xh.rearrange("(c p) n -> p c n", p=128), writes=[xt])
            h0 = 0
        else:
            P.dma("sync", xa, x1T[:, tg * 512:(tg + 1) * 512].rearrange("(c p) n -> p c n", p=128), writes=[xt])
            h0 = 4 + tg * 512
        xs = [(xa[:, c, 0:n], xt) for c in range(NC8)]
        ps_ap, ps_tk = ps_stat.next()
        rstd, rtk = rstd_rot.next()
        rms_stats(cx, xs, n, sq_rot, ps_ap, ps_tk, rstd, rtk, ones_bf, ctk, 1.0 / D)
        for c in range(NC8):
            P.op("dve", lambda e, c=c, xa=xa, rstd=rstd, n=n, h0=h0: e.scalar_tensor_tensor(
                out=hT[:, c, h0:h0 + n], in0=xa[:, c, 0:n], scalar=gcol[:, c:c + 1], in1=rstd[:, 0:n],
                op0=ALU.mult, op1=ALU.mult), reads=[xt, rtk, ctk], writes=[httk])
    P.barrier()
    cx.release(mk)

    GI = cx.sb(None, [128, NT], F32, "GI")
    LF = cx.sb(None, [128, NT], F32, "LF")
    BB = cx.sb(None, [128, NT], F32, "BB")
    T1 = cx.sb(None, [128, NT], F32, "T1")
    wfold = [[cx.sb(None, [128, 16, 128], BF16, "wfold") for _ in range(2)] for _ in range(2)]
    wftk = Tk()
    bdT_sb = topfree[:, 0:6144].rearrange("p (a b) -> p a b", a=48)
    wg_sb = topfree[:, 6144:6528].rearrange("p (a b) -> p a b", a=48)
    btk = Tk()
    for j in range(3 if stage == "B" else 0):
        P.dma("sync", bdT_sb[:, j * 16:(j + 1) * 16, :], bdT[j].rearrange("(c p) n -> p c n", p=128), writes=[btk])
    if stage == "B":
        P.dma("sync", wg_sb, wgate.rearrange("(c p) n -> p c n", p=128), writes=[btk])
    zpad = Rot([topfree[:, 6528 + i * 128:6528 + (i + 1) * 128] for i in range(4)])
    for (za, ztk) in zpad.items:
        P.op("pool", lambda e, za=za: e.memset(za, 0.0), writes=[ztk])
    psF = Rot(bk[0:2])
    for mc in range(16 if stage == "B" else 0):
        for part in range(2):
            for xm_ in range(2):
                ps, pstk = psF.next()
                srcs = (0, 1) if xm_ == 0 else (2,)
                for si, j in enumerate(srcs):
                    za, ztk = zpad.next()
                    P.op("dve", lambda e, za=za, j=j, mc=mc, part=part: e.tensor_copy(
                        out=za[:, 0:4], in_=wg_sb[:, j * 16 + mc, part * 4:part * 4 + 4]), reads=[btk], writes=[ztk])
                    P.op("pe", lambda e, ps=ps, za=za, j=j, mc=mc, si=si, srcs=srcs: e.matmul(
                        ps[:, 0:128], lhsT=bdT_sb[:, j * 16 + mc, :], rhs=za, start=(si == 0), stop=(si == len(srcs) - 1)),
                        reads=[btk, ztk], writes=[pstk])
                P.op("act", lambda e, ps=ps, xm_=xm_, part=part, mc=mc: e.activation(
                    out=wfold[xm_][part][:, mc, :], in_=ps[:, 0:128], func=AF.Copy), reads=[pstk], writes=[wftk])
    P.barrier()

    hdtk = Tk()
    xmtk = [Tk(), Tk()]
    xctk = Tk()
    psA = Rot(bk[0:2])
    wupv = wup.rearrange("(c p) n -> p c n", p=128)
    bdv = bd.rearrange("j (c p) n -> p j c n", p=128)
    state = {"i": 0}

    def head_setup(hd):
        for j in range(3):
            P.dma("sync", bdh_st[:, j, :, :], bdv[:, j, hd * 4:(hd + 1) * 4, :], writes=[hdtk])
        P.op("pool", lambda e: e.tensor_copy(out=bdh, in_=bdh_st), reads=[hdtk], writes=[hdtk])
        for mc in range(4):
            for k in range(4):
                col = L_CONVW + (hd * 4 + mc) * 4 + k
                P.op("act", lambda e, mc=mc, k=k, col=col: e.activation(
                    out=diag[:, mc, k, :], in_=ident_f, func=AF.Copy, scale=cf[:, col:col + 1]),
                    reads=[ctk], writes=[hdtk])
        wx, wxtk = ws.load([wupv[:, :, hd * 512:(hd + 1) * 512]])
        return wx.rearrange("p (c n) -> p c n", c=NC8), wxtk

    def front(hd, tg, wx3, wxtk):
        i = state["i"]
        state["i"] += 1
        xm, xmt = xms[i % 2], xmtk[i % 2]
        xmp, xmpt = xms[(i + 1) % 2], xmtk[(i + 1) % 2]
        for mc in range(4):
            ps, pstk = psA.next()
            for c in range(NC8):
                P.op("pe", lambda e, c=c, mc=mc, ps=ps: e.matmul(
                    ps, lhsT=wx3[:, c, mc * 128:(mc + 1) * 128], rhs=hT[:, c, 4 + tg * 512:4 + (tg + 1) * 512],
                    start=(c == 0), stop=(c == NC8 - 1)), reads=[wxtk], writes=[pstk], signal=(c == NC8 - 1))
            P.op("act", lambda e, ps=ps, mc=mc, xm=xm: e.activation(out=xm[:, mc, 4:516], in_=ps, func=AF.Copy),
                 reads=[pstk], writes=[xmt])
            if tg == 0:
                ps, pstk = psA.next()
                for c in range(NC8):
                    P.op("pe", lambda e, c=c, mc=mc, ps=ps: e.matmul(
                        ps[:, 0:4], lhsT=wx3[:, c, mc * 128:(mc + 1) * 128], rhs=hT[:, c, 0:4],
                        start=(c == 0), stop=(c == NC8 - 1)), reads=[wxtk], writes=[pstk], signal=(c == NC8 - 1))
                P.op("act", lambda e, ps=ps, mc=mc, xm=xm: e.activation(out=xm[:, mc, 0:4], in_=ps[:, 0:4], func=AF.Copy),
                     reads=[pstk], writes=[xmt])
        if tg > 0:
            P.op("pool", lambda e, xm=xm, xmp=xmp: e.tensor_copy(out=xm[:, :, 0:4], in_=xmp[:, :, 512:516]),
                 reads=[xmpt], writes=[xmt])
        for mc in range(4):
            ps, pstk = psA.next()
            for k in range(4):
                P.op("pe", lambda e, k=k, mc=mc, ps=ps, xm=xm: e.matmul(
                    ps, lhsT=diag[:, mc, k, :], rhs=xm[:, mc, 1 + k:1 + k + 512], start=(k == 0), stop=(k == 3)),
                    reads=[hdtk, xmt], writes=[pstk], signal=(k == 3))
            col = L_CONVB + hd * 4 + mc
            P.op("act", lambda e, ps=ps, mc=mc, col=col: e.activation(
                out=xc[:, mc, :], in_=ps, func=AF.Silu, bias=cf[:, col:col + 1]), reads=[pstk, ctk], writes=[xctk])
        return xm, xmt

    gtk = Tk()
    psG = Rot(bk[2:4])
    if stage == "C":
        P.op("pool", lambda e: e.memset(GI, 0.0), writes=[gtk])
        P.op("pool", lambda e: e.memset(GF, 0.0), writes=[gtk])
        P.dma("sync", GI[0:4, :], g_in[0:4, :], writes=[gtk])
        P.dma("sync", GF[0:4, :], g_in[4:8, :], writes=[gtk])
    for hd in range(BH if stage == "B" else 0):
        wx3, wxtk = head_setup(hd)
        for tg in range(4):
            xm, xmt = front(hd, tg, wx3, wxtk)
            for part, Grow in ((0, GI), (1, GF)):
                ps, pstk = psG.next()
                for mc in range(4):
                    P.op("pe", lambda e, ps=ps, mc=mc, part=part: e.matmul(
                        ps, lhsT=wfold[0][part][:, hd * 4 + mc, :], rhs=xc[:, mc, :], start=(mc == 0), stop=False),
                        reads=[wftk, xctk], writes=[pstk], signal=False)
                    P.op("pe", lambda e, ps=ps, mc=mc, part=part, xm=xm: e.matmul(
                        ps, lhsT=wfold[1][part][:, hd * 4 + mc, :], rhs=xm[:, mc, 4:516], start=False, stop=(mc == 3)),
                        reads=[wftk, xmt], writes=[pstk], signal=(mc == 3))
                sl = slice(tg * 512, (tg + 1) * 512)
                if hd == 0:
                    P.op("act", lambda e, ps=ps, Grow=Grow, sl=sl: e.activation(out=Grow[:, sl], in_=ps, func=AF.Copy),
                         reads=[pstk], writes=[gtk])
                else:
                    P.op("dve", lambda e, ps=ps, Grow=Grow, sl=sl: e.tensor_tensor(out=Grow[:, sl], in0=ps, in1=Grow[:, sl], op=ALU.add),
                         reads=[pstk, gtk], writes=[gtk])

    rtk = Tk()
    if stage == "B":
        P.dma("sync", g_out[0:4, :], GI[0:4, :], reads=[gtk])
        P.dma("sync", g_out[4:8, :], GF[0:4, :], reads=[gtk])
    P.op("dve", lambda e: e.tensor_scalar(out=GI, in0=GI, scalar1=cf[:, L_BI:L_BI + 1], scalar2=None, op0=ALU.add),
         reads=[gtk, ctk], writes=[gtk])
    P.op("dve", lambda e: e.tensor_scalar(out=GF, in0=GF, scalar1=cf[:, L_BF:L_BF + 1], scalar2=None, op0=ALU.add),
         reads=[gtk, ctk], writes=[gtk])
    P.op("dve", lambda e: e.tensor_scalar(out=T1, in0=GF, scalar1=-1.0, scalar2=None, op0=ALU.mult), reads=[gtk], writes=[rtk])
    P.op("dve", lambda e: e.tensor_tensor(out=T1, in0=T1, in1=GF, op=ALU.max), reads=[gtk, rtk], writes=[rtk])
    P.op("act", lambda e: e.activation(out=T1, in_=T1, func=AF.Exp, scale=-1.0), reads=[rtk], writes=[rtk])
    P.op("act", lambda e: e.activation(out=T1, in_=T1, func=AF.Ln, bias=one_col), reads=[rtk, ctk], writes=[rtk])
    P.op("dve", lambda e: e.scalar_tensor_tensor(out=LF, in0=GF, scalar=0.0, in1=T1, op0=ALU.min, op1=ALU.subtract),
         reads=[gtk, rtk], writes=[rtk])
    P.op("pool", lambda e: e.memset(T1, 1.0), reads=[rtk], writes=[rtk])
    P.op("dve", lambda e: e.tensor_tensor_scan(out=BB, data0=T1, data1=LF, initial=0.0, op0=ALU.mult, op1=ALU.add),
         reads=[rtk], writes=[rtk])
    P.op("dve", lambda e: e.tensor_tensor(out=T1, in0=GI, in1=BB, op=ALU.subtract), reads=[gtk, rtk], writes=[rtk])
    ALPHA = T1
    psR = Rot([bk[4]])
    tmtk = Tk()

    def to_token_major(row, dst):
        ps, pstk = psR.next()
        for ck in range(NCK):
            P.op("pe", lambda e, ps=ps, ck=ck: e.matmul(ps[:, ck * 4:ck * 4 + 4], lhsT=row[:, ck * 128:(ck + 1) * 128],
                                                        rhs=ident_f[:, 0:4], start=True, stop=True),
                 reads=[rtk, gtk, ctk], writes=[pstk], signal=(ck == NCK - 1))
        P.op("act", lambda e, ps=ps: e.activation(out=dst, in_=ps[:, 0:64].rearrange("p (a b) -> p a b", a=NCK), func=AF.Copy),
             reads=[pstk], writes=[tmtk])

    def replicate_cols(col_ap, dst4):
        ps, pstk = psR.next()
        za = small[:, 32:32 + 4]
        P.op("dve", lambda e: e.tensor_scalar(out=za, in0=ident_f[:, 0:4], scalar1=col_ap, scalar2=None, op0=ALU.mult),
             reads=[rtk, ctk, gtk], writes=[rtk])
        P.op("pe", lambda e, ps=ps: e.matmul(ps[:, 0:4], lhsT=ones_f, rhs=za, start=True, stop=True),
             reads=[rtk, ctk], writes=[pstk])
        P.op("act", lambda e, ps=ps: e.activation(out=dst4, in_=ps[:, 0:4], func=AF.Copy), reads=[pstk], writes=[rtk])

    if stage == "B":
        mx = small[:, 0:1]
        P.op("dve", lambda e: e.tensor_reduce(out=mx, in_=ALPHA, axis=AX.X, op=ALU.max), reads=[rtk], writes=[rtk])
        nb_ = small[:, 1:2]
        P.op("dve", lambda e: e.scalar_tensor_tensor(out=nb_, in0=mx, scalar=-1.0, in1=cf[:, L_LNK:L_LNK + 1],
                                                     op0=ALU.mult, op1=ALU.add), reads=[rtk, ctk], writes=[rtk])
        P.op("act", lambda e: e.activation(out=LF, in_=ALPHA, func=AF.Exp, bias=nb_), reads=[rtk], writes=[rtk])
        to_token_major(LF, TMw)
        ml = small[:, 2:3]
        P.op("dve", lambda e: e.tensor_tensor(out=ml, in0=mx, in1=BB[:, NT - 1:NT], op=ALU.add), reads=[rtk], writes=[rtk])
        fin = cx.sb(None, [128, 8], F32, "fin")
        replicate_cols(BB[:, NT - 1:NT], fin[:, 0:4])
        replicate_cols(ml, fin[:, 4:8])
        P.dma("sync", st_out[:, 10240:10248], fin, reads=[rtk])
        P.barrier()
        cx.release(pers_mark)
        kv_rot = Rot([cx.sb(None, [128, 512], BF16, "kv") for _ in range(4)])
        stC = cx.sb(None, [128, 4, 512], F32, "stC")
        stn = cx.sb(None, [128, 512], F32, "stn")
        sttk = Tk()
        psKV = Rot([bk[0], bk[1], bk[2]])
        for hd in range(BH):
            wx3, wxtk = head_setup(hd)
            cacc = [bk[3 + dc] for dc in range(4)]
            nacc, nacctk = bk[7]
            for tg in range(4):
                xm, xmt = front(hd, tg, wx3, wxtk)
                KV = {}

                def s1_pre(cl):
                    ck = tg * 4 + cl
                    tsl = slice(cl * 128, (cl + 1) * 128)
                    ps, pstk = psKV.next()
                    for mc in range(4):
                        P.op("pe", lambda e, ps=ps, mc=mc, tsl=tsl: e.matmul(
                            ps[:, mc * 128:(mc + 1) * 128], lhsT=xc[:, mc, tsl], rhs=bdh[:, 1, mc, :], start=True, stop=True),
                            reads=[xctk, hdtk], writes=[pstk], signal=(mc == 3))
                    wk, wktk = kv_rot.next()
                    P.op("act", lambda e, ps=ps, wk=wk, ck=ck, hd=hd: e.activation(
                        out=wk, in_=ps, func=AF.Copy, scale=TMw[:, ck, hd:hd + 1]), reads=[pstk, tmtk], writes=[wktk])
                    ps, pstk = psKV.next()
                    for mc in range(4):
                        P.op("pe", lambda e, ps=ps, mc=mc, cl=cl, xm=xm: e.matmul(
                            ps[:, mc * 128:(mc + 1) * 128], lhsT=xm[:, mc, 4 + cl * 128:4 + (cl + 1) * 128], rhs=bdh[:, 2, mc, :],
                            start=True, stop=True), reads=[xmt, hdtk], writes=[pstk], signal=(mc == 3))
                    vv, vtk = kv_rot.next()
                    P.op("act", lambda e, ps=ps, vv=vv: e.activation(out=vv, in_=ps, func=AF.Copy), reads=[pstk], writes=[vtk])
                    KV[cl] = (ck, wk, wktk, vv, vtk)

                def s1_acc(cl):
                    ck, wk, wktk, vv, vtk = KV[cl]
                    last = (ck == NCK - 1)
                    for dc in range(4):
                        P.op("pe", lambda e, dc=dc, wk=wk, vv=vv, ck=ck, last=last: e.matmul(
                            cacc[dc][0], lhsT=wk[:, dc * 128:(dc + 1) * 128], rhs=vv, start=(ck == 0), stop=last),
                            reads=[wktk, vtk], writes=[cacc[dc][1]], signal=True)
                    P.op("pe", lambda e, wk=wk, ck=ck, last=last: e.matmul(
                        nacc, lhsT=ones_bf, rhs=wk, start=(ck == 0), stop=last), reads=[wktk, ctk], writes=[nacctk], signal=True)
                s1_pre(0)
                for cl in range(1, 4):
                    s1_pre(cl)
                    s1_acc(cl - 1)
                s1_acc(3)
            for dc in range(4):
                P.op("act", lambda e, dc=dc: e.activation(out=stC[:, dc, :], in_=cacc[dc][0], func=AF.Copy),
                     reads=[cacc[dc][1]], writes=[sttk])
            P.op("dve", lambda e: e.tensor_copy(out=stn, in_=nacc), reads=[nacctk], writes=[sttk])
            P.dma("sync", st_out[:, hd * 2048:(hd + 1) * 2048], stC.rearrange("p a b -> p (a b)"), reads=[sttk])
            P.dma("sync", st_out[:, 8192 + hd * 512:8192 + (hd + 1) * 512], stn, reads=[sttk])
        P.finish()
        return nc, cx

    ftk = Tk()
    P.dma("sync", fold, st_all[:, :, 10240:10248].rearrange("c p n -> p c n"), writes=[ftk])
    negc = cf[:, L_NEG:L_NEG + 1]
    P.op("dve", lambda e: e.memset(mrun, -1e30), writes=[ftk])
    for cp in range(7):
        mu = cf[:, L_CMASK + cp:L_CMASK + cp + 1]
        P.op("dve", lambda e, cp=cp, mu=mu: e.scalar_tensor_tensor(out=fa, in0=fold[:, cp, 0:4], scalar=mu, in1=mrun,
                                                                    op0=ALU.mult, op1=ALU.add), reads=[ftk, ctk], writes=[ftk])
        P.op("dve", lambda e, cp=cp, mu=mu: e.tensor_scalar(out=fb, in0=fold[:, cp, 4:8], scalar1=mu,
                                                            scalar2=cf[:, L_CNEG + cp:L_CNEG + cp + 1], op0=ALU.mult, op1=ALU.add),
             reads=[ftk, ctk], writes=[ftk])
        P.op("dve", lambda e: e.tensor_tensor(out=fc_, in0=fa, in1=fb, op=ALU.max), reads=[ftk], writes=[ftk])
        P.op("dve", lambda e: e.tensor_tensor(out=fa, in0=fa, in1=fc_, op=ALU.subtract), reads=[ftk], writes=[ftk])
        P.op("dve", lambda e: e.tensor_tensor(out=fb, in0=fb, in1=fc_, op=ALU.subtract), reads=[ftk], writes=[ftk])
        P.op("act", lambda e, cp=cp: e.activation(out=S1[:, cp, :], in_=fa, func=AF.Exp), reads=[ftk], writes=[ftk])
        P.op("act", lambda e: e.activation(out=fb, in_=fb, func=AF.Exp), reads=[ftk], writes=[ftk])
        P.op("dve", lambda e, cp=cp, mu=mu: e.tensor_scalar(out=S2[:, cp, :], in0=fb, scalar1=mu, scalar2=None, op0=ALU.mult),
             reads=[ftk, ctk], writes=[ftk])
        P.op("dve", lambda e: e.tensor_copy(out=mrun, in_=fc_), reads=[ftk], writes=[ftk])
    mst = small[:, 4:5]
    P.op("dve", lambda e: e.tensor_tensor(out=small[:, 8:12], in0=mrun, in1=ident_f[:, 0:4], op=ALU.mult), reads=[ftk, ctk], writes=[rtk])
    P.op("dve", lambda e: e.tensor_reduce(out=mst, in_=small[:, 8:12], axis=AX.X, op=ALU.add), reads=[rtk], writes=[rtk])
    P.op("dve", lambda e: e.tensor_tensor_scan(out=GF, data0=LF, data1=GI, initial=mst, op0=ALU.add, op1=ALU.max),
         reads=[rtk, gtk], writes=[gtk])
    MM = GF
    P.op("dve", lambda e: e.tensor_tensor(out=BETAx[:, 1:NT + 1], in0=MM, in1=BB, op=ALU.subtract), reads=[gtk, rtk], writes=[rtk])
    P.op("dve", lambda e: e.tensor_copy(out=BETAx[:, 0:1], in_=mst), reads=[rtk], writes=[rtk])
    BETA = BETAx[:, 1:NT + 1]
    for ck in range(NCK):
        bl = small[:, 16:17]
        P.op("dve", lambda e, ck=ck: e.scalar_tensor_tensor(out=small[:, 16 + ck % 8:17 + ck % 8], in0=BETAx[:, 128 * (ck + 1):128 * (ck + 1) + 1],
                                                            scalar=-1.0, in1=cf[:, L_LNK:L_LNK + 1], op0=ALU.mult, op1=ALU.add),
             reads=[rtk, ctk], writes=[rtk])
        P.op("act", lambda e, ck=ck: e.activation(out=LF[:, ck * 128:(ck + 1) * 128], in_=ALPHA[:, ck * 128:(ck + 1) * 128],
                                                  func=AF.Exp, bias=small[:, 16 + ck % 8:17 + ck % 8]), reads=[rtk], writes=[rtk])
    to_token_major(LF, TMw)
    to_token_major(ALPHA, TMa)
    P.barrier()
    cx.release(pers_mark)
    BETA = BETAx[:, 1:NT + 1]

    qT = cx.sb(None, [128, 4, 512], BF16, "qT")
    kT = cx.sb(None, [128, 4, 512], BF16, "kT")
    zs = cx.sb(None, [128, 4, 512], BF16, "zs")
    yb = cx.sb(None, [128, 4, 512], BF16, "yb")
    qktk, zstk, ytk = Tk(), Tk(), Tk()
    Csts = [cx.sb(None, [128, 4, 512], F32, "Cst") for _ in range(2)]
    Caugs = [cx.sb(None, [128, 4, 640], BF16, "Caug") for _ in range(2)]
    caugtks = [Tk(), Tk()]
    nrow = cx.sb(None, [128, 512], F32, "nrow")
    nrtk = Tk()
    ncol = cx.sb(None, [128, 4], F32, "ncol")
    nctk = Tk()
    ctk2s = [Tk(), Tk()]
    cpar = {"p": 0}
    clst = Rot([topfree[:, 4096:6144], topfree[:, 6144:8176][:, 0:2032]])
    wk_rot = Rot([cx.sb(None, [128, 512], BF16, "wk") for _ in range(2)])
    va_rot = Rot([cx.sb(None, [128, 640], BF16, "vaug") for _ in range(2)])
    for (va, vatk) in va_rot.items:
        P.op("pool", lambda e, va=va: e.memset(va[:, 512:640], 1.0), writes=[vatk])
    dt_rot = Rot([cx.sb(None, [128, 128], F32, "dtmp") for _ in range(2)])
    sd_rot = Rot([cx.sb(None, [128, 128], BF16, "SdT") for _ in range(2)])
    qs_rot = Rot([cx.sb(None, [128, 4, 128], BF16, "qs") for _ in range(2)])
    hsq_rot = Rot([cx.sb(None, [128, 512], BF16, "hsq") for _ in range(2)])
    dd_rot = Rot([cx.sb(None, [128, 128], F32, "dd") for _ in range(2)])
    rr_rot = Rot([cx.sb(None, [128, 128], F32, "rr") for _ in range(2)])
    sc_rot = Rot([cx.sb(None, [128, 128], F32, "scsb") for _ in range(2)])
    em_rot = Rot([cx.sb(None, [128, 128], F32, "emsb") for _ in range(2)])
    ul_rot = Rot([cx.sb(None, [128, 1], F32, "ulast") for _ in range(2)])
    tt_rot = Rot([cx.sb(None, [128, 128], F32, "tt") for _ in range(3)])
    psB2 = psA
    psS3 = Rot([bk[2]])
    psRP = Rot([bk[2]])
    psH = Rot([bk[4], bk[5]])
    psDS = Rot([bk[6], bk[7]])
    psSS = Rot([bk[3]])
    wzv = wupv
    yview = yscr.rearrange("(c p) n -> p c n", p=128)
    def emit_fold(hd):
        Cst, ctk2 = Csts[hd % 2], ctk2s[hd % 2]
        P.op("pool", lambda e: e.memset(Cst, 0.0), writes=[ctk2])
        P.op("pool", lambda e: e.memset(nrow, 0.0), writes=[nrtk])
        Cflat = Cst.rearrange("p a b -> p (a b)")
        for cp in range(7):
            cl_, cltk = clst.items[0]
            P.dma("sync", cl_, st_all[cp][:, hd * 2048:(hd + 1) * 2048], writes=[cltk])
            P.op("act", lambda e, cp=cp, cl_=cl_: e.activation(out=cl_, in_=cl_, func=AF.Copy, scale=S2[:, cp, hd:hd + 1]),
                 reads=[cltk, ftk], writes=[cltk])
            P.op("dve", lambda e, cp=cp, cl_=cl_: e.scalar_tensor_tensor(out=Cflat, in0=Cflat, scalar=S1[:, cp, hd:hd + 1], in1=cl_,
                                                                          op0=ALU.mult, op1=ALU.add), reads=[cltk, ftk, ctk2], writes=[ctk2])
            nl_, nltk = clst.items[1]
            P.dma("sync", nl_[:, 0:512], st_all[cp][:, 8192 + hd * 512:8192 + (hd + 1) * 512], writes=[nltk])
            P.op("act", lambda e, cp=cp, nl_=nl_: e.activation(out=nl_[:, 0:512], in_=nl_[:, 0:512], func=AF.Copy, scale=S2[:, cp, hd:hd + 1]),
                 reads=[nltk, ftk], writes=[nltk])
            P.op("dve", lambda e, cp=cp, nl_=nl_: e.scalar_tensor_tensor(out=nrow, in0=nrow, scalar=S1[:, cp, hd:hd + 1], in1=nl_[:, 0:512],
                                                                          op0=ALU.mult, op1=ALU.add), reads=[nltk, ftk, nrtk], writes=[nrtk])

    ul_prev = None
    for hd in range(BH):
        wx3, wxtk = head_setup(hd)
        wz, wztk = ws.load([wzv[:, :, 2048 + hd * 512:2048 + (hd + 1) * 512]])
        wz3 = wz.rearrange("p (c n) -> p c n", c=NC8)
        Cst, ctk2 = Csts[hd % 2], ctk2s[hd % 2]
        if hd == 0:
            emit_fold(0)

        def refresh_caug(full):
            Caug, caugtk = Caugs[cpar["p"]], caugtks[cpar["p"]]
            for dc in range(4):
                P.op("act", lambda e, dc=dc: e.activation(out=Caug[:, dc, 0:512], in_=Cst[:, dc, :], func=AF.Copy),
                     reads=[ctk2], writes=[caugtk])
            P.op("dve", lambda e: e.tensor_scalar(out=nrow, in0=nrow, scalar1=cf[:, L_E0:L_E0 + 1], scalar2=None, op0=ALU.mult),
                 reads=[nrtk, ctk], writes=[nrtk])
            ps, pstk = psB2.next()
            for dc in range(4):
                P.op("pe", lambda e, ps=ps, dc=dc: e.matmul(ps[:, dc:dc + 1], lhsT=nrow[:, dc * 128:(dc + 1) * 128], rhs=ones_f[:, 0:1],
                                                            start=True, stop=True), reads=[nrtk, ctk], writes=[pstk], signal=(dc == 3))
            P.op("dve", lambda e, ps=ps: e.tensor_copy(out=ncol, in_=ps[:, 0:4]), reads=[pstk], writes=[nctk])
            P.op("dve", lambda e: e.tensor_copy(out=Caug[:, :, 512:640], in_=ncol.unsqueeze(2).to_broadcast([128, 4, 128])),
                 reads=[nctk], writes=[caugtk])

        refresh_caug(True)
        for tg in range(4):
            xm, xmt = front(hd, tg, wx3, wxtk)
            for mc in range(4):
                ps, pstk = psA.next()
                for c in range(NC8):
                    P.op("pe", lambda e, c=c, mc=mc, ps=ps: e.matmul(
                        ps, lhsT=wz3[:, c, mc * 128:(mc + 1) * 128], rhs=hT[:, c, 4 + tg * 512:4 + (tg + 1) * 512],
                        start=(c == 0), stop=(c == NC8 - 1)), reads=[wztk], writes=[pstk], signal=(c == NC8 - 1))
                P.op("act", lambda e, ps=ps, mc=mc: e.activation(out=zs[:, mc, :], in_=ps, func=AF.Silu), reads=[pstk], writes=[zstk])
            for j, dst, sc_ in ((0, qT, 1.0), (1, kT, KSCALE)):
                for dc in range(4):
                    ps, pstk = psA.next()
                    P.op("pe", lambda e, ps=ps, j=j, dc=dc: e.matmul(ps, lhsT=bdh[:, j, dc, :], rhs=xc[:, dc, :], start=True, stop=True),
                         reads=[hdtk, xctk], writes=[pstk])
                    P.op("act", lambda e, ps=ps, dst=dst, dc=dc, sc_=sc_: e.activation(out=dst[:, dc, :], in_=ps, func=AF.Copy, scale=sc_),
                         reads=[pstk], writes=[qktk])
            RS = {}

            def stage_pre(cl):
                nonlocal ul_prev
                ck = tg * 4 + cl
                tsl = slice(cl * 128, (cl + 1) * 128)
                gsl = slice(ck * 128, (ck + 1) * 128)
                sel = cf[:, L_SEL + hd * 128:L_SEL + (hd + 1) * 128]
                rp, rptk = psRP.next()
                for i3, row in enumerate((BETA, MM)):
                    P.op("pe", lambda e, rp=rp, i3=i3, row=row, gsl=gsl: e.matmul(
                        rp[:, i3 * 128:(i3 + 1) * 128], lhsT=sel, rhs=row[:, gsl], start=True, stop=True),
                        reads=[rtk, gtk, ctk], writes=[rptk], signal=(i3 == 1))
                bprev = mrun[:, hd:hd + 1] if ck == 0 else ul_prev[0]
                bprev_tk = ftk if ck == 0 else ul_prev[1]
                scsb, sctk = sc_rot.next()
                P.op("act", lambda e, rp=rp, scsb=scsb, bprev=bprev: e.activation(out=scsb, in_=rp[:, 0:128], func=AF.Exp, scale=-1.0, bias=bprev),
                     reads=[rptk, bprev_tk], writes=[sctk])
                emsb, emtk = em_rot.next()
                P.op("act", lambda e, rp=rp, emsb=emsb: e.activation(out=emsb, in_=rp[:, 128:256], func=AF.Exp, scale=-1.0),
                     reads=[rptk], writes=[emtk])
                ul_prev = ul_rot.next()
                P.op("act", lambda e, rp=rp, ul_prev=ul_prev: e.activation(out=ul_prev[0], in_=rp[:, 127:128], func=AF.Copy),
                     reads=[rptk], writes=[ul_prev[1]])
                ps, pstk = psB2.next()
                for mc in range(4):
                    P.op("pe", lambda e, ps=ps, mc=mc, tsl=tsl: e.matmul(
                        ps[:, mc * 128:(mc + 1) * 128], lhsT=xc[:, mc, tsl], rhs=bdh[:, 1, mc, :], start=True, stop=True),
                        reads=[xctk, hdtk], writes=[pstk], signal=(mc == 3))
                wk, wktk = wk_rot.next()
                P.op("act", lambda e, ps=ps, wk=wk, ck=ck: e.activation(out=wk, in_=ps, func=AF.Copy, scale=TMw[:, ck, hd:hd + 1]),
                     reads=[pstk, tmtk], writes=[wktk])
                ps, pstk = psB2.next()
                for mc in range(4):
                    P.op("pe", lambda e, ps=ps, mc=mc, cl=cl, xm=xm: e.matmul(
                        ps[:, mc * 128:(mc + 1) * 128], lhsT=xm[:, mc, 4 + cl * 128:4 + (cl + 1) * 128], rhs=bdh[:, 2, mc, :],
                        start=True, stop=True), reads=[xmt, hdtk], writes=[pstk], signal=(mc == 3))
                va, vatk = va_rot.next()
                P.op("act", lambda e, ps=ps, va=va: e.activation(out=va[:, 0:512], in_=ps, func=AF.Copy), reads=[pstk], writes=[vatk])
                pS_, pStk = psS3.next()
                pS = pS_[:, 256:384]
                for dc in range(4):
                    P.op("pe", lambda e, pS=pS, dc=dc, tsl=tsl: e.matmul(pS, lhsT=kT[:, dc, tsl], rhs=qT[:, dc, tsl],
                                                                          start=(dc == 0), stop=(dc == 3)),
                         reads=[qktk], writes=[pStk], signal=(dc == 3))
                dtmp, dttk = dt_rot.next()
                P.op("dve", lambda e, rp=rp, dtmp=dtmp, ck=ck: e.scalar_tensor_tensor(
                    out=dtmp, in0=rp[:, 0:128], scalar=TMa[:, ck, hd:hd + 1], in1=cf[:, L_MASKLOW:L_MASKLOW + 128],
                    op0=ALU.subtract, op1=ALU.max), reads=[rptk, tmtk, ctk], writes=[dttk])
                P.op("act", lambda e, dtmp=dtmp: e.activation(out=dtmp, in_=dtmp, func=AF.Exp, scale=-1.0), reads=[dttk], writes=[dttk])
                sd, sdtk = sd_rot.next()
                P.op("dve", lambda e, pS=pS, dtmp=dtmp, sd=sd: e.tensor_tensor(out=sd, in0=pS, in1=dtmp, op=ALU.mult),
                     reads=[pStk, dttk], writes=[sdtk])
                qs, qstk = qs_rot.next()
                P.op("dve", lambda e, scsb=scsb, qs=qs, tsl=tsl: e.tensor_tensor(
                    out=qs, in0=qT[:, :, tsl], in1=scsb.unsqueeze(1).to_broadcast([128, 4, 128]), op=ALU.mult),
                    reads=[qktk, sctk], writes=[qstk])

                RS[cl] = dict(ck=ck, tsl=tsl, wk=wk, wktk=wktk, va=va, vatk=vatk, sd=sd, sdtk=sdtk, qs=qs, qstk=qstk,
                              scsb=scsb, sctk=sctk, emsb=emsb, emtk=emtk)

            def stage_mid(cl):
                r_ = RS[cl]
                ck, tsl, wk, wktk, va, vatk, sd, sdtk, qs, qstk, scsb, sctk = (r_[k_] for k_ in (
                    "ck", "tsl", "wk", "wktk", "va", "vatk", "sd", "sdtk", "qs", "qstk", "scsb", "sctk"))
                Caug, caugtk = Caugs[cpar["p"]], caugtks[cpar["p"]]
                CaugN, caugNtk = Caugs[1 - cpar["p"]], caugtks[1 - cpar["p"]]
                cpar["p"] = 1 - cpar["p"]
                if dbg_stop is not None and (hd, ck) == tuple(dbg_stop):
                    P.barrier()
                    P.finish()
                    return nc, cx
                decay = scsb[:, 127:128]
                for dc in range(4):
                    ps, pstk = psB2.next()
                    P.op("pe", lambda e, ps=ps, dc=dc, wk=wk, va=va: e.matmul(ps, lhsT=wk[:, dc * 128:(dc + 1) * 128], rhs=va[:, 0:512],
                                                                                start=True, stop=True), reads=[wktk, vatk], writes=[pstk])
                    P.op("dve", lambda e, ps=ps, dc=dc, decay=decay: e.scalar_tensor_tensor(
                        out=Cst[:, dc, :], in0=Cst[:, dc, :], scalar=decay, in1=ps, op0=ALU.mult, op1=ALU.add),
                        reads=[pstk, sctk, ctk2], writes=[ctk2])
                    P.op("dve", lambda e, dc=dc: e.tensor_copy(out=CaugN[:, dc, 0:512], in_=Cst[:, dc, :]),
                         reads=[ctk2], writes=[caugNtk])
                ps, pstk = psB2.next()
                for dc in range(4):
                    P.op("pe", lambda e, ps=ps, dc=dc, wk=wk: e.matmul(ps[:, dc:dc + 1], lhsT=wk[:, dc * 128:(dc + 1) * 128], rhs=ones_bf[:, 0:1],
                                                                         start=True, stop=True), reads=[wktk, ctk], writes=[pstk], signal=(dc == 3))
                P.op("dve", lambda e, ps=ps, decay=decay: e.scalar_tensor_tensor(out=ncol, in0=ncol, scalar=decay, in1=ps[:, 0:4],
                                                                                  op0=ALU.mult, op1=ALU.add),
                     reads=[pstk, sctk, nctk], writes=[nctk])
                P.op("dve", lambda e: e.tensor_copy(out=CaugN[:, :, 512:640], in_=ncol.unsqueeze(2).to_broadcast([128, 4, 128])),
                     reads=[nctk], writes=[caugNtk])
                pH, pHtk = psH.next()
                pD_, pDtk = psDS.next()
                for ec in range(5):
                    o = pH[:, ec * 128:(ec + 1) * 128] if ec < 4 else pD_[:, 0:128]
                    otk = pHtk if ec < 4 else pDtk
                    for dc in range(4):
                        P.op("pe", lambda e, o=o, ec=ec, dc=dc, qs=qs: e.matmul(
                            o, lhsT=Caug[:, dc, ec * 128:(ec + 1) * 128], rhs=qs[:, dc, :], start=(dc == 0), stop=False),
                            reads=[caugtk, qstk], writes=[otk], signal=False)
                    P.op("pe", lambda e, o=o, ec=ec, va=va, sd=sd: e.matmul(
                        o, lhsT=va[:, ec * 128:(ec + 1) * 128], rhs=sd, start=False, stop=True),
                        reads=[vatk, sdtk], writes=[otk], signal=True)

                r_.update(pH=pH, pHtk=pHtk, pD_=pD_, pDtk=pDtk)

            def stage_post(cl):
                r_ = RS[cl]
                ck, tsl, emsb, emtk, pH, pHtk, pD_, pDtk = (r_[k_] for k_ in ("ck", "tsl", "emsb", "emtk", "pH", "pHtk", "pD_", "pDtk"))
                hsq, hsqtk = hsq_rot.next()
                P.op("act", lambda e, pH=pH, hsq=hsq: e.activation(out=hsq, in_=pH, func=AF.Square), reads=[pHtk], writes=[hsqtk])
                pSS_, pSStk = psSS.next()
                pSS = pSS_[:, 0:128]
                for ec in range(4):
                    P.op("pe", lambda e, pSS=pSS, hsq=hsq, ec=ec: e.matmul(pSS, lhsT=ones_bf, rhs=hsq[:, ec * 128:(ec + 1) * 128],
                                                                            start=(ec == 0), stop=(ec == 3)),
                         reads=[hsqtk, ctk], writes=[pSStk], signal=(ec == 3))
                dd, ddtk = dd_rot.next()
                P.op("dve", lambda e, pD_=pD_, dd=dd: e.tensor_scalar(out=dd, in0=pD_[:, 0:128], scalar1=-1.0, scalar2=None, op0=ALU.mult),
                     reads=[pDtk], writes=[ddtk])
                P.op("dve", lambda e, pD_=pD_, dd=dd: e.tensor_tensor(out=dd, in0=dd, in1=pD_[:, 0:128], op=ALU.max),
                     reads=[pDtk, ddtk], writes=[ddtk])
                P.op("dve", lambda e, emsb=emsb, dd=dd: e.tensor_tensor(out=dd, in0=dd, in1=emsb, op=ALU.max),
                     reads=[emtk, ddtk], writes=[ddtk])
                P.op("dve", lambda e, dd=dd: e.scalar_tensor_tensor(out=dd, in0=dd, scalar=EPS, in1=dd, op0=ALU.mult, op1=ALU.mult),
                     reads=[ddtk], writes=[ddtk])
                rr, rrtk = rr_rot.next()
                P.op("dve", lambda e, pSS=pSS, dd=dd, rr=rr: e.scalar_tensor_tensor(out=rr, in0=pSS, scalar=1.0 / DH, in1=dd,
                                                                                     op0=ALU.mult, op1=ALU.add),
                     reads=[pSStk, ddtk], writes=[rrtk])
                P.op("act", lambda e, rr=rr: e.activation(out=rr, in_=rr, func=AF.Sqrt), reads=[rrtk], writes=[rrtk])
                P.op("dve", lambda e, rr=rr: e.reciprocal(out=rr, in_=rr), reads=[rrtk], writes=[rrtk])
                for ec in range(4):
                    ch = hd * 4 + ec
                    tt, tttk = tt_rot.next()
                    P.op("dve", lambda e, pH=pH, ec=ec, ch=ch, rr=rr, tt=tt: e.scalar_tensor_tensor(
                        out=tt, in0=pH[:, ec * 128:(ec + 1) * 128], scalar=cf[:, L_HGAIN + ch:L_HGAIN + ch + 1], in1=rr,
                        op0=ALU.mult, op1=ALU.mult), reads=[pHtk, rrtk, ctk], writes=[tttk])
                    P.op("dve", lambda e, ec=ec, ch=ch, tt=tt, tsl=tsl: e.scalar_tensor_tensor(
                        out=tt, in0=xc[:, ec, tsl], scalar=cf[:, L_SKIP + ch:L_SKIP + ch + 1], in1=tt,
                        op0=ALU.mult, op1=ALU.add), reads=[xctk, tttk, ctk], writes=[tttk])
                    P.op("dve", lambda e, ec=ec, tt=tt, tsl=tsl: e.tensor_tensor(out=yb[:, ec, tsl], in0=tt, in1=zs[:, ec, tsl], op=ALU.mult),
                         reads=[tttk, zstk], writes=[ytk])


            stage_pre(0)
            stage_mid(0)
            for cl in range(1, 4):
                stage_pre(cl)
                stage_mid(cl)
                stage_post(cl - 1)
            stage_post(3)

            if tg == 1 and hd + 1 < BH:
                emit_fold(hd + 1)
            P.dma("sync", yview[:, hd * 4:(hd + 1) * 4, tg * 512:(tg + 1) * 512], yb, reads=[ytk])
    P.barrier()
    cx.release(base_mark)

    X = TOP.rearrange("p (c n) -> p c n", c=NC8)
    Xtk = [[Tk() for _ in range(4)] for _ in range(NC8)]
    for c in range(NC8):
        for tg in range(4):
            P.dma("sync", X[:, c, tg * 512:(tg + 1) * 512], x1T[c * 128:(c + 1) * 128, tg * 512:(tg + 1) * 512], writes=[Xtk[c][tg]])
    mk = cx.mark()
    wdn = cx.sb(None, [128, 16, D], BF16, "wdn")
    wdtk = Tk()
    wdv = wdown.rearrange("(c p) n -> p c n", p=128)
    wstg3 = Rot([cx.sb(None, [128, 4, D], F32, "wstg3") for _ in range(2)])
    for q4 in range(4):
        stg_, stk_ = wstg3.next()
        P.dma("sync", stg_, wdv[:, q4 * 4:(q4 + 1) * 4, :], writes=[stk_])
        P.op("act", lambda e, stg_=stg_, q4=q4: e.activation(out=wdn[:, q4 * 4:(q4 + 1) * 4, :], in_=stg_, func=AF.Copy),
             reads=[stk_], writes=[wdtk])
    yts = [cx.sb(None, [128, 16, 512], BF16, "yt") for _ in range(2)]
    yttk = [Tk(), Tk()]
    psA4 = Rot(bk[0:4])
    for tg in range(4):
        yt, ytt = yts[tg % 2], yttk[tg % 2]
        P.dma("sync", yt, yview[:, :, tg * 512:(tg + 1) * 512], writes=[ytt])
        sl = slice(tg * 512, (tg + 1) * 512)
        for oc in range(NC8):
            ps, pstk = psA4.next()
            for mc in range(16):
                P.op("pe", lambda e, ps=ps, mc=mc, oc=oc, yt=yt: e.matmul(ps, lhsT=wdn[:, mc, oc * 128:(oc + 1) * 128], rhs=yt[:, mc, :],
                                                                           start=(mc == 0), stop=(mc == 15)),
                     reads=[wdtk, ytt], writes=[pstk], signal=(mc == 15))
            P.op("dve", lambda e, ps=ps, oc=oc, sl=sl: e.tensor_tensor(out=X[:, oc, sl], in0=ps, in1=X[:, oc, sl], op=ALU.add),
                 reads=[pstk, Xtk[oc][tg]], writes=[Xtk[oc][tg]])
    P.barrier()
    cx.release(mk)
    if debug:
        emit_store(cx, X, Xtk, dbg_a)
    emit_mlp(cx, X, Xtk, cf[:, L_MLPN:L_MLPN + 8], ctk, w1, w2, ones_bf)
    emit_ple(cx, X, Xtk, cf[:, L_PLEN:L_PLEN + 8], ctk, wg, wp, pT, ones_bf)
    emit_store(cx, X, Xtk, out)
    P.finish()
    return nc, cx


_CACHE = {}


def _prog(key, builder):
    return builder()


def kernel(**inputs):
    inputs = {k: np.asarray(v) for k, v in inputs.items()}
    cores = list(range(NCORES))
    nc, _ = build_layer0()
    in_maps = [layer0_inputs(inputs, c) for c in cores]
    res = run_bass_kernel_spmd(nc, in_maps, core_ids=cores)
    x1T = np.concatenate([r["xout"] for r in res.results], axis=1)
    nc, _ = build_layer1("B")
    in_maps = [layer1_inputs(inputs, c, x1T, "B") for c in cores]
    res = run_bass_kernel_spmd(nc, in_maps, core_ids=cores)
    st_all = np.stack([res.results[c]["st_out"] for c in range(7)])
    g_rows = [res.results[c]["g_out"] for c in cores]
    nc, _ = build_layer1("C")
    in_maps = [layer1_inputs(inputs, c, x1T, "C", st_all, g_rows[c]) for c in cores]
    res = run_bass_kernel_spmd(nc, in_maps, core_ids=cores)
    outT = np.concatenate([r["xout"] for r in res.results], axis=1)
    return np.ascontiguousarray(outT.T)[None].astype(np.float32)
```

```python
import numpy as np
import concourse.bass as bass
import concourse.mybir as mybir
from concourse.bass_utils import run_bass_kernel_spmd

F32 = mybir.dt.float32
BF16 = mybir.dt.bfloat16
AF = mybir.ActivationFunctionType
ALU = mybir.AluOpType
AX = mybir.AxisListType

NCORES = 8
S = 16384
D = 1024
NT = S // NCORES
NC8 = D // 128
EPS = 1e-6
BIG = 30000.0
A_GROUPS = ((128, 1), (512, 4), (2048, 16))
NDMA = 24
SB_F32 = 51968


class Tk:
    __slots__ = ("w", "r")

    def __init__(self):
        self.w = {}
        self.r = {}


class Prog:
    def __init__(self, nc):
        self.nc = nc
        self.eng = {"act": nc.scalar, "dve": nc.vector, "pool": nc.gpsimd, "pe": nc.tensor, "sync": nc.sync}
        self.sem = {e: nc.alloc_semaphore("s_" + e) for e in ("act", "dve", "pool", "pe")}
        self.cnt = {e: 0 for e in ("act", "dve", "pool", "pe")}
        self.seen = {e: {} for e in self.eng}
        self.dsem = [nc.alloc_semaphore("s_dma%d" % i) for i in range(NDMA)]
        self.dcnt = [0] * NDMA
        self.dnext = 0
        self.nins = {e: 0 for e in self.eng}

    def _semof(self, src):
        if isinstance(src, tuple):
            return self.dsem[src[1]]
        return self.sem[src]

    def _deps(self, e, reads, writes, allraw=False):
        deps = {}

        def add(src, n, raw):
            if src == e and not allraw:
                if e == "pe" or not raw:
                    return
            if deps.get(src, 0) < n:
                deps[src] = n

        for t in reads:
            for src, n in t.w.items():
                add(src, n, True)
        for t in writes:
            for src, n in t.w.items():
                add(src, n, False)
            for src, n in t.r.items():
                add(src, n, False)
        return deps

    def _wait(self, e, deps):
        eng = self.eng[e]
        seen = self.seen[e]
        for src, n in deps.items():
            if seen.get(src, 0) >= n:
                continue
            seen[src] = n
            eng.wait_ge(self._semof(src), n)
            self.nins[e] += 1

    def op(self, e, fn, reads=(), writes=(), signal=True):
        self._wait(e, self._deps(e, reads, writes))
        ins = fn(self.eng[e])
        self.nins[e] += 1
        n = self.cnt[e] + 1
        if signal:
            ins.then_inc(self.sem[e], 1)
            self.cnt[e] = n
        for t in reads:
            if t.r.get(e, 0) < n:
                t.r[e] = n
        for t in writes:
            if t.w.get(e, 0) < n:
                t.w[e] = n
        return ins

    def dma(self, q, out, in_, reads=(), writes=()):
        k = self.dnext
        self.dnext = (k + 1) % NDMA
        src = ("dma", k)
        deps = self._deps(q, reads, writes, allraw=True)
        if self.dcnt[k] > 0:
            deps[src] = max(deps.get(src, 0), self.dcnt[k])
        self._wait(q, deps)
        ins = self.eng[q].dma_start(out=out, in_=in_)
        self.nins[q] += 1
        n = self.dcnt[k] + 16
        ins.then_inc(self.dsem[k], 16)
        self.dcnt[k] = n
        for t in reads:
            t.r[src] = n
        for t in writes:
            t.w[src] = n

    def barrier(self):
        for e in self.eng:
            deps = {}
            for s2 in self.cnt:
                if s2 != e and self.cnt[s2] > 0:
                    deps[s2] = self.cnt[s2]
            for k in range(NDMA):
                if self.dcnt[k] > 0:
                    deps[("dma", k)] = self.dcnt[k]
            self._wait(e, deps)

    def finish(self):
        deps = {}
        for k in range(NDMA):
            if self.dcnt[k] > 0:
                deps[("dma", k)] = self.dcnt[k]
        self._wait("sync", deps)


class Rot:
    def __init__(self, aps):
        self.items = [a if isinstance(a, tuple) else (a, Tk()) for a in aps]
        self.i = 0

    def next(self):
        it = self.items[self.i]
        self.i = (self.i + 1) % len(self.items)
        return it


class Ctx:
    def __init__(self, nc):
        self.nc = nc
        self.P = Prog(nc)
        self.banks = [nc.alloc_psum_tensor("psb%d" % i, [128, 512], F32).ap() for i in range(8)]
        self.nalloc = 0

        self.big = nc.alloc_sbuf_tensor("big", [128, SB_F32], F32).ap()
        self.top = 0

    def sb(self, stack, shape, dt, name=None):
        esz = 2 if dt == BF16 else 4
        n = int(np.prod(shape[1:]))
        nbytes = (n * esz + 63) // 64 * 64
        off = self.top
        assert off + nbytes <= SB_F32 * 4, ("SBUF overflow", name, off, nbytes)
        self.top = off + nbytes
        self.log = getattr(self, 'log', [])
        self.log.append((name, off, nbytes))
        ap = self.big[:, off // 4:(off + nbytes) // 4]
        if dt == BF16:
            ap = ap.bitcast(BF16)
        ap = ap[:, 0:n]
        if len(shape) == 3:
            ap = ap.rearrange("p (a b) -> p a b", a=shape[1])
        elif len(shape) == 4:
            ap = ap.rearrange("p (a b c) -> p a b c", a=shape[1], b=shape[2])
        return ap

    def mark(self):
        return self.top

    def release(self, m):
        self.top = m


def load_consts(cx, stack, cst_ap, ncols):
    P = cx.P
    cf = cx.sb(stack, [128, ncols], F32, "cstf")
    cb = cx.sb(stack, [128, C_END_BF], BF16, "cstb")
    tk = Tk()
    P.dma("sync", cf, cst_ap, writes=[tk])
    P.op("dve", lambda e: e.tensor_copy(out=cb, in_=cf[:, 0:C_END_BF]), reads=[tk], writes=[tk])
    return cf, cb, tk


class WStream:
    def __init__(self, cx, stack, nelem, nstage=2, nslot=2):
        self.cx = cx
        self.nelem = nelem
        self.stage = Rot([cx.sb(stack, [128, nelem], F32, "wstg") for _ in range(nstage)])
        self.slots = Rot([cx.sb(stack, [128, nelem], BF16, "wbf") for _ in range(nslot)])

    def load(self, views):
        P = self.cx.P
        stg, stk = self.stage.next()
        wb, wtk = self.slots.next()
        off = 0
        for v in views:
            shp = v.shape
            n = int(np.prod(shp[1:]))
            dst = stg[:, off:off + n]
            if len(shp) == 3:
                dst = dst.rearrange("p (a b) -> p a b", a=shp[1])
            P.dma("sync", dst, v, writes=[stk])
            off += n
        assert off <= self.nelem
        P.op("pool", lambda e: e.tensor_copy(out=wb[:, 0:off], in_=stg[:, 0:off]), reads=[stk], writes=[wtk])
        return wb, wtk


def rms_stats(cx, xs, n, sq_rot, ps_ap, ps_tk, rstd, rstd_tk, ones_bf, ctk, inv_dim):
    P = cx.P
    nx = len(xs)
    for c, (xa, xt) in enumerate(xs):
        sq, sqt = sq_rot.next()
        P.op("act", lambda e, xa=xa, sq=sq: e.activation(out=sq[:, 0:n], in_=xa, func=AF.Square), reads=[xt], writes=[sqt])
        P.op("pe", lambda e, sq=sq, c=c: e.matmul(ps_ap[:, 0:n], lhsT=ones_bf, rhs=sq[:, 0:n], start=(c == 0), stop=(c == nx - 1)),
             reads=[sqt, ctk], writes=[ps_tk])
    P.op("act", lambda e: e.activation(out=rstd[:, 0:n], in_=ps_ap[:, 0:n], func=AF.Sqrt, bias=cx.eps_col, scale=inv_dim),
         reads=[ps_tk, ctk], writes=[rstd_tk])
    P.op("dve", lambda e: e.reciprocal(out=rstd[:, 0:n], in_=rstd[:, 0:n]), reads=[rstd_tk], writes=[rstd_tk])


C_ID, C_ONES, C_BONES, C_DM, C_HONES = 0, 128, 256, 384, 640
C_OZ = 704
C_HZ = 960
C_END_BF = 1216
C_EPS = 1216
C_GAINS = 1217
G0_ANORM = C_GAINS
G0_QG = G0_ANORM + 8
G0_KG = G0_QG + 3
G0_MLPN = G0_KG + 3
G0_PLEN = G0_MLPN + 8
G0_END = G0_PLEN + 8


def base_consts(core, ncols):
    c = np.zeros((128, ncols), np.float32)
    c[:, C_ID:C_ID + 128] = np.eye(128, dtype=np.float32)
    c[:, C_ONES:C_ONES + 128] = 1.0
    c[0:64, C_BONES:C_BONES + 64] = 1.0
    c[64:128, C_BONES + 64:C_BONES + 128] = 1.0
    kk = np.arange(128)[:, None]
    a = np.arange(128)[None, :]
    diag = np.where(kk <= a, a - kk, BIG)
    prev = np.where(kk >= a, 128 + a - kk, BIG)
    c[:, C_DM:C_DM + 128] = diag
    c[:, C_DM + 128:C_DM + 256] = prev
    hv = 0.0 if core == 0 else 1.0
    c[:, C_HONES:C_HONES + 64] = hv
    c[:, C_OZ:C_OZ + 64] = 1.0
    c[:, C_OZ + 128 + 64:C_OZ + 256] = 1.0
    c[:, C_HZ:C_HZ + 64] = hv
    c[:, C_HZ + 128 + 64:C_HZ + 256] = hv
    c[:, C_EPS] = EPS
    return c


def col_layout(v):
    v = np.asarray(v, np.float32).reshape(-1, 128)
    return np.ascontiguousarray(v.T)


def emit_norm_resident(cx, X, Xtk, gcol, ctk, hT, hTtk, sq_rot, rstd_rot, ps_rot, ones_bf):
    P = cx.P
    for tg in range(NT // 512):
        sl = slice(tg * 512, (tg + 1) * 512)
        xs = [(X[:, c, sl], Xtk[c][tg]) for c in range(NC8)]
        ps_ap, ps_tk = ps_rot.next()
        rstd, rtk = rstd_rot.next()
        rms_stats(cx, xs, 512, sq_rot, ps_ap, ps_tk, rstd, rtk, ones_bf, ctk, 1.0 / D)
        for c in range(NC8):
            P.op("dve", lambda e, c=c, sl=sl, rstd=rstd: e.scalar_tensor_tensor(
                out=hT[:, c, sl], in0=X[:, c, sl], scalar=gcol[:, c:c + 1], in1=rstd[:, 0:512],
                op0=ALU.mult, op1=ALU.mult), reads=[Xtk[c][tg], rtk, ctk], writes=[hTtk[c][tg]])


def emit_mlp(cx, X, Xtk, gcol, ctk, w1, w2, ones_bf):
    P = cx.P
    mk = cx.mark()
    st = None
    hT = cx.sb(st, [128, NC8, NT], BF16, "mlp_hT")
    hTtk = [[Tk() for _ in range(4)] for _ in range(NC8)]
    sq_rot = Rot([cx.sb(st, [128, 512], BF16, "sq") for _ in range(4)])
    rstd_rot = Rot([cx.sb(st, [128, 512], F32, "rstd") for _ in range(2)])
    ps_stat = Rot([cx.banks[7]])
    emit_norm_resident(cx, X, Xtk, gcol, ctk, hT, hTtk, sq_rot, rstd_rot, ps_stat, ones_bf)
    ws = WStream(cx, st, 4096, nstage=2, nslot=2)
    hids = [cx.sb(st, [128, 4, NT], BF16, "hid") for _ in range(2)]
    hid_tks = [[[Tk() for _ in range(4)] for _ in range(4)] for _ in range(2)]
    tmp_rot = Rot([cx.sb(st, [128, 512], F32, "rl") for _ in range(3)])
    psA = Rot(cx.banks[0:4])
    psB = Rot(cx.banks[4:7])
    w1v = w1.rearrange("(c p) n -> p c n", p=128)
    w2v = w2.rearrange("(c p) n -> p c n", p=128)
    NHB = 8
    for hb in range(NHB):
        hid = hids[hb % 2]
        htk = hid_tks[hb % 2]
        wa, watk = ws.load([w1v[:, :, hb * 512:(hb + 1) * 512]])
        wa3 = wa.rearrange("p (c n) -> p c n", c=NC8)
        for hc in range(4):
            for tg in range(4):
                sl = slice(tg * 512, (tg + 1) * 512)
                ps, pstk = psA.next()
                for c in range(NC8):
                    P.op("pe", lambda e, c=c, hc=hc, sl=sl, ps=ps, wa3=wa3: e.matmul(
                        ps, lhsT=wa3[:, c, hc * 128:(hc + 1) * 128], rhs=hT[:, c, sl],
                        start=(c == 0), stop=(c == NC8 - 1)),
                        reads=[watk, hTtk[c][tg]], writes=[pstk], signal=(c == NC8 - 1))
                tmp, ttk = tmp_rot.next()
                P.op("act", lambda e, ps=ps, tmp=tmp: e.activation(out=tmp, in_=ps, func=AF.Square),
                     reads=[pstk], writes=[ttk])
                P.op("dve", lambda e, ps=ps, tmp=tmp, hc=hc, sl=sl, hid=hid: e.scalar_tensor_tensor(
                    out=hid[:, hc, sl], in0=ps, scalar=0.0, in1=tmp, op0=ALU.is_gt, op1=ALU.mult),
                    reads=[pstk, ttk], writes=[htk[hc][tg]])
        wb, wbtk = ws.load([w2v[:, hb * 4:(hb + 1) * 4, :]])
        wb3 = wb.rearrange("p (c n) -> p c n", c=4)
        for oc in range(NC8):
            for tg in range(4):
                sl = slice(tg * 512, (tg + 1) * 512)
                ps, pstk = psB.next()
                for hc in range(4):
                    P.op("pe", lambda e, hc=hc, oc=oc, sl=sl, ps=ps, hid=hid, wb3=wb3: e.matmul(
                        ps, lhsT=wb3[:, hc, oc * 128:(oc + 1) * 128], rhs=hid[:, hc, sl],
                        start=(hc == 0), stop=(hc == 3)),
                        reads=[wbtk, htk[hc][tg]], writes=[pstk], signal=(hc == 3))
                P.op("dve", lambda e, oc=oc, sl=sl, ps=ps: e.tensor_tensor(
                    out=X[:, oc, sl], in0=ps, in1=X[:, oc, sl], op=ALU.add),
                    reads=[pstk, Xtk[oc][tg]], writes=[Xtk[oc][tg]])
    P.barrier()
    cx.release(mk)


def emit_ple(cx, X, Xtk, gcol, ctk, wg, wp, pT_dram, ones_bf):
    P = cx.P
    mk = cx.mark()
    st = None
    hT = cx.sb(st, [128, NC8, NT], BF16, "ple_hT")
    hTtk = [[Tk() for _ in range(4)] for _ in range(NC8)]
    sq_rot = Rot([cx.sb(st, [128, 512], BF16, "sq") for _ in range(4)])
    rstd_rot = Rot([cx.sb(st, [128, 512], F32, "rstd") for _ in range(2)])
    ps_stat = Rot([cx.banks[7]])
    emit_norm_resident(cx, X, Xtk, gcol, ctk, hT, hTtk, sq_rot, rstd_rot, ps_stat, ones_bf)
    ws = WStream(cx, st, 4096, nstage=2, nslot=3)
    pst = cx.sb(st, [128, 2, NT], F32, "pstg")
    pb = cx.sb(st, [128, 2, NT], BF16, "pbf")
    ptk = Tk()
    P.dma("sync", pst, pT_dram.rearrange("(c p) n -> p c n", p=128), writes=[ptk])
    P.op("pool", lambda e: e.tensor_copy(out=pb, in_=pst), reads=[ptk], writes=[ptk])
    wpb, wptk = ws.load([wp.rearrange("(c p) n -> p c n", p=128)])
    wp3 = wpb[:, 0:2048].rearrange("p (c n) -> p c n", c=2)
    gt_rot = Rot([cx.sb(st, [128, 512], F32, "gt") for _ in range(3)])
    psA = Rot(cx.banks[0:3])
    psB = Rot(cx.banks[3:6])
    wgv = wg.rearrange("(c p) n -> p c n", p=128)
    for half in range(2):
        wa, watk = ws.load([wgv[:, :, half * 512:(half + 1) * 512]])
        wa3 = wa.rearrange("p (c n) -> p c n", c=NC8)
        for o4 in range(4):
            oc = half * 4 + o4
            for tg in range(4):
                sl = slice(tg * 512, (tg + 1) * 512)
                ps, pstk = psA.next()
                for c in range(NC8):
                    P.op("pe", lambda e, c=c, o4=o4, sl=sl, ps=ps, wa3=wa3: e.matmul(
                        ps, lhsT=wa3[:, c, o4 * 128:(o4 + 1) * 128], rhs=hT[:, c, sl],
                        start=(c == 0), stop=(c == NC8 - 1)),
                        reads=[watk, hTtk[c][tg]], writes=[pstk], signal=(c == NC8 - 1))
                ps2, ps2tk = psB.next()
                for kc in range(2):
                    P.op("pe", lambda e, kc=kc, oc=oc, sl=sl, ps2=ps2: e.matmul(
                        ps2, lhsT=wp3[:, kc, oc * 128:(oc + 1) * 128], rhs=pb[:, kc, sl],
                        start=(kc == 0), stop=(kc == 1)),
                        reads=[wptk, ptk], writes=[ps2tk], signal=(kc == 1))
                gt, gtk = gt_rot.next()
                P.op("act", lambda e, ps=ps, gt=gt: e.activation(out=gt, in_=ps, func=AF.Sigmoid),
                     reads=[pstk], writes=[gtk])
                P.op("dve", lambda e, ps2=ps2, gt=gt: e.tensor_tensor(out=gt, in0=ps2, in1=gt, op=ALU.mult),
                     reads=[ps2tk, gtk], writes=[gtk])
                P.op("dve", lambda e, oc=oc, sl=sl, gt=gt: e.tensor_tensor(
                    out=X[:, oc, sl], in0=gt, in1=X[:, oc, sl], op=ALU.add),
                    reads=[gtk, Xtk[oc][tg]], writes=[Xtk[oc][tg]])
    P.barrier()
    cx.release(mk)


def alibi_slope(h):
    return 2.0 ** (-8.0 * (h + 1) / 16)


def sslice(start, count, step):
    return slice(start, start + (count - 1) * step + 1, step)


def emit_attention(cx, xT_ext, wqkv, wo, cf, cb, ctk, TOP, R1, lvl=9, hps=8):
    P = cx.P
    st = None
    ones_bf = cb[:, C_ONES:C_ONES + 128]
    bones = cb[:, C_BONES:C_BONES + 128]
    Dm = cf[:, C_DM:C_DM + 256]
    hT = TOP.bitcast(BF16).rearrange("p (c n) -> p c n", c=NC8)
    mk = cx.mark()
    sq_rot = Rot([cx.sb(st, [128, 512], BF16, "sq") for _ in range(2)])
    rstd_rot = Rot([cx.sb(st, [128, 512], F32, "rstd") for _ in range(2)])
    xstg = [R1[:, i * 4096:(i + 1) * 4096].rearrange("p (c n) -> p c n", c=NC8) for i in range(2)]
    xstk = [Tk(), Tk()]
    ps_stat = Rot([cx.banks[7], cx.banks[6]])
    httk = Tk()
    gcol = cf[:, G0_ANORM:G0_ANORM + 8]
    for tg in range(8 if lvl >= 1 else 0):
        xa = xstg[tg % 2]
        xt = xstk[tg % 2]
        P.dma("sync", xa, xT_ext[:, tg * 512:(tg + 1) * 512].rearrange("(c p) n -> p c n", p=128), writes=[xt])
        xs = [(xa[:, c, :], xt) for c in range(NC8)]
        ps_ap, ps_tk = ps_stat.next()
        rstd, rtk = rstd_rot.next()
        rms_stats(cx, xs, 512, sq_rot, ps_ap, ps_tk, rstd, rtk, ones_bf, ctk, 1.0 / D)
        for c in range(NC8):
            P.op("dve", lambda e, c=c, tg=tg, xa=xa, rstd=rstd: e.scalar_tensor_tensor(
                out=hT[:, c, tg * 512:(tg + 1) * 512], in0=xa[:, c, :], scalar=gcol[:, c:c + 1], in1=rstd[:, 0:512],
                op0=ALU.mult, op1=ALU.mult), reads=[xt, rtk, ctk], writes=[httk])
    P.op("dve", lambda e: e.tensor_scalar(out=cf[:, G0_QG:G0_QG + 3], in0=cf[:, G0_QG:G0_QG + 3], scalar1=0.125,
                                          scalar2=None, op0=ALU.mult), reads=[ctk], writes=[ctk])
    P.barrier()
    ACC = R1[:, 0:4096].rearrange("p (a n) -> p a n", a=2)
    oT = R1[:, 4096:12288].bitcast(BF16).rearrange("p (c n) -> p c n", c=NC8)
    acctk = Tk()
    ottk = Tk()
    ws = WStream(cx, st, 3072, nstage=1, nslot=2)
    QTz = [cx.sb(st, [128, NT], BF16, "QTz") for _ in range(2)]
    qtk = Tk()
    KT_rot = Rot([cx.sb(st, [128, 2 * NT], BF16, "KT") for _ in range(2)])
    Vz = [cx.sb(st, [128, 32, 128], BF16, "Vz") for _ in range(2)]
    vtk = Tk()
    for e2 in (0, 1):
        P.op("pool", lambda e, e2=e2: e.memset(QTz[e2], 0.0), writes=[qtk])
        P.op("pool", lambda e, e2=e2: e.memset(Vz[e2], 0.0), writes=[vtk])
    onesz = [cb[:, C_OZ:C_OZ + 128], cb[:, C_OZ + 128:C_OZ + 256]]
    honesz = [cb[:, C_HZ:C_HZ + 128], cb[:, C_HZ + 128:C_HZ + 256]]
    tmp_rot = Rot([cx.sb(st, [128, 256], F32, "stmp") for _ in range(4)])
    pt_rots = [Rot([cx.sb(st, [128, 256], BF16, "PT") for _ in range(6)]) for _ in range(2)]
    bk = [(cx.banks[i], Tk()) for i in range(8)]

    def half(i):
        return (bk[i][0][:, 0:256], bk[i][1])

    psQ = Rot([bk[0], bk[1], bk[4], bk[5]])
    psS = Rot([bk[2]])
    psV = Rot([bk[3], bk[6], bk[7]])
    psST0 = Rot([half(4), half(0)])
    psST1 = Rot([half(5), half(1)])
    psND = Rot([half(6), half(7), half(2), half(3)])
    wq_view = wqkv.rearrange("(c p) n -> p c n", p=128)

    def perm(ap2d, d):
        if d == 1:
            return ap2d
        return ap2d.rearrange("p (u r) -> p r u", r=d)

    def proj_piece(w3, wtk, j, e0, n, gain_col, out_buf, out_tk, d, Lx, u0):
        ps, pstk = psQ.next()
        for c in range(NC8):
            P.op("pe", lambda e, c=c, ps=ps: e.matmul(ps[:, 0:n], lhsT=w3[:, j, c, :], rhs=hT[:, c, e0:e0 + n],
                                                      start=(c == 0), stop=(c == NC8 - 1)),
                 reads=[wtk], writes=[pstk], signal=(c == NC8 - 1))
        ps2, ps2tk = psS.next()
        rstd, rtk = rstd_rot.next()
        rms_stats(cx, [(ps[:, 0:n], pstk)], n, sq_rot, ps2, ps2tk, rstd, rtk, bones, ctk, 1.0 / 64)
        outs = out_buf if isinstance(out_buf, list) else [(slice(0, 128), out_buf)]
        for (rows, ob) in outs:
            if d == 1:
                o = ob[rows, u0:u0 + n]
            else:
                o = ob[rows, 0:d * Lx].rearrange("p (r u) -> p r u", r=d)[:, :, u0:u0 + n // d]
            P.op("dve", lambda e, ps=ps, rstd=rstd, o=o, rows=rows: e.scalar_tensor_tensor(
                out=o, in0=perm(ps[rows, 0:n], d), scalar=gain_col[rows, :], in1=perm(rstd[rows, 0:n], d),
                op0=ALU.mult, op1=ALU.mult),
                reads=[pstk, rtk, ctk], writes=[out_tk])

    if lvl < 2:
        hps = 0
        P.op('dve', lambda e: e.memset(R1, 0.0), writes=[ottk])
    for hp in range(hps):
        for g, (W, d) in enumerate(A_GROUPS):
            L = NT // d
            Lk = (W + NT) // d
            nb = Lk // 128
            e_start = NT - W
            base = g * 3072 + hp * 128
            wb, wtk = ws.load([wq_view[:, :, base + j * 1024: base + j * 1024 + 128] for j in range(3)])
            w3 = wb[:, 0:3072].rearrange("p (j c n) -> p j c n", j=3, c=NC8)
            KT, ktk = KT_rot.next()
            for tg in range(4):
                proj_piece(w3, wtk, 0, NT + tg * 512, 512, cf[:, G0_QG + g:G0_QG + g + 1],
                           [(slice(0, 64), QTz[0]), (slice(64, 128), QTz[1])], qtk, d, L, tg * 512 // d)
            pieces = []
            if W < 512:
                pieces.append((e_start, W))
                e = NT
            else:
                e = e_start
            while e < 2 * NT:
                pieces.append((e, 512))
                e += 512
            for (e0, n) in pieces:
                proj_piece(w3, wtk, 1, e0, n, cf[:, G0_KG + g:G0_KG + g + 1], KT, ktk, d, Lk, (e0 - e_start) // d)
            nkb = d * nb if lvl >= 3 else 0
            kb = 0
            while kb < nkb:
                nblk = min(4, nkb - kb)
                psv, psvtk = psV.next()
                for b in range(nblk):
                    r, jb = divmod(kb + b, nb)
                    e_first = e_start + d * 128 * jb + r
                    for c in range(NC8):
                        P.op("pe", lambda e, c=c, b=b, e_first=e_first, psv=psv: e.matmul(
                            psv[:, b * 128:(b + 1) * 128], lhsT=hT[:, c, sslice(e_first, 128, d)], rhs=w3[:, 2, c, :],
                            start=(c == 0), stop=(c == NC8 - 1)),
                            reads=[wtk], writes=[psvtk], signal=(c == NC8 - 1 and b == nblk - 1))
                for e2 in (0, 1):
                    cs = slice(64 * e2, 64 * e2 + 64)
                    P.op("act", lambda e, kb=kb, nblk=nblk, psv=psv, e2=e2, cs=cs: e.activation(
                        out=Vz[e2][:, kb:kb + nblk, cs],
                        in_=psv[:, 0:nblk * 128].rearrange("p (b n) -> p b n", b=nblk)[:, :, cs], func=AF.Copy),
                        reads=[psvtk], writes=[vtk])
                kb += nblk
            PTs = {}

            def score_task(r, jb):
                lo = 128 if jb == 0 else 0
                hi = 128 if jb == nb - 1 else 256
                qb0 = jb if jb == 0 else jb - 1
                q_off = r * L + 128 * qb0
                sTs = [psST0.next(), psST1.next()]
                for e2 in (0, 1):
                    sT, sTtk = sTs[e2]
                    P.op("pe", lambda e, sT=sT, e2=e2: e.matmul(
                        sT[:, lo:hi], lhsT=KT[:, r * Lk + 128 * jb: r * Lk + 128 * jb + 128],
                        rhs=QTz[e2][:, q_off:q_off + (hi - lo)], start=True, stop=True),
                        reads=[ktk, qtk], writes=[sTtk])
                for e2 in (0, 1):
                    sig = alibi_slope(2 * hp + e2) * d
                    sT, sTtk = sTs[e2]
                    tmp, tmtk = tmp_rot.next()
                    P.op("dve", lambda e, sT=sT, tmp=tmp, sig=sig: e.scalar_tensor_tensor(
                        out=tmp[:, lo:hi], in0=Dm[:, lo:hi], scalar=-sig, in1=sT[:, lo:hi],
                        op0=ALU.mult, op1=ALU.add),
                        reads=[sTtk, ctk], writes=[tmtk])
                    pt, pttk = pt_rots[e2].next()
                    P.op("act", lambda e, tmp=tmp, pt=pt: e.activation(
                        out=pt[:, lo:hi], in_=tmp[:, lo:hi], func=AF.Exp), reads=[tmtk], writes=[pttk])
                    PTs[(e2, r, jb)] = (pt, pttk)

            def pv_task(r, j):
                jb = j + 1
                nd, ndtk = psND.next()
                kbp = r * nb + j
                kbd = r * nb + jb
                for part in (0, 1):
                    co = slice(128 * part, 128 * part + 128)
                    for e2 in (0, 1):
                        ptp, ptptk = PTs[(e2, r, j)]
                        ptd, ptdtk = PTs[(e2, r, jb)]
                        if part == 0:
                            lp, ld = Vz[e2][:, kbp, :], Vz[e2][:, kbd, :]
                        else:
                            lp, ld = (honesz[e2] if j == 0 else onesz[e2]), onesz[e2]
                        P.op("pe", lambda e, nd=nd, co=co, lp=lp, ptp=ptp, e2=e2: e.matmul(
                            nd[:, co], lhsT=lp, rhs=ptp[:, 128:256], start=(e2 == 0), stop=False),
                            reads=[vtk, ctk, ptptk], writes=[ndtk], signal=False)
                        P.op("pe", lambda e, nd=nd, co=co, ld=ld, ptd=ptd, e2=e2: e.matmul(
                            nd[:, co], lhsT=ld, rhs=ptd[:, 0:128], start=False, stop=(e2 == 1)),
                            reads=[vtk, ctk, ptdtk], writes=[ndtk], signal=(part == 1 and e2 == 1))
                t0 = r + d * 128 * j
                accv = ACC[:, :, sslice(t0, 128, d)]
                ndv = nd.rearrange("p (a n) -> p a n", a=2)
                if g == 0:
                    P.op("act", lambda e, accv=accv, ndv=ndv: e.activation(out=accv, in_=ndv, func=AF.Copy),
                         reads=[ndtk], writes=[acctk])
                else:
                    P.op("dve", lambda e, accv=accv, ndv=ndv: e.tensor_tensor(out=accv, in0=ndv, in1=accv, op=ALU.add),
                         reads=[ndtk, acctk], writes=[acctk])

            LA = 3
            pending = []
            tasks = [(r, jb) for r in range(d if lvl >= 4 else 0) for jb in range(nb)]
            for i, (r, jb) in enumerate(tasks):
                score_task(r, jb)
                if jb >= 1:
                    pending.append((i, r, jb - 1))
                while pending and pending[0][0] <= i - LA:
                    _, r_, j_ = pending.pop(0)
                    pv_task(r_, j_)
            for (_, r_, j_) in pending:
                pv_task(r_, j_)
        P.op("dve", lambda e: e.reciprocal(out=ACC[:, 1, :], in_=ACC[:, 1, :]), reads=[acctk], writes=[acctk])
        P.op("dve", lambda e, hp=hp: e.tensor_tensor(out=oT[:, hp, :], in0=ACC[:, 0, :], in1=ACC[:, 1, :], op=ALU.mult),
             reads=[acctk], writes=[ottk])
    P.barrier()
    cx.release(mk)
    mk = cx.mark()
    X = TOP.rearrange("p (c n) -> p c n", c=NC8)
    Xtk = [[Tk() for _ in range(4)] for _ in range(NC8)]
    for c in range(NC8):
        for tg in range(4):
            P.dma("sync", X[:, c, tg * 512:(tg + 1) * 512], xT_ext[c * 128:(c + 1) * 128, NT + tg * 512:NT + (tg + 1) * 512],
                  writes=[Xtk[c][tg]])
    ws2 = WStream(cx, st, 4096, nstage=2, nslot=2)
    wov = wo.rearrange("(c p) n -> p c n", p=128)
    psA = Rot(cx.banks[0:4])
    for half in range(2):
        wa, watk = ws2.load([wov[:, :, half * 512:(half + 1) * 512]])
        wa3 = wa.rearrange("p (c n) -> p c n", c=NC8)
        for o4 in range(4):
            oc = half * 4 + o4
            for tg in range(4):
                sl = slice(tg * 512, (tg + 1) * 512)
                ps, pstk = psA.next()
                for c in range(NC8):
                    P.op("pe", lambda e, c=c, o4=o4, sl=sl, ps=ps, wa3=wa3: e.matmul(
                        ps, lhsT=wa3[:, c, o4 * 128:(o4 + 1) * 128], rhs=oT[:, c, sl],
                        start=(c == 0), stop=(c == NC8 - 1)),
                        reads=[watk, ottk], writes=[pstk], signal=(c == NC8 - 1))
                P.op("dve", lambda e, oc=oc, sl=sl, ps=ps: e.tensor_tensor(
                    out=X[:, oc, sl], in0=ps, in1=X[:, oc, sl], op=ALU.add),
                    reads=[pstk, Xtk[oc][tg]], writes=[Xtk[oc][tg]])
    P.barrier()
    cx.release(mk)
    return X, Xtk


def emit_store(cx, X, Xtk, out_dram):
    P = cx.P
    for c in range(NC8):
        P.dma("sync", out_dram[c * 128:(c + 1) * 128, :], X[:, c, :], reads=Xtk[c])


def build_layer0(debug=False, lvl=9, hps=8):
    nc = bass.Bass("TRN2", target_bir_lowering=False)
    xT_ext = nc.dram_tensor("xT_ext", [D, 2 * NT], F32, kind="ExternalInput").ap()
    pT = nc.dram_tensor("pT", [256, NT], F32, kind="ExternalInput").ap()
    cst = nc.dram_tensor("cst", [128, G0_END], F32, kind="ExternalInput").ap()
    wqkv = nc.dram_tensor("a_w_qkv", [D, 9216], F32, kind="ExternalInput").ap()
    wo = nc.dram_tensor("a_w_o", [D, D], F32, kind="ExternalInput").ap()
    w1 = nc.dram_tensor("mlp_w1", [D, 4096], F32, kind="ExternalInput").ap()
    w2 = nc.dram_tensor("mlp_w2", [4096, D], F32, kind="ExternalInput").ap()
    wg = nc.dram_tensor("ple_w_gate", [D, D], F32, kind="ExternalInput").ap()
    wp = nc.dram_tensor("ple_w_proj", [256, D], F32, kind="ExternalInput").ap()
    out = nc.dram_tensor("xout", [D, NT], F32, kind="ExternalOutput").ap()
    if debug:
        dbg_a = nc.dram_tensor("dbg_a", [D, NT], F32, kind="ExternalOutput").ap()
        dbg_m = nc.dram_tensor("dbg_m", [D, NT], F32, kind="ExternalOutput").ap()
    cx = Ctx(nc)
    P = cx.P
    cf, cb, ctk = load_consts(cx, None, cst, G0_END)
    cx.eps_col = cf[:, C_EPS:C_EPS + 1]
    ones_bf = cb[:, C_ONES:C_ONES + 128]
    TOP = cx.sb(None, [128, 16384], F32, "TOP")
    R1 = cx.sb(None, [128, 12288], F32, "R1")
    mk = cx.mark()
    X, Xtk = emit_attention(cx, xT_ext, wqkv, wo, cf, cb, ctk, TOP, R1, lvl=lvl, hps=hps)
    cx.release(mk)
    cx.top = cx.top - 12288 * 4
    if debug:
        emit_store(cx, X, Xtk, dbg_a)
    emit_mlp(cx, X, Xtk, cf[:, G0_MLPN:G0_MLPN + 8], ctk, w1, w2, ones_bf)
    if debug:
        emit_store(cx, X, Xtk, dbg_m)
    emit_ple(cx, X, Xtk, cf[:, G0_PLEN:G0_PLEN + 8], ctk, wg, wp, pT, ones_bf)
    emit_store(cx, X, Xtk, out)
    P.finish()
    return nc, cx


def layer0_inputs(inputs, core):
    x = inputs["x"][0]
    lo = core * NT
    xe = np.zeros((2 * NT, D), np.float32)
    if core > 0:
        xe[:NT] = x[lo - NT:lo]
    xe[NT:] = x[lo:lo + NT]
    c = base_consts(core, G0_END)
    c[:, G0_ANORM:G0_ANORM + 8] = col_layout(inputs["a_norm"][0])
    c[:, G0_QG:G0_QG + 3] = np.tile(inputs["a_q_gain"][0].T, (2, 1))
    c[:, G0_KG:G0_KG + 3] = np.tile(inputs["a_k_gain"][0].T, (2, 1))
    c[:, G0_MLPN:G0_MLPN + 8] = col_layout(inputs["mlp_norm"][0])
    c[:, G0_PLEN:G0_PLEN + 8] = col_layout(inputs["ple_norm"][0])
    return {
        "xT_ext": np.ascontiguousarray(xe.T),
        "pT": np.ascontiguousarray(inputs["p"][0, 0, lo:lo + NT].T),
        "cst": c,
        "a_w_qkv": inputs["a_w_qkv"][0], "a_w_o": inputs["a_w_o"][0],
        "mlp_w1": inputs["mlp_w1"][0], "mlp_w2": inputs["mlp_w2"][0],
        "ple_w_gate": inputs["ple_w_gate"][0], "ple_w_proj": inputs["ple_w_proj"][0],
    }

BH = 4
DH = 512
NCK = NT // 128
KSCALE = DH ** -0.5
NST = 8192 + 2048 + 8

L_BNORM = C_GAINS
L_MLPN = L_BNORM + 8
L_PLEN = L_MLPN + 8
L_CONVW = L_PLEN + 8
L_CONVB = L_CONVW + 64
L_SKIP = L_CONVB + 16
L_HGAIN = L_SKIP + 16
L_BI = L_HGAIN + 16
L_BF = L_BI + 1
L_MASKLOW = L_BF + 1
L_SEL = L_MASKLOW + 128
L_CMASK = L_SEL + 512
L_NEG = L_CMASK + 7
L_E0 = L_NEG + 1
L_LNK = L_E0 + 1
L_CNEG = L_LNK + 1
L_END = L_CNEG + 7


def layer1_consts(inputs, core):
    c = base_consts(core, L_END)
    c[:, L_BNORM:L_BNORM + 8] = col_layout(inputs["b_norm"][0])
    c[:, L_MLPN:L_MLPN + 8] = col_layout(inputs["mlp_norm"][1])
    c[:, L_PLEN:L_PLEN + 8] = col_layout(inputs["ple_norm"][1])
    cw = inputs["b_conv_w"][0]
    c[:, L_CONVW:L_CONVW + 64] = cw.reshape(4, 16, 128).transpose(2, 1, 0).reshape(128, 64)
    c[:, L_CONVB:L_CONVB + 16] = col_layout(inputs["b_conv_b"][0])
    c[:, L_SKIP:L_SKIP + 16] = col_layout(inputs["b_skip"][0])
    c[:, L_HGAIN:L_HGAIN + 16] = col_layout(inputs["b_h_gain"][0])
    bg = inputs["b_b_gate"][0]
    c[0:4, L_BI] = bg[0:4]
    c[0:4, L_BF] = bg[4:8]
    s_ = np.arange(128)[:, None]
    t_ = np.arange(128)[None, :]
    c[:, L_MASKLOW:L_MASKLOW + 128] = np.where(s_ <= t_, 0.0, BIG)
    for hd in range(4):
        c[hd, L_SEL + hd * 128:L_SEL + (hd + 1) * 128] = 1.0
    for cp in range(7):
        c[:, L_CMASK + cp] = 1.0 if cp < core else 0.0
        c[:, L_CNEG + cp] = 0.0 if cp < core else -1e30
    c[:, L_NEG] = -1e30
    c[0, L_E0] = 1.0
    c[:, L_LNK] = np.log(KSCALE)
    return c


def bd_compact(w, transpose=False):
    out = np.zeros((2048, 128), np.float32)
    n = np.arange(512)
    for j in range(4):
        for k in range(4):
            if transpose:
                out[4 * n + k, (4 * n + j) % 128] = w[:, j, k]
            else:
                out[4 * n + j, (4 * n + k) % 128] = w[:, j, k]
    return out


def layer1_inputs(inputs, core, x1T_full, stage, st_all=None, g_in=None):
    lo = core * NT
    xh = np.zeros((D, 4), np.float32)
    if core > 0:
        xh[:, 1:4] = x1T_full[:, lo - 3:lo]
    m = {
        "x1T": np.ascontiguousarray(x1T_full[:, lo:lo + NT]),
        "xh": xh,
        "cst": layer1_consts(inputs, core),
        "b_w_up": inputs["b_w_up"][0],
        "bd": np.stack([bd_compact(inputs["b_w_q"][0]), bd_compact(inputs["b_w_k"][0]), bd_compact(inputs["b_w_v"][0])]),
        "bdT": np.stack([bd_compact(inputs["b_w_q"][0], True), bd_compact(inputs["b_w_k"][0], True),
                         bd_compact(inputs["b_w_v"][0], True)]),
        "w_gate": inputs["b_w_gate"][0],
    }
    if stage == "C":
        m.update({
            "st_all": st_all,
            "g_in": g_in,
            "b_w_down": inputs["b_w_down"][0],
            "pT": np.ascontiguousarray(inputs["p"][1, 0, lo:lo + NT].T),
            "mlp_w1": inputs["mlp_w1"][1], "mlp_w2": inputs["mlp_w2"][1],
            "ple_w_gate": inputs["ple_w_gate"][1], "ple_w_proj": inputs["ple_w_proj"][1],
        })
    return m


def build_layer1(stage, debug=False, dbg_stop=None):
    nc = bass.Bass("TRN2", target_bir_lowering=False)
    x1T = nc.dram_tensor("x1T", [D, NT], F32, kind="ExternalInput").ap()
    xh = nc.dram_tensor("xh", [D, 4], F32, kind="ExternalInput").ap()
    cst = nc.dram_tensor("cst", [128, L_END], F32, kind="ExternalInput").ap()
    wup = nc.dram_tensor("b_w_up", [D, 4096], F32, kind="ExternalInput").ap()
    bd = nc.dram_tensor("bd", [3, 2048, 128], F32, kind="ExternalInput").ap()
    bdT = nc.dram_tensor("bdT", [3, 2048, 128], F32, kind="ExternalInput").ap()
    wgate = nc.dram_tensor("w_gate", [6144, 8], F32, kind="ExternalInput").ap()
    if stage == "B":
        st_out = nc.dram_tensor("st_out", [128, NST], F32, kind="ExternalOutput").ap()
        g_out = nc.dram_tensor("g_out", [8, NT], F32, kind="ExternalOutput").ap()
    else:
        st_all = nc.dram_tensor("st_all", [7, 128, NST], F32, kind="ExternalInput").ap()
        g_in = nc.dram_tensor("g_in", [8, NT], F32, kind="ExternalInput").ap()
        wdown = nc.dram_tensor("b_w_down", [2048, D], F32, kind="ExternalInput").ap()
        pT = nc.dram_tensor("pT", [256, NT], F32, kind="ExternalInput").ap()
        w1 = nc.dram_tensor("mlp_w1", [D, 4096], F32, kind="ExternalInput").ap()
        w2 = nc.dram_tensor("mlp_w2", [4096, D], F32, kind="ExternalInput").ap()
        wg = nc.dram_tensor("ple_w_gate", [D, D], F32, kind="ExternalInput").ap()
        wp = nc.dram_tensor("ple_w_proj", [256, D], F32, kind="ExternalInput").ap()
        out = nc.dram_tensor("xout", [D, NT], F32, kind="ExternalOutput").ap()
        yscr = nc.dram_tensor("yscr", [2048, NT], BF16).ap()
        if debug:
            dbg_a = nc.dram_tensor("dbg_a", [D, NT], F32, kind="ExternalOutput").ap()
    cx = Ctx(nc)
    P = cx.P
    cf, cb, ctk = load_consts(cx, None, cst, L_END)
    cx.eps_col = cf[:, C_EPS:C_EPS + 1]
    ones_bf = cb[:, C_ONES:C_ONES + 128]
    ones_f = cf[:, C_ONES:C_ONES + 128]
    ident_f = cf[:, C_ID:C_ID + 128]
    one_col = cf[:, C_ONES:C_ONES + 1]
    TOP = cx.sb(None, [128, 16384], F32, "TOP")
    hT = TOP[:, 0:8208].bitcast(BF16)[:, 0:8 * 2052].rearrange("p (c n) -> p c n", c=NC8)
    topfree = TOP[:, 8208:16384]
    base_mark = cx.mark()
    bk = [(cx.banks[i], Tk()) for i in range(8)]
    ws = WStream(cx, None, 4096, nstage=0, nslot=3)
    ws.stage = Rot([topfree[:, 0:4096]])
    bdh = cx.sb(None, [128, 3, 4, 128], BF16, "bdh")
    bdh_st = cx.sb(None, [128, 3, 4, 128], F32, "bdh_st")
    diag = cx.sb(None, [128, 4, 4, 128], BF16, "diag")
    xms = [cx.sb(None, [128, 4, 516], BF16, "xm") for _ in range(2)]
    xc = cx.sb(None, [128, 4, 512], BF16, "xc")
    GF = cx.sb(None, [128, NT], F32, "GF")
    BETAx = cx.sb(None, [128, NT + 1], F32, "BETAx")
    small = cx.sb(None, [128, 64], F32, "small")
    TMw = cx.sb(None, [128, NCK, 4], F32, "TMw")
    TMa = cx.sb(None, [128, NCK, 4], F32, "TMa")
    if stage == "C":
        fold = cx.sb(None, [128, 7, 8], F32, "foldin")
        S1 = cx.sb(None, [128, 7, 4], F32, "S1")
        S2 = cx.sb(None, [128, 7, 4], F32, "S2")
        mrun = cx.sb(None, [128, 4], F32, "mrun")
        fa = cx.sb(None, [128, 4], F32, "fa")
        fb = cx.sb(None, [128, 4], F32, "fb")
        fc_ = cx.sb(None, [128, 4], F32, "fc")
    pers_mark = cx.mark()

    mk = cx.mark()
    sq_rot = Rot([cx.sb(None, [128, 512], BF16, "sq") for _ in range(2)])
    rstd_rot = Rot([cx.sb(None, [128, 512], F32, "rstd") for _ in range(2)])
    xstg = [cx.sb(None, [128, NC8, 512], F32, "xstg") for _ in range(2)]
    xstk = [Tk(), Tk()]
    ps_stat = Rot([bk[7], bk[6]])
    gcol = cf[:, L_BNORM:L_BNORM + 8]
    httk = Tk()
    pieces = [(None, 4)] + [(tg, 512) for tg in range(4)]
    for i, (tg, n) in enumerate(pieces):
        xa = xstg[i % 2]
        xt = xstk[i % 2]
        if tg is None:
            P.dma("sync", xa[:, :, 0:4], xh.rearrange("(c p) n -> p c n", p=128), writes=[xt])
            h0 = 0
        else:
            P.dma("sync", xa, x1T[:, tg * 512:(tg + 1) * 512].rearrange("(c p) n -> p c n", p=128), writes=[xt])
            h0 = 4 + tg * 512
        xs = [(xa[:, c, 0:n], xt) for c in range(NC8)]
        ps_ap, ps_tk = ps_stat.next()
        rstd, rtk = rstd_rot.next()
        rms_stats(cx, xs, n, sq_rot, ps_ap, ps_tk, rstd, rtk, ones_bf, ctk, 1.0 / D)
        for c in range(NC8):
            P.op("dve", lambda e, c=c, xa=xa, rstd=rstd, n=n, h0=h0: e.scalar_tensor_tensor(
                out=hT[:, c, h0:h0 + n], in0=xa[:, c, 0:n], scalar=gcol[:, c:c + 1], in1=rstd[:, 0:n],
                op0=ALU.mult, op1=ALU.mult), reads=[xt, rtk, ctk], writes=[httk])
    P.barrier()
    cx.release(mk)

    GI = cx.sb(None, [128, NT], F32, "GI")
    LF = cx.sb(None, [128, NT], F32, "LF")
    BB = cx.sb(None, [128, NT], F32, "BB")
    T1 = cx.sb(None, [128, NT], F32, "T1")
    wfold = [[cx.sb(None, [128, 16, 128], BF16, "wfold") for _ in range(2)] for _ in range(2)]
    wftk = Tk()
    bdT_sb = topfree[:, 0:6144].rearrange("p (a b) -> p a b", a=48)
    wg_sb = topfree[:, 6144:6528].rearrange("p (a b) -> p a b", a=48)
    btk = Tk()
    for j in range(3 if stage == "B" else 0):
        P.dma("sync", bdT_sb[:, j * 16:(j + 1) * 16, :], bdT[j].rearrange("(c p) n -> p c n", p=128), writes=[btk])
    if stage == "B":
        P.dma("sync", wg_sb, wgate.rearrange("(c p) n -> p c n", p=128), writes=[btk])
    zpad = Rot([topfree[:, 6528 + i * 128:6528 + (i + 1) * 128] for i in range(4)])
    for (za, ztk) in zpad.items:
        P.op("pool", lambda e, za=za: e.memset(za, 0.0), writes=[ztk])
    psF = Rot(bk[0:2])
    for mc in range(16 if stage == "B" else 0):
        for part in range(2):
            for xm_ in range(2):
                ps, pstk = psF.next()
                srcs = (0, 1) if xm_ == 0 else (2,)
                for si, j in enumerate(srcs):
                    za, ztk = zpad.next()
                    P.op("dve", lambda e, za=za, j=j, mc=mc, part=part: e.tensor_copy(
                        out=za[:, 0:4], in_=wg_sb[:, j * 16 + mc, part * 4:part * 4 + 4]), reads=[btk], writes=[ztk])
                    P.op("pe", lambda e, ps=ps, za=za, j=j, mc=mc, si=si, srcs=srcs: e.matmul(
                        ps[:, 0:128], lhsT=bdT_sb[:, j * 16 + mc, :], rhs=za, start=(si == 0), stop=(si == len(srcs) - 1)),
                        reads=[btk, ztk], writes=[pstk])
                P.op("act", lambda e, ps=ps, xm_=xm_, part=part, mc=mc: e.activation(
                    out=wfold[xm_][part][:, mc, :], in_=ps[:, 0:128], func=AF.Copy), reads=[pstk], writes=[wftk])
    P.barrier()

    hdtk = Tk()
    xmtk = [Tk(), Tk()]
    xctk = Tk()
    psA = Rot(bk[0:2])
    psA2 = psA
    wupv = wup.rearrange("(c p) n -> p c n", p=128)
    bdv = bd.rearrange("j (c p) n -> p j c n", p=128)
    state = {"i": 0}

    def head_setup(hd):
        for j in range(3):
            P.dma("sync", bdh_st[:, j, :, :], bdv[:, j, hd * 4:(hd + 1) * 4, :], writes=[hdtk])
        P.op("pool", lambda e: e.tensor_copy(out=bdh, in_=bdh_st), reads=[hdtk], writes=[hdtk])
        for mc in range(4):
            for k in range(4):
                col = L_CONVW + (hd * 4 + mc) * 4 + k
                P.op("act", lambda e, mc=mc, k=k, col=col: e.activation(
                    out=diag[:, mc, k, :], in_=ident_f, func=AF.Copy, scale=cf[:, col:col + 1]),
                    reads=[ctk], writes=[hdtk])
        wx, wxtk = ws.load([wupv[:, :, hd * 512:(hd + 1) * 512]])
        return wx.rearrange("p (c n) -> p c n", c=NC8), wxtk

    def front(hd, tg, wx3, wxtk, psA=None):
        psA = psA or psA2
        i = state["i"]
        state["i"] += 1
        xm, xmt = xms[i % 2], xmtk[i % 2]
        xmp, xmpt = xms[(i + 1) % 2], xmtk[(i + 1) % 2]
        for mc in range(4):
            ps, pstk = psA.next()
            for c in range(NC8):
                P.op("pe", lambda e, c=c, mc=mc, ps=ps: e.matmul(
                    ps, lhsT=wx3[:, c, mc * 128:(mc + 1) * 128], rhs=hT[:, c, 4 + tg * 512:4 + (tg + 1) * 512],
                    start=(c == 0), stop=(c == NC8 - 1)), reads=[wxtk], writes=[pstk], signal=(c == NC8 - 1))
            P.op("act", lambda e, ps=ps, mc=mc, xm=xm: e.activation(out=xm[:, mc, 4:516], in_=ps, func=AF.Copy),
                 reads=[pstk], writes=[xmt])
            if tg == 0:
                ps, pstk = psA.next()
                for c in range(NC8):
                    P.op("pe", lambda e, c=c, mc=mc, ps=ps: e.matmul(
                        ps[:, 0:4], lhsT=wx3[:, c, mc * 128:(mc + 1) * 128], rhs=hT[:, c, 0:4],
                        start=(c == 0), stop=(c == NC8 - 1)), reads=[wxtk], writes=[pstk], signal=(c == NC8 - 1))
                P.op("act", lambda e, ps=ps, mc=mc, xm=xm: e.activation(out=xm[:, mc, 0:4], in_=ps[:, 0:4], func=AF.Copy),
                     reads=[pstk], writes=[xmt])
        if tg > 0:
            P.op("pool", lambda e, xm=xm, xmp=xmp: e.tensor_copy(out=xm[:, :, 0:4], in_=xmp[:, :, 512:516]),
                 reads=[xmpt], writes=[xmt])
        for mc in range(4):
            ps, pstk = psA.next()
            for k in range(4):
                P.op("pe", lambda e, k=k, mc=mc, ps=ps, xm=xm: e.matmul(
                    ps, lhsT=diag[:, mc, k, :], rhs=xm[:, mc, 1 + k:1 + k + 512], start=(k == 0), stop=(k == 3)),
                    reads=[hdtk, xmt], writes=[pstk], signal=(k == 3))
            col = L_CONVB + hd * 4 + mc
            P.op("act", lambda e, ps=ps, mc=mc, col=col: e.activation(
                out=xc[:, mc, :], in_=ps, func=AF.Silu, bias=cf[:, col:col + 1]), reads=[pstk, ctk], writes=[xctk])
        return xm, xmt

    gtk = Tk()
    psG = Rot(bk[2:4])
    psFG = Rot([bk[0], bk[1], bk[4], bk[5], bk[6], bk[7]])
    psFS = Rot([bk[0], bk[1], bk[2]])
    psFC = Rot([bk[0], bk[1], bk[2], bk[3]])
    if stage == "C":
        P.op("pool", lambda e: e.memset(GI, 0.0), writes=[gtk])
        P.op("pool", lambda e: e.memset(GF, 0.0), writes=[gtk])
        P.dma("sync", GI[0:4, :], g_in[0:4, :], writes=[gtk])
        P.dma("sync", GF[0:4, :], g_in[4:8, :], writes=[gtk])
    for hd in range(BH if stage == "B" else 0):
        wx3, wxtk = head_setup(hd)
        for tg in range(4):
            xm, xmt = front(hd, tg, wx3, wxtk, psFG)
            for part, Grow in ((0, GI), (1, GF)):
                ps, pstk = psG.next()
                for mc in range(4):
                    P.op("pe", lambda e, ps=ps, mc=mc, part=part: e.matmul(
                        ps, lhsT=wfold[0][part][:, hd * 4 + mc, :], rhs=xc[:, mc, :], start=(mc == 0), stop=False),
                        reads=[wftk, xctk], writes=[pstk], signal=False)
                    P.op("pe", lambda e, ps=ps, mc=mc, part=part, xm=xm: e.matmul(
                        ps, lhsT=wfold[1][part][:, hd * 4 + mc, :], rhs=xm[:, mc, 4:516], start=False, stop=(mc == 3)),
                        reads=[wftk, xmt], writes=[pstk], signal=(mc == 3))
                sl = slice(tg * 512, (tg + 1) * 512)
                if hd == 0:
                    P.op("act", lambda e, ps=ps, Grow=Grow, sl=sl: e.activation(out=Grow[:, sl], in_=ps, func=AF.Copy),
                         reads=[pstk], writes=[gtk])
                else:
                    P.op("dve", lambda e, ps=ps, Grow=Grow, sl=sl: e.tensor_tensor(out=Grow[:, sl], in0=ps, in1=Grow[:, sl], op=ALU.add),
                         reads=[pstk, gtk], writes=[gtk])

    rtk = Tk()
    if stage == "B":
        P.dma("sync", g_out[0:4, :], GI[0:4, :], reads=[gtk])
        P.dma("sync", g_out[4:8, :], GF[0:4, :], reads=[gtk])
    P.op("dve", lambda e: e.tensor_scalar(out=GI, in0=GI, scalar1=cf[:, L_BI:L_BI + 1], scalar2=None, op0=ALU.add),
         reads=[gtk, ctk], writes=[gtk])
    P.op("dve", lambda e: e.tensor_scalar(out=GF, in0=GF, scalar1=cf[:, L_BF:L_BF + 1], scalar2=None, op0=ALU.add),
         reads=[gtk, ctk], writes=[gtk])
    P.op("dve", lambda e: e.tensor_scalar(out=T1, in0=GF, scalar1=-1.0, scalar2=None, op0=ALU.mult), reads=[gtk], writes=[rtk])
    P.op("dve", lambda e: e.tensor_tensor(out=T1, in0=T1, in1=GF, op=ALU.max), reads=[gtk, rtk], writes=[rtk])
    P.op("act", lambda e: e.activation(out=T1, in_=T1, func=AF.Exp, scale=-1.0), reads=[rtk], writes=[rtk])
    P.op("act", lambda e: e.activation(out=T1, in_=T1, func=AF.Ln, bias=one_col), reads=[rtk, ctk], writes=[rtk])
    P.op("dve", lambda e: e.scalar_tensor_tensor(out=LF, in0=GF, scalar=0.0, in1=T1, op0=ALU.min, op1=ALU.subtract),
         reads=[gtk, rtk], writes=[rtk])
    P.op("pool", lambda e: e.memset(T1, 1.0), reads=[rtk], writes=[rtk])
    P.op("dve", lambda e: e.tensor_tensor_scan(out=BB, data0=T1, data1=LF, initial=0.0, op0=ALU.mult, op1=ALU.add),
         reads=[rtk], writes=[rtk])
    P.op("dve", lambda e: e.tensor_tensor(out=T1, in0=GI, in1=BB, op=ALU.subtract), reads=[gtk, rtk], writes=[rtk])
    ALPHA = T1
    psR = Rot([bk[4]])
    tmtk = Tk()

    def to_token_major(row, dst):
        ps, pstk = psR.next()
        for ck in range(NCK):
            P.op("pe", lambda e, ps=ps, ck=ck: e.matmul(ps[:, ck * 4:ck * 4 + 4], lhsT=row[:, ck * 128:(ck + 1) * 128],
                                                        rhs=ident_f[:, 0:4], start=True, stop=True),
                 reads=[rtk, gtk, ctk], writes=[pstk], signal=(ck == NCK - 1))
        P.op("act", lambda e, ps=ps: e.activation(out=dst, in_=ps[:, 0:64].rearrange("p (a b) -> p a b", a=NCK), func=AF.Copy),
             reads=[pstk], writes=[tmtk])

    def replicate_cols(col_ap, dst4):
        ps, pstk = psR.next()
        za = small[:, 32:32 + 4]
        P.op("dve", lambda e: e.tensor_scalar(out=za, in0=ident_f[:, 0:4], scalar1=col_ap, scalar2=None, op0=ALU.mult),
             reads=[rtk, ctk, gtk], writes=[rtk])
        P.op("pe", lambda e, ps=ps: e.matmul(ps[:, 0:4], lhsT=ones_f, rhs=za, start=True, stop=True),
             reads=[rtk, ctk], writes=[pstk])
        P.op("act", lambda e, ps=ps: e.activation(out=dst4, in_=ps[:, 0:4], func=AF.Copy), reads=[pstk], writes=[rtk])

    if stage == "B":
        mx = small[:, 0:1]
        P.op("dve", lambda e: e.tensor_reduce(out=mx, in_=ALPHA, axis=AX.X, op=ALU.max), reads=[rtk], writes=[rtk])
        nb_ = small[:, 1:2]
        P.op("dve", lambda e: e.scalar_tensor_tensor(out=nb_, in0=mx, scalar=-1.0, in1=cf[:, L_LNK:L_LNK + 1],
                                                     op0=ALU.mult, op1=ALU.add), reads=[rtk, ctk], writes=[rtk])
        P.op("act", lambda e: e.activation(out=LF, in_=ALPHA, func=AF.Exp, bias=nb_), reads=[rtk], writes=[rtk])
        to_token_major(LF, TMw)
        ml = small[:, 2:3]
        P.op("dve", lambda e: e.tensor_tensor(out=ml, in0=mx, in1=BB[:, NT - 1:NT], op=ALU.add), reads=[rtk], writes=[rtk])
        fin = cx.sb(None, [128, 8], F32, "fin")
        replicate_cols(BB[:, NT - 1:NT], fin[:, 0:4])
        replicate_cols(ml, fin[:, 4:8])
        P.dma("sync", st_out[:, 10240:10248], fin, reads=[rtk])
        P.barrier()
        cx.release(pers_mark)
        kv_rot = Rot([cx.sb(None, [128, 512], BF16, "kv") for _ in range(4)])
        stC = cx.sb(None, [128, 4, 512], F32, "stC")
        stn = cx.sb(None, [128, 512], F32, "stn")
        sttk = Tk()
        psKV = Rot([bk[0], bk[1], bk[2]])
        for hd in range(BH):
            wx3, wxtk = head_setup(hd)
            cacc = [bk[3 + dc] for dc in range(4)]
            nacc, nacctk = bk[7]
            for tg in range(4):
                xm, xmt = front(hd, tg, wx3, wxtk, psFS)
                KV = {}

                def s1_pre(cl):
                    ck = tg * 4 + cl
                    tsl = slice(cl * 128, (cl + 1) * 128)
                    ps, pstk = psKV.next()
                    for mc in range(4):
                        P.op("pe", lambda e, ps=ps, mc=mc, tsl=tsl: e.matmul(
                            ps[:, mc * 128:(mc + 1) * 128], lhsT=xc[:, mc, tsl], rhs=bdh[:, 1, mc, :], start=True, stop=True),
                            reads=[xctk, hdtk], writes=[pstk], signal=(mc == 3))
                    wk, wktk = kv_rot.next()
                    P.op("act", lambda e, ps=ps, wk=wk, ck=ck, hd=hd: e.activation(
                        out=wk, in_=ps, func=AF.Copy, scale=TMw[:, ck, hd:hd + 1]), reads=[pstk, tmtk], writes=[wktk])
                    ps, pstk = psKV.next()
                    for mc in range(4):
                        P.op("pe", lambda e, ps=ps, mc=mc, cl=cl, xm=xm: e.matmul(
                            ps[:, mc * 128:(mc + 1) * 128], lhsT=xm[:, mc, 4 + cl * 128:4 + (cl + 1) * 128], rhs=bdh[:, 2, mc, :],
                            start=True, stop=True), reads=[xmt, hdtk], writes=[pstk], signal=(mc == 3))
                    vv, vtk = kv_rot.next()
                    P.op("act", lambda e, ps=ps, vv=vv: e.activation(out=vv, in_=ps, func=AF.Copy), reads=[pstk], writes=[vtk])
                    KV[cl] = (ck, wk, wktk, vv, vtk)

                def s1_acc(cl):
                    ck, wk, wktk, vv, vtk = KV[cl]
                    last = (ck == NCK - 1)
                    for dc in range(4):
                        P.op("pe", lambda e, dc=dc, wk=wk, vv=vv, ck=ck, last=last: e.matmul(
                            cacc[dc][0], lhsT=wk[:, dc * 128:(dc + 1) * 128], rhs=vv, start=(ck == 0), stop=last),
                            reads=[wktk, vtk], writes=[cacc[dc][1]], signal=True)
                    P.op("pe", lambda e, wk=wk, ck=ck, last=last: e.matmul(
                        nacc, lhsT=ones_bf, rhs=wk, start=(ck == 0), stop=last), reads=[wktk, ctk], writes=[nacctk], signal=True)
                s1_pre(0)
                for cl in range(1, 4):
                    s1_pre(cl)
                    s1_acc(cl - 1)
                s1_acc(3)
            for dc in range(4):
                P.op("act", lambda e, dc=dc: e.activation(out=stC[:, dc, :], in_=cacc[dc][0], func=AF.Copy),
                     reads=[cacc[dc][1]], writes=[sttk])
            P.op("dve", lambda e: e.tensor_copy(out=stn, in_=nacc), reads=[nacctk], writes=[sttk])
            P.dma("sync", st_out[:, hd * 2048:(hd + 1) * 2048], stC.rearrange("p a b -> p (a b)"), reads=[sttk])
            P.dma("sync", st_out[:, 8192 + hd * 512:8192 + (hd + 1) * 512], stn, reads=[sttk])
        P.finish()
        return nc, cx

    ftk = Tk()
    P.dma("sync", fold, st_all[:, :, 10240:10248].rearrange("c p n -> p c n"), writes=[ftk])
    negc = cf[:, L_NEG:L_NEG + 1]
    P.op("dve", lambda e: e.memset(mrun, -1e30), writes=[ftk])
    for cp in range(7):
        mu = cf[:, L_CMASK + cp:L_CMASK + cp + 1]
        P.op("dve", lambda e, cp=cp, mu=mu: e.scalar_tensor_tensor(out=fa, in0=fold[:, cp, 0:4], scalar=mu, in1=mrun,
                                                                    op0=ALU.mult, op1=ALU.add), reads=[ftk, ctk], writes=[ftk])
        P.op("dve", lambda e, cp=cp, mu=mu: e.tensor_scalar(out=fb, in0=fold[:, cp, 4:8], scalar1=mu,
                                                            scalar2=cf[:, L_CNEG + cp:L_CNEG + cp + 1], op0=ALU.mult, op1=ALU.add),
             reads=[ftk, ctk], writes=[ftk])
        P.op("dve", lambda e: e.tensor_tensor(out=fc_, in0=fa, in1=fb, op=ALU.max), reads=[ftk], writes=[ftk])
        P.op("dve", lambda e: e.tensor_tensor(out=fa, in0=fa, in1=fc_, op=ALU.subtract), reads=[ftk], writes=[ftk])
        P.op("dve", lambda e: e.tensor_tensor(out=fb, in0=fb, in1=fc_, op=ALU.subtract), reads=[ftk], writes=[ftk])
        P.op("act", lambda e, cp=cp: e.activation(out=S1[:, cp, :], in_=fa, func=AF.Exp), reads=[ftk], writes=[ftk])
        P.op("act", lambda e: e.activation(out=fb, in_=fb, func=AF.Exp), reads=[ftk], writes=[ftk])
        P.op("dve", lambda e, cp=cp, mu=mu: e.tensor_scalar(out=S2[:, cp, :], in0=fb, scalar1=mu, scalar2=None, op0=ALU.mult),
             reads=[ftk, ctk], writes=[ftk])
        P.op("dve", lambda e: e.tensor_copy(out=mrun, in_=fc_), reads=[ftk], writes=[ftk])
    mst = small[:, 4:5]
    P.op("dve", lambda e: e.tensor_tensor(out=small[:, 8:12], in0=mrun, in1=ident_f[:, 0:4], op=ALU.mult), reads=[ftk, ctk], writes=[rtk])
    P.op("dve", lambda e: e.tensor_reduce(out=mst, in_=small[:, 8:12], axis=AX.X, op=ALU.add), reads=[rtk], writes=[rtk])
    P.op("dve", lambda e: e.tensor_tensor_scan(out=GF, data0=LF, data1=GI, initial=mst, op0=ALU.add, op1=ALU.max),
         reads=[rtk, gtk], writes=[gtk])
    MM = GF
    P.op("dve", lambda e: e.tensor_tensor(out=BETAx[:, 1:NT + 1], in0=MM, in1=BB, op=ALU.subtract), reads=[gtk, rtk], writes=[rtk])
    P.op("dve", lambda e: e.tensor_copy(out=BETAx[:, 0:1], in_=mst), reads=[rtk], writes=[rtk])
    BETA = BETAx[:, 1:NT + 1]
    for ck in range(NCK):
        bl = small[:, 16:17]
        P.op("dve", lambda e, ck=ck: e.scalar_tensor_tensor(out=small[:, 16 + ck % 8:17 + ck % 8], in0=BETAx[:, 128 * (ck + 1):128 * (ck + 1) + 1],
                                                            scalar=-1.0, in1=cf[:, L_LNK:L_LNK + 1], op0=ALU.mult, op1=ALU.add),
             reads=[rtk, ctk], writes=[rtk])
        P.op("act", lambda e, ck=ck: e.activation(out=LF[:, ck * 128:(ck + 1) * 128], in_=ALPHA[:, ck * 128:(ck + 1) * 128],
                                                  func=AF.Exp, bias=small[:, 16 + ck % 8:17 + ck % 8]), reads=[rtk], writes=[rtk])
    to_token_major(LF, TMw)
    to_token_major(ALPHA, TMa)
    P.barrier()
    cx.release(pers_mark)
    BETA = BETAx[:, 1:NT + 1]

    qT = cx.sb(None, [128, 4, 512], BF16, "qT")
    kT = cx.sb(None, [128, 4, 512], BF16, "kT")
    zs = cx.sb(None, [128, 4, 512], BF16, "zs")
    yb = cx.sb(None, [128, 4, 512], BF16, "yb")
    qktk, zstk, ytk = Tk(), Tk(), Tk()
    Csts = [cx.sb(None, [128, 4, 512], F32, "Cst") for _ in range(2)]
    Caugs = [cx.sb(None, [128, 4, 640], BF16, "Caug") for _ in range(2)]
    caugtks = [Tk(), Tk()]
    nrow = cx.sb(None, [128, 512], F32, "nrow")
    nrtk = Tk()
    ncol = cx.sb(None, [128, 4], F32, "ncol")
    nctk = Tk()
    ctk2s = [Tk(), Tk()]
    cpar = {"p": 0}
    clst = Rot([topfree[:, 4096:6144], topfree[:, 6144:8176][:, 0:2032]])
    wk_rot = Rot([cx.sb(None, [128, 512], BF16, "wk") for _ in range(2)])
    va_rot = Rot([cx.sb(None, [128, 640], BF16, "vaug") for _ in range(2)])
    for (va, vatk) in va_rot.items:
        P.op("pool", lambda e, va=va: e.memset(va[:, 512:640], 1.0), writes=[vatk])
    dt_rot = Rot([cx.sb(None, [128, 128], F32, "dtmp") for _ in range(2)])
    sd_rot = Rot([cx.sb(None, [128, 128], BF16, "SdT") for _ in range(2)])
    qs_rot = Rot([cx.sb(None, [128, 4, 128], BF16, "qs") for _ in range(2)])
    hsq_rot = Rot([cx.sb(None, [128, 512], BF16, "hsq") for _ in range(2)])
    dd_rot = Rot([cx.sb(None, [128, 128], F32, "dd") for _ in range(2)])
    rr_rot = Rot([cx.sb(None, [128, 128], F32, "rr") for _ in range(2)])
    sc_rot = Rot([cx.sb(None, [128, 128], F32, "scsb") for _ in range(2)])
    em_rot = Rot([cx.sb(None, [128, 128], F32, "emsb") for _ in range(2)])
    ul_rot = Rot([cx.sb(None, [128, 1], F32, "ulast") for _ in range(2)])
    tt_rot = Rot([cx.sb(None, [128, 128], F32, "tt") for _ in range(3)])
    psB2 = psA
    psS3 = Rot([bk[2]])
    psRP = Rot([bk[2]])
    psH = Rot([bk[4], bk[5]])
    psDS = Rot([bk[6], bk[7]])
    psSS = Rot([bk[3]])
    wzv = wupv
    yview = yscr.rearrange("(c p) n -> p c n", p=128)
    def emit_fold(hd):
        Cst, ctk2 = Csts[hd % 2], ctk2s[hd % 2]
        P.op("pool", lambda e: e.memset(Cst, 0.0), writes=[ctk2])
        P.op("pool", lambda e: e.memset(nrow, 0.0), writes=[nrtk])
        Cflat = Cst.rearrange("p a b -> p (a b)")
        for cp in range(7):
            cl_, cltk = clst.items[0]
            P.dma("sync", cl_, st_all[cp][:, hd * 2048:(hd + 1) * 2048], writes=[cltk])
            P.op("act", lambda e, cp=cp, cl_=cl_: e.activation(out=cl_, in_=cl_, func=AF.Copy, scale=S2[:, cp, hd:hd + 1]),
                 reads=[cltk, ftk], writes=[cltk])
            P.op("dve", lambda e, cp=cp, cl_=cl_: e.scalar_tensor_tensor(out=Cflat, in0=Cflat, scalar=S1[:, cp, hd:hd + 1], in1=cl_,
                                                                          op0=ALU.mult, op1=ALU.add), reads=[cltk, ftk, ctk2], writes=[ctk2])
            nl_, nltk = clst.items[1]
            P.dma("sync", nl_[:, 0:512], st_all[cp][:, 8192 + hd * 512:8192 + (hd + 1) * 512], writes=[nltk])
            P.op("act", lambda e, cp=cp, nl_=nl_: e.activation(out=nl_[:, 0:512], in_=nl_[:, 0:512], func=AF.Copy, scale=S2[:, cp, hd:hd + 1]),
                 reads=[nltk, ftk], writes=[nltk])
            P.op("dve", lambda e, cp=cp, nl_=nl_: e.scalar_tensor_tensor(out=nrow, in0=nrow, scalar=S1[:, cp, hd:hd + 1], in1=nl_[:, 0:512],
                                                                          op0=ALU.mult, op1=ALU.add), reads=[nltk, ftk, nrtk], writes=[nrtk])

    ul_prev = None
    for hd in range(BH):
        wx3, wxtk = head_setup(hd)
        wz, wztk = ws.load([wzv[:, :, 2048 + hd * 512:2048 + (hd + 1) * 512]])
        wz3 = wz.rearrange("p (c n) -> p c n", c=NC8)
        Cst, ctk2 = Csts[hd % 2], ctk2s[hd % 2]
        if hd == 0:
            emit_fold(0)

        def refresh_caug(full):
            Caug, caugtk = Caugs[cpar["p"]], caugtks[cpar["p"]]
            for dc in range(4):
                P.op("act", lambda e, dc=dc: e.activation(out=Caug[:, dc, 0:512], in_=Cst[:, dc, :], func=AF.Copy),
                     reads=[ctk2], writes=[caugtk])
            P.op("dve", lambda e: e.tensor_scalar(out=nrow, in0=nrow, scalar1=cf[:, L_E0:L_E0 + 1], scalar2=None, op0=ALU.mult),
                 reads=[nrtk, ctk], writes=[nrtk])
            ps, pstk = psB2.next()
            for dc in range(4):
                P.op("pe", lambda e, ps=ps, dc=dc: e.matmul(ps[:, dc:dc + 1], lhsT=nrow[:, dc * 128:(dc + 1) * 128], rhs=ones_f[:, 0:1],
                                                            start=True, stop=True), reads=[nrtk, ctk], writes=[pstk], signal=(dc == 3))
            P.op("dve", lambda e, ps=ps: e.tensor_copy(out=ncol, in_=ps[:, 0:4]), reads=[pstk], writes=[nctk])
            P.op("dve", lambda e: e.tensor_copy(out=Caug[:, :, 512:640], in_=ncol.unsqueeze(2).to_broadcast([128, 4, 128])),
                 reads=[nctk], writes=[caugtk])

        refresh_caug(True)
        for tg in range(4):
            xm, xmt = front(hd, tg, wx3, wxtk, psFC)
            for mc in range(4):
                ps, pstk = psFC.next()
                for c in range(NC8):
                    P.op("pe", lambda e, c=c, mc=mc, ps=ps: e.matmul(
                        ps, lhsT=wz3[:, c, mc * 128:(mc + 1) * 128], rhs=hT[:, c, 4 + tg * 512:4 + (tg + 1) * 512],
                        start=(c == 0), stop=(c == NC8 - 1)), reads=[wztk], writes=[pstk], signal=(c == NC8 - 1))
                P.op("act", lambda e, ps=ps, mc=mc: e.activation(out=zs[:, mc, :], in_=ps, func=AF.Silu), reads=[pstk], writes=[zstk])
            for j, dst, sc_ in ((0, qT, 1.0), (1, kT, KSCALE)):
                for dc in range(4):
                    ps, pstk = psFC.next()
                    P.op("pe", lambda e, ps=ps, j=j, dc=dc: e.matmul(ps, lhsT=bdh[:, j, dc, :], rhs=xc[:, dc, :], start=True, stop=True),
                         reads=[hdtk, xctk], writes=[pstk])
                    P.op("act", lambda e, ps=ps, dst=dst, dc=dc, sc_=sc_: e.activation(out=dst[:, dc, :], in_=ps, func=AF.Copy, scale=sc_),
                         reads=[pstk], writes=[qktk])
            RS = {}

            def stage_pre(cl):
                nonlocal ul_prev
                ck = tg * 4 + cl
                tsl = slice(cl * 128, (cl + 1) * 128)
                gsl = slice(ck * 128, (ck + 1) * 128)
                sel = cf[:, L_SEL + hd * 128:L_SEL + (hd + 1) * 128]
                rp, rptk = psRP.next()
                for i3, row in enumerate((BETA, MM)):
                    P.op("pe", lambda e, rp=rp, i3=i3, row=row, gsl=gsl: e.matmul(
                        rp[:, i3 * 128:(i3 + 1) * 128], lhsT=sel, rhs=row[:, gsl], start=True, stop=True),
                        reads=[rtk, gtk, ctk], writes=[rptk], signal=(i3 == 1))
                bprev = mrun[:, hd:hd + 1] if ck == 0 else ul_prev[0]
                bprev_tk = ftk if ck == 0 else ul_prev[1]
                scsb, sctk = sc_rot.next()
                P.op("act", lambda e, rp=rp, scsb=scsb, bprev=bprev: e.activation(out=scsb, in_=rp[:, 0:128], func=AF.Exp, scale=-1.0, bias=bprev),
                     reads=[rptk, bprev_tk], writes=[sctk])
                emsb, emtk = em_rot.next()
                P.op("act", lambda e, rp=rp, emsb=emsb: e.activation(out=emsb, in_=rp[:, 128:256], func=AF.Exp, scale=-1.0),
                     reads=[rptk], writes=[emtk])
                ul_prev = ul_rot.next()
                P.op("act", lambda e, rp=rp, ul_prev=ul_prev: e.activation(out=ul_prev[0], in_=rp[:, 127:128], func=AF.Copy),
                     reads=[rptk], writes=[ul_prev[1]])
                ps, pstk = psB2.next()
                for mc in range(4):
                    P.op("pe", lambda e, ps=ps, mc=mc, tsl=tsl: e.matmul(
                        ps[:, mc * 128:(mc + 1) * 128], lhsT=xc[:, mc, tsl], rhs=bdh[:, 1, mc, :], start=True, stop=True),
                        reads=[xctk, hdtk], writes=[pstk], signal=(mc == 3))
                wk, wktk = wk_rot.next()
                P.op("act", lambda e, ps=ps, wk=wk, ck=ck: e.activation(out=wk, in_=ps, func=AF.Copy, scale=TMw[:, ck, hd:hd + 1]),
                     reads=[pstk, tmtk], writes=[wktk])
                ps, pstk = psB2.next()
                for mc in range(4):
                    P.op("pe", lambda e, ps=ps, mc=mc, cl=cl, xm=xm: e.matmul(
                        ps[:, mc * 128:(mc + 1) * 128], lhsT=xm[:, mc, 4 + cl * 128:4 + (cl + 1) * 128], rhs=bdh[:, 2, mc, :],
                        start=True, stop=True), reads=[xmt, hdtk], writes=[pstk], signal=(mc == 3))
                va, vatk = va_rot.next()
                P.op("act", lambda e, ps=ps, va=va: e.activation(out=va[:, 0:512], in_=ps, func=AF.Copy), reads=[pstk], writes=[vatk])
                pS_, pStk = psS3.next()
                pS = pS_[:, 256:384]
                for dc in range(4):
                    P.op("pe", lambda e, pS=pS, dc=dc, tsl=tsl: e.matmul(pS, lhsT=kT[:, dc, tsl], rhs=qT[:, dc, tsl],
                                                                          start=(dc == 0), stop=(dc == 3)),
                         reads=[qktk], writes=[pStk], signal=(dc == 3))
                dtmp, dttk = dt_rot.next()
                P.op("dve", lambda e, rp=rp, dtmp=dtmp, ck=ck: e.scalar_tensor_tensor(
                    out=dtmp, in0=rp[:, 0:128], scalar=TMa[:, ck, hd:hd + 1], in1=cf[:, L_MASKLOW:L_MASKLOW + 128],
                    op0=ALU.subtract, op1=ALU.max), reads=[rptk, tmtk, ctk], writes=[dttk])
                P.op("act", lambda e, dtmp=dtmp: e.activation(out=dtmp, in_=dtmp, func=AF.Exp, scale=-1.0), reads=[dttk], writes=[dttk])
                sd, sdtk = sd_rot.next()
                P.op("dve", lambda e, pS=pS, dtmp=dtmp, sd=sd: e.tensor_tensor(out=sd, in0=pS, in1=dtmp, op=ALU.mult),
                     reads=[pStk, dttk], writes=[sdtk])
                qs, qstk = qs_rot.next()
                P.op("dve", lambda e, scsb=scsb, qs=qs, tsl=tsl: e.tensor_tensor(
                    out=qs, in0=qT[:, :, tsl], in1=scsb.unsqueeze(1).to_broadcast([128, 4, 128]), op=ALU.mult),
                    reads=[qktk, sctk], writes=[qstk])

                RS[cl] = dict(ck=ck, tsl=tsl, wk=wk, wktk=wktk, va=va, vatk=vatk, sd=sd, sdtk=sdtk, qs=qs, qstk=qstk,
                              scsb=scsb, sctk=sctk, emsb=emsb, emtk=emtk)

            def stage_mid(cl):
                r_ = RS[cl]
                ck, tsl, wk, wktk, va, vatk, sd, sdtk, qs, qstk, scsb, sctk = (r_[k_] for k_ in (
                    "ck", "tsl", "wk", "wktk", "va", "vatk", "sd", "sdtk", "qs", "qstk", "scsb", "sctk"))
                Caug, caugtk = Caugs[cpar["p"]], caugtks[cpar["p"]]
                CaugN, caugNtk = Caugs[1 - cpar["p"]], caugtks[1 - cpar["p"]]
                cpar["p"] = 1 - cpar["p"]
                if dbg_stop is not None and (hd, ck) == tuple(dbg_stop):
                    P.barrier()
                    P.finish()
                    return nc, cx
                decay = scsb[:, 127:128]
                for dc in range(4):
                    ps, pstk = psB2.next()
                    P.op("pe", lambda e, ps=ps, dc=dc, wk=wk, va=va: e.matmul(ps, lhsT=wk[:, dc * 128:(dc + 1) * 128], rhs=va[:, 0:512],
                                                                                start=True, stop=True), reads=[wktk, vatk], writes=[pstk])
                    P.op("dve", lambda e, ps=ps, dc=dc, decay=decay: e.scalar_tensor_tensor(
                        out=Cst[:, dc, :], in0=Cst[:, dc, :], scalar=decay, in1=ps, op0=ALU.mult, op1=ALU.add),
                        reads=[pstk, sctk, ctk2], writes=[ctk2])
                    P.op("dve", lambda e, dc=dc: e.tensor_copy(out=CaugN[:, dc, 0:512], in_=Cst[:, dc, :]),
                         reads=[ctk2], writes=[caugNtk])
                ps, pstk = psB2.next()
                for dc in range(4):
                    P.op("pe", lambda e, ps=ps, dc=dc, wk=wk: e.matmul(ps[:, dc:dc + 1], lhsT=wk[:, dc * 128:(dc + 1) * 128], rhs=ones_bf[:, 0:1],
                                                                         start=True, stop=True), reads=[wktk, ctk], writes=[pstk], signal=(dc == 3))
                P.op("dve", lambda e, ps=ps, decay=decay: e.scalar_tensor_tensor(out=ncol, in0=ncol, scalar=decay, in1=ps[:, 0:4],
                                                                                  op0=ALU.mult, op1=ALU.add),
                     reads=[pstk, sctk, nctk], writes=[nctk])
                P.op("dve", lambda e: e.tensor_copy(out=CaugN[:, :, 512:640], in_=ncol.unsqueeze(2).to_broadcast([128, 4, 128])),
                     reads=[nctk], writes=[caugNtk])
                pH, pHtk = psH.next()
                pD_, pDtk = psDS.next()
                for ec in range(5):
                    o = pH[:, ec * 128:(ec + 1) * 128] if ec < 4 else pD_[:, 0:128]
                    otk = pHtk if ec < 4 else pDtk
                    for dc in range(4):
                        P.op("pe", lambda e, o=o, ec=ec, dc=dc, qs=qs: e.matmul(
                            o, lhsT=Caug[:, dc, ec * 128:(ec + 1) * 128], rhs=qs[:, dc, :], start=(dc == 0), stop=False),
                            reads=[caugtk, qstk], writes=[otk], signal=False)
                    P.op("pe", lambda e, o=o, ec=ec, va=va, sd=sd: e.matmul(
                        o, lhsT=va[:, ec * 128:(ec + 1) * 128], rhs=sd, start=False, stop=True),
                        reads=[vatk, sdtk], writes=[otk], signal=True)

                r_.update(pH=pH, pHtk=pHtk, pD_=pD_, pDtk=pDtk)

            def stage_post(cl):
                r_ = RS[cl]
                ck, tsl, emsb, emtk, pH, pHtk, pD_, pDtk = (r_[k_] for k_ in ("ck", "tsl", "emsb", "emtk", "pH", "pHtk", "pD_", "pDtk"))
                hsq, hsqtk = hsq_rot.next()
                P.op("act", lambda e, pH=pH, hsq=hsq: e.activation(out=hsq, in_=pH, func=AF.Square), reads=[pHtk], writes=[hsqtk])
                pSS_, pSStk = psSS.next()
                pSS = pSS_[:, 0:128]
                for ec in range(4):
                    P.op("pe", lambda e, pSS=pSS, hsq=hsq, ec=ec: e.matmul(pSS, lhsT=ones_bf, rhs=hsq[:, ec * 128:(ec + 1) * 128],
                                                                            start=(ec == 0), stop=(ec == 3)),
                         reads=[hsqtk, ctk], writes=[pSStk], signal=(ec == 3))
                dd, ddtk = dd_rot.next()
                P.op("dve", lambda e, pD_=pD_, dd=dd: e.tensor_scalar(out=dd, in0=pD_[:, 0:128], scalar1=-1.0, scalar2=None, op0=ALU.mult),
                     reads=[pDtk], writes=[ddtk])
                P.op("dve", lambda e, pD_=pD_, dd=dd: e.tensor_tensor(out=dd, in0=dd, in1=pD_[:, 0:128], op=ALU.max),
                     reads=[pDtk, ddtk], writes=[ddtk])
                P.op("dve", lambda e, emsb=emsb, dd=dd: e.tensor_tensor(out=dd, in0=dd, in1=emsb, op=ALU.max),
                     reads=[emtk, ddtk], writes=[ddtk])
                P.op("dve", lambda e, dd=dd: e.scalar_tensor_tensor(out=dd, in0=dd, scalar=EPS, in1=dd, op0=ALU.mult, op1=ALU.mult),
                     reads=[ddtk], writes=[ddtk])
                rr, rrtk = rr_rot.next()
                P.op("dve", lambda e, pSS=pSS, dd=dd, rr=rr: e.scalar_tensor_tensor(out=rr, in0=pSS, scalar=1.0 / DH, in1=dd,
                                                                                     op0=ALU.mult, op1=ALU.add),
                     reads=[pSStk, ddtk], writes=[rrtk])
                P.op("act", lambda e, rr=rr: e.activation(out=rr, in_=rr, func=AF.Sqrt), reads=[rrtk], writes=[rrtk])
                P.op("dve", lambda e, rr=rr: e.reciprocal(out=rr, in_=rr), reads=[rrtk], writes=[rrtk])
                for ec in range(4):
                    ch = hd * 4 + ec
                    tt, tttk = tt_rot.next()
                    P.op("dve", lambda e, pH=pH, ec=ec, ch=ch, rr=rr, tt=tt: e.scalar_tensor_tensor(
                        out=tt, in0=pH[:, ec * 128:(ec + 1) * 128], scalar=cf[:, L_HGAIN + ch:L_HGAIN + ch + 1], in1=rr,
                        op0=ALU.mult, op1=ALU.mult), reads=[pHtk, rrtk, ctk], writes=[tttk])
                    P.op("dve", lambda e, ec=ec, ch=ch, tt=tt, tsl=tsl: e.scalar_tensor_tensor(
                        out=tt, in0=xc[:, ec, tsl], scalar=cf[:, L_SKIP + ch:L_SKIP + ch + 1], in1=tt,
                        op0=ALU.mult, op1=ALU.add), reads=[xctk, tttk, ctk], writes=[tttk])
                    P.op("dve", lambda e, ec=ec, tt=tt, tsl=tsl: e.tensor_tensor(out=yb[:, ec, tsl], in0=tt, in1=zs[:, ec, tsl], op=ALU.mult),
                         reads=[tttk, zstk], writes=[ytk])


            stage_pre(0)
            stage_mid(0)
            for cl in range(1, 4):
                stage_pre(cl)
                stage_mid(cl)
                stage_post(cl - 1)
            stage_post(3)

            if tg == 1 and hd + 1 < BH:
                emit_fold(hd + 1)
            P.dma("sync", yview[:, hd * 4:(hd + 1) * 4, tg * 512:(tg + 1) * 512], yb, reads=[ytk])
    P.barrier()
    cx.release(base_mark)

    X = TOP.rearrange("p (c n) -> p c n", c=NC8)
    Xtk = [[Tk() for _ in range(4)] for _ in range(NC8)]
    for c in range(NC8):
        for tg in range(4):
            P.dma("sync", X[:, c, tg * 512:(tg + 1) * 512], x1T[c * 128:(c + 1) * 128, tg * 512:(tg + 1) * 512], writes=[Xtk[c][tg]])
    mk = cx.mark()
    wdn = cx.sb(None, [128, 16, D], BF16, "wdn")
    wdtk = Tk()
    wdv = wdown.rearrange("(c p) n -> p c n", p=128)
    wstg3 = Rot([cx.sb(None, [128, 4, D], F32, "wstg3") for _ in range(2)])
    for q4 in range(4):
        stg_, stk_ = wstg3.next()
        P.dma("sync", stg_, wdv[:, q4 * 4:(q4 + 1) * 4, :], writes=[stk_])
        P.op("act", lambda e, stg_=stg_, q4=q4: e.activation(out=wdn[:, q4 * 4:(q4 + 1) * 4, :], in_=stg_, func=AF.Copy),
             reads=[stk_], writes=[wdtk])
    yts = [cx.sb(None, [128, 16, 512], BF16, "yt") for _ in range(2)]
    yttk = [Tk(), Tk()]
    psA4 = Rot(bk[0:4])
    for tg in range(4):
        yt, ytt = yts[tg % 2], yttk[tg % 2]
        P.dma("sync", yt, yview[:, :, tg * 512:(tg + 1) * 512], writes=[ytt])
        sl = slice(tg * 512, (tg + 1) * 512)
        for oc in range(NC8):
            ps, pstk = psA4.next()
            for mc in range(16):
                P.op("pe", lambda e, ps=ps, mc=mc, oc=oc, yt=yt: e.matmul(ps, lhsT=wdn[:, mc, oc * 128:(oc + 1) * 128], rhs=yt[:, mc, :],
                                                                           start=(mc == 0), stop=(mc == 15)),
                     reads=[wdtk, ytt], writes=[pstk], signal=(mc == 15))
            P.op("dve", lambda e, ps=ps, oc=oc, sl=sl: e.tensor_tensor(out=X[:, oc, sl], in0=ps, in1=X[:, oc, sl], op=ALU.add),
                 reads=[pstk, Xtk[oc][tg]], writes=[Xtk[oc][tg]])
    P.barrier()
    cx.release(mk)
    if debug:
        emit_store(cx, X, Xtk, dbg_a)
    emit_mlp(cx, X, Xtk, cf[:, L_MLPN:L_MLPN + 8], ctk, w1, w2, ones_bf)
    emit_ple(cx, X, Xtk, cf[:, L_PLEN:L_PLEN + 8], ctk, wg, wp, pT, ones_bf)
    emit_store(cx, X, Xtk, out)
    P.finish()
    return nc, cx


_CACHE = {}


def _prog(key, builder):
    return builder()


def kernel(**inputs):
    inputs = {k: np.asarray(v) for k, v in inputs.items()}
    cores = list(range(NCORES))
    nc, _ = build_layer0()
    in_maps = [layer0_inputs(inputs, c) for c in cores]
    res = run_bass_kernel_spmd(nc, in_maps, core_ids=cores)
    x1T = np.concatenate([r["xout"] for r in res.results], axis=1)
    nc, _ = build_layer1("B")
    in_maps = [layer1_inputs(inputs, c, x1T, "B") for c in cores]
    res = run_bass_kernel_spmd(nc, in_maps, core_ids=cores)
    st_all = np.stack([res.results[c]["st_out"] for c in range(7)])
    g_rows = [res.results[c]["g_out"] for c in cores]
    nc, _ = build_layer1("C")
    in_maps = [layer1_inputs(inputs, c, x1T, "C", st_all, g_rows[c]) for c in cores]
    res = run_bass_kernel_spmd(nc, in_maps, core_ids=cores)
    outT = np.concatenate([r["xout"] for r in res.results], axis=1)
    return np.ascontiguousarray(outT.T)[None].astype(np.float32)
```

```python
import numpy as np
import concourse.bass as bass
import concourse.mybir as mybir
from concourse.bass_utils import run_bass_kernel_spmd

F32 = mybir.dt.float32
BF16 = mybir.dt.bfloat16
AF = mybir.ActivationFunctionType
ALU = mybir.AluOpType
AX = mybir.AxisListType

NCORES = 8
S = 16384
D = 1024
NT = S // NCORES
NC8 = D // 128
EPS = 1e-6
BIG = 30000.0
A_GROUPS = ((128, 1), (512, 4), (2048, 16))
NDMA = 24
SB_F32 = 51968


class Tk:
    __slots__ = ("w", "r")

    def __init__(self):
        self.w = {}
        self.r = {}


class Prog:
    def __init__(self, nc):
        self.nc = nc
        self.eng = {"act": nc.scalar, "dve": nc.vector, "pool": nc.gpsimd, "pe": nc.tensor, "sync": nc.sync}
        self.sem = {e: nc.alloc_semaphore("s_" + e) for e in ("act", "dve", "pool", "pe")}
        self.cnt = {e: 0 for e in ("act", "dve", "pool", "pe")}
        self.seen = {e: {} for e in self.eng}
        self.dsem = [nc.alloc_semaphore("s_dma%d" % i) for i in range(NDMA)]
        self.dcnt = [0] * NDMA
        self.dnext = 0
        self.nins = {e: 0 for e in self.eng}

    def _semof(self, src):
        if isinstance(src, tuple):
            return self.dsem[src[1]]
        return self.sem[src]

    def _deps(self, e, reads, writes, allraw=False):
        deps = {}

        def add(src, n, raw):
            if src == e and not allraw:
                if e == "pe" or not raw:
                    return
            if deps.get(src, 0) < n:
                deps[src] = n

        for t in reads:
            for src, n in t.w.items():
                add(src, n, True)
        for t in writes:
            for src, n in t.w.items():
                add(src, n, False)
            for src, n in t.r.items():
                add(src, n, False)
        return deps

    def _wait(self, e, deps):
        eng = self.eng[e]
        seen = self.seen[e]
        for src, n in deps.items():
            if seen.get(src, 0) >= n:
                continue
            seen[src] = n
            eng.wait_ge(self._semof(src), n)
            self.nins[e] += 1

    def op(self, e, fn, reads=(), writes=(), signal=True):
        self._wait(e, self._deps(e, reads, writes))
        ins = fn(self.eng[e])
        self.nins[e] += 1
        n = self.cnt[e] + 1
        if signal:
            ins.then_inc(self.sem[e], 1)
            self.cnt[e] = n
        for t in reads:
            if t.r.get(e, 0) < n:
                t.r[e] = n
        for t in writes:
            if t.w.get(e, 0) < n:
                t.w[e] = n
        return ins

    def dma(self, q, out, in_, reads=(), writes=()):
        k = self.dnext
        self.dnext = (k + 1) % NDMA
        src = ("dma", k)
        deps = self._deps(q, reads, writes, allraw=True)
        if self.dcnt[k] > 0:
            deps[src] = max(deps.get(src, 0), self.dcnt[k])
        self._wait(q, deps)
        ins = self.eng[q].dma_start(out=out, in_=in_)
        self.nins[q] += 1
        n = self.dcnt[k] + 16
        ins.then_inc(self.dsem[k], 16)
        self.dcnt[k] = n
        for t in reads:
            t.r[src] = n
        for t in writes:
            t.w[src] = n

    def barrier(self):
        for e in self.eng:
            deps = {}
            for s2 in self.cnt:
                if s2 != e and self.cnt[s2] > 0:
                    deps[s2] = self.cnt[s2]
            for k in range(NDMA):
                if self.dcnt[k] > 0:
                    deps[("dma", k)] = self.dcnt[k]
            self._wait(e, deps)

    def finish(self):
        deps = {}
        for k in range(NDMA):
            if self.dcnt[k] > 0:
                deps[("dma", k)] = self.dcnt[k]
        self._wait("sync", deps)


class Rot:
    def __init__(self, aps):
        self.items = [a if isinstance(a, tuple) else (a, Tk()) for a in aps]
        self.i = 0

    def next(self):
        it = self.items[self.i]
        self.i = (self.i + 1) % len(self.items)
        return it


class Ctx:
    def __init__(self, nc):
        self.nc = nc
        self.P = Prog(nc)
        self.banks = [nc.alloc_psum_tensor("psb%d" % i, [128, 512], F32).ap() for i in range(8)]
        self.nalloc = 0

        self.big = nc.alloc_sbuf_tensor("big", [128, SB_F32], F32).ap()
        self.top = 0

    def sb(self, stack, shape, dt, name=None):
        esz = 2 if dt == BF16 else 4
        n = int(np.prod(shape[1:]))
        nbytes = (n * esz + 63) // 64 * 64
        off = self.top
        assert off + nbytes <= SB_F32 * 4, ("SBUF overflow", name, off, nbytes)
        self.top = off + nbytes
        self.log = getattr(self, 'log', [])
        self.log.append((name, off, nbytes))
        ap = self.big[:, off // 4:(off + nbytes) // 4]
        if dt == BF16:
            ap = ap.bitcast(BF16)
        ap = ap[:, 0:n]
        if len(shape) == 3:
            ap = ap.rearrange("p (a b) -> p a b", a=shape[1])
        elif len(shape) == 4:
            ap = ap.rearrange("p (a b c) -> p a b c", a=shape[1], b=shape[2])
        return ap

    def mark(self):
        return self.top

    def release(self, m):
        self.top = m


def load_consts(cx, stack, cst_ap, ncols):
    P = cx.P
    cf = cx.sb(stack, [128, ncols], F32, "cstf")
    cb = cx.sb(stack, [128, C_END_BF], BF16, "cstb")
    tk = Tk()
    P.dma("sync", cf, cst_ap, writes=[tk])
    P.op("dve", lambda e: e.tensor_copy(out=cb, in_=cf[:, 0:C_END_BF]), reads=[tk], writes=[tk])
    return cf, cb, tk


class WStream:
    def __init__(self, cx, stack, nelem, nstage=2, nslot=2):
        self.cx = cx
        self.nelem = nelem
        self.stage = Rot([cx.sb(stack, [128, nelem], F32, "wstg") for _ in range(nstage)])
        self.slots = Rot([cx.sb(stack, [128, nelem], BF16, "wbf") for _ in range(nslot)])

    def load(self, views):
        P = self.cx.P
        stg, stk = self.stage.next()
        wb, wtk = self.slots.next()
        off = 0
        for v in views:
            shp = v.shape
            n = int(np.prod(shp[1:]))
            dst = stg[:, off:off + n]
            if len(shp) == 3:
                dst = dst.rearrange("p (a b) -> p a b", a=shp[1])
            P.dma("sync", dst, v, writes=[stk])
            off += n
        assert off <= self.nelem
        P.op("pool", lambda e: e.tensor_copy(out=wb[:, 0:off], in_=stg[:, 0:off]), reads=[stk], writes=[wtk])
        return wb, wtk


def rms_stats(cx, xs, n, sq_rot, ps_ap, ps_tk, rstd, rstd_tk, ones_bf, ctk, inv_dim):
    P = cx.P
    nx = len(xs)
    for c, (xa, xt) in enumerate(xs):
        sq, sqt = sq_rot.next()
        P.op("act", lambda e, xa=xa, sq=sq: e.activation(out=sq[:, 0:n], in_=xa, func=AF.Square), reads=[xt], writes=[sqt])
        P.op("pe", lambda e, sq=sq, c=c: e.matmul(ps_ap[:, 0:n], lhsT=ones_bf, rhs=sq[:, 0:n], start=(c == 0), stop=(c == nx - 1)),
             reads=[sqt, ctk], writes=[ps_tk])
    P.op("act", lambda e: e.activation(out=rstd[:, 0:n], in_=ps_ap[:, 0:n], func=AF.Sqrt, bias=cx.eps_col, scale=inv_dim),
         reads=[ps_tk, ctk], writes=[rstd_tk])
    P.op("dve", lambda e: e.reciprocal(out=rstd[:, 0:n], in_=rstd[:, 0:n]), reads=[rstd_tk], writes=[rstd_tk])


C_ID, C_ONES, C_BONES, C_DM, C_HONES = 0, 128, 256, 384, 640
C_OZ = 704
C_HZ = 960
C_END_BF = 1216
C_EPS = 1216
C_GAINS = 1217
G0_ANORM = C_GAINS
G0_QG = G0_ANORM + 8
G0_KG = G0_QG + 3
G0_MLPN = G0_KG + 3
G0_PLEN = G0_MLPN + 8
G0_END = G0_PLEN + 8


def base_consts(core, ncols):
    c = np.zeros((128, ncols), np.float32)
    c[:, C_ID:C_ID + 128] = np.eye(128, dtype=np.float32)
    c[:, C_ONES:C_ONES + 128] = 1.0
    c[0:64, C_BONES:C_BONES + 64] = 1.0
    c[64:128, C_BONES + 64:C_BONES + 128] = 1.0
    kk = np.arange(128)[:, None]
    a = np.arange(128)[None, :]
    diag = np.where(kk <= a, a - kk, BIG)
    prev = np.where(kk >= a, 128 + a - kk, BIG)
    c[:, C_DM:C_DM + 128] = diag
    c[:, C_DM + 128:C_DM + 256] = prev
    hv = 0.0 if core == 0 else 1.0
    c[:, C_HONES:C_HONES + 64] = hv
    c[:, C_OZ:C_OZ + 64] = 1.0
    c[:, C_OZ + 128 + 64:C_OZ + 256] = 1.0
    c[:, C_HZ:C_HZ + 64] = hv
    c[:, C_HZ + 128 + 64:C_HZ + 256] = hv
    c[:, C_EPS] = EPS
    return c


def col_layout(v):
    v = np.asarray(v, np.float32).reshape(-1, 128)
    return np.ascontiguousarray(v.T)


def emit_norm_resident(cx, X, Xtk, gcol, ctk, hT, hTtk, sq_rot, rstd_rot, ps_rot, ones_bf):
    P = cx.P
    for tg in range(NT // 512):
        sl = slice(tg * 512, (tg + 1) * 512)
        xs = [(X[:, c, sl], Xtk[c][tg]) for c in range(NC8)]
        ps_ap, ps_tk = ps_rot.next()
        rstd, rtk = rstd_rot.next()
        rms_stats(cx, xs, 512, sq_rot, ps_ap, ps_tk, rstd, rtk, ones_bf, ctk, 1.0 / D)
        for c in range(NC8):
            P.op("dve", lambda e, c=c, sl=sl, rstd=rstd: e.scalar_tensor_tensor(
                out=hT[:, c, sl], in0=X[:, c, sl], scalar=gcol[:, c:c + 1], in1=rstd[:, 0:512],
                op0=ALU.mult, op1=ALU.mult), reads=[Xtk[c][tg], rtk, ctk], writes=[hTtk[c][tg]])


def emit_mlp(cx, X, Xtk, gcol, ctk, w1, w2, ones_bf):
    P = cx.P
    mk = cx.mark()
    st = None
    hT = cx.sb(st, [128, NC8, NT], BF16, "mlp_hT")
    hTtk = [[Tk() for _ in range(4)] for _ in range(NC8)]
    sq_rot = Rot([cx.sb(st, [128, 512], BF16, "sq") for _ in range(4)])
    rstd_rot = Rot([cx.sb(st, [128, 512], F32, "rstd") for _ in range(2)])
    ps_stat = Rot([cx.banks[7]])
    emit_norm_resident(cx, X, Xtk, gcol, ctk, hT, hTtk, sq_rot, rstd_rot, ps_stat, ones_bf)
    ws = WStream(cx, st, 4096, nstage=2, nslot=2)
    hids = [cx.sb(st, [128, 4, NT], BF16, "hid") for _ in range(2)]
    hid_tks = [[[Tk() for _ in range(4)] for _ in range(4)] for _ in range(2)]
    tmp_rot = Rot([cx.sb(st, [128, 512], F32, "rl") for _ in range(3)])
    psA = Rot(cx.banks[0:4])
    psB = Rot(cx.banks[4:7])
    w1v = w1.rearrange("(c p) n -> p c n", p=128)
    w2v = w2.rearrange("(c p) n -> p c n", p=128)
    NHB = 8
    for hb in range(NHB):
        hid = hids[hb % 2]
        htk = hid_tks[hb % 2]
        wa, watk = ws.load([w1v[:, :, hb * 512:(hb + 1) * 512]])
        wa3 = wa.rearrange("p (c n) -> p c n", c=NC8)
        for hc in range(4):
            for tg in range(4):
                sl = slice(tg * 512, (tg + 1) * 512)
                ps, pstk = psA.next()
                for c in range(NC8):
                    P.op("pe", lambda e, c=c, hc=hc, sl=sl, ps=ps, wa3=wa3: e.matmul(
                        ps, lhsT=wa3[:, c, hc * 128:(hc + 1) * 128], rhs=hT[:, c, sl],
                        start=(c == 0), stop=(c == NC8 - 1)),
                        reads=[watk, hTtk[c][tg]], writes=[pstk], signal=(c == NC8 - 1))
                tmp, ttk = tmp_rot.next()
                P.op("act", lambda e, ps=ps, tmp=tmp: e.activation(out=tmp, in_=ps, func=AF.Square),
                     reads=[pstk], writes=[ttk])
                P.op("dve", lambda e, ps=ps, tmp=tmp, hc=hc, sl=sl, hid=hid: e.scalar_tensor_tensor(
                    out=hid[:, hc, sl], in0=ps, scalar=0.0, in1=tmp, op0=ALU.is_gt, op1=ALU.mult),
                    reads=[pstk, ttk], writes=[htk[hc][tg]])
        wb, wbtk = ws.load([w2v[:, hb * 4:(hb + 1) * 4, :]])
        wb3 = wb.rearrange("p (c n) -> p c n", c=4)
        for oc in range(NC8):
            for tg in range(4):
                sl = slice(tg * 512, (tg + 1) * 512)
                ps, pstk = psB.next()
                for hc in range(4):
                    P.op("pe", lambda e, hc=hc, oc=oc, sl=sl, ps=ps, hid=hid, wb3=wb3: e.matmul(
                        ps, lhsT=wb3[:, hc, oc * 128:(oc + 1) * 128], rhs=hid[:, hc, sl],
                        start=(hc == 0), stop=(hc == 3)),
                        reads=[wbtk, htk[hc][tg]], writes=[pstk], signal=(hc == 3))
                P.op("dve", lambda e, oc=oc, sl=sl, ps=ps: e.tensor_tensor(
                    out=X[:, oc, sl], in0=ps, in1=X[:, oc, sl], op=ALU.add),
                    reads=[pstk, Xtk[oc][tg]], writes=[Xtk[oc][tg]])
    P.barrier()
    cx.release(mk)


def emit_ple(cx, X, Xtk, gcol, ctk, wg, wp, pT_dram, ones_bf):
    P = cx.P
    mk = cx.mark()
    st = None
    hT = cx.sb(st, [128, NC8, NT], BF16, "ple_hT")
    hTtk = [[Tk() for _ in range(4)] for _ in range(NC8)]
    sq_rot = Rot([cx.sb(st, [128, 512], BF16, "sq") for _ in range(4)])
    rstd_rot = Rot([cx.sb(st, [128, 512], F32, "rstd") for _ in range(2)])
    ps_stat = Rot([cx.banks[7]])
    emit_norm_resident(cx, X, Xtk, gcol, ctk, hT, hTtk, sq_rot, rstd_rot, ps_stat, ones_bf)
    ws = WStream(cx, st, 4096, nstage=2, nslot=3)
    pst = cx.sb(st, [128, 2, NT], F32, "pstg")
    pb = cx.sb(st, [128, 2, NT], BF16, "pbf")
    ptk = Tk()
    P.dma("sync", pst, pT_dram.rearrange("(c p) n -> p c n", p=128), writes=[ptk])
    P.op("pool", lambda e: e.tensor_copy(out=pb, in_=pst), reads=[ptk], writes=[ptk])
    wpb, wptk = ws.load([wp.rearrange("(c p) n -> p c n", p=128)])
    wp3 = wpb[:, 0:2048].rearrange("p (c n) -> p c n", c=2)
    gt_rot = Rot([cx.sb(st, [128, 512], F32, "gt") for _ in range(3)])
    psA = Rot(cx.banks[0:3])
    psB = Rot(cx.banks[3:6])
    wgv = wg.rearrange("(c p) n -> p c n", p=128)
    for half in range(2):
        wa, watk = ws.load([wgv[:, :, half * 512:(half + 1) * 512]])
        wa3 = wa.rearrange("p (c n) -> p c n", c=NC8)
        for o4 in range(4):
            oc = half * 4 + o4
            for tg in range(4):
                sl = slice(tg * 512, (tg + 1) * 512)
                ps, pstk = psA.next()
                for c in range(NC8):
                    P.op("pe", lambda e, c=c, o4=o4, sl=sl, ps=ps, wa3=wa3: e.matmul(
                        ps, lhsT=wa3[:, c, o4 * 128:(o4 + 1) * 128], rhs=hT[:, c, sl],
                        start=(c == 0), stop=(c == NC8 - 1)),
                        reads=[watk, hTtk[c][tg]], writes=[pstk], signal=(c == NC8 - 1))
                ps2, ps2tk = psB.next()
                for kc in range(2):
                    P.op("pe", lambda e, kc=kc, oc=oc, sl=sl, ps2=ps2: e.matmul(
                        ps2, lhsT=wp3[:, kc, oc * 128:(oc + 1) * 128], rhs=pb[:, kc, sl],
                        start=(kc == 0), stop=(kc == 1)),
                        reads=[wptk, ptk], writes=[ps2tk], signal=(kc == 1))
                gt, gtk = gt_rot.next()
                P.op("act", lambda e, ps=ps, gt=gt: e.activation(out=gt, in_=ps, func=AF.Sigmoid),
                     reads=[pstk], writes=[gtk])
                P.op("dve", lambda e, ps2=ps2, gt=gt: e.tensor_tensor(out=gt, in0=ps2, in1=gt, op=ALU.mult),
                     reads=[ps2tk, gtk], writes=[gtk])
                P.op("dve", lambda e, oc=oc, sl=sl, gt=gt: e.tensor_tensor(
                    out=X[:, oc, sl], in0=gt, in1=X[:, oc, sl], op=ALU.add),
                    reads=[gtk, Xtk[oc][tg]], writes=[Xtk[oc][tg]])
    P.barrier()
    cx.release(mk)


def alibi_slope(h):
    return 2.0 ** (-8.0 * (h + 1) / 16)


def sslice(start, count, step):
    return slice(start, start + (count - 1) * step + 1, step)


def emit_attention(cx, xT_ext, wqkv, wo, cf, cb, ctk, TOP, R1, lvl=9, hps=8):
    P = cx.P
    st = None
    ones_bf = cb[:, C_ONES:C_ONES + 128]
    bones = cb[:, C_BONES:C_BONES + 128]
    Dm = cf[:, C_DM:C_DM + 256]
    hT = TOP.bitcast(BF16).rearrange("p (c n) -> p c n", c=NC8)
    mk = cx.mark()
    sq_rot = Rot([cx.sb(st, [128, 512], BF16, "sq") for _ in range(3)])
    rstd_rot = Rot([cx.sb(st, [128, 512], F32, "rstd") for _ in range(3)])
    xstg = [R1[:, i * 4096:(i + 1) * 4096].rearrange("p (c n) -> p c n", c=NC8) for i in range(2)]
    xstk = [Tk(), Tk()]
    ps_stat = Rot([cx.banks[7], cx.banks[6]])
    httk = Tk()
    gcol = cf[:, G0_ANORM:G0_ANORM + 8]
    for tg in range(8 if lvl >= 1 else 0):
        xa = xstg[tg % 2]
        xt = xstk[tg % 2]
        P.dma("sync", xa, xT_ext[:, tg * 512:(tg + 1) * 512].rearrange("(c p) n -> p c n", p=128), writes=[xt])
        xs = [(xa[:, c, :], xt) for c in range(NC8)]
        ps_ap, ps_tk = ps_stat.next()
        rstd, rtk = rstd_rot.next()
        rms_stats(cx, xs, 512, sq_rot, ps_ap, ps_tk, rstd, rtk, ones_bf, ctk, 1.0 / D)
        for c in range(NC8):
            P.op("dve", lambda e, c=c, tg=tg, xa=xa, rstd=rstd: e.scalar_tensor_tensor(
                out=hT[:, c, tg * 512:(tg + 1) * 512], in0=xa[:, c, :], scalar=gcol[:, c:c + 1], in1=rstd[:, 0:512],
                op0=ALU.mult, op1=ALU.mult), reads=[xt, rtk, ctk], writes=[httk])
    P.op("dve", lambda e: e.tensor_scalar(out=cf[:, G0_QG:G0_QG + 3], in0=cf[:, G0_QG:G0_QG + 3], scalar1=0.125,
                                          scalar2=None, op0=ALU.mult), reads=[ctk], writes=[ctk])
    P.barrier()
    ACC = R1[:, 0:4096].rearrange("p (a n) -> p a n", a=2)
    oT = R1[:, 4096:12288].bitcast(BF16).rearrange("p (c n) -> p c n", c=NC8)
    acctk = Tk()
    ottk = Tk()
    ws = WStream(cx, st, 3072, nstage=1, nslot=2)
    QTz = [cx.sb(st, [128, NT], BF16, "QTz") for _ in range(2)]
    qtk = Tk()
    KT_rot = Rot([cx.sb(st, [128, 2 * NT], BF16, "KT") for _ in range(2)])
    Vz = [cx.sb(st, [128, 32, 128], BF16, "Vz") for _ in range(2)]
    vtk = Tk()
    for e2 in (0, 1):
        P.op("pool", lambda e, e2=e2: e.memset(QTz[e2], 0.0), writes=[qtk])
        P.op("pool", lambda e, e2=e2: e.memset(Vz[e2], 0.0), writes=[vtk])
    onesz = [cb[:, C_OZ:C_OZ + 128], cb[:, C_OZ + 128:C_OZ + 256]]
    honesz = [cb[:, C_HZ:C_HZ + 128], cb[:, C_HZ + 128:C_HZ + 256]]
    tmp_rot = Rot([cx.sb(st, [128, 256], F32, "stmp") for _ in range(4)])
    pt_rots = [Rot([cx.sb(st, [128, 256], BF16, "PT") for _ in range(6)]) for _ in range(2)]
    bk = [(cx.banks[i], Tk()) for i in range(8)]

    def half(i):
        return (bk[i][0][:, 0:256], bk[i][1])

    psQ = Rot([bk[0], bk[1], bk[4], bk[5]])
    psS = Rot([bk[2]])
    psV = Rot([bk[3], bk[6], bk[7]])
    psST0 = Rot([half(4), half(0)])
    psST1 = Rot([half(5), half(1)])
    psND = Rot([half(6), half(7), half(2), half(3)])
    wq_view = wqkv.rearrange("(c p) n -> p c n", p=128)

    def perm(ap2d, d):
        if d == 1:
            return ap2d
        return ap2d.rearrange("p (u r) -> p r u", r=d)

    def proj_piece(w3, wtk, j, e0, n, gain_col, out_buf, out_tk, d, Lx, u0):
        ps, pstk = psQ.next()
        for c in range(NC8):
            P.op("pe", lambda e, c=c, ps=ps: e.matmul(ps[:, 0:n], lhsT=w3[:, j, c, :], rhs=hT[:, c, e0:e0 + n],
                                                      start=(c == 0), stop=(c == NC8 - 1)),
                 reads=[wtk], writes=[pstk], signal=(c == NC8 - 1))
        ps2, ps2tk = psS.next()
        rstd, rtk = rstd_rot.next()
        rms_stats(cx, [(ps[:, 0:n], pstk)], n, sq_rot, ps2, ps2tk, rstd, rtk, bones, ctk, 1.0 / 64)
        outs = out_buf if isinstance(out_buf, list) else [(slice(0, 128), out_buf)]
        for (rows, ob) in outs:
            if d == 1:
                o = ob[rows, u0:u0 + n]
            else:
                o = ob[rows, 0:d * Lx].rearrange("p (r u) -> p r u", r=d)[:, :, u0:u0 + n // d]
            P.op("dve", lambda e, ps=ps, rstd=rstd, o=o, rows=rows: e.scalar_tensor_tensor(
                out=o, in0=perm(ps[rows, 0:n], d), scalar=gain_col[rows, :], in1=perm(rstd[rows, 0:n], d),
                op0=ALU.mult, op1=ALU.mult),
                reads=[pstk, rtk, ctk], writes=[out_tk])

    if lvl < 2:
        hps = 0
        P.op('dve', lambda e: e.memset(R1, 0.0), writes=[ottk])
    for hp in range(hps):
        for g, (W, d) in enumerate(A_GROUPS):
            L = NT // d
            Lk = (W + NT) // d
            nb = Lk // 128
            e_start = NT - W
            base = g * 3072 + hp * 128
            wb, wtk = ws.load([wq_view[:, :, base + j * 1024: base + j * 1024 + 128] for j in range(3)])
            w3 = wb[:, 0:3072].rearrange("p (j c n) -> p j c n", j=3, c=NC8)
            KT, ktk = KT_rot.next()
            for tg in range(4):
                proj_piece(w3, wtk, 0, NT + tg * 512, 512, cf[:, G0_QG + g:G0_QG + g + 1],
                           [(slice(0, 64), QTz[0]), (slice(64, 128), QTz[1])], qtk, d, L, tg * 512 // d)
            pieces = []
            if W < 512:
                pieces.append((e_start, W))
                e = NT
            else:
                e = e_start
            while e < 2 * NT:
                pieces.append((e, 512))
                e += 512
            for (e0, n) in pieces:
                proj_piece(w3, wtk, 1, e0, n, cf[:, G0_KG + g:G0_KG + g + 1], KT, ktk, d, Lk, (e0 - e_start) // d)
            nkb = d * nb if lvl >= 3 else 0
            kb = 0
            while kb < nkb:
                nblk = min(4, nkb - kb)
                psv, psvtk = psV.next()
                for b in range(nblk):
                    r, jb = divmod(kb + b, nb)
                    e_first = e_start + d * 128 * jb + r
                    for c in range(NC8):
                        P.op("pe", lambda e, c=c, b=b, e_first=e_first, psv=psv: e.matmul(
                            psv[:, b * 128:(b + 1) * 128], lhsT=hT[:, c, sslice(e_first, 128, d)], rhs=w3[:, 2, c, :],
                            start=(c == 0), stop=(c == NC8 - 1)),
                            reads=[wtk], writes=[psvtk], signal=(c == NC8 - 1 and b == nblk - 1))
                for e2 in (0, 1):
                    cs = slice(64 * e2, 64 * e2 + 64)
                    P.op("act", lambda e, kb=kb, nblk=nblk, psv=psv, e2=e2, cs=cs: e.activation(
                        out=Vz[e2][:, kb:kb + nblk, cs],
                        in_=psv[:, 0:nblk * 128].rearrange("p (b n) -> p b n", b=nblk)[:, :, cs], func=AF.Copy),
                        reads=[psvtk], writes=[vtk])
                kb += nblk
            PTs = {}

            def score_task(r, jb):
                lo = 128 if jb == 0 else 0
                hi = 128 if jb == nb - 1 else 256
                qb0 = jb if jb == 0 else jb - 1
                q_off = r * L + 128 * qb0
                sTs = [psST0.next(), psST1.next()]
                for e2 in (0, 1):
                    sT, sTtk = sTs[e2]
                    P.op("pe", lambda e, sT=sT, e2=e2: e.matmul(
                        sT[:, lo:hi], lhsT=KT[:, r * Lk + 128 * jb: r * Lk + 128 * jb + 128],
                        rhs=QTz[e2][:, q_off:q_off + (hi - lo)], start=True, stop=True),
                        reads=[ktk, qtk], writes=[sTtk])
                for e2 in (0, 1):
                    sig = alibi_slope(2 * hp + e2) * d
                    sT, sTtk = sTs[e2]
                    tmp, tmtk = tmp_rot.next()
                    P.op("dve", lambda e, sT=sT, tmp=tmp, sig=sig: e.scalar_tensor_tensor(
                        out=tmp[:, lo:hi], in0=Dm[:, lo:hi], scalar=-sig, in1=sT[:, lo:hi],
                        op0=ALU.mult, op1=ALU.add),
                        reads=[sTtk, ctk], writes=[tmtk])
                    pt, pttk = pt_rots[e2].next()
                    P.op("act", lambda e, tmp=tmp, pt=pt: e.activation(
                        out=pt[:, lo:hi], in_=tmp[:, lo:hi], func=AF.Exp), reads=[tmtk], writes=[pttk])
                    PTs[(e2, r, jb)] = (pt, pttk)

            def pv_task(r, j):
                jb = j + 1
                nd, ndtk = psND.next()
                kbp = r * nb + j
                kbd = r * nb + jb
                for part in (0, 1):
                    co = slice(128 * part, 128 * part + 128)
                    for e2 in (0, 1):
                        ptp, ptptk = PTs[(e2, r, j)]
                        ptd, ptdtk = PTs[(e2, r, jb)]
                        if part == 0:
                            lp, ld = Vz[e2][:, kbp, :], Vz[e2][:, kbd, :]
                        else:
                            lp, ld = (honesz[e2] if j == 0 else onesz[e2]), onesz[e2]
                        P.op("pe", lambda e, nd=nd, co=co, lp=lp, ptp=ptp, e2=e2: e.matmul(
                            nd[:, co], lhsT=lp, rhs=ptp[:, 128:256], start=(e2 == 0), stop=False),
                            reads=[vtk, ctk, ptptk], writes=[ndtk], signal=False)
                        P.op("pe", lambda e, nd=nd, co=co, ld=ld, ptd=ptd, e2=e2: e.matmul(
                            nd[:, co], lhsT=ld, rhs=ptd[:, 0:128], start=False, stop=(e2 == 1)),
                            reads=[vtk, ctk, ptdtk], writes=[ndtk], signal=(part == 1 and e2 == 1))
                t0 = r + d * 128 * j
                accv = ACC[:, :, sslice(t0, 128, d)]
                ndv = nd.rearrange("p (a n) -> p a n", a=2)
                if g == 0:
                    P.op("act", lambda e, accv=accv, ndv=ndv: e.activation(out=accv, in_=ndv, func=AF.Copy),
                         reads=[ndtk], writes=[acctk])
                else:
                    P.op("dve", lambda e, accv=accv, ndv=ndv: e.tensor_tensor(out=accv, in0=ndv, in1=accv, op=ALU.add),
                         reads=[ndtk, acctk], writes=[acctk])

            LA = 3
            pending = []
            tasks = [(r, jb) for r in range(d if lvl >= 4 else 0) for jb in range(nb)]
            for i, (r, jb) in enumerate(tasks):
                score_task(r, jb)
                if jb >= 1:
                    pending.append((i, r, jb - 1))
                while pending and pending[0][0] <= i - LA:
                    _, r_, j_ = pending.pop(0)
                    pv_task(r_, j_)
            for (_, r_, j_) in pending:
                pv_task(r_, j_)
        P.op("dve", lambda e: e.reciprocal(out=ACC[:, 1, :], in_=ACC[:, 1, :]), reads=[acctk], writes=[acctk])
        P.op("dve", lambda e, hp=hp: e.tensor_tensor(out=oT[:, hp, :], in0=ACC[:, 0, :], in1=ACC[:, 1, :], op=ALU.mult),
             reads=[acctk], writes=[ottk])
    P.barrier()
    cx.release(mk)
    mk = cx.mark()
    X = TOP.rearrange("p (c n) -> p c n", c=NC8)
    Xtk = [[Tk() for _ in range(4)] for _ in range(NC8)]
    for c in range(NC8):
        for tg in range(4):
            P.dma("sync", X[:, c, tg * 512:(tg + 1) * 512], xT_ext[c * 128:(c + 1) * 128, NT + tg * 512:NT + (tg + 1) * 512],
                  writes=[Xtk[c][tg]])
    ws2 = WStream(cx, st, 4096, nstage=2, nslot=2)
    wov = wo.rearrange("(c p) n -> p c n", p=128)
    psA = Rot(cx.banks[0:4])
    for half in range(2):
        wa, watk = ws2.load([wov[:, :, half * 512:(half + 1) * 512]])
        wa3 = wa.rearrange("p (c n) -> p c n", c=NC8)
        for o4 in range(4):
            oc = half * 4 + o4
            for tg in range(4):
                sl = slice(tg * 512, (tg + 1) * 512)
                ps, pstk = psA.next()
                for c in range(NC8):
                    P.op("pe", lambda e, c=c, o4=o4, sl=sl, ps=ps, wa3=wa3: e.matmul(
                        ps, lhsT=wa3[:, c, o4 * 128:(o4 + 1) * 128], rhs=oT[:, c, sl],
                        start=(c == 0), stop=(c == NC8 - 1)),
                        reads=[watk, ottk], writes=[pstk], signal=(c == NC8 - 1))
                P.op("dve", lambda e, oc=oc, sl=sl, ps=ps: e.tensor_tensor(
                    out=X[:, oc, sl], in0=ps, in1=X[:, oc, sl], op=ALU.add),
                    reads=[pstk, Xtk[oc][tg]], writes=[Xtk[oc][tg]])
    P.barrier()
    cx.release(mk)
    return X, Xtk


def emit_store(cx, X, Xtk, out_dram):
    P = cx.P
    for c in range(NC8):
        P.dma("sync", out_dram[c * 128:(c + 1) * 128, :], X[:, c, :], reads=Xtk[c])


def build_layer0(debug=False, lvl=9, hps=8):
    nc = bass.Bass("TRN2", target_bir_lowering=False)
    xT_ext = nc.dram_tensor("xT_ext", [D, 2 * NT], F32, kind="ExternalInput").ap()
    pT = nc.dram_tensor("pT", [256, NT], F32, kind="ExternalInput").ap()
    cst = nc.dram_tensor("cst", [128, G0_END], F32, kind="ExternalInput").ap()
    wqkv = nc.dram_tensor("a_w_qkv", [D, 9216], F32, kind="ExternalInput").ap()
    wo = nc.dram_tensor("a_w_o", [D, D], F32, kind="ExternalInput").ap()
    w1 = nc.dram_tensor("mlp_w1", [D, 4096], F32, kind="ExternalInput").ap()
    w2 = nc.dram_tensor("mlp_w2", [4096, D], F32, kind="ExternalInput").ap()
    wg = nc.dram_tensor("ple_w_gate", [D, D], F32, kind="ExternalInput").ap()
    wp = nc.dram_tensor("ple_w_proj", [256, D], F32, kind="ExternalInput").ap()
    out = nc.dram_tensor("xout", [D, NT], F32, kind="ExternalOutput").ap()
    if debug:
        dbg_a = nc.dram_tensor("dbg_a", [D, NT], F32, kind="ExternalOutput").ap()
        dbg_m = nc.dram_tensor("dbg_m", [D, NT], F32, kind="ExternalOutput").ap()
    cx = Ctx(nc)
    P = cx.P
    cf, cb, ctk = load_consts(cx, None, cst, G0_END)
    cx.eps_col = cf[:, C_EPS:C_EPS + 1]
    ones_bf = cb[:, C_ONES:C_ONES + 128]
    TOP = cx.sb(None, [128, 16384], F32, "TOP")
    R1 = cx.sb(None, [128, 12288], F32, "R1")
    mk = cx.mark()
    X, Xtk = emit_attention(cx, xT_ext, wqkv, wo, cf, cb, ctk, TOP, R1, lvl=lvl, hps=hps)
    cx.release(mk)
    cx.top = cx.top - 12288 * 4
    if debug:
        emit_store(cx, X, Xtk, dbg_a)
    emit_mlp(cx, X, Xtk, cf[:, G0_MLPN:G0_MLPN + 8], ctk, w1, w2, ones_bf)
    if debug:
        emit_store(cx, X, Xtk, dbg_m)
    emit_ple(cx, X, Xtk, cf[:, G0_PLEN:G0_PLEN + 8], ctk, wg, wp, pT, ones_bf)
    emit_store(cx, X, Xtk, out)
    P.finish()
    return nc, cx


def layer0_inputs(inputs, core):
    x = inputs["x"][0]
    lo = core * NT
    xe = np.zeros((2 * NT, D), np.float32)
    if core > 0:
        xe[:NT] = x[lo - NT:lo]
    xe[NT:] = x[lo:lo + NT]
    c = base_consts(core, G0_END)
    c[:, G0_ANORM:G0_ANORM + 8] = col_layout(inputs["a_norm"][0])
    c[:, G0_QG:G0_QG + 3] = np.tile(inputs["a_q_gain"][0].T, (2, 1))
    c[:, G0_KG:G0_KG + 3] = np.tile(inputs["a_k_gain"][0].T, (2, 1))
    c[:, G0_MLPN:G0_MLPN + 8] = col_layout(inputs["mlp_norm"][0])
    c[:, G0_PLEN:G0_PLEN + 8] = col_layout(inputs["ple_norm"][0])
    return {
        "xT_ext": np.ascontiguousarray(xe.T),
        "pT": np.ascontiguousarray(inputs["p"][0, 0, lo:lo + NT].T),
        "cst": c,
        "a_w_qkv": inputs["a_w_qkv"][0], "a_w_o": inputs["a_w_o"][0],
        "mlp_w1": inputs["mlp_w1"][0], "mlp_w2": inputs["mlp_w2"][0],
        "ple_w_gate": inputs["ple_w_gate"][0], "ple_w_proj": inputs["ple_w_proj"][0],
    }

BH = 4
DH = 512
NCK = NT // 128
KSCALE = DH ** -0.5
NST = 8192 + 2048 + 8

L_BNORM = C_GAINS
L_MLPN = L_BNORM + 8
L_PLEN = L_MLPN + 8
L_CONVW = L_PLEN + 8
L_CONVB = L_CONVW + 64
L_SKIP = L_CONVB + 16
L_HGAIN = L_SKIP + 16
L_BI = L_HGAIN + 16
L_BF = L_BI + 1
L_MASKLOW = L_BF + 1
L_SEL = L_MASKLOW + 128
L_CMASK = L_SEL + 512
L_NEG = L_CMASK + 7
L_E0 = L_NEG + 1
L_LNK = L_E0 + 1
L_CNEG = L_LNK + 1
L_END = L_CNEG + 7


def layer1_consts(inputs, core):
    c = base_consts(core, L_END)
    c[:, L_BNORM:L_BNORM + 8] = col_layout(inputs["b_norm"][0])
    c[:, L_MLPN:L_MLPN + 8] = col_layout(inputs["mlp_norm"][1])
    c[:, L_PLEN:L_PLEN + 8] = col_layout(inputs["ple_norm"][1])
    cw = inputs["b_conv_w"][0]
    c[:, L_CONVW:L_CONVW + 64] = cw.reshape(4, 16, 128).transpose(2, 1, 0).reshape(128, 64)
    c[:, L_CONVB:L_CONVB + 16] = col_layout(inputs["b_conv_b"][0])
    c[:, L_SKIP:L_SKIP + 16] = col_layout(inputs["b_skip"][0])
    c[:, L_HGAIN:L_HGAIN + 16] = col_layout(inputs["b_h_gain"][0])
    bg = inputs["b_b_gate"][0]
    c[0:4, L_BI] = bg[0:4]
    c[0:4, L_BF] = bg[4:8]
    s_ = np.arange(128)[:, None]
    t_ = np.arange(128)[None, :]
    c[:, L_MASKLOW:L_MASKLOW + 128] = np.where(s_ <= t_, 0.0, BIG)
    for hd in range(4):
        c[hd, L_SEL + hd * 128:L_SEL + (hd + 1) * 128] = 1.0
    for cp in range(7):
        c[:, L_CMASK + cp] = 1.0 if cp < core else 0.0
        c[:, L_CNEG + cp] = 0.0 if cp < core else -1e30
    c[:, L_NEG] = -1e30
    c[0, L_E0] = 1.0
    c[:, L_LNK] = np.log(KSCALE)
    return c


def bd_compact(w, transpose=False):
    out = np.zeros((2048, 128), np.float32)
    n = np.arange(512)
    for j in range(4):
        for k in range(4):
            if transpose:
                out[4 * n + k, (4 * n + j) % 128] = w[:, j, k]
            else:
                out[4 * n + j, (4 * n + k) % 128] = w[:, j, k]
    return out


def layer1_inputs(inputs, core, x1T_full, stage, st_all=None, g_in=None):
    lo = core * NT
    xh = np.zeros((D, 4), np.float32)
    if core > 0:
        xh[:, 1:4] = x1T_full[:, lo - 3:lo]
    m = {
        "x1T": np.ascontiguousarray(x1T_full[:, lo:lo + NT]),
        "xh": xh,
        "cst": layer1_consts(inputs, core),
        "b_w_up": inputs["b_w_up"][0],
        "bd": np.stack([bd_compact(inputs["b_w_q"][0]), bd_compact(inputs["b_w_k"][0]), bd_compact(inputs["b_w_v"][0])]),
        "bdT": np.stack([bd_compact(inputs["b_w_q"][0], True), bd_compact(inputs["b_w_k"][0], True),
                         bd_compact(inputs["b_w_v"][0], True)]),
        "w_gate": inputs["b_w_gate"][0],
    }
    if stage == "C":
        m.update({
            "st_all": st_all,
            "g_in": g_in,
            "b_w_down": inputs["b_w_down"][0],
            "pT": np.ascontiguousarray(inputs["p"][1, 0, lo:lo + NT].T),
            "mlp_w1": inputs["mlp_w1"][1], "mlp_w2": inputs["mlp_w2"][1],
            "ple_w_gate": inputs["ple_w_gate"][1], "ple_w_proj": inputs["ple_w_proj"][1],
        })
    return m


def build_layer1(stage, debug=False, dbg_stop=None):
    nc = bass.Bass("TRN2", target_bir_lowering=False)
    x1T = nc.dram_tensor("x1T", [D, NT], F32, kind="ExternalInput").ap()
    xh = nc.dram_tensor("xh", [D, 4], F32, kind="ExternalInput").ap()
    cst = nc.dram_tensor("cst", [128, L_END], F32, kind="ExternalInput").ap()
    wup = nc.dram_tensor("b_w_up", [D, 4096], F32, kind="ExternalInput").ap()
    bd = nc.dram_tensor("bd", [3, 2048, 128], F32, kind="ExternalInput").ap()
    bdT = nc.dram_tensor("bdT", [3, 2048, 128], F32, kind="ExternalInput").ap()
    wgate = nc.dram_tensor("w_gate", [6144, 8], F32, kind="ExternalInput").ap()
    if stage == "B":
        st_out = nc.dram_tensor("st_out", [128, NST], F32, kind="ExternalOutput").ap()
        g_out = nc.dram_tensor("g_out", [8, NT], F32, kind="ExternalOutput").ap()
    else:
        st_all = nc.dram_tensor("st_all", [7, 128, NST], F32, kind="ExternalInput").ap()
        g_in = nc.dram_tensor("g_in", [8, NT], F32, kind="ExternalInput").ap()
        wdown = nc.dram_tensor("b_w_down", [2048, D], F32, kind="ExternalInput").ap()
        pT = nc.dram_tensor("pT", [256, NT], F32, kind="ExternalInput").ap()
        w1 = nc.dram_tensor("mlp_w1", [D, 4096], F32, kind="ExternalInput").ap()
        w2 = nc.dram_tensor("mlp_w2", [4096, D], F32, kind="ExternalInput").ap()
        wg = nc.dram_tensor("ple_w_gate", [D, D], F32, kind="ExternalInput").ap()
        wp = nc.dram_tensor("ple_w_proj", [256, D], F32, kind="ExternalInput").ap()
        out = nc.dram_tensor("xout", [D, NT], F32, kind="ExternalOutput").ap()
        yscr = nc.dram_tensor("yscr", [2048, NT], BF16).ap()
        if debug:
            dbg_a = nc.dram_tensor("dbg_a", [D, NT], F32, kind="ExternalOutput").ap()
    cx = Ctx(nc)
    P = cx.P
    cf, cb, ctk = load_consts(cx, None, cst, L_END)
    cx.eps_col = cf[:, C_EPS:C_EPS + 1]
    ones_bf = cb[:, C_ONES:C_ONES + 128]
    ones_f = cf[:, C_ONES:C_ONES + 128]
    ident_f = cf[:, C_ID:C_ID + 128]
    one_col = cf[:, C_ONES:C_ONES + 1]
    TOP = cx.sb(None, [128, 16384], F32, "TOP")
    hT = TOP[:, 0:8208].bitcast(BF16)[:, 0:8 * 2052].rearrange("p (c n) -> p c n", c=NC8)
    topfree = TOP[:, 8208:16384]
    base_mark = cx.mark()
    bk = [(cx.banks[i], Tk()) for i in range(8)]
    ws = WStream(cx, None, 4096, nstage=0, nslot=3)
    ws.stage = Rot([topfree[:, 0:4096]])
    bdh = cx.sb(None, [128, 3, 4, 128], BF16, "bdh")
    bdh_st = cx.sb(None, [128, 3, 4, 128], F32, "bdh_st")
    diag = cx.sb(None, [128, 4, 4, 128], BF16, "diag")
    xms = [cx.sb(None, [128, 4, 516], BF16, "xm") for _ in range(2)]
    xc = cx.sb(None, [128, 4, 512], BF16, "xc")
    GF = cx.sb(None, [128, NT], F32, "GF")
    BETAx = cx.sb(None, [128, NT + 1], F32, "BETAx")
    small = cx.sb(None, [128, 64], F32, "small")
    TMw = cx.sb(None, [128, NCK, 4], F32, "TMw")
    TMa = cx.sb(None, [128, NCK, 4], F32, "TMa")
    if stage == "C":
        fold = cx.sb(None, [128, 7, 8], F32, "foldin")
        S1 = cx.sb(None, [128, 7, 4], F32, "S1")
        S2 = cx.sb(None, [128, 7, 4], F32, "S2")
        mrun = cx.sb(None, [128, 4], F32, "mrun")
        fa = cx.sb(None, [128, 4], F32, "fa")
        fb = cx.sb(None, [128, 4], F32, "fb")
        fc_ = cx.sb(None, [128, 4], F32, "fc")
    pers_mark = cx.mark()

    mk = cx.mark()
    sq_rot = Rot([cx.sb(None, [128, 512], BF16, "sq") for _ in range(2)])
    rstd_rot = Rot([cx.sb(None, [128, 512], F32, "rstd") for _ in range(2)])
    xstg = [cx.sb(None, [128, NC8, 512], F32, "xstg") for _ in range(2)]
    xstk = [Tk(), Tk()]
    ps_stat = Rot([bk[7], bk[6]])
    gcol = cf[:, L_BNORM:L_BNORM + 8]
    httk = Tk()
    pieces = [(None, 4)] + [(tg, 512) for tg in range(4)]
    for i, (tg, n) in enumerate(pieces):
        xa = xstg[i % 2]
        xt = xstk[i % 2]
        if tg is None:
            P.dma("sync", xa[:, :, 0:4], xh.rearrange("(c p) n -> p c n", p=128), writes=[xt])
            h0 = 0
        else:
            P.dma("sync", xa, x1T[:, tg * 512:(tg + 1) * 512].rearrange("(c p) n -> p c n", p=128), writes=[xt])
            h0 = 4 + tg * 512
        xs = [(xa[:, c, 0:n], xt) for c in range(NC8)]
        ps_ap, ps_tk = ps_stat.next()
        rstd, rtk = rstd_rot.next()
        rms_stats(cx, xs, n, sq_rot, ps_ap, ps_tk, rstd, rtk, ones_bf, ctk, 1.0 / D)
        for c in range(NC8):
            P.op("dve", lambda e, c=c, xa=xa, rstd=rstd, n=n, h0=h0: e.scalar_tensor_tensor(
                out=hT[:, c, h0:h0 + n], in0=xa[:, c, 0:n], scalar=gcol[:, c:c + 1], in1=rstd[:, 0:n],
                op0=ALU.mult, op1=ALU.mult), reads=[xt, rtk, ctk], writes=[httk])
    P.barrier()
    cx.release(mk)

    GI = cx.sb(None, [128, NT], F32, "GI")
    LF = cx.sb(None, [128, NT], F32, "LF")
    BB = cx.sb(None, [128, NT], F32, "BB")
    T1 = cx.sb(None, [128, NT], F32, "T1")
    wfold = [[cx.sb(None, [128, 16, 128], BF16, "wfold") for _ in range(2)] for _ in range(2)]
    wftk = Tk()
    bdT_sb = topfree[:, 0:6144].rearrange("p (a b) -> p a b", a=48)
    wg_sb = topfree[:, 6144:6528].rearrange("p (a b) -> p a b", a=48)
    btk = Tk()
    for j in range(3 if stage == "B" else 0):
        P.dma("sync", bdT_sb[:, j * 16:(j + 1) * 16, :], bdT[j].rearrange("(c p) n -> p c n", p=128), writes=[btk])
    if stage == "B":
        P.dma("sync", wg_sb, wgate.rearrange("(c p) n -> p c n", p=128), writes=[btk])
    zpad = Rot([topfree[:, 6528 + i * 128:6528 + (i + 1) * 128] for i in range(4)])
    for (za, ztk) in zpad.items:
        P.op("pool", lambda e, za=za: e.memset(za, 0.0), writes=[ztk])
    psF = Rot(bk[0:2])
    for mc in range(16 if stage == "B" else 0):
        for part in range(2):
            for xm_ in range(2):
                ps, pstk = psF.next()
                srcs = (0, 1) if xm_ == 0 else (2,)
                for si, j in enumerate(srcs):
                    za, ztk = zpad.next()
                    P.op("dve", lambda e, za=za, j=j, mc=mc, part=part: e.tensor_copy(
                        out=za[:, 0:4], in_=wg_sb[:, j * 16 + mc, part * 4:part * 4 + 4]), reads=[btk], writes=[ztk])
                    P.op("pe", lambda e, ps=ps, za=za, j=j, mc=mc, si=si, srcs=srcs: e.matmul(
                        ps[:, 0:128], lhsT=bdT_sb[:, j * 16 + mc, :], rhs=za, start=(si == 0), stop=(si == len(srcs) - 1)),
                        reads=[btk, ztk], writes=[pstk])
                P.op("act", lambda e, ps=ps, xm_=xm_, part=part, mc=mc: e.activation(
                    out=wfold[xm_][part][:, mc, :], in_=ps[:, 0:128], func=AF.Copy), reads=[pstk], writes=[wftk])
    P.barrier()

    hdtk = Tk()
    xmtk = [Tk(), Tk()]
    xctk = Tk()
    psA = Rot(bk[0:2])
    psA2 = psA
    wupv = wup.rearrange("(c p) n -> p c n", p=128)
    bdv = bd.rearrange("j (c p) n -> p j c n", p=128)
    state = {"i": 0}

    def head_setup(hd):
        for j in range(3):
            P.dma("sync", bdh_st[:, j, :, :], bdv[:, j, hd * 4:(hd + 1) * 4, :], writes=[hdtk])
        P.op("pool", lambda e: e.tensor_copy(out=bdh, in_=bdh_st), reads=[hdtk], writes=[hdtk])
        for mc in range(4):
            for k in range(4):
                col = L_CONVW + (hd * 4 + mc) * 4 + k
                P.op("act", lambda e, mc=mc, k=k, col=col: e.activation(
                    out=diag[:, mc, k, :], in_=ident_f, func=AF.Copy, scale=cf[:, col:col + 1]),
                    reads=[ctk], writes=[hdtk])
        wx, wxtk = ws.load([wupv[:, :, hd * 512:(hd + 1) * 512]])
        return wx.rearrange("p (c n) -> p c n", c=NC8), wxtk

    def front(hd, tg, wx3, wxtk, psA=None):
        psA = psA or psA2
        i = state["i"]
        state["i"] += 1
        xm, xmt = xms[i % 2], xmtk[i % 2]
        xmp, xmpt = xms[(i + 1) % 2], xmtk[(i + 1) % 2]
        for mc in range(4):
            ps, pstk = psA.next()
            for c in range(NC8):
                P.op("pe", lambda e, c=c, mc=mc, ps=ps: e.matmul(
                    ps, lhsT=wx3[:, c, mc * 128:(mc + 1) * 128], rhs=hT[:, c, 4 + tg * 512:4 + (tg + 1) * 512],
                    start=(c == 0), stop=(c == NC8 - 1)), reads=[wxtk], writes=[pstk], signal=(c == NC8 - 1))
            P.op("act", lambda e, ps=ps, mc=mc, xm=xm: e.activation(out=xm[:, mc, 4:516], in_=ps, func=AF.Copy),
                 reads=[pstk], writes=[xmt])
            if tg == 0:
                ps, pstk = psA.next()
                for c in range(NC8):
                    P.op("pe", lambda e, c=c, mc=mc, ps=ps: e.matmul(
                        ps[:, 0:4], lhsT=wx3[:, c, mc * 128:(mc + 1) * 128], rhs=hT[:, c, 0:4],
                        start=(c == 0), stop=(c == NC8 - 1)), reads=[wxtk], writes=[pstk], signal=(c == NC8 - 1))
                P.op("act", lambda e, ps=ps, mc=mc, xm=xm: e.activation(out=xm[:, mc, 0:4], in_=ps[:, 0:4], func=AF.Copy),
                     reads=[pstk], writes=[xmt])
        if tg > 0:
            P.op("pool", lambda e, xm=xm, xmp=xmp: e.tensor_copy(out=xm[:, :, 0:4], in_=xmp[:, :, 512:516]),
                 reads=[xmpt], writes=[xmt])
        for mc in range(4):
            ps, pstk = psA.next()
            for k in range(4):
                P.op("pe", lambda e, k=k, mc=mc, ps=ps, xm=xm: e.matmul(
                    ps, lhsT=diag[:, mc, k, :], rhs=xm[:, mc, 1 + k:1 + k + 512], start=(k == 0), stop=(k == 3)),
                    reads=[hdtk, xmt], writes=[pstk], signal=(k == 3))
            col = L_CONVB + hd * 4 + mc
            P.op("act", lambda e, ps=ps, mc=mc, col=col: e.activation(
                out=xc[:, mc, :], in_=ps, func=AF.Silu, bias=cf[:, col:col + 1]), reads=[pstk, ctk], writes=[xctk])
        return xm, xmt

    gtk = Tk()
    psG = Rot(bk[2:4])
    psFG = Rot([bk[0], bk[1], bk[4], bk[5], bk[6], bk[7]])
    psFS = Rot([bk[0], bk[1], bk[2]])
    psFC = Rot([bk[0], bk[1], bk[2], bk[3]])
    if stage == "C":
        P.op("pool", lambda e: e.memset(GI, 0.0), writes=[gtk])
        P.op("pool", lambda e: e.memset(GF, 0.0), writes=[gtk])
        P.dma("sync", GI[0:4, :], g_in[0:4, :], writes=[gtk])
        P.dma("sync", GF[0:4, :], g_in[4:8, :], writes=[gtk])
    for hd in range(BH if stage == "B" else 0):
        wx3, wxtk = head_setup(hd)
        for tg in range(4):
            xm, xmt = front(hd, tg, wx3, wxtk, psFG)
            for part, Grow in ((0, GI), (1, GF)):
                ps, pstk = psG.next()
                for mc in range(4):
                    P.op("pe", lambda e, ps=ps, mc=mc, part=part: e.matmul(
                        ps, lhsT=wfold[0][part][:, hd * 4 + mc, :], rhs=xc[:, mc, :], start=(mc == 0), stop=False),
                        reads=[wftk, xctk], writes=[pstk], signal=False)
                    P.op("pe", lambda e, ps=ps, mc=mc, part=part, xm=xm: e.matmul(
                        ps, lhsT=wfold[1][part][:, hd * 4 + mc, :], rhs=xm[:, mc, 4:516], start=False, stop=(mc == 3)),
                        reads=[wftk, xmt], writes=[pstk], signal=(mc == 3))
                sl = slice(tg * 512, (tg + 1) * 512)
                if hd == 0:
                    P.op("act", lambda e, ps=ps, Grow=Grow, sl=sl: e.activation(out=Grow[:, sl], in_=ps, func=AF.Copy),
                         reads=[pstk], writes=[gtk])
                else:
                    P.op("dve", lambda e, ps=ps, Grow=Grow, sl=sl: e.tensor_tensor(out=Grow[:, sl], in0=ps, in1=Grow[:, sl], op=ALU.add),
                         reads=[pstk, gtk], writes=[gtk])

    rtk = Tk()
    if stage == "B":
        P.dma("sync", g_out[0:4, :], GI[0:4, :], reads=[gtk])
        P.dma("sync", g_out[4:8, :], GF[0:4, :], reads=[gtk])
    P.op("dve", lambda e: e.tensor_scalar(out=GI, in0=GI, scalar1=cf[:, L_BI:L_BI + 1], scalar2=None, op0=ALU.add),
         reads=[gtk, ctk], writes=[gtk])
    P.op("dve", lambda e: e.tensor_scalar(out=GF, in0=GF, scalar1=cf[:, L_BF:L_BF + 1], scalar2=None, op0=ALU.add),
         reads=[gtk, ctk], writes=[gtk])
    P.op("dve", lambda e: e.tensor_scalar(out=T1, in0=GF, scalar1=-1.0, scalar2=None, op0=ALU.mult), reads=[gtk], writes=[rtk])
    P.op("dve", lambda e: e.tensor_tensor(out=T1, in0=T1, in1=GF, op=ALU.max), reads=[gtk, rtk], writes=[rtk])
    P.op("act", lambda e: e.activation(out=T1, in_=T1, func=AF.Exp, scale=-1.0), reads=[rtk], writes=[rtk])
    P.op("act", lambda e: e.activation(out=T1, in_=T1, func=AF.Ln, bias=one_col), reads=[rtk, ctk], writes=[rtk])
    P.op("dve", lambda e: e.scalar_tensor_tensor(out=LF, in0=GF, scalar=0.0, in1=T1, op0=ALU.min, op1=ALU.subtract),
         reads=[gtk, rtk], writes=[rtk])
    P.op("pool", lambda e: e.memset(T1, 1.0), reads=[rtk], writes=[rtk])
    P.op("dve", lambda e: e.tensor_tensor_scan(out=BB, data0=T1, data1=LF, initial=0.0, op0=ALU.mult, op1=ALU.add),
         reads=[rtk], writes=[rtk])
    P.op("dve", lambda e: e.tensor_tensor(out=T1, in0=GI, in1=BB, op=ALU.subtract), reads=[gtk, rtk], writes=[rtk])
    ALPHA = T1
    psR = Rot([bk[4]])
    tmtk = Tk()

    def to_token_major(row, dst):
        ps, pstk = psR.next()
        for ck in range(NCK):
            P.op("pe", lambda e, ps=ps, ck=ck: e.matmul(ps[:, ck * 4:ck * 4 + 4], lhsT=row[:, ck * 128:(ck + 1) * 128],
                                                        rhs=ident_f[:, 0:4], start=True, stop=True),
                 reads=[rtk, gtk, ctk], writes=[pstk], signal=(ck == NCK - 1))
        P.op("act", lambda e, ps=ps: e.activation(out=dst, in_=ps[:, 0:64].rearrange("p (a b) -> p a b", a=NCK), func=AF.Copy),
             reads=[pstk], writes=[tmtk])

    def replicate_cols(col_ap, dst4):
        ps, pstk = psR.next()
        za = small[:, 32:32 + 4]
        P.op("dve", lambda e: e.tensor_scalar(out=za, in0=ident_f[:, 0:4], scalar1=col_ap, scalar2=None, op0=ALU.mult),
             reads=[rtk, ctk, gtk], writes=[rtk])
        P.op("pe", lambda e, ps=ps: e.matmul(ps[:, 0:4], lhsT=ones_f, rhs=za, start=True, stop=True),
             reads=[rtk, ctk], writes=[pstk])
        P.op("act", lambda e, ps=ps: e.activation(out=dst4, in_=ps[:, 0:4], func=AF.Copy), reads=[pstk], writes=[rtk])

    if stage == "B":
        mx = small[:, 0:1]
        P.op("dve", lambda e: e.tensor_reduce(out=mx, in_=ALPHA, axis=AX.X, op=ALU.max), reads=[rtk], writes=[rtk])
        nb_ = small[:, 1:2]
        P.op("dve", lambda e: e.scalar_tensor_tensor(out=nb_, in0=mx, scalar=-1.0, in1=cf[:, L_LNK:L_LNK + 1],
                                                     op0=ALU.mult, op1=ALU.add), reads=[rtk, ctk], writes=[rtk])
        P.op("act", lambda e: e.activation(out=LF, in_=ALPHA, func=AF.Exp, bias=nb_), reads=[rtk], writes=[rtk])
        to_token_major(LF, TMw)
        ml = small[:, 2:3]
        P.op("dve", lambda e: e.tensor_tensor(out=ml, in0=mx, in1=BB[:, NT - 1:NT], op=ALU.add), reads=[rtk], writes=[rtk])
        fin = cx.sb(None, [128, 8], F32, "fin")
        replicate_cols(BB[:, NT - 1:NT], fin[:, 0:4])
        replicate_cols(ml, fin[:, 4:8])
        P.dma("sync", st_out[:, 10240:10248], fin, reads=[rtk])
        P.barrier()
        cx.release(pers_mark)
        kv_rot = Rot([cx.sb(None, [128, 512], BF16, "kv") for _ in range(4)])
        stC = cx.sb(None, [128, 4, 512], F32, "stC")
        stn = cx.sb(None, [128, 512], F32, "stn")
        sttk = Tk()
        psKV = Rot([bk[0], bk[1], bk[2]])
        for hd in range(BH):
            wx3, wxtk = head_setup(hd)
            cacc = [bk[3 + dc] for dc in range(4)]
            nacc, nacctk = bk[7]
            for tg in range(4):
                xm, xmt = front(hd, tg, wx3, wxtk, psFS)
                KV = {}

                def s1_pre(cl):
                    ck = tg * 4 + cl
                    tsl = slice(cl * 128, (cl + 1) * 128)
                    ps, pstk = psKV.next()
                    for mc in range(4):
                        P.op("pe", lambda e, ps=ps, mc=mc, tsl=tsl: e.matmul(
                            ps[:, mc * 128:(mc + 1) * 128], lhsT=xc[:, mc, tsl], rhs=bdh[:, 1, mc, :], start=True, stop=True),
                            reads=[xctk, hdtk], writes=[pstk], signal=(mc == 3))
                    wk, wktk = kv_rot.next()
                    P.op("act", lambda e, ps=ps, wk=wk, ck=ck, hd=hd: e.activation(
                        out=wk, in_=ps, func=AF.Copy, scale=TMw[:, ck, hd:hd + 1]), reads=[pstk, tmtk], writes=[wktk])
                    ps, pstk = psKV.next()
                    for mc in range(4):
                        P.op("pe", lambda e, ps=ps, mc=mc, cl=cl, xm=xm: e.matmul(
                            ps[:, mc * 128:(mc + 1) * 128], lhsT=xm[:, mc, 4 + cl * 128:4 + (cl + 1) * 128], rhs=bdh[:, 2, mc, :],
                            start=True, stop=True), reads=[xmt, hdtk], writes=[pstk], signal=(mc == 3))
                    vv, vtk = kv_rot.next()
                    P.op("act", lambda e, ps=ps, vv=vv: e.activation(out=vv, in_=ps, func=AF.Copy), reads=[pstk], writes=[vtk])
                    KV[cl] = (ck, wk, wktk, vv, vtk)

                def s1_acc(cl):
                    ck, wk, wktk, vv, vtk = KV[cl]
                    last = (ck == NCK - 1)
                    for dc in range(4):
                        P.op("pe", lambda e, dc=dc, wk=wk, vv=vv, ck=ck, last=last: e.matmul(
                            cacc[dc][0], lhsT=wk[:, dc * 128:(dc + 1) * 128], rhs=vv, start=(ck == 0), stop=last),
                            reads=[wktk, vtk], writes=[cacc[dc][1]], signal=True)
                    P.op("pe", lambda e, wk=wk, ck=ck, last=last: e.matmul(
                        nacc, lhsT=ones_bf, rhs=wk, start=(ck == 0), stop=last), reads=[wktk, ctk], writes=[nacctk], signal=True)
                s1_pre(0)
                for cl in range(1, 4):
                    s1_pre(cl)
                    s1_acc(cl - 1)
                s1_acc(3)
            for dc in range(4):
                P.op("act", lambda e, dc=dc: e.activation(out=stC[:, dc, :], in_=cacc[dc][0], func=AF.Copy),
                     reads=[cacc[dc][1]], writes=[sttk])
            P.op("dve", lambda e: e.tensor_copy(out=stn, in_=nacc), reads=[nacctk], writes=[sttk])
            P.dma("sync", st_out[:, hd * 2048:(hd + 1) * 2048], stC.rearrange("p a b -> p (a b)"), reads=[sttk])
            P.dma("sync", st_out[:, 8192 + hd * 512:8192 + (hd + 1) * 512], stn, reads=[sttk])
        P.finish()
        return nc, cx

    ftk = Tk()
    P.dma("sync", fold, st_all[:, :, 10240:10248].rearrange("c p n -> p c n"), writes=[ftk])
    negc = cf[:, L_NEG:L_NEG + 1]
    P.op("dve", lambda e: e.memset(mrun, -1e30), writes=[ftk])
    for cp in range(7):
        mu = cf[:, L_CMASK + cp:L_CMASK + cp + 1]
        P.op("dve", lambda e, cp=cp, mu=mu: e.scalar_tensor_tensor(out=fa, in0=fold[:, cp, 0:4], scalar=mu, in1=mrun,
                                                                    op0=ALU.mult, op1=ALU.add), reads=[ftk, ctk], writes=[ftk])
        P.op("dve", lambda e, cp=cp, mu=mu: e.tensor_scalar(out=fb, in0=fold[:, cp, 4:8], scalar1=mu,
                                                            scalar2=cf[:, L_CNEG + cp:L_CNEG + cp + 1], op0=ALU.mult, op1=ALU.add),
             reads=[ftk, ctk], writes=[ftk])
        P.op("dve", lambda e: e.tensor_tensor(out=fc_, in0=fa, in1=fb, op=ALU.max), reads=[ftk], writes=[ftk])
        P.op("dve", lambda e: e.tensor_tensor(out=fa, in0=fa, in1=fc_, op=ALU.subtract), reads=[ftk], writes=[ftk])
        P.op("dve", lambda e: e.tensor_tensor(out=fb, in0=fb, in1=fc_, op=ALU.subtract), reads=[ftk], writes=[ftk])
        P.op("act", lambda e, cp=cp: e.activation(out=S1[:, cp, :], in_=fa, func=AF.Exp), reads=[ftk], writes=[ftk])
        P.op("act", lambda e: e.activation(out=fb, in_=fb, func=AF.Exp), reads=[ftk], writes=[ftk])
        P.op("dve", lambda e, cp=cp, mu=mu: e.tensor_scalar(out=S2[:, cp, :], in0=fb, scalar1=mu, scalar2=None, op0=ALU.mult),
             reads=[ftk, ctk], writes=[ftk])
        P.op("dve", lambda e: e.tensor_copy(out=mrun, in_=fc_), reads=[ftk], writes=[ftk])
    mst = small[:, 4:5]
    P.op("dve", lambda e: e.tensor_tensor(out=small[:, 8:12], in0=mrun, in1=ident_f[:, 0:4], op=ALU.mult), reads=[ftk, ctk], writes=[rtk])
    P.op("dve", lambda e: e.tensor_reduce(out=mst, in_=small[:, 8:12], axis=AX.X, op=ALU.add), reads=[rtk], writes=[rtk])
    P.op("dve", lambda e: e.tensor_tensor_scan(out=GF, data0=LF, data1=GI, initial=mst, op0=ALU.add, op1=ALU.max),
         reads=[rtk, gtk], writes=[gtk])
    MM = GF
    P.op("dve", lambda e: e.tensor_tensor(out=BETAx[:, 1:NT + 1], in0=MM, in1=BB, op=ALU.subtract), reads=[gtk, rtk], writes=[rtk])
    P.op("dve", lambda e: e.tensor_copy(out=BETAx[:, 0:1], in_=mst), reads=[rtk], writes=[rtk])
    BETA = BETAx[:, 1:NT + 1]
    for ck in range(NCK):
        bl = small[:, 16:17]
        P.op("dve", lambda e, ck=ck: e.scalar_tensor_tensor(out=small[:, 16 + ck % 8:17 + ck % 8], in0=BETAx[:, 128 * (ck + 1):128 * (ck + 1) + 1],
                                                            scalar=-1.0, in1=cf[:, L_LNK:L_LNK + 1], op0=ALU.mult, op1=ALU.add),
             reads=[rtk, ctk], writes=[rtk])
        P.op("act", lambda e, ck=ck: e.activation(out=LF[:, ck * 128:(ck + 1) * 128], in_=ALPHA[:, ck * 128:(ck + 1) * 128],
                                                  func=AF.Exp, bias=small[:, 16 + ck % 8:17 + ck % 8]), reads=[rtk], writes=[rtk])
    to_token_major(LF, TMw)
    to_token_major(ALPHA, TMa)
    P.barrier()
    cx.release(pers_mark)
    BETA = BETAx[:, 1:NT + 1]

    qT = cx.sb(None, [128, 4, 512], BF16, "qT")
    kT = cx.sb(None, [128, 4, 512], BF16, "kT")
    zs = cx.sb(None, [128, 4, 512], BF16, "zs")
    yb = cx.sb(None, [128, 4, 512], BF16, "yb")
    qktk, zstk, ytk = Tk(), Tk(), Tk()
    Csts = [cx.sb(None, [128, 4, 512], F32, "Cst") for _ in range(2)]
    Caugs = [cx.sb(None, [128, 4, 640], BF16, "Caug") for _ in range(2)]
    caugtks = [Tk(), Tk()]
    nrow = cx.sb(None, [128, 512], F32, "nrow")
    nrtk = Tk()
    ncol = cx.sb(None, [128, 4], F32, "ncol")
    nctk = Tk()
    ctk2s = [Tk(), Tk()]
    cpar = {"p": 0}
    clst = Rot([topfree[:, 4096:6144], topfree[:, 6144:8176][:, 0:2032]])
    wk_rot = Rot([cx.sb(None, [128, 512], BF16, "wk") for _ in range(2)])
    va_rot = Rot([cx.sb(None, [128, 640], BF16, "vaug") for _ in range(2)])
    for (va, vatk) in va_rot.items:
        P.op("pool", lambda e, va=va: e.memset(va[:, 512:640], 1.0), writes=[vatk])
    dt_rot = Rot([cx.sb(None, [128, 128], F32, "dtmp") for _ in range(2)])
    sd_rot = Rot([cx.sb(None, [128, 128], BF16, "SdT") for _ in range(2)])
    qs_rot = Rot([cx.sb(None, [128, 4, 128], BF16, "qs") for _ in range(2)])
    hsq_rot = Rot([cx.sb(None, [128, 512], BF16, "hsq") for _ in range(2)])
    dd_rot = Rot([cx.sb(None, [128, 128], F32, "dd") for _ in range(2)])
    rr_rot = Rot([cx.sb(None, [128, 128], F32, "rr") for _ in range(2)])
    sc_rot = Rot([cx.sb(None, [128, 128], F32, "scsb") for _ in range(2)])
    em_rot = Rot([cx.sb(None, [128, 128], F32, "emsb") for _ in range(2)])
    ul_rot = Rot([cx.sb(None, [128, 1], F32, "ulast") for _ in range(2)])
    tt_rot = Rot([cx.sb(None, [128, 128], F32, "tt") for _ in range(3)])
    psB2 = psA
    psS3 = Rot([bk[2]])
    psRP = Rot([bk[2]])
    psH = Rot([bk[4], bk[5]])
    psDS = Rot([bk[6], bk[7]])
    psSS = Rot([bk[3]])
    wzv = wupv
    yview = yscr.rearrange("(c p) n -> p c n", p=128)
    def emit_fold(hd):
        Cst, ctk2 = Csts[hd % 2], ctk2s[hd % 2]
        P.op("pool", lambda e: e.memset(Cst, 0.0), writes=[ctk2])
        P.op("pool", lambda e: e.memset(nrow, 0.0), writes=[nrtk])
        Cflat = Cst.rearrange("p a b -> p (a b)")
        for cp in range(7):
            cl_, cltk = clst.items[0]
            P.dma("sync", cl_, st_all[cp][:, hd * 2048:(hd + 1) * 2048], writes=[cltk])
            P.op("act", lambda e, cp=cp, cl_=cl_: e.activation(out=cl_, in_=cl_, func=AF.Copy, scale=S2[:, cp, hd:hd + 1]),
                 reads=[cltk, ftk], writes=[cltk])
            P.op("dve", lambda e, cp=cp, cl_=cl_: e.scalar_tensor_tensor(out=Cflat, in0=Cflat, scalar=S1[:, cp, hd:hd + 1], in1=cl_,
                                                                          op0=ALU.mult, op1=ALU.add), reads=[cltk, ftk, ctk2], writes=[ctk2])
            nl_, nltk = clst.items[1]
            P.dma("sync", nl_[:, 0:512], st_all[cp][:, 8192 + hd * 512:8192 + (hd + 1) * 512], writes=[nltk])
            P.op("act", lambda e, cp=cp, nl_=nl_: e.activation(out=nl_[:, 0:512], in_=nl_[:, 0:512], func=AF.Copy, scale=S2[:, cp, hd:hd + 1]),
                 reads=[nltk, ftk], writes=[nltk])
            P.op("dve", lambda e, cp=cp, nl_=nl_: e.scalar_tensor_tensor(out=nrow, in0=nrow, scalar=S1[:, cp, hd:hd + 1], in1=nl_[:, 0:512],
                                                                          op0=ALU.mult, op1=ALU.add), reads=[nltk, ftk, nrtk], writes=[nrtk])

    ul_prev = None
    for hd in range(BH):
        wx3, wxtk = head_setup(hd)
        wz, wztk = ws.load([wzv[:, :, 2048 + hd * 512:2048 + (hd + 1) * 512]])
        wz3 = wz.rearrange("p (c n) -> p c n", c=NC8)
        Cst, ctk2 = Csts[hd % 2], ctk2s[hd % 2]
        if hd == 0:
            emit_fold(0)

        def refresh_caug(full):
            Caug, caugtk = Caugs[cpar["p"]], caugtks[cpar["p"]]
            for dc in range(4):
                P.op("act", lambda e, dc=dc: e.activation(out=Caug[:, dc, 0:512], in_=Cst[:, dc, :], func=AF.Copy),
                     reads=[ctk2], writes=[caugtk])
            P.op("dve", lambda e: e.tensor_scalar(out=nrow, in0=nrow, scalar1=cf[:, L_E0:L_E0 + 1], scalar2=None, op0=ALU.mult),
                 reads=[nrtk, ctk], writes=[nrtk])
            ps, pstk = psB2.next()
            for dc in range(4):
                P.op("pe", lambda e, ps=ps, dc=dc: e.matmul(ps[:, dc:dc + 1], lhsT=nrow[:, dc * 128:(dc + 1) * 128], rhs=ones_f[:, 0:1],
                                                            start=True, stop=True), reads=[nrtk, ctk], writes=[pstk], signal=(dc == 3))
            P.op("dve", lambda e, ps=ps: e.tensor_copy(out=ncol, in_=ps[:, 0:4]), reads=[pstk], writes=[nctk])
            P.op("dve", lambda e: e.tensor_copy(out=Caug[:, :, 512:640], in_=ncol.unsqueeze(2).to_broadcast([128, 4, 128])),
                 reads=[nctk], writes=[caugtk])

        refresh_caug(True)
        for tg in range(4):
            xm, xmt = front(hd, tg, wx3, wxtk, psFC)
            for mc in range(4):
                ps, pstk = psFC.next()
                for c in range(NC8):
                    P.op("pe", lambda e, c=c, mc=mc, ps=ps: e.matmul(
                        ps, lhsT=wz3[:, c, mc * 128:(mc + 1) * 128], rhs=hT[:, c, 4 + tg * 512:4 + (tg + 1) * 512],
                        start=(c == 0), stop=(c == NC8 - 1)), reads=[wztk], writes=[pstk], signal=(c == NC8 - 1))
                P.op("act", lambda e, ps=ps, mc=mc: e.activation(out=zs[:, mc, :], in_=ps, func=AF.Silu), reads=[pstk], writes=[zstk])
            for j, dst, sc_ in ((0, qT, 1.0), (1, kT, KSCALE)):
                for dc in range(4):
                    ps, pstk = psFC.next()
                    P.op("pe", lambda e, ps=ps, j=j, dc=dc: e.matmul(ps, lhsT=bdh[:, j, dc, :], rhs=xc[:, dc, :], start=True, stop=True),
                         reads=[hdtk, xctk], writes=[pstk])
                    P.op("act", lambda e, ps=ps, dst=dst, dc=dc, sc_=sc_: e.activation(out=dst[:, dc, :], in_=ps, func=AF.Copy, scale=sc_),
                         reads=[pstk], writes=[qktk])
            RS = {}

            def stage_pre(cl):
                nonlocal ul_prev
                ck = tg * 4 + cl
                tsl = slice(cl * 128, (cl + 1) * 128)
                gsl = slice(ck * 128, (ck + 1) * 128)
                sel = cf[:, L_SEL + hd * 128:L_SEL + (hd + 1) * 128]
                rp, rptk = psRP.next()
                for i3, row in enumerate((BETA, MM)):
                    P.op("pe", lambda e, rp=rp, i3=i3, row=row, gsl=gsl: e.matmul(
                        rp[:, i3 * 128:(i3 + 1) * 128], lhsT=sel, rhs=row[:, gsl], start=True, stop=True),
                        reads=[rtk, gtk, ctk], writes=[rptk], signal=(i3 == 1))
                bprev = mrun[:, hd:hd + 1] if ck == 0 else ul_prev[0]
                bprev_tk = ftk if ck == 0 else ul_prev[1]
                scsb, sctk = sc_rot.next()
                P.op("act", lambda e, rp=rp, scsb=scsb, bprev=bprev: e.activation(out=scsb, in_=rp[:, 0:128], func=AF.Exp, scale=-1.0, bias=bprev),
                     reads=[rptk, bprev_tk], writes=[sctk])
                emsb, emtk = em_rot.next()
                P.op("act", lambda e, rp=rp, emsb=emsb: e.activation(out=emsb, in_=rp[:, 128:256], func=AF.Exp, scale=-1.0),
                     reads=[rptk], writes=[emtk])
                ul_prev = ul_rot.next()
                P.op("act", lambda e, rp=rp, ul_prev=ul_prev: e.activation(out=ul_prev[0], in_=rp[:, 127:128], func=AF.Copy),
                     reads=[rptk], writes=[ul_prev[1]])
                ps, pstk = psB2.next()
                for mc in range(4):
                    P.op("pe", lambda e, ps=ps, mc=mc, tsl=tsl: e.matmul(
                        ps[:, mc * 128:(mc + 1) * 128], lhsT=xc[:, mc, tsl], rhs=bdh[:, 1, mc, :], start=True, stop=True),
                        reads=[xctk, hdtk], writes=[pstk], signal=(mc == 3))
                wk, wktk = wk_rot.next()
                P.op("act", lambda e, ps=ps, wk=wk, ck=ck: e.activation(out=wk, in_=ps, func=AF.Copy, scale=TMw[:, ck, hd:hd + 1]),
                     reads=[pstk, tmtk], writes=[wktk])
                ps, pstk = psB2.next()
                for mc in range(4):
                    P.op("pe", lambda e, ps=ps, mc=mc, cl=cl, xm=xm: e.matmul(
                        ps[:, mc * 128:(mc + 1) * 128], lhsT=xm[:, mc, 4 + cl * 128:4 + (cl + 1) * 128], rhs=bdh[:, 2, mc, :],
                        start=True, stop=True), reads=[xmt, hdtk], writes=[pstk], signal=(mc == 3))
                va, vatk = va_rot.next()
                P.op("act", lambda e, ps=ps, va=va: e.activation(out=va[:, 0:512], in_=ps, func=AF.Copy), reads=[pstk], writes=[vatk])
                pS_, pStk = psS3.next()
                pS = pS_[:, 256:384]
                for dc in range(4):
                    P.op("pe", lambda e, pS=pS, dc=dc, tsl=tsl: e.matmul(pS, lhsT=kT[:, dc, tsl], rhs=qT[:, dc, tsl],
                                                                          start=(dc == 0), stop=(dc == 3)),
                         reads=[qktk], writes=[pStk], signal=(dc == 3))
                dtmp, dttk = dt_rot.next()
                P.op("dve", lambda e, rp=rp, dtmp=dtmp, ck=ck: e.scalar_tensor_tensor(
                    out=dtmp, in0=rp[:, 0:128], scalar=TMa[:, ck, hd:hd + 1], in1=cf[:, L_MASKLOW:L_MASKLOW + 128],
                    op0=ALU.subtract, op1=ALU.max), reads=[rptk, tmtk, ctk], writes=[dttk])
                P.op("act", lambda e, dtmp=dtmp: e.activation(out=dtmp, in_=dtmp, func=AF.Exp, scale=-1.0), reads=[dttk], writes=[dttk])
                sd, sdtk = sd_rot.next()
                P.op("dve", lambda e, pS=pS, dtmp=dtmp, sd=sd: e.tensor_tensor(out=sd, in0=pS, in1=dtmp, op=ALU.mult),
                     reads=[pStk, dttk], writes=[sdtk])
                qs, qstk = qs_rot.next()
                P.op("dve", lambda e, scsb=scsb, qs=qs, tsl=tsl: e.tensor_tensor(
                    out=qs, in0=qT[:, :, tsl], in1=scsb.unsqueeze(1).to_broadcast([128, 4, 128]), op=ALU.mult),
                    reads=[qktk, sctk], writes=[qstk])

                RS[cl] = dict(ck=ck, tsl=tsl, wk=wk, wktk=wktk, va=va, vatk=vatk, sd=sd, sdtk=sdtk, qs=qs, qstk=qstk,
                              scsb=scsb, sctk=sctk, emsb=emsb, emtk=emtk)

            def stage_mid(cl):
                r_ = RS[cl]
                ck, tsl, wk, wktk, va, vatk, sd, sdtk, qs, qstk, scsb, sctk = (r_[k_] for k_ in (
                    "ck", "tsl", "wk", "wktk", "va", "vatk", "sd", "sdtk", "qs", "qstk", "scsb", "sctk"))
                Caug, caugtk = Caugs[cpar["p"]], caugtks[cpar["p"]]
                CaugN, caugNtk = Caugs[1 - cpar["p"]], caugtks[1 - cpar["p"]]
                cpar["p"] = 1 - cpar["p"]
                if dbg_stop is not None and (hd, ck) == tuple(dbg_stop):
                    P.barrier()
                    P.finish()
                    return nc, cx
                decay = scsb[:, 127:128]
                for dc in range(4):
                    ps, pstk = psB2.next()
                    P.op("pe", lambda e, ps=ps, dc=dc, wk=wk, va=va: e.matmul(ps, lhsT=wk[:, dc * 128:(dc + 1) * 128], rhs=va[:, 0:512],
                                                                                start=True, stop=True), reads=[wktk, vatk], writes=[pstk])
                    P.op("dve", lambda e, ps=ps, dc=dc, decay=decay: e.scalar_tensor_tensor(
                        out=Cst[:, dc, :], in0=Cst[:, dc, :], scalar=decay, in1=ps, op0=ALU.mult, op1=ALU.add),
                        reads=[pstk, sctk, ctk2], writes=[ctk2])
                    P.op("dve", lambda e, dc=dc: e.tensor_copy(out=CaugN[:, dc, 0:512], in_=Cst[:, dc, :]),
                         reads=[ctk2], writes=[caugNtk])
                ps, pstk = psB2.next()
                for dc in range(4):
                    P.op("pe", lambda e, ps=ps, dc=dc, wk=wk: e.matmul(ps[:, dc:dc + 1], lhsT=wk[:, dc * 128:(dc + 1) * 128], rhs=ones_bf[:, 0:1],
                                                                         start=True, stop=True), reads=[wktk, ctk], writes=[pstk], signal=(dc == 3))
                P.op("dve", lambda e, ps=ps, decay=decay: e.scalar_tensor_tensor(out=ncol, in0=ncol, scalar=decay, in1=ps[:, 0:4],
                                                                                  op0=ALU.mult, op1=ALU.add),
                     reads=[pstk, sctk, nctk], writes=[nctk])
                P.op("dve", lambda e: e.tensor_copy(out=CaugN[:, :, 512:640], in_=ncol.unsqueeze(2).to_broadcast([128, 4, 128])),
                     reads=[nctk], writes=[caugNtk])
                pH, pHtk = psH.next()
                pD_, pDtk = psDS.next()
                for ec in range(5):
                    o = pH[:, ec * 128:(ec + 1) * 128] if ec < 4 else pD_[:, 0:128]
                    otk = pHtk if ec < 4 else pDtk
                    for dc in range(4):
                        P.op("pe", lambda e, o=o, ec=ec, dc=dc, qs=qs: e.matmul(
                            o, lhsT=Caug[:, dc, ec * 128:(ec + 1) * 128], rhs=qs[:, dc, :], start=(dc == 0), stop=False),
                            reads=[caugtk, qstk], writes=[otk], signal=False)
                    P.op("pe", lambda e, o=o, ec=ec, va=va, sd=sd: e.matmul(
                        o, lhsT=va[:, ec * 128:(ec + 1) * 128], rhs=sd, start=False, stop=True),
                        reads=[vatk, sdtk], writes=[otk], signal=True)

                r_.update(pH=pH, pHtk=pHtk, pD_=pD_, pDtk=pDtk)

            def stage_post(cl):
                r_ = RS[cl]
                ck, tsl, emsb, emtk, pH, pHtk, pD_, pDtk = (r_[k_] for k_ in ("ck", "tsl", "emsb", "emtk", "pH", "pHtk", "pD_", "pDtk"))
                hsq, hsqtk = hsq_rot.next()
                P.op("act", lambda e, pH=pH, hsq=hsq: e.activation(out=hsq, in_=pH, func=AF.Square), reads=[pHtk], writes=[hsqtk])
                pSS_, pSStk = psSS.next()
                pSS = pSS_[:, 0:128]
                for ec in range(4):
                    P.op("pe", lambda e, pSS=pSS, hsq=hsq, ec=ec: e.matmul(pSS, lhsT=ones_bf, rhs=hsq[:, ec * 128:(ec + 1) * 128],
                                                                            start=(ec == 0), stop=(ec == 3)),
                         reads=[hsqtk, ctk], writes=[pSStk], signal=(ec == 3))
                dd, ddtk = dd_rot.next()
                P.op("dve", lambda e, pD_=pD_, dd=dd: e.tensor_scalar(out=dd, in0=pD_[:, 0:128], scalar1=-1.0, scalar2=None, op0=ALU.mult),
                     reads=[pDtk], writes=[ddtk])
                P.op("dve", lambda e, pD_=pD_, dd=dd: e.tensor_tensor(out=dd, in0=dd, in1=pD_[:, 0:128], op=ALU.max),
                     reads=[pDtk, ddtk], writes=[ddtk])
                P.op("dve", lambda e, emsb=emsb, dd=dd: e.tensor_tensor(out=dd, in0=dd, in1=emsb, op=ALU.max),
                     reads=[emtk, ddtk], writes=[ddtk])
                P.op("dve", lambda e, dd=dd: e.scalar_tensor_tensor(out=dd, in0=dd, scalar=EPS, in1=dd, op0=ALU.mult, op1=ALU.mult),
                     reads=[ddtk], writes=[ddtk])
                rr, rrtk = rr_rot.next()
                P.op("dve", lambda e, pSS=pSS, dd=dd, rr=rr: e.scalar_tensor_tensor(out=rr, in0=pSS, scalar=1.0 / DH, in1=dd,
                                                                                     op0=ALU.mult, op1=ALU.add),
                     reads=[pSStk, ddtk], writes=[rrtk])
                P.op("act", lambda e, rr=rr: e.activation(out=rr, in_=rr, func=AF.Sqrt), reads=[rrtk], writes=[rrtk])
                P.op("dve", lambda e, rr=rr: e.reciprocal(out=rr, in_=rr), reads=[rrtk], writes=[rrtk])
                for ec in range(4):
                    ch = hd * 4 + ec
                    tt, tttk = tt_rot.next()
                    P.op("dve", lambda e, pH=pH, ec=ec, ch=ch, rr=rr, tt=tt: e.scalar_tensor_tensor(
                        out=tt, in0=pH[:, ec * 128:(ec + 1) * 128], scalar=cf[:, L_HGAIN + ch:L_HGAIN + ch + 1], in1=rr,
                        op0=ALU.mult, op1=ALU.mult), reads=[pHtk, rrtk, ctk], writes=[tttk])
                    P.op("dve", lambda e, ec=ec, ch=ch, tt=tt, tsl=tsl: e.scalar_tensor_tensor(
                        out=tt, in0=xc[:, ec, tsl], scalar=cf[:, L_SKIP + ch:L_SKIP + ch + 1], in1=tt,
                        op0=ALU.mult, op1=ALU.add), reads=[xctk, tttk, ctk], writes=[tttk])
                    P.op("dve", lambda e, ec=ec, tt=tt, tsl=tsl: e.tensor_tensor(out=yb[:, ec, tsl], in0=tt, in1=zs[:, ec, tsl], op=ALU.mult),
                         reads=[tttk, zstk], writes=[ytk])


            stage_pre(0)
            stage_mid(0)
            for cl in range(1, 4):
                stage_pre(cl)
                stage_mid(cl)
                stage_post(cl - 1)
            stage_post(3)

            if tg == 1 and hd + 1 < BH:
                emit_fold(hd + 1)
            P.dma("sync", yview[:, hd * 4:(hd + 1) * 4, tg * 512:(tg + 1) * 512], yb, reads=[ytk])
    P.barrier()
    cx.release(base_mark)

    X = TOP.rearrange("p (c n) -> p c n", c=NC8)
    Xtk = [[Tk() for _ in range(4)] for _ in range(NC8)]
    for c in range(NC8):
        for tg in range(4):
            P.dma("sync", X[:, c, tg * 512:(tg + 1) * 512], x1T[c * 128:(c + 1) * 128, tg * 512:(tg + 1) * 512], writes=[Xtk[c][tg]])
    mk = cx.mark()
    wdn = cx.sb(None, [128, 16, D], BF16, "wdn")
    wdtk = Tk()
    wdv = wdown.rearrange("(c p) n -> p c n", p=128)
    wstg3 = Rot([cx.sb(None, [128, 4, D], F32, "wstg3") for _ in range(2)])
    for q4 in range(4):
        stg_, stk_ = wstg3.next()
        P.dma("sync", stg_, wdv[:, q4 * 4:(q4 + 1) * 4, :], writes=[stk_])
        P.op("act", lambda e, stg_=stg_, q4=q4: e.activation(out=wdn[:, q4 * 4:(q4 + 1) * 4, :], in_=stg_, func=AF.Copy),
             reads=[stk_], writes=[wdtk])
    yts = [cx.sb(None, [128, 16, 512], BF16, "yt") for _ in range(2)]
    yttk = [Tk(), Tk()]
    psA4 = Rot(bk[0:4])
    for tg in range(4):
        yt, ytt = yts[tg % 2], yttk[tg % 2]
        P.dma("sync", yt, yview[:, :, tg * 512:(tg + 1) * 512], writes=[ytt])
        sl = slice(tg * 512, (tg + 1) * 512)
        for oc in range(NC8):
            ps, pstk = psA4.next()
            for mc in range(16):
                P.op("pe", lambda e, ps=ps, mc=mc, oc=oc, yt=yt: e.matmul(ps, lhsT=wdn[:, mc, oc * 128:(oc + 1) * 128], rhs=yt[:, mc, :],
                                                                           start=(mc == 0), stop=(mc == 15)),
                     reads=[wdtk, ytt], writes=[pstk], signal=(mc == 15))
            P.op("dve", lambda e, ps=ps, oc=oc, sl=sl: e.tensor_tensor(out=X[:, oc, sl], in0=ps, in1=X[:, oc, sl], op=ALU.add),
                 reads=[pstk, Xtk[oc][tg]], writes=[Xtk[oc][tg]])
    P.barrier()
    cx.release(mk)
    if debug:
        emit_store(cx, X, Xtk, dbg_a)
    emit_mlp(cx, X, Xtk, cf[:, L_MLPN:L_MLPN + 8], ctk, w1, w2, ones_bf)
    emit_ple(cx, X, Xtk, cf[:, L_PLEN:L_PLEN + 8], ctk, wg, wp, pT, ones_bf)
    emit_store(cx, X, Xtk, out)
    P.finish()
    return nc, cx


_CACHE = {}


def _prog(key, builder):
    return builder()


def kernel(**inputs):
    inputs = {k: np.asarray(v) for k, v in inputs.items()}
    cores = list(range(NCORES))
    nc, _ = build_layer0()
    in_maps = [layer0_inputs(inputs, c) for c in cores]
    res = run_bass_kernel_spmd(nc, in_maps, core_ids=cores)
    x1T = np.concatenate([r["xout"] for r in res.results], axis=1)
    nc, _ = build_layer1("B")
    in_maps = [layer1_inputs(inputs, c, x1T, "B") for c in cores]
    res = run_bass_kernel_spmd(nc, in_maps, core_ids=cores)
    st_all = np.stack([res.results[c]["st_out"] for c in range(7)])
    g_rows = [res.results[c]["g_out"] for c in cores]
    nc, _ = build_layer1("C")
    in_maps = [layer1_inputs(inputs, c, x1T, "C", st_all, g_rows[c]) for c in cores]
    res = run_bass_kernel_spmd(nc, in_maps, core_ids=cores)
    outT = np.concatenate([r["xout"] for r in res.results], axis=1)
    return np.ascontiguousarray(outT.T)[None].astype(np.float32)
```
